# Optimizing a Trainium2 kernel written in Bass

```python
import jax, jax.numpy as jnp
from jax import lax
import numpy as np

D_MODEL = 1024
BATCH = 32
SEQ = 256
DEPTH = 2
DEC_BATCH = 8
DEC_SEQ = 2048
PAST_LEN = 256

GRID_W = 64
EPS = 1e-6
M_HEADS = 6
M_HD = 64
M_W = M_HEADS * M_HD
CHUNK = 64
R_HEADS = 6
R_HD = 64
R_W = R_HEADS * R_HD
R_LORA_W = 64
R_LORA_A = 64
RWKV_DECAY_SCALE = 0.6065306597126334
L_BLOCKS = 4
L_BD = 64
L_W = L_BLOCKS * L_BD
CONV_W = 4
LRU_C = 8.0
MIX_W = M_W + R_W + L_W
M_COLS = 5 * M_W + 4 * M_HEADS
R_SHIFT = 3 * R_W + 2 * R_LORA_W + 2 * R_LORA_A
R_COLS = R_SHIFT + R_W
L_COLS = 2 * L_W
IN_COLS = M_COLS + R_COLS + L_COLS

kernel_name = 'hymba_bidir_mlstm_rwkv7_rglru_dit_step'


def rmsnorm(x, g):
    xf = x.astype(jnp.float32)
    return xf * lax.rsqrt(jnp.mean(xf * xf, axis=-1, keepdims=True) + EPS) * g


def head_rmsnorm(h, g):
    B, T = h.shape[0], h.shape[1]
    hn = h * lax.rsqrt(jnp.mean(h * h, axis=-1, keepdims=True) + EPS)
    return hn.reshape(B, T, -1) * g


def seq_shift(u):
    left = jnp.pad(u[:, :-1], ((0, 0), (1, 0), (0, 0)))
    right = jnp.pad(u[:, 1:], ((0, 0), (0, 1), (0, 0)))
    return 0.5 * (left + right)


def grid_shift(u):
    B, T, C = u.shape
    rows = T // GRID_W
    g = u.reshape(B, rows, GRID_W, C)
    up = jnp.pad(g[:, :-1], ((0, 0), (1, 0), (0, 0), (0, 0)))
    down = jnp.pad(g[:, 1:], ((0, 0), (0, 1), (0, 0), (0, 0)))
    left = jnp.pad(g[:, :, :-1], ((0, 0), (0, 0), (1, 0), (0, 0)))
    right = jnp.pad(g[:, :, 1:], ((0, 0), (0, 0), (0, 1), (0, 0)))
    return (0.25 * (up + down + left + right)).reshape(B, T, C)


def mlstm_chunkwise(q, k, v, i_pre, logf, C0, n0, m0):
    B, H, T, Dh = q.shape
    nc = T // CHUNK
    to_c = lambda t: jnp.moveaxis(t.reshape(B, H, nc, CHUNK, *t.shape[3:]), 2, 0)
    mask = jnp.tril(jnp.ones((CHUNK, CHUNK), dtype=bool))

    def step(carry, inp):
        C, n, m = carry
        qc, kc, vc, ic, fc = inp
        b = jnp.cumsum(fc, axis=-1)
        dmat = jnp.where(mask, b[..., :, None] - b[..., None, :] + ic[..., None, :], -jnp.inf)
        inter = b + m[..., None]
        m_t = jnp.maximum(inter, jnp.max(dmat, axis=-1))
        s = jnp.einsum('bhtd,bhsd->bhts', qc, kc) * jnp.exp(dmat - m_t[..., None])
        sc = jnp.exp(inter - m_t)
        num = sc[..., None] * jnp.einsum('bhtd,bhde->bhte', qc, C) + jnp.einsum('bhts,bhse->bhte', s, vc)
        den = sc * jnp.einsum('bhtd,bhd->bht', qc, n) + jnp.sum(s, axis=-1)
        h = num / jnp.maximum(jnp.abs(den), jnp.exp(-m_t))[..., None]
        bL = b[..., -1]
        g = bL[..., None] - b + ic
        m_new = jnp.maximum(bL + m, jnp.max(g, axis=-1))
        dec = jnp.exp(bL + m - m_new)
        wk = jnp.exp(g - m_new[..., None])
        C_new = dec[..., None, None] * C + jnp.einsum('bhs,bhsd,bhse->bhde', wk, kc, vc)
        n_new = dec[..., None] * n + jnp.einsum('bhs,bhsd->bhd', wk, kc)
        return (C_new, n_new, m_new), h

    (C, n, m), hs = lax.scan(step, (C0, n0, m0), (to_c(q), to_c(k), to_c(v), to_c(i_pre), to_c(logf)))
    return jnp.moveaxis(hs, 0, 2).reshape(B, H, T, Dh), (C, n, m)


def mlstm_branch(u_m, b_i, b_f, norm_g, C0, n0, m0):
    B, T, _ = u_m.shape
    q, k, v, o, z, ig, fg = jnp.split(u_m, [M_W, 2 * M_W, 3 * M_W, 4 * M_W, 5 * M_W, 5 * M_W + 2 * M_HEADS], axis=-1)
    heads = lambda t: t.reshape(B, T, M_HEADS, M_HD).transpose(0, 2, 1, 3)
    q, k, v = heads(q), heads(k) * (M_HD ** -0.5), heads(v)
    ig = (ig.reshape(B, T, 2, M_HEADS) + b_i).transpose(0, 2, 3, 1)
    lf = jax.nn.log_sigmoid(fg.reshape(B, T, 2, M_HEADS) + b_f).transpose(0, 2, 3, 1)
    h_f, (Cf, nf, mf) = mlstm_chunkwise(q, k, v, ig[:, 0], lf[:, 0], C0[:, 0], n0[:, 0], m0[:, 0])
    flip = lambda t: jnp.flip(t, axis=2)
    h_b, (Cb, nb, mb) = mlstm_chunkwise(flip(q), flip(k), flip(v), jnp.flip(ig[:, 1], axis=-1),
                                        jnp.flip(lf[:, 1], axis=-1), C0[:, 1], n0[:, 1], m0[:, 1])
    h = (h_f + flip(h_b)).transpose(0, 2, 1, 3)
    y = head_rmsnorm(h, norm_g) * jax.nn.sigmoid(o) * jax.nn.silu(z)
    return y, (jnp.stack([Cf, Cb], 1), jnp.stack([nf, nb], 1), jnp.stack([mf, mb], 1))


def rwkv_step(S, inp):
    r, w, kh, b, kt, v = inp
    Sk = jnp.einsum('bhvk,bhk->bhv', S, kh)
    S = S * w[:, :, None, :] - Sk[..., None] * b[:, :, None, :] + v[..., None] * kt[:, :, None, :]
    return S, jnp.einsum('bhvk,bhk->bhv', S, r)


def rwkv_branch(u_rs, u_rz, lp, S0, shift_fn):
    B, T, _ = u_rs.shape
    blk = u_rs + lp['r_mu'] * (shift_fn(u_rs) - u_rs)
    r, k, v, wl, al = jnp.split(blk, [R_W, 2 * R_W, 3 * R_W, 3 * R_W + 2 * R_LORA_W], axis=-1)
    wl = wl.reshape(B, T, 2, R_LORA_W)
    al = al.reshape(B, T, 2, R_LORA_A)
    log_w = -RWKV_DECAY_SCALE * jax.nn.sigmoid(lp['r_w0'] + jnp.einsum('btdr,drc->btdc', jnp.tanh(wl), lp['r_w2']))
    a = jax.nn.sigmoid(lp['r_a0'] + jnp.einsum('btdr,drc->btdc', al, lp['r_a2']))
    kappa = (k * lp['r_kk']).reshape(B, T, R_HEADS, R_HD)
    kh = (kappa / jnp.maximum(jnp.sqrt(jnp.sum(kappa * kappa, axis=-1, keepdims=True)), 1e-12)).reshape(B, T, R_W)
    kt = k[:, :, None, :] * (1.0 + (a - 1.0) * lp['r_ka'])
    bvec = kh[:, :, None, :] * a
    tm = lambda t: t.reshape(B, T, R_HEADS, R_HD).transpose(1, 0, 2, 3)
    r_t, kh_t, v_t = tm(r), tm(kh), tm(v)
    ys, finals = [], []
    for d in range(2):
        Sd, yd = lax.scan(rwkv_step, S0[:, d],
                          (r_t, jnp.exp(tm(log_w[:, :, d])), kh_t, tm(bvec[:, :, d]), tm(kt[:, :, d]), v_t),
                          reverse=(d == 1))
        ys.append(yd)
        finals.append(Sd)
    y = (ys[0] + ys[1]).transpose(1, 0, 2, 3)
    bonus = jnp.sum((r * k * lp['r_rk']).reshape(B, T, R_HEADS, R_HD), axis=-1, keepdims=True) * v.reshape(B, T, R_HEADS, R_HD)
    out = head_rmsnorm(y, lp['r_norm']) + bonus.reshape(B, T, R_W)
    return out * jax.nn.silu(u_rz), jnp.stack(finals, 1)


def lru_combine(e1, e2):
    a1, b1 = e1
    a2, b2 = e2
    return a1 * a2, a2 * b1 + b2


def rglru_branch(u_l, lp, h0):
    B, T, _ = u_l.shape
    xl, z = jnp.split(u_l, [L_W], axis=-1)
    xp = jnp.pad(xl, ((0, 0), (CONV_W // 2, CONV_W - 1 - CONV_W // 2), (0, 0)))
    xc = sum(xp[:, j:j + T] * lp['l_conv'][j] for j in range(CONV_W)) + lp['l_conv_b']
    xb = xc.reshape(B, T, L_BLOCKS, L_BD)
    hs, finals = [], []
    for d in range(2):
        rg = jax.nn.sigmoid(jnp.einsum('btni,nio->btno', xb, lp['l_wa'][d]).reshape(B, T, L_W) + lp['l_ba'][d])
        ig = jax.nn.sigmoid(jnp.einsum('btni,nio->btno', xb, lp['l_wx'][d]).reshape(B, T, L_W) + lp['l_bx'][d])
        log_a = -LRU_C * rg * jax.nn.softplus(-lp['l_lambda'][d])
        a = jnp.exp(log_a)
        bt = jnp.sqrt(-jnp.expm1(2.0 * log_a)) * ig * xc
        pos = 0 if d == 0 else T - 1
        bt = bt.at[:, pos].add(a[:, pos] * h0[:, d])
        _, h = lax.associative_scan(lru_combine, (a, bt), reverse=(d == 1), axis=1)
        hs.append(h)
        finals.append(h[:, -1] if d == 0 else h[:, 0])
    return (hs[0] + hs[1]) * jax.nn.silu(z), jnp.stack(finals, 1)


def trunk_layer(x, shift, scale, gate, lp, states, shift_fn):
    h = rmsnorm(x, lp['g_pre']) * (1.0 + scale) + shift
    u = jnp.einsum('btd,dc->btc', h, lp['w_in']).astype(jnp.float32)
    u_m, u_rs, u_rz, u_l = jnp.split(u, [M_COLS, M_COLS + R_SHIFT, M_COLS + R_COLS], axis=-1)
    mC, mn, mm, rS, lh = states
    y_m, (mC2, mn2, mm2) = mlstm_branch(u_m, lp['m_bi'], lp['m_bf'], lp['m_norm'], mC, mn, mm)
    y_r, rS2 = rwkv_branch(u_rs, u_rz, lp, rS, shift_fn)
    y_l, lh2 = rglru_branch(u_l, lp, lh)
    mixed = jnp.concatenate([y_m, y_r, y_l], axis=-1)
    o = jnp.einsum('btc,cd->btd', mixed, lp['w_out'])
    return x + gate * rmsnorm(o, lp['g_post']), (mC2, mn2, mm2, rS2, lh2)


def setup_inputs(seed: int = 0) -> dict:
    key = jax.random.key(seed)
    ks = iter(jax.random.split(key, 48))
    nrm = lambda shape, s: jax.random.normal(next(ks), shape, jnp.float32) * s
    D = D_MODEL
    a_init = jax.random.uniform(next(ks), (DEPTH, 2, L_W), jnp.float32, 0.9, 0.999)
    s_init = a_init ** (1.0 / LRU_C)
    return {
        'x_prompt': nrm((BATCH, SEQ, D), 1.0),
        'x_sample': nrm((DEC_BATCH, DEC_SEQ, D), 1.0),
        'c': nrm((DEC_BATCH, D), 1.0),
        'state_mlstm_C': nrm((DEC_BATCH, DEPTH, 2, M_HEADS, M_HD, M_HD), 0.1),
        'state_mlstm_n': nrm((DEC_BATCH, DEPTH, 2, M_HEADS, M_HD), 0.1),
        'state_mlstm_m': nrm((DEC_BATCH, DEPTH, 2, M_HEADS), 0.5),
        'state_rwkv': nrm((DEC_BATCH, DEPTH, 2, R_HEADS, R_HD, R_HD), 0.1),
        'state_rglru': nrm((DEC_BATCH, DEPTH, 2, L_W), 0.5),
        'c_ctx': nrm((D,), 1.0),
        'g_pre': 1.0 + nrm((DEPTH, D), 0.02),
        'g_post': 1.0 + nrm((DEPTH, D), 0.02),
        'w_mod': nrm((DEPTH, D, 3 * D), 0.5 * D ** -0.5),
        'b_mod': nrm((DEPTH, 3 * D), 0.02),
        'w_in': nrm((DEPTH, D, IN_COLS), D ** -0.5),
        'w_out': nrm((DEPTH, MIX_W, D), MIX_W ** -0.5),
        'm_bi': nrm((DEPTH, 2, M_HEADS), 0.1),
        'm_bf': 3.0 + 3.0 * jax.random.uniform(next(ks), (DEPTH, 2, M_HEADS), jnp.float32),
        'm_norm': 1.0 + nrm((DEPTH, M_W), 0.02),
        'r_mu': jax.random.uniform(next(ks), (DEPTH, R_SHIFT), jnp.float32),
        'r_w0': nrm((DEPTH, 2, R_W), 0.5),
        'r_w2': nrm((DEPTH, 2, R_LORA_W, R_W), 0.5 * R_LORA_W ** -0.5),
        'r_a0': nrm((DEPTH, 2, R_W), 0.1),
        'r_a2': nrm((DEPTH, 2, R_LORA_A, R_W), 0.5 * R_LORA_A ** -0.5),
        'r_kk': 0.85 + nrm((DEPTH, R_W), 0.05),
        'r_ka': 1.0 + nrm((DEPTH, R_W), 0.05),
        'r_rk': nrm((DEPTH, R_W), 0.1),
        'r_norm': 1.0 + nrm((DEPTH, R_W), 0.02),
        'l_conv': nrm((DEPTH, CONV_W, L_W), CONV_W ** -0.5),
        'l_conv_b': nrm((DEPTH, L_W), 0.02),
        'l_wa': nrm((DEPTH, 2, L_BLOCKS, L_BD, L_BD), L_BD ** -0.5),
        'l_ba': nrm((DEPTH, 2, L_W), 0.02),
        'l_wx': nrm((DEPTH, 2, L_BLOCKS, L_BD, L_BD), L_BD ** -0.5),
        'l_bx': nrm((DEPTH, 2, L_W), 0.02),
        'l_lambda': jnp.log(s_init) - jnp.log1p(-s_init),
    }


def reference(x_prompt, x_sample, c, state_mlstm_C, state_mlstm_n, state_mlstm_m, state_rwkv, state_rglru,
              c_ctx, g_pre, g_post, w_mod, b_mod, w_in, w_out, m_bi, m_bf, m_norm, r_mu, r_w0, r_w2, r_a0,
              r_a2, r_kk, r_ka, r_rk, r_norm, l_conv, l_conv_b, l_wa, l_ba, l_wx, l_bx, l_lambda):
    f32 = jnp.float32
    Bp = x_prompt.shape[0]
    zero_states = (jnp.zeros((Bp, 2, M_HEADS, M_HD, M_HD), f32), jnp.zeros((Bp, 2, M_HEADS, M_HD), f32),
                   jnp.zeros((Bp, 2, M_HEADS), f32), jnp.zeros((Bp, 2, R_HEADS, R_HD, R_HD), f32),
                   jnp.zeros((Bp, 2, L_W), f32))
    xp, xs = x_prompt, x_sample
    mC_l, mn_l, mm_l, rS_l, lh_l = [], [], [], [], []
    for l in range(DEPTH):
        lp = dict(g_pre=g_pre[l], g_post=g_post[l], w_in=w_in[l], w_out=w_out[l], m_bi=m_bi[l], m_bf=m_bf[l],
                  m_norm=m_norm[l], r_mu=r_mu[l], r_w0=r_w0[l], r_w2=r_w2[l], r_a0=r_a0[l], r_a2=r_a2[l],
                  r_kk=r_kk[l], r_ka=r_ka[l], r_rk=r_rk[l], r_norm=r_norm[l], l_conv=l_conv[l],
                  l_conv_b=l_conv_b[l], l_wa=l_wa[l], l_ba=l_ba[l], l_wx=l_wx[l], l_bx=l_bx[l],
                  l_lambda=l_lambda[l])
        sh_c, sc_c, gt_c = jnp.split(jax.nn.silu(c_ctx) @ w_mod[l] + b_mod[l], 3, axis=-1)
        xp, st = trunk_layer(xp, sh_c, sc_c, gt_c, lp, zero_states, seq_shift)
        mC_l.append(st[0]); mn_l.append(st[1]); mm_l.append(st[2]); rS_l.append(st[3]); lh_l.append(st[4])
        sh_s, sc_s, gt_s = jnp.split(jax.nn.silu(c) @ w_mod[l] + b_mod[l], 3, axis=-1)
        cache = (state_mlstm_C[:, l].astype(f32), state_mlstm_n[:, l].astype(f32), state_mlstm_m[:, l].astype(f32),
                 state_rwkv[:, l].astype(f32), state_rglru[:, l].astype(f32))
        xs, _ = trunk_layer(xs, sh_s[:, None, :], sc_s[:, None, :], gt_s[:, None, :], lp, cache, grid_shift)
    y_prompt, y_sample = xp, xs
    new_mlstm_C = jnp.stack(mC_l, axis=1)
    new_mlstm_n = jnp.stack(mn_l, axis=1)
    new_mlstm_m = jnp.stack(mm_l, axis=1)
    new_rwkv = jnp.stack(rS_l, axis=1)
    new_rglru = jnp.stack(lh_l, axis=1)
    return (y_prompt, y_sample, new_mlstm_C, new_mlstm_n, new_mlstm_m, new_rwkv, new_rglru)
```

```python
import numpy as np
import concourse.bass as bass
import concourse.mybir as mybir
from concourse.bass_utils import run_bass_kernel_spmd

F32 = mybir.dt.float32
BF16 = mybir.dt.bfloat16
AF = mybir.ActivationFunctionType
ALU = mybir.AluOpType
AX = mybir.AxisListType

D = 1024
IN_COLS = 4248
EPS = 1e-6
DSC = 0.6065306597126334
TB = 256
NCORES = 8


def _prod(xs):
    r = 1
    for x in xs:
        r *= int(x)
    return r


class Sync:
    def __init__(self, nc, n_dma_sems=32):
        self.nc = nc
        self.engs = {'pe': nc.tensor, 'dve': nc.vector, 'act': nc.scalar,
                     'pool': nc.gpsimd, 'sp': nc.sync}
        self.sem = {}
        self.cnt = {}
        for e in ['pe', 'dve', 'act', 'pool']:
            self.sem[e] = nc.alloc_semaphore('sem_' + e)
            self.cnt[e] = 0
        self.seen = {e: {} for e in self.engs}
        self.dma_ring = [nc.alloc_semaphore('dq_%d' % i) for i in range(n_dma_sems)]
        self.dma_uses = [0] * n_dma_sems
        self.dma_next = 0
        self.rec = {}
        self.untracked = set()
        self.n_wait = 0
        self.n_ins = 0
        self.pstep_cache = {}
        self.sb_addr = {}

    def region(self, ap):
        t = ap.tensor
        name = t.name
        apl = [(int(s), int(c)) for (s, c) in ap.ap]
        off = int(ap.offset)
        if type(t).__name__.startswith('DRam'):
            lo = off + sum(min(0, s * (c - 1)) for s, c in apl)
            hi = off + sum(max(0, s * (c - 1)) for s, c in apl) + 1
            return (name, 0, 1, lo, hi)
        pstep = self.pstep_cache.get(name)
        if pstep is None:
            pstep = _prod(list(t.shape)[1:])
            self.pstep_cache[name] = pstep
        p0 = off // pstep
        f0 = off % pstep
        npart = apl[0][1]
        rest = apl[1:]
        lo = f0 + sum(min(0, s * (c - 1)) for s, c in rest)
        hi = f0 + sum(max(0, s * (c - 1)) for s, c in rest) + 1
        if name in self.sb_addr:
            base, es = self.sb_addr[name]
            return ('SB', p0, p0 + npart, base + lo * es, base + hi * es)
        return ('PS:' + name, (p0 // 32) * 32, ((p0 + npart + 31) // 32) * 32, (lo // 512) * 512, ((hi + 511) // 512) * 512)

    @staticmethod
    def _ovl(a, b):
        return a[1] < b[2] and b[1] < a[2] and a[3] < b[4] and b[3] < a[4]

    @staticmethod
    def _contains(a, b):
        return a[1] <= b[1] and b[2] <= a[2] and a[3] <= b[3] and b[4] <= a[4]

    def _collect(self, e, reads, writes):
        deps = {}
        own = self.sem.get(e)
        rregs = [self.region(a) for a in reads]
        wregs = [self.region(a) for a in writes]
        for r in rregs:
            if r[0] in self.untracked:
                continue
            isps = r[0].startswith('PS:')
            for (reg, kind, sem, val) in self.rec.get(r[0], ()):
                if (kind == 'w' or (isps and sem is not own)) and self._ovl(reg, r):
                    if e == 'pe' and sem is own:
                        continue
                    k = id(sem)
                    if deps.get(k, (None, 0))[1] < val:
                        deps[k] = (sem, val)
        for w in wregs:
            if w[0] in self.untracked:
                continue
            for (reg, kind, sem, val) in self.rec.get(w[0], ()):
                if self._ovl(reg, w):
                    if sem is own:
                        continue
                    k = id(sem)
                    if deps.get(k, (None, 0))[1] < val:
                        deps[k] = (sem, val)
        return deps, rregs, wregs

    def _record(self, rregs, wregs, sem, val):
        for r in rregs:
            if r[0] in self.untracked:
                continue
            lst = self.rec.setdefault(r[0], [])
            lst[:] = [x for x in lst if not (x[1] == 'r' and x[2] is sem and self._contains(r, x[0]))]
            lst.append((r, 'r', sem, val))
        for w in wregs:
            if w[0] in self.untracked:
                continue
            lst = self.rec.setdefault(w[0], [])
            lst[:] = [x for x in lst if not self._contains(w, x[0])]
            lst.append((w, 'w', sem, val))

    def wait(self, e, sem, val):
        k = id(sem)
        if self.seen[e].get(k, 0) >= val:
            return
        self.engs[e].wait_ge(sem, val)
        self.seen[e][k] = val
        self.n_wait += 1

    max_ins = None
    paranoid = False

    def emit(self, e, reads, writes, build, inc=True):
        if self.max_ins is not None and self.n_ins >= self.max_ins:
            raise StopIteration
        deps, rregs, wregs = self._collect(e, reads, writes)
        for (sem, val) in deps.values():
            self.wait(e, sem, val)
        if self.paranoid:
            for e2 in ['pe', 'dve', 'act', 'pool']:
                if self.cnt[e2] > 0 and not (e == 'pe' and e2 == 'pe'):
                    self.wait(e, self.sem[e2], self.cnt[e2])
        ins = build(self.engs[e])
        self.n_ins += 1
        if inc:
            self.cnt[e] += 1
            ins.then_inc(self.sem[e], 1)
            val = self.cnt[e]
        else:
            val = self.cnt[e] + 1
        self._record(rregs, wregs, self.sem[e], val)
        return ins

    def dma(self, q, out, in_, **kw):
        if self.max_ins is not None and self.n_ins >= self.max_ins:
            raise StopIteration
        i = self.dma_next
        self.dma_next = (i + 1) % len(self.dma_ring)
        sem = self.dma_ring[i]
        uses = self.dma_uses[i]
        if uses > 0:
            self.wait(q, sem, 16 * uses)
        deps, rregs, wregs = self._collect(q, [in_], [out])
        for (s, v) in deps.values():
            self.wait(q, s, v)
        ins = self.engs[q].dma_start(out=out, in_=in_, **kw)
        ins.then_inc(sem, 16)
        self.n_ins += 1
        self.dma_uses[i] = uses + 1
        self._record(rregs, wregs, sem, 16 * (uses + 1))
        return ins

    def finish(self, q='sp'):
        for i, sem in enumerate(self.dma_ring):
            if self.dma_uses[i] > 0:
                self.wait(q, sem, 16 * self.dma_uses[i])
        for e in ['pe', 'dve', 'act', 'pool']:
            if self.cnt[e] > 0:
                self.wait(q, self.sem[e], self.cnt[e])


class Arena:
    def __init__(self, base, size):
        self.base, self.size, self.ptr = base, size, 0

    def take(self, nbytes):
        off = (self.ptr + 31) // 32 * 32
        self.ptr = off + nbytes
        assert self.ptr <= self.size, ("arena overflow", self.ptr, self.size)
        return self.base + off


PL = 72
PC = dict(G_PRE=0, M_NORM=8, R_MU=11, R_W0=22, R_A0=28, R_KK=34, R_KA=37, R_RK=40, R_NORM=43,
          L_CONV=46, L_CONVB=54, L_BA=56, L_BX=60, L_LAM=64, M_BI=68, M_BF=70)
DL = 32
DC = dict(OMKA=0, CLAM=3, C2LAM=7, NBF=11, OMMU=16)
CC = dict(IDENT=0, BONES=128, MKF=256, MKB=384, MNF=512, MNB=576, MUI=640, MLI=704, RMASK=768,
          LSEL=1536, PSEL=1664, NMKF=1668, NMKB=1796, ONES=1924)
NCST = 1924 + 128


def make_consts():
    c = np.zeros((128, NCST), np.float32)
    c[:, 0:128] = np.eye(128)
    c[0:64, 128:192] = 1.0
    c[64:128, 192:256] = 1.0
    s = np.arange(64)[:, None]
    t = np.arange(64)[None, :]
    us, ui = (s < t).astype(np.float32), (s <= t).astype(np.float32)
    ls, li = (s > t).astype(np.float32), (s >= t).astype(np.float32)
    c[0:64, 256:320], c[0:64, 320:384] = us, ui
    c[0:64, 384:448], c[0:64, 448:512] = ls, li
    c[0:64, 512:576] = -ls
    c[0:64, 576:640] = -us
    c[0:64, 640:704] = ui
    c[0:64, 704:768] = li
    rm = np.ones(768, np.float32)
    rm[::64] = 0.0
    c[:, 768:1536] = rm[None, :]
    for k in range(6):
        c[k, 1536 + (k % 2) * 64: 1536 + (k % 2) * 64 + 64] = 1.0
        c[k, 1664 + k // 2] = 1.0
    c[0:64, 1668:1796] = -c[0:64, 256:384]
    c[0:64, 1796:1924] = -c[0:64, 384:512]
    c[:, 1924:2052] = 1.0
    return c


class Builder:
    def __init__(self, NP, TS, debug=False, stop=None):
        self.stop = stop
        self.NP, self.TS = NP, TS
        self.NTOK = NP * 256 + TS
        self.debug = debug
        nc = self.nc = bass.Bass("TRN2", target_bir_lowering=False)
        self.S = Sync(nc)
        self._decl_dram()
        self._alloc()

    def _decl_dram(self):
        nc, NP, TS = self.nc, self.NP, self.TS
        di = lambda n, s: nc.dram_tensor(n, list(s), F32, kind="ExternalInput").ap()
        do = lambda n, s: nc.dram_tensor(n, list(s), F32, kind="ExternalOutput").ap()
        dx = lambda n, s: nc.dram_tensor(n, list(s), F32, kind="Internal").ap()
        self.x_in = di("x_in", [self.NTOK, D])
        self.cc = di("cc", [128, 8, 2])
        self.w_mod = di("w_mod", [2, D, 3 * D])
        self.bmodT = di("bmodT", [128, 2, 16])
        self.bmodg = di("bmodg", [2, D])
        self.gpost = di("gpost", [2, D])
        self.w_in = di("w_in", [2, D, IN_COLS])
        self.w_out = di("w_out", [2, D, D])
        self.pp = di("pp", [128, 2 * PL])
        self.cst = di("cst", [128, NCST])
        self.lora = di("lora", [128, 2, 2, 384])
        self.lruw = di("lruw", [128, 2, 8, 128])
        self.st_mC = di("st_mC", [2, 2, 128, 3, 65])
        self.st_mm = di("st_mm", [6, 2, 2])
        self.st_rH = di("st_rH", [2, 2, 128, 3, 64])
        self.st_l = di("st_l", [128, 2, 2, 2])
        for n in ["x_in", "cc", "w_mod", "bmodT", "bmodg", "gpost", "w_in", "w_out", "pp", "cst", "lora",
                  "lruw", "st_mC", "st_mm", "st_rH", "st_l"]:
            self.S.untracked.add(n)
        self.y_out = do("y_out", [self.NTOK, D])
        self.o_mC = do("o_mC", [NP, 2, 2, 128, 3, 65])
        self.o_mm = do("o_mm", [NP, 2, 2, 6, 1])
        self.o_rH = do("o_rH", [NP, 2, 2, 128, 3, 64])
        self.o_l = do("o_l", [NP, 2, 2, 128, 2])
        self.x1 = dx("x1", [self.NTOK, D])
        self.sHB = dx("sHB", [max(TS, 64), 384])
        self.sYB = dx("sYB", [max(TS, 64), 384])
        self.sLB = dx("sLB", [128, 2, max(TS, 64)])
        if self.debug:
            self.dbg = do("dbg", [128, 32768])
            self.dbg_map = {}
            self.dbg_off = 0

    def T(self, name, shape, dtype, arena):
        es = 2 if dtype == BF16 else 4
        nb = _prod(shape[1:]) * es
        off = arena.take(nb)
        t = self.nc.alloc_sbuf_tensor_at(name, list(shape), dtype, offset=off)
        self.S.sb_addr[t.name] = (off, es)
        return t

    def _alloc(self):
        nc = self.nc
        B0 = 16384 + 256
        LIM = 224 * 1024 - 256
        P = Arena(B0, LIM - B0)
        T = self.T
        self.W_IN = T("W_IN", [128, 8, IN_COLS], BF16, P)
        self.W_OUT = T("W_OUT", [128, 8, D], BF16, P)
        self.CST = T("CST", [128, NCST], F32, P)
        self.PP = T("PP", [128, 2 * PL], F32, P)
        self.DV = T("DV", [128, 2 * DL], F32, P)
        self.LORA = T("LORA", [128, 2, 2, 384], F32, P)
        self.LRUW = T("LRUW", [128, 2, 8, 128], F32, P)
        self.RKD = T("RKD", [128, 2, 3, 128], F32, P)
        self.GS = T("GS", [128, 2, 8], F32, P)
        self.SH = T("SH", [128, 2, 8], F32, P)
        self.GATEB = T("GATEB", [128, 2, D], F32, P)
        self.mC = [T("mC%d" % d, [128, 3, 65], F32, P) for d in range(2)]
        self.mM = [T("mM%d" % d, [6, 1], F32, P) for d in range(2)]
        self.rH = [T("rH%d" % d, [128, 3, 64], F32, P) for d in range(2)]
        self.lS = [T("lS%d" % d, [128, 2], F32, P) for d in range(2)]
        XB = Arena(P.take(16384), 16384)
        self.HT = T("HT", [128, 8, 384], BF16, P)
        self.MIXT = T("MIXT", [128, 8, 256], BF16, P)
        self.BLK = T("BLK", [128, 11, 256], F32, P)
        abase = P.take(0)
        asize = P.size - P.ptr
        self.asize = asize
        mk = lambda: Arena(abase, asize)
        a = Arena(XB.base, XB.size)
        self.XW = [T("XW%d" % i, [128, D], F32, a) for i in range(2)]
        self.XN = [T("XN%d" % i, [128, D], F32, a) for i in range(2)]
        a = Arena(XB.base, XB.size)
        self.YB = T("YB", [64, 4, 384], F32, a)
        self.VTOK = T("VTOK", [64, 4, 384], BF16, a)
        self.RZT = T("RZT", [128, 3, 256], F32, a)
        a = mk()
        self.WM = T("WM", [128, 8, 512], F32, a)
        self.SCT = T("SCT", [128, 8, 2], F32, a)
        self.SCB = T("SCB", [128, 2, 8, 128], F32, a)
        self.MODT = T("MODT", [128, 16, 2], F32, a)
        self.BMT = T("BMT", [128, 2, 16], F32, a)
        self.BG = T("BG", [128, D], F32, a)
        self.GP = T("GP", [128, D], F32, a)
        self.TMPS = T("TMPS", [128, 32], F32, a)
        a = mk()
        self.QT = T("QT", [128, 3, 256], BF16, a)
        self.KT = T("KT", [128, 3, 256], BF16, a)
        self.GOZ = T("GOZ", [128, 3, 256], F32, a)
        self.KTOK = T("KTOK", [64, 4, 384], BF16, a)
        self.VAUG = T("VAUG", [64, 4, 6, 65], BF16, a)
        self.HB = T("HB", [64, 4, 384], F32, a)
        self.mg = []
        for d in range(2):
            g = {}
            for n in ["gi", "lf", "pre", "bb", "gg", "ee", "fl"]:
                g[n] = T("mg_%s%d" % (n, d), [6, 256], F32, a)
            for n in ["mx", "mch", "mprev", "MM", "dec"]:
                g[n] = T("mg_%s%d" % (n, d), [6, 4], F32, a)
            g["X2"] = T("mg_X2%d" % d, [6, 4, 3], F32, a)
            g["etok"] = T("mg_etok%d" % d, [64, 4, 6], F32, a)
            g["fltok"] = T("mg_fltok%d" % d, [64, 4, 6], F32, a)
            g["decb"] = T("mg_decb%d" % d, [128, 4, 3], F32, a)
            self.mg.append(g)
        self.STSB = [T("STSB%d" % i, [64, 6, 64], BF16, a) for i in range(2)]
        self.VP = [T("VP%d" % i, [64, 6, 65], BF16, a) for i in range(2)]
        self.CDEC = T("CDEC", [128, 3, 65], F32, a)
        self.CDBF = T("CDBF", [128, 3, 65], BF16, a)
        self.DN = T("DN", [64, 6], F32, a)
        self.RDN = T("RDN", [64, 6], F32, a)
        self.HD = T("HD", [64, 6, 64], F32, a)
        self.SQ = T("SQ", [64, 4, 384], F32, a)
        self.SSQ = T("SSQ", [64, 24], F32, a)
        self.RSTD = T("RSTD", [64, 24], F32, a)
        self.ZT = [T("ZT%d" % i, [128, 256], F32, a) for i in range(2)]
        a = mk()
        self.XL = T("XL", [128, 2, 392], F32, a)
        self.XC = T("XC", [128, 2, 256], F32, a)
        self.LZT = T("LZT", [128, 2, 256], F32, a)
        self.lt = {n: T("lt_" + n, [128, 2, 256], F32, a) for n in ["rg", "ig", "aa", "a2", "bt"]}
        self.HL = [T("HL%d" % d, [128, 2, 256], F32, a) for d in range(2)]
        a = mk()
        self.URS = T("URS", [128, 11, 384], F32, a)
        a = mk()
        self.KH = T("KH", [128, 3, 256], F32, a)
        blk4 = a.take(4 * 3072)
        a4 = Arena(blk4, 4 * 3072)
        self.rt = {n: T("rt_" + n, [128, 3, 256], F32, a4) for n in ["sg", "aa", "kt", "bb"]}
        self.RTBIG = T("RTBIG", [128, 11, 256], F32, Arena(blk4, 4 * 3072))
        self.RSQ = T("RSQ", [64, 4, 384], F32, Arena(blk4, 4 * 3072))
        for n in ["cs", "d", "E"]:
            self.rt[n] = T("rt_" + n, [128, 3, 256], F32, a)
        self.KR = T("KR", [128, 3, 4, 2, 64], BF16, a)
        self.BTT = T("BTT", [128, 3, 256], BF16, a)
        self.KTT = T("KTT", [128, 3, 256], BF16, a)
        self.KTTOK = T("KTTOK", [64, 4, 384], BF16, a)
        self.BTTOK = T("BTTOK", [64, 4, 384], BF16, a)
        self.TWL = T("TWL", [128, 256], F32, a)
        self.GL = T("GL", [128, 3, 4], F32, a)
        self.A1 = T("A1", [64, 6, 128], BF16, a)
        self.A2 = T("A2", [64, 6, 128], BF16, a)
        self.NN = [T("NN%d" % i, [64, 6, 64], BF16, a) for i in range(2)]
        self.NTT = [T("NTT%d" % i, [64, 6, 64], BF16, a) for i in range(2)]
        self.U = T("U", [64, 6, 64], F32, a)
        self.UBF = T("UBF", [64, 6, 64], BF16, a)
        self.HBF = T("HBF", [128, 3, 64], BF16, a)
        self.HTMP = T("HTMP", [128, 3, 64], F32, a)
        self.rSSQ = T("rSSQ", [64, 24], F32, a)
        self.rRSTD = T("rRSTD", [64, 24], F32, a)
        self.PA = nc.alloc_psum_tensor("PA", [128, 1024], F32)
        self.PB = nc.alloc_psum_tensor("PB", [128, 1024], F32)
        self.PQ = [nc.alloc_psum_tensor("PQ%d" % i, [128, 512], F32) for i in range(4)]
        self.pq_i = 0

    def pq(self):
        t = self.PQ[self.pq_i % 4]
        self.pq_i += 1
        return t

    def V(self, t, p0, npart, off, dims):
        pstep = _prod(list(t.shape)[1:])
        return bass.AP(t, p0 * pstep + off, [[pstep, npart]] + [list(d) for d in dims])

    def tt(self, e, out, in0, in1, op):
        return self.S.emit(e, [in0, in1], [out], lambda g: g.tensor_tensor(out=out, in0=in0, in1=in1, op=op))

    def ts(self, e, out, in0, s1, s2, op0, op1=None):
        rd = [in0] + [s for s in (s1, s2) if not isinstance(s, (int, float)) and s is not None]
        if op1 is None:
            return self.S.emit(e, rd, [out], lambda g: g.tensor_scalar(out=out, in0=in0, scalar1=s1, scalar2=None, op0=op0))
        return self.S.emit(e, rd, [out], lambda g: g.tensor_scalar(out=out, in0=in0, scalar1=s1, scalar2=s2, op0=op0, op1=op1))

    def stt(self, out, in0, sc, in1, op0, op1):
        rd = [in0, in1] + ([] if isinstance(sc, (int, float)) else [sc])
        return self.S.emit('dve', rd, [out], lambda g: g.scalar_tensor_tensor(out=out, in0=in0, scalar=sc, in1=in1, op0=op0, op1=op1))

    def act(self, out, in_, func, bias=None, scale=None, accum=None):
        rd = [in_]
        kw = {}
        if bias is not None:
            kw['bias'] = bias
            if not isinstance(bias, (int, float)):
                rd.append(bias)
        if scale is not None:
            kw['scale'] = scale
            if not isinstance(scale, (int, float)):
                rd.append(scale)
        wr = [out]
        if accum is not None:
            kw['accum_out'] = accum
            wr.append(accum)
        return self.S.emit('act', rd, wr, lambda g: g.activation(out=out, in_=in_, func=func, **kw))

    def cp(self, e, out, in_):
        if e == 'act':
            return self.act(out, in_, AF.Copy)
        return self.S.emit(e, [in_], [out], lambda g: g.tensor_copy(out=out, in_=in_))

    def _pe_rowtile_guard(self, lhsT, out):
        S = self.S
        st = S.region(lhsT)
        k = st[2] - st[1]
        kr = 32 if k <= 32 else (64 if k <= 64 else 128)
        rows = (st[1], st[1] + kr)
        oreg = S.region(out)
        last = getattr(self, '_last_pe', None)
        if last is not None:
            lrows, loreg, lins, linc = last
            disjoint = rows[1] <= lrows[0] or lrows[1] <= rows[0]
            samebank = (loreg[0] == oreg[0]) and loreg[3] < oreg[4] and oreg[3] < loreg[4]
            if disjoint and samebank:
                if not linc:
                    S.cnt['pe'] += 1
                    lins.then_inc(S.sem['pe'], 1)
                S.wait('pe', S.sem['pe'], S.cnt['pe'])
        return rows, oreg

    def mm(self, out, lhsT, rhs, start=True, stop=True, inc=None):
        if inc is None:
            inc = stop
        rows, oreg = self._pe_rowtile_guard(lhsT, out)
        ins = self.S.emit('pe', [lhsT, rhs], [out],
                          lambda g: g.matmul(out, lhsT=lhsT, rhs=rhs, start=start, stop=stop), inc=inc)
        self._last_pe = (rows, oreg, ins, inc)
        return ins

    def tr(self, out, in_, inc=True):
        n = in_.shape[0]
        ident = self.CST[0:n, CC['IDENT']:CC['IDENT'] + n]
        rows, oreg = self._pe_rowtile_guard(in_, out)
        ins = self.S.emit('pe', [in_, ident], [out],
                          lambda g: g.transpose(out=out, in_=in_, identity=ident), inc=inc)
        self._last_pe = (rows, oreg, ins, inc)
        return ins

    def memset(self, e, ap, v):
        return self.S.emit(e, [], [ap], lambda g: g.memset(ap, v))

    def scan(self, out, d0, d1, init, op0, op1):
        rd = [d0, d1] + ([] if isinstance(init, (int, float)) else [init])
        return self.S.emit('dve', rd, [out], lambda g: g.tensor_tensor_scan(out=out, data0=d0, data1=d1, initial=init, op0=op0, op1=op1))

    def recip(self, out, in_):
        return self.S.emit('dve', [in_], [out], lambda g: g.reciprocal(out=out, in_=in_))

    def reduce(self, out, in_, op, axis=AX.X):
        return self.S.emit('dve', [in_], [out], lambda g: g.tensor_reduce(out=out, in_=in_, axis=axis, op=op))

    def dump(self, name, ap):
        if not self.debug or name in self.dbg_map:
            return
        if getattr(self, 'dbg_filter', None) is not None and not any(name.startswith(p) for p in self.dbg_filter):
            return
        shp = list(ap.shape)
        npart, nfree = shp[0], _prod(shp[1:])
        stage = self.XN[1]
        assert nfree <= 1024
        dst = self.V(stage, 0, npart, 0, [[_prod(shp[i + 1:]), shp[i]] for i in range(1, len(shp))])
        self.cp('dve', dst, ap)
        self.S.dma('sp', self.dbg[0:npart, self.dbg_off:self.dbg_off + nfree], stage[0:npart, 0:nfree], allow_slow_non_contiguous=True)
        self.dbg_map[name] = (self.dbg_off, npart, shp[1:])
        self.dbg_off += nfree

    def ppc(self, l, key, j=0, rows=128):
        c = l * PL + PC[key] + j
        return self.PP[0:rows, c:c + 1]

    def dvc(self, l, key, j=0, rows=128):
        c = l * DL + DC[key] + j
        return self.DV[0:rows, c:c + 1]

    def setup(self):
        S = self.S
        S.dma('sp', self.CST[:, :], self.cst[:, :])
        S.dma('sp', self.PP[:, :], self.pp[:, :])
        S.dma('sp', self.LORA[:, :, :, :], self.lora[:, :, :, :])
        S.dma('sp', self.LRUW[:, :, :, :], self.lruw[:, :, :, :])
        self.memset('dve', self.DV[:, :], 0.0)
        for l in range(2):
            self.ts('dve', self.DV[:, l * DL + DC['OMKA']: l * DL + DC['OMKA'] + 3],
                    self.PP[:, l * PL + PC['R_KA']: l * PL + PC['R_KA'] + 3], -1.0, 1.0, ALU.mult, ALU.add)
            self.ts('dve', self.DV[:, l * DL + DC['OMMU']: l * DL + DC['OMMU'] + 11],
                    self.PP[:, l * PL + PC['R_MU']: l * PL + PC['R_MU'] + 11], -1.0, 1.0, ALU.mult, ALU.add)
            lam = self.PP[:, l * PL + PC['L_LAM']: l * PL + PC['L_LAM'] + 4]
            t0 = self.TMPS[:, 0:4]
            self.act(t0, lam, AF.Exp, scale=-1.0)
            self.act(t0, t0, AF.Ln, bias=1.0)
            self.ts('dve', self.DV[:, l * DL + DC['CLAM']: l * DL + DC['CLAM'] + 4], t0, -8.0, None, ALU.mult)
            self.ts('dve', self.DV[:, l * DL + DC['C2LAM']: l * DL + DC['C2LAM'] + 4], t0, -16.0, None, ALU.mult)
            self.ts('dve', self.DV[0:6, l * DL + DC['NBF']: l * DL + DC['NBF'] + 2],
                    self.PP[0:6, l * PL + PC['M_BF']: l * PL + PC['M_BF'] + 2], -1.0, None, ALU.mult)
            for hp in range(3):
                self.ts('dve', self.RKD[:, l, hp, :], self.CST[:, CC['BONES']:CC['BONES'] + 128],
                        self.ppc(l, 'R_RK', hp), None, ALU.mult)
        self.memset('dve', self.XL[:, :, 0:2], 0.0)

    def load_layer(self, l):
        S = self.S
        wsrc = self.w_in[l].rearrange("(kc p) c -> p kc c", p=128)
        NB = 8
        cb = IN_COLS // NB
        for i in range(NB):
            S.dma('pool', self.W_IN[:, :, i * cb:(i + 1) * cb], wsrc[:, :, i * cb:(i + 1) * cb])
        wo = self.w_out[l].rearrange("(kc p) c -> p kc c", p=128)
        for i in range(2):
            S.dma('pool', self.W_OUT[:, :, i * 512:(i + 1) * 512], wo[:, :, i * 512:(i + 1) * 512])
        if self.stop == 'load_w':
            raise StopIteration
        S.dma('sp', self.SCT[:, :, :], self.cc[:, :, :])
        S.dma('sp', self.BMT[:, :, :], self.bmodT[:, :, :])
        self.act(self.SCT[:, :, :], self.SCT[:, :, :], AF.Silu)
        for m in range(2):
            for kc in range(8):
                src = self.V(self.SCT, 0, 128, kc * 2 + m, [[0, 128]])
                self.cp('dve', self.SCB[:, m, kc, :], src)
        wm = self.w_mod[l].rearrange("(kc p) c -> p kc c", p=128)
        for blk in range(6):
            S.dma('sp', self.WM[:, :, :], wm[:, :, blk * 512:(blk + 1) * 512])
            if blk < 4:
                ps = self.pq()
                for j in range(4):
                    for kc in range(8):
                        self.mm(ps[:, j * 2:j * 2 + 2], self.WM[:, kc, j * 128:(j + 1) * 128], self.SCT[:, kc, :],
                                start=(kc == 0), stop=(kc == 7))
                o = self.MODT[:, blk * 4:(blk + 1) * 4, :]
                bsrc = self.V(self.BMT, 0, 128, l * 16 + blk * 4, [[1, 4], [0, 2]])
                self.tt('dve', o, self.V(ps, 0, 128, 0, [[2, 4], [1, 2]]), bsrc, ALU.add)
            else:
                half = blk - 4
                for m in range(2):
                    ps = self.pq()
                    for kc in range(8):
                        self.mm(ps[:, :], self.SCB[:, m, kc, :], self.WM[:, kc, :], start=(kc == 0), stop=(kc == 7))
                    self.cp('act', self.GATEB[:, m, half * 512:(half + 1) * 512], ps[:, :])
        if self.stop == 'load_m':
            raise StopIteration
        for m in range(2):
            self.cp('dve', self.SH[:, m, :], self.V(self.MODT, 0, 128, m, [[2, 8]]))
            t0 = self.TMPS[:, 8:16]
            self.ts('dve', t0, self.V(self.MODT, 0, 128, 16 + m, [[2, 8]]), 1.0, None, ALU.add)
            self.tt('dve', self.GS[:, m, :], t0, self.PP[:, l * PL + PC['G_PRE']: l * PL + PC['G_PRE'] + 8], ALU.mult)
        if self.stop == 'load_g':
            raise StopIteration
        S.dma('sp', self.BG[:, :], bass.AP(self.bmodg.tensor, l * D, [[0, 128], [1, D]]))
        S.dma('sp', self.GP[:, :], bass.AP(self.gpost.tensor, l * D, [[0, 128], [1, D]]))
        if self.stop == 'load_b':
            raise StopIteration
        for m in range(2):
            self.tt('dve', self.GATEB[:, m, :], self.GATEB[:, m, :], self.BG[:, :], ALU.add)
            self.tt('dve', self.GATEB[:, m, :], self.GATEB[:, m, :], self.GP[:, :], ALU.mult)

    def proj_fm(self, c0, ncols, t_off, ntok, evac):
        ps = self.pq()
        for kc in range(8):
            self.mm(ps[0:ncols, 0:ntok], self.W_IN[:, kc, c0:c0 + ncols], self.HT[:, kc, t_off:t_off + ntok],
                    start=(kc == 0), stop=(kc == 7))
        evac(ps[0:ncols, 0:ntok])

    def proj_tm(self, c0, ncols, t_off, evac):
        ps = self.pq()
        for kc in range(8):
            self.mm(ps[0:64, 0:ncols], self.HT[:, kc, t_off:t_off + 64], self.W_IN[:, kc, c0:c0 + ncols],
                    start=(kc == 0), stop=(kc == 7))
        evac(ps[0:64, 0:ncols])

    def visit(self, l, mod, xsrc, xdst, row0, T, t0, dirs, grid, seq_idx, first, last):
        S = self.S
        w0 = max(0, t0 - 64)
        w1 = min(T, t0 + TB + 64)
        W = w1 - w0
        co = t0 - w0
        do_f = 0 in dirs
        prompt = (mod == 0)
        ntile = (W + 127) // 128
        for i in range(ntile):
            n = min(128, W - i * 128)
            xw, xn = self.XW[i % 2], self.XN[i % 2]
            r = row0 + w0 + i * 128
            S.dma('sp', xw[0:n, :], xsrc[r:r + n, :])
            ssq = self.TMPS[0:n, 16 + i:17 + i]
            self.act(xn[0:n, :], xw[0:n, :], AF.Square, accum=ssq)
            if self.stop == 'n1':
                raise StopIteration
            rs = self.TMPS[0:n, 20 + i:21 + i]
            self.ts('dve', rs, ssq, 1.0 / D, EPS, ALU.mult, ALU.add)
            self.act(rs, rs, AF.Sqrt)
            self.recip(rs, rs)
            if self.stop == 'n2':
                raise StopIteration
            self.act(xn[0:n, :], xw[0:n, :], AF.Copy, scale=rs)
            if self.stop == 'n3':
                raise StopIteration
            for half in range(2):
                ps = self.pq()
                for j in range(4):
                    kc = half * 4 + j
                    self.tr(ps[:, j * 128:j * 128 + n], xn[0:n, kc * 128:(kc + 1) * 128])
                if self.stop == 'n4':
                    raise StopIteration
                for j in range(4):
                    kc = half * 4 + j
                    o = self.HT[:, kc, i * 128:i * 128 + n]
                    if half == 0:
                        self.ts('dve', o, ps[:, j * 128:j * 128 + n], self.GS[:, mod, kc:kc + 1], self.SH[:, mod, kc:kc + 1], ALU.mult, ALU.add)
                    else:
                        self.act(o, ps[:, j * 128:j * 128 + n], AF.Identity, bias=self.SH[:, mod, kc:kc + 1], scale=self.GS[:, mod, kc:kc + 1])
        if self.stop in ('norm', 'n5a', 'n5d'):
            raise StopIteration
        self.stage_mlstm(l, t0, co, dirs, prompt, seq_idx, first, last)
        if self.stop == 'mlstm':
            raise StopIteration
        self.stage_lru(l, t0, co, W, w0, w1, T, dirs, prompt, seq_idx, first, last)
        if self.stop == 'lru':
            raise StopIteration
        self.stage_rwkv(l, t0, co, W, w0, w1, T, dirs, grid, prompt, seq_idx, first, last)
        for kc in range(8):
            self.dump('mix%d' % kc, self.MIXT[:, kc, :])
        if self.stop == 'rwkv':
            raise StopIteration
        if do_f:
            self.stage_out(l, mod, xsrc, xdst, row0 + t0)
        if self.stop == 'out':
            raise StopIteration

    def stage_mlstm(self, l, t0, co, dirs, prompt, seq_idx, first, last):
        S = self.S
        do_f = 0 in dirs
        for hp in range(3):
            self.proj_fm(hp * 128, 128, co, 256, lambda ps, hp=hp: self.cp('act', self.QT[:, hp, :], ps))
            self.proj_fm(384 + hp * 128, 128, co, 256, lambda ps, hp=hp: self.act(self.KT[:, hp, :], ps, AF.Copy, scale=0.125))
        if do_f:
            for hp in range(3):
                def ev_o(ps, hp=hp):
                    self.act(self.GOZ[:, hp, :], ps, AF.Sigmoid)
                self.proj_fm(1152 + hp * 128, 128, co, 256, ev_o)
                def ev_z2(ps, hp=hp):
                    tz = self.ZT[hp % 2]
                    self.act(tz[:, :], ps, AF.Silu)
                    self.tt('dve', self.GOZ[:, hp, :], self.GOZ[:, hp, :], tz[:, :], ALU.mult)
                self.proj_fm(1536 + hp * 128, 128, co, 256, ev_z2)
        for c in range(4):
            self.proj_tm(384, 384, co + c * 64, lambda ps, c=c: self.act(self.KTOK[:, c, :], ps, AF.Copy, scale=0.125))
            def ev_v(ps, c=c):
                self.cp('dve', self.VAUG[:, c, :, 0:64], self.V(ps.tensor, 0, 64, 0, [[64, 6], [1, 64]]))
            self.proj_tm(768, 384, co + c * 64, ev_v)
        self.memset('dve', self.VAUG[:, :, :, 64:65], 1.0)
        for d in dirs:
            g = self.mg[d]
            self.proj_fm(1920 + d * 6, 6, co, 256, lambda ps, g=g, d=d: self.act(g["gi"][:, :], ps, AF.Identity, bias=self.ppc(l, 'M_BI', d, 6)))
            def ev_f(ps, g=g, d=d):
                self.act(g["lf"][:, :], ps, AF.Exp, bias=self.dvc(l, 'NBF', d, 6), scale=-1.0)
                self.act(g["lf"][:, :], g["lf"][:, :], AF.Ln, bias=1.0)
                self.ts('dve', g["lf"][:, :], g["lf"][:, :], -1.0, None, ALU.mult)
            self.proj_fm(1932 + d * 6, 6, co, 256, ev_f)
        for d in dirs:
            if first[d]:
                if prompt:
                    self.memset('dve', self.mC[d][:, :, :], 0.0)
                    self.memset('dve', self.mM[d][:, :], 0.0)
                else:
                    S.dma('sp', self.mC[d][:, :, :], self.st_mC[l, d])
                    S.dma('sp', self.mM[d][:, :], self.st_mm[:, l, d:d + 1], allow_slow_non_contiguous=True)
        if do_f and not (1 in dirs):
            S.dma('sp', self.HB[:, :, :], self.sHB[t0:t0 + 256, :].rearrange("(c s) f -> s c f", s=64))
        for d in dirs:
            g = self.mg[d]
            v3 = lambda t: self.V(t, 0, 6, 0, [[64, 4], [1, 64]])
            self.scan(g["pre"][:, :], self.CST[0:6, CC['RMASK']:CC['RMASK'] + 256], g["lf"][:, :], 0.0, ALU.mult, ALU.add)
            bL = self.V(g["pre"], 0, 6, 63, [[64, 4]])
            bLb = self.V(g["pre"], 0, 6, 63, [[64, 4], [0, 64]])
            if d == 0:
                bsrc = g["pre"]
            else:
                self.tt('dve', v3(g["bb"]), bLb, v3(g["pre"]), ALU.subtract)
                self.tt('dve', g["bb"][:, :], g["bb"][:, :], g["lf"][:, :], ALU.add)
                bsrc = g["bb"]
            self.tt('dve', g["gg"][:, :], g["gi"][:, :], bsrc[:, :], ALU.subtract)
            self.reduce(g["mx"][:, :], v3(g["gg"]), ALU.max)
            if d == 0:
                mo, mxv, blv = g["mch"][:, :], g["mx"][:, :], bL
            else:
                mo = self.V(g["mch"], 0, 6, 3, [[-1, 4]])
                mxv = self.V(g["mx"], 0, 6, 3, [[-1, 4]])
                blv = self.V(g["pre"], 0, 6, 63 + 3 * 64, [[-64, 4]])
            self.scan(mo, mxv, blv, self.mM[d][:, 0:1], ALU.max, ALU.add)
            if d == 0:
                self.cp('dve', g["mprev"][:, 1:4], g["mch"][:, 0:3])
                self.cp('dve', g["mprev"][:, 0:1], self.mM[d][:, 0:1])
                mfin = g["mch"][:, 3:4]
            else:
                self.cp('dve', g["mprev"][:, 0:3], g["mch"][:, 1:4])
                self.cp('dve', g["mprev"][:, 3:4], self.mM[d][:, 0:1])
                mfin = g["mch"][:, 0:1]
            self.tt('dve', g["MM"][:, :], g["mprev"][:, :], g["mx"][:, :], ALU.max)
            self.tt('dve', g["dec"][:, :], g["mprev"][:, :], g["MM"][:, :], ALU.subtract)
            self.act(g["dec"][:, :], g["dec"][:, :], AF.Exp)
            self.cp('dve', self.mM[d][:, 0:1], mfin)
            MMb = self.V(g["MM"], 0, 6, 0, [[1, 4], [0, 64]])
            self.tt('dve', v3(g["ee"]), v3(g["gg"]), MMb, ALU.subtract)
            self.act(g["ee"][:, :], g["ee"][:, :], AF.Exp)
            self.tt('dve', v3(g["fl"]), v3(bsrc), MMb, ALU.add)
            self.act(g["fl"][:, :], g["fl"][:, :], AF.Exp, scale=-1.0)
            ps = self.pq()
            for c in range(4):
                self.tr(ps[0:64, c * 6:c * 6 + 6], g["ee"][:, c * 64:(c + 1) * 64])
                self.tr(ps[0:64, 24 + c * 6:24 + c * 6 + 6], g["fl"][:, c * 64:(c + 1) * 64])
            self.cp('dve', g["etok"][:, :, :], self.V(ps, 0, 64, 0, [[6, 4], [1, 6]]))
            self.cp('dve', g["fltok"][:, :, :], self.V(ps, 0, 64, 24, [[6, 4], [1, 6]]))
            self.tt('dve', g["X2"][:, :, :], self.V(g["dec"], 0, 6, 0, [[1, 4], [0, 3]]),
                    self.V(self.CST, 0, 6, CC['PSEL'], [[0, 4], [1, 3]]), ALU.mult)
            ps2 = self.pq()
            self.mm(ps2[:, 0:12], self.CST[0:6, CC['LSEL']:CC['LSEL'] + 128], self.V(g["X2"], 0, 6, 0, [[1, 12]]))
            self.cp('dve', g["decb"][:, :, :], self.V(ps2, 0, 128, 0, [[3, 4], [1, 3]]))
        for d in sorted(dirs, reverse=True):
            g = self.mg[d]
            mask = self.CST[0:64, CC['MUI']:CC['MUI'] + 64] if d == 0 else self.CST[0:64, CC['MLI']:CC['MLI'] + 64]
            maskb = self.V(self.CST, 0, 64, CC['MUI'] if d == 0 else CC['MLI'], [[0, 6], [1, 64]])
            for j in range(4):
                c = j if d == 0 else 3 - j
                cs = slice(c * 64, (c + 1) * 64)
                stsb, vp = self.STSB[j % 2], self.VP[j % 2]
                ps = self.pq()
                for h in range(6):
                    hp, pb = h // 2, 64 * (h % 2)
                    self.mm(ps[0:64, h * 64:(h + 1) * 64], self.KT[pb:pb + 64, hp, cs], self.QT[pb:pb + 64, hp, cs], inc=(h == 5))
                self.tt('dve', stsb[:, :, :], self.V(ps, 0, 64, 0, [[64, 6], [1, 64]]), maskb, ALU.mult)
                self.tt('dve', vp[:, :, :], self.VAUG[:, c, :, :], self.V(g["etok"], 0, 64, c * 6, [[1, 6], [0, 65]]), ALU.mult)
                self.tt('dve', self.CDEC[:, :, :], self.mC[d][:, :, :], self.V(g["decb"], 0, 128, c * 3, [[1, 3], [0, 65]]), ALU.mult)
                self.cp('act', self.CDBF[:, :, :], self.CDEC[:, :, :])
                ph = self.pq()
                for h in range(6):
                    hp, pb = h // 2, 64 * (h % 2)
                    o = ph[0:64, h * 65:(h + 1) * 65]
                    self.mm(o, stsb[:, h, :], vp[:, h, :], start=True, stop=False)
                    self.mm(o, self.QT[pb:pb + 64, hp, cs], self.CDBF[pb:pb + 64, hp, :], start=False, stop=True, inc=(h == 5))
                pc = self.pq()
                for h in range(6):
                    hp, pb = h // 2, 64 * (h % 2)
                    self.mm(pc[pb:pb + 64, hp * 65:(hp + 1) * 65], self.KTOK[:, c, h * 64:(h + 1) * 64], vp[:, h, :], inc=(h == 5))
                self.tt('dve', self.mC[d][:, :, :], self.CDEC[:, :, :], self.V(pc, 0, 128, 0, [[65, 3], [1, 65]]), ALU.add)
                self.act(self.DN[:, :], self.V(ph, 0, 64, 64, [[65, 6]]), AF.Abs)
                self.tt('dve', self.DN[:, :], self.DN[:, :], g["fltok"][:, c, :], ALU.max)
                self.recip(self.RDN[:, :], self.DN[:, :])
                hsrc = self.V(ph, 0, 64, 0, [[65, 6], [1, 64]])
                rb = self.V(self.RDN, 0, 64, 0, [[1, 6], [0, 64]])
                hbv = self.V(self.HB, 0, 64, c * 384, [[64, 6], [1, 64]])
                if d == 1:
                    self.tt('dve', hbv, hsrc, rb, ALU.mult)
                else:
                    self.tt('dve', self.HD[:, :, :], hsrc, rb, ALU.mult)
                    self.tt('dve', hbv, hbv, self.HD[:, :, :], ALU.add)
            if last[d] and prompt:
                S.dma('sp', self.o_mC[seq_idx, l, d], self.mC[d][:, :, :])
                S.dma('sp', self.o_mm[seq_idx, l, d], self.mM[d][:, 0:1])
        if not do_f:
            S.dma('sp', self.sHB[t0:t0 + 256, :].rearrange("(c s) f -> s c f", s=64), self.HB[:, :, :])
            return
        self.tt('dve', self.SQ[:, :, :], self.HB[:, :, :], self.HB[:, :, :], ALU.mult)
        self.reduce(self.SSQ[:, :], self.V(self.SQ, 0, 64, 0, [[64, 24], [1, 64]]), ALU.add)
        self.ts('dve', self.SSQ[:, :], self.SSQ[:, :], 1.0 / 64, EPS, ALU.mult, ALU.add)
        self.act(self.SSQ[:, :], self.SSQ[:, :], AF.Sqrt)
        self.recip(self.RSTD[:, :], self.SSQ[:, :])
        self.tt('dve', self.V(self.SQ, 0, 64, 0, [[64, 24], [1, 64]]), self.V(self.HB, 0, 64, 0, [[64, 24], [1, 64]]),
                self.V(self.RSTD, 0, 64, 0, [[1, 24], [0, 64]]), ALU.mult)
        for hp in range(3):
            ps = self.pq()
            for c in range(4):
                self.tr(ps[:, c * 64:(c + 1) * 64], self.SQ[:, c, hp * 128:(hp + 1) * 128], inc=(c == 3))
            self.stt(self.MIXT[:, hp, :], ps[:, 0:256], self.ppc(l, 'M_NORM', hp), self.GOZ[:, hp, :], ALU.mult, ALU.mult)

    def stage_lru(self, l, t0, co, W, w0, w1, T, dirs, prompt, seq_idx, first, last):
        S = self.S
        do_f = 0 in dirs
        for pr in range(2):
            self.proj_fm(3736 + pr * 128, 128, 0, W, lambda ps, pr=pr: self.cp('act', self.XL[:, pr, 2:2 + W], ps))
            if do_f:
                self.proj_fm(3992 + pr * 128, 128, co, 256, lambda ps, pr=pr: self.act(self.LZT[:, pr, :], ps, AF.Silu))
        if w1 == T:
            self.memset('dve', self.XL[:, :, 2 + W:2 + W + 1], 0.0)
        if w0 == 0:
            self.memset('dve', self.XL[:, :, 0:2], 0.0)
        for pr in range(2):
            self.ts('dve', self.XC[:, pr, :], self.XL[:, pr, co:co + 256], self.ppc(l, 'L_CONV', 0 * 2 + pr), self.ppc(l, 'L_CONVB', pr), ALU.mult, ALU.add)
            for j in range(1, 4):
                self.stt(self.XC[:, pr, :], self.XL[:, pr, co + j:co + j + 256], self.ppc(l, 'L_CONV', j * 2 + pr), self.XC[:, pr, :], ALU.mult, ALU.add)
        for d in dirs:
            if first[d]:
                if prompt:
                    self.memset('dve', self.lS[d][:, :], 0.0)
                else:
                    S.dma('sp', self.lS[d][:, :], self.st_l[:, l, d, :])
        if do_f and not (1 in dirs):
            S.dma('sp', self.HL[1][:, :, :], self.sLB[:, :, t0:t0 + 256])
        lt = self.lt
        for d in sorted(dirs, reverse=True):
            for pr in range(2):
                ps = self.pq()
                self.mm(ps[:, 0:256], self.LRUW[:, l, (0 * 2 + d) * 2 + pr, :], self.XC[:, pr, :])
                self.mm(ps[:, 256:512], self.LRUW[:, l, (1 * 2 + d) * 2 + pr, :], self.XC[:, pr, :])
                self.act(lt["rg"][:, pr, :], ps[:, 0:256], AF.Sigmoid, bias=self.ppc(l, 'L_BA', d * 2 + pr))
                self.act(lt["ig"][:, pr, :], ps[:, 256:512], AF.Sigmoid, bias=self.ppc(l, 'L_BX', d * 2 + pr))
                self.act(lt["aa"][:, pr, :], lt["rg"][:, pr, :], AF.Exp, scale=self.dvc(l, 'CLAM', d * 2 + pr))
                self.act(lt["a2"][:, pr, :], lt["rg"][:, pr, :], AF.Exp, scale=self.dvc(l, 'C2LAM', d * 2 + pr))
                self.ts('dve', lt["a2"][:, pr, :], lt["a2"][:, pr, :], -1.0, 1.0, ALU.mult, ALU.add)
                self.act(lt["a2"][:, pr, :], lt["a2"][:, pr, :], AF.Sqrt)
                self.tt('dve', lt["bt"][:, pr, :], lt["a2"][:, pr, :], lt["ig"][:, pr, :], ALU.mult)
                self.tt('dve', lt["bt"][:, pr, :], lt["bt"][:, pr, :], self.XC[:, pr, :], ALU.mult)
                if d == 0:
                    self.scan(self.HL[0][:, pr, :], lt["aa"][:, pr, :], lt["bt"][:, pr, :], self.lS[0][:, pr:pr + 1], ALU.mult, ALU.add)
                    self.cp('dve', self.lS[0][:, pr:pr + 1], self.HL[0][:, pr, 255:256])
                else:
                    rv = lambda t: self.V(t, 0, 128, pr * 256 + 255, [[-1, 256]])
                    self.scan(rv(self.HL[1]), rv(lt["aa"]), rv(lt["bt"]), self.lS[1][:, pr:pr + 1], ALU.mult, ALU.add)
                    self.cp('dve', self.lS[1][:, pr:pr + 1], self.HL[1][:, pr, 0:1])
            if last[d] and prompt:
                S.dma('sp', self.o_l[seq_idx, l, d], self.lS[d][:, :])
        if not do_f:
            S.dma('sp', self.sLB[:, :, t0:t0 + 256], self.HL[1][:, :, :])
            return
        self.tt('dve', self.HL[0][:, :, :], self.HL[0][:, :, :], self.HL[1][:, :, :], ALU.add)
        self.tt('dve', self.MIXT[:, 6:8, :], self.HL[0][:, :, :], self.LZT[:, :, :], ALU.mult)

    def stage_rwkv(self, l, t0, co, W, w0, w1, T, dirs, grid, prompt, seq_idx, first, last):
        S = self.S
        do_f = 0 in dirs
        for ch in range(11):
            self.proj_fm(1944 + ch * 128, 128, 0, W, lambda ps, ch=ch: self.cp('act' if ch % 2 else 'dve', self.URS[:, ch, 0:W], ps))
        if do_f:
            for hp in range(3):
                self.proj_fm(3352 + hp * 128, 128, co, 256, lambda ps, hp=hp: self.act(self.RZT[:, hp, :], ps, AF.Silu))
        U3 = lambda off, n: self.V(self.URS, 0, 128, off, [[384, 11], [1, n]])
        B3 = lambda off, n: self.V(self.BLK, 0, 128, off, [[256, 11], [1, n]])
        if not grid:
            self.cp('dve', B3(1, 255), U3(0, 255))
            self.memset('dve', B3(0, 1), 0.0)
            self.tt('dve', B3(0, 255), B3(0, 255), U3(1, 255), ALU.add)
            wsh = 0.5
        else:
            U4 = lambda off, r, n: self.V(self.URS, 0, 128, off, [[384, 11], [64, r], [1, n]])
            B4 = lambda off, r, n: self.V(self.BLK, 0, 128, off, [[256, 11], [64, r], [1, n]])
            self.cp('dve', B4(1, 4, 63), U4(co, 4, 63))
            self.memset('dve', B4(0, 4, 1), 0.0)
            self.tt('dve', B4(0, 4, 63), B4(0, 4, 63), U4(co + 1, 4, 63), ALU.add)
            if t0 > 0:
                self.tt('dve', B3(0, 256), B3(0, 256), U3(co - 64, 256), ALU.add)
            else:
                self.tt('dve', B3(64, 192), B3(64, 192), U3(0, 192), ALU.add)
            if t0 + TB < T:
                self.tt('dve', B3(0, 256), B3(0, 256), U3(co + 64, 256), ALU.add)
            else:
                self.tt('dve', B3(0, 192), B3(0, 192), U3(co + 64, 192), ALU.add)
            wsh = 0.25
        mu = self.V(self.PP, 0, 128, l * PL + PC['R_MU'], [[1, 11], [0, 256]])
        self.tt('dve', B3(0, 256), B3(0, 256), mu, ALU.mult)
        omm = self.V(self.DV, 0, 128, l * DL + DC['OMMU'], [[1, 11], [0, 256]])
        self.tt('dve', U3(co, 256), U3(co, 256), omm, ALU.mult)
        self.stt(B3(0, 256), B3(0, 256), wsh, U3(co, 256), ALU.mult, ALU.add)
        for nm, ch in (('blk_r', 0), ('blk_k', 3), ('blk_v', 6), ('blk_wl', 9), ('blk_al', 10)):
            self.dump(nm, self.BLK[:, ch, :])
        rt = self.rt
        kk = self.V(self.PP, 0, 128, l * PL + PC['R_KK'], [[1, 3], [0, 256]])
        kap = rt["d"]
        self.tt('dve', kap[:, :, :], self.BLK[:, 3:6, :], kk, ALU.mult)
        ksq = rt["E"]
        self.tt('dve', ksq[:, :, :], kap[:, :, :], kap[:, :, :], ALU.mult)
        for hp in range(3):
            self.mm(self.PA[:, hp * 256:(hp + 1) * 256], self.CST[:, CC['BONES']:CC['BONES'] + 128], ksq[:, hp, :])
        self.act(ksq[:, :, :], self.V(self.PA, 0, 128, 0, [[256, 3], [1, 256]]), AF.Sqrt)
        self.ts('dve', ksq[:, :, :], ksq[:, :, :], 1e-12, None, ALU.max)
        self.recip(ksq[:, :, :], ksq[:, :, :])
        self.tt('dve', self.KH[:, :, :], kap[:, :, :], ksq[:, :, :], ALU.mult)
        self.dump('kh', self.KH[:, 0, :])
        for half in range(2):
            for cc_ in range(2):
                c = half * 2 + cc_
                for hp in range(3):
                    self.tr(self.PB[0:64, cc_ * 384 + hp * 128: cc_ * 384 + (hp + 1) * 128], self.BLK[:, 6 + hp, c * 64:(c + 1) * 64])
            self.cp('act', self.VTOK[:, half * 2:half * 2 + 2, :], self.V(self.PB, 0, 64, 0, [[384, 2], [1, 384]]))
        for d in dirs:
            if first[d]:
                if prompt:
                    self.memset('dve', self.rH[d][:, :, :], 0.0)
                else:
                    S.dma('sp', self.rH[d][:, :, :], self.st_rH[l, d])
        if do_f and not (1 in dirs):
            S.dma('sp', self.YB[:, :, :], self.sYB[t0:t0 + 256, :].rearrange("(c s) f -> s c f", s=64))
        for d in sorted(dirs, reverse=True):
            self.rwkv_dir(l, d, seq_idx, prompt, last)
        if not do_f:
            S.dma('sp', self.sYB[t0:t0 + 256, :].rearrange("(c s) f -> s c f", s=64), self.YB[:, :, :])
            return
        self.tt('dve', self.RSQ[:, :, :], self.YB[:, :, :], self.YB[:, :, :], ALU.mult)
        self.reduce(self.rSSQ[:, :], self.V(self.RSQ, 0, 64, 0, [[64, 24], [1, 64]]), ALU.add)
        self.ts('dve', self.rSSQ[:, :], self.rSSQ[:, :], 1.0 / 64, EPS, ALU.mult, ALU.add)
        self.act(self.rSSQ[:, :], self.rSSQ[:, :], AF.Sqrt)
        self.recip(self.rRSTD[:, :], self.rSSQ[:, :])
        self.tt('dve', self.V(self.RSQ, 0, 64, 0, [[64, 24], [1, 64]]), self.V(self.YB, 0, 64, 0, [[64, 24], [1, 64]]),
                self.V(self.rRSTD, 0, 64, 0, [[1, 24], [0, 64]]), ALU.mult)
        rk = rt["kt"]
        self.tt('dve', rk[:, :, :], self.BLK[:, 0:3, :], self.BLK[:, 3:6, :], ALU.mult)
        for hp in range(3):
            self.mm(self.PA[:, hp * 256:(hp + 1) * 256], self.RKD[:, l, hp, :], rk[:, hp, :])
        bon = rt["bb"]
        self.tt('dve', bon[:, :, :], self.V(self.PA, 0, 128, 0, [[256, 3], [1, 256]]), self.BLK[:, 6:9, :], ALU.mult)
        for hp in range(3):
            ps = self.pq()
            for c in range(4):
                self.tr(ps[:, c * 64:(c + 1) * 64], self.RSQ[:, c, hp * 128:(hp + 1) * 128], inc=(c == 3))
            t1 = rt["cs"]
            self.stt(t1[:, hp, :], ps[:, 0:256], self.ppc(l, 'R_NORM', hp), bon[:, hp, :], ALU.mult, ALU.add)
            self.tt('dve', self.MIXT[:, 3 + hp, :], t1[:, hp, :], self.RZT[:, hp, :], ALU.mult)

    def rwkv_dir(self, l, d, seq_idx, prompt, last):
        S = self.S
        rt = self.rt
        pb_d = 64 * d
        f3 = lambda t: t[:, :, :]
        v4 = lambda t: self.V(t, 0, 128, 0, [[256, 3], [64, 4], [1, 64]])
        self.act(self.TWL[pb_d:pb_d + 64, :], self.BLK[pb_d:pb_d + 64, 9, :], AF.Tanh)
        for hp in range(3):
            self.mm(self.PA[:, hp * 256:(hp + 1) * 256], self.LORA[pb_d:pb_d + 64, l, 0, hp * 128:(hp + 1) * 128], self.TWL[pb_d:pb_d + 64, :])
        for hp in range(3):
            self.act(rt["sg"][:, hp, :], self.PA[:, hp * 256:(hp + 1) * 256], AF.Sigmoid, bias=self.ppc(l, 'R_W0', d * 3 + hp))
        for hp in range(3):
            self.mm(self.PB[:, hp * 256:(hp + 1) * 256], self.LORA[pb_d:pb_d + 64, l, 1, hp * 128:(hp + 1) * 128], self.BLK[pb_d:pb_d + 64, 10, :])
        for hp in range(3):
            self.act(rt["aa"][:, hp, :], self.PB[:, hp * 256:(hp + 1) * 256], AF.Sigmoid, bias=self.ppc(l, 'R_A0', d * 3 + hp))
        for hp in range(3):
            self.ts('dve', rt["kt"][:, hp, :], rt["aa"][:, hp, :], self.ppc(l, 'R_KA', hp), self.dvc(l, 'OMKA', hp), ALU.mult, ALU.add)
        self.tt('dve', f3(rt["kt"]), f3(rt["kt"]), self.BLK[:, 3:6, :], ALU.mult)
        self.tt('dve', f3(rt["bb"]), self.KH[:, :, :], f3(rt["aa"]), ALU.mult)
        self.dump('sg%d' % d, rt["sg"][:, 0, :])
        self.dump('aa%d' % d, rt["aa"][:, 0, :])
        self.dump('kt%d' % d, rt["kt"][:, 0, :])
        self.dump('bb%d' % d, rt["bb"][:, 0, :])
        flat = lambda t: self.V(t, 0, 128, 0, [[1, 768]])
        self.scan(flat(rt["cs"]), self.CST[:, CC['RMASK']:CC['RMASK'] + 768], flat(rt["sg"]), 0.0, ALU.mult, ALU.add)
        if d == 1:
            csLb = self.V(rt["cs"], 0, 128, 63, [[256, 3], [64, 4], [0, 64]])
            self.tt('dve', v4(rt["d"]), csLb, v4(rt["cs"]), ALU.subtract)
            self.cp('dve', self.GL[:, :, :], self.V(rt["cs"], 0, 128, 63, [[256, 3], [64, 4]]))
            self.tt('dve', f3(rt["cs"]), f3(rt["d"]), f3(rt["sg"]), ALU.add)
        else:
            self.cp('dve', self.GL[:, :, :], self.V(rt["cs"], 0, 128, 63, [[256, 3], [64, 4]]))
        GLb = self.V(self.GL, 0, 128, 0, [[4, 3], [1, 4], [0, 64]])
        self.act(f3(rt["E"]), f3(rt["cs"]), AF.Exp, scale=-DSC)
        self.tt('dve', self.V(self.KR, 0, 128, 64, [[512, 3], [128, 4], [1, 64]]), self.V(self.BLK, 0, 128, 0, [[256, 3], [64, 4], [1, 64]]), v4(rt["E"]), ALU.mult)
        self.tt('dve', f3(rt["d"]), f3(rt["cs"]), f3(rt["sg"]), ALU.subtract)
        self.act(f3(rt["E"]), f3(rt["d"]), AF.Exp, scale=-DSC)
        self.tt('dve', self.V(self.KR, 0, 128, 0, [[512, 3], [128, 4], [1, 64]]), v4(self.KH), v4(rt["E"]), ALU.mult)
        self.act(f3(rt["E"]), f3(rt["cs"]), AF.Exp, scale=DSC)
        self.tt('dve', self.BTT[:, :, :], f3(rt["bb"]), f3(rt["E"]), ALU.mult)
        self.tt('dve', self.KTT[:, :, :], f3(rt["kt"]), f3(rt["E"]), ALU.mult)
        self.tt('dve', v4(rt["d"]), GLb, v4(rt["cs"]), ALU.subtract)
        self.act(f3(rt["E"]), f3(rt["d"]), AF.Exp, scale=-DSC)
        self.tt('dve', f3(rt["sg"]), f3(rt["kt"]), f3(rt["E"]), ALU.mult)
        self.tt('dve', f3(rt["aa"]), f3(rt["bb"]), f3(rt["E"]), ALU.mult)
        self.act(self.GL[:, :, :], self.GL[:, :, :], AF.Exp, scale=-DSC)
        for (src, dst, neg) in ((rt["sg"], self.KTTOK, False), (rt["aa"], self.BTTOK, True)):
            for half in range(2):
                for cc_ in range(2):
                    c = half * 2 + cc_
                    for hp in range(3):
                        self.tr(self.PB[0:64, cc_ * 384 + hp * 128: cc_ * 384 + (hp + 1) * 128], src[:, hp, c * 64:(c + 1) * 64])
                o = dst[:, half * 2:half * 2 + 2, :]
                i_ = self.V(self.PB, 0, 64, 0, [[384, 2], [1, 384]])
                if neg:
                    self.act(o, i_, AF.Copy, scale=-1.0)
                else:
                    self.cp('dve', o, i_)
        self.cp('act', self.HBF[:, :, :], self.rH[d][:, :, :])
        self.dump('cs%d' % d, rt["cs"][:, 0, :])
        self.dump('gl%d' % d, self.GL[:, :, :])
        self.dump('kr%d' % d, self.V(self.KR, 0, 128, 0, [[1, 512]]))
        self.dump('btt%d' % d, self.BTT[:, 0, :])
        self.dump('ktt%d' % d, self.KTT[:, 0, :])
        self.dump('kttok%d' % d, self.KTTOK[:, 0, :])
        self.dump('bttok%d' % d, self.BTTOK[:, 0, :])
        self.dump('vtok', self.VTOK[:, 0, :])
        mk = CC['MKF'] if d == 0 else CC['MKB']
        nmk = CC['NMKF'] if d == 0 else CC['NMKB']
        mn = CC['MNF'] if d == 0 else CC['MNB']
        MK = self.V(self.CST, 0, 64, mk, [[0, 6], [1, 128]])
        NMK = self.V(self.CST, 0, 64, nmk, [[0, 6], [1, 128]])
        MN = self.V(self.CST, 0, 64, mn, [[0, 6], [1, 64]])
        P6 = lambda t, w: self.V(t, 0, 64, 0, [[w, 6], [1, w]])
        for j in range(4):
            c = j if d == 0 else 3 - j
            cs = slice(c * 64, (c + 1) * 64)
            hd = lambda h: (h // 2, 64 * (h % 2))
            KRc = lambda h, which: self.V(self.KR, hd(h)[1], 64, hd(h)[0] * 512 + c * 128 + which * 64, [[1, 64]])
            KRb = lambda h: self.V(self.KR, hd(h)[1], 64, hd(h)[0] * 512 + c * 128, [[1, 128]])
            for h in range(6):
                hp, pb = hd(h)
                self.mm(self.PA[0:64, h * 128:(h + 1) * 128], self.KTT[pb:pb + 64, hp, cs], KRb(h), inc=(h == 5))
            for h in range(6):
                hp, pb = hd(h)
                self.mm(self.PB[0:64, h * 128:(h + 1) * 128], self.BTT[pb:pb + 64, hp, cs], KRb(h), inc=(h == 5))
            p3 = self.pq()
            for h in range(6):
                hp, pb = hd(h)
                self.mm(p3[0:64, h * 64:(h + 1) * 64], KRc(h, 0), self.BTT[pb:pb + 64, hp, cs], inc=(h == 5))
            self.tt('dve', self.A1[:, :, :], P6(self.PA, 128), MK, ALU.mult)
            self.tt('dve', self.A2[:, :, :], P6(self.PB, 128), NMK, ALU.mult)
            self.tt('dve', self.NN[0][:, :, :], P6(p3, 64), MN, ALU.mult)
            if j == 0:
                self.dump('A1_%d' % d, self.V(self.A1, 0, 64, 0, [[1, 768]]))
                self.dump('A2_%d' % d, self.V(self.A2, 0, 64, 0, [[1, 768]]))
                self.dump('N0_%d' % d, self.V(self.NN[0], 0, 64, 0, [[1, 384]]))
            pr_ = self.pq()
            for h in range(6):
                hp, pb = hd(h)
                o = pr_[0:64, h * 64:(h + 1) * 64]
                self.mm(o, KRc(h, 0), self.HBF[pb:pb + 64, hp, :], start=True, stop=False)
                self.mm(o, self.A1[:, h, 0:64], self.VTOK[:, c, h * 64:(h + 1) * 64], start=False, stop=True, inc=(h == 5))
            self.cp('dve', self.U[:, :, :], P6(pr_, 64))
            self.cp('act', self.UBF[:, :, :], self.U[:, :, :])
            if j == 0:
                self.dump('U0_%d' % d, self.V(self.U, 0, 64, 0, [[1, 384]]))
            for k in range(6):
                NTk = self.A2[:, :, 0:64] if k == 0 else self.NTT[k % 2][:, :, :]
                Nk = self.NN[k % 2]
                pu = self.pq()
                for h in range(6):
                    self.mm(pu[0:64, h * 64:(h + 1) * 64], NTk[:, h, :], self.UBF[:, h, :], inc=(h == 5))
                if k < 5:
                    pn = self.pq()
                    for h in range(6):
                        self.mm(pn[0:64, h * 64:(h + 1) * 64], NTk[:, h, :], Nk[:, h, :], inc=(h == 5))
                    pnt = self.pq()
                    for h in range(6):
                        self.mm(pnt[0:64, h * 64:(h + 1) * 64], Nk[:, h, :], NTk[:, h, :], inc=(h == 5))
                self.tt('dve', self.U[:, :, :], self.U[:, :, :], P6(pu, 64), ALU.add)
                self.cp('act', self.UBF[:, :, :], self.U[:, :, :])
                if k < 5:
                    self.cp('act', self.NN[(k + 1) % 2][:, :, :], P6(pn, 64))
                    self.cp('dve', self.NTT[(k + 1) % 2][:, :, :], P6(pnt, 64))
            if j == 0:
                self.dump('U6_%d' % d, self.V(self.U, 0, 64, 0, [[1, 384]]))
            py = self.pq()
            for h in range(6):
                hp, pb = hd(h)
                o = py[0:64, h * 64:(h + 1) * 64]
                self.mm(o, KRc(h, 1), self.HBF[pb:pb + 64, hp, :], start=True, stop=False)
                self.mm(o, self.A1[:, h, 64:128], self.VTOK[:, c, h * 64:(h + 1) * 64], start=False, stop=False)
                self.mm(o, self.A2[:, h, 64:128], self.UBF[:, h, :], start=False, stop=True, inc=(h == 5))
            ybv = self.V(self.YB, 0, 64, c * 384, [[1, 384]])
            if d == 1:
                self.cp('act', ybv, py[0:64, 0:384])
            else:
                self.tt('dve', ybv, ybv, py[0:64, 0:384], ALU.add)
            ph = self.pq()
            for h in range(6):
                hp, pb = hd(h)
                o = ph[pb:pb + 64, hp * 64:(hp + 1) * 64]
                self.mm(o, self.KTTOK[:, c, h * 64:(h + 1) * 64], self.VTOK[:, c, h * 64:(h + 1) * 64], start=True, stop=False)
                self.mm(o, self.BTTOK[:, c, h * 64:(h + 1) * 64], self.UBF[:, h, :], start=False, stop=True, inc=(h == 5))
            self.tt('dve', self.HTMP[:, :, :], self.rH[d][:, :, :], self.V(self.GL, 0, 128, c, [[4, 3], [0, 64]]), ALU.mult)
            self.tt('dve', self.rH[d][:, :, :], self.HTMP[:, :, :], self.V(ph, 0, 128, 0, [[64, 3], [1, 64]]), ALU.add)
            self.cp('act', self.HBF[:, :, :], self.rH[d][:, :, :])
            if j == 0:
                self.dump('y_%d' % d, self.V(self.YB, 0, 64, c * 384, [[1, 384]]))
                self.dump('H1_%d' % d, self.V(self.rH[d], 0, 128, 0, [[1, 192]]))
        if last[d] and prompt:
            S.dma('sp', self.o_rH[seq_idx, l, d], self.rH[d][:, :, :])

    def stage_out(self, l, mod, xsrc, xdst, r0):
        S = self.S
        for tt_ in range(2):
            o, xw = self.XN[tt_], self.XW[tt_]
            S.dma('sp', xw[:, :], xsrc[r0 + tt_ * 128: r0 + (tt_ + 1) * 128, :])
            for ch in range(2):
                ps = self.pq()
                for kc in range(8):
                    self.mm(ps[:, :], self.MIXT[:, kc, tt_ * 128:(tt_ + 1) * 128], self.W_OUT[:, kc, ch * 512:(ch + 1) * 512],
                            start=(kc == 0), stop=(kc == 7))
                self.cp('act' if ch else 'dve', o[:, ch * 512:(ch + 1) * 512], ps[:, :])
            self.dump('o_proj%d' % tt_, o[:, :])
            self.dump('o_x%d' % tt_, xw[:, :])
            ssq = self.TMPS[:, 24 + tt_:25 + tt_]
            junk = self.V(self.BLK, 0, 128, 0, [[1, D]])
            self.act(junk, o[:, :], AF.Square, accum=ssq)
            rs = self.TMPS[:, 26 + tt_:27 + tt_]
            self.ts('dve', rs, ssq, 1.0 / D, EPS, ALU.mult, ALU.add)
            self.act(rs, rs, AF.Sqrt)
            self.recip(rs, rs)
            self.dump('o_rs%d' % tt_, rs)
            self.stt(o[:, :], o[:, :], rs, self.GATEB[:, mod, :], ALU.mult, ALU.mult)
            self.dump('o_g%d' % tt_, o[:, :])
            self.tt('dve', o[:, :], o[:, :], xw[:, :], ALU.add)
            S.dma('sp', xdst[r0 + tt_ * 128: r0 + (tt_ + 1) * 128, :], o[:, :])

    def build(self, layers=(0, 1)):
        try:
            self._build(layers)
        except StopIteration:
            pass
        self.S.finish('sp')
        return self.nc

    def _build(self, layers):
        NP, TS = self.NP, self.TS
        self.setup()
        if self.stop == 'setup':
            raise StopIteration
        for li, l in enumerate(layers):
            self.load_layer(l)
            if self.stop == 'load':
                raise StopIteration
            xsrc = self.x_in if li == 0 else self.x1
            xdst = self.y_out if li == len(layers) - 1 else self.x1
            T_, F_ = {0: True, 1: True}, {0: False, 1: False}
            for s in range(NP):
                self.visit(l, 0, xsrc, xdst, s * 256, 256, 0, [0, 1], False, s, T_, T_)
            if TS > 0:
                nb = TS // TB
                row0 = NP * 256
                for b in range(nb - 1, -1, -1):
                    self.visit(l, 1, xsrc, xdst, row0, TS, b * TB, [1], True, 0,
                               {0: False, 1: b == nb - 1}, {0: False, 1: b == 0})
                for b in range(nb):
                    self.visit(l, 1, xsrc, xdst, row0, TS, b * TB, [0], True, 0,
                               {0: b == 0, 1: False}, {0: b == nb - 1, 1: False})


def prep_shared(inp):
    f = lambda a: np.ascontiguousarray(np.asarray(a, dtype=np.float32))
    b_mod = f(inp['b_mod'])
    sh = {}
    sh['w_mod'] = f(inp['w_mod'])
    sh['w_in'] = f(inp['w_in'])
    sh['w_out'] = f(inp['w_out'])
    sh['bmodT'] = f(b_mod[:, :2048].reshape(2, 16, 128).transpose(2, 0, 1))
    sh['bmodg'] = f(b_mod[:, 2048:3072])
    sh['gpost'] = f(inp['g_post'])
    pp = np.zeros((128, 2 * PL), np.float32)
    for l in range(2):
        o = l * PL
        def put(key, arr, n):
            pp[:, o + PC[key]: o + PC[key] + n] = np.asarray(arr, np.float32).reshape(n, 128).T
        put('G_PRE', inp['g_pre'][l], 8)
        put('M_NORM', inp['m_norm'][l], 3)
        put('R_MU', inp['r_mu'][l], 11)
        put('R_W0', np.asarray(inp['r_w0'][l]).reshape(-1), 6)
        put('R_A0', np.asarray(inp['r_a0'][l]).reshape(-1), 6)
        put('R_KK', inp['r_kk'][l], 3)
        put('R_KA', inp['r_ka'][l], 3)
        put('R_RK', inp['r_rk'][l], 3)
        put('R_NORM', inp['r_norm'][l], 3)
        put('L_CONV', np.asarray(inp['l_conv'][l]).reshape(-1), 8)
        put('L_CONVB', inp['l_conv_b'][l], 2)
        put('L_BA', np.asarray(inp['l_ba'][l]).reshape(-1), 4)
        put('L_BX', np.asarray(inp['l_bx'][l]).reshape(-1), 4)
        put('L_LAM', np.asarray(inp['l_lambda'][l]).reshape(-1), 4)
        pp[0:6, o + PC['M_BI']: o + PC['M_BI'] + 2] = np.asarray(inp['m_bi'][l], np.float32).T
        pp[0:6, o + PC['M_BF']: o + PC['M_BF'] + 2] = np.asarray(inp['m_bf'][l], np.float32).T
    sh['pp'] = pp
    sh['cst'] = make_consts()
    lora = np.zeros((128, 2, 2, 384), np.float32)
    for wi, key in enumerate(['r_w2', 'r_a2']):
        a = np.asarray(inp[key], np.float32)
        lora[:, :, wi, :] = a.transpose(1, 2, 0, 3).reshape(128, 2, 384)
    sh['lora'] = lora
    lruw = np.zeros((128, 2, 8, 128), np.float32)
    for gi, key in enumerate(['l_wa', 'l_wx']):
        a = np.asarray(inp[key], np.float32)
        for l in range(2):
            for d in range(2):
                for pr in range(2):
                    for hb in range(2):
                        n = 2 * pr + hb
                        lruw[hb * 64:(hb + 1) * 64, l, (gi * 2 + d) * 2 + pr, hb * 64:(hb + 1) * 64] = a[l, d, n]
    sh['lruw'] = lruw
    return sh


def prep_core(inp, b, NP, TS):
    f = lambda a: np.ascontiguousarray(np.asarray(a, dtype=np.float32))
    m = {}
    xp = np.asarray(inp['x_prompt'], np.float32)[b * NP:(b + 1) * NP].reshape(NP * 256, D)
    if TS > 0:
        xs = np.asarray(inp['x_sample'], np.float32)[b]
        m['x_in'] = f(np.concatenate([xp, xs], 0))
    else:
        m['x_in'] = f(xp)
    cc = np.stack([np.asarray(inp['c_ctx'], np.float32), np.asarray(inp['c'], np.float32)[b]], -1)
    m['cc'] = f(cc.reshape(8, 128, 2).transpose(1, 0, 2))
    C = np.asarray(inp['state_mlstm_C'], np.float32)[b]
    n = np.asarray(inp['state_mlstm_n'], np.float32)[b]
    Cn = np.concatenate([C, n[..., None]], -1)
    Cn = Cn.reshape(2, 2, 3, 2, 64, 65).transpose(0, 1, 3, 4, 2, 5).reshape(2, 2, 128, 3, 65)
    m['st_mC'] = f(Cn)
    m['st_mm'] = f(np.asarray(inp['state_mlstm_m'], np.float32)[b].transpose(2, 0, 1))
    R = np.asarray(inp['state_rwkv'], np.float32)[b]
    R = R.transpose(0, 1, 2, 4, 3)
    R = R.reshape(2, 2, 3, 2, 64, 64).transpose(0, 1, 3, 4, 2, 5).reshape(2, 2, 128, 3, 64)
    m['st_rH'] = f(R)
    L = np.asarray(inp['state_rglru'], np.float32)[b]
    m['st_l'] = f(L.reshape(2, 2, 2, 128).transpose(3, 0, 1, 2))
    return m


def unpack_core(r, NP, TS):
    y = r['y_out']
    yp = y[:NP * 256].reshape(NP, 256, D)
    ys = y[NP * 256:]
    mC = r['o_mC'].reshape(NP, 2, 2, 2, 64, 3, 65).transpose(0, 1, 2, 5, 3, 4, 6).reshape(NP, 2, 2, 6, 64, 65)
    newC = np.ascontiguousarray(mC[..., :64])
    newn = np.ascontiguousarray(mC[..., 64])
    newm = r['o_mm'].reshape(NP, 2, 2, 6)
    rH = r['o_rH'].reshape(NP, 2, 2, 2, 64, 3, 64).transpose(0, 1, 2, 5, 3, 4, 6).reshape(NP, 2, 2, 6, 64, 64)
    newr = np.ascontiguousarray(rH.transpose(0, 1, 2, 3, 5, 4))
    newl = np.ascontiguousarray(r['o_l'].transpose(0, 1, 2, 4, 3).reshape(NP, 2, 2, 256))
    return yp, ys, newC, newn, newm, newr, newl


_NC_CACHE = {}


def kernel(**inputs):
    NP, TS = 4, 2048
    key = (NP, TS)
    if key not in _NC_CACHE:
        _NC_CACHE[key] = Builder(NP, TS).build()
    nc = _NC_CACHE[key]
    sh = prep_shared(inputs)
    in_maps = []
    for b in range(NCORES):
        m = dict(sh)
        m.update(prep_core(inputs, b, NP, TS))
        in_maps.append(m)
    res = run_bass_kernel_spmd(nc, in_maps, core_ids=list(range(NCORES)))
    outs = [unpack_core(r, NP, TS) for r in res.results]
    y_prompt = np.concatenate([o[0] for o in outs], 0)
    y_sample = np.stack([o[1] for o in outs], 0)
    cat = lambda i: np.concatenate([o[i] for o in outs], 0)
    return (y_prompt.astype(np.float32), y_sample.astype(np.float32), cat(2).astype(np.float32),
            cat(3).astype(np.float32), cat(4).astype(np.float32), cat(5).astype(np.float32), cat(6).astype(np.float32))
```

```python
import numpy as np
import concourse.bass as bass
import concourse.mybir as mybir
from concourse.bass_utils import run_bass_kernel_spmd

F32 = mybir.dt.float32
BF16 = mybir.dt.bfloat16
AF = mybir.ActivationFunctionType
ALU = mybir.AluOpType
AX = mybir.AxisListType

D = 1024
IN_COLS = 4248
EPS = 1e-6
DSC = 0.6065306597126334
TB = 256
NCORES = 8


def _prod(xs):
    r = 1
    for x in xs:
        r *= int(x)
    return r


class Sync:
    def __init__(self, nc, n_dma_sems=32):
        self.nc = nc
        self.engs = {'pe': nc.tensor, 'dve': nc.vector, 'act': nc.scalar,
                     'pool': nc.gpsimd, 'sp': nc.sync}
        self.sem = {}
        self.cnt = {}
        for e in ['pe', 'dve', 'act', 'pool']:
            self.sem[e] = nc.alloc_semaphore('sem_' + e)
            self.cnt[e] = 0
        self.seen = {e: {} for e in self.engs}
        self.dma_ring = [nc.alloc_semaphore('dq_%d' % i) for i in range(n_dma_sems)]
        self.dma_uses = [0] * n_dma_sems
        self.dma_next = 0
        self.rec = {}
        self.untracked = set()
        self.n_wait = 0
        self.n_ins = 0
        self.pstep_cache = {}
        self.sb_addr = {}

    def region(self, ap):
        t = ap.tensor
        name = t.name
        apl = [(int(s), int(c)) for (s, c) in ap.ap]
        off = int(ap.offset)
        if type(t).__name__.startswith('DRam'):
            lo = off + sum(min(0, s * (c - 1)) for s, c in apl)
            hi = off + sum(max(0, s * (c - 1)) for s, c in apl) + 1
            return (name, 0, 1, lo, hi)
        pstep = self.pstep_cache.get(name)
        if pstep is None:
            pstep = _prod(list(t.shape)[1:])
            self.pstep_cache[name] = pstep
        p0 = off // pstep
        f0 = off % pstep
        npart = apl[0][1]
        rest = apl[1:]
        lo = f0 + sum(min(0, s * (c - 1)) for s, c in rest)
        hi = f0 + sum(max(0, s * (c - 1)) for s, c in rest) + 1
        if name in self.sb_addr:
            base, es = self.sb_addr[name]
            return ('SB', p0, p0 + npart, base + lo * es, base + hi * es)
        return ('PS:' + name, (p0 // 32) * 32, ((p0 + npart + 31) // 32) * 32, (lo // 512) * 512, ((hi + 511) // 512) * 512)

    @staticmethod
    def _ovl(a, b):
        return a[1] < b[2] and b[1] < a[2] and a[3] < b[4] and b[3] < a[4]

    @staticmethod
    def _contains(a, b):
        return a[1] <= b[1] and b[2] <= a[2] and a[3] <= b[3] and b[4] <= a[4]

    def _collect(self, e, reads, writes):
        deps = {}
        own = self.sem.get(e)
        rregs = [self.region(a) for a in reads]
        wregs = [self.region(a) for a in writes]
        for r in rregs:
            if r[0] in self.untracked:
                continue
            isps = r[0].startswith('PS:')
            for (reg, kind, sem, val) in self.rec.get(r[0], ()):
                if (kind == 'w' or (isps and sem is not own)) and self._ovl(reg, r):
                    if e == 'pe' and sem is own:
                        continue
                    k = id(sem)
                    if deps.get(k, (None, 0))[1] < val:
                        deps[k] = (sem, val)
        for w in wregs:
            if w[0] in self.untracked:
                continue
            for (reg, kind, sem, val) in self.rec.get(w[0], ()):
                if self._ovl(reg, w):
                    if sem is own:
                        continue
                    k = id(sem)
                    if deps.get(k, (None, 0))[1] < val:
                        deps[k] = (sem, val)
        return deps, rregs, wregs

    def _record(self, rregs, wregs, sem, val):
        for r in rregs:
            if r[0] in self.untracked:
                continue
            lst = self.rec.setdefault(r[0], [])
            lst[:] = [x for x in lst if not (x[1] == 'r' and x[2] is sem and self._contains(r, x[0]))]
            lst.append((r, 'r', sem, val))
        for w in wregs:
            if w[0] in self.untracked:
                continue
            lst = self.rec.setdefault(w[0], [])
            lst[:] = [x for x in lst if not self._contains(w, x[0])]
            lst.append((w, 'w', sem, val))

    def wait(self, e, sem, val):
        k = id(sem)
        if self.seen[e].get(k, 0) >= val:
            return
        self.engs[e].wait_ge(sem, val)
        self.seen[e][k] = val
        self.n_wait += 1

    max_ins = None
    paranoid = False

    def emit(self, e, reads, writes, build, inc=True):
        if self.max_ins is not None and self.n_ins >= self.max_ins:
            raise StopIteration
        deps, rregs, wregs = self._collect(e, reads, writes)
        for (sem, val) in deps.values():
            self.wait(e, sem, val)
        if self.paranoid:
            for e2 in ['pe', 'dve', 'act', 'pool']:
                if self.cnt[e2] > 0 and not (e == 'pe' and e2 == 'pe'):
                    self.wait(e, self.sem[e2], self.cnt[e2])
        ins = build(self.engs[e])
        self.n_ins += 1
        if inc:
            self.cnt[e] += 1
            ins.then_inc(self.sem[e], 1)
            val = self.cnt[e]
        else:
            val = self.cnt[e] + 1
        self._record(rregs, wregs, self.sem[e], val)
        return ins

    def dma(self, q, out, in_, **kw):
        if self.max_ins is not None and self.n_ins >= self.max_ins:
            raise StopIteration
        i = self.dma_next
        self.dma_next = (i + 1) % len(self.dma_ring)
        sem = self.dma_ring[i]
        uses = self.dma_uses[i]
        if uses > 0:
            self.wait(q, sem, 16 * uses)
        deps, rregs, wregs = self._collect(q, [in_], [out])
        for (s, v) in deps.values():
            self.wait(q, s, v)
        ins = self.engs[q].dma_start(out=out, in_=in_, **kw)
        ins.then_inc(sem, 16)
        self.n_ins += 1
        self.dma_uses[i] = uses + 1
        self._record(rregs, wregs, sem, 16 * (uses + 1))
        return ins

    def finish(self, q='sp'):
        for i, sem in enumerate(self.dma_ring):
            if self.dma_uses[i] > 0:
                self.wait(q, sem, 16 * self.dma_uses[i])
        for e in ['pe', 'dve', 'act', 'pool']:
            if self.cnt[e] > 0:
                self.wait(q, self.sem[e], self.cnt[e])


class Arena:
    def __init__(self, base, size):
        self.base, self.size, self.ptr = base, size, 0

    def take(self, nbytes):
        off = (self.ptr + 31) // 32 * 32
        self.ptr = off + nbytes
        assert self.ptr <= self.size, ("arena overflow", self.ptr, self.size)
        return self.base + off


PL = 72
PC = dict(G_PRE=0, M_NORM=8, R_MU=11, R_W0=22, R_A0=28, R_KK=34, R_KA=37, R_RK=40, R_NORM=43,
          L_CONV=46, L_CONVB=54, L_BA=56, L_BX=60, L_LAM=64, M_BI=68, M_BF=70)
DL = 32
DC = dict(OMKA=0, CLAM=3, C2LAM=7, NBF=11, OMMU=16)
CC = dict(IDENT=0, BONES=128, MKF=256, MKB=384, MNF=512, MNB=576, MUI=640, MLI=704, RMASK=768,
          LSEL=1536, PSEL=1664, NMKF=1668, NMKB=1796, ONES=1924, ISTK=2052)
NCST = 2052 + 64


def make_consts():
    c = np.zeros((128, NCST), np.float32)
    c[:, 0:128] = np.eye(128)
    c[0:64, 128:192] = 1.0
    c[64:128, 192:256] = 1.0
    s = np.arange(64)[:, None]
    t = np.arange(64)[None, :]
    us, ui = (s < t).astype(np.float32), (s <= t).astype(np.float32)
    ls, li = (s > t).astype(np.float32), (s >= t).astype(np.float32)
    c[0:64, 256:320], c[0:64, 320:384] = us, ui
    c[0:64, 384:448], c[0:64, 448:512] = ls, li
    c[0:64, 512:576] = -ls
    c[0:64, 576:640] = -us
    c[0:64, 640:704] = ui
    c[0:64, 704:768] = li
    rm = np.ones(768, np.float32)
    rm[::64] = 0.0
    c[:, 768:1536] = rm[None, :]
    for k in range(6):
        c[k, 1536 + (k % 2) * 64: 1536 + (k % 2) * 64 + 64] = 1.0
        c[k, 1664 + k // 2] = 1.0
    c[0:64, 1668:1796] = -c[0:64, 256:384]
    c[0:64, 1796:1924] = -c[0:64, 384:512]
    c[:, 1924:2052] = 1.0
    for (a, b) in ((256, 768), (1668, 1924)):
        c[64:128, a:b] = c[0:64, a:b]
    c[0:64, 2052:2116] = np.eye(64)
    c[64:128, 2052:2116] = np.eye(64)
    return c


class Builder:
    def __init__(self, NP, TS, debug=False, stop=None):
        self.stop = stop
        self.NP, self.TS = NP, TS
        self.NTOK = NP * 256 + TS
        self.debug = debug
        nc = self.nc = bass.Bass("TRN2", target_bir_lowering=False)
        self.S = Sync(nc)
        self._decl_dram()
        self._alloc()

    def _decl_dram(self):
        nc, NP, TS = self.nc, self.NP, self.TS
        di = lambda n, s: nc.dram_tensor(n, list(s), F32, kind="ExternalInput").ap()
        do = lambda n, s: nc.dram_tensor(n, list(s), F32, kind="ExternalOutput").ap()
        dx = lambda n, s: nc.dram_tensor(n, list(s), F32, kind="Internal").ap()
        self.x_in = di("x_in", [self.NTOK, D])
        self.cc = di("cc", [128, 8, 2])
        self.w_mod = di("w_mod", [2, D, 3 * D])
        self.bmodT = di("bmodT", [128, 2, 16])
        self.bmodg = di("bmodg", [2, D])
        self.gpost = di("gpost", [2, D])
        self.w_in = di("w_in", [2, D, IN_COLS])
        self.w_out = di("w_out", [2, D, D])
        self.pp = di("pp", [128, 2 * PL])
        self.cst = di("cst", [128, NCST])
        self.lora = di("lora", [128, 2, 2, 384])
        self.lruw = di("lruw", [128, 2, 8, 128])
        self.st_mC = di("st_mC", [2, 2, 128, 3, 65])
        self.st_mm = di("st_mm", [6, 2, 2])
        self.st_rH = di("st_rH", [2, 2, 128, 3, 64])
        self.st_l = di("st_l", [128, 2, 2, 2])
        for n in ["x_in", "cc", "w_mod", "bmodT", "bmodg", "gpost", "w_in", "w_out", "pp", "cst", "lora",
                  "lruw", "st_mC", "st_mm", "st_rH", "st_l"]:
            self.S.untracked.add(n)
        self.y_out = do("y_out", [self.NTOK, D])
        self.o_mC = do("o_mC", [NP, 2, 2, 128, 3, 65])
        self.o_mm = do("o_mm", [NP, 2, 2, 6, 1])
        self.o_rH = do("o_rH", [NP, 2, 2, 128, 3, 64])
        self.o_l = do("o_l", [NP, 2, 2, 128, 2])
        self.x1 = dx("x1", [self.NTOK, D])
        self.sHB = dx("sHB", [max(TS, 64), 384])
        self.sYB = dx("sYB", [max(TS // 256, 1), 128, 768])
        self.sLB = dx("sLB", [128, 2, max(TS, 64)])
        if self.debug:
            self.dbg = do("dbg", [128, 32768])
            self.dbg_map = {}
            self.dbg_off = 0

    def T(self, name, shape, dtype, arena):
        es = 2 if dtype == BF16 else 4
        nb = _prod(shape[1:]) * es
        off = arena.take(nb)
        t = self.nc.alloc_sbuf_tensor_at(name, list(shape), dtype, offset=off)
        self.S.sb_addr[t.name] = (off, es)
        return t

    def _alloc(self):
        nc = self.nc
        B0 = 16384 + 256
        LIM = 224 * 1024 - 256
        P = Arena(B0, LIM - B0)
        T = self.T
        self.W_IN = T("W_IN", [128, 8, IN_COLS], BF16, P)
        self.W_OUT = T("W_OUT", [128, 8, D], BF16, P)
        self.CST = T("CST", [128, NCST], F32, P)
        self.PP = T("PP", [128, 2 * PL], F32, P)
        self.DV = T("DV", [128, 2 * DL], F32, P)
        self.LORA = T("LORA", [128, 2, 2, 384], F32, P)
        self.LRUW = T("LRUW", [128, 2, 8, 128], F32, P)
        self.RKD = T("RKD", [128, 2, 3, 128], F32, P)
        self.GS = T("GS", [128, 2, 8], F32, P)
        self.SH = T("SH", [128, 2, 8], F32, P)
        self.GATEB = T("GATEB", [128, 2, D], F32, P)
        self.IDB = T("IDB", [128, 128], BF16, P)
        self.ISTKB = T("ISTKB", [128, 64], BF16, P)
        self.mC = [T("mC%d" % d, [128, 3, 65], F32, P) for d in range(2)]
        self.mM = [T("mM%d" % d, [6, 1], F32, P) for d in range(2)]
        self.rH = [T("rH%d" % d, [128, 3, 64], F32, P) for d in range(2)]
        self.lS = [T("lS%d" % d, [128, 2], F32, P) for d in range(2)]
        XB = Arena(P.take(16384), 16384)
        self.HT = T("HT", [128, 8, 384], BF16, P)
        self.MIXT = T("MIXT", [128, 8, 256], BF16, P)
        self.BLK = T("BLK", [128, 11, 256], F32, P)
        abase = P.take(0)
        asize = P.size - P.ptr
        self.asize = asize
        mk = lambda: Arena(abase, asize)
        a = Arena(XB.base, XB.size)
        self.XW = [T("XW%d" % i, [128, D], F32, a) for i in range(2)]
        self.XN = [T("XN%d" % i, [128, D], F32, a) for i in range(2)]
        a = Arena(XB.base, XB.size)
        self.YB = T("YB", [128, 4, 3, 64], F32, a)
        self.RZT = T("RZT", [128, 3, 256], F32, a)
        self.WSTG = [T("WSTG0", [128, 8, 256], F32, Arena(self.S.sb_addr[self.BLK.name][0], 11264)),
                     T("WSTG1", [128, 8, 256], F32, Arena(XB.base, 8192)),
                     T("WSTG2", [128, 8, 256], F32, Arena(XB.base + 8192, 8192))]
        a = mk()
        self.WM = T("WM", [128, 8, 512], F32, a)
        self.SCT = T("SCT", [128, 8, 2], F32, a)
        self.SCB = T("SCB", [128, 2, 8, 128], F32, a)
        self.MODT = T("MODT", [128, 16, 2], F32, a)
        self.BMT = T("BMT", [128, 2, 16], F32, a)
        self.BG = T("BG", [128, D], F32, a)
        self.GP = T("GP", [128, D], F32, a)
        self.TMPS = T("TMPS", [128, 32], F32, a)
        a = mk()
        self.QT = T("QT", [128, 3, 256], BF16, a)
        self.KT = T("KT", [128, 3, 256], BF16, a)
        self.GOZ = T("GOZ", [128, 3, 256], F32, a)
        self.KTOK = T("KTOK", [64, 4, 384], BF16, a)
        self.VAUG = T("VAUG", [64, 4, 6, 65], BF16, a)
        self.HB = T("HB", [64, 4, 384], F32, a)
        self.mg = []
        for d in range(2):
            g = {}
            for n in ["gi", "lf", "pre", "bb", "gg", "ee", "fl"]:
                g[n] = T("mg_%s%d" % (n, d), [6, 256], F32, a)
            for n in ["mx", "mch", "mprev", "MM", "dec"]:
                g[n] = T("mg_%s%d" % (n, d), [6, 4], F32, a)
            g["X2"] = T("mg_X2%d" % d, [6, 4, 3], F32, a)
            g["etok"] = T("mg_etok%d" % d, [64, 4, 6], F32, a)
            g["fltok"] = T("mg_fltok%d" % d, [64, 4, 6], F32, a)
            g["decb"] = T("mg_decb%d" % d, [128, 4, 3], F32, a)
            self.mg.append(g)
        self.STSB = [T("STSB%d" % i, [64, 6, 64], BF16, a) for i in range(2)]
        self.VP = [T("VP%d" % i, [64, 6, 65], BF16, a) for i in range(2)]
        self.CDEC = T("CDEC", [128, 3, 65], F32, a)
        self.CDBF = T("CDBF", [128, 3, 65], BF16, a)
        self.DN = T("DN", [64, 6], F32, a)
        self.RDN = T("RDN", [64, 6], F32, a)
        self.HD = T("HD", [64, 6, 64], F32, a)
        self.SQ = T("SQ", [64, 4, 384], F32, a)
        self.SSQ = T("SSQ", [64, 24], F32, a)
        self.RSTD = T("RSTD", [64, 24], F32, a)
        self.ZT = [T("ZT%d" % i, [128, 256], F32, a) for i in range(2)]
        a = mk()
        self.XL = T("XL", [128, 2, 392], F32, a)
        self.XC = T("XC", [128, 2, 256], F32, a)
        self.LZT = T("LZT", [128, 2, 256], F32, a)
        self.lt = {n: T("lt_" + n, [128, 2, 256], F32, a) for n in ["rg", "ig", "aa", "a2", "bt"]}
        self.HL = [T("HL%d" % d, [128, 2, 256], F32, a) for d in range(2)]
        a = mk()
        self.URS = T("URS", [128, 11, 384], F32, a)
        a = mk()
        self.KH = T("KH", [128, 3, 256], F32, a)
        R1 = a.take(7 * 3072)
        a1 = Arena(R1, 7 * 3072)
        self.rt = {n: T("rt_" + n, [128, 3, 256], F32, a1) for n in ["sg", "aa", "kt", "bb", "cs", "d", "E"]}
        a2 = Arena(R1, 7 * 3072)
        self.A1BD = T("A1BD", [128, 3, 2, 128], BF16, a2)
        self.A2BD = T("A2BD", [128, 3, 2, 128], BF16, a2)
        self.NN = [T("NN%d" % i, [128, 3, 128], BF16, a2) for i in range(2)]
        self.NTT = [T("NTT%d" % i, [128, 3, 128], BF16, a2) for i in range(2)]
        self.U = T("U", [128, 3, 64], F32, a2)
        self.UBF = T("UBF", [128, 3, 64], BF16, a2)
        self.HBF = T("HBF", [128, 3, 64], BF16, a2)
        self.HTMP = T("HTMP", [128, 3, 64], F32, a2)
        self.YBD = T("YBD", [128, 3, 128], F32, a2)
        self.KTTOK = T("KTTOK", [128, 4, 3, 128], BF16, a2)
        self.BTTOK = T("BTTOK", [128, 4, 3, 128], BF16, a2)
        self.rSSQ = T("rSSQ", [128, 12], F32, a2)
        self.rRSTD = T("rRSTD", [128, 12], F32, a2)
        self.KR = T("KR", [128, 3, 4, 2, 64], BF16, a)
        self.BTT = T("BTT", [128, 3, 256], BF16, a)
        self.KTT = T("KTT", [128, 3, 256], BF16, a)
        bdr = a.take(4 * 3072)
        a3 = Arena(bdr, 4 * 3072)
        self.BD = {n: T("BD_" + n, [128, 3, 4, 128], BF16, a3) for n in ["kt", "b", "kh", "r"]}
        a3 = Arena(bdr, 4 * 3072)
        self.ft = {n: T("ft_" + n, [128, 3, 256], F32, a3) for n in ["rk", "bon", "t1"]}
        self.YSQ = T("YSQ", [128, 4, 3, 64], F32, a3)
        self.BDV = T("BDV", [128, 3, 4, 128], BF16, a)
        self.VSTK = T("VSTK", [128, 4, 3, 64], BF16, a)
        self.TWL = T("TWL", [128, 256], F32, a)
        self.GL = T("GL", [128, 3, 4], F32, a)
        self.PA = nc.alloc_psum_tensor("PA", [128, 1024], F32)
        self.PB = nc.alloc_psum_tensor("PB", [128, 1024], F32)
        self.PQ = [nc.alloc_psum_tensor("PQ%d" % i, [128, 512], F32) for i in range(3)]
        self.PTB = nc.alloc_psum_tensor("PTB", [128, 1024], BF16)
        self.pq_i = 0

    def pq(self):
        t = self.PQ[self.pq_i % 3]
        self.pq_i += 1
        return t

    def V(self, t, p0, npart, off, dims):
        pstep = _prod(list(t.shape)[1:])
        return bass.AP(t, p0 * pstep + off, [[pstep, npart]] + [list(d) for d in dims])

    def tt(self, e, out, in0, in1, op):
        return self.S.emit(e, [in0, in1], [out], lambda g: g.tensor_tensor(out=out, in0=in0, in1=in1, op=op))

    def ts(self, e, out, in0, s1, s2, op0, op1=None):
        rd = [in0] + [s for s in (s1, s2) if not isinstance(s, (int, float)) and s is not None]
        if op1 is None:
            return self.S.emit(e, rd, [out], lambda g: g.tensor_scalar(out=out, in0=in0, scalar1=s1, scalar2=None, op0=op0))
        return self.S.emit(e, rd, [out], lambda g: g.tensor_scalar(out=out, in0=in0, scalar1=s1, scalar2=s2, op0=op0, op1=op1))

    def stt(self, out, in0, sc, in1, op0, op1):
        rd = [in0, in1] + ([] if isinstance(sc, (int, float)) else [sc])
        return self.S.emit('dve', rd, [out], lambda g: g.scalar_tensor_tensor(out=out, in0=in0, scalar=sc, in1=in1, op0=op0, op1=op1))

    def act(self, out, in_, func, bias=None, scale=None, accum=None):
        rd = [in_]
        kw = {}
        if bias is not None:
            kw['bias'] = bias
            if not isinstance(bias, (int, float)):
                rd.append(bias)
        if scale is not None:
            kw['scale'] = scale
            if not isinstance(scale, (int, float)):
                rd.append(scale)
        wr = [out]
        if accum is not None:
            kw['accum_out'] = accum
            wr.append(accum)
        return self.S.emit('act', rd, wr, lambda g: g.activation(out=out, in_=in_, func=func, **kw))

    def cp(self, e, out, in_):
        if e == 'act':
            return self.act(out, in_, AF.Copy)
        return self.S.emit(e, [in_], [out], lambda g: g.tensor_copy(out=out, in_=in_))

    def _pe_rowtile_guard(self, lhsT, out):
        S = self.S
        st = S.region(lhsT)
        k = st[2] - st[1]
        kr = 32 if k <= 32 else (64 if k <= 64 else 128)
        rows = (st[1], st[1] + kr)
        oreg = S.region(out)
        last = getattr(self, '_last_pe', None)
        if last is not None:
            lrows, loreg, lins, linc = last
            disjoint = rows[1] <= lrows[0] or lrows[1] <= rows[0]
            samebank = (loreg[0] == oreg[0]) and loreg[3] < oreg[4] and oreg[3] < loreg[4]
            if disjoint and samebank:
                if not linc:
                    S.cnt['pe'] += 1
                    lins.then_inc(S.sem['pe'], 1)
                S.wait('pe', S.sem['pe'], S.cnt['pe'])
        return rows, oreg

    def mm(self, out, lhsT, rhs, start=True, stop=True, inc=None):
        if inc is None:
            inc = stop
        rows, oreg = self._pe_rowtile_guard(lhsT, out)
        ins = self.S.emit('pe', [lhsT, rhs], [out],
                          lambda g: g.matmul(out, lhsT=lhsT, rhs=rhs, start=start, stop=stop), inc=inc)
        self._last_pe = (rows, oreg, ins, inc)
        return ins

    def tr(self, out, in_, inc=True, bf=False):
        n = in_.shape[0]
        ident = self.IDB[0:n, 0:n] if bf else self.CST[0:n, CC['IDENT']:CC['IDENT'] + n]
        rows, oreg = self._pe_rowtile_guard(in_, out)
        ins = self.S.emit('pe', [in_, ident], [out],
                          lambda g: g.transpose(out=out, in_=in_, identity=ident), inc=inc)
        self._last_pe = (rows, oreg, ins, inc)
        return ins

    def memset(self, e, ap, v):
        return self.S.emit(e, [], [ap], lambda g: g.memset(ap, v))

    def scan(self, out, d0, d1, init, op0, op1):
        rd = [d0, d1] + ([] if isinstance(init, (int, float)) else [init])
        return self.S.emit('dve', rd, [out], lambda g: g.tensor_tensor_scan(out=out, data0=d0, data1=d1, initial=init, op0=op0, op1=op1))

    def recip(self, out, in_):
        return self.S.emit('dve', [in_], [out], lambda g: g.reciprocal(out=out, in_=in_))

    def reduce(self, out, in_, op, axis=AX.X):
        return self.S.emit('dve', [in_], [out], lambda g: g.tensor_reduce(out=out, in_=in_, axis=axis, op=op))

    def dump(self, name, ap):
        if not self.debug or name in self.dbg_map:
            return
        if getattr(self, 'dbg_filter', None) is not None and not any(name.startswith(p) for p in self.dbg_filter):
            return
        shp = list(ap.shape)
        npart, nfree = shp[0], _prod(shp[1:])
        stage = self.XN[1]
        assert nfree <= 1024
        dst = self.V(stage, 0, npart, 0, [[_prod(shp[i + 1:]), shp[i]] for i in range(1, len(shp))])
        self.cp('dve', dst, ap)
        self.S.dma('sp', self.dbg[0:npart, self.dbg_off:self.dbg_off + nfree], stage[0:npart, 0:nfree], allow_slow_non_contiguous=True)
        self.dbg_map[name] = (self.dbg_off, npart, shp[1:])
        self.dbg_off += nfree

    def ppc(self, l, key, j=0, rows=128):
        c = l * PL + PC[key] + j
        return self.PP[0:rows, c:c + 1]

    def dvc(self, l, key, j=0, rows=128):
        c = l * DL + DC[key] + j
        return self.DV[0:rows, c:c + 1]

    def setup(self):
        S = self.S
        S.dma('sp', self.CST[:, :], self.cst[:, :])
        S.dma('sp', self.PP[:, :], self.pp[:, :])
        S.dma('sp', self.LORA[:, :, :, :], self.lora[:, :, :, :])
        S.dma('sp', self.LRUW[:, :, :, :], self.lruw[:, :, :, :])
        self.memset('dve', self.DV[:, :], 0.0)
        self.cp('dve', self.IDB[:, :], self.CST[:, CC['IDENT']:CC['IDENT'] + 128])
        self.cp('dve', self.ISTKB[:, :], self.CST[:, CC['ISTK']:CC['ISTK'] + 64])
        for l in range(2):
            self.ts('dve', self.DV[:, l * DL + DC['OMKA']: l * DL + DC['OMKA'] + 3],
                    self.PP[:, l * PL + PC['R_KA']: l * PL + PC['R_KA'] + 3], -1.0, 1.0, ALU.mult, ALU.add)
            self.ts('dve', self.DV[:, l * DL + DC['OMMU']: l * DL + DC['OMMU'] + 11],
                    self.PP[:, l * PL + PC['R_MU']: l * PL + PC['R_MU'] + 11], -1.0, 1.0, ALU.mult, ALU.add)
            lam = self.PP[:, l * PL + PC['L_LAM']: l * PL + PC['L_LAM'] + 4]
            t0 = self.TMPS[:, 0:4]
            self.act(t0, lam, AF.Exp, scale=-1.0)
            self.act(t0, t0, AF.Ln, bias=1.0)
            self.ts('dve', self.DV[:, l * DL + DC['CLAM']: l * DL + DC['CLAM'] + 4], t0, -8.0, None, ALU.mult)
            self.ts('dve', self.DV[:, l * DL + DC['C2LAM']: l * DL + DC['C2LAM'] + 4], t0, -16.0, None, ALU.mult)
            self.ts('dve', self.DV[0:6, l * DL + DC['NBF']: l * DL + DC['NBF'] + 2],
                    self.PP[0:6, l * PL + PC['M_BF']: l * PL + PC['M_BF'] + 2], -1.0, None, ALU.mult)
            for hp in range(3):
                self.ts('dve', self.RKD[:, l, hp, :], self.CST[:, CC['BONES']:CC['BONES'] + 128],
                        self.ppc(l, 'R_RK', hp), None, ALU.mult)
        self.memset('dve', self.XL[:, :, 0:2], 0.0)

    def load_layer(self, l):
        S = self.S
        wsrc = self.w_in[l].rearrange("(kc p) c -> p kc c", p=128)
        wo = self.w_out[l].rearrange("(kc p) c -> p kc c", p=128)
        pieces = [(self.W_IN, wsrc, c0, min(256, IN_COLS - c0)) for c0 in range(0, IN_COLS, 256)]
        pieces += [(self.W_OUT, wo, c0, 256) for c0 in range(0, D, 256)]
        for i, (dst, src, c0, n) in enumerate(pieces):
            stg = self.WSTG[i % 3]
            S.dma('sp', stg[:, :, 0:n], src[:, :, c0:c0 + n])
            self.cp('pool', dst[:, :, c0:c0 + n], stg[:, :, 0:n])
        if self.stop == 'load_w':
            raise StopIteration
        S.dma('sp', self.SCT[:, :, :], self.cc[:, :, :])
        S.dma('sp', self.BMT[:, :, :], self.bmodT[:, :, :])
        self.act(self.SCT[:, :, :], self.SCT[:, :, :], AF.Silu)
        for m in range(2):
            for kc in range(8):
                src = self.V(self.SCT, 0, 128, kc * 2 + m, [[0, 128]])
                self.cp('dve', self.SCB[:, m, kc, :], src)
        wm = self.w_mod[l].rearrange("(kc p) c -> p kc c", p=128)
        for blk in range(6):
            S.dma('sp', self.WM[:, :, :], wm[:, :, blk * 512:(blk + 1) * 512])
            if blk < 4:
                ps = self.pq()
                for j in range(4):
                    for kc in range(8):
                        self.mm(ps[:, j * 2:j * 2 + 2], self.WM[:, kc, j * 128:(j + 1) * 128], self.SCT[:, kc, :],
                                start=(kc == 0), stop=(kc == 7))
                o = self.MODT[:, blk * 4:(blk + 1) * 4, :]
                bsrc = self.V(self.BMT, 0, 128, l * 16 + blk * 4, [[1, 4], [0, 2]])
                self.tt('dve', o, self.V(ps, 0, 128, 0, [[2, 4], [1, 2]]), bsrc, ALU.add)
            else:
                half = blk - 4
                for m in range(2):
                    ps = self.pq()
                    for kc in range(8):
                        self.mm(ps[:, :], self.SCB[:, m, kc, :], self.WM[:, kc, :], start=(kc == 0), stop=(kc == 7))
                    self.cp('act', self.GATEB[:, m, half * 512:(half + 1) * 512], ps[:, :])
        if self.stop == 'load_m':
            raise StopIteration
        for m in range(2):
            self.cp('dve', self.SH[:, m, :], self.V(self.MODT, 0, 128, m, [[2, 8]]))
            t0 = self.TMPS[:, 8:16]
            self.ts('dve', t0, self.V(self.MODT, 0, 128, 16 + m, [[2, 8]]), 1.0, None, ALU.add)
            self.tt('dve', self.GS[:, m, :], t0, self.PP[:, l * PL + PC['G_PRE']: l * PL + PC['G_PRE'] + 8], ALU.mult)
        if self.stop == 'load_g':
            raise StopIteration
        S.dma('sp', self.BG[:, :], bass.AP(self.bmodg.tensor, l * D, [[0, 128], [1, D]]))
        S.dma('sp', self.GP[:, :], bass.AP(self.gpost.tensor, l * D, [[0, 128], [1, D]]))
        if self.stop == 'load_b':
            raise StopIteration
        for m in range(2):
            self.tt('dve', self.GATEB[:, m, :], self.GATEB[:, m, :], self.BG[:, :], ALU.add)
            self.tt('dve', self.GATEB[:, m, :], self.GATEB[:, m, :], self.GP[:, :], ALU.mult)

    def proj_fm(self, c0, ncols, t_off, ntok, evac):
        ps = self.pq()
        for kc in range(8):
            self.mm(ps[0:ncols, 0:ntok], self.W_IN[:, kc, c0:c0 + ncols], self.HT[:, kc, t_off:t_off + ntok],
                    start=(kc == 0), stop=(kc == 7))
        evac(ps[0:ncols, 0:ntok])

    def proj_tm(self, c0, ncols, t_off, evac):
        ps = self.pq()
        for kc in range(8):
            self.mm(ps[0:64, 0:ncols], self.HT[:, kc, t_off:t_off + 64], self.W_IN[:, kc, c0:c0 + ncols],
                    start=(kc == 0), stop=(kc == 7))
        evac(ps[0:64, 0:ncols])

    def visit(self, l, mod, xsrc, xdst, row0, T, t0, dirs, grid, seq_idx, first, last):
        S = self.S
        w0 = max(0, t0 - 64)
        w1 = min(T, t0 + TB + 64)
        W = w1 - w0
        co = t0 - w0
        do_f = 0 in dirs
        prompt = (mod == 0)
        ntile = (W + 127) // 128
        for i in range(ntile):
            n = min(128, W - i * 128)
            xw, xn = self.XW[i % 2], self.XN[i % 2]
            r = row0 + w0 + i * 128
            S.dma('sp', xw[0:n, :], xsrc[r:r + n, :])
            ssq = self.TMPS[0:n, 16 + i:17 + i]
            self.act(xn[0:n, :], xw[0:n, :], AF.Square, accum=ssq)
            if self.stop == 'n1':
                raise StopIteration
            rs = self.TMPS[0:n, 20 + i:21 + i]
            self.ts('dve', rs, ssq, 1.0 / D, EPS, ALU.mult, ALU.add)
            self.act(rs, rs, AF.Sqrt)
            self.recip(rs, rs)
            if self.stop == 'n2':
                raise StopIteration
            self.act(xn[0:n, :], xw[0:n, :], AF.Copy, scale=rs)
            if self.stop == 'n3':
                raise StopIteration
            for half in range(2):
                ps = self.pq()
                for j in range(4):
                    kc = half * 4 + j
                    self.tr(ps[:, j * 128:j * 128 + n], xn[0:n, kc * 128:(kc + 1) * 128])
                if self.stop == 'n4':
                    raise StopIteration
                for j in range(4):
                    kc = half * 4 + j
                    o = self.HT[:, kc, i * 128:i * 128 + n]
                    if half == 0:
                        self.ts('dve', o, ps[:, j * 128:j * 128 + n], self.GS[:, mod, kc:kc + 1], self.SH[:, mod, kc:kc + 1], ALU.mult, ALU.add)
                    else:
                        self.act(o, ps[:, j * 128:j * 128 + n], AF.Identity, bias=self.SH[:, mod, kc:kc + 1], scale=self.GS[:, mod, kc:kc + 1])
        if self.stop in ('norm', 'n5a', 'n5d'):
            raise StopIteration
        self.stage_mlstm(l, t0, co, dirs, prompt, seq_idx, first, last)
        if self.stop == 'mlstm':
            raise StopIteration
        self.stage_lru(l, t0, co, W, w0, w1, T, dirs, prompt, seq_idx, first, last)
        if self.stop == 'lru':
            raise StopIteration
        self.stage_rwkv(l, t0, co, W, w0, w1, T, dirs, grid, prompt, seq_idx, first, last)
        for kc in range(8):
            self.dump('mix%d' % kc, self.MIXT[:, kc, :])
        if self.stop == 'rwkv':
            raise StopIteration
        if do_f:
            self.stage_out(l, mod, xsrc, xdst, row0 + t0)
        if self.stop == 'out':
            raise StopIteration

    def stage_mlstm(self, l, t0, co, dirs, prompt, seq_idx, first, last):
        S = self.S
        do_f = 0 in dirs
        for hp in range(3):
            self.proj_fm(hp * 128, 128, co, 256, lambda ps, hp=hp: self.cp('act', self.QT[:, hp, :], ps))
            self.proj_fm(384 + hp * 128, 128, co, 256, lambda ps, hp=hp: self.act(self.KT[:, hp, :], ps, AF.Copy, scale=0.125))
        if do_f:
            for hp in range(3):
                def ev_o(ps, hp=hp):
                    self.act(self.GOZ[:, hp, :], ps, AF.Sigmoid)
                self.proj_fm(1152 + hp * 128, 128, co, 256, ev_o)
                def ev_z2(ps, hp=hp):
                    tz = self.ZT[hp % 2]
                    self.act(tz[:, :], ps, AF.Silu)
                    self.tt('dve', self.GOZ[:, hp, :], self.GOZ[:, hp, :], tz[:, :], ALU.mult)
                self.proj_fm(1536 + hp * 128, 128, co, 256, ev_z2)
        for c in range(4):
            self.proj_tm(384, 384, co + c * 64, lambda ps, c=c: self.act(self.KTOK[:, c, :], ps, AF.Copy, scale=0.125))
            def ev_v(ps, c=c):
                self.cp('dve', self.VAUG[:, c, :, 0:64], self.V(ps.tensor, 0, 64, 0, [[64, 6], [1, 64]]))
            self.proj_tm(768, 384, co + c * 64, ev_v)
        self.memset('dve', self.VAUG[:, :, :, 64:65], 1.0)
        for d in dirs:
            g = self.mg[d]
            self.proj_fm(1920 + d * 6, 6, co, 256, lambda ps, g=g, d=d: self.act(g["gi"][:, :], ps, AF.Identity, bias=self.ppc(l, 'M_BI', d, 6)))
            def ev_f(ps, g=g, d=d):
                self.act(g["lf"][:, :], ps, AF.Exp, bias=self.dvc(l, 'NBF', d, 6), scale=-1.0)
                self.act(g["lf"][:, :], g["lf"][:, :], AF.Ln, bias=1.0)
                self.ts('dve', g["lf"][:, :], g["lf"][:, :], -1.0, None, ALU.mult)
            self.proj_fm(1932 + d * 6, 6, co, 256, ev_f)
        for d in dirs:
            if first[d]:
                if prompt:
                    self.memset('dve', self.mC[d][:, :, :], 0.0)
                    self.memset('dve', self.mM[d][:, :], 0.0)
                else:
                    S.dma('sp', self.mC[d][:, :, :], self.st_mC[l, d])
                    S.dma('sp', self.mM[d][:, :], self.st_mm[:, l, d:d + 1], allow_slow_non_contiguous=True)
        if do_f and not (1 in dirs):
            S.dma('sp', self.HB[:, :, :], self.sHB[t0:t0 + 256, :].rearrange("(c s) f -> s c f", s=64))
        for d in dirs:
            g = self.mg[d]
            v3 = lambda t: self.V(t, 0, 6, 0, [[64, 4], [1, 64]])
            self.scan(g["pre"][:, :], self.CST[0:6, CC['RMASK']:CC['RMASK'] + 256], g["lf"][:, :], 0.0, ALU.mult, ALU.add)
            bL = self.V(g["pre"], 0, 6, 63, [[64, 4]])
            bLb = self.V(g["pre"], 0, 6, 63, [[64, 4], [0, 64]])
            if d == 0:
                bsrc = g["pre"]
            else:
                self.tt('dve', v3(g["bb"]), bLb, v3(g["pre"]), ALU.subtract)
                self.tt('dve', g["bb"][:, :], g["bb"][:, :], g["lf"][:, :], ALU.add)
                bsrc = g["bb"]
            self.tt('dve', g["gg"][:, :], g["gi"][:, :], bsrc[:, :], ALU.subtract)
            self.reduce(g["mx"][:, :], v3(g["gg"]), ALU.max)
            if d == 0:
                mo, mxv, blv = g["mch"][:, :], g["mx"][:, :], bL
            else:
                mo = self.V(g["mch"], 0, 6, 3, [[-1, 4]])
                mxv = self.V(g["mx"], 0, 6, 3, [[-1, 4]])
                blv = self.V(g["pre"], 0, 6, 63 + 3 * 64, [[-64, 4]])
            self.scan(mo, mxv, blv, self.mM[d][:, 0:1], ALU.max, ALU.add)
            if d == 0:
                self.cp('dve', g["mprev"][:, 1:4], g["mch"][:, 0:3])
                self.cp('dve', g["mprev"][:, 0:1], self.mM[d][:, 0:1])
                mfin = g["mch"][:, 3:4]
            else:
                self.cp('dve', g["mprev"][:, 0:3], g["mch"][:, 1:4])
                self.cp('dve', g["mprev"][:, 3:4], self.mM[d][:, 0:1])
                mfin = g["mch"][:, 0:1]
            self.tt('dve', g["MM"][:, :], g["mprev"][:, :], g["mx"][:, :], ALU.max)
            self.tt('dve', g["dec"][:, :], g["mprev"][:, :], g["MM"][:, :], ALU.subtract)
            self.act(g["dec"][:, :], g["dec"][:, :], AF.Exp)
            self.cp('dve', self.mM[d][:, 0:1], mfin)
            MMb = self.V(g["MM"], 0, 6, 0, [[1, 4], [0, 64]])
            self.tt('dve', v3(g["ee"]), v3(g["gg"]), MMb, ALU.subtract)
            self.act(g["ee"][:, :], g["ee"][:, :], AF.Exp)
            self.tt('dve', v3(g["fl"]), v3(bsrc), MMb, ALU.add)
            self.act(g["fl"][:, :], g["fl"][:, :], AF.Exp, scale=-1.0)
            ps = self.pq()
            for c in range(4):
                self.tr(ps[0:64, c * 6:c * 6 + 6], g["ee"][:, c * 64:(c + 1) * 64])
                self.tr(ps[0:64, 24 + c * 6:24 + c * 6 + 6], g["fl"][:, c * 64:(c + 1) * 64])
            self.cp('dve', g["etok"][:, :, :], self.V(ps, 0, 64, 0, [[6, 4], [1, 6]]))
            self.cp('dve', g["fltok"][:, :, :], self.V(ps, 0, 64, 24, [[6, 4], [1, 6]]))
            self.tt('dve', g["X2"][:, :, :], self.V(g["dec"], 0, 6, 0, [[1, 4], [0, 3]]),
                    self.V(self.CST, 0, 6, CC['PSEL'], [[0, 4], [1, 3]]), ALU.mult)
            ps2 = self.pq()
            self.mm(ps2[:, 0:12], self.CST[0:6, CC['LSEL']:CC['LSEL'] + 128], self.V(g["X2"], 0, 6, 0, [[1, 12]]))
            self.cp('dve', g["decb"][:, :, :], self.V(ps2, 0, 128, 0, [[3, 4], [1, 3]]))
        for d in sorted(dirs, reverse=True):
            g = self.mg[d]
            mask = self.CST[0:64, CC['MUI']:CC['MUI'] + 64] if d == 0 else self.CST[0:64, CC['MLI']:CC['MLI'] + 64]
            maskb = self.V(self.CST, 0, 64, CC['MUI'] if d == 0 else CC['MLI'], [[0, 6], [1, 64]])
            for j in range(4):
                c = j if d == 0 else 3 - j
                cs = slice(c * 64, (c + 1) * 64)
                stsb, vp = self.STSB[j % 2], self.VP[j % 2]
                ps = self.pq()
                for h in range(6):
                    hp, pb = h // 2, 64 * (h % 2)
                    self.mm(ps[0:64, h * 64:(h + 1) * 64], self.KT[pb:pb + 64, hp, cs], self.QT[pb:pb + 64, hp, cs], inc=(h == 5))
                self.tt('dve', stsb[:, :, :], self.V(ps, 0, 64, 0, [[64, 6], [1, 64]]), maskb, ALU.mult)
                self.tt('dve', vp[:, :, :], self.VAUG[:, c, :, :], self.V(g["etok"], 0, 64, c * 6, [[1, 6], [0, 65]]), ALU.mult)
                self.tt('dve', self.CDEC[:, :, :], self.mC[d][:, :, :], self.V(g["decb"], 0, 128, c * 3, [[1, 3], [0, 65]]), ALU.mult)
                self.cp('act', self.CDBF[:, :, :], self.CDEC[:, :, :])
                ph = self.pq()
                for h in range(6):
                    hp, pb = h // 2, 64 * (h % 2)
                    o = ph[0:64, h * 65:(h + 1) * 65]
                    self.mm(o, stsb[:, h, :], vp[:, h, :], start=True, stop=False)
                    self.mm(o, self.QT[pb:pb + 64, hp, cs], self.CDBF[pb:pb + 64, hp, :], start=False, stop=True, inc=(h == 5))
                pc = self.pq()
                for h in range(6):
                    hp, pb = h // 2, 64 * (h % 2)
                    self.mm(pc[pb:pb + 64, hp * 65:(hp + 1) * 65], self.KTOK[:, c, h * 64:(h + 1) * 64], vp[:, h, :], inc=(h == 5))
                self.tt('dve', self.mC[d][:, :, :], self.CDEC[:, :, :], self.V(pc, 0, 128, 0, [[65, 3], [1, 65]]), ALU.add)
                self.act(self.DN[:, :], self.V(ph, 0, 64, 64, [[65, 6]]), AF.Abs)
                self.tt('dve', self.DN[:, :], self.DN[:, :], g["fltok"][:, c, :], ALU.max)
                self.recip(self.RDN[:, :], self.DN[:, :])
                hsrc = self.V(ph, 0, 64, 0, [[65, 6], [1, 64]])
                rb = self.V(self.RDN, 0, 64, 0, [[1, 6], [0, 64]])
                hbv = self.V(self.HB, 0, 64, c * 384, [[64, 6], [1, 64]])
                if d == 1:
                    self.tt('dve', hbv, hsrc, rb, ALU.mult)
                else:
                    self.tt('dve', self.HD[:, :, :], hsrc, rb, ALU.mult)
                    self.tt('dve', hbv, hbv, self.HD[:, :, :], ALU.add)
            if last[d] and prompt:
                S.dma('sp', self.o_mC[seq_idx, l, d], self.mC[d][:, :, :])
                S.dma('sp', self.o_mm[seq_idx, l, d], self.mM[d][:, 0:1])
        if not do_f:
            S.dma('sp', self.sHB[t0:t0 + 256, :].rearrange("(c s) f -> s c f", s=64), self.HB[:, :, :])
            return
        self.tt('dve', self.SQ[:, :, :], self.HB[:, :, :], self.HB[:, :, :], ALU.mult)
        self.reduce(self.SSQ[:, :], self.V(self.SQ, 0, 64, 0, [[64, 24], [1, 64]]), ALU.add)
        self.ts('dve', self.SSQ[:, :], self.SSQ[:, :], 1.0 / 64, EPS, ALU.mult, ALU.add)
        self.act(self.SSQ[:, :], self.SSQ[:, :], AF.Sqrt)
        self.recip(self.RSTD[:, :], self.SSQ[:, :])
        self.tt('dve', self.V(self.SQ, 0, 64, 0, [[64, 24], [1, 64]]), self.V(self.HB, 0, 64, 0, [[64, 24], [1, 64]]),
                self.V(self.RSTD, 0, 64, 0, [[1, 24], [0, 64]]), ALU.mult)
        for hp in range(3):
            ps = self.pq()
            for c in range(4):
                self.tr(ps[:, c * 64:(c + 1) * 64], self.SQ[:, c, hp * 128:(hp + 1) * 128], inc=(c == 3))
            self.stt(self.MIXT[:, hp, :], ps[:, 0:256], self.ppc(l, 'M_NORM', hp), self.GOZ[:, hp, :], ALU.mult, ALU.mult)

    def stage_lru(self, l, t0, co, W, w0, w1, T, dirs, prompt, seq_idx, first, last):
        S = self.S
        do_f = 0 in dirs
        for pr in range(2):
            self.proj_fm(3736 + pr * 128, 128, 0, W, lambda ps, pr=pr: self.cp('act', self.XL[:, pr, 2:2 + W], ps))
            if do_f:
                self.proj_fm(3992 + pr * 128, 128, co, 256, lambda ps, pr=pr: self.act(self.LZT[:, pr, :], ps, AF.Silu))
        if w1 == T:
            self.memset('dve', self.XL[:, :, 2 + W:2 + W + 1], 0.0)
        if w0 == 0:
            self.memset('dve', self.XL[:, :, 0:2], 0.0)
        for pr in range(2):
            self.ts('dve', self.XC[:, pr, :], self.XL[:, pr, co:co + 256], self.ppc(l, 'L_CONV', 0 * 2 + pr), self.ppc(l, 'L_CONVB', pr), ALU.mult, ALU.add)
            for j in range(1, 4):
                self.stt(self.XC[:, pr, :], self.XL[:, pr, co + j:co + j + 256], self.ppc(l, 'L_CONV', j * 2 + pr), self.XC[:, pr, :], ALU.mult, ALU.add)
        for d in dirs:
            if first[d]:
                if prompt:
                    self.memset('dve', self.lS[d][:, :], 0.0)
                else:
                    S.dma('sp', self.lS[d][:, :], self.st_l[:, l, d, :])
        if do_f and not (1 in dirs):
            S.dma('sp', self.HL[1][:, :, :], self.sLB[:, :, t0:t0 + 256])
        lt = self.lt
        for d in sorted(dirs, reverse=True):
            for pr in range(2):
                ps = self.pq()
                self.mm(ps[:, 0:256], self.LRUW[:, l, (0 * 2 + d) * 2 + pr, :], self.XC[:, pr, :])
                self.mm(ps[:, 256:512], self.LRUW[:, l, (1 * 2 + d) * 2 + pr, :], self.XC[:, pr, :])
                self.act(lt["rg"][:, pr, :], ps[:, 0:256], AF.Sigmoid, bias=self.ppc(l, 'L_BA', d * 2 + pr))
                self.act(lt["ig"][:, pr, :], ps[:, 256:512], AF.Sigmoid, bias=self.ppc(l, 'L_BX', d * 2 + pr))
                self.act(lt["aa"][:, pr, :], lt["rg"][:, pr, :], AF.Exp, scale=self.dvc(l, 'CLAM', d * 2 + pr))
                self.act(lt["a2"][:, pr, :], lt["rg"][:, pr, :], AF.Exp, scale=self.dvc(l, 'C2LAM', d * 2 + pr))
                self.ts('dve', lt["a2"][:, pr, :], lt["a2"][:, pr, :], -1.0, 1.0, ALU.mult, ALU.add)
                self.act(lt["a2"][:, pr, :], lt["a2"][:, pr, :], AF.Sqrt)
                self.tt('dve', lt["bt"][:, pr, :], lt["a2"][:, pr, :], lt["ig"][:, pr, :], ALU.mult)
                self.tt('dve', lt["bt"][:, pr, :], lt["bt"][:, pr, :], self.XC[:, pr, :], ALU.mult)
                if d == 0:
                    self.scan(self.HL[0][:, pr, :], lt["aa"][:, pr, :], lt["bt"][:, pr, :], self.lS[0][:, pr:pr + 1], ALU.mult, ALU.add)
                    self.cp('dve', self.lS[0][:, pr:pr + 1], self.HL[0][:, pr, 255:256])
                else:
                    rv = lambda t: self.V(t, 0, 128, pr * 256 + 255, [[-1, 256]])
                    self.scan(rv(self.HL[1]), rv(lt["aa"]), rv(lt["bt"]), self.lS[1][:, pr:pr + 1], ALU.mult, ALU.add)
                    self.cp('dve', self.lS[1][:, pr:pr + 1], self.HL[1][:, pr, 0:1])
            if last[d] and prompt:
                S.dma('sp', self.o_l[seq_idx, l, d], self.lS[d][:, :])
        if not do_f:
            S.dma('sp', self.sLB[:, :, t0:t0 + 256], self.HL[1][:, :, :])
            return
        self.tt('dve', self.HL[0][:, :, :], self.HL[0][:, :, :], self.HL[1][:, :, :], ALU.add)
        self.tt('dve', self.MIXT[:, 6:8, :], self.HL[0][:, :, :], self.LZT[:, :, :], ALU.mult)

    def stage_rwkv(self, l, t0, co, W, w0, w1, T, dirs, grid, prompt, seq_idx, first, last):
        S = self.S
        do_f = 0 in dirs
        for ch in range(11):
            self.proj_fm(1944 + ch * 128, 128, 0, W, lambda ps, ch=ch: self.cp('act' if ch % 2 else 'dve', self.URS[:, ch, 0:W], ps))
        if do_f:
            for hp in range(3):
                self.proj_fm(3352 + hp * 128, 128, co, 256, lambda ps, hp=hp: self.act(self.RZT[:, hp, :], ps, AF.Silu))
        U3 = lambda off, n: self.V(self.URS, 0, 128, off, [[384, 11], [1, n]])
        B3 = lambda off, n: self.V(self.BLK, 0, 128, off, [[256, 11], [1, n]])
        if not grid:
            self.cp('dve', B3(1, 255), U3(0, 255))
            self.memset('dve', B3(0, 1), 0.0)
            self.tt('dve', B3(0, 255), B3(0, 255), U3(1, 255), ALU.add)
            wsh = 0.5
        else:
            U4 = lambda off, r, n: self.V(self.URS, 0, 128, off, [[384, 11], [64, r], [1, n]])
            B4 = lambda off, r, n: self.V(self.BLK, 0, 128, off, [[256, 11], [64, r], [1, n]])
            self.cp('dve', B4(1, 4, 63), U4(co, 4, 63))
            self.memset('dve', B4(0, 4, 1), 0.0)
            self.tt('dve', B4(0, 4, 63), B4(0, 4, 63), U4(co + 1, 4, 63), ALU.add)
            if t0 > 0:
                self.tt('dve', B3(0, 256), B3(0, 256), U3(co - 64, 256), ALU.add)
            else:
                self.tt('dve', B3(64, 192), B3(64, 192), U3(0, 192), ALU.add)
            if t0 + TB < T:
                self.tt('dve', B3(0, 256), B3(0, 256), U3(co + 64, 256), ALU.add)
            else:
                self.tt('dve', B3(0, 192), B3(0, 192), U3(co + 64, 192), ALU.add)
            wsh = 0.25
        mu = self.V(self.PP, 0, 128, l * PL + PC['R_MU'], [[1, 11], [0, 256]])
        self.tt('dve', B3(0, 256), B3(0, 256), mu, ALU.mult)
        omm = self.V(self.DV, 0, 128, l * DL + DC['OMMU'], [[1, 11], [0, 256]])
        self.tt('dve', U3(co, 256), U3(co, 256), omm, ALU.mult)
        self.stt(B3(0, 256), B3(0, 256), wsh, U3(co, 256), ALU.mult, ALU.add)
        for nm, ch in (('blk_r', 0), ('blk_k', 3), ('blk_v', 6), ('blk_wl', 9), ('blk_al', 10)):
            self.dump(nm, self.BLK[:, ch, :])
        rt = self.rt
        kk = self.V(self.PP, 0, 128, l * PL + PC['R_KK'], [[1, 3], [0, 256]])
        kap = rt["d"]
        self.tt('dve', kap[:, :, :], self.BLK[:, 3:6, :], kk, ALU.mult)
        ksq = rt["E"]
        self.tt('dve', ksq[:, :, :], kap[:, :, :], kap[:, :, :], ALU.mult)
        for hp in range(3):
            self.mm(self.PA[:, hp * 256:(hp + 1) * 256], self.CST[:, CC['BONES']:CC['BONES'] + 128], ksq[:, hp, :])
        self.act(ksq[:, :, :], self.V(self.PA, 0, 128, 0, [[256, 3], [1, 256]]), AF.Sqrt)
        self.ts('dve', ksq[:, :, :], ksq[:, :, :], 1e-12, None, ALU.max)
        self.recip(ksq[:, :, :], ksq[:, :, :])
        self.tt('dve', self.KH[:, :, :], kap[:, :, :], ksq[:, :, :], ALU.mult)
        self.dump('kh', self.KH[:, 0, :])
        self.bd_fill(self.BDV, lambda par: self.V(self.BLK, par * 64, 64, 6 * 256, [[256, 3], [64, 4], [1, 64]]))
        for c in range(4):
            for hp in range(3):
                self.mm(self.PB[:, (c * 3 + hp) * 64:(c * 3 + hp + 1) * 64], self.BDV[:, hp, c, :], self.ISTKB[:, :])
        self.cp('act', self.V(self.VSTK, 0, 128, 0, [[1, 768]]), self.PB[:, 0:768])
        for d in dirs:
            if first[d]:
                if prompt:
                    self.memset('dve', self.rH[d][:, :, :], 0.0)
                else:
                    S.dma('sp', self.rH[d][:, :, :], self.st_rH[l, d])
        ybflat = self.V(self.YB, 0, 128, 0, [[1, 768]])
        if do_f and not (1 in dirs):
            S.dma('sp', ybflat, self.sYB[t0 // 256])
        for d in sorted(dirs, reverse=True):
            self.rwkv_dir(l, d, seq_idx, prompt, last)
        if not do_f:
            S.dma('sp', self.sYB[t0 // 256], ybflat)
            return
        ft = self.ft
        self.tt('dve', self.YSQ[:, :, :, :], self.YB[:, :, :, :], self.YB[:, :, :, :], ALU.mult)
        self.reduce(self.rSSQ[:, :], self.V(self.YSQ, 0, 128, 0, [[64, 12], [1, 64]]), ALU.add)
        self.ts('dve', self.rSSQ[:, :], self.rSSQ[:, :], 1.0 / 64, EPS, ALU.mult, ALU.add)
        self.act(self.rSSQ[:, :], self.rSSQ[:, :], AF.Sqrt)
        self.recip(self.rRSTD[:, :], self.rSSQ[:, :])
        self.tt('dve', ft["rk"][:, :, :], self.BLK[:, 0:3, :], self.BLK[:, 3:6, :], ALU.mult)
        for hp in range(3):
            self.mm(self.PB[:, hp * 256:(hp + 1) * 256], self.RKD[:, l, hp, :], ft["rk"][:, hp, :])
        self.tt('dve', ft["bon"][:, :, :], self.V(self.PB, 0, 128, 0, [[256, 3], [1, 256]]), self.BLK[:, 6:9, :], ALU.mult)
        self.memset('pool', self.YBD[:, :, :], 0.0)
        for c in range(4):
            for par in range(2):
                self.tt('dve', self.V(self.YBD, par * 64, 64, par * 64, [[128, 3], [1, 64]]),
                        self.V(self.YB, par * 64, 64, c * 192, [[64, 3], [1, 64]]),
                        self.V(self.rRSTD, par * 64, 64, c * 3, [[1, 3], [0, 64]]), ALU.mult)
            for hp in range(3):
                self.mm(self.PA[:, hp * 256 + c * 64: hp * 256 + (c + 1) * 64], self.YBD[:, hp, :],
                        self.CST[:, CC['ISTK']:CC['ISTK'] + 64])
        for hp in range(3):
            self.stt(ft["t1"][:, hp, :], self.PA[:, hp * 256:(hp + 1) * 256], self.ppc(l, 'R_NORM', hp), ft["bon"][:, hp, :], ALU.mult, ALU.add)
            self.tt('dve', self.MIXT[:, 3 + hp, :], ft["t1"][:, hp, :], self.RZT[:, hp, :], ALU.mult)

    def bd_fill(self, bd, src_of_par, eng='pool'):
        self.memset(eng, bd[:, :, :, :], 0.0)
        for par in range(2):
            self.cp(eng, self.V(bd, par * 64, 64, par * 64, [[512, 3], [128, 4], [1, 64]]), src_of_par(par))

    def rwkv_dir(self, l, d, seq_idx, prompt, last):
        S = self.S
        rt = self.rt
        pb_d = 64 * d
        f3 = lambda t: t[:, :, :]
        v4 = lambda t: self.V(t, 0, 128, 0, [[256, 3], [64, 4], [1, 64]])
        self.act(self.TWL[pb_d:pb_d + 64, :], self.BLK[pb_d:pb_d + 64, 9, :], AF.Tanh)
        for hp in range(3):
            self.mm(self.PA[:, hp * 256:(hp + 1) * 256], self.LORA[pb_d:pb_d + 64, l, 0, hp * 128:(hp + 1) * 128], self.TWL[pb_d:pb_d + 64, :])
        for hp in range(3):
            self.act(rt["sg"][:, hp, :], self.PA[:, hp * 256:(hp + 1) * 256], AF.Sigmoid, bias=self.ppc(l, 'R_W0', d * 3 + hp))
        for hp in range(3):
            self.mm(self.PB[:, hp * 256:(hp + 1) * 256], self.LORA[pb_d:pb_d + 64, l, 1, hp * 128:(hp + 1) * 128], self.BLK[pb_d:pb_d + 64, 10, :])
        for hp in range(3):
            self.act(rt["aa"][:, hp, :], self.PB[:, hp * 256:(hp + 1) * 256], AF.Sigmoid, bias=self.ppc(l, 'R_A0', d * 3 + hp))
        for hp in range(3):
            self.ts('dve', rt["kt"][:, hp, :], rt["aa"][:, hp, :], self.ppc(l, 'R_KA', hp), self.dvc(l, 'OMKA', hp), ALU.mult, ALU.add)
        self.tt('dve', f3(rt["kt"]), f3(rt["kt"]), self.BLK[:, 3:6, :], ALU.mult)
        self.tt('dve', f3(rt["bb"]), self.KH[:, :, :], f3(rt["aa"]), ALU.mult)
        flat = lambda t: self.V(t, 0, 128, 0, [[1, 768]])
        self.scan(flat(rt["cs"]), self.CST[:, CC['RMASK']:CC['RMASK'] + 768], flat(rt["sg"]), 0.0, ALU.mult, ALU.add)
        self.cp('dve', self.GL[:, :, :], self.V(rt["cs"], 0, 128, 63, [[256, 3], [64, 4]]))
        if d == 1:
            csLb = self.V(rt["cs"], 0, 128, 63, [[256, 3], [64, 4], [0, 64]])
            self.tt('dve', v4(rt["d"]), csLb, v4(rt["cs"]), ALU.subtract)
            self.tt('dve', f3(rt["cs"]), f3(rt["d"]), f3(rt["sg"]), ALU.add)
        self.act(f3(rt["E"]), f3(rt["cs"]), AF.Exp, scale=-DSC)
        self.tt('dve', self.V(self.KR, 0, 128, 64, [[512, 3], [128, 4], [1, 64]]),
                self.V(self.BLK, 0, 128, 0, [[256, 3], [64, 4], [1, 64]]), v4(rt["E"]), ALU.mult)
        self.tt('dve', f3(rt["d"]), f3(rt["cs"]), f3(rt["sg"]), ALU.subtract)
        self.act(f3(rt["E"]), f3(rt["d"]), AF.Exp, scale=-DSC)
        self.tt('dve', self.V(self.KR, 0, 128, 0, [[512, 3], [128, 4], [1, 64]]), v4(self.KH), v4(rt["E"]), ALU.mult)
        self.act(f3(rt["E"]), f3(rt["cs"]), AF.Exp, scale=DSC)
        self.tt('dve', self.BTT[:, :, :], f3(rt["bb"]), f3(rt["E"]), ALU.mult)
        self.tt('dve', self.KTT[:, :, :], f3(rt["kt"]), f3(rt["E"]), ALU.mult)
        self.act(self.GL[:, :, :], self.GL[:, :, :], AF.Exp, scale=-DSC)
        BD = self.BD
        c4 = lambda t, par: self.V(t, par * 64, 64, 0, [[256, 3], [64, 4], [1, 64]])
        self.bd_fill(BD["kt"], lambda par: c4(self.KTT, par))
        self.bd_fill(BD["b"], lambda par: c4(self.BTT, par))
        self.bd_fill(BD["kh"], lambda par: self.V(self.KR, par * 64, 64, 0, [[512, 3], [128, 4], [1, 64]]))
        self.bd_fill(BD["r"], lambda par: self.V(self.KR, par * 64, 64, 64, [[512, 3], [128, 4], [1, 64]]))
        for (src, dst, neg) in ((BD["kt"], self.KTTOK, False), (BD["b"], self.BTTOK, True)):
            for half in range(2):
                for cc_ in range(2):
                    c = half * 2 + cc_
                    for hp in range(3):
                        self.tr(self.PTB[:, (cc_ * 3 + hp) * 128:(cc_ * 3 + hp + 1) * 128], src[:, hp, c, :], bf=True)
                o = self.V(dst, 0, 128, half * 768, [[1, 768]])
                if neg:
                    self.act(o, self.PTB[:, 0:768], AF.Copy, scale=-1.0)
                else:
                    self.cp('dve', o, self.PTB[:, 0:768])
        self.cp('act', self.HBF[:, :, :], self.rH[d][:, :, :])
        self.memset('pool', self.A1BD[:, :, :, :], 0.0)
        self.memset('pool', self.A2BD[:, :, :, :], 0.0)
        self.memset('pool', self.NN[0][:, :, :], 0.0)
        mk = CC['MKF'] if d == 0 else CC['MKB']
        nmk = CC['NMKF'] if d == 0 else CC['NMKB']
        mn = CC['MNF'] if d == 0 else CC['MNB']
        for j in range(4):
            c = j if d == 0 else 3 - j
            cs = slice(c * 64, (c + 1) * 64)
            KRc = lambda hp: self.V(self.KR, 0, 128, hp * 512 + c * 128, [[1, 128]])
            p1, p2, p3 = self.pq(), self.pq(), self.pq()
            for hp in range(3):
                self.mm(p1[:, hp * 128:(hp + 1) * 128], BD["kt"][:, hp, c, :], KRc(hp), inc=(hp == 2))
            for hp in range(3):
                self.mm(p2[:, hp * 128:(hp + 1) * 128], BD["b"][:, hp, c, :], KRc(hp), inc=(hp == 2))
            for hp in range(3):
                self.mm(p3[:, hp * 64:(hp + 1) * 64], BD["kh"][:, hp, c, :], self.BTT[:, hp, cs], inc=(hp == 2))
            for par in range(2):
                pp_ = par * 64
                self.tt('dve', self.V(self.A1BD, pp_, 64, pp_, [[256, 3], [128, 2], [1, 64]]),
                        self.V(p1, pp_, 64, 0, [[128, 3], [64, 2], [1, 64]]),
                        self.V(self.CST, pp_, 64, mk, [[0, 3], [64, 2], [1, 64]]), ALU.mult)
                self.tt('dve', self.V(self.A2BD, pp_, 64, pp_, [[256, 3], [128, 2], [1, 64]]),
                        self.V(p2, pp_, 64, 0, [[128, 3], [64, 2], [1, 64]]),
                        self.V(self.CST, pp_, 64, nmk, [[0, 3], [64, 2], [1, 64]]), ALU.mult)
                self.tt('dve', self.V(self.NN[0], pp_, 64, pp_, [[128, 3], [1, 64]]),
                        self.V(p3, pp_, 64, 0, [[64, 3], [1, 64]]),
                        self.V(self.CST, pp_, 64, mn, [[0, 3], [1, 64]]), ALU.mult)
            pr_ = self.pq()
            for hp in range(3):
                o = pr_[:, hp * 64:(hp + 1) * 64]
                self.mm(o, BD["kh"][:, hp, c, :], self.HBF[:, hp, :], start=True, stop=False)
                self.mm(o, self.A1BD[:, hp, 0, :], self.VSTK[:, c, hp, :], start=False, stop=True, inc=(hp == 2))
            u192 = self.V(self.U, 0, 128, 0, [[1, 192]])
            ub192 = self.V(self.UBF, 0, 128, 0, [[1, 192]])
            self.cp('dve', u192, pr_[:, 0:192])
            self.cp('act', ub192, u192)
            for k in range(6):
                NTk = self.A2BD[:, :, 0, :] if k == 0 else self.NTT[k % 2][:, :, :]
                Nk = self.NN[k % 2]
                pu = self.pq()
                for hp in range(3):
                    self.mm(pu[:, hp * 64:(hp + 1) * 64], NTk[:, hp, :], self.UBF[:, hp, :], inc=(hp == 2))
                if k < 5:
                    pn = self.pq()
                    for hp in range(3):
                        self.mm(pn[:, hp * 128:(hp + 1) * 128], NTk[:, hp, :], Nk[:, hp, :], inc=(hp == 2))
                    pnt = self.pq()
                    for hp in range(3):
                        self.mm(pnt[:, hp * 128:(hp + 1) * 128], Nk[:, hp, :], NTk[:, hp, :], inc=(hp == 2))
                self.tt('dve', u192, u192, pu[:, 0:192], ALU.add)
                self.cp('act', ub192, u192)
                if k < 5:
                    self.cp('act', self.V(self.NN[(k + 1) % 2], 0, 128, 0, [[1, 384]]), pn[:, 0:384])
                    self.cp('dve', self.V(self.NTT[(k + 1) % 2], 0, 128, 0, [[1, 384]]), pnt[:, 0:384])
            py = self.pq()
            for hp in range(3):
                o = py[:, hp * 64:(hp + 1) * 64]
                self.mm(o, BD["r"][:, hp, c, :], self.HBF[:, hp, :], start=True, stop=False)
                self.mm(o, self.A1BD[:, hp, 1, :], self.VSTK[:, c, hp, :], start=False, stop=False)
                self.mm(o, self.A2BD[:, hp, 1, :], self.UBF[:, hp, :], start=False, stop=True, inc=(hp == 2))
            ybv = self.V(self.YB, 0, 128, c * 192, [[1, 192]])
            if d == 1:
                self.cp('act', ybv, py[:, 0:192])
            else:
                self.tt('dve', ybv, ybv, py[:, 0:192], ALU.add)
            ph = self.pq()
            for hp in range(3):
                o = ph[:, hp * 64:(hp + 1) * 64]
                self.mm(o, self.KTTOK[:, c, hp, :], self.VSTK[:, c, hp, :], start=True, stop=False)
                self.mm(o, self.BTTOK[:, c, hp, :], self.UBF[:, hp, :], start=False, stop=True, inc=(hp == 2))
            self.tt('dve', self.HTMP[:, :, :], self.rH[d][:, :, :], self.V(ph, 0, 128, 0, [[64, 3], [1, 64]]), ALU.add)
            self.tt('dve', self.rH[d][:, :, :], self.HTMP[:, :, :], self.V(self.GL, 0, 128, c, [[4, 3], [0, 64]]), ALU.mult)
            self.cp('act', self.HBF[:, :, :], self.rH[d][:, :, :])
        if last[d] and prompt:
            S.dma('sp', self.o_rH[seq_idx, l, d], self.rH[d][:, :, :])

    def stage_out(self, l, mod, xsrc, xdst, r0):
        S = self.S
        for tt_ in range(2):
            o, xw = self.XN[tt_], self.XW[tt_]
            S.dma('sp', xw[:, :], xsrc[r0 + tt_ * 128: r0 + (tt_ + 1) * 128, :])
            for ch in range(2):
                ps = self.pq()
                for kc in range(8):
                    self.mm(ps[:, :], self.MIXT[:, kc, tt_ * 128:(tt_ + 1) * 128], self.W_OUT[:, kc, ch * 512:(ch + 1) * 512],
                            start=(kc == 0), stop=(kc == 7))
                self.cp('act' if ch else 'dve', o[:, ch * 512:(ch + 1) * 512], ps[:, :])
            self.dump('o_proj%d' % tt_, o[:, :])
            self.dump('o_x%d' % tt_, xw[:, :])
            ssq = self.TMPS[:, 24 + tt_:25 + tt_]
            junk = self.V(self.BLK, 0, 128, 0, [[1, D]])
            self.act(junk, o[:, :], AF.Square, accum=ssq)
            rs = self.TMPS[:, 26 + tt_:27 + tt_]
            self.ts('dve', rs, ssq, 1.0 / D, EPS, ALU.mult, ALU.add)
            self.act(rs, rs, AF.Sqrt)
            self.recip(rs, rs)
            self.dump('o_rs%d' % tt_, rs)
            self.stt(o[:, :], o[:, :], rs, self.GATEB[:, mod, :], ALU.mult, ALU.mult)
            self.dump('o_g%d' % tt_, o[:, :])
            self.tt('dve', o[:, :], o[:, :], xw[:, :], ALU.add)
            S.dma('sp', xdst[r0 + tt_ * 128: r0 + (tt_ + 1) * 128, :], o[:, :])

    def build(self, layers=(0, 1)):
        try:
            self._build(layers)
        except StopIteration:
            pass
        self.S.finish('sp')
        return self.nc

    def _build(self, layers):
        NP, TS = self.NP, self.TS
        self.setup()
        if self.stop == 'setup':
            raise StopIteration
        for li, l in enumerate(layers):
            self.load_layer(l)
            if self.stop == 'load':
                raise StopIteration
            xsrc = self.x_in if li == 0 else self.x1
            xdst = self.y_out if li == len(layers) - 1 else self.x1
            T_, F_ = {0: True, 1: True}, {0: False, 1: False}
            for s in range(NP):
                self.visit(l, 0, xsrc, xdst, s * 256, 256, 0, [0, 1], False, s, T_, T_)
            if TS > 0:
                nb = TS // TB
                row0 = NP * 256
                for b in range(nb - 1, -1, -1):
                    self.visit(l, 1, xsrc, xdst, row0, TS, b * TB, [1], True, 0,
                               {0: False, 1: b == nb - 1}, {0: False, 1: b == 0})
                for b in range(nb):
                    self.visit(l, 1, xsrc, xdst, row0, TS, b * TB, [0], True, 0,
                               {0: b == 0, 1: False}, {0: b == nb - 1, 1: False})


def prep_shared(inp):
    f = lambda a: np.ascontiguousarray(np.asarray(a, dtype=np.float32))
    b_mod = f(inp['b_mod'])
    sh = {}
    sh['w_mod'] = f(inp['w_mod'])
    sh['w_in'] = f(inp['w_in'])
    sh['w_out'] = f(inp['w_out'])
    sh['bmodT'] = f(b_mod[:, :2048].reshape(2, 16, 128).transpose(2, 0, 1))
    sh['bmodg'] = f(b_mod[:, 2048:3072])
    sh['gpost'] = f(inp['g_post'])
    pp = np.zeros((128, 2 * PL), np.float32)
    for l in range(2):
        o = l * PL
        def put(key, arr, n):
            pp[:, o + PC[key]: o + PC[key] + n] = np.asarray(arr, np.float32).reshape(n, 128).T
        put('G_PRE', inp['g_pre'][l], 8)
        put('M_NORM', inp['m_norm'][l], 3)
        put('R_MU', inp['r_mu'][l], 11)
        put('R_W0', np.asarray(inp['r_w0'][l]).reshape(-1), 6)
        put('R_A0', np.asarray(inp['r_a0'][l]).reshape(-1), 6)
        put('R_KK', inp['r_kk'][l], 3)
        put('R_KA', inp['r_ka'][l], 3)
        put('R_RK', inp['r_rk'][l], 3)
        put('R_NORM', inp['r_norm'][l], 3)
        put('L_CONV', np.asarray(inp['l_conv'][l]).reshape(-1), 8)
        put('L_CONVB', inp['l_conv_b'][l], 2)
        put('L_BA', np.asarray(inp['l_ba'][l]).reshape(-1), 4)
        put('L_BX', np.asarray(inp['l_bx'][l]).reshape(-1), 4)
        put('L_LAM', np.asarray(inp['l_lambda'][l]).reshape(-1), 4)
        pp[0:6, o + PC['M_BI']: o + PC['M_BI'] + 2] = np.asarray(inp['m_bi'][l], np.float32).T
        pp[0:6, o + PC['M_BF']: o + PC['M_BF'] + 2] = np.asarray(inp['m_bf'][l], np.float32).T
    sh['pp'] = pp
    sh['cst'] = make_consts()
    lora = np.zeros((128, 2, 2, 384), np.float32)
    for wi, key in enumerate(['r_w2', 'r_a2']):
        a = np.asarray(inp[key], np.float32)
        lora[:, :, wi, :] = a.transpose(1, 2, 0, 3).reshape(128, 2, 384)
    sh['lora'] = lora
    lruw = np.zeros((128, 2, 8, 128), np.float32)
    for gi, key in enumerate(['l_wa', 'l_wx']):
        a = np.asarray(inp[key], np.float32)
        for l in range(2):
            for d in range(2):
                for pr in range(2):
                    for hb in range(2):
                        n = 2 * pr + hb
                        lruw[hb * 64:(hb + 1) * 64, l, (gi * 2 + d) * 2 + pr, hb * 64:(hb + 1) * 64] = a[l, d, n]
    sh['lruw'] = lruw
    return sh


def prep_core(inp, b, NP, TS):
    f = lambda a: np.ascontiguousarray(np.asarray(a, dtype=np.float32))
    m = {}
    xp = np.asarray(inp['x_prompt'], np.float32)[b * NP:(b + 1) * NP].reshape(NP * 256, D)
    if TS > 0:
        xs = np.asarray(inp['x_sample'], np.float32)[b]
        m['x_in'] = f(np.concatenate([xp, xs], 0))
    else:
        m['x_in'] = f(xp)
    cc = np.stack([np.asarray(inp['c_ctx'], np.float32), np.asarray(inp['c'], np.float32)[b]], -1)
    m['cc'] = f(cc.reshape(8, 128, 2).transpose(1, 0, 2))
    C = np.asarray(inp['state_mlstm_C'], np.float32)[b]
    n = np.asarray(inp['state_mlstm_n'], np.float32)[b]
    Cn = np.concatenate([C, n[..., None]], -1)
    Cn = Cn.reshape(2, 2, 3, 2, 64, 65).transpose(0, 1, 3, 4, 2, 5).reshape(2, 2, 128, 3, 65)
    m['st_mC'] = f(Cn)
    m['st_mm'] = f(np.asarray(inp['state_mlstm_m'], np.float32)[b].transpose(2, 0, 1))
    R = np.asarray(inp['state_rwkv'], np.float32)[b]
    R = R.transpose(0, 1, 2, 4, 3)
    R = R.reshape(2, 2, 3, 2, 64, 64).transpose(0, 1, 3, 4, 2, 5).reshape(2, 2, 128, 3, 64)
    m['st_rH'] = f(R)
    L = np.asarray(inp['state_rglru'], np.float32)[b]
    m['st_l'] = f(L.reshape(2, 2, 2, 128).transpose(3, 0, 1, 2))
    return m


def unpack_core(r, NP, TS):
    y = r['y_out']
    yp = y[:NP * 256].reshape(NP, 256, D)
    ys = y[NP * 256:]
    mC = r['o_mC'].reshape(NP, 2, 2, 2, 64, 3, 65).transpose(0, 1, 2, 5, 3, 4, 6).reshape(NP, 2, 2, 6, 64, 65)
    newC = np.ascontiguousarray(mC[..., :64])
    newn = np.ascontiguousarray(mC[..., 64])
    newm = r['o_mm'].reshape(NP, 2, 2, 6)
    rH = r['o_rH'].reshape(NP, 2, 2, 2, 64, 3, 64).transpose(0, 1, 2, 5, 3, 4, 6).reshape(NP, 2, 2, 6, 64, 64)
    newr = np.ascontiguousarray(rH.transpose(0, 1, 2, 3, 5, 4))
    newl = np.ascontiguousarray(r['o_l'].transpose(0, 1, 2, 4, 3).reshape(NP, 2, 2, 256))
    return yp, ys, newC, newn, newm, newr, newl


_NC_CACHE = {}


def kernel(**inputs):
    NP, TS = 4, 2048
    key = (NP, TS)
    if key not in _NC_CACHE:
        _NC_CACHE[key] = Builder(NP, TS).build()
    nc = _NC_CACHE[key]
    sh = prep_shared(inputs)
    in_maps = []
    for b in range(NCORES):
        m = dict(sh)
        m.update(prep_core(inputs, b, NP, TS))
        in_maps.append(m)
    res = run_bass_kernel_spmd(nc, in_maps, core_ids=list(range(NCORES)))
    outs = [unpack_core(r, NP, TS) for r in res.results]
    y_prompt = np.concatenate([o[0] for o in outs], 0)
    y_sample = np.stack([o[1] for o in outs], 0)
    cat = lambda i: np.concatenate([o[i] for o in outs], 0)
    return (y_prompt.astype(np.float32), y_sample.astype(np.float32), cat(2).astype(np.float32),
            cat(3).astype(np.float32), cat(4).astype(np.float32), cat(5).astype(np.float32), cat(6).astype(np.float32))
```

```python
import numpy as np
import concourse.bass as bass
import concourse.mybir as mybir
from concourse.bass_utils import run_bass_kernel_spmd

F32 = mybir.dt.float32
BF16 = mybir.dt.bfloat16
AF = mybir.ActivationFunctionType
ALU = mybir.AluOpType
AX = mybir.AxisListType

D = 1024
IN_COLS = 4248
EPS = 1e-6
DSC = 0.6065306597126334
TB = 256
NCORES = 8


def _prod(xs):
    r = 1
    for x in xs:
        r *= int(x)
    return r


class Sync:
    def __init__(self, nc, n_dma_sems=32):
        self.nc = nc
        self.engs = {'pe': nc.tensor, 'dve': nc.vector, 'act': nc.scalar,
                     'pool': nc.gpsimd, 'sp': nc.sync}
        self.sem = {}
        self.cnt = {}
        for e in ['pe', 'dve', 'act', 'pool']:
            self.sem[e] = nc.alloc_semaphore('sem_' + e)
            self.cnt[e] = 0
        self.seen = {e: {} for e in self.engs}
        self.dma_ring = [nc.alloc_semaphore('dq_%d' % i) for i in range(n_dma_sems)]
        self.dma_uses = [0] * n_dma_sems
        self.dma_next = 0
        self.rec = {}
        self.untracked = set()
        self.n_wait = 0
        self.n_ins = 0
        self.pstep_cache = {}
        self.sb_addr = {}

    def region(self, ap):
        t = ap.tensor
        name = t.name
        apl = [(int(s), int(c)) for (s, c) in ap.ap]
        off = int(ap.offset)
        if type(t).__name__.startswith('DRam'):
            lo = off + sum(min(0, s * (c - 1)) for s, c in apl)
            hi = off + sum(max(0, s * (c - 1)) for s, c in apl) + 1
            return (name, 0, 1, lo, hi)
        pstep = self.pstep_cache.get(name)
        if pstep is None:
            pstep = _prod(list(t.shape)[1:])
            self.pstep_cache[name] = pstep
        p0 = off // pstep
        f0 = off % pstep
        npart = apl[0][1]
        rest = apl[1:]
        lo = f0 + sum(min(0, s * (c - 1)) for s, c in rest)
        hi = f0 + sum(max(0, s * (c - 1)) for s, c in rest) + 1
        if name in self.sb_addr:
            base, es = self.sb_addr[name]
            return ('SB', p0, p0 + npart, base + lo * es, base + hi * es)
        return ('PS:' + name, (p0 // 32) * 32, ((p0 + npart + 31) // 32) * 32, (lo // 512) * 512, ((hi + 511) // 512) * 512)

    @staticmethod
    def _ovl(a, b):
        return a[1] < b[2] and b[1] < a[2] and a[3] < b[4] and b[3] < a[4]

    @staticmethod
    def _contains(a, b):
        return a[1] <= b[1] and b[2] <= a[2] and a[3] <= b[3] and b[4] <= a[4]

    def _collect(self, e, reads, writes):
        deps = {}
        own = self.sem.get(e)
        rregs = [self.region(a) for a in reads]
        wregs = [self.region(a) for a in writes]
        for r in rregs:
            if r[0] in self.untracked:
                continue
            isps = r[0].startswith('PS:')
            for (reg, kind, sem, val) in self.rec.get(r[0], ()):
                if (kind == 'w' or (isps and sem is not own)) and self._ovl(reg, r):
                    if e == 'pe' and sem is own:
                        continue
                    k = id(sem)
                    if deps.get(k, (None, 0))[1] < val:
                        deps[k] = (sem, val)
        for w in wregs:
            if w[0] in self.untracked:
                continue
            for (reg, kind, sem, val) in self.rec.get(w[0], ()):
                if self._ovl(reg, w):
                    if sem is own:
                        continue
                    k = id(sem)
                    if deps.get(k, (None, 0))[1] < val:
                        deps[k] = (sem, val)
        return deps, rregs, wregs

    def _record(self, rregs, wregs, sem, val):
        for r in rregs:
            if r[0] in self.untracked:
                continue
            lst = self.rec.setdefault(r[0], [])
            lst[:] = [x for x in lst if not (x[1] == 'r' and x[2] is sem and self._contains(r, x[0]))]
            lst.append((r, 'r', sem, val))
        for w in wregs:
            if w[0] in self.untracked:
                continue
            lst = self.rec.setdefault(w[0], [])
            lst[:] = [x for x in lst if not self._contains(w, x[0])]
            lst.append((w, 'w', sem, val))

    def wait(self, e, sem, val):
        k = id(sem)
        if self.seen[e].get(k, 0) >= val:
            return
        self.engs[e].wait_ge(sem, val)
        self.seen[e][k] = val
        self.n_wait += 1

    max_ins = None
    paranoid = False

    def emit(self, e, reads, writes, build, inc=True):
        if self.max_ins is not None and self.n_ins >= self.max_ins:
            raise StopIteration
        deps, rregs, wregs = self._collect(e, reads, writes)
        for (sem, val) in deps.values():
            self.wait(e, sem, val)
        if self.paranoid:
            for e2 in ['pe', 'dve', 'act', 'pool']:
                if self.cnt[e2] > 0 and not (e == 'pe' and e2 == 'pe'):
                    self.wait(e, self.sem[e2], self.cnt[e2])
        ins = build(self.engs[e])
        self.n_ins += 1
        if inc:
            self.cnt[e] += 1
            ins.then_inc(self.sem[e], 1)
            val = self.cnt[e]
        else:
            val = self.cnt[e] + 1
        self._record(rregs, wregs, self.sem[e], val)
        return ins

    def dma(self, q, out, in_, **kw):
        if self.max_ins is not None and self.n_ins >= self.max_ins:
            raise StopIteration
        i = self.dma_next
        self.dma_next = (i + 1) % len(self.dma_ring)
        sem = self.dma_ring[i]
        uses = self.dma_uses[i]
        if uses > 0:
            self.wait(q, sem, 16 * uses)
        deps, rregs, wregs = self._collect(q, [in_], [out])
        for (s, v) in deps.values():
            self.wait(q, s, v)
        ins = self.engs[q].dma_start(out=out, in_=in_, **kw)
        ins.then_inc(sem, 16)
        self.n_ins += 1
        self.dma_uses[i] = uses + 1
        self._record(rregs, wregs, sem, 16 * (uses + 1))
        return ins

    def finish(self, q='sp'):
        for i, sem in enumerate(self.dma_ring):
            if self.dma_uses[i] > 0:
                self.wait(q, sem, 16 * self.dma_uses[i])
        for e in ['pe', 'dve', 'act', 'pool']:
            if self.cnt[e] > 0:
                self.wait(q, self.sem[e], self.cnt[e])


class Arena:
    def __init__(self, base, size):
        self.base, self.size, self.ptr = base, size, 0

    def take(self, nbytes):
        off = (self.ptr + 31) // 32 * 32
        self.ptr = off + nbytes
        assert self.ptr <= self.size, ("arena overflow", self.ptr, self.size)
        return self.base + off


PL = 72
PC = dict(G_PRE=0, M_NORM=8, R_MU=11, R_W0=22, R_A0=28, R_KK=34, R_KA=37, R_RK=40, R_NORM=43,
          L_CONV=46, L_CONVB=54, L_BA=56, L_BX=60, L_LAM=64, M_BI=68, M_BF=70)
DL = 32
DC = dict(OMKA=0, CLAM=3, C2LAM=7, NBF=11, OMMU=16)
CC = dict(IDENT=0, BONES=128, MKF=256, MKB=384, MNF=512, MNB=576, MUI=640, MLI=704, RMASK=768,
          LSEL=1536, PSEL=1664, NMKF=1668, NMKB=1796, ONES=1924, ISTK=2052)
NCST = 2052 + 64


def make_consts():
    c = np.zeros((128, NCST), np.float32)
    c[:, 0:128] = np.eye(128)
    c[0:64, 128:192] = 1.0
    c[64:128, 192:256] = 1.0
    s = np.arange(64)[:, None]
    t = np.arange(64)[None, :]
    us, ui = (s < t).astype(np.float32), (s <= t).astype(np.float32)
    ls, li = (s > t).astype(np.float32), (s >= t).astype(np.float32)
    c[0:64, 256:320], c[0:64, 320:384] = us, ui
    c[0:64, 384:448], c[0:64, 448:512] = ls, li
    c[0:64, 512:576] = -ls
    c[0:64, 576:640] = -us
    c[0:64, 640:704] = ui
    c[0:64, 704:768] = li
    rm = np.ones(768, np.float32)
    rm[::64] = 0.0
    c[:, 768:1536] = rm[None, :]
    for k in range(6):
        c[k, 1536 + (k % 2) * 64: 1536 + (k % 2) * 64 + 64] = 1.0
        c[k, 1664 + k // 2] = 1.0
    c[0:64, 1668:1796] = -c[0:64, 256:384]
    c[0:64, 1796:1924] = -c[0:64, 384:512]
    c[:, 1924:2052] = 1.0
    for (a, b) in ((256, 768), (1668, 1924)):
        c[64:128, a:b] = c[0:64, a:b]
    c[0:64, 2052:2116] = np.eye(64)
    c[64:128, 2052:2116] = np.eye(64)
    return c


class Builder:
    def __init__(self, NP, TS, debug=False, stop=None):
        self.stop = stop
        self.NP, self.TS = NP, TS
        self.NTOK = NP * 256 + TS
        self.debug = debug
        nc = self.nc = bass.Bass("TRN2", target_bir_lowering=False)
        self.S = Sync(nc)
        self._decl_dram()
        self._alloc()

    def _decl_dram(self):
        nc, NP, TS = self.nc, self.NP, self.TS
        di = lambda n, s: nc.dram_tensor(n, list(s), F32, kind="ExternalInput").ap()
        do = lambda n, s: nc.dram_tensor(n, list(s), F32, kind="ExternalOutput").ap()
        dx = lambda n, s: nc.dram_tensor(n, list(s), F32, kind="Internal").ap()
        self.x_in = di("x_in", [self.NTOK, D])
        self.cc = di("cc", [128, 8, 2])
        self.w_mod = di("w_mod", [2, D, 3 * D])
        self.bmodT = di("bmodT", [128, 2, 16])
        self.bmodg = di("bmodg", [2, D])
        self.gpost = di("gpost", [2, D])
        self.w_in = di("w_in", [2, D, IN_COLS])
        self.w_out = di("w_out", [2, D, D])
        self.pp = di("pp", [128, 2 * PL])
        self.cst = di("cst", [128, NCST])
        self.lora = di("lora", [128, 2, 2, 384])
        self.lruw = di("lruw", [128, 2, 8, 128])
        self.st_mC = di("st_mC", [2, 2, 128, 3, 65])
        self.st_mm = di("st_mm", [6, 2, 2])
        self.st_rH = di("st_rH", [2, 2, 128, 3, 64])
        self.st_l = di("st_l", [128, 2, 2, 2])
        for n in ["x_in", "cc", "w_mod", "bmodT", "bmodg", "gpost", "w_in", "w_out", "pp", "cst", "lora",
                  "lruw", "st_mC", "st_mm", "st_rH", "st_l"]:
            self.S.untracked.add(n)
        self.y_out = do("y_out", [self.NTOK, D])
        self.o_mC = do("o_mC", [NP, 2, 2, 128, 3, 65])
        self.o_mm = do("o_mm", [NP, 2, 2, 6, 1])
        self.o_rH = do("o_rH", [NP, 2, 2, 128, 3, 64])
        self.o_l = do("o_l", [NP, 2, 2, 128, 2])
        self.x1 = dx("x1", [self.NTOK, D])
        self.sHB = dx("sHB", [max(TS, 64), 384])
        self.sYB = dx("sYB", [max(TS // 256, 1), 128, 768])
        self.sLB = dx("sLB", [128, 2, max(TS, 64)])
        if self.debug:
            self.dbg = do("dbg", [128, 32768])
            self.dbg_map = {}
            self.dbg_off = 0

    def T(self, name, shape, dtype, arena):
        es = 2 if dtype == BF16 else 4
        nb = _prod(shape[1:]) * es
        off = arena.take(nb)
        t = self.nc.alloc_sbuf_tensor_at(name, list(shape), dtype, offset=off)
        self.S.sb_addr[t.name] = (off, es)
        return t

    def _alloc(self):
        nc = self.nc
        B0 = 16384 + 256
        LIM = 224 * 1024 - 256
        P = Arena(B0, LIM - B0)
        T = self.T
        self.W_IN = T("W_IN", [128, 8, IN_COLS], BF16, P)
        self.W_OUT = T("W_OUT", [128, 8, D], BF16, P)
        self.CST = T("CST", [128, NCST], F32, P)
        self.PP = T("PP", [128, 2 * PL], F32, P)
        self.DV = T("DV", [128, 2 * DL], F32, P)
        self.LORA = T("LORA", [128, 2, 2, 384], F32, P)
        self.LRUW = T("LRUW", [128, 2, 8, 128], F32, P)
        self.RKD = T("RKD", [128, 2, 3, 128], F32, P)
        self.GS = T("GS", [128, 2, 8], F32, P)
        self.SH = T("SH", [128, 2, 8], F32, P)
        self.GATEB = T("GATEB", [128, 2, D], F32, P)
        self.IDB = T("IDB", [128, 128], BF16, P)
        self.ISTKB = T("ISTKB", [128, 64], BF16, P)
        self.mC = [T("mC%d" % d, [128, 3, 65], F32, P) for d in range(2)]
        self.mM = [T("mM%d" % d, [6, 1], F32, P) for d in range(2)]
        self.rH = [T("rH%d" % d, [128, 3, 64], F32, P) for d in range(2)]
        self.lS = [T("lS%d" % d, [128, 2], F32, P) for d in range(2)]
        XB = Arena(P.take(16384), 16384)
        self.HT = T("HT", [128, 8, 384], BF16, P)
        self.MIXT = T("MIXT", [128, 8, 256], BF16, P)
        self.BLK = T("BLK", [128, 11, 256], F32, P)
        abase = P.take(0)
        asize = P.size - P.ptr
        self.asize = asize
        mk = lambda: Arena(abase, asize)
        a = Arena(XB.base, XB.size)
        self.XW = [T("XW%d" % i, [128, D], F32, a) for i in range(2)]
        self.XN = [T("XN%d" % i, [128, D], F32, a) for i in range(2)]
        a = Arena(XB.base, XB.size)
        self.YB = T("YB", [128, 4, 3, 64], F32, a)
        self.RZT = T("RZT", [128, 3, 256], F32, a)
        self.WSTG = [T("WSTG0", [128, 8, 256], F32, Arena(self.S.sb_addr[self.BLK.name][0], 11264)),
                     T("WSTG1", [128, 8, 256], F32, Arena(XB.base, 8192)),
                     T("WSTG2", [128, 8, 256], F32, Arena(XB.base + 8192, 8192))]
        a = mk()
        self.WM = T("WM", [128, 8, 512], F32, a)
        self.SCT = T("SCT", [128, 8, 2], F32, a)
        self.SCB = T("SCB", [128, 2, 8, 128], F32, a)
        self.MODT = T("MODT", [128, 16, 2], F32, a)
        self.BMT = T("BMT", [128, 2, 16], F32, a)
        self.BG = T("BG", [128, D], F32, a)
        self.GP = T("GP", [128, D], F32, a)
        self.TMPS = T("TMPS", [128, 32], F32, a)
        a = mk()
        self.QT = T("QT", [128, 3, 256], BF16, a)
        self.KT = T("KT", [128, 3, 256], BF16, a)
        self.GOZ = T("GOZ", [128, 3, 256], F32, a)
        self.KTOK = T("KTOK", [64, 4, 384], BF16, a)
        self.VAUG = T("VAUG", [64, 4, 6, 65], BF16, a)
        self.HB = T("HB", [64, 4, 384], F32, a)
        self.mg = []
        for d in range(2):
            g = {}
            for n in ["gi", "lf", "pre", "bb", "gg", "ee", "fl"]:
                g[n] = T("mg_%s%d" % (n, d), [6, 256], F32, a)
            for n in ["mx", "mch", "mprev", "MM", "dec"]:
                g[n] = T("mg_%s%d" % (n, d), [6, 4], F32, a)
            g["X2"] = T("mg_X2%d" % d, [6, 4, 3], F32, a)
            g["etok"] = T("mg_etok%d" % d, [64, 4, 6], F32, a)
            g["fltok"] = T("mg_fltok%d" % d, [64, 4, 6], F32, a)
            g["decb"] = T("mg_decb%d" % d, [128, 4, 3], F32, a)
            self.mg.append(g)
        self.STSB = [T("STSB%d" % i, [64, 6, 64], BF16, a) for i in range(2)]
        self.VP = [T("VP%d" % i, [64, 6, 65], BF16, a) for i in range(2)]
        self.CDEC = T("CDEC", [128, 3, 65], F32, a)
        self.CDBF = T("CDBF", [128, 3, 65], BF16, a)
        self.DN = T("DN", [64, 6], F32, a)
        self.RDN = T("RDN", [64, 6], F32, a)
        self.HD = T("HD", [64, 6, 64], F32, a)
        self.SQ = T("SQ", [64, 4, 384], F32, a)
        self.SSQ = T("SSQ", [64, 24], F32, a)
        self.RSTD = T("RSTD", [64, 24], F32, a)
        self.ZT = [T("ZT%d" % i, [128, 256], F32, a) for i in range(2)]
        a = mk()
        self.XL = T("XL", [128, 2, 392], F32, a)
        self.XC = T("XC", [128, 2, 256], F32, a)
        self.LZT = T("LZT", [128, 2, 256], F32, a)
        self.lt = {n: T("lt_" + n, [128, 2, 256], F32, a) for n in ["rg", "ig", "aa", "a2", "bt"]}
        self.HL = [T("HL%d" % d, [128, 2, 256], F32, a) for d in range(2)]
        a = mk()
        self.URS = T("URS", [128, 11, 384], F32, a)
        a = mk()
        self.KH = T("KH", [128, 3, 256], F32, a)
        R1 = a.take(7 * 3072)
        a1 = Arena(R1, 7 * 3072)
        self.rt = {n: T("rt_" + n, [128, 3, 256], F32, a1) for n in ["sg", "aa", "kt", "bb", "cs", "d", "E"]}
        a2 = Arena(R1, 7 * 3072)
        self.A1BD = T("A1BD", [128, 3, 2, 128], BF16, a2)
        self.A2BD = T("A2BD", [128, 3, 2, 128], BF16, a2)
        self.NN = [T("NN%d" % i, [128, 3, 128], BF16, a2) for i in range(2)]
        self.NTT = [T("NTT%d" % i, [128, 3, 128], BF16, a2) for i in range(2)]
        self.U = T("U", [128, 3, 64], F32, a2)
        self.UBF = T("UBF", [128, 3, 64], BF16, a2)
        self.HBF = T("HBF", [128, 3, 64], BF16, a2)
        self.HTMP = T("HTMP", [128, 3, 64], F32, a2)
        self.YBD = T("YBD", [128, 3, 128], F32, a2)
        self.KTTOK = T("KTTOK", [128, 4, 3, 128], BF16, a2)
        self.BTTOK = T("BTTOK", [128, 4, 3, 128], BF16, a2)
        self.rSSQ = T("rSSQ", [128, 12], F32, a2)
        self.rRSTD = T("rRSTD", [128, 12], F32, a2)
        self.KR = T("KR", [128, 3, 4, 2, 64], BF16, a)
        self.BTT = T("BTT", [128, 3, 256], BF16, a)
        self.KTT = T("KTT", [128, 3, 256], BF16, a)
        bdr = a.take(4 * 3072)
        a3 = Arena(bdr, 4 * 3072)
        self.BD = {n: T("BD_" + n, [128, 3, 4, 128], BF16, a3) for n in ["kt", "b", "kh", "r"]}
        a3 = Arena(bdr, 4 * 3072)
        self.ft = {n: T("ft_" + n, [128, 3, 256], F32, a3) for n in ["rk", "bon", "t1"]}
        self.YSQ = T("YSQ", [128, 4, 3, 64], F32, a3)
        self.BDV = T("BDV", [128, 3, 4, 128], BF16, a)
        self.VSTK = T("VSTK", [128, 4, 3, 64], BF16, a)
        self.TWL = T("TWL", [128, 256], F32, a)
        self.GL = T("GL", [128, 3, 4], F32, a)
        self.PA = nc.alloc_psum_tensor("PA", [128, 1024], F32)
        self.PB = nc.alloc_psum_tensor("PB", [128, 1024], F32)
        self.PQ = [nc.alloc_psum_tensor("PQ%d" % i, [128, 512], F32) for i in range(3)]
        self.PTB = nc.alloc_psum_tensor("PTB", [128, 1024], BF16)
        self.pq_i = 0

    def pq(self):
        t = self.PQ[self.pq_i % 3]
        self.pq_i += 1
        return t

    def V(self, t, p0, npart, off, dims):
        pstep = _prod(list(t.shape)[1:])
        return bass.AP(t, p0 * pstep + off, [[pstep, npart]] + [list(d) for d in dims])

    def tt(self, e, out, in0, in1, op):
        return self.S.emit(e, [in0, in1], [out], lambda g: g.tensor_tensor(out=out, in0=in0, in1=in1, op=op))

    def ts(self, e, out, in0, s1, s2, op0, op1=None):
        rd = [in0] + [s for s in (s1, s2) if not isinstance(s, (int, float)) and s is not None]
        if op1 is None:
            return self.S.emit(e, rd, [out], lambda g: g.tensor_scalar(out=out, in0=in0, scalar1=s1, scalar2=None, op0=op0))
        return self.S.emit(e, rd, [out], lambda g: g.tensor_scalar(out=out, in0=in0, scalar1=s1, scalar2=s2, op0=op0, op1=op1))

    def stt(self, out, in0, sc, in1, op0, op1):
        rd = [in0, in1] + ([] if isinstance(sc, (int, float)) else [sc])
        return self.S.emit('dve', rd, [out], lambda g: g.scalar_tensor_tensor(out=out, in0=in0, scalar=sc, in1=in1, op0=op0, op1=op1))

    def act(self, out, in_, func, bias=None, scale=None, accum=None):
        rd = [in_]
        kw = {}
        if bias is not None:
            kw['bias'] = bias
            if not isinstance(bias, (int, float)):
                rd.append(bias)
        if scale is not None:
            kw['scale'] = scale
            if not isinstance(scale, (int, float)):
                rd.append(scale)
        wr = [out]
        if accum is not None:
            kw['accum_out'] = accum
            wr.append(accum)
        return self.S.emit('act', rd, wr, lambda g: g.activation(out=out, in_=in_, func=func, **kw))

    def cp(self, e, out, in_):
        if e == 'act':
            return self.act(out, in_, AF.Copy)
        return self.S.emit(e, [in_], [out], lambda g: g.tensor_copy(out=out, in_=in_))

    def _pe_rowtile_guard(self, lhsT, out):
        S = self.S
        st = S.region(lhsT)
        k = st[2] - st[1]
        kr = 32 if k <= 32 else (64 if k <= 64 else 128)
        rows = (st[1], st[1] + kr)
        oreg = S.region(out)
        last = getattr(self, '_last_pe', None)
        if last is not None:
            lrows, loreg, lins, linc = last
            disjoint = rows[1] <= lrows[0] or lrows[1] <= rows[0]
            samebank = (loreg[0] == oreg[0]) and loreg[3] < oreg[4] and oreg[3] < loreg[4]
            if disjoint and samebank:
                if not linc:
                    S.cnt['pe'] += 1
                    lins.then_inc(S.sem['pe'], 1)
                S.wait('pe', S.sem['pe'], S.cnt['pe'])
        return rows, oreg

    def mm(self, out, lhsT, rhs, start=True, stop=True, inc=None):
        if inc is None:
            inc = stop
        rows, oreg = self._pe_rowtile_guard(lhsT, out)
        ins = self.S.emit('pe', [lhsT, rhs], [out],
                          lambda g: g.matmul(out, lhsT=lhsT, rhs=rhs, start=start, stop=stop), inc=inc)
        self._last_pe = (rows, oreg, ins, inc)
        return ins

    def tr(self, out, in_, inc=True, bf=False):
        n = in_.shape[0]
        ident = self.IDB[0:n, 0:n] if bf else self.CST[0:n, CC['IDENT']:CC['IDENT'] + n]
        rows, oreg = self._pe_rowtile_guard(in_, out)
        ins = self.S.emit('pe', [in_, ident], [out],
                          lambda g: g.transpose(out=out, in_=in_, identity=ident), inc=inc)
        self._last_pe = (rows, oreg, ins, inc)
        return ins

    def memset(self, e, ap, v):
        return self.S.emit(e, [], [ap], lambda g: g.memset(ap, v))

    def scan(self, out, d0, d1, init, op0, op1):
        rd = [d0, d1] + ([] if isinstance(init, (int, float)) else [init])
        return self.S.emit('dve', rd, [out], lambda g: g.tensor_tensor_scan(out=out, data0=d0, data1=d1, initial=init, op0=op0, op1=op1))

    def recip(self, out, in_):
        return self.S.emit('dve', [in_], [out], lambda g: g.reciprocal(out=out, in_=in_))

    def reduce(self, out, in_, op, axis=AX.X):
        return self.S.emit('dve', [in_], [out], lambda g: g.tensor_reduce(out=out, in_=in_, axis=axis, op=op))

    def dump(self, name, ap):
        if not self.debug or name in self.dbg_map:
            return
        if getattr(self, 'dbg_filter', None) is not None and not any(name.startswith(p) for p in self.dbg_filter):
            return
        shp = list(ap.shape)
        npart, nfree = shp[0], _prod(shp[1:])
        stage = self.XN[1]
        assert nfree <= 1024
        dst = self.V(stage, 0, npart, 0, [[_prod(shp[i + 1:]), shp[i]] for i in range(1, len(shp))])
        self.cp('dve', dst, ap)
        self.S.dma('sp', self.dbg[0:npart, self.dbg_off:self.dbg_off + nfree], stage[0:npart, 0:nfree], allow_slow_non_contiguous=True)
        self.dbg_map[name] = (self.dbg_off, npart, shp[1:])
        self.dbg_off += nfree

    def ppc(self, l, key, j=0, rows=128):
        c = l * PL + PC[key] + j
        return self.PP[0:rows, c:c + 1]

    def dvc(self, l, key, j=0, rows=128):
        c = l * DL + DC[key] + j
        return self.DV[0:rows, c:c + 1]

    def setup(self):
        S = self.S
        S.dma('sp', self.CST[:, :], self.cst[:, :])
        S.dma('sp', self.PP[:, :], self.pp[:, :])
        S.dma('sp', self.LORA[:, :, :, :], self.lora[:, :, :, :])
        S.dma('sp', self.LRUW[:, :, :, :], self.lruw[:, :, :, :])
        self.memset('dve', self.DV[:, :], 0.0)
        self.cp('dve', self.IDB[:, :], self.CST[:, CC['IDENT']:CC['IDENT'] + 128])
        self.cp('dve', self.ISTKB[:, :], self.CST[:, CC['ISTK']:CC['ISTK'] + 64])
        for l in range(2):
            self.ts('dve', self.DV[:, l * DL + DC['OMKA']: l * DL + DC['OMKA'] + 3],
                    self.PP[:, l * PL + PC['R_KA']: l * PL + PC['R_KA'] + 3], -1.0, 1.0, ALU.mult, ALU.add)
            self.ts('dve', self.DV[:, l * DL + DC['OMMU']: l * DL + DC['OMMU'] + 11],
                    self.PP[:, l * PL + PC['R_MU']: l * PL + PC['R_MU'] + 11], -1.0, 1.0, ALU.mult, ALU.add)
            lam = self.PP[:, l * PL + PC['L_LAM']: l * PL + PC['L_LAM'] + 4]
            t0 = self.TMPS[:, 0:4]
            self.act(t0, lam, AF.Exp, scale=-1.0)
            self.act(t0, t0, AF.Ln, bias=1.0)
            self.ts('dve', self.DV[:, l * DL + DC['CLAM']: l * DL + DC['CLAM'] + 4], t0, -8.0, None, ALU.mult)
            self.ts('dve', self.DV[:, l * DL + DC['C2LAM']: l * DL + DC['C2LAM'] + 4], t0, -16.0, None, ALU.mult)
            self.ts('dve', self.DV[0:6, l * DL + DC['NBF']: l * DL + DC['NBF'] + 2],
                    self.PP[0:6, l * PL + PC['M_BF']: l * PL + PC['M_BF'] + 2], -1.0, None, ALU.mult)
            for hp in range(3):
                self.ts('dve', self.RKD[:, l, hp, :], self.CST[:, CC['BONES']:CC['BONES'] + 128],
                        self.ppc(l, 'R_RK', hp), None, ALU.mult)
        self.memset('dve', self.XL[:, :, 0:2], 0.0)

    def load_layer(self, l):
        S = self.S
        wsrc = self.w_in[l].rearrange("(kc p) c -> p kc c", p=128)
        wo = self.w_out[l].rearrange("(kc p) c -> p kc c", p=128)
        pieces = [(self.W_IN, wsrc, c0, min(256, IN_COLS - c0)) for c0 in range(0, IN_COLS, 256)]
        pieces += [(self.W_OUT, wo, c0, 256) for c0 in range(0, D, 256)]
        for i, (dst, src, c0, n) in enumerate(pieces):
            stg = self.WSTG[i % 3]
            S.dma('sp', stg[:, :, 0:n], src[:, :, c0:c0 + n])
            self.cp('pool', dst[:, :, c0:c0 + n], stg[:, :, 0:n])
        if self.stop == 'load_w':
            raise StopIteration
        S.dma('sp', self.SCT[:, :, :], self.cc[:, :, :])
        S.dma('sp', self.BMT[:, :, :], self.bmodT[:, :, :])
        self.act(self.SCT[:, :, :], self.SCT[:, :, :], AF.Silu)
        for m in range(2):
            for kc in range(8):
                src = self.V(self.SCT, 0, 128, kc * 2 + m, [[0, 128]])
                self.cp('dve', self.SCB[:, m, kc, :], src)
        wm = self.w_mod[l].rearrange("(kc p) c -> p kc c", p=128)
        for blk in range(6):
            S.dma('sp', self.WM[:, :, :], wm[:, :, blk * 512:(blk + 1) * 512])
            if blk < 4:
                ps = self.pq()
                for j in range(4):
                    for kc in range(8):
                        self.mm(ps[:, j * 2:j * 2 + 2], self.WM[:, kc, j * 128:(j + 1) * 128], self.SCT[:, kc, :],
                                start=(kc == 0), stop=(kc == 7))
                o = self.MODT[:, blk * 4:(blk + 1) * 4, :]
                bsrc = self.V(self.BMT, 0, 128, l * 16 + blk * 4, [[1, 4], [0, 2]])
                self.tt('dve', o, self.V(ps, 0, 128, 0, [[2, 4], [1, 2]]), bsrc, ALU.add)
            else:
                half = blk - 4
                for m in range(2):
                    ps = self.pq()
                    for kc in range(8):
                        self.mm(ps[:, :], self.SCB[:, m, kc, :], self.WM[:, kc, :], start=(kc == 0), stop=(kc == 7))
                    self.cp('act', self.GATEB[:, m, half * 512:(half + 1) * 512], ps[:, :])
        if self.stop == 'load_m':
            raise StopIteration
        for m in range(2):
            self.cp('dve', self.SH[:, m, :], self.V(self.MODT, 0, 128, m, [[2, 8]]))
            t0 = self.TMPS[:, 8:16]
            self.ts('dve', t0, self.V(self.MODT, 0, 128, 16 + m, [[2, 8]]), 1.0, None, ALU.add)
            self.tt('dve', self.GS[:, m, :], t0, self.PP[:, l * PL + PC['G_PRE']: l * PL + PC['G_PRE'] + 8], ALU.mult)
        if self.stop == 'load_g':
            raise StopIteration
        S.dma('sp', self.BG[:, :], bass.AP(self.bmodg.tensor, l * D, [[0, 128], [1, D]]))
        S.dma('sp', self.GP[:, :], bass.AP(self.gpost.tensor, l * D, [[0, 128], [1, D]]))
        if self.stop == 'load_b':
            raise StopIteration
        for m in range(2):
            self.tt('dve', self.GATEB[:, m, :], self.GATEB[:, m, :], self.BG[:, :], ALU.add)
            self.tt('dve', self.GATEB[:, m, :], self.GATEB[:, m, :], self.GP[:, :], ALU.mult)

    def proj_fm(self, c0, ncols, t_off, ntok, evac):
        ps = self.pq()
        for kc in range(8):
            self.mm(ps[0:ncols, 0:ntok], self.W_IN[:, kc, c0:c0 + ncols], self.HT[:, kc, t_off:t_off + ntok],
                    start=(kc == 0), stop=(kc == 7))
        evac(ps[0:ncols, 0:ntok])

    def proj_tm(self, c0, ncols, t_off, evac):
        ps = self.pq()
        for kc in range(8):
            self.mm(ps[0:64, 0:ncols], self.HT[:, kc, t_off:t_off + 64], self.W_IN[:, kc, c0:c0 + ncols],
                    start=(kc == 0), stop=(kc == 7))
        evac(ps[0:64, 0:ncols])

    def visit(self, l, mod, xsrc, xdst, row0, T, t0, dirs, grid, seq_idx, first, last):
        S = self.S
        w0 = max(0, t0 - 64)
        w1 = min(T, t0 + TB + 64)
        W = w1 - w0
        co = t0 - w0
        do_f = 0 in dirs
        prompt = (mod == 0)
        ntile = (W + 127) // 128
        for i in range(ntile):
            n = min(128, W - i * 128)
            xw, xn = self.XW[i % 2], self.XN[i % 2]
            r = row0 + w0 + i * 128
            S.dma('sp', xw[0:n, :], xsrc[r:r + n, :])
            ssq = self.TMPS[0:n, 16 + i:17 + i]
            self.act(xn[0:n, :], xw[0:n, :], AF.Square, accum=ssq)
            if self.stop == 'n1':
                raise StopIteration
            rs = self.TMPS[0:n, 20 + i:21 + i]
            self.ts('dve', rs, ssq, 1.0 / D, EPS, ALU.mult, ALU.add)
            self.act(rs, rs, AF.Sqrt)
            self.recip(rs, rs)
            if self.stop == 'n2':
                raise StopIteration
            self.act(xn[0:n, :], xw[0:n, :], AF.Copy, scale=rs)
            if self.stop == 'n3':
                raise StopIteration
            for half in range(2):
                ps = self.pq()
                for j in range(4):
                    kc = half * 4 + j
                    self.tr(ps[:, j * 128:j * 128 + n], xn[0:n, kc * 128:(kc + 1) * 128])
                if self.stop == 'n4':
                    raise StopIteration
                for j in range(4):
                    kc = half * 4 + j
                    o = self.HT[:, kc, i * 128:i * 128 + n]
                    if half == 0:
                        self.ts('dve', o, ps[:, j * 128:j * 128 + n], self.GS[:, mod, kc:kc + 1], self.SH[:, mod, kc:kc + 1], ALU.mult, ALU.add)
                    else:
                        self.act(o, ps[:, j * 128:j * 128 + n], AF.Identity, bias=self.SH[:, mod, kc:kc + 1], scale=self.GS[:, mod, kc:kc + 1])
        if self.stop in ('norm', 'n5a', 'n5d'):
            raise StopIteration
        self.stage_mlstm(l, t0, co, dirs, prompt, seq_idx, first, last)
        if self.stop == 'mlstm':
            raise StopIteration
        self.stage_lru(l, t0, co, W, w0, w1, T, dirs, prompt, seq_idx, first, last)
        if self.stop == 'lru':
            raise StopIteration
        self.stage_rwkv(l, t0, co, W, w0, w1, T, dirs, grid, prompt, seq_idx, first, last)
        for kc in range(8):
            self.dump('mix%d' % kc, self.MIXT[:, kc, :])
        if self.stop == 'rwkv':
            raise StopIteration
        if do_f:
            self.stage_out(l, mod, xsrc, xdst, row0 + t0)
        if self.stop == 'out':
            raise StopIteration

    def stage_mlstm(self, l, t0, co, dirs, prompt, seq_idx, first, last):
        S = self.S
        do_f = 0 in dirs
        for hp in range(3):
            self.proj_fm(hp * 128, 128, co, 256, lambda ps, hp=hp: self.cp('act', self.QT[:, hp, :], ps))
            self.proj_fm(384 + hp * 128, 128, co, 256, lambda ps, hp=hp: self.act(self.KT[:, hp, :], ps, AF.Copy, scale=0.125))
        if do_f:
            for hp in range(3):
                def ev_o(ps, hp=hp):
                    self.act(self.GOZ[:, hp, :], ps, AF.Sigmoid)
                self.proj_fm(1152 + hp * 128, 128, co, 256, ev_o)
                def ev_z2(ps, hp=hp):
                    tz = self.ZT[hp % 2]
                    self.act(tz[:, :], ps, AF.Silu)
                    self.tt('dve', self.GOZ[:, hp, :], self.GOZ[:, hp, :], tz[:, :], ALU.mult)
                self.proj_fm(1536 + hp * 128, 128, co, 256, ev_z2)
        for c in range(4):
            self.proj_tm(384, 384, co + c * 64, lambda ps, c=c: self.act(self.KTOK[:, c, :], ps, AF.Copy, scale=0.125))
            def ev_v(ps, c=c):
                self.cp('dve', self.VAUG[:, c, :, 0:64], self.V(ps.tensor, 0, 64, 0, [[64, 6], [1, 64]]))
            self.proj_tm(768, 384, co + c * 64, ev_v)
        self.memset('dve', self.VAUG[:, :, :, 64:65], 1.0)
        for d in dirs:
            g = self.mg[d]
            self.proj_fm(1920 + d * 6, 6, co, 256, lambda ps, g=g, d=d: self.act(g["gi"][:, :], ps, AF.Identity, bias=self.ppc(l, 'M_BI', d, 6)))
            def ev_f(ps, g=g, d=d):
                self.act(g["lf"][:, :], ps, AF.Exp, bias=self.dvc(l, 'NBF', d, 6), scale=-1.0)
                self.act(g["lf"][:, :], g["lf"][:, :], AF.Ln, bias=1.0)
                self.ts('dve', g["lf"][:, :], g["lf"][:, :], -1.0, None, ALU.mult)
            self.proj_fm(1932 + d * 6, 6, co, 256, ev_f)
        for d in dirs:
            if first[d]:
                if prompt:
                    self.memset('dve', self.mC[d][:, :, :], 0.0)
                    self.memset('dve', self.mM[d][:, :], 0.0)
                else:
                    S.dma('sp', self.mC[d][:, :, :], self.st_mC[l, d])
                    S.dma('sp', self.mM[d][:, :], self.st_mm[:, l, d:d + 1], allow_slow_non_contiguous=True)
        if do_f and not (1 in dirs):
            S.dma('sp', self.HB[:, :, :], self.sHB[t0:t0 + 256, :].rearrange("(c s) f -> s c f", s=64))
        for d in dirs:
            g = self.mg[d]
            v3 = lambda t: self.V(t, 0, 6, 0, [[64, 4], [1, 64]])
            self.scan(g["pre"][:, :], self.CST[0:6, CC['RMASK']:CC['RMASK'] + 256], g["lf"][:, :], 0.0, ALU.mult, ALU.add)
            bL = self.V(g["pre"], 0, 6, 63, [[64, 4]])
            bLb = self.V(g["pre"], 0, 6, 63, [[64, 4], [0, 64]])
            if d == 0:
                bsrc = g["pre"]
            else:
                self.tt('dve', v3(g["bb"]), bLb, v3(g["pre"]), ALU.subtract)
                self.tt('dve', g["bb"][:, :], g["bb"][:, :], g["lf"][:, :], ALU.add)
                bsrc = g["bb"]
            self.tt('dve', g["gg"][:, :], g["gi"][:, :], bsrc[:, :], ALU.subtract)
            self.reduce(g["mx"][:, :], v3(g["gg"]), ALU.max)
            if d == 0:
                mo, mxv, blv = g["mch"][:, :], g["mx"][:, :], bL
            else:
                mo = self.V(g["mch"], 0, 6, 3, [[-1, 4]])
                mxv = self.V(g["mx"], 0, 6, 3, [[-1, 4]])
                blv = self.V(g["pre"], 0, 6, 63 + 3 * 64, [[-64, 4]])
            self.scan(mo, mxv, blv, self.mM[d][:, 0:1], ALU.max, ALU.add)
            if d == 0:
                self.cp('dve', g["mprev"][:, 1:4], g["mch"][:, 0:3])
                self.cp('dve', g["mprev"][:, 0:1], self.mM[d][:, 0:1])
                mfin = g["mch"][:, 3:4]
            else:
                self.cp('dve', g["mprev"][:, 0:3], g["mch"][:, 1:4])
                self.cp('dve', g["mprev"][:, 3:4], self.mM[d][:, 0:1])
                mfin = g["mch"][:, 0:1]
            self.tt('dve', g["MM"][:, :], g["mprev"][:, :], g["mx"][:, :], ALU.max)
            self.tt('dve', g["dec"][:, :], g["mprev"][:, :], g["MM"][:, :], ALU.subtract)
            self.act(g["dec"][:, :], g["dec"][:, :], AF.Exp)
            self.cp('dve', self.mM[d][:, 0:1], mfin)
            MMb = self.V(g["MM"], 0, 6, 0, [[1, 4], [0, 64]])
            self.tt('dve', v3(g["ee"]), v3(g["gg"]), MMb, ALU.subtract)
            self.act(g["ee"][:, :], g["ee"][:, :], AF.Exp)
            self.tt('dve', v3(g["fl"]), v3(bsrc), MMb, ALU.add)
            self.act(g["fl"][:, :], g["fl"][:, :], AF.Exp, scale=-1.0)
            ps = self.pq()
            for c in range(4):
                self.tr(ps[0:64, c * 6:c * 6 + 6], g["ee"][:, c * 64:(c + 1) * 64])
                self.tr(ps[0:64, 24 + c * 6:24 + c * 6 + 6], g["fl"][:, c * 64:(c + 1) * 64])
            self.cp('dve', g["etok"][:, :, :], self.V(ps, 0, 64, 0, [[6, 4], [1, 6]]))
            self.cp('dve', g["fltok"][:, :, :], self.V(ps, 0, 64, 24, [[6, 4], [1, 6]]))
            self.tt('dve', g["X2"][:, :, :], self.V(g["dec"], 0, 6, 0, [[1, 4], [0, 3]]),
                    self.V(self.CST, 0, 6, CC['PSEL'], [[0, 4], [1, 3]]), ALU.mult)
            ps2 = self.pq()
            self.mm(ps2[:, 0:12], self.CST[0:6, CC['LSEL']:CC['LSEL'] + 128], self.V(g["X2"], 0, 6, 0, [[1, 12]]))
            self.cp('dve', g["decb"][:, :, :], self.V(ps2, 0, 128, 0, [[3, 4], [1, 3]]))
        for d in sorted(dirs, reverse=True):
            g = self.mg[d]
            mask = self.CST[0:64, CC['MUI']:CC['MUI'] + 64] if d == 0 else self.CST[0:64, CC['MLI']:CC['MLI'] + 64]
            maskb = self.V(self.CST, 0, 64, CC['MUI'] if d == 0 else CC['MLI'], [[0, 6], [1, 64]])
            for j in range(4):
                c = j if d == 0 else 3 - j
                cs = slice(c * 64, (c + 1) * 64)
                stsb, vp = self.STSB[j % 2], self.VP[j % 2]
                ps = self.pq()
                for h in range(6):
                    hp, pb = h // 2, 64 * (h % 2)
                    self.mm(ps[0:64, h * 64:(h + 1) * 64], self.KT[pb:pb + 64, hp, cs], self.QT[pb:pb + 64, hp, cs], inc=(h == 5))
                self.tt('dve', stsb[:, :, :], self.V(ps, 0, 64, 0, [[64, 6], [1, 64]]), maskb, ALU.mult)
                self.tt('dve', vp[:, :, :], self.VAUG[:, c, :, :], self.V(g["etok"], 0, 64, c * 6, [[1, 6], [0, 65]]), ALU.mult)
                self.tt('dve', self.CDEC[:, :, :], self.mC[d][:, :, :], self.V(g["decb"], 0, 128, c * 3, [[1, 3], [0, 65]]), ALU.mult)
                self.cp('act', self.CDBF[:, :, :], self.CDEC[:, :, :])
                ph = self.pq()
                for h in range(6):
                    hp, pb = h // 2, 64 * (h % 2)
                    o = ph[0:64, h * 65:(h + 1) * 65]
                    self.mm(o, stsb[:, h, :], vp[:, h, :], start=True, stop=False)
                    self.mm(o, self.QT[pb:pb + 64, hp, cs], self.CDBF[pb:pb + 64, hp, :], start=False, stop=True, inc=(h == 5))
                pc = self.pq()
                for h in range(6):
                    hp, pb = h // 2, 64 * (h % 2)
                    self.mm(pc[pb:pb + 64, hp * 65:(hp + 1) * 65], self.KTOK[:, c, h * 64:(h + 1) * 64], vp[:, h, :], inc=(h == 5))
                self.tt('dve', self.mC[d][:, :, :], self.CDEC[:, :, :], self.V(pc, 0, 128, 0, [[65, 3], [1, 65]]), ALU.add)
                self.act(self.DN[:, :], self.V(ph, 0, 64, 64, [[65, 6]]), AF.Abs)
                self.tt('dve', self.DN[:, :], self.DN[:, :], g["fltok"][:, c, :], ALU.max)
                self.recip(self.RDN[:, :], self.DN[:, :])
                hsrc = self.V(ph, 0, 64, 0, [[65, 6], [1, 64]])
                rb = self.V(self.RDN, 0, 64, 0, [[1, 6], [0, 64]])
                hbv = self.V(self.HB, 0, 64, c * 384, [[64, 6], [1, 64]])
                if d == 1:
                    self.tt('dve', hbv, hsrc, rb, ALU.mult)
                else:
                    self.tt('dve', self.HD[:, :, :], hsrc, rb, ALU.mult)
                    self.tt('dve', hbv, hbv, self.HD[:, :, :], ALU.add)
            if last[d] and prompt:
                S.dma('sp', self.o_mC[seq_idx, l, d], self.mC[d][:, :, :])
                S.dma('sp', self.o_mm[seq_idx, l, d], self.mM[d][:, 0:1])
        if not do_f:
            S.dma('sp', self.sHB[t0:t0 + 256, :].rearrange("(c s) f -> s c f", s=64), self.HB[:, :, :])
            return
        self.tt('dve', self.SQ[:, :, :], self.HB[:, :, :], self.HB[:, :, :], ALU.mult)
        self.reduce(self.SSQ[:, :], self.V(self.SQ, 0, 64, 0, [[64, 24], [1, 64]]), ALU.add)
        self.ts('dve', self.SSQ[:, :], self.SSQ[:, :], 1.0 / 64, EPS, ALU.mult, ALU.add)
        self.act(self.SSQ[:, :], self.SSQ[:, :], AF.Sqrt)
        self.recip(self.RSTD[:, :], self.SSQ[:, :])
        self.tt('dve', self.V(self.SQ, 0, 64, 0, [[64, 24], [1, 64]]), self.V(self.HB, 0, 64, 0, [[64, 24], [1, 64]]),
                self.V(self.RSTD, 0, 64, 0, [[1, 24], [0, 64]]), ALU.mult)
        for hp in range(3):
            ps = self.pq()
            for c in range(4):
                self.tr(ps[:, c * 64:(c + 1) * 64], self.SQ[:, c, hp * 128:(hp + 1) * 128], inc=(c == 3))
            self.stt(self.MIXT[:, hp, :], ps[:, 0:256], self.ppc(l, 'M_NORM', hp), self.GOZ[:, hp, :], ALU.mult, ALU.mult)

    def stage_lru(self, l, t0, co, W, w0, w1, T, dirs, prompt, seq_idx, first, last):
        S = self.S
        do_f = 0 in dirs
        for pr in range(2):
            self.proj_fm(3736 + pr * 128, 128, 0, W, lambda ps, pr=pr: self.cp('act', self.XL[:, pr, 2:2 + W], ps))
            if do_f:
                self.proj_fm(3992 + pr * 128, 128, co, 256, lambda ps, pr=pr: self.act(self.LZT[:, pr, :], ps, AF.Silu))
        if w1 == T:
            self.memset('dve', self.XL[:, :, 2 + W:2 + W + 1], 0.0)
        if w0 == 0:
            self.memset('dve', self.XL[:, :, 0:2], 0.0)
        for pr in range(2):
            self.ts('dve', self.XC[:, pr, :], self.XL[:, pr, co:co + 256], self.ppc(l, 'L_CONV', 0 * 2 + pr), self.ppc(l, 'L_CONVB', pr), ALU.mult, ALU.add)
            for j in range(1, 4):
                self.stt(self.XC[:, pr, :], self.XL[:, pr, co + j:co + j + 256], self.ppc(l, 'L_CONV', j * 2 + pr), self.XC[:, pr, :], ALU.mult, ALU.add)
        for d in dirs:
            if first[d]:
                if prompt:
                    self.memset('dve', self.lS[d][:, :], 0.0)
                else:
                    S.dma('sp', self.lS[d][:, :], self.st_l[:, l, d, :])
        if do_f and not (1 in dirs):
            S.dma('sp', self.HL[1][:, :, :], self.sLB[:, :, t0:t0 + 256])
        lt = self.lt
        for d in sorted(dirs, reverse=True):
            for pr in range(2):
                ps = self.pq()
                self.mm(ps[:, 0:256], self.LRUW[:, l, (0 * 2 + d) * 2 + pr, :], self.XC[:, pr, :])
                self.mm(ps[:, 256:512], self.LRUW[:, l, (1 * 2 + d) * 2 + pr, :], self.XC[:, pr, :])
                self.act(lt["rg"][:, pr, :], ps[:, 0:256], AF.Sigmoid, bias=self.ppc(l, 'L_BA', d * 2 + pr))
                self.act(lt["ig"][:, pr, :], ps[:, 256:512], AF.Sigmoid, bias=self.ppc(l, 'L_BX', d * 2 + pr))
                self.act(lt["aa"][:, pr, :], lt["rg"][:, pr, :], AF.Exp, scale=self.dvc(l, 'CLAM', d * 2 + pr))
                self.act(lt["a2"][:, pr, :], lt["rg"][:, pr, :], AF.Exp, scale=self.dvc(l, 'C2LAM', d * 2 + pr))
                self.ts('dve', lt["a2"][:, pr, :], lt["a2"][:, pr, :], -1.0, 1.0, ALU.mult, ALU.add)
                self.act(lt["a2"][:, pr, :], lt["a2"][:, pr, :], AF.Sqrt)
                self.tt('dve', lt["bt"][:, pr, :], lt["a2"][:, pr, :], lt["ig"][:, pr, :], ALU.mult)
                self.tt('dve', lt["bt"][:, pr, :], lt["bt"][:, pr, :], self.XC[:, pr, :], ALU.mult)
                if d == 0:
                    self.scan(self.HL[0][:, pr, :], lt["aa"][:, pr, :], lt["bt"][:, pr, :], self.lS[0][:, pr:pr + 1], ALU.mult, ALU.add)
                    self.cp('dve', self.lS[0][:, pr:pr + 1], self.HL[0][:, pr, 255:256])
                else:
                    rv = lambda t: self.V(t, 0, 128, pr * 256 + 255, [[-1, 256]])
                    self.scan(rv(self.HL[1]), rv(lt["aa"]), rv(lt["bt"]), self.lS[1][:, pr:pr + 1], ALU.mult, ALU.add)
                    self.cp('dve', self.lS[1][:, pr:pr + 1], self.HL[1][:, pr, 0:1])
            if last[d] and prompt:
                S.dma('sp', self.o_l[seq_idx, l, d], self.lS[d][:, :])
        if not do_f:
            S.dma('sp', self.sLB[:, :, t0:t0 + 256], self.HL[1][:, :, :])
            return
        self.tt('dve', self.HL[0][:, :, :], self.HL[0][:, :, :], self.HL[1][:, :, :], ALU.add)
        self.tt('dve', self.MIXT[:, 6:8, :], self.HL[0][:, :, :], self.LZT[:, :, :], ALU.mult)

    def stage_rwkv(self, l, t0, co, W, w0, w1, T, dirs, grid, prompt, seq_idx, first, last):
        S = self.S
        do_f = 0 in dirs
        for ch in range(11):
            self.proj_fm(1944 + ch * 128, 128, 0, W, lambda ps, ch=ch: self.cp('act' if ch % 2 else 'dve', self.URS[:, ch, 0:W], ps))
        if do_f:
            for hp in range(3):
                self.proj_fm(3352 + hp * 128, 128, co, 256, lambda ps, hp=hp: self.act(self.RZT[:, hp, :], ps, AF.Silu))
        U3 = lambda off, n: self.V(self.URS, 0, 128, off, [[384, 11], [1, n]])
        B3 = lambda off, n: self.V(self.BLK, 0, 128, off, [[256, 11], [1, n]])
        if not grid:
            self.cp('dve', B3(1, 255), U3(0, 255))
            self.memset('dve', B3(0, 1), 0.0)
            self.tt('dve', B3(0, 255), B3(0, 255), U3(1, 255), ALU.add)
            wsh = 0.5
        else:
            U4 = lambda off, r, n: self.V(self.URS, 0, 128, off, [[384, 11], [64, r], [1, n]])
            B4 = lambda off, r, n: self.V(self.BLK, 0, 128, off, [[256, 11], [64, r], [1, n]])
            self.cp('dve', B4(1, 4, 63), U4(co, 4, 63))
            self.memset('dve', B4(0, 4, 1), 0.0)
            self.tt('dve', B4(0, 4, 63), B4(0, 4, 63), U4(co + 1, 4, 63), ALU.add)
            if t0 > 0:
                self.tt('dve', B3(0, 256), B3(0, 256), U3(co - 64, 256), ALU.add)
            else:
                self.tt('dve', B3(64, 192), B3(64, 192), U3(0, 192), ALU.add)
            if t0 + TB < T:
                self.tt('dve', B3(0, 256), B3(0, 256), U3(co + 64, 256), ALU.add)
            else:
                self.tt('dve', B3(0, 192), B3(0, 192), U3(co + 64, 192), ALU.add)
            wsh = 0.25
        mu = self.V(self.PP, 0, 128, l * PL + PC['R_MU'], [[1, 11], [0, 256]])
        self.tt('dve', B3(0, 256), B3(0, 256), mu, ALU.mult)
        omm = self.V(self.DV, 0, 128, l * DL + DC['OMMU'], [[1, 11], [0, 256]])
        self.tt('dve', U3(co, 256), U3(co, 256), omm, ALU.mult)
        self.stt(B3(0, 256), B3(0, 256), wsh, U3(co, 256), ALU.mult, ALU.add)
        for nm, ch in (('blk_r', 0), ('blk_k', 3), ('blk_v', 6), ('blk_wl', 9), ('blk_al', 10)):
            self.dump(nm, self.BLK[:, ch, :])
        rt = self.rt
        kk = self.V(self.PP, 0, 128, l * PL + PC['R_KK'], [[1, 3], [0, 256]])
        kap = rt["d"]
        self.tt('dve', kap[:, :, :], self.BLK[:, 3:6, :], kk, ALU.mult)
        ksq = rt["E"]
        self.tt('dve', ksq[:, :, :], kap[:, :, :], kap[:, :, :], ALU.mult)
        for hp in range(3):
            self.mm(self.PA[:, hp * 256:(hp + 1) * 256], self.CST[:, CC['BONES']:CC['BONES'] + 128], ksq[:, hp, :])
        self.act(ksq[:, :, :], self.V(self.PA, 0, 128, 0, [[256, 3], [1, 256]]), AF.Sqrt)
        self.ts('dve', ksq[:, :, :], ksq[:, :, :], 1e-12, None, ALU.max)
        self.recip(ksq[:, :, :], ksq[:, :, :])
        self.tt('dve', self.KH[:, :, :], kap[:, :, :], ksq[:, :, :], ALU.mult)
        self.dump('kh', self.KH[:, 0, :])
        self.bd_fill(self.BDV, lambda par: self.V(self.BLK, par * 64, 64, 6 * 256, [[256, 3], [64, 4], [1, 64]]))
        for c in range(4):
            for hp in range(3):
                self.mm(self.PB[:, (c * 3 + hp) * 64:(c * 3 + hp + 1) * 64], self.BDV[:, hp, c, :], self.ISTKB[:, :])
        self.cp('act', self.V(self.VSTK, 0, 128, 0, [[1, 768]]), self.PB[:, 0:768])
        for d in dirs:
            if first[d]:
                if prompt:
                    self.memset('dve', self.rH[d][:, :, :], 0.0)
                else:
                    S.dma('sp', self.rH[d][:, :, :], self.st_rH[l, d])
        ybflat = self.V(self.YB, 0, 128, 0, [[1, 768]])
        if do_f and not (1 in dirs):
            S.dma('sp', ybflat, self.sYB[t0 // 256])
        for d in sorted(dirs, reverse=True):
            self.rwkv_dir(l, d, seq_idx, prompt, last)
        if not do_f:
            S.dma('sp', self.sYB[t0 // 256], ybflat)
            return
        ft = self.ft
        self.tt('dve', self.YSQ[:, :, :, :], self.YB[:, :, :, :], self.YB[:, :, :, :], ALU.mult)
        self.reduce(self.rSSQ[:, :], self.V(self.YSQ, 0, 128, 0, [[64, 12], [1, 64]]), ALU.add)
        self.ts('dve', self.rSSQ[:, :], self.rSSQ[:, :], 1.0 / 64, EPS, ALU.mult, ALU.add)
        self.act(self.rSSQ[:, :], self.rSSQ[:, :], AF.Sqrt)
        self.recip(self.rRSTD[:, :], self.rSSQ[:, :])
        self.tt('dve', ft["rk"][:, :, :], self.BLK[:, 0:3, :], self.BLK[:, 3:6, :], ALU.mult)
        for hp in range(3):
            self.mm(self.PB[:, hp * 256:(hp + 1) * 256], self.RKD[:, l, hp, :], ft["rk"][:, hp, :])
        self.tt('dve', ft["bon"][:, :, :], self.V(self.PB, 0, 128, 0, [[256, 3], [1, 256]]), self.BLK[:, 6:9, :], ALU.mult)
        self.memset('pool', self.YBD[:, :, :], 0.0)
        for c in range(4):
            for par in range(2):
                self.tt('dve', self.V(self.YBD, par * 64, 64, par * 64, [[128, 3], [1, 64]]),
                        self.V(self.YB, par * 64, 64, c * 192, [[64, 3], [1, 64]]),
                        self.V(self.rRSTD, par * 64, 64, c * 3, [[1, 3], [0, 64]]), ALU.mult)
            for hp in range(3):
                self.mm(self.PA[:, hp * 256 + c * 64: hp * 256 + (c + 1) * 64], self.YBD[:, hp, :],
                        self.CST[:, CC['ISTK']:CC['ISTK'] + 64])
        for hp in range(3):
            self.stt(ft["t1"][:, hp, :], self.PA[:, hp * 256:(hp + 1) * 256], self.ppc(l, 'R_NORM', hp), ft["bon"][:, hp, :], ALU.mult, ALU.add)
            self.tt('dve', self.MIXT[:, 3 + hp, :], ft["t1"][:, hp, :], self.RZT[:, hp, :], ALU.mult)

    def bd_fill(self, bd, src_of_par, eng='pool'):
        self.memset(eng, bd[:, :, :, :], 0.0)
        for par in range(2):
            self.cp(eng, self.V(bd, par * 64, 64, par * 64, [[512, 3], [128, 4], [1, 64]]), src_of_par(par))

    def rwkv_dir(self, l, d, seq_idx, prompt, last):
        S = self.S
        rt = self.rt
        pb_d = 64 * d
        f3 = lambda t: t[:, :, :]
        v4 = lambda t: self.V(t, 0, 128, 0, [[256, 3], [64, 4], [1, 64]])
        self.act(self.TWL[pb_d:pb_d + 64, :], self.BLK[pb_d:pb_d + 64, 9, :], AF.Tanh)
        for hp in range(3):
            self.mm(self.PA[:, hp * 256:(hp + 1) * 256], self.LORA[pb_d:pb_d + 64, l, 0, hp * 128:(hp + 1) * 128], self.TWL[pb_d:pb_d + 64, :])
        for hp in range(3):
            self.act(rt["sg"][:, hp, :], self.PA[:, hp * 256:(hp + 1) * 256], AF.Sigmoid, bias=self.ppc(l, 'R_W0', d * 3 + hp))
        for hp in range(3):
            self.mm(self.PB[:, hp * 256:(hp + 1) * 256], self.LORA[pb_d:pb_d + 64, l, 1, hp * 128:(hp + 1) * 128], self.BLK[pb_d:pb_d + 64, 10, :])
        for hp in range(3):
            self.act(rt["aa"][:, hp, :], self.PB[:, hp * 256:(hp + 1) * 256], AF.Sigmoid, bias=self.ppc(l, 'R_A0', d * 3 + hp))
        for hp in range(3):
            self.ts('dve', rt["kt"][:, hp, :], rt["aa"][:, hp, :], self.ppc(l, 'R_KA', hp), self.dvc(l, 'OMKA', hp), ALU.mult, ALU.add)
        self.tt('dve', f3(rt["kt"]), f3(rt["kt"]), self.BLK[:, 3:6, :], ALU.mult)
        self.tt('dve', f3(rt["bb"]), self.KH[:, :, :], f3(rt["aa"]), ALU.mult)
        flat = lambda t: self.V(t, 0, 128, 0, [[1, 768]])
        self.scan(flat(rt["cs"]), self.CST[:, CC['RMASK']:CC['RMASK'] + 768], flat(rt["sg"]), 0.0, ALU.mult, ALU.add)
        self.cp('dve', self.GL[:, :, :], self.V(rt["cs"], 0, 128, 63, [[256, 3], [64, 4]]))
        if d == 1:
            csLb = self.V(rt["cs"], 0, 128, 63, [[256, 3], [64, 4], [0, 64]])
            self.tt('dve', v4(rt["d"]), csLb, v4(rt["cs"]), ALU.subtract)
            self.tt('dve', f3(rt["cs"]), f3(rt["d"]), f3(rt["sg"]), ALU.add)
        self.act(f3(rt["E"]), f3(rt["cs"]), AF.Exp, scale=-DSC)
        self.tt('dve', self.V(self.KR, 0, 128, 64, [[512, 3], [128, 4], [1, 64]]),
                self.V(self.BLK, 0, 128, 0, [[256, 3], [64, 4], [1, 64]]), v4(rt["E"]), ALU.mult)
        self.tt('dve', f3(rt["d"]), f3(rt["cs"]), f3(rt["sg"]), ALU.subtract)
        self.act(f3(rt["E"]), f3(rt["d"]), AF.Exp, scale=-DSC)
        self.tt('dve', self.V(self.KR, 0, 128, 0, [[512, 3], [128, 4], [1, 64]]), v4(self.KH), v4(rt["E"]), ALU.mult)
        self.act(f3(rt["E"]), f3(rt["cs"]), AF.Exp, scale=DSC)
        self.tt('dve', self.BTT[:, :, :], f3(rt["bb"]), f3(rt["E"]), ALU.mult)
        self.tt('dve', self.KTT[:, :, :], f3(rt["kt"]), f3(rt["E"]), ALU.mult)
        self.act(self.GL[:, :, :], self.GL[:, :, :], AF.Exp, scale=-DSC)
        BD = self.BD
        c4 = lambda t, par: self.V(t, par * 64, 64, 0, [[256, 3], [64, 4], [1, 64]])
        self.bd_fill(BD["kt"], lambda par: c4(self.KTT, par))
        self.bd_fill(BD["b"], lambda par: c4(self.BTT, par))
        self.bd_fill(BD["kh"], lambda par: self.V(self.KR, par * 64, 64, 0, [[512, 3], [128, 4], [1, 64]]))
        self.bd_fill(BD["r"], lambda par: self.V(self.KR, par * 64, 64, 64, [[512, 3], [128, 4], [1, 64]]))
        for (src, dst, neg) in ((BD["kt"], self.KTTOK, False), (BD["b"], self.BTTOK, True)):
            for half in range(2):
                for cc_ in range(2):
                    c = half * 2 + cc_
                    for hp in range(3):
                        self.tr(self.PTB[:, (cc_ * 3 + hp) * 128:(cc_ * 3 + hp + 1) * 128], src[:, hp, c, :], bf=True)
                o = self.V(dst, 0, 128, half * 768, [[1, 768]])
                if neg:
                    self.act(o, self.PTB[:, 0:768], AF.Copy, scale=-1.0)
                else:
                    self.cp('dve', o, self.PTB[:, 0:768])
        self.cp('act', self.HBF[:, :, :], self.rH[d][:, :, :])
        self.memset('pool', self.A1BD[:, :, :, :], 0.0)
        self.memset('pool', self.A2BD[:, :, :, :], 0.0)
        self.memset('pool', self.NN[0][:, :, :], 0.0)
        mk = CC['MKF'] if d == 0 else CC['MKB']
        nmk = CC['NMKF'] if d == 0 else CC['NMKB']
        mn = CC['MNF'] if d == 0 else CC['MNB']
        for j in range(4):
            c = j if d == 0 else 3 - j
            cs = slice(c * 64, (c + 1) * 64)
            KRc = lambda hp: self.V(self.KR, 0, 128, hp * 512 + c * 128, [[1, 128]])
            p1, p2, p3 = self.pq(), self.pq(), self.pq()
            for hp in range(3):
                self.mm(p1[:, hp * 128:(hp + 1) * 128], BD["kt"][:, hp, c, :], KRc(hp), inc=(hp == 2))
            for hp in range(3):
                self.mm(p2[:, hp * 128:(hp + 1) * 128], BD["b"][:, hp, c, :], KRc(hp), inc=(hp == 2))
            for hp in range(3):
                self.mm(p3[:, hp * 64:(hp + 1) * 64], BD["kh"][:, hp, c, :], self.BTT[:, hp, cs], inc=(hp == 2))
            for par in range(2):
                pp_ = par * 64
                self.tt('dve', self.V(self.A1BD, pp_, 64, pp_, [[256, 3], [128, 2], [1, 64]]),
                        self.V(p1, pp_, 64, 0, [[128, 3], [64, 2], [1, 64]]),
                        self.V(self.CST, pp_, 64, mk, [[0, 3], [64, 2], [1, 64]]), ALU.mult)
                self.tt('dve', self.V(self.A2BD, pp_, 64, pp_, [[256, 3], [128, 2], [1, 64]]),
                        self.V(p2, pp_, 64, 0, [[128, 3], [64, 2], [1, 64]]),
                        self.V(self.CST, pp_, 64, nmk, [[0, 3], [64, 2], [1, 64]]), ALU.mult)
                self.tt('dve', self.V(self.NN[0], pp_, 64, pp_, [[128, 3], [1, 64]]),
                        self.V(p3, pp_, 64, 0, [[64, 3], [1, 64]]),
                        self.V(self.CST, pp_, 64, mn, [[0, 3], [1, 64]]), ALU.mult)
            pr_ = self.pq()
            for hp in range(3):
                o = pr_[:, hp * 64:(hp + 1) * 64]
                self.mm(o, BD["kh"][:, hp, c, :], self.HBF[:, hp, :], start=True, stop=False)
                self.mm(o, self.A1BD[:, hp, 0, :], self.VSTK[:, c, hp, :], start=False, stop=True, inc=(hp == 2))
            u192 = self.V(self.U, 0, 128, 0, [[1, 192]])
            ub192 = self.V(self.UBF, 0, 128, 0, [[1, 192]])
            self.cp('dve', u192, pr_[:, 0:192])
            self.cp('act', ub192, u192)
            for k in range(6):
                NTk = self.A2BD[:, :, 0, :] if k == 0 else self.NTT[k % 2][:, :, :]
                Nk = self.NN[k % 2]
                pu = self.pq()
                for hp in range(3):
                    self.mm(pu[:, hp * 64:(hp + 1) * 64], NTk[:, hp, :], self.UBF[:, hp, :], inc=(hp == 2))
                if k < 5:
                    pnt = self.pq()
                    for hp in range(3):
                        self.mm(pnt[:, hp * 128:(hp + 1) * 128], Nk[:, hp, :], NTk[:, hp, :], inc=(hp == 2))
                    pn = self.pq()
                    for hp in range(3):
                        self.mm(pn[:, hp * 128:(hp + 1) * 128], NTk[:, hp, :], Nk[:, hp, :], inc=(hp == 2))
                self.tt('dve', ub192, u192, pu[:, 0:192], ALU.add)
                if k < 5:
                    self.cp('act', self.V(self.NTT[(k + 1) % 2], 0, 128, 0, [[1, 384]]), pnt[:, 0:384])
                    self.cp('act', self.V(self.NN[(k + 1) % 2], 0, 128, 0, [[1, 384]]), pn[:, 0:384])
                    self.tt('dve', u192, u192, pu[:, 0:192], ALU.add)
            py = self.pq()
            for hp in range(3):
                o = py[:, hp * 64:(hp + 1) * 64]
                self.mm(o, BD["r"][:, hp, c, :], self.HBF[:, hp, :], start=True, stop=False)
                self.mm(o, self.A1BD[:, hp, 1, :], self.VSTK[:, c, hp, :], start=False, stop=False)
                self.mm(o, self.A2BD[:, hp, 1, :], self.UBF[:, hp, :], start=False, stop=True, inc=(hp == 2))
            ybv = self.V(self.YB, 0, 128, c * 192, [[1, 192]])
            if d == 1:
                self.cp('act', ybv, py[:, 0:192])
            else:
                self.tt('dve', ybv, ybv, py[:, 0:192], ALU.add)
            ph = self.pq()
            for hp in range(3):
                o = ph[:, hp * 64:(hp + 1) * 64]
                self.mm(o, self.KTTOK[:, c, hp, :], self.VSTK[:, c, hp, :], start=True, stop=False)
                self.mm(o, self.BTTOK[:, c, hp, :], self.UBF[:, hp, :], start=False, stop=True, inc=(hp == 2))
            self.tt('dve', self.HTMP[:, :, :], self.rH[d][:, :, :], self.V(ph, 0, 128, 0, [[64, 3], [1, 64]]), ALU.add)
            self.tt('dve', self.rH[d][:, :, :], self.HTMP[:, :, :], self.V(self.GL, 0, 128, c, [[4, 3], [0, 64]]), ALU.mult)
            self.cp('act', self.HBF[:, :, :], self.rH[d][:, :, :])
        if last[d] and prompt:
            S.dma('sp', self.o_rH[seq_idx, l, d], self.rH[d][:, :, :])

    def stage_out(self, l, mod, xsrc, xdst, r0):
        S = self.S
        for tt_ in range(2):
            o, xw = self.XN[tt_], self.XW[tt_]
            S.dma('sp', xw[:, :], xsrc[r0 + tt_ * 128: r0 + (tt_ + 1) * 128, :])
            for ch in range(2):
                ps = self.pq()
                for kc in range(8):
                    self.mm(ps[:, :], self.MIXT[:, kc, tt_ * 128:(tt_ + 1) * 128], self.W_OUT[:, kc, ch * 512:(ch + 1) * 512],
                            start=(kc == 0), stop=(kc == 7))
                self.cp('act' if ch else 'dve', o[:, ch * 512:(ch + 1) * 512], ps[:, :])
            self.dump('o_proj%d' % tt_, o[:, :])
            self.dump('o_x%d' % tt_, xw[:, :])
            ssq = self.TMPS[:, 24 + tt_:25 + tt_]
            junk = self.V(self.BLK, 0, 128, 0, [[1, D]])
            self.act(junk, o[:, :], AF.Square, accum=ssq)
            rs = self.TMPS[:, 26 + tt_:27 + tt_]
            self.ts('dve', rs, ssq, 1.0 / D, EPS, ALU.mult, ALU.add)
            self.act(rs, rs, AF.Sqrt)
            self.recip(rs, rs)
            self.dump('o_rs%d' % tt_, rs)
            self.stt(o[:, :], o[:, :], rs, self.GATEB[:, mod, :], ALU.mult, ALU.mult)
            self.dump('o_g%d' % tt_, o[:, :])
            self.tt('dve', o[:, :], o[:, :], xw[:, :], ALU.add)
            S.dma('sp', xdst[r0 + tt_ * 128: r0 + (tt_ + 1) * 128, :], o[:, :])

    def build(self, layers=(0, 1)):
        try:
            self._build(layers)
        except StopIteration:
            pass
        self.S.finish('sp')
        return self.nc

    def _build(self, layers):
        NP, TS = self.NP, self.TS
        self.setup()
        if self.stop == 'setup':
            raise StopIteration
        for li, l in enumerate(layers):
            self.load_layer(l)
            if self.stop == 'load':
                raise StopIteration
            xsrc = self.x_in if li == 0 else self.x1
            xdst = self.y_out if li == len(layers) - 1 else self.x1
            T_, F_ = {0: True, 1: True}, {0: False, 1: False}
            for s in range(NP):
                self.visit(l, 0, xsrc, xdst, s * 256, 256, 0, [0, 1], False, s, T_, T_)
            if TS > 0:
                nb = TS // TB
                row0 = NP * 256
                for b in range(nb - 1, -1, -1):
                    self.visit(l, 1, xsrc, xdst, row0, TS, b * TB, [1], True, 0,
                               {0: False, 1: b == nb - 1}, {0: False, 1: b == 0})
                for b in range(nb):
                    self.visit(l, 1, xsrc, xdst, row0, TS, b * TB, [0], True, 0,
                               {0: b == 0, 1: False}, {0: b == nb - 1, 1: False})


def prep_shared(inp):
    f = lambda a: np.ascontiguousarray(np.asarray(a, dtype=np.float32))
    b_mod = f(inp['b_mod'])
    sh = {}
    sh['w_mod'] = f(inp['w_mod'])
    sh['w_in'] = f(inp['w_in'])
    sh['w_out'] = f(inp['w_out'])
    sh['bmodT'] = f(b_mod[:, :2048].reshape(2, 16, 128).transpose(2, 0, 1))
    sh['bmodg'] = f(b_mod[:, 2048:3072])
    sh['gpost'] = f(inp['g_post'])
    pp = np.zeros((128, 2 * PL), np.float32)
    for l in range(2):
        o = l * PL
        def put(key, arr, n):
            pp[:, o + PC[key]: o + PC[key] + n] = np.asarray(arr, np.float32).reshape(n, 128).T
        put('G_PRE', inp['g_pre'][l], 8)
        put('M_NORM', inp['m_norm'][l], 3)
        put('R_MU', inp['r_mu'][l], 11)
        put('R_W0', np.asarray(inp['r_w0'][l]).reshape(-1), 6)
        put('R_A0', np.asarray(inp['r_a0'][l]).reshape(-1), 6)
        put('R_KK', inp['r_kk'][l], 3)
        put('R_KA', inp['r_ka'][l], 3)
        put('R_RK', inp['r_rk'][l], 3)
        put('R_NORM', inp['r_norm'][l], 3)
        put('L_CONV', np.asarray(inp['l_conv'][l]).reshape(-1), 8)
        put('L_CONVB', inp['l_conv_b'][l], 2)
        put('L_BA', np.asarray(inp['l_ba'][l]).reshape(-1), 4)
        put('L_BX', np.asarray(inp['l_bx'][l]).reshape(-1), 4)
        put('L_LAM', np.asarray(inp['l_lambda'][l]).reshape(-1), 4)
        pp[0:6, o + PC['M_BI']: o + PC['M_BI'] + 2] = np.asarray(inp['m_bi'][l], np.float32).T
        pp[0:6, o + PC['M_BF']: o + PC['M_BF'] + 2] = np.asarray(inp['m_bf'][l], np.float32).T
    sh['pp'] = pp
    sh['cst'] = make_consts()
    lora = np.zeros((128, 2, 2, 384), np.float32)
    for wi, key in enumerate(['r_w2', 'r_a2']):
        a = np.asarray(inp[key], np.float32)
        lora[:, :, wi, :] = a.transpose(1, 2, 0, 3).reshape(128, 2, 384)
    sh['lora'] = lora
    lruw = np.zeros((128, 2, 8, 128), np.float32)
    for gi, key in enumerate(['l_wa', 'l_wx']):
        a = np.asarray(inp[key], np.float32)
        for l in range(2):
            for d in range(2):
                for pr in range(2):
                    for hb in range(2):
                        n = 2 * pr + hb
                        lruw[hb * 64:(hb + 1) * 64, l, (gi * 2 + d) * 2 + pr, hb * 64:(hb + 1) * 64] = a[l, d, n]
    sh['lruw'] = lruw
    return sh


def prep_core(inp, b, NP, TS):
    f = lambda a: np.ascontiguousarray(np.asarray(a, dtype=np.float32))
    m = {}
    xp = np.asarray(inp['x_prompt'], np.float32)[b * NP:(b + 1) * NP].reshape(NP * 256, D)
    if TS > 0:
        xs = np.asarray(inp['x_sample'], np.float32)[b]
        m['x_in'] = f(np.concatenate([xp, xs], 0))
    else:
        m['x_in'] = f(xp)
    cc = np.stack([np.asarray(inp['c_ctx'], np.float32), np.asarray(inp['c'], np.float32)[b]], -1)
    m['cc'] = f(cc.reshape(8, 128, 2).transpose(1, 0, 2))
    C = np.asarray(inp['state_mlstm_C'], np.float32)[b]
    n = np.asarray(inp['state_mlstm_n'], np.float32)[b]
    Cn = np.concatenate([C, n[..., None]], -1)
    Cn = Cn.reshape(2, 2, 3, 2, 64, 65).transpose(0, 1, 3, 4, 2, 5).reshape(2, 2, 128, 3, 65)
    m['st_mC'] = f(Cn)
    m['st_mm'] = f(np.asarray(inp['state_mlstm_m'], np.float32)[b].transpose(2, 0, 1))
    R = np.asarray(inp['state_rwkv'], np.float32)[b]
    R = R.transpose(0, 1, 2, 4, 3)
    R = R.reshape(2, 2, 3, 2, 64, 64).transpose(0, 1, 3, 4, 2, 5).reshape(2, 2, 128, 3, 64)
    m['st_rH'] = f(R)
    L = np.asarray(inp['state_rglru'], np.float32)[b]
    m['st_l'] = f(L.reshape(2, 2, 2, 128).transpose(3, 0, 1, 2))
    return m


def unpack_core(r, NP, TS):
    y = r['y_out']
    yp = y[:NP * 256].reshape(NP, 256, D)
    ys = y[NP * 256:]
    mC = r['o_mC'].reshape(NP, 2, 2, 2, 64, 3, 65).transpose(0, 1, 2, 5, 3, 4, 6).reshape(NP, 2, 2, 6, 64, 65)
    newC = np.ascontiguousarray(mC[..., :64])
    newn = np.ascontiguousarray(mC[..., 64])
    newm = r['o_mm'].reshape(NP, 2, 2, 6)
    rH = r['o_rH'].reshape(NP, 2, 2, 2, 64, 3, 64).transpose(0, 1, 2, 5, 3, 4, 6).reshape(NP, 2, 2, 6, 64, 64)
    newr = np.ascontiguousarray(rH.transpose(0, 1, 2, 3, 5, 4))
    newl = np.ascontiguousarray(r['o_l'].transpose(0, 1, 2, 4, 3).reshape(NP, 2, 2, 256))
    return yp, ys, newC, newn, newm, newr, newl


_NC_CACHE = {}


def kernel(**inputs):
    NP, TS = 4, 2048
    key = (NP, TS)
    if key not in _NC_CACHE:
        _NC_CACHE[key] = Builder(NP, TS).build()
    nc = _NC_CACHE[key]
    sh = prep_shared(inputs)
    in_maps = []
    for b in range(NCORES):
        m = dict(sh)
        m.update(prep_core(inputs, b, NP, TS))
        in_maps.append(m)
    res = run_bass_kernel_spmd(nc, in_maps, core_ids=list(range(NCORES)))
    outs = [unpack_core(r, NP, TS) for r in res.results]
    y_prompt = np.concatenate([o[0] for o in outs], 0)
    y_sample = np.stack([o[1] for o in outs], 0)
    cat = lambda i: np.concatenate([o[i] for o in outs], 0)
    return (y_prompt.astype(np.float32), y_sample.astype(np.float32), cat(2).astype(np.float32),
            cat(3).astype(np.float32), cat(4).astype(np.float32), cat(5).astype(np.float32), cat(6).astype(np.float32))
```

```python
import numpy as np
import concourse.bass as bass
import concourse.mybir as mybir
from concourse.bass_utils import run_bass_kernel_spmd

F32 = mybir.dt.float32
BF16 = mybir.dt.bfloat16
AF = mybir.ActivationFunctionType
ALU = mybir.AluOpType
AX = mybir.AxisListType

D = 1024
IN_COLS = 4248
EPS = 1e-6
DSC = 0.6065306597126334
TB = 256
NCORES = 8


def _prod(xs):
    r = 1
    for x in xs:
        r *= int(x)
    return r


class Sync:
    def __init__(self, nc, n_dma_sems=32):
        self.nc = nc
        self.engs = {'pe': nc.tensor, 'dve': nc.vector, 'act': nc.scalar,
                     'pool': nc.gpsimd, 'sp': nc.sync}
        self.sem = {}
        self.cnt = {}
        for e in ['pe', 'dve', 'act', 'pool']:
            self.sem[e] = nc.alloc_semaphore('sem_' + e)
            self.cnt[e] = 0
        self.seen = {e: {} for e in self.engs}
        self.dma_ring = [nc.alloc_semaphore('dq_%d' % i) for i in range(n_dma_sems)]
        self.dma_uses = [0] * n_dma_sems
        self.dma_next = 0
        self.rec = {}
        self.untracked = set()
        self.n_wait = 0
        self.n_ins = 0
        self.pstep_cache = {}
        self.sb_addr = {}

    def region(self, ap):
        t = ap.tensor
        name = t.name
        apl = [(int(s), int(c)) for (s, c) in ap.ap]
        off = int(ap.offset)
        if type(t).__name__.startswith('DRam'):
            lo = off + sum(min(0, s * (c - 1)) for s, c in apl)
            hi = off + sum(max(0, s * (c - 1)) for s, c in apl) + 1
            return (name, 0, 1, lo, hi)
        pstep = self.pstep_cache.get(name)
        if pstep is None:
            pstep = _prod(list(t.shape)[1:])
            self.pstep_cache[name] = pstep
        p0 = off // pstep
        f0 = off % pstep
        npart = apl[0][1]
        rest = apl[1:]
        lo = f0 + sum(min(0, s * (c - 1)) for s, c in rest)
        hi = f0 + sum(max(0, s * (c - 1)) for s, c in rest) + 1
        if name in self.sb_addr:
            base, es = self.sb_addr[name]
            return ('SB', p0, p0 + npart, base + lo * es, base + hi * es)
        return ('PS:' + name, (p0 // 32) * 32, ((p0 + npart + 31) // 32) * 32, (lo // 512) * 512, ((hi + 511) // 512) * 512)

    @staticmethod
    def _ovl(a, b):
        return a[1] < b[2] and b[1] < a[2] and a[3] < b[4] and b[3] < a[4]

    @staticmethod
    def _contains(a, b):
        return a[1] <= b[1] and b[2] <= a[2] and a[3] <= b[3] and b[4] <= a[4]

    def _collect(self, e, reads, writes):
        deps = {}
        own = self.sem.get(e)
        rregs = [self.region(a) for a in reads]
        wregs = [self.region(a) for a in writes]
        for r in rregs:
            if r[0] in self.untracked:
                continue
            isps = r[0].startswith('PS:')
            for (reg, kind, sem, val) in self.rec.get(r[0], ()):
                if (kind == 'w' or (isps and sem is not own)) and self._ovl(reg, r):
                    if e == 'pe' and sem is own:
                        continue
                    k = id(sem)
                    if deps.get(k, (None, 0))[1] < val:
                        deps[k] = (sem, val)
        for w in wregs:
            if w[0] in self.untracked:
                continue
            for (reg, kind, sem, val) in self.rec.get(w[0], ()):
                if self._ovl(reg, w):
                    if sem is own:
                        continue
                    k = id(sem)
                    if deps.get(k, (None, 0))[1] < val:
                        deps[k] = (sem, val)
        return deps, rregs, wregs

    def _record(self, rregs, wregs, sem, val):
        for r in rregs:
            if r[0] in self.untracked:
                continue
            lst = self.rec.setdefault(r[0], [])
            lst[:] = [x for x in lst if not (x[1] == 'r' and x[2] is sem and self._contains(r, x[0]))]
            lst.append((r, 'r', sem, val))
        for w in wregs:
            if w[0] in self.untracked:
                continue
            lst = self.rec.setdefault(w[0], [])
            lst[:] = [x for x in lst if not self._contains(w, x[0])]
            lst.append((w, 'w', sem, val))

    def wait(self, e, sem, val):
        k = id(sem)
        if self.seen[e].get(k, 0) >= val:
            return
        self.engs[e].wait_ge(sem, val)
        self.seen[e][k] = val
        self.n_wait += 1

    max_ins = None
    paranoid = False

    def emit(self, e, reads, writes, build, inc=True):
        if self.max_ins is not None and self.n_ins >= self.max_ins:
            raise StopIteration
        deps, rregs, wregs = self._collect(e, reads, writes)
        for (sem, val) in deps.values():
            self.wait(e, sem, val)
        if self.paranoid:
            for e2 in ['pe', 'dve', 'act', 'pool']:
                if self.cnt[e2] > 0 and not (e == 'pe' and e2 == 'pe'):
                    self.wait(e, self.sem[e2], self.cnt[e2])
        ins = build(self.engs[e])
        self.n_ins += 1
        if inc:
            self.cnt[e] += 1
            ins.then_inc(self.sem[e], 1)
            val = self.cnt[e]
        else:
            val = self.cnt[e] + 1
        self._record(rregs, wregs, self.sem[e], val)
        return ins

    def dma(self, q, out, in_, **kw):
        if self.max_ins is not None and self.n_ins >= self.max_ins:
            raise StopIteration
        i = self.dma_next
        self.dma_next = (i + 1) % len(self.dma_ring)
        sem = self.dma_ring[i]
        uses = self.dma_uses[i]
        if uses > 0:
            self.wait(q, sem, 16 * uses)
        deps, rregs, wregs = self._collect(q, [in_], [out])
        for (s, v) in deps.values():
            self.wait(q, s, v)
        ins = self.engs[q].dma_start(out=out, in_=in_, **kw)
        ins.then_inc(sem, 16)
        self.n_ins += 1
        self.dma_uses[i] = uses + 1
        self._record(rregs, wregs, sem, 16 * (uses + 1))
        return ins

    def finish(self, q='sp'):
        for i, sem in enumerate(self.dma_ring):
            if self.dma_uses[i] > 0:
                self.wait(q, sem, 16 * self.dma_uses[i])
        for e in ['pe', 'dve', 'act', 'pool']:
            if self.cnt[e] > 0:
                self.wait(q, self.sem[e], self.cnt[e])


class Arena:
    def __init__(self, base, size):
        self.base, self.size, self.ptr = base, size, 0

    def take(self, nbytes):
        off = (self.ptr + 31) // 32 * 32
        self.ptr = off + nbytes
        assert self.ptr <= self.size, ("arena overflow", self.ptr, self.size)
        return self.base + off


PL = 72
PC = dict(G_PRE=0, M_NORM=8, R_MU=11, R_W0=22, R_A0=28, R_KK=34, R_KA=37, R_RK=40, R_NORM=43,
          L_CONV=46, L_CONVB=54, L_BA=56, L_BX=60, L_LAM=64, M_BI=68, M_BF=70)
DL = 32
DC = dict(OMKA=0, CLAM=3, C2LAM=7, NBF=11, OMMU=16)
CC = dict(IDENT=0, BONES=128, MKF=256, MKB=384, MNF=512, MNB=576, MUI=640, MLI=704, RMASK=768,
          LSEL=1536, PSEL=1664, NMKF=1668, NMKB=1796, ONES=1924, ISTK=2052)
NCST = 2052 + 64


def make_consts():
    c = np.zeros((128, NCST), np.float32)
    c[:, 0:128] = np.eye(128)
    c[0:64, 128:192] = 1.0
    c[64:128, 192:256] = 1.0
    s = np.arange(64)[:, None]
    t = np.arange(64)[None, :]
    us, ui = (s < t).astype(np.float32), (s <= t).astype(np.float32)
    ls, li = (s > t).astype(np.float32), (s >= t).astype(np.float32)
    c[0:64, 256:320], c[0:64, 320:384] = us, ui
    c[0:64, 384:448], c[0:64, 448:512] = ls, li
    c[0:64, 512:576] = -ls
    c[0:64, 576:640] = -us
    c[0:64, 640:704] = ui
    c[0:64, 704:768] = li
    rm = np.ones(768, np.float32)
    rm[::64] = 0.0
    c[:, 768:1536] = rm[None, :]
    for k in range(6):
        c[k, 1536 + (k % 2) * 64: 1536 + (k % 2) * 64 + 64] = 1.0
        c[k, 1664 + k // 2] = 1.0
    c[0:64, 1668:1796] = -c[0:64, 256:384]
    c[0:64, 1796:1924] = -c[0:64, 384:512]
    c[:, 1924:2052] = 1.0
    for (a, b) in ((256, 768), (1668, 1924)):
        c[64:128, a:b] = c[0:64, a:b]
    c[0:64, 2052:2116] = np.eye(64)
    c[64:128, 2052:2116] = np.eye(64)
    return c


class Builder:
    def __init__(self, NP, TS, debug=False, stop=None):
        self.stop = stop
        self.NP, self.TS = NP, TS
        self.NTOK = NP * 256 + TS
        self.debug = debug
        nc = self.nc = bass.Bass("TRN2", target_bir_lowering=False)
        self.S = Sync(nc)
        self._decl_dram()
        self._alloc()

    def _decl_dram(self):
        nc, NP, TS = self.nc, self.NP, self.TS
        di = lambda n, s: nc.dram_tensor(n, list(s), F32, kind="ExternalInput").ap()
        do = lambda n, s: nc.dram_tensor(n, list(s), F32, kind="ExternalOutput").ap()
        dx = lambda n, s: nc.dram_tensor(n, list(s), F32, kind="Internal").ap()
        self.x_in = di("x_in", [self.NTOK, D])
        self.cc = di("cc", [128, 8, 2])
        self.w_mod = di("w_mod", [2, D, 3 * D])
        self.bmodT = di("bmodT", [128, 2, 16])
        self.bmodg = di("bmodg", [2, D])
        self.gpost = di("gpost", [2, D])
        self.w_in = di("w_in", [2, D, IN_COLS])
        self.w_out = di("w_out", [2, D, D])
        self.pp = di("pp", [128, 2 * PL])
        self.cst = di("cst", [128, NCST])
        self.lora = di("lora", [128, 2, 2, 384])
        self.lruw = di("lruw", [128, 2, 8, 128])
        self.st_mC = di("st_mC", [2, 2, 128, 3, 65])
        self.st_mm = di("st_mm", [6, 2, 2])
        self.st_rH = di("st_rH", [2, 2, 128, 3, 64])
        self.st_l = di("st_l", [128, 2, 2, 2])
        for n in ["x_in", "cc", "w_mod", "bmodT", "bmodg", "gpost", "w_in", "w_out", "pp", "cst", "lora",
                  "lruw", "st_mC", "st_mm", "st_rH", "st_l"]:
            self.S.untracked.add(n)
        self.y_out = do("y_out", [self.NTOK, D])
        self.o_mC = do("o_mC", [NP, 2, 2, 128, 3, 65])
        self.o_mm = do("o_mm", [NP, 2, 2, 6, 1])
        self.o_rH = do("o_rH", [NP, 2, 2, 128, 3, 64])
        self.o_l = do("o_l", [NP, 2, 2, 128, 2])
        self.x1 = dx("x1", [self.NTOK, D])
        self.sHB = dx("sHB", [max(TS, 64), 384])
        self.sYB = dx("sYB", [max(TS // 256, 1), 128, 768])
        self.sLB = dx("sLB", [128, 2, max(TS, 64)])
        if self.debug:
            self.dbg = do("dbg", [128, 32768])
            self.dbg_map = {}
            self.dbg_off = 0

    def T(self, name, shape, dtype, arena):
        es = 2 if dtype == BF16 else 4
        nb = _prod(shape[1:]) * es
        off = arena.take(nb)
        t = self.nc.alloc_sbuf_tensor_at(name, list(shape), dtype, offset=off)
        self.S.sb_addr[t.name] = (off, es)
        return t

    def _alloc(self):
        nc = self.nc
        B0 = 16384 + 256
        LIM = 224 * 1024 - 256
        P = Arena(B0, LIM - B0)
        T = self.T
        self.W_IN = T("W_IN", [128, 8, IN_COLS], BF16, P)
        self.W_OUT = T("W_OUT", [128, 8, D], BF16, P)
        self.CST = T("CST", [128, NCST], F32, P)
        self.PP = T("PP", [128, 2 * PL], F32, P)
        self.DV = T("DV", [128, 2 * DL], F32, P)
        self.LORA = T("LORA", [128, 2, 2, 384], F32, P)
        self.LRUW = T("LRUW", [128, 2, 8, 128], F32, P)
        self.RKD = T("RKD", [128, 2, 3, 128], F32, P)
        self.GS = T("GS", [128, 2, 8], F32, P)
        self.SH = T("SH", [128, 2, 8], F32, P)
        self.GATEB = T("GATEB", [128, 2, D], F32, P)
        self.IDB = T("IDB", [128, 128], BF16, P)
        self.ISTKB = T("ISTKB", [128, 64], BF16, P)
        self.mC = [T("mC%d" % d, [128, 3, 65], F32, P) for d in range(2)]
        self.mM = [T("mM%d" % d, [6, 1], F32, P) for d in range(2)]
        self.rH = [T("rH%d" % d, [128, 3, 64], F32, P) for d in range(2)]
        self.lS = [T("lS%d" % d, [128, 2], F32, P) for d in range(2)]
        XB = Arena(P.take(16384), 16384)
        self.HT = T("HT", [128, 8, 384], BF16, P)
        self.MIXT = T("MIXT", [128, 8, 256], BF16, P)
        self.BLK = T("BLK", [128, 11, 256], F32, P)
        abase = P.take(0)
        asize = P.size - P.ptr
        self.asize = asize
        mk = lambda: Arena(abase, asize)
        a = Arena(XB.base, XB.size)
        self.XW = [T("XW%d" % i, [128, D], F32, a) for i in range(2)]
        self.XN = [T("XN%d" % i, [128, D], F32, a) for i in range(2)]
        a = Arena(XB.base, XB.size)
        self.YB = T("YB", [128, 4, 3, 64], F32, a)
        self.RZT = T("RZT", [128, 3, 256], F32, a)
        self.WSTG = [T("WSTG0", [128, 8, 256], F32, Arena(self.S.sb_addr[self.BLK.name][0], 11264)),
                     T("WSTG1", [128, 8, 256], F32, Arena(XB.base, 8192)),
                     T("WSTG2", [128, 8, 256], F32, Arena(XB.base + 8192, 8192))]
        a = mk()
        self.WM = T("WM", [128, 8, 512], F32, a)
        self.SCT = T("SCT", [128, 8, 2], F32, a)
        self.SCB = T("SCB", [128, 2, 8, 128], F32, a)
        self.MODT = T("MODT", [128, 16, 2], F32, a)
        self.BMT = T("BMT", [128, 2, 16], F32, a)
        self.BG = T("BG", [128, D], F32, a)
        self.GP = T("GP", [128, D], F32, a)
        self.TMPS = T("TMPS", [128, 32], F32, a)
        a = mk()
        self.QT = T("QT", [128, 3, 256], BF16, a)
        self.KT = T("KT", [128, 3, 256], BF16, a)
        self.GOZ = T("GOZ", [128, 3, 256], F32, a)
        self.KTOK = T("KTOK", [64, 4, 384], BF16, a)
        self.VAUG = T("VAUG", [64, 4, 6, 65], BF16, a)
        self.HB = T("HB", [64, 4, 384], F32, a)
        self.mg = []
        for d in range(2):
            g = {}
            for n in ["gi", "lf", "pre", "bb", "gg", "ee", "fl"]:
                g[n] = T("mg_%s%d" % (n, d), [6, 256], F32, a)
            for n in ["mx", "mch", "mprev", "MM", "dec"]:
                g[n] = T("mg_%s%d" % (n, d), [6, 4], F32, a)
            g["X2"] = T("mg_X2%d" % d, [6, 4, 3], F32, a)
            g["etok"] = T("mg_etok%d" % d, [64, 4, 6], F32, a)
            g["fltok"] = T("mg_fltok%d" % d, [64, 4, 6], F32, a)
            g["decb"] = T("mg_decb%d" % d, [128, 4, 3], F32, a)
            self.mg.append(g)
        self.STSB = [T("STSB%d" % i, [64, 6, 64], BF16, a) for i in range(2)]
        self.VP = [T("VP%d" % i, [64, 6, 65], BF16, a) for i in range(2)]
        self.CDEC = T("CDEC", [128, 3, 65], F32, a)
        self.CDBF = T("CDBF", [128, 3, 65], BF16, a)
        self.DN = T("DN", [64, 6], F32, a)
        self.RDN = T("RDN", [64, 6], F32, a)
        self.HD = T("HD", [64, 6, 64], F32, a)
        self.SQ = T("SQ", [64, 4, 384], F32, a)
        self.SSQ = T("SSQ", [64, 24], F32, a)
        self.RSTD = T("RSTD", [64, 24], F32, a)
        self.ZT = [T("ZT%d" % i, [128, 256], F32, a) for i in range(2)]
        a = mk()
        self.XL = T("XL", [128, 2, 392], F32, a)
        self.XC = T("XC", [128, 2, 256], F32, a)
        self.LZT = T("LZT", [128, 2, 256], F32, a)
        self.lt = {n: T("lt_" + n, [128, 2, 256], F32, a) for n in ["rg", "ig", "aa", "a2", "bt"]}
        self.HL = [T("HL%d" % d, [128, 2, 256], F32, a) for d in range(2)]
        a = mk()
        self.URS = T("URS", [128, 11, 384], F32, a)
        a = mk()
        self.KH = T("KH", [128, 3, 256], F32, a)
        R1 = a.take(7 * 3072)
        a1 = Arena(R1, 7 * 3072)
        self.rt = {n: T("rt_" + n, [128, 3, 256], F32, a1) for n in ["sg", "aa", "kt", "bb", "cs", "d", "E"]}
        a2 = Arena(R1, 7 * 3072)
        self.A1BD = T("A1BD", [128, 3, 2, 128], BF16, a2)
        self.A2BD = T("A2BD", [128, 3, 2, 128], BF16, a2)
        self.NN = [T("NN%d" % i, [128, 3, 128], BF16, a2) for i in range(2)]
        self.NTT = [T("NTT%d" % i, [128, 3, 128], BF16, a2) for i in range(2)]
        self.U = T("U", [128, 3, 64], F32, a2)
        self.UBF = T("UBF", [128, 3, 64], BF16, a2)
        self.HBF = T("HBF", [128, 3, 64], BF16, a2)
        self.HTMP = T("HTMP", [128, 3, 64], F32, a2)
        self.YBD = T("YBD", [128, 3, 128], F32, a2)
        self.KTTOK = T("KTTOK", [128, 4, 3, 128], BF16, a2)
        self.BTTOK = T("BTTOK", [128, 4, 3, 128], BF16, a2)
        self.rSSQ = T("rSSQ", [128, 12], F32, a2)
        self.rRSTD = T("rRSTD", [128, 12], F32, a2)
        self.KR = T("KR", [128, 3, 4, 2, 64], BF16, a)
        self.BTT = T("BTT", [128, 3, 256], BF16, a)
        self.KTT = T("KTT", [128, 3, 256], BF16, a)
        bdr = a.take(4 * 3072)
        a3 = Arena(bdr, 4 * 3072)
        self.BD = {n: T("BD_" + n, [128, 3, 4, 128], BF16, a3) for n in ["kt", "b", "kh", "r"]}
        a3 = Arena(bdr, 4 * 3072)
        self.ft = {n: T("ft_" + n, [128, 3, 256], F32, a3) for n in ["rk", "bon", "t1"]}
        self.YSQ = T("YSQ", [128, 4, 3, 64], F32, a3)
        self.BDV = T("BDV", [128, 3, 4, 128], BF16, a)
        self.VSTK = T("VSTK", [128, 4, 3, 64], BF16, a)
        self.TWL = T("TWL", [128, 256], F32, a)
        self.GL = T("GL", [128, 3, 4], F32, a)
        self.PA = nc.alloc_psum_tensor("PA", [128, 1024], F32)
        self.PB = nc.alloc_psum_tensor("PB", [128, 1024], F32)
        self.PQ = [nc.alloc_psum_tensor("PQ%d" % i, [128, 512], F32) for i in range(3)]
        self.PTB = nc.alloc_psum_tensor("PTB", [128, 1024], BF16)
        self.pq_i = 0

    def pq(self):
        t = self.PQ[self.pq_i % 3]
        self.pq_i += 1
        return t

    def V(self, t, p0, npart, off, dims):
        pstep = _prod(list(t.shape)[1:])
        return bass.AP(t, p0 * pstep + off, [[pstep, npart]] + [list(d) for d in dims])

    def tt(self, e, out, in0, in1, op):
        return self.S.emit(e, [in0, in1], [out], lambda g: g.tensor_tensor(out=out, in0=in0, in1=in1, op=op))

    def ts(self, e, out, in0, s1, s2, op0, op1=None):
        rd = [in0] + [s for s in (s1, s2) if not isinstance(s, (int, float)) and s is not None]
        if op1 is None:
            return self.S.emit(e, rd, [out], lambda g: g.tensor_scalar(out=out, in0=in0, scalar1=s1, scalar2=None, op0=op0))
        return self.S.emit(e, rd, [out], lambda g: g.tensor_scalar(out=out, in0=in0, scalar1=s1, scalar2=s2, op0=op0, op1=op1))

    def stt(self, out, in0, sc, in1, op0, op1):
        rd = [in0, in1] + ([] if isinstance(sc, (int, float)) else [sc])
        return self.S.emit('dve', rd, [out], lambda g: g.scalar_tensor_tensor(out=out, in0=in0, scalar=sc, in1=in1, op0=op0, op1=op1))

    def act(self, out, in_, func, bias=None, scale=None, accum=None):
        rd = [in_]
        kw = {}
        if bias is not None:
            kw['bias'] = bias
            if not isinstance(bias, (int, float)):
                rd.append(bias)
        if scale is not None:
            kw['scale'] = scale
            if not isinstance(scale, (int, float)):
                rd.append(scale)
        wr = [out]
        if accum is not None:
            kw['accum_out'] = accum
            wr.append(accum)
        return self.S.emit('act', rd, wr, lambda g: g.activation(out=out, in_=in_, func=func, **kw))

    def cp(self, e, out, in_):
        if e == 'act':
            return self.act(out, in_, AF.Copy)
        return self.S.emit(e, [in_], [out], lambda g: g.tensor_copy(out=out, in_=in_))

    def _pe_rowtile_guard(self, lhsT, out):
        S = self.S
        st = S.region(lhsT)
        k = st[2] - st[1]
        kr = 32 if k <= 32 else (64 if k <= 64 else 128)
        rows = (st[1], st[1] + kr)
        oreg = S.region(out)
        last = getattr(self, '_last_pe', None)
        if last is not None:
            lrows, loreg, lins, linc = last
            disjoint = rows[1] <= lrows[0] or lrows[1] <= rows[0]
            samebank = (loreg[0] == oreg[0]) and loreg[3] < oreg[4] and oreg[3] < loreg[4]
            if disjoint and samebank:
                if not linc:
                    S.cnt['pe'] += 1
                    lins.then_inc(S.sem['pe'], 1)
                S.wait('pe', S.sem['pe'], S.cnt['pe'])
        return rows, oreg

    def mm(self, out, lhsT, rhs, start=True, stop=True, inc=None):
        if inc is None:
            inc = stop
        rows, oreg = self._pe_rowtile_guard(lhsT, out)
        ins = self.S.emit('pe', [lhsT, rhs], [out],
                          lambda g: g.matmul(out, lhsT=lhsT, rhs=rhs, start=start, stop=stop), inc=inc)
        self._last_pe = (rows, oreg, ins, inc)
        return ins

    def tr(self, out, in_, inc=True, bf=False):
        n = in_.shape[0]
        ident = self.IDB[0:n, 0:n] if bf else self.CST[0:n, CC['IDENT']:CC['IDENT'] + n]
        rows, oreg = self._pe_rowtile_guard(in_, out)
        ins = self.S.emit('pe', [in_, ident], [out],
                          lambda g: g.transpose(out=out, in_=in_, identity=ident), inc=inc)
        self._last_pe = (rows, oreg, ins, inc)
        return ins

    def memset(self, e, ap, v):
        return self.S.emit(e, [], [ap], lambda g: g.memset(ap, v))

    def scan(self, out, d0, d1, init, op0, op1):
        rd = [d0, d1] + ([] if isinstance(init, (int, float)) else [init])
        return self.S.emit('dve', rd, [out], lambda g: g.tensor_tensor_scan(out=out, data0=d0, data1=d1, initial=init, op0=op0, op1=op1))

    def recip(self, out, in_):
        return self.S.emit('dve', [in_], [out], lambda g: g.reciprocal(out=out, in_=in_))

    def reduce(self, out, in_, op, axis=AX.X):
        return self.S.emit('dve', [in_], [out], lambda g: g.tensor_reduce(out=out, in_=in_, axis=axis, op=op))

    def dump(self, name, ap):
        if not self.debug or name in self.dbg_map:
            return
        if getattr(self, 'dbg_filter', None) is not None and not any(name.startswith(p) for p in self.dbg_filter):
            return
        shp = list(ap.shape)
        npart, nfree = shp[0], _prod(shp[1:])
        stage = self.XN[1]
        assert nfree <= 1024
        dst = self.V(stage, 0, npart, 0, [[_prod(shp[i + 1:]), shp[i]] for i in range(1, len(shp))])
        self.cp('dve', dst, ap)
        self.S.dma('sp', self.dbg[0:npart, self.dbg_off:self.dbg_off + nfree], stage[0:npart, 0:nfree], allow_slow_non_contiguous=True)
        self.dbg_map[name] = (self.dbg_off, npart, shp[1:])
        self.dbg_off += nfree

    def ppc(self, l, key, j=0, rows=128):
        c = l * PL + PC[key] + j
        return self.PP[0:rows, c:c + 1]

    def dvc(self, l, key, j=0, rows=128):
        c = l * DL + DC[key] + j
        return self.DV[0:rows, c:c + 1]

    def setup(self):
        S = self.S
        S.dma('sp', self.CST[:, :], self.cst[:, :])
        S.dma('sp', self.PP[:, :], self.pp[:, :])
        S.dma('sp', self.LORA[:, :, :, :], self.lora[:, :, :, :])
        S.dma('sp', self.LRUW[:, :, :, :], self.lruw[:, :, :, :])
        self.memset('dve', self.DV[:, :], 0.0)
        self.cp('dve', self.IDB[:, :], self.CST[:, CC['IDENT']:CC['IDENT'] + 128])
        self.cp('dve', self.ISTKB[:, :], self.CST[:, CC['ISTK']:CC['ISTK'] + 64])
        for l in range(2):
            self.ts('dve', self.DV[:, l * DL + DC['OMKA']: l * DL + DC['OMKA'] + 3],
                    self.PP[:, l * PL + PC['R_KA']: l * PL + PC['R_KA'] + 3], -1.0, 1.0, ALU.mult, ALU.add)
            self.ts('dve', self.DV[:, l * DL + DC['OMMU']: l * DL + DC['OMMU'] + 11],
                    self.PP[:, l * PL + PC['R_MU']: l * PL + PC['R_MU'] + 11], -1.0, 1.0, ALU.mult, ALU.add)
            lam = self.PP[:, l * PL + PC['L_LAM']: l * PL + PC['L_LAM'] + 4]
            t0 = self.TMPS[:, 0:4]
            self.act(t0, lam, AF.Exp, scale=-1.0)
            self.act(t0, t0, AF.Ln, bias=1.0)
            self.ts('dve', self.DV[:, l * DL + DC['CLAM']: l * DL + DC['CLAM'] + 4], t0, -8.0, None, ALU.mult)
            self.ts('dve', self.DV[:, l * DL + DC['C2LAM']: l * DL + DC['C2LAM'] + 4], t0, -16.0, None, ALU.mult)
            self.ts('dve', self.DV[0:6, l * DL + DC['NBF']: l * DL + DC['NBF'] + 2],
                    self.PP[0:6, l * PL + PC['M_BF']: l * PL + PC['M_BF'] + 2], -1.0, None, ALU.mult)
            for hp in range(3):
                self.ts('dve', self.RKD[:, l, hp, :], self.CST[:, CC['BONES']:CC['BONES'] + 128],
                        self.ppc(l, 'R_RK', hp), None, ALU.mult)
        self.memset('dve', self.XL[:, :, 0:2], 0.0)

    def load_layer(self, l):
        S = self.S
        wsrc = self.w_in[l].rearrange("(kc p) c -> p kc c", p=128)
        wo = self.w_out[l].rearrange("(kc p) c -> p kc c", p=128)
        pieces = [(self.W_IN, wsrc, c0, min(256, IN_COLS - c0)) for c0 in range(0, IN_COLS, 256)]
        pieces += [(self.W_OUT, wo, c0, 256) for c0 in range(0, D, 256)]
        for i, (dst, src, c0, n) in enumerate(pieces):
            stg = self.WSTG[i % 3]
            S.dma('sp', stg[:, :, 0:n], src[:, :, c0:c0 + n])
            self.cp('pool', dst[:, :, c0:c0 + n], stg[:, :, 0:n])
        if self.stop == 'load_w':
            raise StopIteration
        S.dma('sp', self.SCT[:, :, :], self.cc[:, :, :])
        S.dma('sp', self.BMT[:, :, :], self.bmodT[:, :, :])
        self.act(self.SCT[:, :, :], self.SCT[:, :, :], AF.Silu)
        for m in range(2):
            for kc in range(8):
                src = self.V(self.SCT, 0, 128, kc * 2 + m, [[0, 128]])
                self.cp('dve', self.SCB[:, m, kc, :], src)
        wm = self.w_mod[l].rearrange("(kc p) c -> p kc c", p=128)
        for blk in range(6):
            S.dma('sp', self.WM[:, :, :], wm[:, :, blk * 512:(blk + 1) * 512])
            if blk < 4:
                ps = self.pq()
                for j in range(4):
                    for kc in range(8):
                        self.mm(ps[:, j * 2:j * 2 + 2], self.WM[:, kc, j * 128:(j + 1) * 128], self.SCT[:, kc, :],
                                start=(kc == 0), stop=(kc == 7))
                o = self.MODT[:, blk * 4:(blk + 1) * 4, :]
                bsrc = self.V(self.BMT, 0, 128, l * 16 + blk * 4, [[1, 4], [0, 2]])
                self.tt('dve', o, self.V(ps, 0, 128, 0, [[2, 4], [1, 2]]), bsrc, ALU.add)
            else:
                half = blk - 4
                for m in range(2):
                    ps = self.pq()
                    for kc in range(8):
                        self.mm(ps[:, :], self.SCB[:, m, kc, :], self.WM[:, kc, :], start=(kc == 0), stop=(kc == 7))
                    self.cp('act', self.GATEB[:, m, half * 512:(half + 1) * 512], ps[:, :])
        if self.stop == 'load_m':
            raise StopIteration
        for m in range(2):
            self.cp('dve', self.SH[:, m, :], self.V(self.MODT, 0, 128, m, [[2, 8]]))
            t0 = self.TMPS[:, 8:16]
            self.ts('dve', t0, self.V(self.MODT, 0, 128, 16 + m, [[2, 8]]), 1.0, None, ALU.add)
            self.tt('dve', self.GS[:, m, :], t0, self.PP[:, l * PL + PC['G_PRE']: l * PL + PC['G_PRE'] + 8], ALU.mult)
        if self.stop == 'load_g':
            raise StopIteration
        S.dma('sp', self.BG[:, :], bass.AP(self.bmodg.tensor, l * D, [[0, 128], [1, D]]))
        S.dma('sp', self.GP[:, :], bass.AP(self.gpost.tensor, l * D, [[0, 128], [1, D]]))
        if self.stop == 'load_b':
            raise StopIteration
        for m in range(2):
            self.tt('dve', self.GATEB[:, m, :], self.GATEB[:, m, :], self.BG[:, :], ALU.add)
            self.tt('dve', self.GATEB[:, m, :], self.GATEB[:, m, :], self.GP[:, :], ALU.mult)

    def proj_fm(self, c0, ncols, t_off, ntok, evac):
        ps = self.pq()
        for kc in range(8):
            self.mm(ps[0:ncols, 0:ntok], self.W_IN[:, kc, c0:c0 + ncols], self.HT[:, kc, t_off:t_off + ntok],
                    start=(kc == 0), stop=(kc == 7))
        evac(ps[0:ncols, 0:ntok])

    def proj_tm(self, c0, ncols, t_off, evac):
        ps = self.pq()
        for kc in range(8):
            self.mm(ps[0:64, 0:ncols], self.HT[:, kc, t_off:t_off + 64], self.W_IN[:, kc, c0:c0 + ncols],
                    start=(kc == 0), stop=(kc == 7))
        evac(ps[0:64, 0:ncols])

    def visit(self, l, mod, xsrc, xdst, row0, T, t0, dirs, grid, seq_idx, first, last):
        S = self.S
        w0 = max(0, t0 - 64)
        w1 = min(T, t0 + TB + 64)
        W = w1 - w0
        co = t0 - w0
        do_f = 0 in dirs
        prompt = (mod == 0)
        ntile = (W + 127) // 128
        for i in range(ntile):
            n = min(128, W - i * 128)
            xw, xn = self.XW[i % 2], self.XN[i % 2]
            r = row0 + w0 + i * 128
            S.dma('sp', xw[0:n, :], xsrc[r:r + n, :])
            ssq = self.TMPS[0:n, 16 + i:17 + i]
            self.act(xn[0:n, :], xw[0:n, :], AF.Square, accum=ssq)
            if self.stop == 'n1':
                raise StopIteration
            rs = self.TMPS[0:n, 20 + i:21 + i]
            self.ts('dve', rs, ssq, 1.0 / D, EPS, ALU.mult, ALU.add)
            self.act(rs, rs, AF.Sqrt)
            self.recip(rs, rs)
            if self.stop == 'n2':
                raise StopIteration
            self.act(xn[0:n, :], xw[0:n, :], AF.Copy, scale=rs)
            if self.stop == 'n3':
                raise StopIteration
            for half in range(2):
                ps = self.pq()
                for j in range(4):
                    kc = half * 4 + j
                    self.tr(ps[:, j * 128:j * 128 + n], xn[0:n, kc * 128:(kc + 1) * 128])
                if self.stop == 'n4':
                    raise StopIteration
                for j in range(4):
                    kc = half * 4 + j
                    o = self.HT[:, kc, i * 128:i * 128 + n]
                    if half == 0:
                        self.ts('dve', o, ps[:, j * 128:j * 128 + n], self.GS[:, mod, kc:kc + 1], self.SH[:, mod, kc:kc + 1], ALU.mult, ALU.add)
                    else:
                        self.act(o, ps[:, j * 128:j * 128 + n], AF.Identity, bias=self.SH[:, mod, kc:kc + 1], scale=self.GS[:, mod, kc:kc + 1])
        if self.stop in ('norm', 'n5a', 'n5d'):
            raise StopIteration
        self.stage_mlstm(l, t0, co, dirs, prompt, seq_idx, first, last)
        if self.stop == 'mlstm':
            raise StopIteration
        self.stage_lru(l, t0, co, W, w0, w1, T, dirs, prompt, seq_idx, first, last)
        if self.stop == 'lru':
            raise StopIteration
        self.stage_rwkv(l, t0, co, W, w0, w1, T, dirs, grid, prompt, seq_idx, first, last)
        for kc in range(8):
            self.dump('mix%d' % kc, self.MIXT[:, kc, :])
        if self.stop == 'rwkv':
            raise StopIteration
        if do_f:
            self.stage_out(l, mod, xsrc, xdst, row0 + t0)
        if self.stop == 'out':
            raise StopIteration

    def stage_mlstm(self, l, t0, co, dirs, prompt, seq_idx, first, last):
        S = self.S
        do_f = 0 in dirs
        for hp in range(3):
            self.proj_fm(hp * 128, 128, co, 256, lambda ps, hp=hp: self.cp('act', self.QT[:, hp, :], ps))
            self.proj_fm(384 + hp * 128, 128, co, 256, lambda ps, hp=hp: self.act(self.KT[:, hp, :], ps, AF.Copy, scale=0.125))
        if do_f:
            for hp in range(3):
                def ev_o(ps, hp=hp):
                    self.act(self.GOZ[:, hp, :], ps, AF.Sigmoid)
                self.proj_fm(1152 + hp * 128, 128, co, 256, ev_o)
                def ev_z2(ps, hp=hp):
                    tz = self.ZT[hp % 2]
                    self.act(tz[:, :], ps, AF.Silu)
                    self.tt('dve', self.GOZ[:, hp, :], self.GOZ[:, hp, :], tz[:, :], ALU.mult)
                self.proj_fm(1536 + hp * 128, 128, co, 256, ev_z2)
        for c in range(4):
            self.proj_tm(384, 384, co + c * 64, lambda ps, c=c: self.act(self.KTOK[:, c, :], ps, AF.Copy, scale=0.125))
            def ev_v(ps, c=c):
                self.cp('dve', self.VAUG[:, c, :, 0:64], self.V(ps.tensor, 0, 64, 0, [[64, 6], [1, 64]]))
            self.proj_tm(768, 384, co + c * 64, ev_v)
        self.memset('dve', self.VAUG[:, :, :, 64:65], 1.0)
        for d in dirs:
            g = self.mg[d]
            self.proj_fm(1920 + d * 6, 6, co, 256, lambda ps, g=g, d=d: self.act(g["gi"][:, :], ps, AF.Identity, bias=self.ppc(l, 'M_BI', d, 6)))
            def ev_f(ps, g=g, d=d):
                self.act(g["lf"][:, :], ps, AF.Exp, bias=self.dvc(l, 'NBF', d, 6), scale=-1.0)
                self.act(g["lf"][:, :], g["lf"][:, :], AF.Ln, bias=1.0)
                self.ts('dve', g["lf"][:, :], g["lf"][:, :], -1.0, None, ALU.mult)
            self.proj_fm(1932 + d * 6, 6, co, 256, ev_f)
        for d in dirs:
            if first[d]:
                if prompt:
                    self.memset('dve', self.mC[d][:, :, :], 0.0)
                    self.memset('dve', self.mM[d][:, :], 0.0)
                else:
                    S.dma('sp', self.mC[d][:, :, :], self.st_mC[l, d])
                    S.dma('sp', self.mM[d][:, :], self.st_mm[:, l, d:d + 1], allow_slow_non_contiguous=True)
        if do_f and not (1 in dirs):
            S.dma('sp', self.HB[:, :, :], self.sHB[t0:t0 + 256, :].rearrange("(c s) f -> s c f", s=64))
        for d in dirs:
            g = self.mg[d]
            v3 = lambda t: self.V(t, 0, 6, 0, [[64, 4], [1, 64]])
            self.scan(g["pre"][:, :], self.CST[0:6, CC['RMASK']:CC['RMASK'] + 256], g["lf"][:, :], 0.0, ALU.mult, ALU.add)
            bL = self.V(g["pre"], 0, 6, 63, [[64, 4]])
            bLb = self.V(g["pre"], 0, 6, 63, [[64, 4], [0, 64]])
            if d == 0:
                bsrc = g["pre"]
            else:
                self.tt('dve', v3(g["bb"]), bLb, v3(g["pre"]), ALU.subtract)
                self.tt('dve', g["bb"][:, :], g["bb"][:, :], g["lf"][:, :], ALU.add)
                bsrc = g["bb"]
            self.tt('dve', g["gg"][:, :], g["gi"][:, :], bsrc[:, :], ALU.subtract)
            self.reduce(g["mx"][:, :], v3(g["gg"]), ALU.max)
            if d == 0:
                mo, mxv, blv = g["mch"][:, :], g["mx"][:, :], bL
            else:
                mo = self.V(g["mch"], 0, 6, 3, [[-1, 4]])
                mxv = self.V(g["mx"], 0, 6, 3, [[-1, 4]])
                blv = self.V(g["pre"], 0, 6, 63 + 3 * 64, [[-64, 4]])
            self.scan(mo, mxv, blv, self.mM[d][:, 0:1], ALU.max, ALU.add)
            if d == 0:
                self.cp('dve', g["mprev"][:, 1:4], g["mch"][:, 0:3])
                self.cp('dve', g["mprev"][:, 0:1], self.mM[d][:, 0:1])
                mfin = g["mch"][:, 3:4]
            else:
                self.cp('dve', g["mprev"][:, 0:3], g["mch"][:, 1:4])
                self.cp('dve', g["mprev"][:, 3:4], self.mM[d][:, 0:1])
                mfin = g["mch"][:, 0:1]
            self.tt('dve', g["MM"][:, :], g["mprev"][:, :], g["mx"][:, :], ALU.max)
            self.tt('dve', g["dec"][:, :], g["mprev"][:, :], g["MM"][:, :], ALU.subtract)
            self.act(g["dec"][:, :], g["dec"][:, :], AF.Exp)
            self.cp('dve', self.mM[d][:, 0:1], mfin)
            MMb = self.V(g["MM"], 0, 6, 0, [[1, 4], [0, 64]])
            self.tt('dve', v3(g["ee"]), v3(g["gg"]), MMb, ALU.subtract)
            self.act(g["ee"][:, :], g["ee"][:, :], AF.Exp)
            self.tt('dve', v3(g["fl"]), v3(bsrc), MMb, ALU.add)
            self.act(g["fl"][:, :], g["fl"][:, :], AF.Exp, scale=-1.0)
            ps = self.pq()
            for c in range(4):
                self.tr(ps[0:64, c * 6:c * 6 + 6], g["ee"][:, c * 64:(c + 1) * 64])
                self.tr(ps[0:64, 24 + c * 6:24 + c * 6 + 6], g["fl"][:, c * 64:(c + 1) * 64])
            self.cp('dve', g["etok"][:, :, :], self.V(ps, 0, 64, 0, [[6, 4], [1, 6]]))
            self.cp('dve', g["fltok"][:, :, :], self.V(ps, 0, 64, 24, [[6, 4], [1, 6]]))
            self.tt('dve', g["X2"][:, :, :], self.V(g["dec"], 0, 6, 0, [[1, 4], [0, 3]]),
                    self.V(self.CST, 0, 6, CC['PSEL'], [[0, 4], [1, 3]]), ALU.mult)
            ps2 = self.pq()
            self.mm(ps2[:, 0:12], self.CST[0:6, CC['LSEL']:CC['LSEL'] + 128], self.V(g["X2"], 0, 6, 0, [[1, 12]]))
            self.cp('dve', g["decb"][:, :, :], self.V(ps2, 0, 128, 0, [[3, 4], [1, 3]]))
        for d in sorted(dirs, reverse=True):
            g = self.mg[d]
            mask = self.CST[0:64, CC['MUI']:CC['MUI'] + 64] if d == 0 else self.CST[0:64, CC['MLI']:CC['MLI'] + 64]
            maskb = self.V(self.CST, 0, 64, CC['MUI'] if d == 0 else CC['MLI'], [[0, 6], [1, 64]])
            for j in range(4):
                c = j if d == 0 else 3 - j
                cs = slice(c * 64, (c + 1) * 64)
                stsb, vp = self.STSB[j % 2], self.VP[j % 2]
                ps = self.pq()
                for h in (0, 2, 4, 1, 3, 5):
                    hp, pb = h // 2, 64 * (h % 2)
                    self.mm(ps[0:64, h * 64:(h + 1) * 64], self.KT[pb:pb + 64, hp, cs], self.QT[pb:pb + 64, hp, cs], inc=(h == 5))
                self.tt('dve', stsb[:, :, :], self.V(ps, 0, 64, 0, [[64, 6], [1, 64]]), maskb, ALU.mult)
                self.tt('dve', vp[:, :, :], self.VAUG[:, c, :, :], self.V(g["etok"], 0, 64, c * 6, [[1, 6], [0, 65]]), ALU.mult)
                self.tt('dve', self.CDEC[:, :, :], self.mC[d][:, :, :], self.V(g["decb"], 0, 128, c * 3, [[1, 3], [0, 65]]), ALU.mult)
                self.cp('act', self.CDBF[:, :, :], self.CDEC[:, :, :])
                ph = self.pq()
                for h in (0, 2, 4, 1, 3, 5):
                    hp, pb = h // 2, 64 * (h % 2)
                    o = ph[0:64, h * 65:(h + 1) * 65]
                    self.mm(o, stsb[:, h, :], vp[:, h, :], start=True, stop=False)
                    self.mm(o, self.QT[pb:pb + 64, hp, cs], self.CDBF[pb:pb + 64, hp, :], start=False, stop=True, inc=(h == 5))
                pc = self.pq()
                for h in range(6):
                    hp, pb = h // 2, 64 * (h % 2)
                    self.mm(pc[pb:pb + 64, hp * 65:(hp + 1) * 65], self.KTOK[:, c, h * 64:(h + 1) * 64], vp[:, h, :], inc=(h == 5))
                self.tt('dve', self.mC[d][:, :, :], self.CDEC[:, :, :], self.V(pc, 0, 128, 0, [[65, 3], [1, 65]]), ALU.add)
                self.act(self.DN[:, :], self.V(ph, 0, 64, 64, [[65, 6]]), AF.Abs)
                self.tt('dve', self.DN[:, :], self.DN[:, :], g["fltok"][:, c, :], ALU.max)
                self.recip(self.RDN[:, :], self.DN[:, :])
                hsrc = self.V(ph, 0, 64, 0, [[65, 6], [1, 64]])
                rb = self.V(self.RDN, 0, 64, 0, [[1, 6], [0, 64]])
                hbv = self.V(self.HB, 0, 64, c * 384, [[64, 6], [1, 64]])
                if d == 1:
                    self.tt('dve', hbv, hsrc, rb, ALU.mult)
                else:
                    self.tt('dve', self.HD[:, :, :], hsrc, rb, ALU.mult)
                    self.tt('dve', hbv, hbv, self.HD[:, :, :], ALU.add)
            if last[d] and prompt:
                S.dma('sp', self.o_mC[seq_idx, l, d], self.mC[d][:, :, :])
                S.dma('sp', self.o_mm[seq_idx, l, d], self.mM[d][:, 0:1])
        if not do_f:
            S.dma('sp', self.sHB[t0:t0 + 256, :].rearrange("(c s) f -> s c f", s=64), self.HB[:, :, :])
            return
        self.tt('dve', self.SQ[:, :, :], self.HB[:, :, :], self.HB[:, :, :], ALU.mult)
        self.reduce(self.SSQ[:, :], self.V(self.SQ, 0, 64, 0, [[64, 24], [1, 64]]), ALU.add)
        self.ts('dve', self.SSQ[:, :], self.SSQ[:, :], 1.0 / 64, EPS, ALU.mult, ALU.add)
        self.act(self.SSQ[:, :], self.SSQ[:, :], AF.Sqrt)
        self.recip(self.RSTD[:, :], self.SSQ[:, :])
        self.tt('dve', self.V(self.SQ, 0, 64, 0, [[64, 24], [1, 64]]), self.V(self.HB, 0, 64, 0, [[64, 24], [1, 64]]),
                self.V(self.RSTD, 0, 64, 0, [[1, 24], [0, 64]]), ALU.mult)
        for hp in range(3):
            ps = self.pq()
            for c in range(4):
                self.tr(ps[:, c * 64:(c + 1) * 64], self.SQ[:, c, hp * 128:(hp + 1) * 128], inc=(c == 3))
            self.stt(self.MIXT[:, hp, :], ps[:, 0:256], self.ppc(l, 'M_NORM', hp), self.GOZ[:, hp, :], ALU.mult, ALU.mult)

    def stage_lru(self, l, t0, co, W, w0, w1, T, dirs, prompt, seq_idx, first, last):
        S = self.S
        do_f = 0 in dirs
        for pr in range(2):
            self.proj_fm(3736 + pr * 128, 128, 0, W, lambda ps, pr=pr: self.cp('act', self.XL[:, pr, 2:2 + W], ps))
            if do_f:
                self.proj_fm(3992 + pr * 128, 128, co, 256, lambda ps, pr=pr: self.act(self.LZT[:, pr, :], ps, AF.Silu))
        if w1 == T:
            self.memset('dve', self.XL[:, :, 2 + W:2 + W + 1], 0.0)
        if w0 == 0:
            self.memset('dve', self.XL[:, :, 0:2], 0.0)
        for pr in range(2):
            self.ts('dve', self.XC[:, pr, :], self.XL[:, pr, co:co + 256], self.ppc(l, 'L_CONV', 0 * 2 + pr), self.ppc(l, 'L_CONVB', pr), ALU.mult, ALU.add)
            for j in range(1, 4):
                self.stt(self.XC[:, pr, :], self.XL[:, pr, co + j:co + j + 256], self.ppc(l, 'L_CONV', j * 2 + pr), self.XC[:, pr, :], ALU.mult, ALU.add)
        for d in dirs:
            if first[d]:
                if prompt:
                    self.memset('dve', self.lS[d][:, :], 0.0)
                else:
                    S.dma('sp', self.lS[d][:, :], self.st_l[:, l, d, :])
        if do_f and not (1 in dirs):
            S.dma('sp', self.HL[1][:, :, :], self.sLB[:, :, t0:t0 + 256])
        lt = self.lt
        for d in sorted(dirs, reverse=True):
            for pr in range(2):
                ps = self.pq()
                self.mm(ps[:, 0:256], self.LRUW[:, l, (0 * 2 + d) * 2 + pr, :], self.XC[:, pr, :])
                self.mm(ps[:, 256:512], self.LRUW[:, l, (1 * 2 + d) * 2 + pr, :], self.XC[:, pr, :])
                self.act(lt["rg"][:, pr, :], ps[:, 0:256], AF.Sigmoid, bias=self.ppc(l, 'L_BA', d * 2 + pr))
                self.act(lt["ig"][:, pr, :], ps[:, 256:512], AF.Sigmoid, bias=self.ppc(l, 'L_BX', d * 2 + pr))
                self.act(lt["aa"][:, pr, :], lt["rg"][:, pr, :], AF.Exp, scale=self.dvc(l, 'CLAM', d * 2 + pr))
                self.act(lt["a2"][:, pr, :], lt["rg"][:, pr, :], AF.Exp, scale=self.dvc(l, 'C2LAM', d * 2 + pr))
                self.ts('dve', lt["a2"][:, pr, :], lt["a2"][:, pr, :], -1.0, 1.0, ALU.mult, ALU.add)
                self.act(lt["a2"][:, pr, :], lt["a2"][:, pr, :], AF.Sqrt)
                self.tt('dve', lt["bt"][:, pr, :], lt["a2"][:, pr, :], lt["ig"][:, pr, :], ALU.mult)
                self.tt('dve', lt["bt"][:, pr, :], lt["bt"][:, pr, :], self.XC[:, pr, :], ALU.mult)
                if d == 0:
                    self.scan(self.HL[0][:, pr, :], lt["aa"][:, pr, :], lt["bt"][:, pr, :], self.lS[0][:, pr:pr + 1], ALU.mult, ALU.add)
                    self.cp('dve', self.lS[0][:, pr:pr + 1], self.HL[0][:, pr, 255:256])
                else:
                    rv = lambda t: self.V(t, 0, 128, pr * 256 + 255, [[-1, 256]])
                    self.scan(rv(self.HL[1]), rv(lt["aa"]), rv(lt["bt"]), self.lS[1][:, pr:pr + 1], ALU.mult, ALU.add)
                    self.cp('dve', self.lS[1][:, pr:pr + 1], self.HL[1][:, pr, 0:1])
            if last[d] and prompt:
                S.dma('sp', self.o_l[seq_idx, l, d], self.lS[d][:, :])
        if not do_f:
            S.dma('sp', self.sLB[:, :, t0:t0 + 256], self.HL[1][:, :, :])
            return
        self.tt('dve', self.HL[0][:, :, :], self.HL[0][:, :, :], self.HL[1][:, :, :], ALU.add)
        self.tt('dve', self.MIXT[:, 6:8, :], self.HL[0][:, :, :], self.LZT[:, :, :], ALU.mult)

    def stage_rwkv(self, l, t0, co, W, w0, w1, T, dirs, grid, prompt, seq_idx, first, last):
        S = self.S
        do_f = 0 in dirs
        for ch in range(11):
            self.proj_fm(1944 + ch * 128, 128, 0, W, lambda ps, ch=ch: self.cp('act' if ch % 2 else 'dve', self.URS[:, ch, 0:W], ps))
        if do_f:
            for hp in range(3):
                self.proj_fm(3352 + hp * 128, 128, co, 256, lambda ps, hp=hp: self.act(self.RZT[:, hp, :], ps, AF.Silu))
        U3 = lambda off, n: self.V(self.URS, 0, 128, off, [[384, 11], [1, n]])
        B3 = lambda off, n: self.V(self.BLK, 0, 128, off, [[256, 11], [1, n]])
        if not grid:
            self.cp('dve', B3(1, 255), U3(0, 255))
            self.memset('dve', B3(0, 1), 0.0)
            self.tt('dve', B3(0, 255), B3(0, 255), U3(1, 255), ALU.add)
            wsh = 0.5
        else:
            U4 = lambda off, r, n: self.V(self.URS, 0, 128, off, [[384, 11], [64, r], [1, n]])
            B4 = lambda off, r, n: self.V(self.BLK, 0, 128, off, [[256, 11], [64, r], [1, n]])
            self.cp('dve', B4(1, 4, 63), U4(co, 4, 63))
            self.memset('dve', B4(0, 4, 1), 0.0)
            self.tt('dve', B4(0, 4, 63), B4(0, 4, 63), U4(co + 1, 4, 63), ALU.add)
            if t0 > 0:
                self.tt('dve', B3(0, 256), B3(0, 256), U3(co - 64, 256), ALU.add)
            else:
                self.tt('dve', B3(64, 192), B3(64, 192), U3(0, 192), ALU.add)
            if t0 + TB < T:
                self.tt('dve', B3(0, 256), B3(0, 256), U3(co + 64, 256), ALU.add)
            else:
                self.tt('dve', B3(0, 192), B3(0, 192), U3(co + 64, 192), ALU.add)
            wsh = 0.25
        mu = self.V(self.PP, 0, 128, l * PL + PC['R_MU'], [[1, 11], [0, 256]])
        self.tt('dve', B3(0, 256), B3(0, 256), mu, ALU.mult)
        omm = self.V(self.DV, 0, 128, l * DL + DC['OMMU'], [[1, 11], [0, 256]])
        self.tt('dve', U3(co, 256), U3(co, 256), omm, ALU.mult)
        self.stt(B3(0, 256), B3(0, 256), wsh, U3(co, 256), ALU.mult, ALU.add)
        for nm, ch in (('blk_r', 0), ('blk_k', 3), ('blk_v', 6), ('blk_wl', 9), ('blk_al', 10)):
            self.dump(nm, self.BLK[:, ch, :])
        rt = self.rt
        kk = self.V(self.PP, 0, 128, l * PL + PC['R_KK'], [[1, 3], [0, 256]])
        kap = rt["d"]
        self.tt('dve', kap[:, :, :], self.BLK[:, 3:6, :], kk, ALU.mult)
        ksq = rt["E"]
        self.tt('dve', ksq[:, :, :], kap[:, :, :], kap[:, :, :], ALU.mult)
        for hp in range(3):
            self.mm(self.PA[:, hp * 256:(hp + 1) * 256], self.CST[:, CC['BONES']:CC['BONES'] + 128], ksq[:, hp, :])
        self.act(ksq[:, :, :], self.V(self.PA, 0, 128, 0, [[256, 3], [1, 256]]), AF.Sqrt)
        self.ts('dve', ksq[:, :, :], ksq[:, :, :], 1e-12, None, ALU.max)
        self.recip(ksq[:, :, :], ksq[:, :, :])
        self.tt('dve', self.KH[:, :, :], kap[:, :, :], ksq[:, :, :], ALU.mult)
        self.dump('kh', self.KH[:, 0, :])
        self.bd_fill(self.BDV, lambda par: self.V(self.BLK, par * 64, 64, 6 * 256, [[256, 3], [64, 4], [1, 64]]))
        for c in range(4):
            for hp in range(3):
                self.mm(self.PB[:, (c * 3 + hp) * 64:(c * 3 + hp + 1) * 64], self.BDV[:, hp, c, :], self.ISTKB[:, :])
        self.cp('act', self.V(self.VSTK, 0, 128, 0, [[1, 768]]), self.PB[:, 0:768])
        for d in dirs:
            if first[d]:
                if prompt:
                    self.memset('dve', self.rH[d][:, :, :], 0.0)
                else:
                    S.dma('sp', self.rH[d][:, :, :], self.st_rH[l, d])
        ybflat = self.V(self.YB, 0, 128, 0, [[1, 768]])
        if do_f and not (1 in dirs):
            S.dma('sp', ybflat, self.sYB[t0 // 256])
        for d in sorted(dirs, reverse=True):
            self.rwkv_dir(l, d, seq_idx, prompt, last)
        if not do_f:
            S.dma('sp', self.sYB[t0 // 256], ybflat)
            return
        ft = self.ft
        self.tt('dve', self.YSQ[:, :, :, :], self.YB[:, :, :, :], self.YB[:, :, :, :], ALU.mult)
        self.reduce(self.rSSQ[:, :], self.V(self.YSQ, 0, 128, 0, [[64, 12], [1, 64]]), ALU.add)
        self.ts('dve', self.rSSQ[:, :], self.rSSQ[:, :], 1.0 / 64, EPS, ALU.mult, ALU.add)
        self.act(self.rSSQ[:, :], self.rSSQ[:, :], AF.Sqrt)
        self.recip(self.rRSTD[:, :], self.rSSQ[:, :])
        self.tt('dve', ft["rk"][:, :, :], self.BLK[:, 0:3, :], self.BLK[:, 3:6, :], ALU.mult)
        for hp in range(3):
            self.mm(self.PB[:, hp * 256:(hp + 1) * 256], self.RKD[:, l, hp, :], ft["rk"][:, hp, :])
        self.tt('dve', ft["bon"][:, :, :], self.V(self.PB, 0, 128, 0, [[256, 3], [1, 256]]), self.BLK[:, 6:9, :], ALU.mult)
        self.memset('pool', self.YBD[:, :, :], 0.0)
        for c in range(4):
            for par in range(2):
                self.tt('dve', self.V(self.YBD, par * 64, 64, par * 64, [[128, 3], [1, 64]]),
                        self.V(self.YB, par * 64, 64, c * 192, [[64, 3], [1, 64]]),
                        self.V(self.rRSTD, par * 64, 64, c * 3, [[1, 3], [0, 64]]), ALU.mult)
            for hp in range(3):
                self.mm(self.PA[:, hp * 256 + c * 64: hp * 256 + (c + 1) * 64], self.YBD[:, hp, :],
                        self.CST[:, CC['ISTK']:CC['ISTK'] + 64])
        for hp in range(3):
            self.stt(ft["t1"][:, hp, :], self.PA[:, hp * 256:(hp + 1) * 256], self.ppc(l, 'R_NORM', hp), ft["bon"][:, hp, :], ALU.mult, ALU.add)
            self.tt('dve', self.MIXT[:, 3 + hp, :], ft["t1"][:, hp, :], self.RZT[:, hp, :], ALU.mult)

    def bd_fill(self, bd, src_of_par, eng='pool'):
        self.memset(eng, bd[:, :, :, :], 0.0)
        for par in range(2):
            self.cp(eng, self.V(bd, par * 64, 64, par * 64, [[512, 3], [128, 4], [1, 64]]), src_of_par(par))

    def rwkv_dir(self, l, d, seq_idx, prompt, last):
        S = self.S
        rt = self.rt
        pb_d = 64 * d
        f3 = lambda t: t[:, :, :]
        v4 = lambda t: self.V(t, 0, 128, 0, [[256, 3], [64, 4], [1, 64]])
        self.act(self.TWL[pb_d:pb_d + 64, :], self.BLK[pb_d:pb_d + 64, 9, :], AF.Tanh)
        for hp in range(3):
            self.mm(self.PA[:, hp * 256:(hp + 1) * 256], self.LORA[pb_d:pb_d + 64, l, 0, hp * 128:(hp + 1) * 128], self.TWL[pb_d:pb_d + 64, :])
        for hp in range(3):
            self.act(rt["sg"][:, hp, :], self.PA[:, hp * 256:(hp + 1) * 256], AF.Sigmoid, bias=self.ppc(l, 'R_W0', d * 3 + hp))
        for hp in range(3):
            self.mm(self.PB[:, hp * 256:(hp + 1) * 256], self.LORA[pb_d:pb_d + 64, l, 1, hp * 128:(hp + 1) * 128], self.BLK[pb_d:pb_d + 64, 10, :])
        for hp in range(3):
            self.act(rt["aa"][:, hp, :], self.PB[:, hp * 256:(hp + 1) * 256], AF.Sigmoid, bias=self.ppc(l, 'R_A0', d * 3 + hp))
        for hp in range(3):
            self.ts('dve', rt["kt"][:, hp, :], rt["aa"][:, hp, :], self.ppc(l, 'R_KA', hp), self.dvc(l, 'OMKA', hp), ALU.mult, ALU.add)
        self.tt('dve', f3(rt["kt"]), f3(rt["kt"]), self.BLK[:, 3:6, :], ALU.mult)
        self.tt('dve', f3(rt["bb"]), self.KH[:, :, :], f3(rt["aa"]), ALU.mult)
        flat = lambda t: self.V(t, 0, 128, 0, [[1, 768]])
        self.scan(flat(rt["cs"]), self.CST[:, CC['RMASK']:CC['RMASK'] + 768], flat(rt["sg"]), 0.0, ALU.mult, ALU.add)
        self.cp('dve', self.GL[:, :, :], self.V(rt["cs"], 0, 128, 63, [[256, 3], [64, 4]]))
        if d == 1:
            csLb = self.V(rt["cs"], 0, 128, 63, [[256, 3], [64, 4], [0, 64]])
            self.tt('dve', v4(rt["d"]), csLb, v4(rt["cs"]), ALU.subtract)
            self.tt('dve', f3(rt["cs"]), f3(rt["d"]), f3(rt["sg"]), ALU.add)
        self.act(f3(rt["E"]), f3(rt["cs"]), AF.Exp, scale=-DSC)
        self.tt('dve', self.V(self.KR, 0, 128, 64, [[512, 3], [128, 4], [1, 64]]),
                self.V(self.BLK, 0, 128, 0, [[256, 3], [64, 4], [1, 64]]), v4(rt["E"]), ALU.mult)
        self.tt('dve', f3(rt["d"]), f3(rt["cs"]), f3(rt["sg"]), ALU.subtract)
        self.act(f3(rt["E"]), f3(rt["d"]), AF.Exp, scale=-DSC)
        self.tt('dve', self.V(self.KR, 0, 128, 0, [[512, 3], [128, 4], [1, 64]]), v4(self.KH), v4(rt["E"]), ALU.mult)
        self.act(f3(rt["E"]), f3(rt["cs"]), AF.Exp, scale=DSC)
        self.tt('dve', self.BTT[:, :, :], f3(rt["bb"]), f3(rt["E"]), ALU.mult)
        self.tt('dve', self.KTT[:, :, :], f3(rt["kt"]), f3(rt["E"]), ALU.mult)
        self.act(self.GL[:, :, :], self.GL[:, :, :], AF.Exp, scale=-DSC)
        BD = self.BD
        self.memset('pool', self.A1BD[:, :, :, :], 0.0)
        self.memset('pool', self.A2BD[:, :, :, :], 0.0)
        self.memset('pool', self.NN[0][:, :, :], 0.0)
        c4 = lambda t, par: self.V(t, par * 64, 64, 0, [[256, 3], [64, 4], [1, 64]])
        self.bd_fill(BD["kt"], lambda par: c4(self.KTT, par))
        self.bd_fill(BD["b"], lambda par: c4(self.BTT, par))
        self.bd_fill(BD["kh"], lambda par: self.V(self.KR, par * 64, 64, 0, [[512, 3], [128, 4], [1, 64]]))
        self.bd_fill(BD["r"], lambda par: self.V(self.KR, par * 64, 64, 64, [[512, 3], [128, 4], [1, 64]]))
        for (src, dst, neg) in ((BD["kt"], self.KTTOK, False), (BD["b"], self.BTTOK, True)):
            for half in range(2):
                for cc_ in range(2):
                    c = half * 2 + cc_
                    for hp in range(3):
                        self.tr(self.PTB[:, (cc_ * 3 + hp) * 128:(cc_ * 3 + hp + 1) * 128], src[:, hp, c, :], bf=True)
                o = self.V(dst, 0, 128, half * 768, [[1, 768]])
                if neg:
                    self.act(o, self.PTB[:, 0:768], AF.Copy, scale=-1.0)
                else:
                    self.cp('dve', o, self.PTB[:, 0:768])
        self.cp('act', self.HBF[:, :, :], self.rH[d][:, :, :])
        mk = CC['MKF'] if d == 0 else CC['MKB']
        nmk = CC['NMKF'] if d == 0 else CC['NMKB']
        mn = CC['MNF'] if d == 0 else CC['MNB']
        for j in range(4):
            c = j if d == 0 else 3 - j
            cs = slice(c * 64, (c + 1) * 64)
            KRc = lambda hp: self.V(self.KR, 0, 128, hp * 512 + c * 128, [[1, 128]])
            p1, p2, p3 = self.pq(), self.pq(), self.pq()
            for hp in range(3):
                self.mm(p1[:, hp * 128:(hp + 1) * 128], BD["kt"][:, hp, c, :], KRc(hp), inc=(hp == 2))
            for hp in range(3):
                self.mm(p2[:, hp * 128:(hp + 1) * 128], BD["b"][:, hp, c, :], KRc(hp), inc=(hp == 2))
            for hp in range(3):
                self.mm(p3[:, hp * 64:(hp + 1) * 64], BD["kh"][:, hp, c, :], self.BTT[:, hp, cs], inc=(hp == 2))
            for par in range(2):
                pp_ = par * 64
                self.tt('dve', self.V(self.A1BD, pp_, 64, pp_, [[256, 3], [128, 2], [1, 64]]),
                        self.V(p1, pp_, 64, 0, [[128, 3], [64, 2], [1, 64]]),
                        self.V(self.CST, pp_, 64, mk, [[0, 3], [64, 2], [1, 64]]), ALU.mult)
                self.tt('dve', self.V(self.A2BD, pp_, 64, pp_, [[256, 3], [128, 2], [1, 64]]),
                        self.V(p2, pp_, 64, 0, [[128, 3], [64, 2], [1, 64]]),
                        self.V(self.CST, pp_, 64, nmk, [[0, 3], [64, 2], [1, 64]]), ALU.mult)
                self.tt('dve', self.V(self.NN[0], pp_, 64, pp_, [[128, 3], [1, 64]]),
                        self.V(p3, pp_, 64, 0, [[64, 3], [1, 64]]),
                        self.V(self.CST, pp_, 64, mn, [[0, 3], [1, 64]]), ALU.mult)
            pr_ = self.pq()
            for hp in range(3):
                o = pr_[:, hp * 64:(hp + 1) * 64]
                self.mm(o, BD["kh"][:, hp, c, :], self.HBF[:, hp, :], start=True, stop=False)
                self.mm(o, self.A1BD[:, hp, 0, :], self.VSTK[:, c, hp, :], start=False, stop=True, inc=(hp == 2))
            u192 = self.V(self.U, 0, 128, 0, [[1, 192]])
            ub192 = self.V(self.UBF, 0, 128, 0, [[1, 192]])
            self.cp('dve', u192, pr_[:, 0:192])
            self.cp('act', ub192, u192)
            for k in range(6):
                NTk = self.A2BD[:, :, 0, :] if k == 0 else self.NTT[k % 2][:, :, :]
                Nk = self.NN[k % 2]
                pu = self.pq()
                for hp in range(3):
                    self.mm(pu[:, hp * 64:(hp + 1) * 64], NTk[:, hp, :], self.UBF[:, hp, :], inc=(hp == 2))
                if k < 5:
                    pnt = self.pq()
                    for hp in range(3):
                        self.mm(pnt[:, hp * 128:(hp + 1) * 128], Nk[:, hp, :], NTk[:, hp, :], inc=(hp == 2))
                    pn = self.pq()
                    for hp in range(3):
                        self.mm(pn[:, hp * 128:(hp + 1) * 128], NTk[:, hp, :], Nk[:, hp, :], inc=(hp == 2))
                self.tt('dve', ub192, u192, pu[:, 0:192], ALU.add)
                if k < 5:
                    self.cp('act', self.V(self.NTT[(k + 1) % 2], 0, 128, 0, [[1, 384]]), pnt[:, 0:384])
                    self.cp('act', self.V(self.NN[(k + 1) % 2], 0, 128, 0, [[1, 384]]), pn[:, 0:384])
                    self.tt('dve', u192, u192, pu[:, 0:192], ALU.add)
            py = self.pq()
            for hp in range(3):
                o = py[:, hp * 64:(hp + 1) * 64]
                self.mm(o, BD["r"][:, hp, c, :], self.HBF[:, hp, :], start=True, stop=False)
                self.mm(o, self.A1BD[:, hp, 1, :], self.VSTK[:, c, hp, :], start=False, stop=False)
                self.mm(o, self.A2BD[:, hp, 1, :], self.UBF[:, hp, :], start=False, stop=True, inc=(hp == 2))
            ybv = self.V(self.YB, 0, 128, c * 192, [[1, 192]])
            if d == 1:
                self.cp('act', ybv, py[:, 0:192])
            else:
                self.tt('dve', ybv, ybv, py[:, 0:192], ALU.add)
            ph = self.pq()
            for hp in range(3):
                o = ph[:, hp * 64:(hp + 1) * 64]
                self.mm(o, self.KTTOK[:, c, hp, :], self.VSTK[:, c, hp, :], start=True, stop=False)
                self.mm(o, self.BTTOK[:, c, hp, :], self.UBF[:, hp, :], start=False, stop=True, inc=(hp == 2))
            self.tt('dve', self.HTMP[:, :, :], self.rH[d][:, :, :], self.V(ph, 0, 128, 0, [[64, 3], [1, 64]]), ALU.add)
            self.tt('dve', self.rH[d][:, :, :], self.HTMP[:, :, :], self.V(self.GL, 0, 128, c, [[4, 3], [0, 64]]), ALU.mult)
            self.cp('act', self.HBF[:, :, :], self.rH[d][:, :, :])
        if last[d] and prompt:
            S.dma('sp', self.o_rH[seq_idx, l, d], self.rH[d][:, :, :])

    def stage_out(self, l, mod, xsrc, xdst, r0):
        S = self.S
        for tt_ in range(2):
            o, xw = self.XN[tt_], self.XW[tt_]
            S.dma('sp', xw[:, :], xsrc[r0 + tt_ * 128: r0 + (tt_ + 1) * 128, :])
            for ch in range(2):
                ps = self.pq()
                for kc in range(8):
                    self.mm(ps[:, :], self.MIXT[:, kc, tt_ * 128:(tt_ + 1) * 128], self.W_OUT[:, kc, ch * 512:(ch + 1) * 512],
                            start=(kc == 0), stop=(kc == 7))
                self.cp('act' if ch else 'dve', o[:, ch * 512:(ch + 1) * 512], ps[:, :])
            self.dump('o_proj%d' % tt_, o[:, :])
            self.dump('o_x%d' % tt_, xw[:, :])
            ssq = self.TMPS[:, 24 + tt_:25 + tt_]
            junk = self.V(self.BLK, 0, 128, 0, [[1, D]])
            self.act(junk, o[:, :], AF.Square, accum=ssq)
            rs = self.TMPS[:, 26 + tt_:27 + tt_]
            self.ts('dve', rs, ssq, 1.0 / D, EPS, ALU.mult, ALU.add)
            self.act(rs, rs, AF.Sqrt)
            self.recip(rs, rs)
            self.dump('o_rs%d' % tt_, rs)
            self.stt(o[:, :], o[:, :], rs, self.GATEB[:, mod, :], ALU.mult, ALU.mult)
            self.dump('o_g%d' % tt_, o[:, :])
            self.tt('dve', o[:, :], o[:, :], xw[:, :], ALU.add)
            S.dma('sp', xdst[r0 + tt_ * 128: r0 + (tt_ + 1) * 128, :], o[:, :])

    def build(self, layers=(0, 1)):
        try:
            self._build(layers)
        except StopIteration:
            pass
        self.S.finish('sp')
        return self.nc

    def _build(self, layers):
        NP, TS = self.NP, self.TS
        self.setup()
        if self.stop == 'setup':
            raise StopIteration
        for li, l in enumerate(layers):
            self.load_layer(l)
            if self.stop == 'load':
                raise StopIteration
            xsrc = self.x_in if li == 0 else self.x1
            xdst = self.y_out if li == len(layers) - 1 else self.x1
            T_, F_ = {0: True, 1: True}, {0: False, 1: False}
            for s in range(NP):
                self.visit(l, 0, xsrc, xdst, s * 256, 256, 0, [0, 1], False, s, T_, T_)
            if TS > 0:
                nb = TS // TB
                row0 = NP * 256
                for b in range(nb - 1, -1, -1):
                    self.visit(l, 1, xsrc, xdst, row0, TS, b * TB, [1], True, 0,
                               {0: False, 1: b == nb - 1}, {0: False, 1: b == 0})
                for b in range(nb):
                    self.visit(l, 1, xsrc, xdst, row0, TS, b * TB, [0], True, 0,
                               {0: b == 0, 1: False}, {0: b == nb - 1, 1: False})


def prep_shared(inp):
    f = lambda a: np.ascontiguousarray(np.asarray(a, dtype=np.float32))
    b_mod = f(inp['b_mod'])
    sh = {}
    sh['w_mod'] = f(inp['w_mod'])
    sh['w_in'] = f(inp['w_in'])
    sh['w_out'] = f(inp['w_out'])
    sh['bmodT'] = f(b_mod[:, :2048].reshape(2, 16, 128).transpose(2, 0, 1))
    sh['bmodg'] = f(b_mod[:, 2048:3072])
    sh['gpost'] = f(inp['g_post'])
    pp = np.zeros((128, 2 * PL), np.float32)
    for l in range(2):
        o = l * PL
        def put(key, arr, n):
            pp[:, o + PC[key]: o + PC[key] + n] = np.asarray(arr, np.float32).reshape(n, 128).T
        put('G_PRE', inp['g_pre'][l], 8)
        put('M_NORM', inp['m_norm'][l], 3)
        put('R_MU', inp['r_mu'][l], 11)
        put('R_W0', np.asarray(inp['r_w0'][l]).reshape(-1), 6)
        put('R_A0', np.asarray(inp['r_a0'][l]).reshape(-1), 6)
        put('R_KK', inp['r_kk'][l], 3)
        put('R_KA', inp['r_ka'][l], 3)
        put('R_RK', inp['r_rk'][l], 3)
        put('R_NORM', inp['r_norm'][l], 3)
        put('L_CONV', np.asarray(inp['l_conv'][l]).reshape(-1), 8)
        put('L_CONVB', inp['l_conv_b'][l], 2)
        put('L_BA', np.asarray(inp['l_ba'][l]).reshape(-1), 4)
        put('L_BX', np.asarray(inp['l_bx'][l]).reshape(-1), 4)
        put('L_LAM', np.asarray(inp['l_lambda'][l]).reshape(-1), 4)
        pp[0:6, o + PC['M_BI']: o + PC['M_BI'] + 2] = np.asarray(inp['m_bi'][l], np.float32).T
        pp[0:6, o + PC['M_BF']: o + PC['M_BF'] + 2] = np.asarray(inp['m_bf'][l], np.float32).T
    sh['pp'] = pp
    sh['cst'] = make_consts()
    lora = np.zeros((128, 2, 2, 384), np.float32)
    for wi, key in enumerate(['r_w2', 'r_a2']):
        a = np.asarray(inp[key], np.float32)
        lora[:, :, wi, :] = a.transpose(1, 2, 0, 3).reshape(128, 2, 384)
    sh['lora'] = lora
    lruw = np.zeros((128, 2, 8, 128), np.float32)
    for gi, key in enumerate(['l_wa', 'l_wx']):
        a = np.asarray(inp[key], np.float32)
        for l in range(2):
            for d in range(2):
                for pr in range(2):
                    for hb in range(2):
                        n = 2 * pr + hb
                        lruw[hb * 64:(hb + 1) * 64, l, (gi * 2 + d) * 2 + pr, hb * 64:(hb + 1) * 64] = a[l, d, n]
    sh['lruw'] = lruw
    return sh


def prep_core(inp, b, NP, TS):
    f = lambda a: np.ascontiguousarray(np.asarray(a, dtype=np.float32))
    m = {}
    xp = np.asarray(inp['x_prompt'], np.float32)[b * NP:(b + 1) * NP].reshape(NP * 256, D)
    if TS > 0:
        xs = np.asarray(inp['x_sample'], np.float32)[b]
        m['x_in'] = f(np.concatenate([xp, xs], 0))
    else:
        m['x_in'] = f(xp)
    cc = np.stack([np.asarray(inp['c_ctx'], np.float32), np.asarray(inp['c'], np.float32)[b]], -1)
    m['cc'] = f(cc.reshape(8, 128, 2).transpose(1, 0, 2))
    C = np.asarray(inp['state_mlstm_C'], np.float32)[b]
    n = np.asarray(inp['state_mlstm_n'], np.float32)[b]
    Cn = np.concatenate([C, n[..., None]], -1)
    Cn = Cn.reshape(2, 2, 3, 2, 64, 65).transpose(0, 1, 3, 4, 2, 5).reshape(2, 2, 128, 3, 65)
    m['st_mC'] = f(Cn)
    m['st_mm'] = f(np.asarray(inp['state_mlstm_m'], np.float32)[b].transpose(2, 0, 1))
    R = np.asarray(inp['state_rwkv'], np.float32)[b]
    R = R.transpose(0, 1, 2, 4, 3)
    R = R.reshape(2, 2, 3, 2, 64, 64).transpose(0, 1, 3, 4, 2, 5).reshape(2, 2, 128, 3, 64)
    m['st_rH'] = f(R)
    L = np.asarray(inp['state_rglru'], np.float32)[b]
    m['st_l'] = f(L.reshape(2, 2, 2, 128).transpose(3, 0, 1, 2))
    return m


def unpack_core(r, NP, TS):
    y = r['y_out']
    yp = y[:NP * 256].reshape(NP, 256, D)
    ys = y[NP * 256:]
    mC = r['o_mC'].reshape(NP, 2, 2, 2, 64, 3, 65).transpose(0, 1, 2, 5, 3, 4, 6).reshape(NP, 2, 2, 6, 64, 65)
    newC = np.ascontiguousarray(mC[..., :64])
    newn = np.ascontiguousarray(mC[..., 64])
    newm = r['o_mm'].reshape(NP, 2, 2, 6)
    rH = r['o_rH'].reshape(NP, 2, 2, 2, 64, 3, 64).transpose(0, 1, 2, 5, 3, 4, 6).reshape(NP, 2, 2, 6, 64, 64)
    newr = np.ascontiguousarray(rH.transpose(0, 1, 2, 3, 5, 4))
    newl = np.ascontiguousarray(r['o_l'].transpose(0, 1, 2, 4, 3).reshape(NP, 2, 2, 256))
    return yp, ys, newC, newn, newm, newr, newl


_NC_CACHE = {}


def kernel(**inputs):
    NP, TS = 4, 2048
    key = (NP, TS)
    if key not in _NC_CACHE:
        _NC_CACHE[key] = Builder(NP, TS).build()
    nc = _NC_CACHE[key]
    sh = prep_shared(inputs)
    in_maps = []
    for b in range(NCORES):
        m = dict(sh)
        m.update(prep_core(inputs, b, NP, TS))
        in_maps.append(m)
    res = run_bass_kernel_spmd(nc, in_maps, core_ids=list(range(NCORES)))
    outs = [unpack_core(r, NP, TS) for r in res.results]
    y_prompt = np.concatenate([o[0] for o in outs], 0)
    y_sample = np.stack([o[1] for o in outs], 0)
    cat = lambda i: np.concatenate([o[i] for o in outs], 0)
    return (y_prompt.astype(np.float32), y_sample.astype(np.float32), cat(2).astype(np.float32),
            cat(3).astype(np.float32), cat(4).astype(np.float32), cat(5).astype(np.float32), cat(6).astype(np.float32))
```

```python
import numpy as np
import concourse.bass as bass
import concourse.mybir as mybir
from concourse.bass_utils import run_bass_kernel_spmd

F32 = mybir.dt.float32
BF16 = mybir.dt.bfloat16
AF = mybir.ActivationFunctionType
ALU = mybir.AluOpType
AX = mybir.AxisListType

D = 1024
IN_COLS = 4248
EPS = 1e-6
DSC = 0.6065306597126334
TB = 256
NCORES = 8


def _prod(xs):
    r = 1
    for x in xs:
        r *= int(x)
    return r


class Sync:
    def __init__(self, nc, n_dma_sems=32):
        self.nc = nc
        self.engs = {'pe': nc.tensor, 'dve': nc.vector, 'act': nc.scalar,
                     'pool': nc.gpsimd, 'sp': nc.sync}
        self.sem = {}
        self.cnt = {}
        for e in ['pe', 'dve', 'act', 'pool']:
            self.sem[e] = nc.alloc_semaphore('sem_' + e)
            self.cnt[e] = 0
        self.seen = {e: {} for e in self.engs}
        self.dma_ring = [nc.alloc_semaphore('dq_%d' % i) for i in range(n_dma_sems)]
        self.dma_uses = [0] * n_dma_sems
        self.dma_next = 0
        self.rec = {}
        self.untracked = set()
        self.n_wait = 0
        self.n_ins = 0
        self.pstep_cache = {}
        self.sb_addr = {}

    def region(self, ap):
        t = ap.tensor
        name = t.name
        apl = [(int(s), int(c)) for (s, c) in ap.ap]
        off = int(ap.offset)
        if type(t).__name__.startswith('DRam'):
            lo = off + sum(min(0, s * (c - 1)) for s, c in apl)
            hi = off + sum(max(0, s * (c - 1)) for s, c in apl) + 1
            return (name, 0, 1, lo, hi)
        pstep = self.pstep_cache.get(name)
        if pstep is None:
            pstep = _prod(list(t.shape)[1:])
            self.pstep_cache[name] = pstep
        p0 = off // pstep
        f0 = off % pstep
        npart = apl[0][1]
        rest = apl[1:]
        lo = f0 + sum(min(0, s * (c - 1)) for s, c in rest)
        hi = f0 + sum(max(0, s * (c - 1)) for s, c in rest) + 1
        if name in self.sb_addr:
            base, es = self.sb_addr[name]
            return ('SB', p0, p0 + npart, base + lo * es, base + hi * es)
        return ('PS:' + name, (p0 // 32) * 32, ((p0 + npart + 31) // 32) * 32, (lo // 512) * 512, ((hi + 511) // 512) * 512)

    @staticmethod
    def _ovl(a, b):
        return a[1] < b[2] and b[1] < a[2] and a[3] < b[4] and b[3] < a[4]

    @staticmethod
    def _contains(a, b):
        return a[1] <= b[1] and b[2] <= a[2] and a[3] <= b[3] and b[4] <= a[4]

    def _collect(self, e, reads, writes):
        deps = {}
        own = self.sem.get(e)
        rregs = [self.region(a) for a in reads]
        wregs = [self.region(a) for a in writes]
        for r in rregs:
            if r[0] in self.untracked:
                continue
            isps = r[0].startswith('PS:')
            for (reg, kind, sem, val) in self.rec.get(r[0], ()):
                if (kind == 'w' or (isps and sem is not own)) and self._ovl(reg, r):
                    if e == 'pe' and sem is own:
                        continue
                    k = id(sem)
                    if deps.get(k, (None, 0))[1] < val:
                        deps[k] = (sem, val)
        for w in wregs:
            if w[0] in self.untracked:
                continue
            for (reg, kind, sem, val) in self.rec.get(w[0], ()):
                if self._ovl(reg, w):
                    if sem is own:
                        continue
                    k = id(sem)
                    if deps.get(k, (None, 0))[1] < val:
                        deps[k] = (sem, val)
        return deps, rregs, wregs

    def _record(self, rregs, wregs, sem, val):
        for r in rregs:
            if r[0] in self.untracked:
                continue
            lst = self.rec.setdefault(r[0], [])
            lst[:] = [x for x in lst if not (x[1] == 'r' and x[2] is sem and self._contains(r, x[0]))]
            lst.append((r, 'r', sem, val))
        for w in wregs:
            if w[0] in self.untracked:
                continue
            lst = self.rec.setdefault(w[0], [])
            lst[:] = [x for x in lst if not self._contains(w, x[0])]
            lst.append((w, 'w', sem, val))

    def wait(self, e, sem, val):
        k = id(sem)
        if self.seen[e].get(k, 0) >= val:
            return
        self.engs[e].wait_ge(sem, val)
        self.seen[e][k] = val
        self.n_wait += 1

    max_ins = None
    paranoid = False

    def emit(self, e, reads, writes, build, inc=True):
        if self.max_ins is not None and self.n_ins >= self.max_ins:
            raise StopIteration
        deps, rregs, wregs = self._collect(e, reads, writes)
        for (sem, val) in deps.values():
            self.wait(e, sem, val)
        if self.paranoid:
            for e2 in ['pe', 'dve', 'act', 'pool']:
                if self.cnt[e2] > 0 and not (e == 'pe' and e2 == 'pe'):
                    self.wait(e, self.sem[e2], self.cnt[e2])
        ins = build(self.engs[e])
        self.n_ins += 1
        if inc:
            self.cnt[e] += 1
            ins.then_inc(self.sem[e], 1)
            val = self.cnt[e]
        else:
            val = self.cnt[e] + 1
        self._record(rregs, wregs, self.sem[e], val)
        return ins

    def dma(self, q, out, in_, **kw):
        if self.max_ins is not None and self.n_ins >= self.max_ins:
            raise StopIteration
        i = self.dma_next
        self.dma_next = (i + 1) % len(self.dma_ring)
        sem = self.dma_ring[i]
        uses = self.dma_uses[i]
        if uses > 0:
            self.wait(q, sem, 16 * uses)
        deps, rregs, wregs = self._collect(q, [in_], [out])
        for (s, v) in deps.values():
            self.wait(q, s, v)
        ins = self.engs[q].dma_start(out=out, in_=in_, **kw)
        ins.then_inc(sem, 16)
        self.n_ins += 1
        self.dma_uses[i] = uses + 1
        self._record(rregs, wregs, sem, 16 * (uses + 1))
        return ins

    def finish(self, q='sp'):
        for i, sem in enumerate(self.dma_ring):
            if self.dma_uses[i] > 0:
                self.wait(q, sem, 16 * self.dma_uses[i])
        for e in ['pe', 'dve', 'act', 'pool']:
            if self.cnt[e] > 0:
                self.wait(q, self.sem[e], self.cnt[e])


class Arena:
    def __init__(self, base, size):
        self.base, self.size, self.ptr = base, size, 0

    def take(self, nbytes):
        off = (self.ptr + 31) // 32 * 32
        self.ptr = off + nbytes
        assert self.ptr <= self.size, ("arena overflow", self.ptr, self.size)
        return self.base + off


PL = 72
PC = dict(G_PRE=0, M_NORM=8, R_MU=11, R_W0=22, R_A0=28, R_KK=34, R_KA=37, R_RK=40, R_NORM=43,
          L_CONV=46, L_CONVB=54, L_BA=56, L_BX=60, L_LAM=64, M_BI=68, M_BF=70)
DL = 32
DC = dict(OMKA=0, CLAM=3, C2LAM=7, NBF=11, OMMU=16)
CC = dict(IDENT=0, BONES=128, MKF=256, MKB=384, MNF=512, MNB=576, MUI=640, MLI=704, RMASK=768,
          LSEL=1536, PSEL=1664, NMKF=1668, NMKB=1796, ONES=1924, ISTK=2052)
NCST = 2052 + 64


def make_consts():
    c = np.zeros((128, NCST), np.float32)
    c[:, 0:128] = np.eye(128)
    c[0:64, 128:192] = 1.0
    c[64:128, 192:256] = 1.0
    s = np.arange(64)[:, None]
    t = np.arange(64)[None, :]
    us, ui = (s < t).astype(np.float32), (s <= t).astype(np.float32)
    ls, li = (s > t).astype(np.float32), (s >= t).astype(np.float32)
    c[0:64, 256:320], c[0:64, 320:384] = us, ui
    c[0:64, 384:448], c[0:64, 448:512] = ls, li
    c[0:64, 512:576] = -ls
    c[0:64, 576:640] = -us
    c[0:64, 640:704] = ui
    c[0:64, 704:768] = li
    rm = np.ones(768, np.float32)
    rm[::64] = 0.0
    c[:, 768:1536] = rm[None, :]
    for k in range(6):
        c[k, 1536 + (k % 2) * 64: 1536 + (k % 2) * 64 + 64] = 1.0
        c[k, 1664 + k // 2] = 1.0
    c[0:64, 1668:1796] = -c[0:64, 256:384]
    c[0:64, 1796:1924] = -c[0:64, 384:512]
    c[:, 1924:2052] = 1.0
    for (a, b) in ((256, 768), (1668, 1924)):
        c[64:128, a:b] = c[0:64, a:b]
    c[0:64, 2052:2116] = np.eye(64)
    c[64:128, 2052:2116] = np.eye(64)
    return c


class Builder:
    def __init__(self, NP, TS, debug=False, stop=None):
        self.stop = stop
        self.NP, self.TS = NP, TS
        self.NTOK = NP * 256 + TS
        self.debug = debug
        nc = self.nc = bass.Bass("TRN2", target_bir_lowering=False)
        self.S = Sync(nc)
        self._decl_dram()
        self._alloc()

    def _decl_dram(self):
        nc, NP, TS = self.nc, self.NP, self.TS
        di = lambda n, s: nc.dram_tensor(n, list(s), F32, kind="ExternalInput").ap()
        do = lambda n, s: nc.dram_tensor(n, list(s), F32, kind="ExternalOutput").ap()
        dx = lambda n, s: nc.dram_tensor(n, list(s), F32, kind="Internal").ap()
        self.x_in = di("x_in", [self.NTOK, D])
        self.cc = di("cc", [128, 8, 2])
        self.w_mod = di("w_mod", [2, D, 3 * D])
        self.bmodT = di("bmodT", [128, 2, 16])
        self.bmodg = di("bmodg", [2, D])
        self.gpost = di("gpost", [2, D])
        self.w_in = di("w_in", [2, D, IN_COLS])
        self.w_out = di("w_out", [2, D, D])
        self.pp = di("pp", [128, 2 * PL])
        self.cst = di("cst", [128, NCST])
        self.lora = di("lora", [128, 2, 2, 384])
        self.lruw = di("lruw", [128, 2, 8, 128])
        self.st_mC = di("st_mC", [2, 2, 128, 3, 65])
        self.st_mm = di("st_mm", [6, 2, 2])
        self.st_rH = di("st_rH", [2, 2, 128, 3, 64])
        self.st_l = di("st_l", [128, 2, 2, 2])
        for n in ["x_in", "cc", "w_mod", "bmodT", "bmodg", "gpost", "w_in", "w_out", "pp", "cst", "lora",
                  "lruw", "st_mC", "st_mm", "st_rH", "st_l"]:
            self.S.untracked.add(n)
        self.y_out = do("y_out", [self.NTOK, D])
        self.o_mC = do("o_mC", [NP, 2, 2, 128, 3, 65])
        self.o_mm = do("o_mm", [NP, 2, 2, 6, 1])
        self.o_rH = do("o_rH", [NP, 2, 2, 128, 3, 64])
        self.o_l = do("o_l", [NP, 2, 2, 128, 2])
        self.x1 = dx("x1", [self.NTOK, D])
        self.sHB = dx("sHB", [max(TS, 64), 384])
        self.sYB = dx("sYB", [max(TS // 256, 1), 128, 768])
        self.sLB = dx("sLB", [128, 2, max(TS, 64)])
        nb = max(TS // 256, 1)
        dxt = lambda n, s_, dt: nc.dram_tensor(n, list(s_), dt, kind="Internal").ap()
        self.sc = {
            'QT': dxt("sc_QT", [nb, 128, 3, 256], BF16), 'KT': dxt("sc_KT", [nb, 128, 3, 256], BF16),
            'KTOK': dxt("sc_KTOK", [nb, 64, 4, 384], BF16), 'VAUG': dxt("sc_VAUG", [nb, 64, 4, 6, 65], BF16),
            'GOZ': dxt("sc_GOZ", [nb, 128, 3, 256], F32), 'GI': dxt("sc_GI", [nb, 6, 256], F32),
            'LF': dxt("sc_LF", [nb, 6, 256], F32), 'XC': dxt("sc_XC", [nb, 128, 2, 256], F32),
            'LZT': dxt("sc_LZT", [nb, 128, 2, 256], F32), 'BLK': dxt("sc_BLK", [nb, 128, 11, 256], F32),
            'RZT': dxt("sc_RZT", [nb, 128, 3, 256], F32), 'KH': dxt("sc_KH", [nb, 128, 3, 256], F32),
            'VSTK': dxt("sc_VSTK", [nb, 128, 4, 3, 64], BF16),
        }
        if self.debug:
            self.dbg = do("dbg", [128, 32768])
            self.dbg_map = {}
            self.dbg_off = 0

    def T(self, name, shape, dtype, arena):
        es = 2 if dtype == BF16 else 4
        nb = _prod(shape[1:]) * es
        off = arena.take(nb)
        t = self.nc.alloc_sbuf_tensor_at(name, list(shape), dtype, offset=off)
        self.S.sb_addr[t.name] = (off, es)
        return t

    def _alloc(self):
        nc = self.nc
        B0 = 16384 + 256
        LIM = 224 * 1024 - 256
        P = Arena(B0, LIM - B0)
        T = self.T
        self.W_IN = T("W_IN", [128, 8, IN_COLS], BF16, P)
        self.W_OUT = T("W_OUT", [128, 8, D], BF16, P)
        self.CST = T("CST", [128, NCST], F32, P)
        self.PP = T("PP", [128, 2 * PL], F32, P)
        self.DV = T("DV", [128, 2 * DL], F32, P)
        self.LORA = T("LORA", [128, 2, 2, 384], F32, P)
        self.LRUW = T("LRUW", [128, 2, 8, 128], F32, P)
        self.RKD = T("RKD", [128, 2, 3, 128], F32, P)
        self.GS = T("GS", [128, 2, 8], F32, P)
        self.SH = T("SH", [128, 2, 8], F32, P)
        self.GATEB = T("GATEB", [128, 2, D], F32, P)
        self.IDB = T("IDB", [128, 128], BF16, P)
        self.ISTKB = T("ISTKB", [128, 64], BF16, P)
        self.mC = [T("mC%d" % d, [128, 3, 65], F32, P) for d in range(2)]
        self.mM = [T("mM%d" % d, [6, 1], F32, P) for d in range(2)]
        self.rH = [T("rH%d" % d, [128, 3, 64], F32, P) for d in range(2)]
        self.lS = [T("lS%d" % d, [128, 2], F32, P) for d in range(2)]
        XB = Arena(P.take(16384), 16384)
        self.HT = T("HT", [128, 8, 384], BF16, P)
        self.MIXT = T("MIXT", [128, 8, 256], BF16, P)
        self.BLK = T("BLK", [128, 11, 256], F32, P)
        abase = P.take(0)
        asize = P.size - P.ptr
        self.asize = asize
        mk = lambda: Arena(abase, asize)
        a = Arena(XB.base, XB.size)
        self.XW = [T("XW%d" % i, [128, D], F32, a) for i in range(2)]
        self.XN = [T("XN%d" % i, [128, D], F32, a) for i in range(2)]
        a = Arena(XB.base, XB.size)
        self.YB = T("YB", [128, 4, 3, 64], F32, a)
        self.RZT = T("RZT", [128, 3, 256], F32, a)
        self.WSTG = [T("WSTG0", [128, 8, 256], F32, Arena(self.S.sb_addr[self.BLK.name][0], 11264)),
                     T("WSTG1", [128, 8, 256], F32, Arena(XB.base, 8192)),
                     T("WSTG2", [128, 8, 256], F32, Arena(XB.base + 8192, 8192))]
        a = mk()
        self.WM = T("WM", [128, 8, 512], F32, a)
        self.SCT = T("SCT", [128, 8, 2], F32, a)
        self.SCB = T("SCB", [128, 2, 8, 128], F32, a)
        self.MODT = T("MODT", [128, 16, 2], F32, a)
        self.BMT = T("BMT", [128, 2, 16], F32, a)
        self.BG = T("BG", [128, D], F32, a)
        self.GP = T("GP", [128, D], F32, a)
        self.TMPS = T("TMPS", [128, 32], F32, a)
        a = mk()
        self.QT = T("QT", [128, 3, 256], BF16, a)
        self.KT = T("KT", [128, 3, 256], BF16, a)
        self.GOZ = T("GOZ", [128, 3, 256], F32, a)
        self.KTOK = T("KTOK", [64, 4, 384], BF16, a)
        self.VAUG = T("VAUG", [64, 4, 6, 65], BF16, a)
        self.HB = T("HB", [64, 4, 384], F32, a)
        self.mg = []
        for d in range(2):
            g = {}
            for n in ["gi", "lf", "pre", "bb", "gg", "ee", "fl"]:
                g[n] = T("mg_%s%d" % (n, d), [6, 256], F32, a)
            for n in ["mx", "mch", "mprev", "MM", "dec"]:
                g[n] = T("mg_%s%d" % (n, d), [6, 4], F32, a)
            g["X2"] = T("mg_X2%d" % d, [6, 4, 3], F32, a)
            g["etok"] = T("mg_etok%d" % d, [64, 4, 6], F32, a)
            g["fltok"] = T("mg_fltok%d" % d, [64, 4, 6], F32, a)
            g["decb"] = T("mg_decb%d" % d, [128, 4, 3], F32, a)
            self.mg.append(g)
        self.STSB = [T("STSB%d" % i, [64, 6, 64], BF16, a) for i in range(2)]
        self.VP = [T("VP%d" % i, [64, 6, 65], BF16, a) for i in range(2)]
        self.CDEC = T("CDEC", [128, 3, 65], F32, a)
        self.CDBF = T("CDBF", [128, 3, 65], BF16, a)
        self.DN = T("DN", [64, 6], F32, a)
        self.RDN = T("RDN", [64, 6], F32, a)
        self.HD = T("HD", [64, 6, 64], F32, a)
        self.SQ = T("SQ", [64, 4, 384], F32, a)
        self.SSQ = T("SSQ", [64, 24], F32, a)
        self.RSTD = T("RSTD", [64, 24], F32, a)
        self.ZT = [T("ZT%d" % i, [128, 256], F32, a) for i in range(2)]
        a = mk()
        self.XL = T("XL", [128, 2, 392], F32, a)
        self.XC = T("XC", [128, 2, 256], F32, a)
        self.LZT = T("LZT", [128, 2, 256], F32, a)
        self.lt = {n: T("lt_" + n, [128, 2, 256], F32, a) for n in ["rg", "ig", "aa", "a2", "bt"]}
        self.HL = [T("HL%d" % d, [128, 2, 256], F32, a) for d in range(2)]
        a = mk()
        self.URS = T("URS", [128, 11, 384], F32, a)
        a = mk()
        self.KH = T("KH", [128, 3, 256], F32, a)
        R1 = a.take(7 * 3072)
        a1 = Arena(R1, 7 * 3072)
        self.rt = {n: T("rt_" + n, [128, 3, 256], F32, a1) for n in ["sg", "aa", "kt", "bb", "cs", "d", "E"]}
        a2 = Arena(R1, 7 * 3072)
        self.A1BD = T("A1BD", [128, 3, 2, 128], BF16, a2)
        self.A2BD = T("A2BD", [128, 3, 2, 128], BF16, a2)
        self.NN = [T("NN%d" % i, [128, 3, 128], BF16, a2) for i in range(2)]
        self.NTT = [T("NTT%d" % i, [128, 3, 128], BF16, a2) for i in range(2)]
        self.U = T("U", [128, 3, 64], F32, a2)
        self.UBF = T("UBF", [128, 3, 64], BF16, a2)
        self.HBF = T("HBF", [128, 3, 64], BF16, a2)
        self.HTMP = T("HTMP", [128, 3, 64], F32, a2)
        self.YBD = T("YBD", [128, 3, 128], F32, a2)
        self.KTTOK = T("KTTOK", [128, 4, 3, 128], BF16, a2)
        self.BTTOK = T("BTTOK", [128, 4, 3, 128], BF16, a2)
        self.rSSQ = T("rSSQ", [128, 12], F32, a2)
        self.rRSTD = T("rRSTD", [128, 12], F32, a2)
        self.KR = T("KR", [128, 3, 4, 2, 64], BF16, a)
        self.BTT = T("BTT", [128, 3, 256], BF16, a)
        self.KTT = T("KTT", [128, 3, 256], BF16, a)
        bdr = a.take(4 * 3072)
        a3 = Arena(bdr, 4 * 3072)
        self.BD = {n: T("BD_" + n, [128, 3, 4, 128], BF16, a3) for n in ["kt", "b", "kh", "r"]}
        a3 = Arena(bdr, 4 * 3072)
        self.ft = {n: T("ft_" + n, [128, 3, 256], F32, a3) for n in ["rk", "bon", "t1"]}
        self.YSQ = T("YSQ", [128, 4, 3, 64], F32, a3)
        self.BDV = T("BDV", [128, 3, 4, 128], BF16, a)
        self.VSTK = T("VSTK", [128, 4, 3, 64], BF16, a)
        self.TWL = T("TWL", [128, 256], F32, a)
        self.GL = T("GL", [128, 3, 4], F32, a)
        self.PA = nc.alloc_psum_tensor("PA", [128, 1024], F32)
        self.PB = nc.alloc_psum_tensor("PB", [128, 1024], F32)
        self.PQ = [nc.alloc_psum_tensor("PQ%d" % i, [128, 512], F32) for i in range(3)]
        self.PTB = nc.alloc_psum_tensor("PTB", [128, 1024], BF16)
        self.pq_i = 0

    def pq(self):
        t = self.PQ[self.pq_i % 3]
        self.pq_i += 1
        return t

    def V(self, t, p0, npart, off, dims):
        pstep = _prod(list(t.shape)[1:])
        return bass.AP(t, p0 * pstep + off, [[pstep, npart]] + [list(d) for d in dims])

    def tt(self, e, out, in0, in1, op):
        return self.S.emit(e, [in0, in1], [out], lambda g: g.tensor_tensor(out=out, in0=in0, in1=in1, op=op))

    def ts(self, e, out, in0, s1, s2, op0, op1=None):
        rd = [in0] + [s for s in (s1, s2) if not isinstance(s, (int, float)) and s is not None]
        if op1 is None:
            return self.S.emit(e, rd, [out], lambda g: g.tensor_scalar(out=out, in0=in0, scalar1=s1, scalar2=None, op0=op0))
        return self.S.emit(e, rd, [out], lambda g: g.tensor_scalar(out=out, in0=in0, scalar1=s1, scalar2=s2, op0=op0, op1=op1))

    def stt(self, out, in0, sc, in1, op0, op1):
        rd = [in0, in1] + ([] if isinstance(sc, (int, float)) else [sc])
        return self.S.emit('dve', rd, [out], lambda g: g.scalar_tensor_tensor(out=out, in0=in0, scalar=sc, in1=in1, op0=op0, op1=op1))

    def act(self, out, in_, func, bias=None, scale=None, accum=None):
        rd = [in_]
        kw = {}
        if bias is not None:
            kw['bias'] = bias
            if not isinstance(bias, (int, float)):
                rd.append(bias)
        if scale is not None:
            kw['scale'] = scale
            if not isinstance(scale, (int, float)):
                rd.append(scale)
        wr = [out]
        if accum is not None:
            kw['accum_out'] = accum
            wr.append(accum)
        return self.S.emit('act', rd, wr, lambda g: g.activation(out=out, in_=in_, func=func, **kw))

    def cp(self, e, out, in_):
        if e == 'act':
            return self.act(out, in_, AF.Copy)
        return self.S.emit(e, [in_], [out], lambda g: g.tensor_copy(out=out, in_=in_))

    def _pe_rowtile_guard(self, lhsT, out):
        S = self.S
        st = S.region(lhsT)
        k = st[2] - st[1]
        kr = 32 if k <= 32 else (64 if k <= 64 else 128)
        rows = (st[1], st[1] + kr)
        oreg = S.region(out)
        last = getattr(self, '_last_pe', None)
        if last is not None:
            lrows, loreg, lins, linc = last
            disjoint = rows[1] <= lrows[0] or lrows[1] <= rows[0]
            samebank = (loreg[0] == oreg[0]) and loreg[3] < oreg[4] and oreg[3] < loreg[4]
            if disjoint and samebank:
                if not linc:
                    S.cnt['pe'] += 1
                    lins.then_inc(S.sem['pe'], 1)
                S.wait('pe', S.sem['pe'], S.cnt['pe'])
        return rows, oreg

    def mm(self, out, lhsT, rhs, start=True, stop=True, inc=None):
        if inc is None:
            inc = stop
        rows, oreg = self._pe_rowtile_guard(lhsT, out)
        ins = self.S.emit('pe', [lhsT, rhs], [out],
                          lambda g: g.matmul(out, lhsT=lhsT, rhs=rhs, start=start, stop=stop), inc=inc)
        self._last_pe = (rows, oreg, ins, inc)
        return ins

    def tr(self, out, in_, inc=True, bf=False):
        n = in_.shape[0]
        ident = self.IDB[0:n, 0:n] if bf else self.CST[0:n, CC['IDENT']:CC['IDENT'] + n]
        rows, oreg = self._pe_rowtile_guard(in_, out)
        ins = self.S.emit('pe', [in_, ident], [out],
                          lambda g: g.transpose(out=out, in_=in_, identity=ident), inc=inc)
        self._last_pe = (rows, oreg, ins, inc)
        return ins

    def memset(self, e, ap, v):
        return self.S.emit(e, [], [ap], lambda g: g.memset(ap, v))

    def scan(self, out, d0, d1, init, op0, op1):
        rd = [d0, d1] + ([] if isinstance(init, (int, float)) else [init])
        return self.S.emit('dve', rd, [out], lambda g: g.tensor_tensor_scan(out=out, data0=d0, data1=d1, initial=init, op0=op0, op1=op1))

    def recip(self, out, in_):
        return self.S.emit('dve', [in_], [out], lambda g: g.reciprocal(out=out, in_=in_))

    def reduce(self, out, in_, op, axis=AX.X):
        return self.S.emit('dve', [in_], [out], lambda g: g.tensor_reduce(out=out, in_=in_, axis=axis, op=op))

    def dump(self, name, ap):
        if not self.debug or name in self.dbg_map:
            return
        if getattr(self, 'dbg_filter', None) is not None and not any(name.startswith(p) for p in self.dbg_filter):
            return
        shp = list(ap.shape)
        npart, nfree = shp[0], _prod(shp[1:])
        stage = self.XN[1]
        assert nfree <= 1024
        dst = self.V(stage, 0, npart, 0, [[_prod(shp[i + 1:]), shp[i]] for i in range(1, len(shp))])
        self.cp('dve', dst, ap)
        self.S.dma('sp', self.dbg[0:npart, self.dbg_off:self.dbg_off + nfree], stage[0:npart, 0:nfree], allow_slow_non_contiguous=True)
        self.dbg_map[name] = (self.dbg_off, npart, shp[1:])
        self.dbg_off += nfree

    def ppc(self, l, key, j=0, rows=128):
        c = l * PL + PC[key] + j
        return self.PP[0:rows, c:c + 1]

    def dvc(self, l, key, j=0, rows=128):
        c = l * DL + DC[key] + j
        return self.DV[0:rows, c:c + 1]

    def setup(self):
        S = self.S
        S.dma('sp', self.CST[:, :], self.cst[:, :])
        S.dma('sp', self.PP[:, :], self.pp[:, :])
        S.dma('sp', self.LORA[:, :, :, :], self.lora[:, :, :, :])
        S.dma('sp', self.LRUW[:, :, :, :], self.lruw[:, :, :, :])
        self.memset('dve', self.DV[:, :], 0.0)
        self.cp('dve', self.IDB[:, :], self.CST[:, CC['IDENT']:CC['IDENT'] + 128])
        self.cp('dve', self.ISTKB[:, :], self.CST[:, CC['ISTK']:CC['ISTK'] + 64])
        for l in range(2):
            self.ts('dve', self.DV[:, l * DL + DC['OMKA']: l * DL + DC['OMKA'] + 3],
                    self.PP[:, l * PL + PC['R_KA']: l * PL + PC['R_KA'] + 3], -1.0, 1.0, ALU.mult, ALU.add)
            self.ts('dve', self.DV[:, l * DL + DC['OMMU']: l * DL + DC['OMMU'] + 11],
                    self.PP[:, l * PL + PC['R_MU']: l * PL + PC['R_MU'] + 11], -1.0, 1.0, ALU.mult, ALU.add)
            lam = self.PP[:, l * PL + PC['L_LAM']: l * PL + PC['L_LAM'] + 4]
            t0 = self.TMPS[:, 0:4]
            self.act(t0, lam, AF.Exp, scale=-1.0)
            self.act(t0, t0, AF.Ln, bias=1.0)
            self.ts('dve', self.DV[:, l * DL + DC['CLAM']: l * DL + DC['CLAM'] + 4], t0, -8.0, None, ALU.mult)
            self.ts('dve', self.DV[:, l * DL + DC['C2LAM']: l * DL + DC['C2LAM'] + 4], t0, -16.0, None, ALU.mult)
            self.ts('dve', self.DV[0:6, l * DL + DC['NBF']: l * DL + DC['NBF'] + 2],
                    self.PP[0:6, l * PL + PC['M_BF']: l * PL + PC['M_BF'] + 2], -1.0, None, ALU.mult)
            for hp in range(3):
                self.ts('dve', self.RKD[:, l, hp, :], self.CST[:, CC['BONES']:CC['BONES'] + 128],
                        self.ppc(l, 'R_RK', hp), None, ALU.mult)
        self.memset('dve', self.XL[:, :, 0:2], 0.0)

    def load_layer(self, l):
        S = self.S
        wsrc = self.w_in[l].rearrange("(kc p) c -> p kc c", p=128)
        wo = self.w_out[l].rearrange("(kc p) c -> p kc c", p=128)
        pieces = [(self.W_IN, wsrc, c0, min(256, IN_COLS - c0)) for c0 in range(0, IN_COLS, 256)]
        pieces += [(self.W_OUT, wo, c0, 256) for c0 in range(0, D, 256)]
        for i, (dst, src, c0, n) in enumerate(pieces):
            stg = self.WSTG[i % 3]
            S.dma('sp', stg[:, :, 0:n], src[:, :, c0:c0 + n])
            self.cp('pool', dst[:, :, c0:c0 + n], stg[:, :, 0:n])
        if self.stop == 'load_w':
            raise StopIteration
        S.dma('sp', self.SCT[:, :, :], self.cc[:, :, :])
        S.dma('sp', self.BMT[:, :, :], self.bmodT[:, :, :])
        self.act(self.SCT[:, :, :], self.SCT[:, :, :], AF.Silu)
        for m in range(2):
            for kc in range(8):
                src = self.V(self.SCT, 0, 128, kc * 2 + m, [[0, 128]])
                self.cp('dve', self.SCB[:, m, kc, :], src)
        wm = self.w_mod[l].rearrange("(kc p) c -> p kc c", p=128)
        for blk in range(6):
            S.dma('sp', self.WM[:, :, :], wm[:, :, blk * 512:(blk + 1) * 512])
            if blk < 4:
                ps = self.pq()
                for j in range(4):
                    for kc in range(8):
                        self.mm(ps[:, j * 2:j * 2 + 2], self.WM[:, kc, j * 128:(j + 1) * 128], self.SCT[:, kc, :],
                                start=(kc == 0), stop=(kc == 7))
                o = self.MODT[:, blk * 4:(blk + 1) * 4, :]
                bsrc = self.V(self.BMT, 0, 128, l * 16 + blk * 4, [[1, 4], [0, 2]])
                self.tt('dve', o, self.V(ps, 0, 128, 0, [[2, 4], [1, 2]]), bsrc, ALU.add)
            else:
                half = blk - 4
                for m in range(2):
                    ps = self.pq()
                    for kc in range(8):
                        self.mm(ps[:, :], self.SCB[:, m, kc, :], self.WM[:, kc, :], start=(kc == 0), stop=(kc == 7))
                    self.cp('act', self.GATEB[:, m, half * 512:(half + 1) * 512], ps[:, :])
        if self.stop == 'load_m':
            raise StopIteration
        for m in range(2):
            self.cp('dve', self.SH[:, m, :], self.V(self.MODT, 0, 128, m, [[2, 8]]))
            t0 = self.TMPS[:, 8:16]
            self.ts('dve', t0, self.V(self.MODT, 0, 128, 16 + m, [[2, 8]]), 1.0, None, ALU.add)
            self.tt('dve', self.GS[:, m, :], t0, self.PP[:, l * PL + PC['G_PRE']: l * PL + PC['G_PRE'] + 8], ALU.mult)
        if self.stop == 'load_g':
            raise StopIteration
        S.dma('sp', self.BG[:, :], bass.AP(self.bmodg.tensor, l * D, [[0, 128], [1, D]]))
        S.dma('sp', self.GP[:, :], bass.AP(self.gpost.tensor, l * D, [[0, 128], [1, D]]))
        if self.stop == 'load_b':
            raise StopIteration
        for m in range(2):
            self.tt('dve', self.GATEB[:, m, :], self.GATEB[:, m, :], self.BG[:, :], ALU.add)
            self.tt('dve', self.GATEB[:, m, :], self.GATEB[:, m, :], self.GP[:, :], ALU.mult)

    def proj_fm(self, c0, ncols, t_off, ntok, evac):
        ps = self.pq()
        for kc in range(8):
            self.mm(ps[0:ncols, 0:ntok], self.W_IN[:, kc, c0:c0 + ncols], self.HT[:, kc, t_off:t_off + ntok],
                    start=(kc == 0), stop=(kc == 7))
        evac(ps[0:ncols, 0:ntok])

    def proj_tm(self, c0, ncols, t_off, evac):
        ps = self.pq()
        for kc in range(8):
            self.mm(ps[0:64, 0:ncols], self.HT[:, kc, t_off:t_off + 64], self.W_IN[:, kc, c0:c0 + ncols],
                    start=(kc == 0), stop=(kc == 7))
        evac(ps[0:64, 0:ncols])

    def visit(self, l, mod, xsrc, xdst, row0, T, t0, dirs, grid, seq_idx, first, last, mode='full'):
        S = self.S
        w0 = max(0, t0 - 64)
        w1 = min(T, t0 + TB + 64)
        W = w1 - w0
        co = t0 - w0
        do_f = 0 in dirs
        prompt = (mod == 0)
        if mode != 'load':
            ntile = (W + 127) // 128
            for i in range(ntile):
                n = min(128, W - i * 128)
                xw, xn = self.XW[i % 2], self.XN[i % 2]
                r = row0 + w0 + i * 128
                S.dma('sp', xw[0:n, :], xsrc[r:r + n, :])
                ssq = self.TMPS[0:n, 16 + i:17 + i]
                self.act(xn[0:n, :], xw[0:n, :], AF.Square, accum=ssq)
                if self.stop == 'n1':
                    raise StopIteration
                rs = self.TMPS[0:n, 20 + i:21 + i]
                self.ts('dve', rs, ssq, 1.0 / D, EPS, ALU.mult, ALU.add)
                self.act(rs, rs, AF.Sqrt)
                self.recip(rs, rs)
                if self.stop == 'n2':
                    raise StopIteration
                self.act(xn[0:n, :], xw[0:n, :], AF.Copy, scale=rs)
                if self.stop == 'n3':
                    raise StopIteration
                for half in range(2):
                    ps = self.pq()
                    for j in range(4):
                        kc = half * 4 + j
                        self.tr(ps[:, j * 128:j * 128 + n], xn[0:n, kc * 128:(kc + 1) * 128])
                    if self.stop == 'n4':
                        raise StopIteration
                    for j in range(4):
                        kc = half * 4 + j
                        o = self.HT[:, kc, i * 128:i * 128 + n]
                        if half == 0:
                            self.ts('dve', o, ps[:, j * 128:j * 128 + n], self.GS[:, mod, kc:kc + 1], self.SH[:, mod, kc:kc + 1], ALU.mult, ALU.add)
                        else:
                            self.act(o, ps[:, j * 128:j * 128 + n], AF.Identity, bias=self.SH[:, mod, kc:kc + 1], scale=self.GS[:, mod, kc:kc + 1])
        if self.stop in ('norm', 'n5a', 'n5d'):
            raise StopIteration
        self.stage_mlstm(l, t0, co, dirs, prompt, seq_idx, first, last, mode)
        if self.stop == 'mlstm':
            raise StopIteration
        self.stage_lru(l, t0, co, W, w0, w1, T, dirs, prompt, seq_idx, first, last, mode)
        if self.stop == 'lru':
            raise StopIteration
        self.stage_rwkv(l, t0, co, W, w0, w1, T, dirs, grid, prompt, seq_idx, first, last, mode)
        for kc in range(8):
            self.dump('mix%d' % kc, self.MIXT[:, kc, :])
        if self.stop == 'rwkv':
            raise StopIteration
        if do_f:
            self.stage_out(l, mod, xsrc, xdst, row0 + t0)
        if self.stop == 'out':
            raise StopIteration

    def stage_mlstm(self, l, t0, co, dirs, prompt, seq_idx, first, last, mode='full'):
        S = self.S
        do_f = 0 in dirs
        if mode != 'load':
            for hp in range(3):
                self.proj_fm(hp * 128, 128, co, 256, lambda ps, hp=hp: self.cp('act', self.QT[:, hp, :], ps))
                self.proj_fm(384 + hp * 128, 128, co, 256, lambda ps, hp=hp: self.act(self.KT[:, hp, :], ps, AF.Copy, scale=0.125))
            if do_f or mode == 'store':
                for hp in range(3):
                    def ev_o(ps, hp=hp):
                        self.act(self.GOZ[:, hp, :], ps, AF.Sigmoid)
                    self.proj_fm(1152 + hp * 128, 128, co, 256, ev_o)
                    def ev_z2(ps, hp=hp):
                        tz = self.ZT[hp % 2]
                        self.act(tz[:, :], ps, AF.Silu)
                        self.tt('dve', self.GOZ[:, hp, :], self.GOZ[:, hp, :], tz[:, :], ALU.mult)
                    self.proj_fm(1536 + hp * 128, 128, co, 256, ev_z2)
            for c in range(4):
                self.proj_tm(384, 384, co + c * 64, lambda ps, c=c: self.act(self.KTOK[:, c, :], ps, AF.Copy, scale=0.125))
                def ev_v(ps, c=c):
                    self.cp('dve', self.VAUG[:, c, :, 0:64], self.V(ps.tensor, 0, 64, 0, [[64, 6], [1, 64]]))
                self.proj_tm(768, 384, co + c * 64, ev_v)
            self.memset('dve', self.VAUG[:, :, :, 64:65], 1.0)
            for d in (sorted(set(dirs) | {0}) if mode == 'store' else dirs):
                g = self.mg[d]
                self.proj_fm(1920 + d * 6, 6, co, 256, lambda ps, g=g, d=d: self.act(g["gi"][:, :], ps, AF.Identity, bias=self.ppc(l, 'M_BI', d, 6)))
                def ev_f(ps, g=g, d=d):
                    self.act(g["lf"][:, :], ps, AF.Exp, bias=self.dvc(l, 'NBF', d, 6), scale=-1.0)
                    self.act(g["lf"][:, :], g["lf"][:, :], AF.Ln, bias=1.0)
                    self.ts('dve', g["lf"][:, :], g["lf"][:, :], -1.0, None, ALU.mult)
                self.proj_fm(1932 + d * 6, 6, co, 256, ev_f)
        bi = t0 // 256
        ml_items = [(self.QT[:, :, :], 'QT'), (self.KT[:, :, :], 'KT'), (self.KTOK[:, :, :], 'KTOK'), (self.VAUG[:, :, :, :], 'VAUG'),
                    (self.GOZ[:, :, :], 'GOZ'), (self.mg[0]["gi"][:, :], 'GI'), (self.mg[0]["lf"][:, :], 'LF')]
        if mode == 'store':
            for ap_, k_ in ml_items:
                S.dma('sp', self.sc[k_][bi], ap_)
        if mode == 'load':
            for ap_, k_ in ml_items:
                S.dma('sp', ap_, self.sc[k_][bi])
        for d in dirs:
            if first[d]:
                if prompt:
                    self.memset('dve', self.mC[d][:, :, :], 0.0)
                    self.memset('dve', self.mM[d][:, :], 0.0)
                else:
                    S.dma('sp', self.mC[d][:, :, :], self.st_mC[l, d])
                    S.dma('sp', self.mM[d][:, :], self.st_mm[:, l, d:d + 1], allow_slow_non_contiguous=True)
        if do_f and not (1 in dirs):
            S.dma('sp', self.HB[:, :, :], self.sHB[t0:t0 + 256, :].rearrange("(c s) f -> s c f", s=64))
        for d in dirs:
            g = self.mg[d]
            v3 = lambda t: self.V(t, 0, 6, 0, [[64, 4], [1, 64]])
            self.scan(g["pre"][:, :], self.CST[0:6, CC['RMASK']:CC['RMASK'] + 256], g["lf"][:, :], 0.0, ALU.mult, ALU.add)
            bL = self.V(g["pre"], 0, 6, 63, [[64, 4]])
            bLb = self.V(g["pre"], 0, 6, 63, [[64, 4], [0, 64]])
            if d == 0:
                bsrc = g["pre"]
            else:
                self.tt('dve', v3(g["bb"]), bLb, v3(g["pre"]), ALU.subtract)
                self.tt('dve', g["bb"][:, :], g["bb"][:, :], g["lf"][:, :], ALU.add)
                bsrc = g["bb"]
            self.tt('dve', g["gg"][:, :], g["gi"][:, :], bsrc[:, :], ALU.subtract)
            self.reduce(g["mx"][:, :], v3(g["gg"]), ALU.max)
            if d == 0:
                mo, mxv, blv = g["mch"][:, :], g["mx"][:, :], bL
            else:
                mo = self.V(g["mch"], 0, 6, 3, [[-1, 4]])
                mxv = self.V(g["mx"], 0, 6, 3, [[-1, 4]])
                blv = self.V(g["pre"], 0, 6, 63 + 3 * 64, [[-64, 4]])
            self.scan(mo, mxv, blv, self.mM[d][:, 0:1], ALU.max, ALU.add)
            if d == 0:
                self.cp('dve', g["mprev"][:, 1:4], g["mch"][:, 0:3])
                self.cp('dve', g["mprev"][:, 0:1], self.mM[d][:, 0:1])
                mfin = g["mch"][:, 3:4]
            else:
                self.cp('dve', g["mprev"][:, 0:3], g["mch"][:, 1:4])
                self.cp('dve', g["mprev"][:, 3:4], self.mM[d][:, 0:1])
                mfin = g["mch"][:, 0:1]
            self.tt('dve', g["MM"][:, :], g["mprev"][:, :], g["mx"][:, :], ALU.max)
            self.tt('dve', g["dec"][:, :], g["mprev"][:, :], g["MM"][:, :], ALU.subtract)
            self.act(g["dec"][:, :], g["dec"][:, :], AF.Exp)
            self.cp('dve', self.mM[d][:, 0:1], mfin)
            MMb = self.V(g["MM"], 0, 6, 0, [[1, 4], [0, 64]])
            self.tt('dve', v3(g["ee"]), v3(g["gg"]), MMb, ALU.subtract)
            self.act(g["ee"][:, :], g["ee"][:, :], AF.Exp)
            self.tt('dve', v3(g["fl"]), v3(bsrc), MMb, ALU.add)
            self.act(g["fl"][:, :], g["fl"][:, :], AF.Exp, scale=-1.0)
            ps = self.pq()
            for c in range(4):
                self.tr(ps[0:64, c * 6:c * 6 + 6], g["ee"][:, c * 64:(c + 1) * 64])
                self.tr(ps[0:64, 24 + c * 6:24 + c * 6 + 6], g["fl"][:, c * 64:(c + 1) * 64])
            self.cp('dve', g["etok"][:, :, :], self.V(ps, 0, 64, 0, [[6, 4], [1, 6]]))
            self.cp('dve', g["fltok"][:, :, :], self.V(ps, 0, 64, 24, [[6, 4], [1, 6]]))
            self.tt('dve', g["X2"][:, :, :], self.V(g["dec"], 0, 6, 0, [[1, 4], [0, 3]]),
                    self.V(self.CST, 0, 6, CC['PSEL'], [[0, 4], [1, 3]]), ALU.mult)
            ps2 = self.pq()
            self.mm(ps2[:, 0:12], self.CST[0:6, CC['LSEL']:CC['LSEL'] + 128], self.V(g["X2"], 0, 6, 0, [[1, 12]]))
            self.cp('dve', g["decb"][:, :, :], self.V(ps2, 0, 128, 0, [[3, 4], [1, 3]]))
        for d in sorted(dirs, reverse=True):
            g = self.mg[d]
            mask = self.CST[0:64, CC['MUI']:CC['MUI'] + 64] if d == 0 else self.CST[0:64, CC['MLI']:CC['MLI'] + 64]
            maskb = self.V(self.CST, 0, 64, CC['MUI'] if d == 0 else CC['MLI'], [[0, 6], [1, 64]])
            for j in range(4):
                c = j if d == 0 else 3 - j
                cs = slice(c * 64, (c + 1) * 64)
                stsb, vp = self.STSB[j % 2], self.VP[j % 2]
                ps = self.pq()
                for h in (0, 2, 4, 1, 3, 5):
                    hp, pb = h // 2, 64 * (h % 2)
                    self.mm(ps[0:64, h * 64:(h + 1) * 64], self.KT[pb:pb + 64, hp, cs], self.QT[pb:pb + 64, hp, cs], inc=(h == 5))
                self.tt('dve', stsb[:, :, :], self.V(ps, 0, 64, 0, [[64, 6], [1, 64]]), maskb, ALU.mult)
                self.tt('dve', vp[:, :, :], self.VAUG[:, c, :, :], self.V(g["etok"], 0, 64, c * 6, [[1, 6], [0, 65]]), ALU.mult)
                self.tt('dve', self.CDEC[:, :, :], self.mC[d][:, :, :], self.V(g["decb"], 0, 128, c * 3, [[1, 3], [0, 65]]), ALU.mult)
                self.cp('act', self.CDBF[:, :, :], self.CDEC[:, :, :])
                ph = self.pq()
                for h in (0, 2, 4, 1, 3, 5):
                    hp, pb = h // 2, 64 * (h % 2)
                    o = ph[0:64, h * 65:(h + 1) * 65]
                    self.mm(o, stsb[:, h, :], vp[:, h, :], start=True, stop=False)
                    self.mm(o, self.QT[pb:pb + 64, hp, cs], self.CDBF[pb:pb + 64, hp, :], start=False, stop=True, inc=(h == 5))
                pc = self.pq()
                for h in range(6):
                    hp, pb = h // 2, 64 * (h % 2)
                    self.mm(pc[pb:pb + 64, hp * 65:(hp + 1) * 65], self.KTOK[:, c, h * 64:(h + 1) * 64], vp[:, h, :], inc=(h == 5))
                self.tt('dve', self.mC[d][:, :, :], self.CDEC[:, :, :], self.V(pc, 0, 128, 0, [[65, 3], [1, 65]]), ALU.add)
                self.act(self.DN[:, :], self.V(ph, 0, 64, 64, [[65, 6]]), AF.Abs)
                self.tt('dve', self.DN[:, :], self.DN[:, :], g["fltok"][:, c, :], ALU.max)
                self.recip(self.RDN[:, :], self.DN[:, :])
                hsrc = self.V(ph, 0, 64, 0, [[65, 6], [1, 64]])
                rb = self.V(self.RDN, 0, 64, 0, [[1, 6], [0, 64]])
                hbv = self.V(self.HB, 0, 64, c * 384, [[64, 6], [1, 64]])
                if d == 1:
                    self.tt('dve', hbv, hsrc, rb, ALU.mult)
                else:
                    self.tt('dve', self.HD[:, :, :], hsrc, rb, ALU.mult)
                    self.tt('dve', hbv, hbv, self.HD[:, :, :], ALU.add)
            if last[d] and prompt:
                S.dma('sp', self.o_mC[seq_idx, l, d], self.mC[d][:, :, :])
                S.dma('sp', self.o_mm[seq_idx, l, d], self.mM[d][:, 0:1])
        if not do_f:
            S.dma('sp', self.sHB[t0:t0 + 256, :].rearrange("(c s) f -> s c f", s=64), self.HB[:, :, :])
            return
        self.tt('dve', self.SQ[:, :, :], self.HB[:, :, :], self.HB[:, :, :], ALU.mult)
        self.reduce(self.SSQ[:, :], self.V(self.SQ, 0, 64, 0, [[64, 24], [1, 64]]), ALU.add)
        self.ts('dve', self.SSQ[:, :], self.SSQ[:, :], 1.0 / 64, EPS, ALU.mult, ALU.add)
        self.act(self.SSQ[:, :], self.SSQ[:, :], AF.Sqrt)
        self.recip(self.RSTD[:, :], self.SSQ[:, :])
        self.tt('dve', self.V(self.SQ, 0, 64, 0, [[64, 24], [1, 64]]), self.V(self.HB, 0, 64, 0, [[64, 24], [1, 64]]),
                self.V(self.RSTD, 0, 64, 0, [[1, 24], [0, 64]]), ALU.mult)
        for hp in range(3):
            ps = self.pq()
            for c in range(4):
                self.tr(ps[:, c * 64:(c + 1) * 64], self.SQ[:, c, hp * 128:(hp + 1) * 128], inc=(c == 3))
            self.stt(self.MIXT[:, hp, :], ps[:, 0:256], self.ppc(l, 'M_NORM', hp), self.GOZ[:, hp, :], ALU.mult, ALU.mult)

    def stage_lru(self, l, t0, co, W, w0, w1, T, dirs, prompt, seq_idx, first, last, mode='full'):
        S = self.S
        do_f = 0 in dirs
        if mode != 'load':
            for pr in range(2):
                self.proj_fm(3736 + pr * 128, 128, 0, W, lambda ps, pr=pr: self.cp('act', self.XL[:, pr, 2:2 + W], ps))
                if do_f or mode == 'store':
                    self.proj_fm(3992 + pr * 128, 128, co, 256, lambda ps, pr=pr: self.act(self.LZT[:, pr, :], ps, AF.Silu))
            if w1 == T:
                self.memset('dve', self.XL[:, :, 2 + W:2 + W + 1], 0.0)
            if w0 == 0:
                self.memset('dve', self.XL[:, :, 0:2], 0.0)
            for pr in range(2):
                self.ts('dve', self.XC[:, pr, :], self.XL[:, pr, co:co + 256], self.ppc(l, 'L_CONV', 0 * 2 + pr), self.ppc(l, 'L_CONVB', pr), ALU.mult, ALU.add)
                for j in range(1, 4):
                    self.stt(self.XC[:, pr, :], self.XL[:, pr, co + j:co + j + 256], self.ppc(l, 'L_CONV', j * 2 + pr), self.XC[:, pr, :], ALU.mult, ALU.add)
        bi = t0 // 256
        lr_items = [(self.XC[:, :, :], 'XC'), (self.LZT[:, :, :], 'LZT')]
        if mode == 'store':
            for ap_, k_ in lr_items:
                S.dma('sp', self.sc[k_][bi], ap_)
        if mode == 'load':
            for ap_, k_ in lr_items:
                S.dma('sp', ap_, self.sc[k_][bi])
        for d in dirs:
            if first[d]:
                if prompt:
                    self.memset('dve', self.lS[d][:, :], 0.0)
                else:
                    S.dma('sp', self.lS[d][:, :], self.st_l[:, l, d, :])
        if do_f and not (1 in dirs):
            S.dma('sp', self.HL[1][:, :, :], self.sLB[:, :, t0:t0 + 256])
        lt = self.lt
        for d in sorted(dirs, reverse=True):
            for pr in range(2):
                ps = self.pq()
                self.mm(ps[:, 0:256], self.LRUW[:, l, (0 * 2 + d) * 2 + pr, :], self.XC[:, pr, :])
                self.mm(ps[:, 256:512], self.LRUW[:, l, (1 * 2 + d) * 2 + pr, :], self.XC[:, pr, :])
                self.act(lt["rg"][:, pr, :], ps[:, 0:256], AF.Sigmoid, bias=self.ppc(l, 'L_BA', d * 2 + pr))
                self.act(lt["ig"][:, pr, :], ps[:, 256:512], AF.Sigmoid, bias=self.ppc(l, 'L_BX', d * 2 + pr))
                self.act(lt["aa"][:, pr, :], lt["rg"][:, pr, :], AF.Exp, scale=self.dvc(l, 'CLAM', d * 2 + pr))
                self.act(lt["a2"][:, pr, :], lt["rg"][:, pr, :], AF.Exp, scale=self.dvc(l, 'C2LAM', d * 2 + pr))
                self.ts('dve', lt["a2"][:, pr, :], lt["a2"][:, pr, :], -1.0, 1.0, ALU.mult, ALU.add)
                self.act(lt["a2"][:, pr, :], lt["a2"][:, pr, :], AF.Sqrt)
                self.tt('dve', lt["bt"][:, pr, :], lt["a2"][:, pr, :], lt["ig"][:, pr, :], ALU.mult)
                self.tt('dve', lt["bt"][:, pr, :], lt["bt"][:, pr, :], self.XC[:, pr, :], ALU.mult)
                if d == 0:
                    self.scan(self.HL[0][:, pr, :], lt["aa"][:, pr, :], lt["bt"][:, pr, :], self.lS[0][:, pr:pr + 1], ALU.mult, ALU.add)
                    self.cp('dve', self.lS[0][:, pr:pr + 1], self.HL[0][:, pr, 255:256])
                else:
                    rv = lambda t: self.V(t, 0, 128, pr * 256 + 255, [[-1, 256]])
                    self.scan(rv(self.HL[1]), rv(lt["aa"]), rv(lt["bt"]), self.lS[1][:, pr:pr + 1], ALU.mult, ALU.add)
                    self.cp('dve', self.lS[1][:, pr:pr + 1], self.HL[1][:, pr, 0:1])
            if last[d] and prompt:
                S.dma('sp', self.o_l[seq_idx, l, d], self.lS[d][:, :])
        if not do_f:
            S.dma('sp', self.sLB[:, :, t0:t0 + 256], self.HL[1][:, :, :])
            return
        self.tt('dve', self.HL[0][:, :, :], self.HL[0][:, :, :], self.HL[1][:, :, :], ALU.add)
        self.tt('dve', self.MIXT[:, 6:8, :], self.HL[0][:, :, :], self.LZT[:, :, :], ALU.mult)

    def stage_rwkv(self, l, t0, co, W, w0, w1, T, dirs, grid, prompt, seq_idx, first, last, mode='full'):
        S = self.S
        do_f = 0 in dirs
        if mode != 'load':
            for ch in range(11):
                self.proj_fm(1944 + ch * 128, 128, 0, W, lambda ps, ch=ch: self.cp('act' if ch % 2 else 'dve', self.URS[:, ch, 0:W], ps))
            if do_f or mode == 'store':
                for hp in range(3):
                    self.proj_fm(3352 + hp * 128, 128, co, 256, lambda ps, hp=hp: self.act(self.RZT[:, hp, :], ps, AF.Silu))
            U3 = lambda off, n: self.V(self.URS, 0, 128, off, [[384, 11], [1, n]])
            B3 = lambda off, n: self.V(self.BLK, 0, 128, off, [[256, 11], [1, n]])
            if not grid:
                self.cp('dve', B3(1, 255), U3(0, 255))
                self.memset('dve', B3(0, 1), 0.0)
                self.tt('dve', B3(0, 255), B3(0, 255), U3(1, 255), ALU.add)
                wsh = 0.5
            else:
                U4 = lambda off, r, n: self.V(self.URS, 0, 128, off, [[384, 11], [64, r], [1, n]])
                B4 = lambda off, r, n: self.V(self.BLK, 0, 128, off, [[256, 11], [64, r], [1, n]])
                self.cp('dve', B4(1, 4, 63), U4(co, 4, 63))
                self.memset('dve', B4(0, 4, 1), 0.0)
                self.tt('dve', B4(0, 4, 63), B4(0, 4, 63), U4(co + 1, 4, 63), ALU.add)
                if t0 > 0:
                    self.tt('dve', B3(0, 256), B3(0, 256), U3(co - 64, 256), ALU.add)
                else:
                    self.tt('dve', B3(64, 192), B3(64, 192), U3(0, 192), ALU.add)
                if t0 + TB < T:
                    self.tt('dve', B3(0, 256), B3(0, 256), U3(co + 64, 256), ALU.add)
                else:
                    self.tt('dve', B3(0, 192), B3(0, 192), U3(co + 64, 192), ALU.add)
                wsh = 0.25
            mu = self.V(self.PP, 0, 128, l * PL + PC['R_MU'], [[1, 11], [0, 256]])
            self.tt('dve', B3(0, 256), B3(0, 256), mu, ALU.mult)
            omm = self.V(self.DV, 0, 128, l * DL + DC['OMMU'], [[1, 11], [0, 256]])
            self.tt('dve', U3(co, 256), U3(co, 256), omm, ALU.mult)
            self.stt(B3(0, 256), B3(0, 256), wsh, U3(co, 256), ALU.mult, ALU.add)
            for nm, ch in (('blk_r', 0), ('blk_k', 3), ('blk_v', 6), ('blk_wl', 9), ('blk_al', 10)):
                self.dump(nm, self.BLK[:, ch, :])
            rt = self.rt
            kk = self.V(self.PP, 0, 128, l * PL + PC['R_KK'], [[1, 3], [0, 256]])
            kap = rt["d"]
            self.tt('dve', kap[:, :, :], self.BLK[:, 3:6, :], kk, ALU.mult)
            ksq = rt["E"]
            self.tt('dve', ksq[:, :, :], kap[:, :, :], kap[:, :, :], ALU.mult)
            for hp in range(3):
                self.mm(self.PA[:, hp * 256:(hp + 1) * 256], self.CST[:, CC['BONES']:CC['BONES'] + 128], ksq[:, hp, :])
            self.act(ksq[:, :, :], self.V(self.PA, 0, 128, 0, [[256, 3], [1, 256]]), AF.Sqrt)
            self.ts('dve', ksq[:, :, :], ksq[:, :, :], 1e-12, None, ALU.max)
            self.recip(ksq[:, :, :], ksq[:, :, :])
            self.tt('dve', self.KH[:, :, :], kap[:, :, :], ksq[:, :, :], ALU.mult)
            self.dump('kh', self.KH[:, 0, :])
            self.bd_fill(self.BDV, lambda par: self.V(self.BLK, par * 64, 64, 6 * 256, [[256, 3], [64, 4], [1, 64]]))
            for c in range(4):
                for hp in range(3):
                    self.mm(self.PB[:, (c * 3 + hp) * 64:(c * 3 + hp + 1) * 64], self.BDV[:, hp, c, :], self.ISTKB[:, :])
            self.cp('act', self.V(self.VSTK, 0, 128, 0, [[1, 768]]), self.PB[:, 0:768])
        bi = t0 // 256
        rw_items = [(self.BLK[:, :, :], 'BLK'), (self.RZT[:, :, :], 'RZT'), (self.KH[:, :, :], 'KH'), (self.VSTK[:, :, :, :], 'VSTK')]
        if mode == 'store':
            for ap_, k_ in rw_items:
                S.dma('sp', self.sc[k_][bi], ap_)
        if mode == 'load':
            for ap_, k_ in rw_items:
                S.dma('sp', ap_, self.sc[k_][bi])
        for d in dirs:
            if first[d]:
                if prompt:
                    self.memset('dve', self.rH[d][:, :, :], 0.0)
                else:
                    S.dma('sp', self.rH[d][:, :, :], self.st_rH[l, d])
        ybflat = self.V(self.YB, 0, 128, 0, [[1, 768]])
        if do_f and not (1 in dirs):
            S.dma('sp', ybflat, self.sYB[t0 // 256])
        for d in sorted(dirs, reverse=True):
            self.rwkv_dir(l, d, seq_idx, prompt, last)
        if not do_f:
            S.dma('sp', self.sYB[t0 // 256], ybflat)
            return
        ft = self.ft
        self.tt('dve', self.YSQ[:, :, :, :], self.YB[:, :, :, :], self.YB[:, :, :, :], ALU.mult)
        self.reduce(self.rSSQ[:, :], self.V(self.YSQ, 0, 128, 0, [[64, 12], [1, 64]]), ALU.add)
        self.ts('dve', self.rSSQ[:, :], self.rSSQ[:, :], 1.0 / 64, EPS, ALU.mult, ALU.add)
        self.act(self.rSSQ[:, :], self.rSSQ[:, :], AF.Sqrt)
        self.recip(self.rRSTD[:, :], self.rSSQ[:, :])
        self.tt('dve', ft["rk"][:, :, :], self.BLK[:, 0:3, :], self.BLK[:, 3:6, :], ALU.mult)
        for hp in range(3):
            self.mm(self.PB[:, hp * 256:(hp + 1) * 256], self.RKD[:, l, hp, :], ft["rk"][:, hp, :])
        self.tt('dve', ft["bon"][:, :, :], self.V(self.PB, 0, 128, 0, [[256, 3], [1, 256]]), self.BLK[:, 6:9, :], ALU.mult)
        self.memset('pool', self.YBD[:, :, :], 0.0)
        for c in range(4):
            for par in range(2):
                self.tt('dve', self.V(self.YBD, par * 64, 64, par * 64, [[128, 3], [1, 64]]),
                        self.V(self.YB, par * 64, 64, c * 192, [[64, 3], [1, 64]]),
                        self.V(self.rRSTD, par * 64, 64, c * 3, [[1, 3], [0, 64]]), ALU.mult)
            for hp in range(3):
                self.mm(self.PA[:, hp * 256 + c * 64: hp * 256 + (c + 1) * 64], self.YBD[:, hp, :],
                        self.CST[:, CC['ISTK']:CC['ISTK'] + 64])
        for hp in range(3):
            self.stt(ft["t1"][:, hp, :], self.PA[:, hp * 256:(hp + 1) * 256], self.ppc(l, 'R_NORM', hp), ft["bon"][:, hp, :], ALU.mult, ALU.add)
            self.tt('dve', self.MIXT[:, 3 + hp, :], ft["t1"][:, hp, :], self.RZT[:, hp, :], ALU.mult)

    def bd_fill(self, bd, src_of_par, eng='pool'):
        self.memset(eng, bd[:, :, :, :], 0.0)
        for par in range(2):
            self.cp(eng, self.V(bd, par * 64, 64, par * 64, [[512, 3], [128, 4], [1, 64]]), src_of_par(par))

    def rwkv_dir(self, l, d, seq_idx, prompt, last):
        S = self.S
        rt = self.rt
        pb_d = 64 * d
        f3 = lambda t: t[:, :, :]
        v4 = lambda t: self.V(t, 0, 128, 0, [[256, 3], [64, 4], [1, 64]])
        self.act(self.TWL[pb_d:pb_d + 64, :], self.BLK[pb_d:pb_d + 64, 9, :], AF.Tanh)
        for hp in range(3):
            self.mm(self.PA[:, hp * 256:(hp + 1) * 256], self.LORA[pb_d:pb_d + 64, l, 0, hp * 128:(hp + 1) * 128], self.TWL[pb_d:pb_d + 64, :])
        for hp in range(3):
            self.act(rt["sg"][:, hp, :], self.PA[:, hp * 256:(hp + 1) * 256], AF.Sigmoid, bias=self.ppc(l, 'R_W0', d * 3 + hp))
        for hp in range(3):
            self.mm(self.PB[:, hp * 256:(hp + 1) * 256], self.LORA[pb_d:pb_d + 64, l, 1, hp * 128:(hp + 1) * 128], self.BLK[pb_d:pb_d + 64, 10, :])
        for hp in range(3):
            self.act(rt["aa"][:, hp, :], self.PB[:, hp * 256:(hp + 1) * 256], AF.Sigmoid, bias=self.ppc(l, 'R_A0', d * 3 + hp))
        for hp in range(3):
            self.ts('dve', rt["kt"][:, hp, :], rt["aa"][:, hp, :], self.ppc(l, 'R_KA', hp), self.dvc(l, 'OMKA', hp), ALU.mult, ALU.add)
        self.tt('dve', f3(rt["kt"]), f3(rt["kt"]), self.BLK[:, 3:6, :], ALU.mult)
        self.tt('dve', f3(rt["bb"]), self.KH[:, :, :], f3(rt["aa"]), ALU.mult)
        flat = lambda t: self.V(t, 0, 128, 0, [[1, 768]])
        self.scan(flat(rt["cs"]), self.CST[:, CC['RMASK']:CC['RMASK'] + 768], flat(rt["sg"]), 0.0, ALU.mult, ALU.add)
        self.cp('dve', self.GL[:, :, :], self.V(rt["cs"], 0, 128, 63, [[256, 3], [64, 4]]))
        if d == 1:
            csLb = self.V(rt["cs"], 0, 128, 63, [[256, 3], [64, 4], [0, 64]])
            self.tt('dve', v4(rt["d"]), csLb, v4(rt["cs"]), ALU.subtract)
            self.tt('dve', f3(rt["cs"]), f3(rt["d"]), f3(rt["sg"]), ALU.add)
        self.act(f3(rt["E"]), f3(rt["cs"]), AF.Exp, scale=-DSC)
        self.tt('dve', self.V(self.KR, 0, 128, 64, [[512, 3], [128, 4], [1, 64]]),
                self.V(self.BLK, 0, 128, 0, [[256, 3], [64, 4], [1, 64]]), v4(rt["E"]), ALU.mult)
        self.tt('dve', f3(rt["d"]), f3(rt["cs"]), f3(rt["sg"]), ALU.subtract)
        self.act(f3(rt["E"]), f3(rt["d"]), AF.Exp, scale=-DSC)
        self.tt('dve', self.V(self.KR, 0, 128, 0, [[512, 3], [128, 4], [1, 64]]), v4(self.KH), v4(rt["E"]), ALU.mult)
        self.act(f3(rt["E"]), f3(rt["cs"]), AF.Exp, scale=DSC)
        self.tt('dve', self.BTT[:, :, :], f3(rt["bb"]), f3(rt["E"]), ALU.mult)
        self.tt('dve', self.KTT[:, :, :], f3(rt["kt"]), f3(rt["E"]), ALU.mult)
        self.act(self.GL[:, :, :], self.GL[:, :, :], AF.Exp, scale=-DSC)
        BD = self.BD
        self.memset('pool', self.A1BD[:, :, :, :], 0.0)
        self.memset('pool', self.A2BD[:, :, :, :], 0.0)
        self.memset('pool', self.NN[0][:, :, :], 0.0)
        c4 = lambda t, par: self.V(t, par * 64, 64, 0, [[256, 3], [64, 4], [1, 64]])
        self.bd_fill(BD["kt"], lambda par: c4(self.KTT, par))
        self.bd_fill(BD["b"], lambda par: c4(self.BTT, par))
        self.bd_fill(BD["kh"], lambda par: self.V(self.KR, par * 64, 64, 0, [[512, 3], [128, 4], [1, 64]]))
        self.bd_fill(BD["r"], lambda par: self.V(self.KR, par * 64, 64, 64, [[512, 3], [128, 4], [1, 64]]))
        for (src, dst, neg) in ((BD["kt"], self.KTTOK, False), (BD["b"], self.BTTOK, True)):
            for half in range(2):
                for cc_ in range(2):
                    c = half * 2 + cc_
                    for hp in range(3):
                        self.tr(self.PTB[:, (cc_ * 3 + hp) * 128:(cc_ * 3 + hp + 1) * 128], src[:, hp, c, :], bf=True)
                o = self.V(dst, 0, 128, half * 768, [[1, 768]])
                if neg:
                    self.act(o, self.PTB[:, 0:768], AF.Copy, scale=-1.0)
                else:
                    self.cp('dve', o, self.PTB[:, 0:768])
        self.cp('act', self.HBF[:, :, :], self.rH[d][:, :, :])
        mk = CC['MKF'] if d == 0 else CC['MKB']
        nmk = CC['NMKF'] if d == 0 else CC['NMKB']
        mn = CC['MNF'] if d == 0 else CC['MNB']
        for j in range(4):
            c = j if d == 0 else 3 - j
            cs = slice(c * 64, (c + 1) * 64)
            KRc = lambda hp: self.V(self.KR, 0, 128, hp * 512 + c * 128, [[1, 128]])
            p1, p2, p3 = self.pq(), self.pq(), self.pq()
            for hp in range(3):
                self.mm(p1[:, hp * 128:(hp + 1) * 128], BD["kt"][:, hp, c, :], KRc(hp), inc=(hp == 2))
            for hp in range(3):
                self.mm(p2[:, hp * 128:(hp + 1) * 128], BD["b"][:, hp, c, :], KRc(hp), inc=(hp == 2))
            for hp in range(3):
                self.mm(p3[:, hp * 64:(hp + 1) * 64], BD["kh"][:, hp, c, :], self.BTT[:, hp, cs], inc=(hp == 2))
            for par in range(2):
                pp_ = par * 64
                self.tt('dve', self.V(self.A1BD, pp_, 64, pp_, [[256, 3], [128, 2], [1, 64]]),
                        self.V(p1, pp_, 64, 0, [[128, 3], [64, 2], [1, 64]]),
                        self.V(self.CST, pp_, 64, mk, [[0, 3], [64, 2], [1, 64]]), ALU.mult)
                self.tt('dve', self.V(self.A2BD, pp_, 64, pp_, [[256, 3], [128, 2], [1, 64]]),
                        self.V(p2, pp_, 64, 0, [[128, 3], [64, 2], [1, 64]]),
                        self.V(self.CST, pp_, 64, nmk, [[0, 3], [64, 2], [1, 64]]), ALU.mult)
                self.tt('dve', self.V(self.NN[0], pp_, 64, pp_, [[128, 3], [1, 64]]),
                        self.V(p3, pp_, 64, 0, [[64, 3], [1, 64]]),
                        self.V(self.CST, pp_, 64, mn, [[0, 3], [1, 64]]), ALU.mult)
            pr_ = self.pq()
            for hp in range(3):
                o = pr_[:, hp * 64:(hp + 1) * 64]
                self.mm(o, BD["kh"][:, hp, c, :], self.HBF[:, hp, :], start=True, stop=False)
                self.mm(o, self.A1BD[:, hp, 0, :], self.VSTK[:, c, hp, :], start=False, stop=True, inc=(hp == 2))
            u192 = self.V(self.U, 0, 128, 0, [[1, 192]])
            ub192 = self.V(self.UBF, 0, 128, 0, [[1, 192]])
            self.cp('dve', u192, pr_[:, 0:192])
            self.cp('act', ub192, u192)
            for k in range(6):
                NTk = self.A2BD[:, :, 0, :] if k == 0 else self.NTT[k % 2][:, :, :]
                Nk = self.NN[k % 2]
                pu = self.pq()
                for hp in range(3):
                    self.mm(pu[:, hp * 64:(hp + 1) * 64], NTk[:, hp, :], self.UBF[:, hp, :], inc=(hp == 2))
                if k < 5:
                    pnt = self.pq()
                    for hp in range(3):
                        self.mm(pnt[:, hp * 128:(hp + 1) * 128], Nk[:, hp, :], NTk[:, hp, :], inc=(hp == 2))
                    pn = self.pq()
                    for hp in range(3):
                        self.mm(pn[:, hp * 128:(hp + 1) * 128], NTk[:, hp, :], Nk[:, hp, :], inc=(hp == 2))
                self.tt('dve', ub192, u192, pu[:, 0:192], ALU.add)
                if k < 5:
                    self.cp('act', self.V(self.NTT[(k + 1) % 2], 0, 128, 0, [[1, 384]]), pnt[:, 0:384])
                    self.cp('act', self.V(self.NN[(k + 1) % 2], 0, 128, 0, [[1, 384]]), pn[:, 0:384])
                    self.tt('dve', u192, u192, pu[:, 0:192], ALU.add)
            py = self.pq()
            for hp in range(3):
                o = py[:, hp * 64:(hp + 1) * 64]
                self.mm(o, BD["r"][:, hp, c, :], self.HBF[:, hp, :], start=True, stop=False)
                self.mm(o, self.A1BD[:, hp, 1, :], self.VSTK[:, c, hp, :], start=False, stop=False)
                self.mm(o, self.A2BD[:, hp, 1, :], self.UBF[:, hp, :], start=False, stop=True, inc=(hp == 2))
            ybv = self.V(self.YB, 0, 128, c * 192, [[1, 192]])
            if d == 1:
                self.cp('act', ybv, py[:, 0:192])
            else:
                self.tt('dve', ybv, ybv, py[:, 0:192], ALU.add)
            ph = self.pq()
            for hp in range(3):
                o = ph[:, hp * 64:(hp + 1) * 64]
                self.mm(o, self.KTTOK[:, c, hp, :], self.VSTK[:, c, hp, :], start=True, stop=False)
                self.mm(o, self.BTTOK[:, c, hp, :], self.UBF[:, hp, :], start=False, stop=True, inc=(hp == 2))
            self.tt('dve', self.HTMP[:, :, :], self.rH[d][:, :, :], self.V(ph, 0, 128, 0, [[64, 3], [1, 64]]), ALU.add)
            self.tt('dve', self.rH[d][:, :, :], self.HTMP[:, :, :], self.V(self.GL, 0, 128, c, [[4, 3], [0, 64]]), ALU.mult)
            self.cp('act', self.HBF[:, :, :], self.rH[d][:, :, :])
        if last[d] and prompt:
            S.dma('sp', self.o_rH[seq_idx, l, d], self.rH[d][:, :, :])

    def stage_out(self, l, mod, xsrc, xdst, r0):
        S = self.S
        for tt_ in range(2):
            o, xw = self.XN[tt_], self.XW[tt_]
            S.dma('sp', xw[:, :], xsrc[r0 + tt_ * 128: r0 + (tt_ + 1) * 128, :])
            for ch in range(2):
                ps = self.pq()
                for kc in range(8):
                    self.mm(ps[:, :], self.MIXT[:, kc, tt_ * 128:(tt_ + 1) * 128], self.W_OUT[:, kc, ch * 512:(ch + 1) * 512],
                            start=(kc == 0), stop=(kc == 7))
                self.cp('act' if ch else 'dve', o[:, ch * 512:(ch + 1) * 512], ps[:, :])
            self.dump('o_proj%d' % tt_, o[:, :])
            self.dump('o_x%d' % tt_, xw[:, :])
            ssq = self.TMPS[:, 24 + tt_:25 + tt_]
            junk = self.V(self.BLK, 0, 128, 0, [[1, D]])
            self.act(junk, o[:, :], AF.Square, accum=ssq)
            rs = self.TMPS[:, 26 + tt_:27 + tt_]
            self.ts('dve', rs, ssq, 1.0 / D, EPS, ALU.mult, ALU.add)
            self.act(rs, rs, AF.Sqrt)
            self.recip(rs, rs)
            self.dump('o_rs%d' % tt_, rs)
            self.stt(o[:, :], o[:, :], rs, self.GATEB[:, mod, :], ALU.mult, ALU.mult)
            self.dump('o_g%d' % tt_, o[:, :])
            self.tt('dve', o[:, :], o[:, :], xw[:, :], ALU.add)
            S.dma('sp', xdst[r0 + tt_ * 128: r0 + (tt_ + 1) * 128, :], o[:, :])

    def build(self, layers=(0, 1)):
        try:
            self._build(layers)
        except StopIteration:
            pass
        self.S.finish('sp')
        return self.nc

    def _build(self, layers):
        NP, TS = self.NP, self.TS
        self.setup()
        if self.stop == 'setup':
            raise StopIteration
        for li, l in enumerate(layers):
            self.load_layer(l)
            if self.stop == 'load':
                raise StopIteration
            xsrc = self.x_in if li == 0 else self.x1
            xdst = self.y_out if li == len(layers) - 1 else self.x1
            T_, F_ = {0: True, 1: True}, {0: False, 1: False}
            for s in range(NP):
                self.visit(l, 0, xsrc, xdst, s * 256, 256, 0, [0, 1], False, s, T_, T_)
            if TS > 0:
                nb = TS // TB
                row0 = NP * 256
                for b in range(nb - 1, -1, -1):
                    self.visit(l, 1, xsrc, xdst, row0, TS, b * TB, [1], True, 0,
                               {0: False, 1: b == nb - 1}, {0: False, 1: b == 0}, mode='store')
                for b in range(nb):
                    self.visit(l, 1, xsrc, xdst, row0, TS, b * TB, [0], True, 0,
                               {0: b == 0, 1: False}, {0: b == nb - 1, 1: False}, mode='load')


def prep_shared(inp):
    f = lambda a: np.ascontiguousarray(np.asarray(a, dtype=np.float32))
    b_mod = f(inp['b_mod'])
    sh = {}
    sh['w_mod'] = f(inp['w_mod'])
    sh['w_in'] = f(inp['w_in'])
    sh['w_out'] = f(inp['w_out'])
    sh['bmodT'] = f(b_mod[:, :2048].reshape(2, 16, 128).transpose(2, 0, 1))
    sh['bmodg'] = f(b_mod[:, 2048:3072])
    sh['gpost'] = f(inp['g_post'])
    pp = np.zeros((128, 2 * PL), np.float32)
    for l in range(2):
        o = l * PL
        def put(key, arr, n):
            pp[:, o + PC[key]: o + PC[key] + n] = np.asarray(arr, np.float32).reshape(n, 128).T
        put('G_PRE', inp['g_pre'][l], 8)
        put('M_NORM', inp['m_norm'][l], 3)
        put('R_MU', inp['r_mu'][l], 11)
        put('R_W0', np.asarray(inp['r_w0'][l]).reshape(-1), 6)
        put('R_A0', np.asarray(inp['r_a0'][l]).reshape(-1), 6)
        put('R_KK', inp['r_kk'][l], 3)
        put('R_KA', inp['r_ka'][l], 3)
        put('R_RK', inp['r_rk'][l], 3)
        put('R_NORM', inp['r_norm'][l], 3)
        put('L_CONV', np.asarray(inp['l_conv'][l]).reshape(-1), 8)
        put('L_CONVB', inp['l_conv_b'][l], 2)
        put('L_BA', np.asarray(inp['l_ba'][l]).reshape(-1), 4)
        put('L_BX', np.asarray(inp['l_bx'][l]).reshape(-1), 4)
        put('L_LAM', np.asarray(inp['l_lambda'][l]).reshape(-1), 4)
        pp[0:6, o + PC['M_BI']: o + PC['M_BI'] + 2] = np.asarray(inp['m_bi'][l], np.float32).T
        pp[0:6, o + PC['M_BF']: o + PC['M_BF'] + 2] = np.asarray(inp['m_bf'][l], np.float32).T
    sh['pp'] = pp
    sh['cst'] = make_consts()
    lora = np.zeros((128, 2, 2, 384), np.float32)
    for wi, key in enumerate(['r_w2', 'r_a2']):
        a = np.asarray(inp[key], np.float32)
        lora[:, :, wi, :] = a.transpose(1, 2, 0, 3).reshape(128, 2, 384)
    sh['lora'] = lora
    lruw = np.zeros((128, 2, 8, 128), np.float32)
    for gi, key in enumerate(['l_wa', 'l_wx']):
        a = np.asarray(inp[key], np.float32)
        for l in range(2):
            for d in range(2):
                for pr in range(2):
                    for hb in range(2):
                        n = 2 * pr + hb
                        lruw[hb * 64:(hb + 1) * 64, l, (gi * 2 + d) * 2 + pr, hb * 64:(hb + 1) * 64] = a[l, d, n]
    sh['lruw'] = lruw
    return sh


def prep_core(inp, b, NP, TS):
    f = lambda a: np.ascontiguousarray(np.asarray(a, dtype=np.float32))
    m = {}
    xp = np.asarray(inp['x_prompt'], np.float32)[b * NP:(b + 1) * NP].reshape(NP * 256, D)
    if TS > 0:
        xs = np.asarray(inp['x_sample'], np.float32)[b]
        m['x_in'] = f(np.concatenate([xp, xs], 0))
    else:
        m['x_in'] = f(xp)
    cc = np.stack([np.asarray(inp['c_ctx'], np.float32), np.asarray(inp['c'], np.float32)[b]], -1)
    m['cc'] = f(cc.reshape(8, 128, 2).transpose(1, 0, 2))
    C = np.asarray(inp['state_mlstm_C'], np.float32)[b]
    n = np.asarray(inp['state_mlstm_n'], np.float32)[b]
    Cn = np.concatenate([C, n[..., None]], -1)
    Cn = Cn.reshape(2, 2, 3, 2, 64, 65).transpose(0, 1, 3, 4, 2, 5).reshape(2, 2, 128, 3, 65)
    m['st_mC'] = f(Cn)
    m['st_mm'] = f(np.asarray(inp['state_mlstm_m'], np.float32)[b].transpose(2, 0, 1))
    R = np.asarray(inp['state_rwkv'], np.float32)[b]
    R = R.transpose(0, 1, 2, 4, 3)
    R = R.reshape(2, 2, 3, 2, 64, 64).transpose(0, 1, 3, 4, 2, 5).reshape(2, 2, 128, 3, 64)
    m['st_rH'] = f(R)
    L = np.asarray(inp['state_rglru'], np.float32)[b]
    m['st_l'] = f(L.reshape(2, 2, 2, 128).transpose(3, 0, 1, 2))
    return m


def unpack_core(r, NP, TS):
    y = r['y_out']
    yp = y[:NP * 256].reshape(NP, 256, D)
    ys = y[NP * 256:]
    mC = r['o_mC'].reshape(NP, 2, 2, 2, 64, 3, 65).transpose(0, 1, 2, 5, 3, 4, 6).reshape(NP, 2, 2, 6, 64, 65)
    newC = np.ascontiguousarray(mC[..., :64])
    newn = np.ascontiguousarray(mC[..., 64])
    newm = r['o_mm'].reshape(NP, 2, 2, 6)
    rH = r['o_rH'].reshape(NP, 2, 2, 2, 64, 3, 64).transpose(0, 1, 2, 5, 3, 4, 6).reshape(NP, 2, 2, 6, 64, 64)
    newr = np.ascontiguousarray(rH.transpose(0, 1, 2, 3, 5, 4))
    newl = np.ascontiguousarray(r['o_l'].transpose(0, 1, 2, 4, 3).reshape(NP, 2, 2, 256))
    return yp, ys, newC, newn, newm, newr, newl


_NC_CACHE = {}


def kernel(**inputs):
    NP, TS = 4, 2048
    key = (NP, TS)
    if key not in _NC_CACHE:
        _NC_CACHE[key] = Builder(NP, TS).build()
    nc = _NC_CACHE[key]
    sh = prep_shared(inputs)
    in_maps = []
    for b in range(NCORES):
        m = dict(sh)
        m.update(prep_core(inputs, b, NP, TS))
        in_maps.append(m)
    res = run_bass_kernel_spmd(nc, in_maps, core_ids=list(range(NCORES)))
    outs = [unpack_core(r, NP, TS) for r in res.results]
    y_prompt = np.concatenate([o[0] for o in outs], 0)
    y_sample = np.stack([o[1] for o in outs], 0)
    cat = lambda i: np.concatenate([o[i] for o in outs], 0)
    return (y_prompt.astype(np.float32), y_sample.astype(np.float32), cat(2).astype(np.float32),
            cat(3).astype(np.float32), cat(4).astype(np.float32), cat(5).astype(np.float32), cat(6).astype(np.float32))
```

```python
import numpy as np
import concourse.bass as bass
import concourse.mybir as mybir
from concourse.bass_utils import run_bass_kernel_spmd

F32 = mybir.dt.float32
BF16 = mybir.dt.bfloat16
AF = mybir.ActivationFunctionType
ALU = mybir.AluOpType
AX = mybir.AxisListType

D = 1024
IN_COLS = 4248
EPS = 1e-6
DSC = 0.6065306597126334
TB = 256
NCORES = 8


def _prod(xs):
    r = 1
    for x in xs:
        r *= int(x)
    return r


class Sync:
    def __init__(self, nc, n_dma_sems=32):
        self.nc = nc
        self.engs = {'pe': nc.tensor, 'dve': nc.vector, 'act': nc.scalar,
                     'pool': nc.gpsimd, 'sp': nc.sync}
        self.sem = {}
        self.cnt = {}
        for e in ['pe', 'dve', 'act', 'pool']:
            self.sem[e] = nc.alloc_semaphore('sem_' + e)
            self.cnt[e] = 0
        self.seen = {e: {} for e in self.engs}
        self.dma_ring = [nc.alloc_semaphore('dq_%d' % i) for i in range(n_dma_sems)]
        self.dma_uses = [0] * n_dma_sems
        self.dma_next = 0
        self.rec = {}
        self.untracked = set()
        self.n_wait = 0
        self.n_ins = 0
        self.pstep_cache = {}
        self.sb_addr = {}

    def region(self, ap):
        t = ap.tensor
        name = t.name
        apl = [(int(s), int(c)) for (s, c) in ap.ap]
        off = int(ap.offset)
        if type(t).__name__.startswith('DRam'):
            lo = off + sum(min(0, s * (c - 1)) for s, c in apl)
            hi = off + sum(max(0, s * (c - 1)) for s, c in apl) + 1
            return (name, 0, 1, lo, hi)
        pstep = self.pstep_cache.get(name)
        if pstep is None:
            pstep = _prod(list(t.shape)[1:])
            self.pstep_cache[name] = pstep
        p0 = off // pstep
        f0 = off % pstep
        npart = apl[0][1]
        rest = apl[1:]
        lo = f0 + sum(min(0, s * (c - 1)) for s, c in rest)
        hi = f0 + sum(max(0, s * (c - 1)) for s, c in rest) + 1
        if name in self.sb_addr:
            base, es = self.sb_addr[name]
            return ('SB', p0, p0 + npart, base + lo * es, base + hi * es)
        return ('PS:' + name, (p0 // 32) * 32, ((p0 + npart + 31) // 32) * 32, (lo // 512) * 512, ((hi + 511) // 512) * 512)

    @staticmethod
    def _ovl(a, b):
        return a[1] < b[2] and b[1] < a[2] and a[3] < b[4] and b[3] < a[4]

    @staticmethod
    def _contains(a, b):
        return a[1] <= b[1] and b[2] <= a[2] and a[3] <= b[3] and b[4] <= a[4]

    def _collect(self, e, reads, writes):
        deps = {}
        own = self.sem.get(e)
        rregs = [self.region(a) for a in reads]
        wregs = [self.region(a) for a in writes]
        for r in rregs:
            if r[0] in self.untracked:
                continue
            isps = r[0].startswith('PS:')
            for (reg, kind, sem, val) in self.rec.get(r[0], ()):
                if (kind == 'w' or (isps and sem is not own)) and self._ovl(reg, r):
                    if e == 'pe' and sem is own:
                        continue
                    k = id(sem)
                    if deps.get(k, (None, 0))[1] < val:
                        deps[k] = (sem, val)
        for w in wregs:
            if w[0] in self.untracked:
                continue
            for (reg, kind, sem, val) in self.rec.get(w[0], ()):
                if self._ovl(reg, w):
                    if sem is own:
                        continue
                    k = id(sem)
                    if deps.get(k, (None, 0))[1] < val:
                        deps[k] = (sem, val)
        return deps, rregs, wregs

    def _record(self, rregs, wregs, sem, val):
        for r in rregs:
            if r[0] in self.untracked:
                continue
            lst = self.rec.setdefault(r[0], [])
            lst[:] = [x for x in lst if not (x[1] == 'r' and x[2] is sem and self._contains(r, x[0]))]
            lst.append((r, 'r', sem, val))
        for w in wregs:
            if w[0] in self.untracked:
                continue
            lst = self.rec.setdefault(w[0], [])
            lst[:] = [x for x in lst if not self._contains(w, x[0])]
            lst.append((w, 'w', sem, val))

    def wait(self, e, sem, val):
        k = id(sem)
        if self.seen[e].get(k, 0) >= val:
            return
        self.engs[e].wait_ge(sem, val)
        self.seen[e][k] = val
        self.n_wait += 1

    max_ins = None
    paranoid = False

    def emit(self, e, reads, writes, build, inc=True):
        if self.max_ins is not None and self.n_ins >= self.max_ins:
            raise StopIteration
        deps, rregs, wregs = self._collect(e, reads, writes)
        for (sem, val) in deps.values():
            self.wait(e, sem, val)
        if self.paranoid:
            for e2 in ['pe', 'dve', 'act', 'pool']:
                if self.cnt[e2] > 0 and not (e == 'pe' and e2 == 'pe'):
                    self.wait(e, self.sem[e2], self.cnt[e2])
        ins = build(self.engs[e])
        self.n_ins += 1
        if inc:
            self.cnt[e] += 1
            ins.then_inc(self.sem[e], 1)
            val = self.cnt[e]
        else:
            val = self.cnt[e] + 1
        self._record(rregs, wregs, self.sem[e], val)
        return ins

    def dma(self, q, out, in_, **kw):
        if self.max_ins is not None and self.n_ins >= self.max_ins:
            raise StopIteration
        i = self.dma_next
        self.dma_next = (i + 1) % len(self.dma_ring)
        sem = self.dma_ring[i]
        uses = self.dma_uses[i]
        if uses > 0:
            self.wait(q, sem, 16 * uses)
        deps, rregs, wregs = self._collect(q, [in_], [out])
        for (s, v) in deps.values():
            self.wait(q, s, v)
        ins = self.engs[q].dma_start(out=out, in_=in_, **kw)
        ins.then_inc(sem, 16)
        self.n_ins += 1
        self.dma_uses[i] = uses + 1
        self._record(rregs, wregs, sem, 16 * (uses + 1))
        return ins

    def finish(self, q='sp'):
        for i, sem in enumerate(self.dma_ring):
            if self.dma_uses[i] > 0:
                self.wait(q, sem, 16 * self.dma_uses[i])
        for e in ['pe', 'dve', 'act', 'pool']:
            if self.cnt[e] > 0:
                self.wait(q, self.sem[e], self.cnt[e])


class Arena:
    def __init__(self, base, size):
        self.base, self.size, self.ptr = base, size, 0

    def take(self, nbytes):
        off = (self.ptr + 31) // 32 * 32
        self.ptr = off + nbytes
        assert self.ptr <= self.size, ("arena overflow", self.ptr, self.size)
        return self.base + off


PL = 72
PC = dict(G_PRE=0, M_NORM=8, R_MU=11, R_W0=22, R_A0=28, R_KK=34, R_KA=37, R_RK=40, R_NORM=43,
          L_CONV=46, L_CONVB=54, L_BA=56, L_BX=60, L_LAM=64, M_BI=68, M_BF=70)
DL = 32
DC = dict(OMKA=0, CLAM=3, C2LAM=7, NBF=11, OMMU=16)
CC = dict(IDENT=0, BONES=128, MKF=256, MKB=384, MNF=512, MNB=576, MUI=640, MLI=704, RMASK=768,
          LSEL=1536, PSEL=1664, NMKF=1668, NMKB=1796, ONES=1924, ISTK=2052)
NCST = 2052 + 64


def make_consts():
    c = np.zeros((128, NCST), np.float32)
    c[:, 0:128] = np.eye(128)
    c[0:64, 128:192] = 1.0
    c[64:128, 192:256] = 1.0
    s = np.arange(64)[:, None]
    t = np.arange(64)[None, :]
    us, ui = (s < t).astype(np.float32), (s <= t).astype(np.float32)
    ls, li = (s > t).astype(np.float32), (s >= t).astype(np.float32)
    c[0:64, 256:320], c[0:64, 320:384] = us, ui
    c[0:64, 384:448], c[0:64, 448:512] = ls, li
    c[0:64, 512:576] = -ls
    c[0:64, 576:640] = -us
    c[0:64, 640:704] = ui
    c[0:64, 704:768] = li
    rm = np.ones(768, np.float32)
    rm[::64] = 0.0
    c[:, 768:1536] = rm[None, :]
    for k in range(6):
        c[k, 1536 + (k % 2) * 64: 1536 + (k % 2) * 64 + 64] = 1.0
        c[k, 1664 + k // 2] = 1.0
    c[0:64, 1668:1796] = -c[0:64, 256:384]
    c[0:64, 1796:1924] = -c[0:64, 384:512]
    c[:, 1924:2052] = 1.0
    for (a, b) in ((256, 768), (1668, 1924)):
        c[64:128, a:b] = c[0:64, a:b]
    c[0:64, 2052:2116] = np.eye(64)
    c[64:128, 2052:2116] = np.eye(64)
    return c


class Builder:
    def __init__(self, NP, TS, debug=False, stop=None):
        self.stop = stop
        self.NP, self.TS = NP, TS
        self.NTOK = NP * 256 + TS
        self.debug = debug
        nc = self.nc = bass.Bass("TRN2", target_bir_lowering=False)
        self.S = Sync(nc)
        self._decl_dram()
        self._alloc()

    def _decl_dram(self):
        nc, NP, TS = self.nc, self.NP, self.TS
        di = lambda n, s: nc.dram_tensor(n, list(s), F32, kind="ExternalInput").ap()
        do = lambda n, s: nc.dram_tensor(n, list(s), F32, kind="ExternalOutput").ap()
        dx = lambda n, s: nc.dram_tensor(n, list(s), F32, kind="Internal").ap()
        self.x_in = di("x_in", [self.NTOK, D])
        self.cc = di("cc", [128, 8, 2])
        self.w_mod = di("w_mod", [2, D, 3 * D])
        self.bmodT = di("bmodT", [128, 2, 16])
        self.bmodg = di("bmodg", [2, D])
        self.gpost = di("gpost", [2, D])
        self.w_in = di("w_in", [2, D, IN_COLS])
        self.w_out = di("w_out", [2, D, D])
        self.pp = di("pp", [128, 2 * PL])
        self.cst = di("cst", [128, NCST])
        self.lora = di("lora", [128, 2, 2, 384])
        self.lruw = di("lruw", [128, 2, 8, 128])
        self.st_mC = di("st_mC", [2, 2, 128, 3, 65])
        self.st_mm = di("st_mm", [6, 2, 2])
        self.st_rH = di("st_rH", [2, 2, 128, 3, 64])
        self.st_l = di("st_l", [128, 2, 2, 2])
        for n in ["x_in", "cc", "w_mod", "bmodT", "bmodg", "gpost", "w_in", "w_out", "pp", "cst", "lora",
                  "lruw", "st_mC", "st_mm", "st_rH", "st_l"]:
            self.S.untracked.add(n)
        self.y_out = do("y_out", [self.NTOK, D])
        self.o_mC = do("o_mC", [NP, 2, 2, 128, 3, 65])
        self.o_mm = do("o_mm", [NP, 2, 2, 6, 1])
        self.o_rH = do("o_rH", [NP, 2, 2, 128, 3, 64])
        self.o_l = do("o_l", [NP, 2, 2, 128, 2])
        self.x1 = dx("x1", [self.NTOK, D])
        self.sHB = dx("sHB", [max(TS, 64), 384])
        self.sYB = dx("sYB", [max(TS // 256, 1), 128, 768])
        self.sLB = dx("sLB", [128, 2, max(TS, 64)])
        nb = max(TS // 256, 1)
        dxt = lambda n, s_, dt: nc.dram_tensor(n, list(s_), dt, kind="Internal").ap()
        self.sc = {
            'QT': dxt("sc_QT", [nb, 128, 3, 256], BF16), 'KT': dxt("sc_KT", [nb, 128, 3, 256], BF16),
            'KTOK': dxt("sc_KTOK", [nb, 64, 4, 384], BF16), 'VAUG': dxt("sc_VAUG", [nb, 64, 4, 6, 65], BF16),
            'GOZ': dxt("sc_GOZ", [nb, 128, 3, 256], F32), 'GI': dxt("sc_GI", [nb, 6, 256], F32),
            'LF': dxt("sc_LF", [nb, 6, 256], F32), 'XC': dxt("sc_XC", [nb, 128, 2, 256], F32),
            'LZT': dxt("sc_LZT", [nb, 128, 2, 256], F32), 'BLK': dxt("sc_BLK", [nb, 128, 11, 256], F32),
            'RZT': dxt("sc_RZT", [nb, 128, 3, 256], F32), 'KH': dxt("sc_KH", [nb, 128, 3, 256], F32),
            'VSTK': dxt("sc_VSTK", [nb, 128, 4, 3, 64], BF16),
        }
        if self.debug:
            self.dbg = do("dbg", [128, 32768])
            self.dbg_map = {}
            self.dbg_off = 0

    def T(self, name, shape, dtype, arena):
        es = 2 if dtype == BF16 else 4
        nb = _prod(shape[1:]) * es
        off = arena.take(nb)
        t = self.nc.alloc_sbuf_tensor_at(name, list(shape), dtype, offset=off)
        self.S.sb_addr[t.name] = (off, es)
        return t

    def _alloc(self):
        nc = self.nc
        B0 = 16384 + 256
        LIM = 224 * 1024 - 256
        P = Arena(B0, LIM - B0)
        T = self.T
        self.W_IN = T("W_IN", [128, 8, IN_COLS], BF16, P)
        self.W_OUT = T("W_OUT", [128, 8, D], BF16, P)
        self.CST = T("CST", [128, NCST], F32, P)
        self.PP = T("PP", [128, 2 * PL], F32, P)
        self.DV = T("DV", [128, 2 * DL], F32, P)
        self.LORA = T("LORA", [128, 2, 2, 384], F32, P)
        self.LRUW = T("LRUW", [128, 2, 8, 128], F32, P)
        self.RKD = T("RKD", [128, 2, 3, 128], F32, P)
        self.GS = T("GS", [128, 2, 8], F32, P)
        self.SH = T("SH", [128, 2, 8], F32, P)
        self.GATEB = T("GATEB", [128, 2, D], F32, P)
        self.IDB = T("IDB", [128, 128], BF16, P)
        self.ISTKB = T("ISTKB", [128, 64], BF16, P)
        self.mC = [T("mC%d" % d, [128, 3, 65], F32, P) for d in range(2)]
        self.mM = [T("mM%d" % d, [6, 1], F32, P) for d in range(2)]
        self.rH = [T("rH%d" % d, [128, 3, 64], F32, P) for d in range(2)]
        self.lS = [T("lS%d" % d, [128, 2], F32, P) for d in range(2)]
        XB = Arena(P.take(16384), 16384)
        self.HT = T("HT", [128, 8, 384], BF16, P)
        self.MIXT = T("MIXT", [128, 8, 256], BF16, P)
        self.BLK = T("BLK", [128, 11, 256], F32, P)
        abase = P.take(0)
        asize = P.size - P.ptr
        self.asize = asize
        mk = lambda: Arena(abase, asize)
        a = Arena(XB.base, XB.size)
        self.XW = [T("XW%d" % i, [128, D], F32, a) for i in range(2)]
        self.XN = [T("XN%d" % i, [128, D], F32, a) for i in range(2)]
        a = Arena(XB.base, XB.size)
        self.YB = T("YB", [128, 4, 3, 64], F32, a)
        self.RZT = T("RZT", [128, 3, 256], F32, a)
        self.WSTG = [T("WSTG0", [128, 8, 256], F32, Arena(self.S.sb_addr[self.BLK.name][0], 11264)),
                     T("WSTG1", [128, 8, 256], F32, Arena(XB.base, 8192)),
                     T("WSTG2", [128, 8, 256], F32, Arena(XB.base + 8192, 8192))]
        a = mk()
        self.WM = T("WM", [128, 8, 512], F32, a)
        self.SCT = T("SCT", [128, 8, 2], F32, a)
        self.SCB = T("SCB", [128, 2, 8, 128], F32, a)
        self.MODT = T("MODT", [128, 16, 2], F32, a)
        self.BMT = T("BMT", [128, 2, 16], F32, a)
        self.BG = T("BG", [128, D], F32, a)
        self.GP = T("GP", [128, D], F32, a)
        self.TMPS = T("TMPS", [128, 32], F32, a)
        a = mk()
        self.QT = T("QT", [128, 3, 256], BF16, a)
        self.KT = T("KT", [128, 3, 256], BF16, a)
        self.GOZ = T("GOZ", [128, 3, 256], F32, a)
        self.KTOK = T("KTOK", [64, 4, 384], BF16, a)
        self.VAUG = T("VAUG", [64, 4, 6, 65], BF16, a)
        self.HB = T("HB", [64, 4, 384], F32, a)
        self.mg = []
        for d in range(2):
            g = {}
            for n in ["gi", "lf", "pre", "bb", "gg", "ee", "fl"]:
                g[n] = T("mg_%s%d" % (n, d), [6, 256], F32, a)
            for n in ["mx", "mch", "mprev", "MM", "dec"]:
                g[n] = T("mg_%s%d" % (n, d), [6, 4], F32, a)
            g["X2"] = T("mg_X2%d" % d, [6, 4, 3], F32, a)
            g["etok"] = T("mg_etok%d" % d, [64, 4, 6], F32, a)
            g["fltok"] = T("mg_fltok%d" % d, [64, 4, 6], F32, a)
            g["decb"] = T("mg_decb%d" % d, [128, 4, 3], F32, a)
            self.mg.append(g)
        self.STSB = [T("STSB%d" % i, [64, 6, 64], BF16, a) for i in range(2)]
        self.VP = [T("VP%d" % i, [64, 6, 65], BF16, a) for i in range(2)]
        self.CDEC = T("CDEC", [128, 3, 65], F32, a)
        self.CDBF = T("CDBF", [128, 3, 65], BF16, a)
        self.DN = T("DN", [64, 6], F32, a)
        self.RDN = T("RDN", [64, 6], F32, a)
        self.HD = T("HD", [64, 6, 64], F32, a)
        self.SQ = T("SQ", [64, 4, 384], F32, a)
        self.SSQ = T("SSQ", [64, 24], F32, a)
        self.RSTD = T("RSTD", [64, 24], F32, a)
        self.ZT = [T("ZT%d" % i, [128, 256], F32, a) for i in range(2)]
        a = mk()
        self.XL = T("XL", [128, 2, 392], F32, a)
        self.XC = T("XC", [128, 2, 256], F32, a)
        self.LZT = T("LZT", [128, 2, 256], F32, a)
        self.ltd = [{n: T("lt%d_%s" % (d_, n), [128, 2, 256], F32, a) for n in ["rg", "ig", "aa", "a2", "bt"]} for d_ in range(2)]
        self.HL = [T("HL%d" % d, [128, 2, 256], F32, a) for d in range(2)]
        a = mk()
        self.URS = T("URS", [128, 11, 384], F32, a)
        a = mk()
        self.KH = T("KH", [128, 3, 256], F32, a)
        R1 = a.take(7 * 3072)
        a1 = Arena(R1, 7 * 3072)
        self.rt = {n: T("rt_" + n, [128, 3, 256], F32, a1) for n in ["sg", "aa", "kt", "bb", "cs", "d", "E"]}
        a2 = Arena(R1, 7 * 3072)
        self.A1BD = T("A1BD", [128, 3, 2, 128], BF16, a2)
        self.A2BD = T("A2BD", [128, 3, 2, 128], BF16, a2)
        self.NN = [T("NN%d" % i, [128, 3, 128], BF16, a2) for i in range(2)]
        self.NTT = [T("NTT%d" % i, [128, 3, 128], BF16, a2) for i in range(2)]
        self.U = T("U", [128, 3, 64], F32, a2)
        self.UBF = T("UBF", [128, 3, 64], BF16, a2)
        self.HBF = T("HBF", [128, 3, 64], BF16, a2)
        self.HTMP = T("HTMP", [128, 3, 64], F32, a2)
        self.YBD = T("YBD", [128, 3, 128], F32, a2)
        self.KTTOK = T("KTTOK", [128, 4, 3, 128], BF16, a2)
        self.BTTOK = T("BTTOK", [128, 4, 3, 128], BF16, a2)
        self.rSSQ = T("rSSQ", [128, 12], F32, a2)
        self.rRSTD = T("rRSTD", [128, 12], F32, a2)
        self.KR = T("KR", [128, 3, 4, 2, 64], BF16, a)
        self.BTT = T("BTT", [128, 3, 256], BF16, a)
        self.KTT = T("KTT", [128, 3, 256], BF16, a)
        bdr = a.take(4 * 3072)
        a3 = Arena(bdr, 4 * 3072)
        self.BD = {n: T("BD_" + n, [128, 3, 4, 128], BF16, a3) for n in ["kt", "b", "kh", "r"]}
        a3 = Arena(bdr, 4 * 3072)
        self.ft = {n: T("ft_" + n, [128, 3, 256], F32, a3) for n in ["rk", "bon", "t1"]}
        self.YSQ = T("YSQ", [128, 4, 3, 64], F32, a3)
        self.BDV = T("BDV", [128, 3, 4, 128], BF16, a)
        self.VSTK = T("VSTK", [128, 4, 3, 64], BF16, a)
        self.TWL = T("TWL", [128, 256], F32, a)
        self.GL = T("GL", [128, 3, 4], F32, a)
        self.PA = nc.alloc_psum_tensor("PA", [128, 1024], F32)
        self.PB = nc.alloc_psum_tensor("PB", [128, 1024], F32)
        self.PQ = [nc.alloc_psum_tensor("PQ%d" % i, [128, 512], F32) for i in range(3)]
        self.PTB = nc.alloc_psum_tensor("PTB", [128, 1024], BF16)
        self.pq_i = 0

    def pq(self):
        t = self.PQ[self.pq_i % 3]
        self.pq_i += 1
        return t

    def V(self, t, p0, npart, off, dims):
        pstep = _prod(list(t.shape)[1:])
        return bass.AP(t, p0 * pstep + off, [[pstep, npart]] + [list(d) for d in dims])

    def tt(self, e, out, in0, in1, op):
        return self.S.emit(e, [in0, in1], [out], lambda g: g.tensor_tensor(out=out, in0=in0, in1=in1, op=op))

    def ts(self, e, out, in0, s1, s2, op0, op1=None):
        rd = [in0] + [s for s in (s1, s2) if not isinstance(s, (int, float)) and s is not None]
        if op1 is None:
            return self.S.emit(e, rd, [out], lambda g: g.tensor_scalar(out=out, in0=in0, scalar1=s1, scalar2=None, op0=op0))
        return self.S.emit(e, rd, [out], lambda g: g.tensor_scalar(out=out, in0=in0, scalar1=s1, scalar2=s2, op0=op0, op1=op1))

    def stt(self, out, in0, sc, in1, op0, op1):
        rd = [in0, in1] + ([] if isinstance(sc, (int, float)) else [sc])
        return self.S.emit('dve', rd, [out], lambda g: g.scalar_tensor_tensor(out=out, in0=in0, scalar=sc, in1=in1, op0=op0, op1=op1))

    def act(self, out, in_, func, bias=None, scale=None, accum=None):
        rd = [in_]
        kw = {}
        if bias is not None:
            kw['bias'] = bias
            if not isinstance(bias, (int, float)):
                rd.append(bias)
        if scale is not None:
            kw['scale'] = scale
            if not isinstance(scale, (int, float)):
                rd.append(scale)
        wr = [out]
        if accum is not None:
            kw['accum_out'] = accum
            wr.append(accum)
        return self.S.emit('act', rd, wr, lambda g: g.activation(out=out, in_=in_, func=func, **kw))

    def cp(self, e, out, in_):
        if e == 'act':
            return self.act(out, in_, AF.Copy)
        return self.S.emit(e, [in_], [out], lambda g: g.tensor_copy(out=out, in_=in_))

    def _pe_rowtile_guard(self, lhsT, out):
        S = self.S
        st = S.region(lhsT)
        k = st[2] - st[1]
        kr = 32 if k <= 32 else (64 if k <= 64 else 128)
        rows = (st[1], st[1] + kr)
        oreg = S.region(out)
        last = getattr(self, '_last_pe', None)
        if last is not None:
            lrows, loreg, lins, linc = last
            disjoint = rows[1] <= lrows[0] or lrows[1] <= rows[0]
            samebank = (loreg[0] == oreg[0]) and loreg[3] < oreg[4] and oreg[3] < loreg[4]
            if disjoint and samebank:
                if not linc:
                    S.cnt['pe'] += 1
                    lins.then_inc(S.sem['pe'], 1)
                S.wait('pe', S.sem['pe'], S.cnt['pe'])
        return rows, oreg

    def mm(self, out, lhsT, rhs, start=True, stop=True, inc=None):
        if inc is None:
            inc = stop
        rows, oreg = self._pe_rowtile_guard(lhsT, out)
        ins = self.S.emit('pe', [lhsT, rhs], [out],
                          lambda g: g.matmul(out, lhsT=lhsT, rhs=rhs, start=start, stop=stop), inc=inc)
        self._last_pe = (rows, oreg, ins, inc)
        return ins

    def tr(self, out, in_, inc=True, bf=False):
        n = in_.shape[0]
        ident = self.IDB[0:n, 0:n] if bf else self.CST[0:n, CC['IDENT']:CC['IDENT'] + n]
        rows, oreg = self._pe_rowtile_guard(in_, out)
        ins = self.S.emit('pe', [in_, ident], [out],
                          lambda g: g.transpose(out=out, in_=in_, identity=ident), inc=inc)
        self._last_pe = (rows, oreg, ins, inc)
        return ins

    def memset(self, e, ap, v):
        return self.S.emit(e, [], [ap], lambda g: g.memset(ap, v))

    def scan(self, out, d0, d1, init, op0, op1):
        rd = [d0, d1] + ([] if isinstance(init, (int, float)) else [init])
        return self.S.emit('dve', rd, [out], lambda g: g.tensor_tensor_scan(out=out, data0=d0, data1=d1, initial=init, op0=op0, op1=op1))

    def recip(self, out, in_):
        return self.S.emit('dve', [in_], [out], lambda g: g.reciprocal(out=out, in_=in_))

    def reduce(self, out, in_, op, axis=AX.X):
        return self.S.emit('dve', [in_], [out], lambda g: g.tensor_reduce(out=out, in_=in_, axis=axis, op=op))

    def dump(self, name, ap):
        if not self.debug or name in self.dbg_map:
            return
        if getattr(self, 'dbg_filter', None) is not None and not any(name.startswith(p) for p in self.dbg_filter):
            return
        shp = list(ap.shape)
        npart, nfree = shp[0], _prod(shp[1:])
        stage = self.XN[1]
        assert nfree <= 1024
        dst = self.V(stage, 0, npart, 0, [[_prod(shp[i + 1:]), shp[i]] for i in range(1, len(shp))])
        self.cp('dve', dst, ap)
        self.S.dma('sp', self.dbg[0:npart, self.dbg_off:self.dbg_off + nfree], stage[0:npart, 0:nfree], allow_slow_non_contiguous=True)
        self.dbg_map[name] = (self.dbg_off, npart, shp[1:])
        self.dbg_off += nfree

    def ppc(self, l, key, j=0, rows=128):
        c = l * PL + PC[key] + j
        return self.PP[0:rows, c:c + 1]

    def dvc(self, l, key, j=0, rows=128):
        c = l * DL + DC[key] + j
        return self.DV[0:rows, c:c + 1]

    def setup(self):
        S = self.S
        S.dma('sp', self.CST[:, :], self.cst[:, :])
        S.dma('sp', self.PP[:, :], self.pp[:, :])
        S.dma('sp', self.LORA[:, :, :, :], self.lora[:, :, :, :])
        S.dma('sp', self.LRUW[:, :, :, :], self.lruw[:, :, :, :])
        self.memset('dve', self.DV[:, :], 0.0)
        self.cp('dve', self.IDB[:, :], self.CST[:, CC['IDENT']:CC['IDENT'] + 128])
        self.cp('dve', self.ISTKB[:, :], self.CST[:, CC['ISTK']:CC['ISTK'] + 64])
        for l in range(2):
            self.ts('dve', self.DV[:, l * DL + DC['OMKA']: l * DL + DC['OMKA'] + 3],
                    self.PP[:, l * PL + PC['R_KA']: l * PL + PC['R_KA'] + 3], -1.0, 1.0, ALU.mult, ALU.add)
            self.ts('dve', self.DV[:, l * DL + DC['OMMU']: l * DL + DC['OMMU'] + 11],
                    self.PP[:, l * PL + PC['R_MU']: l * PL + PC['R_MU'] + 11], -1.0, 1.0, ALU.mult, ALU.add)
            lam = self.PP[:, l * PL + PC['L_LAM']: l * PL + PC['L_LAM'] + 4]
            t0 = self.TMPS[:, 0:4]
            self.act(t0, lam, AF.Exp, scale=-1.0)
            self.act(t0, t0, AF.Ln, bias=1.0)
            self.ts('dve', self.DV[:, l * DL + DC['CLAM']: l * DL + DC['CLAM'] + 4], t0, -8.0, None, ALU.mult)
            self.ts('dve', self.DV[:, l * DL + DC['C2LAM']: l * DL + DC['C2LAM'] + 4], t0, -16.0, None, ALU.mult)
            self.ts('dve', self.DV[0:6, l * DL + DC['NBF']: l * DL + DC['NBF'] + 2],
                    self.PP[0:6, l * PL + PC['M_BF']: l * PL + PC['M_BF'] + 2], -1.0, None, ALU.mult)
            for hp in range(3):
                self.ts('dve', self.RKD[:, l, hp, :], self.CST[:, CC['BONES']:CC['BONES'] + 128],
                        self.ppc(l, 'R_RK', hp), None, ALU.mult)
        self.memset('dve', self.XL[:, :, 0:2], 0.0)

    def load_layer(self, l):
        S = self.S
        wsrc = self.w_in[l].rearrange("(kc p) c -> p kc c", p=128)
        wo = self.w_out[l].rearrange("(kc p) c -> p kc c", p=128)
        pieces = [(self.W_IN, wsrc, c0, min(256, IN_COLS - c0)) for c0 in range(0, IN_COLS, 256)]
        pieces += [(self.W_OUT, wo, c0, 256) for c0 in range(0, D, 256)]
        for i, (dst, src, c0, n) in enumerate(pieces):
            stg = self.WSTG[i % 3]
            S.dma('sp', stg[:, :, 0:n], src[:, :, c0:c0 + n])
            self.cp('pool', dst[:, :, c0:c0 + n], stg[:, :, 0:n])
        if self.stop == 'load_w':
            raise StopIteration
        S.dma('sp', self.SCT[:, :, :], self.cc[:, :, :])
        S.dma('sp', self.BMT[:, :, :], self.bmodT[:, :, :])
        self.act(self.SCT[:, :, :], self.SCT[:, :, :], AF.Silu)
        for m in range(2):
            for kc in range(8):
                src = self.V(self.SCT, 0, 128, kc * 2 + m, [[0, 128]])
                self.cp('dve', self.SCB[:, m, kc, :], src)
        wm = self.w_mod[l].rearrange("(kc p) c -> p kc c", p=128)
        for blk in range(6):
            S.dma('sp', self.WM[:, :, :], wm[:, :, blk * 512:(blk + 1) * 512])
            if blk < 4:
                ps = self.pq()
                for j in range(4):
                    for kc in range(8):
                        self.mm(ps[:, j * 2:j * 2 + 2], self.WM[:, kc, j * 128:(j + 1) * 128], self.SCT[:, kc, :],
                                start=(kc == 0), stop=(kc == 7))
                o = self.MODT[:, blk * 4:(blk + 1) * 4, :]
                bsrc = self.V(self.BMT, 0, 128, l * 16 + blk * 4, [[1, 4], [0, 2]])
                self.tt('dve', o, self.V(ps, 0, 128, 0, [[2, 4], [1, 2]]), bsrc, ALU.add)
            else:
                half = blk - 4
                for m in range(2):
                    ps = self.pq()
                    for kc in range(8):
                        self.mm(ps[:, :], self.SCB[:, m, kc, :], self.WM[:, kc, :], start=(kc == 0), stop=(kc == 7))
                    self.cp('act', self.GATEB[:, m, half * 512:(half + 1) * 512], ps[:, :])
        if self.stop == 'load_m':
            raise StopIteration
        for m in range(2):
            self.cp('dve', self.SH[:, m, :], self.V(self.MODT, 0, 128, m, [[2, 8]]))
            t0 = self.TMPS[:, 8:16]
            self.ts('dve', t0, self.V(self.MODT, 0, 128, 16 + m, [[2, 8]]), 1.0, None, ALU.add)
            self.tt('dve', self.GS[:, m, :], t0, self.PP[:, l * PL + PC['G_PRE']: l * PL + PC['G_PRE'] + 8], ALU.mult)
        if self.stop == 'load_g':
            raise StopIteration
        S.dma('sp', self.BG[:, :], bass.AP(self.bmodg.tensor, l * D, [[0, 128], [1, D]]))
        S.dma('sp', self.GP[:, :], bass.AP(self.gpost.tensor, l * D, [[0, 128], [1, D]]))
        if self.stop == 'load_b':
            raise StopIteration
        for m in range(2):
            self.tt('dve', self.GATEB[:, m, :], self.GATEB[:, m, :], self.BG[:, :], ALU.add)
            self.tt('dve', self.GATEB[:, m, :], self.GATEB[:, m, :], self.GP[:, :], ALU.mult)

    def proj_fm(self, c0, ncols, t_off, ntok, evac):
        ps = self.pq()
        for kc in range(8):
            self.mm(ps[0:ncols, 0:ntok], self.W_IN[:, kc, c0:c0 + ncols], self.HT[:, kc, t_off:t_off + ntok],
                    start=(kc == 0), stop=(kc == 7))
        evac(ps[0:ncols, 0:ntok])

    def proj_tm(self, c0, ncols, t_off, evac):
        ps = self.pq()
        for kc in range(8):
            self.mm(ps[0:64, 0:ncols], self.HT[:, kc, t_off:t_off + 64], self.W_IN[:, kc, c0:c0 + ncols],
                    start=(kc == 0), stop=(kc == 7))
        evac(ps[0:64, 0:ncols])

    def visit(self, l, mod, xsrc, xdst, row0, T, t0, dirs, grid, seq_idx, first, last, mode='full'):
        S = self.S
        w0 = max(0, t0 - 64)
        w1 = min(T, t0 + TB + 64)
        W = w1 - w0
        co = t0 - w0
        do_f = 0 in dirs
        prompt = (mod == 0)
        if mode != 'load':
            ntile = (W + 127) // 128
            for i in range(ntile):
                n = min(128, W - i * 128)
                xw, xn = self.XW[i % 2], self.XN[i % 2]
                r = row0 + w0 + i * 128
                S.dma('sp', xw[0:n, :], xsrc[r:r + n, :])
                ssq = self.TMPS[0:n, 16 + i:17 + i]
                self.act(xn[0:n, :], xw[0:n, :], AF.Square, accum=ssq)
                if self.stop == 'n1':
                    raise StopIteration
                rs = self.TMPS[0:n, 20 + i:21 + i]
                self.ts('dve', rs, ssq, 1.0 / D, EPS, ALU.mult, ALU.add)
                self.act(rs, rs, AF.Sqrt)
                self.recip(rs, rs)
                if self.stop == 'n2':
                    raise StopIteration
                self.act(xn[0:n, :], xw[0:n, :], AF.Copy, scale=rs)
                if self.stop == 'n3':
                    raise StopIteration
                for half in range(2):
                    ps = self.pq()
                    for j in range(4):
                        kc = half * 4 + j
                        self.tr(ps[:, j * 128:j * 128 + n], xn[0:n, kc * 128:(kc + 1) * 128])
                    if self.stop == 'n4':
                        raise StopIteration
                    for j in range(4):
                        kc = half * 4 + j
                        o = self.HT[:, kc, i * 128:i * 128 + n]
                        if half == 0:
                            self.ts('dve', o, ps[:, j * 128:j * 128 + n], self.GS[:, mod, kc:kc + 1], self.SH[:, mod, kc:kc + 1], ALU.mult, ALU.add)
                        else:
                            self.act(o, ps[:, j * 128:j * 128 + n], AF.Identity, bias=self.SH[:, mod, kc:kc + 1], scale=self.GS[:, mod, kc:kc + 1])
        if self.stop in ('norm', 'n5a', 'n5d'):
            raise StopIteration
        self.stage_mlstm(l, t0, co, dirs, prompt, seq_idx, first, last, mode)
        if self.stop == 'mlstm':
            raise StopIteration
        self.stage_lru(l, t0, co, W, w0, w1, T, dirs, prompt, seq_idx, first, last, mode)
        if self.stop == 'lru':
            raise StopIteration
        self.stage_rwkv(l, t0, co, W, w0, w1, T, dirs, grid, prompt, seq_idx, first, last, mode)
        for kc in range(8):
            self.dump('mix%d' % kc, self.MIXT[:, kc, :])
        if self.stop == 'rwkv':
            raise StopIteration
        if do_f:
            self.stage_out(l, mod, xsrc, xdst, row0 + t0)
        if self.stop == 'out':
            raise StopIteration

    def stage_mlstm(self, l, t0, co, dirs, prompt, seq_idx, first, last, mode='full'):
        S = self.S
        do_f = 0 in dirs
        if mode != 'load':
            for d in (sorted(set(dirs) | {0}) if mode == 'store' else dirs):
                g = self.mg[d]
                self.proj_fm(1920 + d * 6, 6, co, 256, lambda ps, g=g, d=d: self.act(g["gi"][:, :], ps, AF.Identity, bias=self.ppc(l, 'M_BI', d, 6)))
                def ev_f(ps, g=g, d=d):
                    self.act(g["lf"][:, :], ps, AF.Exp, bias=self.dvc(l, 'NBF', d, 6), scale=-1.0)
                    self.act(g["lf"][:, :], g["lf"][:, :], AF.Ln, bias=1.0)
                    self.ts('dve', g["lf"][:, :], g["lf"][:, :], -1.0, None, ALU.mult)
                self.proj_fm(1932 + d * 6, 6, co, 256, ev_f)
        bi = t0 // 256
        ml_items = [(self.QT[:, :, :], 'QT'), (self.KT[:, :, :], 'KT'), (self.KTOK[:, :, :], 'KTOK'), (self.VAUG[:, :, :, :], 'VAUG'),
                    (self.GOZ[:, :, :], 'GOZ'), (self.mg[0]["gi"][:, :], 'GI'), (self.mg[0]["lf"][:, :], 'LF')]
        if mode == 'load':
            for ap_, k_ in ml_items:
                S.dma('sp', ap_, self.sc[k_][bi])
        for d in dirs:
            if first[d]:
                if prompt:
                    self.memset('dve', self.mC[d][:, :, :], 0.0)
                    self.memset('dve', self.mM[d][:, :], 0.0)
                else:
                    S.dma('sp', self.mC[d][:, :, :], self.st_mC[l, d])
                    S.dma('sp', self.mM[d][:, :], self.st_mm[:, l, d:d + 1], allow_slow_non_contiguous=True)
        if do_f and not (1 in dirs):
            S.dma('sp', self.HB[:, :, :], self.sHB[t0:t0 + 256, :].rearrange("(c s) f -> s c f", s=64))
        for d in dirs:
            g = self.mg[d]
            v3 = lambda t: self.V(t, 0, 6, 0, [[64, 4], [1, 64]])
            self.scan(g["pre"][:, :], self.CST[0:6, CC['RMASK']:CC['RMASK'] + 256], g["lf"][:, :], 0.0, ALU.mult, ALU.add)
            bL = self.V(g["pre"], 0, 6, 63, [[64, 4]])
            bLb = self.V(g["pre"], 0, 6, 63, [[64, 4], [0, 64]])
            if d == 0:
                bsrc = g["pre"]
            else:
                self.tt('dve', v3(g["bb"]), bLb, v3(g["pre"]), ALU.subtract)
                self.tt('dve', g["bb"][:, :], g["bb"][:, :], g["lf"][:, :], ALU.add)
                bsrc = g["bb"]
            self.tt('dve', g["gg"][:, :], g["gi"][:, :], bsrc[:, :], ALU.subtract)
            self.reduce(g["mx"][:, :], v3(g["gg"]), ALU.max)
            if d == 0:
                mo, mxv, blv = g["mch"][:, :], g["mx"][:, :], bL
            else:
                mo = self.V(g["mch"], 0, 6, 3, [[-1, 4]])
                mxv = self.V(g["mx"], 0, 6, 3, [[-1, 4]])
                blv = self.V(g["pre"], 0, 6, 63 + 3 * 64, [[-64, 4]])
            self.scan(mo, mxv, blv, self.mM[d][:, 0:1], ALU.max, ALU.add)
            if d == 0:
                self.cp('dve', g["mprev"][:, 1:4], g["mch"][:, 0:3])
                self.cp('dve', g["mprev"][:, 0:1], self.mM[d][:, 0:1])
                mfin = g["mch"][:, 3:4]
            else:
                self.cp('dve', g["mprev"][:, 0:3], g["mch"][:, 1:4])
                self.cp('dve', g["mprev"][:, 3:4], self.mM[d][:, 0:1])
                mfin = g["mch"][:, 0:1]
            self.tt('dve', g["MM"][:, :], g["mprev"][:, :], g["mx"][:, :], ALU.max)
            self.tt('dve', g["dec"][:, :], g["mprev"][:, :], g["MM"][:, :], ALU.subtract)
            self.act(g["dec"][:, :], g["dec"][:, :], AF.Exp)
            self.cp('dve', self.mM[d][:, 0:1], mfin)
            MMb = self.V(g["MM"], 0, 6, 0, [[1, 4], [0, 64]])
            self.tt('dve', v3(g["ee"]), v3(g["gg"]), MMb, ALU.subtract)
            self.act(g["ee"][:, :], g["ee"][:, :], AF.Exp)
            self.tt('dve', v3(g["fl"]), v3(bsrc), MMb, ALU.add)
            self.act(g["fl"][:, :], g["fl"][:, :], AF.Exp, scale=-1.0)
        if mode != 'load':
            for hp in range(3):
                self.proj_fm(hp * 128, 128, co, 256, lambda ps, hp=hp: self.cp('act', self.QT[:, hp, :], ps))
                self.proj_fm(384 + hp * 128, 128, co, 256, lambda ps, hp=hp: self.act(self.KT[:, hp, :], ps, AF.Copy, scale=0.125))
            if do_f or mode == 'store':
                for hp in range(3):
                    def ev_o(ps, hp=hp):
                        self.act(self.GOZ[:, hp, :], ps, AF.Sigmoid)
                    self.proj_fm(1152 + hp * 128, 128, co, 256, ev_o)
                    def ev_z2(ps, hp=hp):
                        tz = self.ZT[hp % 2]
                        self.act(tz[:, :], ps, AF.Silu)
                        self.tt('dve', self.GOZ[:, hp, :], self.GOZ[:, hp, :], tz[:, :], ALU.mult)
                    self.proj_fm(1536 + hp * 128, 128, co, 256, ev_z2)
            for c in range(4):
                self.proj_tm(384, 384, co + c * 64, lambda ps, c=c: self.act(self.KTOK[:, c, :], ps, AF.Copy, scale=0.125))
                def ev_v(ps, c=c):
                    self.cp('dve', self.VAUG[:, c, :, 0:64], self.V(ps.tensor, 0, 64, 0, [[64, 6], [1, 64]]))
                self.proj_tm(768, 384, co + c * 64, ev_v)
            self.memset('dve', self.VAUG[:, :, :, 64:65], 1.0)
        for d in dirs:
            g = self.mg[d]
            ps = self.pq()
            for c in range(4):
                self.tr(ps[0:64, c * 6:c * 6 + 6], g["ee"][:, c * 64:(c + 1) * 64])
                self.tr(ps[0:64, 24 + c * 6:24 + c * 6 + 6], g["fl"][:, c * 64:(c + 1) * 64])
            self.cp('dve', g["etok"][:, :, :], self.V(ps, 0, 64, 0, [[6, 4], [1, 6]]))
            self.cp('dve', g["fltok"][:, :, :], self.V(ps, 0, 64, 24, [[6, 4], [1, 6]]))
            self.tt('dve', g["X2"][:, :, :], self.V(g["dec"], 0, 6, 0, [[1, 4], [0, 3]]),
                    self.V(self.CST, 0, 6, CC['PSEL'], [[0, 4], [1, 3]]), ALU.mult)
            ps2 = self.pq()
            self.mm(ps2[:, 0:12], self.CST[0:6, CC['LSEL']:CC['LSEL'] + 128], self.V(g["X2"], 0, 6, 0, [[1, 12]]))
            self.cp('dve', g["decb"][:, :, :], self.V(ps2, 0, 128, 0, [[3, 4], [1, 3]]))
        if mode == 'store':
            for ap_, k_ in ml_items:
                S.dma('sp', self.sc[k_][bi], ap_)
        for d in sorted(dirs, reverse=True):
            g = self.mg[d]
            mask = self.CST[0:64, CC['MUI']:CC['MUI'] + 64] if d == 0 else self.CST[0:64, CC['MLI']:CC['MLI'] + 64]
            maskb = self.V(self.CST, 0, 64, CC['MUI'] if d == 0 else CC['MLI'], [[0, 6], [1, 64]])
            for j in range(4):
                c = j if d == 0 else 3 - j
                cs = slice(c * 64, (c + 1) * 64)
                stsb, vp = self.STSB[j % 2], self.VP[j % 2]
                ps = self.pq()
                for h in (0, 2, 4, 1, 3, 5):
                    hp, pb = h // 2, 64 * (h % 2)
                    self.mm(ps[0:64, h * 64:(h + 1) * 64], self.KT[pb:pb + 64, hp, cs], self.QT[pb:pb + 64, hp, cs], inc=(h == 5))
                self.tt('dve', stsb[:, :, :], self.V(ps, 0, 64, 0, [[64, 6], [1, 64]]), maskb, ALU.mult)
                self.tt('dve', vp[:, :, :], self.VAUG[:, c, :, :], self.V(g["etok"], 0, 64, c * 6, [[1, 6], [0, 65]]), ALU.mult)
                self.tt('dve', self.CDEC[:, :, :], self.mC[d][:, :, :], self.V(g["decb"], 0, 128, c * 3, [[1, 3], [0, 65]]), ALU.mult)
                self.cp('act', self.CDBF[:, :, :], self.CDEC[:, :, :])
                ph = self.pq()
                for h in (0, 2, 4, 1, 3, 5):
                    hp, pb = h // 2, 64 * (h % 2)
                    o = ph[0:64, h * 65:(h + 1) * 65]
                    self.mm(o, stsb[:, h, :], vp[:, h, :], start=True, stop=False)
                    self.mm(o, self.QT[pb:pb + 64, hp, cs], self.CDBF[pb:pb + 64, hp, :], start=False, stop=True, inc=(h == 5))
                pc = self.pq()
                for h in range(6):
                    hp, pb = h // 2, 64 * (h % 2)
                    self.mm(pc[pb:pb + 64, hp * 65:(hp + 1) * 65], self.KTOK[:, c, h * 64:(h + 1) * 64], vp[:, h, :], inc=(h == 5))
                self.tt('dve', self.mC[d][:, :, :], self.CDEC[:, :, :], self.V(pc, 0, 128, 0, [[65, 3], [1, 65]]), ALU.add)
                self.act(self.DN[:, :], self.V(ph, 0, 64, 64, [[65, 6]]), AF.Abs)
                self.tt('dve', self.DN[:, :], self.DN[:, :], g["fltok"][:, c, :], ALU.max)
                self.recip(self.RDN[:, :], self.DN[:, :])
                hsrc = self.V(ph, 0, 64, 0, [[65, 6], [1, 64]])
                rb = self.V(self.RDN, 0, 64, 0, [[1, 6], [0, 64]])
                hbv = self.V(self.HB, 0, 64, c * 384, [[64, 6], [1, 64]])
                if d == 1:
                    self.tt('dve', hbv, hsrc, rb, ALU.mult)
                else:
                    self.tt('dve', self.HD[:, :, :], hsrc, rb, ALU.mult)
                    self.tt('dve', hbv, hbv, self.HD[:, :, :], ALU.add)
            if last[d] and prompt:
                S.dma('sp', self.o_mC[seq_idx, l, d], self.mC[d][:, :, :])
                S.dma('sp', self.o_mm[seq_idx, l, d], self.mM[d][:, 0:1])
        if not do_f:
            S.dma('sp', self.sHB[t0:t0 + 256, :].rearrange("(c s) f -> s c f", s=64), self.HB[:, :, :])
            return
        self.tt('dve', self.SQ[:, :, :], self.HB[:, :, :], self.HB[:, :, :], ALU.mult)
        self.reduce(self.SSQ[:, :], self.V(self.SQ, 0, 64, 0, [[64, 24], [1, 64]]), ALU.add)
        self.ts('dve', self.SSQ[:, :], self.SSQ[:, :], 1.0 / 64, EPS, ALU.mult, ALU.add)
        self.act(self.SSQ[:, :], self.SSQ[:, :], AF.Sqrt)
        self.recip(self.RSTD[:, :], self.SSQ[:, :])
        self.tt('dve', self.V(self.SQ, 0, 64, 0, [[64, 24], [1, 64]]), self.V(self.HB, 0, 64, 0, [[64, 24], [1, 64]]),
                self.V(self.RSTD, 0, 64, 0, [[1, 24], [0, 64]]), ALU.mult)
        for hp in range(3):
            ps = self.pq()
            for c in range(4):
                self.tr(ps[:, c * 64:(c + 1) * 64], self.SQ[:, c, hp * 128:(hp + 1) * 128], inc=(c == 3))
            self.stt(self.MIXT[:, hp, :], ps[:, 0:256], self.ppc(l, 'M_NORM', hp), self.GOZ[:, hp, :], ALU.mult, ALU.mult)

    def stage_lru(self, l, t0, co, W, w0, w1, T, dirs, prompt, seq_idx, first, last, mode='full'):
        S = self.S
        do_f = 0 in dirs
        if mode != 'load':
            for pr in range(2):
                self.proj_fm(3736 + pr * 128, 128, 0, W, lambda ps, pr=pr: self.cp('act', self.XL[:, pr, 2:2 + W], ps))
                if do_f or mode == 'store':
                    self.proj_fm(3992 + pr * 128, 128, co, 256, lambda ps, pr=pr: self.act(self.LZT[:, pr, :], ps, AF.Silu))
            if w1 == T:
                self.memset('dve', self.XL[:, :, 2 + W:2 + W + 1], 0.0)
            if w0 == 0:
                self.memset('dve', self.XL[:, :, 0:2], 0.0)
            for pr in range(2):
                self.ts('dve', self.XC[:, pr, :], self.XL[:, pr, co:co + 256], self.ppc(l, 'L_CONV', 0 * 2 + pr), self.ppc(l, 'L_CONVB', pr), ALU.mult, ALU.add)
                for j in range(1, 4):
                    self.stt(self.XC[:, pr, :], self.XL[:, pr, co + j:co + j + 256], self.ppc(l, 'L_CONV', j * 2 + pr), self.XC[:, pr, :], ALU.mult, ALU.add)
        bi = t0 // 256
        lr_items = [(self.XC[:, :, :], 'XC'), (self.LZT[:, :, :], 'LZT')]
        if mode == 'store':
            for ap_, k_ in lr_items:
                S.dma('sp', self.sc[k_][bi], ap_)
        if mode == 'load':
            for ap_, k_ in lr_items:
                S.dma('sp', ap_, self.sc[k_][bi])
        for d in dirs:
            if first[d]:
                if prompt:
                    self.memset('dve', self.lS[d][:, :], 0.0)
                else:
                    S.dma('sp', self.lS[d][:, :], self.st_l[:, l, d, :])
        if do_f and not (1 in dirs):
            S.dma('sp', self.HL[1][:, :, :], self.sLB[:, :, t0:t0 + 256])
        combos = [(d, pr) for d in sorted(dirs, reverse=True) for pr in range(2)]
        LT = lambda d, n, pr: self.ltd[d][n][:, pr, :]
        pss = {}
        for i, (d, pr) in enumerate(combos):
            ps = self.PA if i % 2 == 0 else self.PB
            off = (i // 2) * 512
            pss[(d, pr)] = (ps, off)
            self.mm(ps[:, off:off + 256], self.LRUW[:, l, (0 * 2 + d) * 2 + pr, :], self.XC[:, pr, :])
            self.mm(ps[:, off + 256:off + 512], self.LRUW[:, l, (1 * 2 + d) * 2 + pr, :], self.XC[:, pr, :])
        for (d, pr) in combos:
            ps, off = pss[(d, pr)]
            self.act(LT(d, "rg", pr), ps[:, off:off + 256], AF.Sigmoid, bias=self.ppc(l, 'L_BA', d * 2 + pr))
            self.act(LT(d, "ig", pr), ps[:, off + 256:off + 512], AF.Sigmoid, bias=self.ppc(l, 'L_BX', d * 2 + pr))
        for (d, pr) in combos:
            self.act(LT(d, "aa", pr), LT(d, "rg", pr), AF.Exp, scale=self.dvc(l, 'CLAM', d * 2 + pr))
            self.act(LT(d, "a2", pr), LT(d, "rg", pr), AF.Exp, scale=self.dvc(l, 'C2LAM', d * 2 + pr))
        for (d, pr) in combos:
            self.ts('dve', LT(d, "a2", pr), LT(d, "a2", pr), -1.0, 1.0, ALU.mult, ALU.add)
            self.tt('dve', LT(d, "bt", pr), LT(d, "ig", pr), self.XC[:, pr, :], ALU.mult)
        for (d, pr) in combos:
            self.act(LT(d, "a2", pr), LT(d, "a2", pr), AF.Sqrt)
        for (d, pr) in combos:
            self.tt('dve', LT(d, "bt", pr), LT(d, "bt", pr), LT(d, "a2", pr), ALU.mult)
        for (d, pr) in combos:
            if d == 0:
                self.scan(self.HL[0][:, pr, :], LT(0, "aa", pr), LT(0, "bt", pr), self.lS[0][:, pr:pr + 1], ALU.mult, ALU.add)
                self.cp('act', self.lS[0][:, pr:pr + 1], self.HL[0][:, pr, 255:256])
            else:
                rv = lambda t: self.V(t, 0, 128, pr * 256 + 255, [[-1, 256]])
                self.scan(rv(self.HL[1]), rv(self.ltd[1]["aa"]), rv(self.ltd[1]["bt"]), self.lS[1][:, pr:pr + 1], ALU.mult, ALU.add)
                self.cp('act', self.lS[1][:, pr:pr + 1], self.HL[1][:, pr, 0:1])
        for d in sorted(dirs, reverse=True):
            if last[d] and prompt:
                S.dma('sp', self.o_l[seq_idx, l, d], self.lS[d][:, :])
        if not do_f:
            S.dma('sp', self.sLB[:, :, t0:t0 + 256], self.HL[1][:, :, :])
            return
        self.tt('dve', self.HL[0][:, :, :], self.HL[0][:, :, :], self.HL[1][:, :, :], ALU.add)
        self.tt('dve', self.MIXT[:, 6:8, :], self.HL[0][:, :, :], self.LZT[:, :, :], ALU.mult)

    def stage_rwkv(self, l, t0, co, W, w0, w1, T, dirs, grid, prompt, seq_idx, first, last, mode='full'):
        S = self.S
        do_f = 0 in dirs
        if mode != 'load':
            for ch in range(11):
                self.proj_fm(1944 + ch * 128, 128, 0, W, lambda ps, ch=ch: self.cp('act' if ch % 2 else 'dve', self.URS[:, ch, 0:W], ps))
            if do_f or mode == 'store':
                for hp in range(3):
                    self.proj_fm(3352 + hp * 128, 128, co, 256, lambda ps, hp=hp: self.act(self.RZT[:, hp, :], ps, AF.Silu))
            U3 = lambda off, n: self.V(self.URS, 0, 128, off, [[384, 11], [1, n]])
            B3 = lambda off, n: self.V(self.BLK, 0, 128, off, [[256, 11], [1, n]])
            if not grid:
                self.cp('dve', B3(1, 255), U3(0, 255))
                self.memset('dve', B3(0, 1), 0.0)
                self.tt('dve', B3(0, 255), B3(0, 255), U3(1, 255), ALU.add)
                wsh = 0.5
            else:
                U4 = lambda off, r, n: self.V(self.URS, 0, 128, off, [[384, 11], [64, r], [1, n]])
                B4 = lambda off, r, n: self.V(self.BLK, 0, 128, off, [[256, 11], [64, r], [1, n]])
                self.cp('dve', B4(1, 4, 63), U4(co, 4, 63))
                self.memset('dve', B4(0, 4, 1), 0.0)
                self.tt('dve', B4(0, 4, 63), B4(0, 4, 63), U4(co + 1, 4, 63), ALU.add)
                if t0 > 0:
                    self.tt('dve', B3(0, 256), B3(0, 256), U3(co - 64, 256), ALU.add)
                else:
                    self.tt('dve', B3(64, 192), B3(64, 192), U3(0, 192), ALU.add)
                if t0 + TB < T:
                    self.tt('dve', B3(0, 256), B3(0, 256), U3(co + 64, 256), ALU.add)
                else:
                    self.tt('dve', B3(0, 192), B3(0, 192), U3(co + 64, 192), ALU.add)
                wsh = 0.25
            mu = self.V(self.PP, 0, 128, l * PL + PC['R_MU'], [[1, 11], [0, 256]])
            self.tt('dve', B3(0, 256), B3(0, 256), mu, ALU.mult)
            omm = self.V(self.DV, 0, 128, l * DL + DC['OMMU'], [[1, 11], [0, 256]])
            self.tt('dve', U3(co, 256), U3(co, 256), omm, ALU.mult)
            self.stt(B3(0, 256), B3(0, 256), wsh, U3(co, 256), ALU.mult, ALU.add)
            for nm, ch in (('blk_r', 0), ('blk_k', 3), ('blk_v', 6), ('blk_wl', 9), ('blk_al', 10)):
                self.dump(nm, self.BLK[:, ch, :])
            rt = self.rt
            kk = self.V(self.PP, 0, 128, l * PL + PC['R_KK'], [[1, 3], [0, 256]])
            kap = rt["d"]
            self.tt('dve', kap[:, :, :], self.BLK[:, 3:6, :], kk, ALU.mult)
            ksq = rt["E"]
            self.tt('dve', ksq[:, :, :], kap[:, :, :], kap[:, :, :], ALU.mult)
            for hp in range(3):
                self.mm(self.PA[:, hp * 256:(hp + 1) * 256], self.CST[:, CC['BONES']:CC['BONES'] + 128], ksq[:, hp, :])
            self.act(ksq[:, :, :], self.V(self.PA, 0, 128, 0, [[256, 3], [1, 256]]), AF.Sqrt)
            self.ts('dve', ksq[:, :, :], ksq[:, :, :], 1e-12, None, ALU.max)
            self.recip(ksq[:, :, :], ksq[:, :, :])
            self.tt('dve', self.KH[:, :, :], kap[:, :, :], ksq[:, :, :], ALU.mult)
            self.dump('kh', self.KH[:, 0, :])
            self.bd_fill(self.BDV, lambda par: self.V(self.BLK, par * 64, 64, 6 * 256, [[256, 3], [64, 4], [1, 64]]))
            for c in range(4):
                for hp in range(3):
                    self.mm(self.PB[:, (c * 3 + hp) * 64:(c * 3 + hp + 1) * 64], self.BDV[:, hp, c, :], self.ISTKB[:, :])
            self.cp('act', self.V(self.VSTK, 0, 128, 0, [[1, 768]]), self.PB[:, 0:768])
        bi = t0 // 256
        rw_items = [(self.BLK[:, :, :], 'BLK'), (self.RZT[:, :, :], 'RZT'), (self.KH[:, :, :], 'KH'), (self.VSTK[:, :, :, :], 'VSTK')]
        if mode == 'store':
            for ap_, k_ in rw_items:
                S.dma('sp', self.sc[k_][bi], ap_)
        if mode == 'load':
            for ap_, k_ in rw_items:
                S.dma('sp', ap_, self.sc[k_][bi])
        for d in dirs:
            if first[d]:
                if prompt:
                    self.memset('dve', self.rH[d][:, :, :], 0.0)
                else:
                    S.dma('sp', self.rH[d][:, :, :], self.st_rH[l, d])
        ybflat = self.V(self.YB, 0, 128, 0, [[1, 768]])
        if do_f and not (1 in dirs):
            S.dma('sp', ybflat, self.sYB[t0 // 256])
        for d in sorted(dirs, reverse=True):
            self.rwkv_dir(l, d, seq_idx, prompt, last)
        if not do_f:
            S.dma('sp', self.sYB[t0 // 256], ybflat)
            return
        ft = self.ft
        self.tt('dve', self.YSQ[:, :, :, :], self.YB[:, :, :, :], self.YB[:, :, :, :], ALU.mult)
        self.reduce(self.rSSQ[:, :], self.V(self.YSQ, 0, 128, 0, [[64, 12], [1, 64]]), ALU.add)
        self.ts('dve', self.rSSQ[:, :], self.rSSQ[:, :], 1.0 / 64, EPS, ALU.mult, ALU.add)
        self.act(self.rSSQ[:, :], self.rSSQ[:, :], AF.Sqrt)
        self.recip(self.rRSTD[:, :], self.rSSQ[:, :])
        self.tt('dve', ft["rk"][:, :, :], self.BLK[:, 0:3, :], self.BLK[:, 3:6, :], ALU.mult)
        for hp in range(3):
            self.mm(self.PB[:, hp * 256:(hp + 1) * 256], self.RKD[:, l, hp, :], ft["rk"][:, hp, :])
        self.tt('dve', ft["bon"][:, :, :], self.V(self.PB, 0, 128, 0, [[256, 3], [1, 256]]), self.BLK[:, 6:9, :], ALU.mult)
        self.memset('pool', self.YBD[:, :, :], 0.0)
        for c in range(4):
            for par in range(2):
                self.tt('dve', self.V(self.YBD, par * 64, 64, par * 64, [[128, 3], [1, 64]]),
                        self.V(self.YB, par * 64, 64, c * 192, [[64, 3], [1, 64]]),
                        self.V(self.rRSTD, par * 64, 64, c * 3, [[1, 3], [0, 64]]), ALU.mult)
            for hp in range(3):
                self.mm(self.PA[:, hp * 256 + c * 64: hp * 256 + (c + 1) * 64], self.YBD[:, hp, :],
                        self.CST[:, CC['ISTK']:CC['ISTK'] + 64])
        for hp in range(3):
            self.stt(ft["t1"][:, hp, :], self.PA[:, hp * 256:(hp + 1) * 256], self.ppc(l, 'R_NORM', hp), ft["bon"][:, hp, :], ALU.mult, ALU.add)
            self.tt('dve', self.MIXT[:, 3 + hp, :], ft["t1"][:, hp, :], self.RZT[:, hp, :], ALU.mult)

    def bd_fill(self, bd, src_of_par, eng='pool'):
        self.memset(eng, bd[:, :, :, :], 0.0)
        for par in range(2):
            self.cp(eng, self.V(bd, par * 64, 64, par * 64, [[512, 3], [128, 4], [1, 64]]), src_of_par(par))

    def rwkv_dir(self, l, d, seq_idx, prompt, last):
        S = self.S
        rt = self.rt
        pb_d = 64 * d
        f3 = lambda t: t[:, :, :]
        v4 = lambda t: self.V(t, 0, 128, 0, [[256, 3], [64, 4], [1, 64]])
        self.act(self.TWL[pb_d:pb_d + 64, :], self.BLK[pb_d:pb_d + 64, 9, :], AF.Tanh)
        for hp in range(3):
            self.mm(self.PA[:, hp * 256:(hp + 1) * 256], self.LORA[pb_d:pb_d + 64, l, 0, hp * 128:(hp + 1) * 128], self.TWL[pb_d:pb_d + 64, :])
        for hp in range(3):
            self.act(rt["sg"][:, hp, :], self.PA[:, hp * 256:(hp + 1) * 256], AF.Sigmoid, bias=self.ppc(l, 'R_W0', d * 3 + hp))
        for hp in range(3):
            self.mm(self.PB[:, hp * 256:(hp + 1) * 256], self.LORA[pb_d:pb_d + 64, l, 1, hp * 128:(hp + 1) * 128], self.BLK[pb_d:pb_d + 64, 10, :])
        for hp in range(3):
            self.act(rt["aa"][:, hp, :], self.PB[:, hp * 256:(hp + 1) * 256], AF.Sigmoid, bias=self.ppc(l, 'R_A0', d * 3 + hp))
        for hp in range(3):
            self.ts('dve', rt["kt"][:, hp, :], rt["aa"][:, hp, :], self.ppc(l, 'R_KA', hp), self.dvc(l, 'OMKA', hp), ALU.mult, ALU.add)
        self.tt('dve', f3(rt["kt"]), f3(rt["kt"]), self.BLK[:, 3:6, :], ALU.mult)
        self.tt('dve', f3(rt["bb"]), self.KH[:, :, :], f3(rt["aa"]), ALU.mult)
        flat = lambda t: self.V(t, 0, 128, 0, [[1, 768]])
        self.scan(flat(rt["cs"]), self.CST[:, CC['RMASK']:CC['RMASK'] + 768], flat(rt["sg"]), 0.0, ALU.mult, ALU.add)
        self.cp('dve', self.GL[:, :, :], self.V(rt["cs"], 0, 128, 63, [[256, 3], [64, 4]]))
        if d == 1:
            csLb = self.V(rt["cs"], 0, 128, 63, [[256, 3], [64, 4], [0, 64]])
            self.tt('dve', v4(rt["d"]), csLb, v4(rt["cs"]), ALU.subtract)
            self.tt('dve', f3(rt["cs"]), f3(rt["d"]), f3(rt["sg"]), ALU.add)
        self.act(f3(rt["E"]), f3(rt["cs"]), AF.Exp, scale=-DSC)
        self.tt('dve', self.V(self.KR, 0, 128, 64, [[512, 3], [128, 4], [1, 64]]),
                self.V(self.BLK, 0, 128, 0, [[256, 3], [64, 4], [1, 64]]), v4(rt["E"]), ALU.mult)
        self.tt('dve', f3(rt["d"]), f3(rt["cs"]), f3(rt["sg"]), ALU.subtract)
        self.act(f3(rt["E"]), f3(rt["d"]), AF.Exp, scale=-DSC)
        self.tt('dve', self.V(self.KR, 0, 128, 0, [[512, 3], [128, 4], [1, 64]]), v4(self.KH), v4(rt["E"]), ALU.mult)
        self.act(f3(rt["E"]), f3(rt["cs"]), AF.Exp, scale=DSC)
        self.tt('dve', self.BTT[:, :, :], f3(rt["bb"]), f3(rt["E"]), ALU.mult)
        self.tt('dve', self.KTT[:, :, :], f3(rt["kt"]), f3(rt["E"]), ALU.mult)
        self.act(self.GL[:, :, :], self.GL[:, :, :], AF.Exp, scale=-DSC)
        BD = self.BD
        self.memset('pool', self.A1BD[:, :, :, :], 0.0)
        self.memset('pool', self.A2BD[:, :, :, :], 0.0)
        self.memset('pool', self.NN[0][:, :, :], 0.0)
        c4 = lambda t, par: self.V(t, par * 64, 64, 0, [[256, 3], [64, 4], [1, 64]])
        self.bd_fill(BD["kt"], lambda par: c4(self.KTT, par))
        self.bd_fill(BD["b"], lambda par: c4(self.BTT, par))
        self.bd_fill(BD["kh"], lambda par: self.V(self.KR, par * 64, 64, 0, [[512, 3], [128, 4], [1, 64]]))
        self.bd_fill(BD["r"], lambda par: self.V(self.KR, par * 64, 64, 64, [[512, 3], [128, 4], [1, 64]]))
        for (src, dst, neg) in ((BD["kt"], self.KTTOK, False), (BD["b"], self.BTTOK, True)):
            for half in range(2):
                for cc_ in range(2):
                    c = half * 2 + cc_
                    for hp in range(3):
                        self.tr(self.PTB[:, (cc_ * 3 + hp) * 128:(cc_ * 3 + hp + 1) * 128], src[:, hp, c, :], bf=True)
                o = self.V(dst, 0, 128, half * 768, [[1, 768]])
                if neg:
                    self.act(o, self.PTB[:, 0:768], AF.Copy, scale=-1.0)
                else:
                    self.cp('dve', o, self.PTB[:, 0:768])
        self.cp('act', self.HBF[:, :, :], self.rH[d][:, :, :])
        mk = CC['MKF'] if d == 0 else CC['MKB']
        nmk = CC['NMKF'] if d == 0 else CC['NMKB']
        mn = CC['MNF'] if d == 0 else CC['MNB']
        for j in range(4):
            c = j if d == 0 else 3 - j
            cs = slice(c * 64, (c + 1) * 64)
            KRc = lambda hp: self.V(self.KR, 0, 128, hp * 512 + c * 128, [[1, 128]])
            p1, p2, p3 = self.pq(), self.pq(), self.pq()
            for hp in range(3):
                self.mm(p1[:, hp * 128:(hp + 1) * 128], BD["kt"][:, hp, c, :], KRc(hp), inc=(hp == 2))
            for hp in range(3):
                self.mm(p2[:, hp * 128:(hp + 1) * 128], BD["b"][:, hp, c, :], KRc(hp), inc=(hp == 2))
            for hp in range(3):
                self.mm(p3[:, hp * 64:(hp + 1) * 64], BD["kh"][:, hp, c, :], self.BTT[:, hp, cs], inc=(hp == 2))
            for par in range(2):
                pp_ = par * 64
                self.tt('dve', self.V(self.A1BD, pp_, 64, pp_, [[256, 3], [128, 2], [1, 64]]),
                        self.V(p1, pp_, 64, 0, [[128, 3], [64, 2], [1, 64]]),
                        self.V(self.CST, pp_, 64, mk, [[0, 3], [64, 2], [1, 64]]), ALU.mult)
                self.tt('dve', self.V(self.A2BD, pp_, 64, pp_, [[256, 3], [128, 2], [1, 64]]),
                        self.V(p2, pp_, 64, 0, [[128, 3], [64, 2], [1, 64]]),
                        self.V(self.CST, pp_, 64, nmk, [[0, 3], [64, 2], [1, 64]]), ALU.mult)
                self.tt('dve', self.V(self.NN[0], pp_, 64, pp_, [[128, 3], [1, 64]]),
                        self.V(p3, pp_, 64, 0, [[64, 3], [1, 64]]),
                        self.V(self.CST, pp_, 64, mn, [[0, 3], [1, 64]]), ALU.mult)
            pr_ = self.pq()
            for hp in range(3):
                o = pr_[:, hp * 64:(hp + 1) * 64]
                self.mm(o, BD["kh"][:, hp, c, :], self.HBF[:, hp, :], start=True, stop=False)
                self.mm(o, self.A1BD[:, hp, 0, :], self.VSTK[:, c, hp, :], start=False, stop=True, inc=(hp == 2))
            u192 = self.V(self.U, 0, 128, 0, [[1, 192]])
            ub192 = self.V(self.UBF, 0, 128, 0, [[1, 192]])
            self.cp('dve', u192, pr_[:, 0:192])
            self.cp('act', ub192, u192)
            for k in range(6):
                NTk = self.A2BD[:, :, 0, :] if k == 0 else self.NTT[k % 2][:, :, :]
                Nk = self.NN[k % 2]
                pu = self.pq()
                for hp in range(3):
                    self.mm(pu[:, hp * 64:(hp + 1) * 64], NTk[:, hp, :], self.UBF[:, hp, :], inc=(hp == 2))
                if k < 5:
                    pnt = self.pq()
                    for hp in range(3):
                        self.mm(pnt[:, hp * 128:(hp + 1) * 128], Nk[:, hp, :], NTk[:, hp, :], inc=(hp == 2))
                    pn = self.pq()
                    for hp in range(3):
                        self.mm(pn[:, hp * 128:(hp + 1) * 128], NTk[:, hp, :], Nk[:, hp, :], inc=(hp == 2))
                self.tt('dve', ub192, u192, pu[:, 0:192], ALU.add)
                if k < 5:
                    self.cp('act', self.V(self.NTT[(k + 1) % 2], 0, 128, 0, [[1, 384]]), pnt[:, 0:384])
                    self.cp('act', self.V(self.NN[(k + 1) % 2], 0, 128, 0, [[1, 384]]), pn[:, 0:384])
                    self.tt('dve', u192, u192, pu[:, 0:192], ALU.add)
            py = self.pq()
            for hp in range(3):
                o = py[:, hp * 64:(hp + 1) * 64]
                self.mm(o, BD["r"][:, hp, c, :], self.HBF[:, hp, :], start=True, stop=False)
                self.mm(o, self.A1BD[:, hp, 1, :], self.VSTK[:, c, hp, :], start=False, stop=False)
                self.mm(o, self.A2BD[:, hp, 1, :], self.UBF[:, hp, :], start=False, stop=True, inc=(hp == 2))
            ybv = self.V(self.YB, 0, 128, c * 192, [[1, 192]])
            if d == 1:
                self.cp('act', ybv, py[:, 0:192])
            else:
                self.tt('dve', ybv, ybv, py[:, 0:192], ALU.add)
            ph = self.pq()
            for hp in range(3):
                o = ph[:, hp * 64:(hp + 1) * 64]
                self.mm(o, self.KTTOK[:, c, hp, :], self.VSTK[:, c, hp, :], start=True, stop=False)
                self.mm(o, self.BTTOK[:, c, hp, :], self.UBF[:, hp, :], start=False, stop=True, inc=(hp == 2))
            self.tt('dve', self.HTMP[:, :, :], self.rH[d][:, :, :], self.V(ph, 0, 128, 0, [[64, 3], [1, 64]]), ALU.add)
            self.tt('dve', self.rH[d][:, :, :], self.HTMP[:, :, :], self.V(self.GL, 0, 128, c, [[4, 3], [0, 64]]), ALU.mult)
            self.cp('act', self.HBF[:, :, :], self.rH[d][:, :, :])
        if last[d] and prompt:
            S.dma('sp', self.o_rH[seq_idx, l, d], self.rH[d][:, :, :])

    def stage_out(self, l, mod, xsrc, xdst, r0):
        S = self.S
        for tt_ in range(2):
            o, xw = self.XN[tt_], self.XW[tt_]
            S.dma('sp', xw[:, :], xsrc[r0 + tt_ * 128: r0 + (tt_ + 1) * 128, :])
            for ch in range(2):
                ps = self.pq()
                for kc in range(8):
                    self.mm(ps[:, :], self.MIXT[:, kc, tt_ * 128:(tt_ + 1) * 128], self.W_OUT[:, kc, ch * 512:(ch + 1) * 512],
                            start=(kc == 0), stop=(kc == 7))
                self.cp('act' if ch else 'dve', o[:, ch * 512:(ch + 1) * 512], ps[:, :])
            self.dump('o_proj%d' % tt_, o[:, :])
            self.dump('o_x%d' % tt_, xw[:, :])
            ssq = self.TMPS[:, 24 + tt_:25 + tt_]
            junk = self.V(self.BLK, 0, 128, 0, [[1, D]])
            self.act(junk, o[:, :], AF.Square, accum=ssq)
            rs = self.TMPS[:, 26 + tt_:27 + tt_]
            self.ts('dve', rs, ssq, 1.0 / D, EPS, ALU.mult, ALU.add)
            self.act(rs, rs, AF.Sqrt)
            self.recip(rs, rs)
            self.dump('o_rs%d' % tt_, rs)
            self.stt(o[:, :], o[:, :], rs, self.GATEB[:, mod, :], ALU.mult, ALU.mult)
            self.dump('o_g%d' % tt_, o[:, :])
            self.tt('dve', o[:, :], o[:, :], xw[:, :], ALU.add)
            S.dma('sp', xdst[r0 + tt_ * 128: r0 + (tt_ + 1) * 128, :], o[:, :])

    def build(self, layers=(0, 1)):
        try:
            self._build(layers)
        except StopIteration:
            pass
        self.S.finish('sp')
        return self.nc

    def _build(self, layers):
        NP, TS = self.NP, self.TS
        self.setup()
        if self.stop == 'setup':
            raise StopIteration
        for li, l in enumerate(layers):
            self.load_layer(l)
            if self.stop == 'load':
                raise StopIteration
            xsrc = self.x_in if li == 0 else self.x1
            xdst = self.y_out if li == len(layers) - 1 else self.x1
            T_, F_ = {0: True, 1: True}, {0: False, 1: False}
            for s in range(NP):
                self.visit(l, 0, xsrc, xdst, s * 256, 256, 0, [0, 1], False, s, T_, T_)
            if TS > 0:
                nb = TS // TB
                row0 = NP * 256
                for b in range(nb - 1, -1, -1):
                    self.visit(l, 1, xsrc, xdst, row0, TS, b * TB, [1], True, 0,
                               {0: False, 1: b == nb - 1}, {0: False, 1: b == 0}, mode='store')
                for b in range(nb):
                    self.visit(l, 1, xsrc, xdst, row0, TS, b * TB, [0], True, 0,
                               {0: b == 0, 1: False}, {0: b == nb - 1, 1: False}, mode='load')


def prep_shared(inp):
    f = lambda a: np.ascontiguousarray(np.asarray(a, dtype=np.float32))
    b_mod = f(inp['b_mod'])
    sh = {}
    sh['w_mod'] = f(inp['w_mod'])
    sh['w_in'] = f(inp['w_in'])
    sh['w_out'] = f(inp['w_out'])
    sh['bmodT'] = f(b_mod[:, :2048].reshape(2, 16, 128).transpose(2, 0, 1))
    sh['bmodg'] = f(b_mod[:, 2048:3072])
    sh['gpost'] = f(inp['g_post'])
    pp = np.zeros((128, 2 * PL), np.float32)
    for l in range(2):
        o = l * PL
        def put(key, arr, n):
            pp[:, o + PC[key]: o + PC[key] + n] = np.asarray(arr, np.float32).reshape(n, 128).T
        put('G_PRE', inp['g_pre'][l], 8)
        put('M_NORM', inp['m_norm'][l], 3)
        put('R_MU', inp['r_mu'][l], 11)
        put('R_W0', np.asarray(inp['r_w0'][l]).reshape(-1), 6)
        put('R_A0', np.asarray(inp['r_a0'][l]).reshape(-1), 6)
        put('R_KK', inp['r_kk'][l], 3)
        put('R_KA', inp['r_ka'][l], 3)
        put('R_RK', inp['r_rk'][l], 3)
        put('R_NORM', inp['r_norm'][l], 3)
        put('L_CONV', np.asarray(inp['l_conv'][l]).reshape(-1), 8)
        put('L_CONVB', inp['l_conv_b'][l], 2)
        put('L_BA', np.asarray(inp['l_ba'][l]).reshape(-1), 4)
        put('L_BX', np.asarray(inp['l_bx'][l]).reshape(-1), 4)
        put('L_LAM', np.asarray(inp['l_lambda'][l]).reshape(-1), 4)
        pp[0:6, o + PC['M_BI']: o + PC['M_BI'] + 2] = np.asarray(inp['m_bi'][l], np.float32).T
        pp[0:6, o + PC['M_BF']: o + PC['M_BF'] + 2] = np.asarray(inp['m_bf'][l], np.float32).T
    sh['pp'] = pp
    sh['cst'] = make_consts()
    lora = np.zeros((128, 2, 2, 384), np.float32)
    for wi, key in enumerate(['r_w2', 'r_a2']):
        a = np.asarray(inp[key], np.float32)
        lora[:, :, wi, :] = a.transpose(1, 2, 0, 3).reshape(128, 2, 384)
    sh['lora'] = lora
    lruw = np.zeros((128, 2, 8, 128), np.float32)
    for gi, key in enumerate(['l_wa', 'l_wx']):
        a = np.asarray(inp[key], np.float32)
        for l in range(2):
            for d in range(2):
                for pr in range(2):
                    for hb in range(2):
                        n = 2 * pr + hb
                        lruw[hb * 64:(hb + 1) * 64, l, (gi * 2 + d) * 2 + pr, hb * 64:(hb + 1) * 64] = a[l, d, n]
    sh['lruw'] = lruw
    return sh


def prep_core(inp, b, NP, TS):
    f = lambda a: np.ascontiguousarray(np.asarray(a, dtype=np.float32))
    m = {}
    xp = np.asarray(inp['x_prompt'], np.float32)[b * NP:(b + 1) * NP].reshape(NP * 256, D)
    if TS > 0:
        xs = np.asarray(inp['x_sample'], np.float32)[b]
        m['x_in'] = f(np.concatenate([xp, xs], 0))
    else:
        m['x_in'] = f(xp)
    cc = np.stack([np.asarray(inp['c_ctx'], np.float32), np.asarray(inp['c'], np.float32)[b]], -1)
    m['cc'] = f(cc.reshape(8, 128, 2).transpose(1, 0, 2))
    C = np.asarray(inp['state_mlstm_C'], np.float32)[b]
    n = np.asarray(inp['state_mlstm_n'], np.float32)[b]
    Cn = np.concatenate([C, n[..., None]], -1)
    Cn = Cn.reshape(2, 2, 3, 2, 64, 65).transpose(0, 1, 3, 4, 2, 5).reshape(2, 2, 128, 3, 65)
    m['st_mC'] = f(Cn)
    m['st_mm'] = f(np.asarray(inp['state_mlstm_m'], np.float32)[b].transpose(2, 0, 1))
    R = np.asarray(inp['state_rwkv'], np.float32)[b]
    R = R.transpose(0, 1, 2, 4, 3)
    R = R.reshape(2, 2, 3, 2, 64, 64).transpose(0, 1, 3, 4, 2, 5).reshape(2, 2, 128, 3, 64)
    m['st_rH'] = f(R)
    L = np.asarray(inp['state_rglru'], np.float32)[b]
    m['st_l'] = f(L.reshape(2, 2, 2, 128).transpose(3, 0, 1, 2))
    return m


def unpack_core(r, NP, TS):
    y = r['y_out']
    yp = y[:NP * 256].reshape(NP, 256, D)
    ys = y[NP * 256:]
    mC = r['o_mC'].reshape(NP, 2, 2, 2, 64, 3, 65).transpose(0, 1, 2, 5, 3, 4, 6).reshape(NP, 2, 2, 6, 64, 65)
    newC = np.ascontiguousarray(mC[..., :64])
    newn = np.ascontiguousarray(mC[..., 64])
    newm = r['o_mm'].reshape(NP, 2, 2, 6)
    rH = r['o_rH'].reshape(NP, 2, 2, 2, 64, 3, 64).transpose(0, 1, 2, 5, 3, 4, 6).reshape(NP, 2, 2, 6, 64, 64)
    newr = np.ascontiguousarray(rH.transpose(0, 1, 2, 3, 5, 4))
    newl = np.ascontiguousarray(r['o_l'].transpose(0, 1, 2, 4, 3).reshape(NP, 2, 2, 256))
    return yp, ys, newC, newn, newm, newr, newl


_NC_CACHE = {}


def kernel(**inputs):
    NP, TS = 4, 2048
    key = (NP, TS)
    if key not in _NC_CACHE:
        _NC_CACHE[key] = Builder(NP, TS).build()
    nc = _NC_CACHE[key]
    sh = prep_shared(inputs)
    in_maps = []
    for b in range(NCORES):
        m = dict(sh)
        m.update(prep_core(inputs, b, NP, TS))
        in_maps.append(m)
    res = run_bass_kernel_spmd(nc, in_maps, core_ids=list(range(NCORES)))
    outs = [unpack_core(r, NP, TS) for r in res.results]
    y_prompt = np.concatenate([o[0] for o in outs], 0)
    y_sample = np.stack([o[1] for o in outs], 0)
    cat = lambda i: np.concatenate([o[i] for o in outs], 0)
    return (y_prompt.astype(np.float32), y_sample.astype(np.float32), cat(2).astype(np.float32),
            cat(3).astype(np.float32), cat(4).astype(np.float32), cat(5).astype(np.float32), cat(6).astype(np.float32))
```

```python
import numpy as np
import concourse.bass as bass
import concourse.mybir as mybir
from concourse.bass_utils import run_bass_kernel_spmd

F32 = mybir.dt.float32
BF16 = mybir.dt.bfloat16
AF = mybir.ActivationFunctionType
ALU = mybir.AluOpType
AX = mybir.AxisListType

D = 1024
IN_COLS = 4248
EPS = 1e-6
DSC = 0.6065306597126334
TB = 256
NCORES = 8


def _prod(xs):
    r = 1
    for x in xs:
        r *= int(x)
    return r


class Sync:
    def __init__(self, nc, n_dma_sems=32):
        self.nc = nc
        self.engs = {'pe': nc.tensor, 'dve': nc.vector, 'act': nc.scalar,
                     'pool': nc.gpsimd, 'sp': nc.sync}
        self.sem = {}
        self.cnt = {}
        for e in ['pe', 'dve', 'act', 'pool']:
            self.sem[e] = nc.alloc_semaphore('sem_' + e)
            self.cnt[e] = 0
        self.seen = {e: {} for e in self.engs}
        self.dma_ring = [nc.alloc_semaphore('dq_%d' % i) for i in range(n_dma_sems)]
        self.dma_uses = [0] * n_dma_sems
        self.dma_next = 0
        self.rec = {}
        self.untracked = set()
        self.n_wait = 0
        self.n_ins = 0
        self.pstep_cache = {}
        self.sb_addr = {}

    def region(self, ap):
        t = ap.tensor
        name = t.name
        apl = [(int(s), int(c)) for (s, c) in ap.ap]
        off = int(ap.offset)
        if type(t).__name__.startswith('DRam'):
            lo = off + sum(min(0, s * (c - 1)) for s, c in apl)
            hi = off + sum(max(0, s * (c - 1)) for s, c in apl) + 1
            return (name, 0, 1, lo, hi)
        pstep = self.pstep_cache.get(name)
        if pstep is None:
            pstep = _prod(list(t.shape)[1:])
            self.pstep_cache[name] = pstep
        p0 = off // pstep
        f0 = off % pstep
        npart = apl[0][1]
        rest = apl[1:]
        lo = f0 + sum(min(0, s * (c - 1)) for s, c in rest)
        hi = f0 + sum(max(0, s * (c - 1)) for s, c in rest) + 1
        if name in self.sb_addr:
            base, es = self.sb_addr[name]
            return ('SB', p0, p0 + npart, base + lo * es, base + hi * es)
        return ('PS:' + name, (p0 // 32) * 32, ((p0 + npart + 31) // 32) * 32, (lo // 512) * 512, ((hi + 511) // 512) * 512)

    @staticmethod
    def _ovl(a, b):
        return a[1] < b[2] and b[1] < a[2] and a[3] < b[4] and b[3] < a[4]

    @staticmethod
    def _contains(a, b):
        return a[1] <= b[1] and b[2] <= a[2] and a[3] <= b[3] and b[4] <= a[4]

    def _collect(self, e, reads, writes):
        deps = {}
        own = self.sem.get(e)
        rregs = [self.region(a) for a in reads]
        wregs = [self.region(a) for a in writes]
        for r in rregs:
            if r[0] in self.untracked:
                continue
            isps = r[0].startswith('PS:')
            for (reg, kind, sem, val) in self.rec.get(r[0], ()):
                if (kind == 'w' or (isps and sem is not own)) and self._ovl(reg, r):
                    if e == 'pe' and sem is own:
                        continue
                    k = id(sem)
                    if deps.get(k, (None, 0))[1] < val:
                        deps[k] = (sem, val)
        for w in wregs:
            if w[0] in self.untracked:
                continue
            for (reg, kind, sem, val) in self.rec.get(w[0], ()):
                if self._ovl(reg, w):
                    if sem is own:
                        continue
                    k = id(sem)
                    if deps.get(k, (None, 0))[1] < val:
                        deps[k] = (sem, val)
        return deps, rregs, wregs

    def _record(self, rregs, wregs, sem, val):
        for r in rregs:
            if r[0] in self.untracked:
                continue
            lst = self.rec.setdefault(r[0], [])
            lst[:] = [x for x in lst if not (x[1] == 'r' and x[2] is sem and self._contains(r, x[0]))]
            lst.append((r, 'r', sem, val))
        for w in wregs:
            if w[0] in self.untracked:
                continue
            lst = self.rec.setdefault(w[0], [])
            lst[:] = [x for x in lst if not self._contains(w, x[0])]
            lst.append((w, 'w', sem, val))

    def wait(self, e, sem, val):
        k = id(sem)
        if self.seen[e].get(k, 0) >= val:
            return
        self.engs[e].wait_ge(sem, val)
        self.seen[e][k] = val
        self.n_wait += 1

    max_ins = None
    paranoid = False

    def emit(self, e, reads, writes, build, inc=True):
        if self.max_ins is not None and self.n_ins >= self.max_ins:
            raise StopIteration
        deps, rregs, wregs = self._collect(e, reads, writes)
        for (sem, val) in deps.values():
            self.wait(e, sem, val)
        if self.paranoid:
            for e2 in ['pe', 'dve', 'act', 'pool']:
                if self.cnt[e2] > 0 and not (e == 'pe' and e2 == 'pe'):
                    self.wait(e, self.sem[e2], self.cnt[e2])
        ins = build(self.engs[e])
        self.n_ins += 1
        if inc:
            self.cnt[e] += 1
            ins.then_inc(self.sem[e], 1)
            val = self.cnt[e]
        else:
            val = self.cnt[e] + 1
        self._record(rregs, wregs, self.sem[e], val)
        return ins

    def dma(self, q, out, in_, **kw):
        if self.max_ins is not None and self.n_ins >= self.max_ins:
            raise StopIteration
        i = self.dma_next
        self.dma_next = (i + 1) % len(self.dma_ring)
        sem = self.dma_ring[i]
        uses = self.dma_uses[i]
        if uses > 0:
            self.wait(q, sem, 16 * uses)
        deps, rregs, wregs = self._collect(q, [in_], [out])
        for (s, v) in deps.values():
            self.wait(q, s, v)
        ins = self.engs[q].dma_start(out=out, in_=in_, **kw)
        ins.then_inc(sem, 16)
        self.n_ins += 1
        self.dma_uses[i] = uses + 1
        self._record(rregs, wregs, sem, 16 * (uses + 1))
        return ins

    def finish(self, q='sp'):
        for i, sem in enumerate(self.dma_ring):
            if self.dma_uses[i] > 0:
                self.wait(q, sem, 16 * self.dma_uses[i])
        for e in ['pe', 'dve', 'act', 'pool']:
            if self.cnt[e] > 0:
                self.wait(q, self.sem[e], self.cnt[e])


class Arena:
    def __init__(self, base, size):
        self.base, self.size, self.ptr = base, size, 0

    def take(self, nbytes):
        off = (self.ptr + 31) // 32 * 32
        self.ptr = off + nbytes
        assert self.ptr <= self.size, ("arena overflow", self.ptr, self.size)
        return self.base + off


PL = 72
PC = dict(G_PRE=0, M_NORM=8, R_MU=11, R_W0=22, R_A0=28, R_KK=34, R_KA=37, R_RK=40, R_NORM=43,
          L_CONV=46, L_CONVB=54, L_BA=56, L_BX=60, L_LAM=64, M_BI=68, M_BF=70)
DL = 32
DC = dict(OMKA=0, CLAM=3, C2LAM=7, NBF=11, OMMU=16)
CC = dict(IDENT=0, BONES=128, MKF=256, MKB=384, MNF=512, MNB=576, MUI=640, MLI=704, RMASK=768,
          LSEL=1536, PSEL=1664, NMKF=1668, NMKB=1796, ONES=1924, ISTK=2052)
NCST = 2052 + 64


def make_consts():
    c = np.zeros((128, NCST), np.float32)
    c[:, 0:128] = np.eye(128)
    c[0:64, 128:192] = 1.0
    c[64:128, 192:256] = 1.0
    s = np.arange(64)[:, None]
    t = np.arange(64)[None, :]
    us, ui = (s < t).astype(np.float32), (s <= t).astype(np.float32)
    ls, li = (s > t).astype(np.float32), (s >= t).astype(np.float32)
    c[0:64, 256:320], c[0:64, 320:384] = us, ui
    c[0:64, 384:448], c[0:64, 448:512] = ls, li
    c[0:64, 512:576] = -ls
    c[0:64, 576:640] = -us
    c[0:64, 640:704] = ui
    c[0:64, 704:768] = li
    rm = np.ones(768, np.float32)
    rm[::64] = 0.0
    c[:, 768:1536] = rm[None, :]
    for k in range(6):
        c[k, 1536 + (k % 2) * 64: 1536 + (k % 2) * 64 + 64] = 1.0
        c[k, 1664 + k // 2] = 1.0
    c[0:64, 1668:1796] = -c[0:64, 256:384]
    c[0:64, 1796:1924] = -c[0:64, 384:512]
    c[:, 1924:2052] = 1.0
    for (a, b) in ((256, 768), (1668, 1924)):
        c[64:128, a:b] = c[0:64, a:b]
    c[0:64, 2052:2116] = np.eye(64)
    c[64:128, 2052:2116] = np.eye(64)
    return c


class Builder:
    def __init__(self, NP, TS, debug=False, stop=None):
        self.stop = stop
        self.NP, self.TS = NP, TS
        self.NTOK = NP * 256 + TS
        self.debug = debug
        nc = self.nc = bass.Bass("TRN2", target_bir_lowering=False)
        self.S = Sync(nc)
        self._decl_dram()
        self._alloc()

    def _decl_dram(self):
        nc, NP, TS = self.nc, self.NP, self.TS
        di = lambda n, s: nc.dram_tensor(n, list(s), F32, kind="ExternalInput").ap()
        do = lambda n, s: nc.dram_tensor(n, list(s), F32, kind="ExternalOutput").ap()
        dx = lambda n, s: nc.dram_tensor(n, list(s), F32, kind="Internal").ap()
        self.x_in = di("x_in", [self.NTOK, D])
        self.cc = di("cc", [128, 8, 2])
        self.w_mod = di("w_mod", [2, D, 3 * D])
        self.bmodT = di("bmodT", [128, 2, 16])
        self.bmodg = di("bmodg", [2, D])
        self.gpost = di("gpost", [2, D])
        self.w_in = di("w_in", [2, D, IN_COLS])
        self.w_out = di("w_out", [2, D, D])
        self.pp = di("pp", [128, 2 * PL])
        self.cst = di("cst", [128, NCST])
        self.lora = di("lora", [128, 2, 2, 384])
        self.lruw = di("lruw", [128, 2, 8, 128])
        self.st_mC = di("st_mC", [2, 2, 128, 3, 65])
        self.st_mm = di("st_mm", [6, 2, 2])
        self.st_rH = di("st_rH", [2, 2, 128, 3, 64])
        self.st_l = di("st_l", [128, 2, 2, 2])
        for n in ["x_in", "cc", "w_mod", "bmodT", "bmodg", "gpost", "w_in", "w_out", "pp", "cst", "lora",
                  "lruw", "st_mC", "st_mm", "st_rH", "st_l"]:
            self.S.untracked.add(n)
        self.y_out = do("y_out", [self.NTOK, D])
        self.o_mC = do("o_mC", [NP, 2, 2, 128, 3, 65])
        self.o_mm = do("o_mm", [NP, 2, 2, 6, 1])
        self.o_rH = do("o_rH", [NP, 2, 2, 128, 3, 64])
        self.o_l = do("o_l", [NP, 2, 2, 128, 2])
        self.x1 = dx("x1", [self.NTOK, D])
        self.sHB = dx("sHB", [max(TS, 64), 384])
        self.sYB = dx("sYB", [max(TS // 256, 1), 128, 768])
        self.sLB = dx("sLB", [128, 2, max(TS, 64)])
        nb = max(TS // 256, 1)
        dxt = lambda n, s_, dt: nc.dram_tensor(n, list(s_), dt, kind="Internal").ap()
        self.sc = {
            'QT': dxt("sc_QT", [nb, 128, 3, 256], BF16), 'KT': dxt("sc_KT", [nb, 128, 3, 256], BF16),
            'KTOK': dxt("sc_KTOK", [nb, 64, 4, 384], BF16), 'VAUG': dxt("sc_VAUG", [nb, 64, 4, 6, 65], BF16),
            'GOZ': dxt("sc_GOZ", [nb, 128, 3, 256], F32), 'GI': dxt("sc_GI", [nb, 6, 256], F32),
            'LF': dxt("sc_LF", [nb, 6, 256], F32), 'XC': dxt("sc_XC", [nb, 128, 2, 256], F32),
            'LZT': dxt("sc_LZT", [nb, 128, 2, 256], F32), 'BLK': dxt("sc_BLK", [nb, 128, 11, 256], F32),
            'RZT': dxt("sc_RZT", [nb, 128, 3, 256], F32), 'KH': dxt("sc_KH", [nb, 128, 3, 256], F32),
            'VSTK': dxt("sc_VSTK", [nb, 128, 4, 3, 64], BF16),
        }
        if self.debug:
            self.dbg = do("dbg", [128, 32768])
            self.dbg_map = {}
            self.dbg_off = 0

    def T(self, name, shape, dtype, arena):
        es = 2 if dtype == BF16 else 4
        nb = _prod(shape[1:]) * es
        off = arena.take(nb)
        t = self.nc.alloc_sbuf_tensor_at(name, list(shape), dtype, offset=off)
        self.S.sb_addr[t.name] = (off, es)
        return t

    def _alloc(self):
        nc = self.nc
        B0 = 16384 + 256
        LIM = 224 * 1024 - 256
        P = Arena(B0, LIM - B0)
        T = self.T
        self.W_IN = T("W_IN", [128, 8, IN_COLS], BF16, P)
        self.W_OUT = T("W_OUT", [128, 8, D], BF16, P)
        self.CST = T("CST", [128, NCST], F32, P)
        self.PP = T("PP", [128, 2 * PL], F32, P)
        self.DV = T("DV", [128, 2 * DL], F32, P)
        self.LORA = T("LORA", [128, 2, 2, 384], F32, P)
        self.LRUW = T("LRUW", [128, 2, 8, 128], F32, P)
        self.RKD = T("RKD", [128, 2, 3, 128], F32, P)
        self.GS = T("GS", [128, 2, 8], F32, P)
        self.SH = T("SH", [128, 2, 8], F32, P)
        self.GATEB = T("GATEB", [128, 2, D], F32, P)
        self.IDB = T("IDB", [128, 128], BF16, P)
        self.ISTKB = T("ISTKB", [128, 64], BF16, P)
        self.mC = [T("mC%d" % d, [128, 3, 65], F32, P) for d in range(2)]
        self.mM = [T("mM%d" % d, [6, 1], F32, P) for d in range(2)]
        self.rH = [T("rH%d" % d, [128, 3, 64], F32, P) for d in range(2)]
        self.lS = [T("lS%d" % d, [128, 2], F32, P) for d in range(2)]
        XB = Arena(P.take(16384), 16384)
        self.HT = T("HT", [128, 8, 384], BF16, P)
        self.MIXT = T("MIXT", [128, 8, 256], BF16, P)
        self.BLK = T("BLK", [128, 11, 256], F32, P)
        abase = P.take(0)
        asize = P.size - P.ptr
        self.asize = asize
        mk = lambda: Arena(abase, asize)
        a = Arena(XB.base, XB.size)
        self.XW = [T("XW%d" % i, [128, D], F32, a) for i in range(2)]
        self.XN = [T("XN%d" % i, [128, D], F32, a) for i in range(2)]
        a = Arena(XB.base, XB.size)
        self.YB = T("YB", [128, 4, 3, 64], F32, a)
        self.RZT = T("RZT", [128, 3, 256], F32, a)
        self.WSTG = [T("WSTG0", [128, 8, 256], F32, Arena(self.S.sb_addr[self.BLK.name][0], 11264)),
                     T("WSTG1", [128, 8, 256], F32, Arena(XB.base, 8192)),
                     T("WSTG2", [128, 8, 256], F32, Arena(XB.base + 8192, 8192))]
        a = mk()
        self.WM = T("WM", [128, 8, 512], F32, a)
        self.SCT = T("SCT", [128, 8, 2], F32, a)
        self.SCB = T("SCB", [128, 2, 8, 128], F32, a)
        self.MODT = T("MODT", [128, 16, 2], F32, a)
        self.BMT = T("BMT", [128, 2, 16], F32, a)
        self.BG = T("BG", [128, D], F32, a)
        self.GP = T("GP", [128, D], F32, a)
        self.TMPS = T("TMPS", [128, 32], F32, a)
        self.MODROW = T("MODROW", [2, 512], F32, a)
        a = mk()
        self.QT = T("QT", [128, 3, 256], BF16, a)
        self.KT = T("KT", [128, 3, 256], BF16, a)
        self.GOZ = T("GOZ", [128, 3, 256], F32, a)
        self.KTOK = T("KTOK", [64, 4, 384], BF16, a)
        self.VAUG = T("VAUG", [64, 4, 6, 65], BF16, a)
        self.HB = T("HB", [64, 4, 384], F32, a)
        self.mg = []
        for d in range(2):
            g = {}
            for n in ["gi", "lf", "pre", "bb", "gg", "ee", "fl"]:
                g[n] = T("mg_%s%d" % (n, d), [6, 256], F32, a)
            for n in ["mx", "mch", "mprev", "MM", "dec"]:
                g[n] = T("mg_%s%d" % (n, d), [6, 4], F32, a)
            g["X2"] = T("mg_X2%d" % d, [6, 4, 3], F32, a)
            g["etok"] = T("mg_etok%d" % d, [64, 4, 6], F32, a)
            g["fltok"] = T("mg_fltok%d" % d, [64, 4, 6], F32, a)
            g["decb"] = T("mg_decb%d" % d, [128, 4, 3], F32, a)
            self.mg.append(g)
        self.STSB = [T("STSB%d" % i, [64, 6, 64], BF16, a) for i in range(2)]
        self.VP = [T("VP%d" % i, [64, 6, 65], BF16, a) for i in range(2)]
        self.CDEC = T("CDEC", [128, 3, 65], F32, a)
        self.CDBF = T("CDBF", [128, 3, 65], BF16, a)
        self.DN = T("DN", [64, 6], F32, a)
        self.RDN = T("RDN", [64, 6], F32, a)
        self.HD = T("HD", [64, 6, 64], F32, a)
        self.SQ = T("SQ", [64, 4, 384], F32, a)
        self.SSQ = T("SSQ", [64, 24], F32, a)
        self.RSTD = T("RSTD", [64, 24], F32, a)
        self.ZT = [T("ZT%d" % i, [128, 256], F32, a) for i in range(2)]
        a = mk()
        self.XL = T("XL", [128, 2, 392], F32, a)
        self.XC = T("XC", [128, 2, 256], F32, a)
        self.LZT = T("LZT", [128, 2, 256], F32, a)
        self.ltd = [{n: T("lt%d_%s" % (d_, n), [128, 2, 256], F32, a) for n in ["rg", "ig", "aa", "a2", "bt"]} for d_ in range(2)]
        self.HL = [T("HL%d" % d, [128, 2, 256], F32, a) for d in range(2)]
        a = mk()
        self.URS = T("URS", [128, 11, 384], F32, a)
        a = mk()
        self.KH = T("KH", [128, 3, 256], F32, a)
        R1 = a.take(7 * 3072)
        a1 = Arena(R1, 7 * 3072)
        self.rt = {n: T("rt_" + n, [128, 3, 256], F32, a1) for n in ["sg", "aa", "kt", "bb", "cs", "d", "E"]}
        a2 = Arena(R1, 7 * 3072)
        self.A1BD = T("A1BD", [128, 3, 2, 128], BF16, a2)
        self.A2BD = T("A2BD", [128, 3, 2, 128], BF16, a2)
        self.NN = [T("NN%d" % i, [128, 3, 128], BF16, a2) for i in range(2)]
        self.NTT = [T("NTT%d" % i, [128, 3, 128], BF16, a2) for i in range(2)]
        self.U = T("U", [128, 3, 64], F32, a2)
        self.UBF = T("UBF", [128, 3, 64], BF16, a2)
        self.HBF = T("HBF", [128, 3, 64], BF16, a2)
        self.HTMP = T("HTMP", [128, 3, 64], F32, a2)
        self.YBD = T("YBD", [128, 3, 128], F32, a2)
        self.KTTOK = T("KTTOK", [128, 4, 3, 128], BF16, a2)
        self.BTTOK = T("BTTOK", [128, 4, 3, 128], BF16, a2)
        self.rSSQ = T("rSSQ", [128, 12], F32, a2)
        self.rRSTD = T("rRSTD", [128, 12], F32, a2)
        self.KR = T("KR", [128, 3, 4, 2, 64], BF16, a)
        self.BTT = T("BTT", [128, 3, 256], BF16, a)
        self.KTT = T("KTT", [128, 3, 256], BF16, a)
        bdr = a.take(4 * 3072)
        a3 = Arena(bdr, 4 * 3072)
        self.BD = {n: T("BD_" + n, [128, 3, 4, 128], BF16, a3) for n in ["kt", "b", "kh", "r"]}
        a3 = Arena(bdr, 4 * 3072)
        self.ft = {n: T("ft_" + n, [128, 3, 256], F32, a3) for n in ["rk", "bon", "t1"]}
        self.YSQ = T("YSQ", [128, 4, 3, 64], F32, a3)
        self.BDV = T("BDV", [128, 3, 4, 128], BF16, a)
        self.VSTK = T("VSTK", [128, 4, 3, 64], BF16, a)
        self.TWL = T("TWL", [128, 256], F32, a)
        self.GL = T("GL", [128, 3, 4], F32, a)
        self.PA = nc.alloc_psum_tensor("PA", [128, 1024], F32)
        self.PB = nc.alloc_psum_tensor("PB", [128, 1024], F32)
        self.PQ = [nc.alloc_psum_tensor("PQ%d" % i, [128, 512], F32) for i in range(3)]
        self.PTB = nc.alloc_psum_tensor("PTB", [128, 1024], BF16)
        self.pq_i = 0

    def pq(self):
        t = self.PQ[self.pq_i % 3]
        self.pq_i += 1
        return t

    def V(self, t, p0, npart, off, dims):
        pstep = _prod(list(t.shape)[1:])
        return bass.AP(t, p0 * pstep + off, [[pstep, npart]] + [list(d) for d in dims])

    def tt(self, e, out, in0, in1, op):
        return self.S.emit(e, [in0, in1], [out], lambda g: g.tensor_tensor(out=out, in0=in0, in1=in1, op=op))

    def ts(self, e, out, in0, s1, s2, op0, op1=None):
        rd = [in0] + [s for s in (s1, s2) if not isinstance(s, (int, float)) and s is not None]
        if op1 is None:
            return self.S.emit(e, rd, [out], lambda g: g.tensor_scalar(out=out, in0=in0, scalar1=s1, scalar2=None, op0=op0))
        return self.S.emit(e, rd, [out], lambda g: g.tensor_scalar(out=out, in0=in0, scalar1=s1, scalar2=s2, op0=op0, op1=op1))

    def stt(self, out, in0, sc, in1, op0, op1):
        rd = [in0, in1] + ([] if isinstance(sc, (int, float)) else [sc])
        return self.S.emit('dve', rd, [out], lambda g: g.scalar_tensor_tensor(out=out, in0=in0, scalar=sc, in1=in1, op0=op0, op1=op1))

    def act(self, out, in_, func, bias=None, scale=None, accum=None):
        rd = [in_]
        kw = {}
        if bias is not None:
            kw['bias'] = bias
            if not isinstance(bias, (int, float)):
                rd.append(bias)
        if scale is not None:
            kw['scale'] = scale
            if not isinstance(scale, (int, float)):
                rd.append(scale)
        wr = [out]
        if accum is not None:
            kw['accum_out'] = accum
            wr.append(accum)
        return self.S.emit('act', rd, wr, lambda g: g.activation(out=out, in_=in_, func=func, **kw))

    def cp(self, e, out, in_):
        if e == 'act':
            return self.act(out, in_, AF.Copy)
        return self.S.emit(e, [in_], [out], lambda g: g.tensor_copy(out=out, in_=in_))

    def _pe_rowtile_guard(self, lhsT, out):
        S = self.S
        st = S.region(lhsT)
        k = st[2] - st[1]
        kr = 32 if k <= 32 else (64 if k <= 64 else 128)
        rows = (st[1], st[1] + kr)
        oreg = S.region(out)
        last = getattr(self, '_last_pe', None)
        if last is not None:
            lrows, loreg, lins, linc = last
            disjoint = rows[1] <= lrows[0] or lrows[1] <= rows[0]
            samebank = (loreg[0] == oreg[0]) and loreg[3] < oreg[4] and oreg[3] < loreg[4]
            if disjoint and samebank:
                if not linc:
                    S.cnt['pe'] += 1
                    lins.then_inc(S.sem['pe'], 1)
                S.wait('pe', S.sem['pe'], S.cnt['pe'])
        return rows, oreg

    def mm(self, out, lhsT, rhs, start=True, stop=True, inc=None):
        if inc is None:
            inc = stop
        rows, oreg = self._pe_rowtile_guard(lhsT, out)
        ins = self.S.emit('pe', [lhsT, rhs], [out],
                          lambda g: g.matmul(out, lhsT=lhsT, rhs=rhs, start=start, stop=stop), inc=inc)
        self._last_pe = (rows, oreg, ins, inc)
        return ins

    def tr(self, out, in_, inc=True, bf=False):
        n = in_.shape[0]
        ident = self.IDB[0:n, 0:n] if bf else self.CST[0:n, CC['IDENT']:CC['IDENT'] + n]
        rows, oreg = self._pe_rowtile_guard(in_, out)
        ins = self.S.emit('pe', [in_, ident], [out],
                          lambda g: g.transpose(out=out, in_=in_, identity=ident), inc=inc)
        self._last_pe = (rows, oreg, ins, inc)
        return ins

    def memset(self, e, ap, v):
        return self.S.emit(e, [], [ap], lambda g: g.memset(ap, v))

    def scan(self, out, d0, d1, init, op0, op1):
        rd = [d0, d1] + ([] if isinstance(init, (int, float)) else [init])
        return self.S.emit('dve', rd, [out], lambda g: g.tensor_tensor_scan(out=out, data0=d0, data1=d1, initial=init, op0=op0, op1=op1))

    def recip(self, out, in_):
        return self.S.emit('dve', [in_], [out], lambda g: g.reciprocal(out=out, in_=in_))

    def reduce(self, out, in_, op, axis=AX.X):
        return self.S.emit('dve', [in_], [out], lambda g: g.tensor_reduce(out=out, in_=in_, axis=axis, op=op))

    def dump(self, name, ap):
        if not self.debug or name in self.dbg_map:
            return
        if getattr(self, 'dbg_filter', None) is not None and not any(name.startswith(p) for p in self.dbg_filter):
            return
        shp = list(ap.shape)
        npart, nfree = shp[0], _prod(shp[1:])
        stage = self.XN[1]
        assert nfree <= 1024
        dst = self.V(stage, 0, npart, 0, [[_prod(shp[i + 1:]), shp[i]] for i in range(1, len(shp))])
        self.cp('dve', dst, ap)
        self.S.dma('sp', self.dbg[0:npart, self.dbg_off:self.dbg_off + nfree], stage[0:npart, 0:nfree], allow_slow_non_contiguous=True)
        self.dbg_map[name] = (self.dbg_off, npart, shp[1:])
        self.dbg_off += nfree

    def ppc(self, l, key, j=0, rows=128):
        c = l * PL + PC[key] + j
        return self.PP[0:rows, c:c + 1]

    def dvc(self, l, key, j=0, rows=128):
        c = l * DL + DC[key] + j
        return self.DV[0:rows, c:c + 1]

    def setup(self):
        S = self.S
        S.dma('sp', self.CST[:, :], self.cst[:, :])
        S.dma('sp', self.PP[:, :], self.pp[:, :])
        S.dma('sp', self.LORA[:, :, :, :], self.lora[:, :, :, :])
        S.dma('sp', self.LRUW[:, :, :, :], self.lruw[:, :, :, :])
        self.memset('dve', self.DV[:, :], 0.0)
        self.cp('dve', self.IDB[:, :], self.CST[:, CC['IDENT']:CC['IDENT'] + 128])
        self.cp('dve', self.ISTKB[:, :], self.CST[:, CC['ISTK']:CC['ISTK'] + 64])
        for l in range(2):
            self.ts('dve', self.DV[:, l * DL + DC['OMKA']: l * DL + DC['OMKA'] + 3],
                    self.PP[:, l * PL + PC['R_KA']: l * PL + PC['R_KA'] + 3], -1.0, 1.0, ALU.mult, ALU.add)
            self.ts('dve', self.DV[:, l * DL + DC['OMMU']: l * DL + DC['OMMU'] + 11],
                    self.PP[:, l * PL + PC['R_MU']: l * PL + PC['R_MU'] + 11], -1.0, 1.0, ALU.mult, ALU.add)
            lam = self.PP[:, l * PL + PC['L_LAM']: l * PL + PC['L_LAM'] + 4]
            t0 = self.TMPS[:, 0:4]
            self.act(t0, lam, AF.Exp, scale=-1.0)
            self.act(t0, t0, AF.Ln, bias=1.0)
            self.ts('dve', self.DV[:, l * DL + DC['CLAM']: l * DL + DC['CLAM'] + 4], t0, -8.0, None, ALU.mult)
            self.ts('dve', self.DV[:, l * DL + DC['C2LAM']: l * DL + DC['C2LAM'] + 4], t0, -16.0, None, ALU.mult)
            self.ts('dve', self.DV[0:6, l * DL + DC['NBF']: l * DL + DC['NBF'] + 2],
                    self.PP[0:6, l * PL + PC['M_BF']: l * PL + PC['M_BF'] + 2], -1.0, None, ALU.mult)
            for hp in range(3):
                self.ts('dve', self.RKD[:, l, hp, :], self.CST[:, CC['BONES']:CC['BONES'] + 128],
                        self.ppc(l, 'R_RK', hp), None, ALU.mult)
        self.memset('dve', self.XL[:, :, 0:2], 0.0)

    def load_layer(self, l):
        S = self.S
        wsrc = self.w_in[l].rearrange("(kc p) c -> p kc c", p=128)
        wo = self.w_out[l].rearrange("(kc p) c -> p kc c", p=128)
        pieces = [(self.W_IN, wsrc, c0, min(256, IN_COLS - c0)) for c0 in range(0, IN_COLS, 256)]
        pieces += [(self.W_OUT, wo, c0, 256) for c0 in range(0, D, 256)]
        for i, (dst, src, c0, n) in enumerate(pieces):
            stg = self.WSTG[i % 3]
            S.dma('sp', stg[:, :, 0:n], src[:, :, c0:c0 + n])
            self.cp('pool', dst[:, :, c0:c0 + n], stg[:, :, 0:n])
        if self.stop == 'load_w':
            raise StopIteration
        S.dma('sp', self.SCT[:, :, :], self.cc[:, :, :])
        S.dma('sp', self.BMT[:, :, :], self.bmodT[:, :, :])
        self.act(self.SCT[:, :, :], self.SCT[:, :, :], AF.Silu)
        for m in range(2):
            for kc in range(8):
                src = self.V(self.SCT, 0, 128, kc * 2 + m, [[0, 128]])
                self.cp('dve', self.SCB[:, m, kc, :], src)
        wm = self.w_mod[l].rearrange("(kc p) c -> p kc c", p=128)
        for blk in range(6):
            S.dma('sp', self.WM[:, :, :], wm[:, :, blk * 512:(blk + 1) * 512])
            if blk < 4:
                ps = self.pq()
                for kc in range(8):
                    self.mm(ps[0:2, :], self.SCT[:, kc, :], self.WM[:, kc, :], start=(kc == 0), stop=(kc == 7))
                self.cp('act', self.MODROW[0:2, :], ps[0:2, :])
                ps2 = self.pq()
                for j in range(4):
                    self.tr(ps2[:, j * 2:j * 2 + 2], self.MODROW[0:2, j * 128:(j + 1) * 128])
                o = self.MODT[:, blk * 4:(blk + 1) * 4, :]
                bsrc = self.V(self.BMT, 0, 128, l * 16 + blk * 4, [[1, 4], [0, 2]])
                self.tt('dve', o, self.V(ps2, 0, 128, 0, [[2, 4], [1, 2]]), bsrc, ALU.add)
            else:
                half = blk - 4
                for m in range(2):
                    ps = self.pq()
                    for kc in range(8):
                        self.mm(ps[:, :], self.SCB[:, m, kc, :], self.WM[:, kc, :], start=(kc == 0), stop=(kc == 7))
                    self.cp('act', self.GATEB[:, m, half * 512:(half + 1) * 512], ps[:, :])
        if self.stop == 'load_m':
            raise StopIteration
        for m in range(2):
            self.cp('dve', self.SH[:, m, :], self.V(self.MODT, 0, 128, m, [[2, 8]]))
            t0 = self.TMPS[:, 8:16]
            self.ts('dve', t0, self.V(self.MODT, 0, 128, 16 + m, [[2, 8]]), 1.0, None, ALU.add)
            self.tt('dve', self.GS[:, m, :], t0, self.PP[:, l * PL + PC['G_PRE']: l * PL + PC['G_PRE'] + 8], ALU.mult)
        if self.stop == 'load_g':
            raise StopIteration
        S.dma('sp', self.BG[:, :], bass.AP(self.bmodg.tensor, l * D, [[0, 128], [1, D]]))
        S.dma('sp', self.GP[:, :], bass.AP(self.gpost.tensor, l * D, [[0, 128], [1, D]]))
        if self.stop == 'load_b':
            raise StopIteration
        for m in range(2):
            self.tt('dve', self.GATEB[:, m, :], self.GATEB[:, m, :], self.BG[:, :], ALU.add)
            self.tt('dve', self.GATEB[:, m, :], self.GATEB[:, m, :], self.GP[:, :], ALU.mult)

    def proj_fm(self, c0, ncols, t_off, ntok, evac):
        ps = self.pq()
        for kc in range(8):
            self.mm(ps[0:ncols, 0:ntok], self.W_IN[:, kc, c0:c0 + ncols], self.HT[:, kc, t_off:t_off + ntok],
                    start=(kc == 0), stop=(kc == 7))
        evac(ps[0:ncols, 0:ntok])

    def proj_tm(self, c0, ncols, t_off, evac):
        ps = self.pq()
        for kc in range(8):
            self.mm(ps[0:64, 0:ncols], self.HT[:, kc, t_off:t_off + 64], self.W_IN[:, kc, c0:c0 + ncols],
                    start=(kc == 0), stop=(kc == 7))
        evac(ps[0:64, 0:ncols])

    def visit(self, l, mod, xsrc, xdst, row0, T, t0, dirs, grid, seq_idx, first, last, mode='full'):
        S = self.S
        w0 = max(0, t0 - 64)
        w1 = min(T, t0 + TB + 64)
        W = w1 - w0
        co = t0 - w0
        do_f = 0 in dirs
        prompt = (mod == 0)
        if mode != 'load':
            ntile = (W + 127) // 128
            for i in range(ntile):
                n = min(128, W - i * 128)
                xw, xn = self.XW[i % 2], self.XN[i % 2]
                r = row0 + w0 + i * 128
                S.dma('sp', xw[0:n, :], xsrc[r:r + n, :])
                ssq = self.TMPS[0:n, 16 + i:17 + i]
                self.act(xn[0:n, :], xw[0:n, :], AF.Square, accum=ssq)
                if self.stop == 'n1':
                    raise StopIteration
                rs = self.TMPS[0:n, 20 + i:21 + i]
                self.ts('dve', rs, ssq, 1.0 / D, EPS, ALU.mult, ALU.add)
                self.act(rs, rs, AF.Sqrt)
                self.recip(rs, rs)
                if self.stop == 'n2':
                    raise StopIteration
                self.act(xn[0:n, :], xw[0:n, :], AF.Copy, scale=rs)
                if self.stop == 'n3':
                    raise StopIteration
                for half in range(2):
                    ps = self.pq()
                    for j in range(4):
                        kc = half * 4 + j
                        self.tr(ps[:, j * 128:j * 128 + n], xn[0:n, kc * 128:(kc + 1) * 128])
                    if self.stop == 'n4':
                        raise StopIteration
                    for j in range(4):
                        kc = half * 4 + j
                        o = self.HT[:, kc, i * 128:i * 128 + n]
                        if half == 0:
                            self.ts('dve', o, ps[:, j * 128:j * 128 + n], self.GS[:, mod, kc:kc + 1], self.SH[:, mod, kc:kc + 1], ALU.mult, ALU.add)
                        else:
                            self.act(o, ps[:, j * 128:j * 128 + n], AF.Identity, bias=self.SH[:, mod, kc:kc + 1], scale=self.GS[:, mod, kc:kc + 1])
        if self.stop in ('norm', 'n5a', 'n5d'):
            raise StopIteration
        self.stage_mlstm(l, t0, co, dirs, prompt, seq_idx, first, last, mode)
        if self.stop == 'mlstm':
            raise StopIteration
        self.stage_lru(l, t0, co, W, w0, w1, T, dirs, prompt, seq_idx, first, last, mode)
        if self.stop == 'lru':
            raise StopIteration
        self.stage_rwkv(l, t0, co, W, w0, w1, T, dirs, grid, prompt, seq_idx, first, last, mode)
        for kc in range(8):
            self.dump('mix%d' % kc, self.MIXT[:, kc, :])
        if self.stop == 'rwkv':
            raise StopIteration
        if do_f:
            self.stage_out(l, mod, xsrc, xdst, row0 + t0)
        if self.stop == 'out':
            raise StopIteration

    def stage_mlstm(self, l, t0, co, dirs, prompt, seq_idx, first, last, mode='full'):
        S = self.S
        do_f = 0 in dirs
        if mode != 'load':
            for d in (sorted(set(dirs) | {0}) if mode == 'store' else dirs):
                g = self.mg[d]
                self.proj_fm(1920 + d * 6, 6, co, 256, lambda ps, g=g, d=d: self.act(g["gi"][:, :], ps, AF.Identity, bias=self.ppc(l, 'M_BI', d, 6)))
                def ev_f(ps, g=g, d=d):
                    self.act(g["lf"][:, :], ps, AF.Exp, bias=self.dvc(l, 'NBF', d, 6), scale=-1.0)
                    self.act(g["lf"][:, :], g["lf"][:, :], AF.Ln, bias=1.0)
                    self.ts('dve', g["lf"][:, :], g["lf"][:, :], -1.0, None, ALU.mult)
                self.proj_fm(1932 + d * 6, 6, co, 256, ev_f)
        bi = t0 // 256
        ml_items = [(self.QT[:, :, :], 'QT'), (self.KT[:, :, :], 'KT'), (self.KTOK[:, :, :], 'KTOK'), (self.VAUG[:, :, :, :], 'VAUG'),
                    (self.GOZ[:, :, :], 'GOZ'), (self.mg[0]["gi"][:, :], 'GI'), (self.mg[0]["lf"][:, :], 'LF')]
        if mode == 'load':
            for ap_, k_ in ml_items:
                S.dma('sp', ap_, self.sc[k_][bi])
        for d in dirs:
            if first[d]:
                if prompt:
                    self.memset('dve', self.mC[d][:, :, :], 0.0)
                    self.memset('dve', self.mM[d][:, :], 0.0)
                else:
                    S.dma('sp', self.mC[d][:, :, :], self.st_mC[l, d])
                    S.dma('sp', self.mM[d][:, :], self.st_mm[:, l, d:d + 1], allow_slow_non_contiguous=True)
        if do_f and not (1 in dirs):
            S.dma('sp', self.HB[:, :, :], self.sHB[t0:t0 + 256, :].rearrange("(c s) f -> s c f", s=64))
        for d in dirs:
            g = self.mg[d]
            v3 = lambda t: self.V(t, 0, 6, 0, [[64, 4], [1, 64]])
            self.scan(g["pre"][:, :], self.CST[0:6, CC['RMASK']:CC['RMASK'] + 256], g["lf"][:, :], 0.0, ALU.mult, ALU.add)
            bL = self.V(g["pre"], 0, 6, 63, [[64, 4]])
            bLb = self.V(g["pre"], 0, 6, 63, [[64, 4], [0, 64]])
            if d == 0:
                bsrc = g["pre"]
            else:
                self.tt('dve', v3(g["bb"]), bLb, v3(g["pre"]), ALU.subtract)
                self.tt('dve', g["bb"][:, :], g["bb"][:, :], g["lf"][:, :], ALU.add)
                bsrc = g["bb"]
            self.tt('dve', g["gg"][:, :], g["gi"][:, :], bsrc[:, :], ALU.subtract)
            self.reduce(g["mx"][:, :], v3(g["gg"]), ALU.max)
            if d == 0:
                mo, mxv, blv = g["mch"][:, :], g["mx"][:, :], bL
            else:
                mo = self.V(g["mch"], 0, 6, 3, [[-1, 4]])
                mxv = self.V(g["mx"], 0, 6, 3, [[-1, 4]])
                blv = self.V(g["pre"], 0, 6, 63 + 3 * 64, [[-64, 4]])
            self.scan(mo, mxv, blv, self.mM[d][:, 0:1], ALU.max, ALU.add)
            if d == 0:
                self.cp('dve', g["mprev"][:, 1:4], g["mch"][:, 0:3])
                self.cp('dve', g["mprev"][:, 0:1], self.mM[d][:, 0:1])
                mfin = g["mch"][:, 3:4]
            else:
                self.cp('dve', g["mprev"][:, 0:3], g["mch"][:, 1:4])
                self.cp('dve', g["mprev"][:, 3:4], self.mM[d][:, 0:1])
                mfin = g["mch"][:, 0:1]
            self.tt('dve', g["MM"][:, :], g["mprev"][:, :], g["mx"][:, :], ALU.max)
            self.tt('dve', g["dec"][:, :], g["mprev"][:, :], g["MM"][:, :], ALU.subtract)
            self.act(g["dec"][:, :], g["dec"][:, :], AF.Exp)
            self.cp('dve', self.mM[d][:, 0:1], mfin)
            MMb = self.V(g["MM"], 0, 6, 0, [[1, 4], [0, 64]])
            self.tt('dve', v3(g["ee"]), v3(g["gg"]), MMb, ALU.subtract)
            self.act(g["ee"][:, :], g["ee"][:, :], AF.Exp)
            self.tt('dve', v3(g["fl"]), v3(bsrc), MMb, ALU.add)
            self.act(g["fl"][:, :], g["fl"][:, :], AF.Exp, scale=-1.0)
        if mode != 'load':
            for hp in range(3):
                self.proj_fm(hp * 128, 128, co, 256, lambda ps, hp=hp: self.cp('act', self.QT[:, hp, :], ps))
                self.proj_fm(384 + hp * 128, 128, co, 256, lambda ps, hp=hp: self.act(self.KT[:, hp, :], ps, AF.Copy, scale=0.125))
            if do_f or mode == 'store':
                for hp in range(3):
                    def ev_o(ps, hp=hp):
                        self.act(self.GOZ[:, hp, :], ps, AF.Sigmoid)
                    self.proj_fm(1152 + hp * 128, 128, co, 256, ev_o)
                    def ev_z2(ps, hp=hp):
                        tz = self.ZT[hp % 2]
                        self.act(tz[:, :], ps, AF.Silu)
                        self.tt('dve', self.GOZ[:, hp, :], self.GOZ[:, hp, :], tz[:, :], ALU.mult)
                    self.proj_fm(1536 + hp * 128, 128, co, 256, ev_z2)
            for c in range(4):
                self.proj_tm(384, 384, co + c * 64, lambda ps, c=c: self.act(self.KTOK[:, c, :], ps, AF.Copy, scale=0.125))
                def ev_v(ps, c=c):
                    self.cp('dve', self.VAUG[:, c, :, 0:64], self.V(ps.tensor, 0, 64, 0, [[64, 6], [1, 64]]))
                self.proj_tm(768, 384, co + c * 64, ev_v)
            self.memset('dve', self.VAUG[:, :, :, 64:65], 1.0)
        for d in dirs:
            g = self.mg[d]
            ps = self.pq()
            for c in range(4):
                self.tr(ps[0:64, c * 6:c * 6 + 6], g["ee"][:, c * 64:(c + 1) * 64])
                self.tr(ps[0:64, 24 + c * 6:24 + c * 6 + 6], g["fl"][:, c * 64:(c + 1) * 64])
            self.cp('dve', g["etok"][:, :, :], self.V(ps, 0, 64, 0, [[6, 4], [1, 6]]))
            self.cp('dve', g["fltok"][:, :, :], self.V(ps, 0, 64, 24, [[6, 4], [1, 6]]))
            self.tt('dve', g["X2"][:, :, :], self.V(g["dec"], 0, 6, 0, [[1, 4], [0, 3]]),
                    self.V(self.CST, 0, 6, CC['PSEL'], [[0, 4], [1, 3]]), ALU.mult)
            ps2 = self.pq()
            self.mm(ps2[:, 0:12], self.CST[0:6, CC['LSEL']:CC['LSEL'] + 128], self.V(g["X2"], 0, 6, 0, [[1, 12]]))
            self.cp('dve', g["decb"][:, :, :], self.V(ps2, 0, 128, 0, [[3, 4], [1, 3]]))
        if mode == 'store':
            for ap_, k_ in ml_items:
                S.dma('sp', self.sc[k_][bi], ap_)
        for d in sorted(dirs, reverse=True):
            g = self.mg[d]
            mask = self.CST[0:64, CC['MUI']:CC['MUI'] + 64] if d == 0 else self.CST[0:64, CC['MLI']:CC['MLI'] + 64]
            maskb = self.V(self.CST, 0, 64, CC['MUI'] if d == 0 else CC['MLI'], [[0, 6], [1, 64]])
            for j in range(4):
                c = j if d == 0 else 3 - j
                cs = slice(c * 64, (c + 1) * 64)
                stsb, vp = self.STSB[j % 2], self.VP[j % 2]
                ps = self.pq()
                for h in (0, 2, 4, 1, 3, 5):
                    hp, pb = h // 2, 64 * (h % 2)
                    self.mm(ps[0:64, h * 64:(h + 1) * 64], self.KT[pb:pb + 64, hp, cs], self.QT[pb:pb + 64, hp, cs], inc=(h == 5))
                self.tt('dve', stsb[:, :, :], self.V(ps, 0, 64, 0, [[64, 6], [1, 64]]), maskb, ALU.mult)
                self.tt('dve', vp[:, :, :], self.VAUG[:, c, :, :], self.V(g["etok"], 0, 64, c * 6, [[1, 6], [0, 65]]), ALU.mult)
                self.tt('dve', self.CDEC[:, :, :], self.mC[d][:, :, :], self.V(g["decb"], 0, 128, c * 3, [[1, 3], [0, 65]]), ALU.mult)
                self.cp('act', self.CDBF[:, :, :], self.CDEC[:, :, :])
                ph = self.pq()
                for h in (0, 2, 4, 1, 3, 5):
                    hp, pb = h // 2, 64 * (h % 2)
                    o = ph[0:64, h * 65:(h + 1) * 65]
                    self.mm(o, stsb[:, h, :], vp[:, h, :], start=True, stop=False)
                    self.mm(o, self.QT[pb:pb + 64, hp, cs], self.CDBF[pb:pb + 64, hp, :], start=False, stop=True, inc=(h == 5))
                pc = self.pq()
                for h in range(6):
                    hp, pb = h // 2, 64 * (h % 2)
                    self.mm(pc[pb:pb + 64, hp * 65:(hp + 1) * 65], self.KTOK[:, c, h * 64:(h + 1) * 64], vp[:, h, :], inc=(h == 5))
                self.tt('dve', self.mC[d][:, :, :], self.CDEC[:, :, :], self.V(pc, 0, 128, 0, [[65, 3], [1, 65]]), ALU.add)
                self.act(self.DN[:, :], self.V(ph, 0, 64, 64, [[65, 6]]), AF.Abs)
                self.tt('dve', self.DN[:, :], self.DN[:, :], g["fltok"][:, c, :], ALU.max)
                self.recip(self.RDN[:, :], self.DN[:, :])
                hsrc = self.V(ph, 0, 64, 0, [[65, 6], [1, 64]])
                rb = self.V(self.RDN, 0, 64, 0, [[1, 6], [0, 64]])
                hbv = self.V(self.HB, 0, 64, c * 384, [[64, 6], [1, 64]])
                if d == 1:
                    self.tt('dve', hbv, hsrc, rb, ALU.mult)
                else:
                    self.tt('dve', self.HD[:, :, :], hsrc, rb, ALU.mult)
                    self.tt('dve', hbv, hbv, self.HD[:, :, :], ALU.add)
            if last[d] and prompt:
                S.dma('sp', self.o_mC[seq_idx, l, d], self.mC[d][:, :, :])
                S.dma('sp', self.o_mm[seq_idx, l, d], self.mM[d][:, 0:1])
        if not do_f:
            S.dma('sp', self.sHB[t0:t0 + 256, :].rearrange("(c s) f -> s c f", s=64), self.HB[:, :, :])
            return
        self.tt('dve', self.SQ[:, :, :], self.HB[:, :, :], self.HB[:, :, :], ALU.mult)
        self.reduce(self.SSQ[:, :], self.V(self.SQ, 0, 64, 0, [[64, 24], [1, 64]]), ALU.add)
        self.ts('dve', self.SSQ[:, :], self.SSQ[:, :], 1.0 / 64, EPS, ALU.mult, ALU.add)
        self.act(self.SSQ[:, :], self.SSQ[:, :], AF.Sqrt)
        self.recip(self.RSTD[:, :], self.SSQ[:, :])
        self.tt('dve', self.V(self.SQ, 0, 64, 0, [[64, 24], [1, 64]]), self.V(self.HB, 0, 64, 0, [[64, 24], [1, 64]]),
                self.V(self.RSTD, 0, 64, 0, [[1, 24], [0, 64]]), ALU.mult)
        for hp in range(3):
            ps = self.pq()
            for c in range(4):
                self.tr(ps[:, c * 64:(c + 1) * 64], self.SQ[:, c, hp * 128:(hp + 1) * 128], inc=(c == 3))
            self.stt(self.MIXT[:, hp, :], ps[:, 0:256], self.ppc(l, 'M_NORM', hp), self.GOZ[:, hp, :], ALU.mult, ALU.mult)

    def stage_lru(self, l, t0, co, W, w0, w1, T, dirs, prompt, seq_idx, first, last, mode='full'):
        S = self.S
        do_f = 0 in dirs
        if mode != 'load':
            for pr in range(2):
                self.proj_fm(3736 + pr * 128, 128, 0, W, lambda ps, pr=pr: self.cp('act', self.XL[:, pr, 2:2 + W], ps))
                if do_f or mode == 'store':
                    self.proj_fm(3992 + pr * 128, 128, co, 256, lambda ps, pr=pr: self.act(self.LZT[:, pr, :], ps, AF.Silu))
            if w1 == T:
                self.memset('dve', self.XL[:, :, 2 + W:2 + W + 1], 0.0)
            if w0 == 0:
                self.memset('dve', self.XL[:, :, 0:2], 0.0)
            for pr in range(2):
                self.ts('dve', self.XC[:, pr, :], self.XL[:, pr, co:co + 256], self.ppc(l, 'L_CONV', 0 * 2 + pr), self.ppc(l, 'L_CONVB', pr), ALU.mult, ALU.add)
                for j in range(1, 4):
                    self.stt(self.XC[:, pr, :], self.XL[:, pr, co + j:co + j + 256], self.ppc(l, 'L_CONV', j * 2 + pr), self.XC[:, pr, :], ALU.mult, ALU.add)
        bi = t0 // 256
        lr_items = [(self.XC[:, :, :], 'XC'), (self.LZT[:, :, :], 'LZT')]
        if mode == 'store':
            for ap_, k_ in lr_items:
                S.dma('sp', self.sc[k_][bi], ap_)
        if mode == 'load':
            for ap_, k_ in lr_items:
                S.dma('sp', ap_, self.sc[k_][bi])
        for d in dirs:
            if first[d]:
                if prompt:
                    self.memset('dve', self.lS[d][:, :], 0.0)
                else:
                    S.dma('sp', self.lS[d][:, :], self.st_l[:, l, d, :])
        if do_f and not (1 in dirs):
            S.dma('sp', self.HL[1][:, :, :], self.sLB[:, :, t0:t0 + 256])
        combos = [(d, pr) for d in sorted(dirs, reverse=True) for pr in range(2)]
        LT = lambda d, n, pr: self.ltd[d][n][:, pr, :]
        pss = {}
        for i, (d, pr) in enumerate(combos):
            ps = self.PA if i % 2 == 0 else self.PB
            off = (i // 2) * 512
            pss[(d, pr)] = (ps, off)
            self.mm(ps[:, off:off + 256], self.LRUW[:, l, (0 * 2 + d) * 2 + pr, :], self.XC[:, pr, :])
            self.mm(ps[:, off + 256:off + 512], self.LRUW[:, l, (1 * 2 + d) * 2 + pr, :], self.XC[:, pr, :])
        for (d, pr) in combos:
            ps, off = pss[(d, pr)]
            self.act(LT(d, "rg", pr), ps[:, off:off + 256], AF.Sigmoid, bias=self.ppc(l, 'L_BA', d * 2 + pr))
            self.act(LT(d, "ig", pr), ps[:, off + 256:off + 512], AF.Sigmoid, bias=self.ppc(l, 'L_BX', d * 2 + pr))
        for (d, pr) in combos:
            self.act(LT(d, "aa", pr), LT(d, "rg", pr), AF.Exp, scale=self.dvc(l, 'CLAM', d * 2 + pr))
            self.act(LT(d, "a2", pr), LT(d, "rg", pr), AF.Exp, scale=self.dvc(l, 'C2LAM', d * 2 + pr))
        for (d, pr) in combos:
            self.ts('dve', LT(d, "a2", pr), LT(d, "a2", pr), -1.0, 1.0, ALU.mult, ALU.add)
            self.tt('dve', LT(d, "bt", pr), LT(d, "ig", pr), self.XC[:, pr, :], ALU.mult)
        for (d, pr) in combos:
            self.act(LT(d, "a2", pr), LT(d, "a2", pr), AF.Sqrt)
        for (d, pr) in combos:
            self.tt('dve', LT(d, "bt", pr), LT(d, "bt", pr), LT(d, "a2", pr), ALU.mult)
        for (d, pr) in combos:
            if d == 0:
                self.scan(self.HL[0][:, pr, :], LT(0, "aa", pr), LT(0, "bt", pr), self.lS[0][:, pr:pr + 1], ALU.mult, ALU.add)
                self.cp('act', self.lS[0][:, pr:pr + 1], self.HL[0][:, pr, 255:256])
            else:
                rv = lambda t: self.V(t, 0, 128, pr * 256 + 255, [[-1, 256]])
                self.scan(rv(self.HL[1]), rv(self.ltd[1]["aa"]), rv(self.ltd[1]["bt"]), self.lS[1][:, pr:pr + 1], ALU.mult, ALU.add)
                self.cp('act', self.lS[1][:, pr:pr + 1], self.HL[1][:, pr, 0:1])
        for d in sorted(dirs, reverse=True):
            if last[d] and prompt:
                S.dma('sp', self.o_l[seq_idx, l, d], self.lS[d][:, :])
        if not do_f:
            S.dma('sp', self.sLB[:, :, t0:t0 + 256], self.HL[1][:, :, :])
            return
        self.tt('dve', self.HL[0][:, :, :], self.HL[0][:, :, :], self.HL[1][:, :, :], ALU.add)
        self.tt('dve', self.MIXT[:, 6:8, :], self.HL[0][:, :, :], self.LZT[:, :, :], ALU.mult)

    def stage_rwkv(self, l, t0, co, W, w0, w1, T, dirs, grid, prompt, seq_idx, first, last, mode='full'):
        S = self.S
        do_f = 0 in dirs
        if mode != 'load':
            for ch in range(11):
                self.proj_fm(1944 + ch * 128, 128, 0, W, lambda ps, ch=ch: self.cp('act' if ch % 2 else 'dve', self.URS[:, ch, 0:W], ps))
            if do_f or mode == 'store':
                for hp in range(3):
                    self.proj_fm(3352 + hp * 128, 128, co, 256, lambda ps, hp=hp: self.act(self.RZT[:, hp, :], ps, AF.Silu))
            U3 = lambda off, n: self.V(self.URS, 0, 128, off, [[384, 11], [1, n]])
            B3 = lambda off, n: self.V(self.BLK, 0, 128, off, [[256, 11], [1, n]])
            if not grid:
                self.cp('dve', B3(1, 255), U3(0, 255))
                self.memset('dve', B3(0, 1), 0.0)
                self.tt('dve', B3(0, 255), B3(0, 255), U3(1, 255), ALU.add)
                wsh = 0.5
            else:
                U4 = lambda off, r, n: self.V(self.URS, 0, 128, off, [[384, 11], [64, r], [1, n]])
                B4 = lambda off, r, n: self.V(self.BLK, 0, 128, off, [[256, 11], [64, r], [1, n]])
                self.cp('dve', B4(1, 4, 63), U4(co, 4, 63))
                self.memset('dve', B4(0, 4, 1), 0.0)
                self.tt('dve', B4(0, 4, 63), B4(0, 4, 63), U4(co + 1, 4, 63), ALU.add)
                if t0 > 0:
                    self.tt('dve', B3(0, 256), B3(0, 256), U3(co - 64, 256), ALU.add)
                else:
                    self.tt('dve', B3(64, 192), B3(64, 192), U3(0, 192), ALU.add)
                if t0 + TB < T:
                    self.tt('dve', B3(0, 256), B3(0, 256), U3(co + 64, 256), ALU.add)
                else:
                    self.tt('dve', B3(0, 192), B3(0, 192), U3(co + 64, 192), ALU.add)
                wsh = 0.25
            mu = self.V(self.PP, 0, 128, l * PL + PC['R_MU'], [[1, 11], [0, 256]])
            self.tt('dve', B3(0, 256), B3(0, 256), mu, ALU.mult)
            omm = self.V(self.DV, 0, 128, l * DL + DC['OMMU'], [[1, 11], [0, 256]])
            self.tt('dve', U3(co, 256), U3(co, 256), omm, ALU.mult)
            self.stt(B3(0, 256), B3(0, 256), wsh, U3(co, 256), ALU.mult, ALU.add)
            for nm, ch in (('blk_r', 0), ('blk_k', 3), ('blk_v', 6), ('blk_wl', 9), ('blk_al', 10)):
                self.dump(nm, self.BLK[:, ch, :])
            rt = self.rt
            kk = self.V(self.PP, 0, 128, l * PL + PC['R_KK'], [[1, 3], [0, 256]])
            kap = rt["d"]
            self.tt('dve', kap[:, :, :], self.BLK[:, 3:6, :], kk, ALU.mult)
            ksq = rt["E"]
            self.tt('dve', ksq[:, :, :], kap[:, :, :], kap[:, :, :], ALU.mult)
            for hp in range(3):
                self.mm(self.PA[:, hp * 256:(hp + 1) * 256], self.CST[:, CC['BONES']:CC['BONES'] + 128], ksq[:, hp, :])
            self.act(ksq[:, :, :], self.V(self.PA, 0, 128, 0, [[256, 3], [1, 256]]), AF.Sqrt)
            self.ts('dve', ksq[:, :, :], ksq[:, :, :], 1e-12, None, ALU.max)
            self.recip(ksq[:, :, :], ksq[:, :, :])
            self.tt('dve', self.KH[:, :, :], kap[:, :, :], ksq[:, :, :], ALU.mult)
            self.dump('kh', self.KH[:, 0, :])
            self.bd_fill(self.BDV, lambda par: self.V(self.BLK, par * 64, 64, 6 * 256, [[256, 3], [64, 4], [1, 64]]))
            for c in range(4):
                for hp in range(3):
                    self.mm(self.PB[:, (c * 3 + hp) * 64:(c * 3 + hp + 1) * 64], self.BDV[:, hp, c, :], self.ISTKB[:, :])
            self.cp('act', self.V(self.VSTK, 0, 128, 0, [[1, 768]]), self.PB[:, 0:768])
        bi = t0 // 256
        rw_items = [(self.BLK[:, :, :], 'BLK'), (self.RZT[:, :, :], 'RZT'), (self.KH[:, :, :], 'KH'), (self.VSTK[:, :, :, :], 'VSTK')]
        if mode == 'store':
            for ap_, k_ in rw_items:
                S.dma('sp', self.sc[k_][bi], ap_)
        if mode == 'load':
            for ap_, k_ in rw_items:
                S.dma('sp', ap_, self.sc[k_][bi])
        for d in dirs:
            if first[d]:
                if prompt:
                    self.memset('dve', self.rH[d][:, :, :], 0.0)
                else:
                    S.dma('sp', self.rH[d][:, :, :], self.st_rH[l, d])
        ybflat = self.V(self.YB, 0, 128, 0, [[1, 768]])
        if do_f and not (1 in dirs):
            S.dma('sp', ybflat, self.sYB[t0 // 256])
        for d in sorted(dirs, reverse=True):
            self.rwkv_dir(l, d, seq_idx, prompt, last)
        if not do_f:
            S.dma('sp', self.sYB[t0 // 256], ybflat)
            return
        ft = self.ft
        self.tt('dve', self.YSQ[:, :, :, :], self.YB[:, :, :, :], self.YB[:, :, :, :], ALU.mult)
        self.reduce(self.rSSQ[:, :], self.V(self.YSQ, 0, 128, 0, [[64, 12], [1, 64]]), ALU.add)
        self.ts('dve', self.rSSQ[:, :], self.rSSQ[:, :], 1.0 / 64, EPS, ALU.mult, ALU.add)
        self.act(self.rSSQ[:, :], self.rSSQ[:, :], AF.Sqrt)
        self.recip(self.rRSTD[:, :], self.rSSQ[:, :])
        self.tt('dve', ft["rk"][:, :, :], self.BLK[:, 0:3, :], self.BLK[:, 3:6, :], ALU.mult)
        for hp in range(3):
            self.mm(self.PB[:, hp * 256:(hp + 1) * 256], self.RKD[:, l, hp, :], ft["rk"][:, hp, :])
        self.tt('dve', ft["bon"][:, :, :], self.V(self.PB, 0, 128, 0, [[256, 3], [1, 256]]), self.BLK[:, 6:9, :], ALU.mult)
        self.memset('pool', self.YBD[:, :, :], 0.0)
        for c in range(4):
            for par in range(2):
                self.tt('dve', self.V(self.YBD, par * 64, 64, par * 64, [[128, 3], [1, 64]]),
                        self.V(self.YB, par * 64, 64, c * 192, [[64, 3], [1, 64]]),
                        self.V(self.rRSTD, par * 64, 64, c * 3, [[1, 3], [0, 64]]), ALU.mult)
            for hp in range(3):
                self.mm(self.PA[:, hp * 256 + c * 64: hp * 256 + (c + 1) * 64], self.YBD[:, hp, :],
                        self.CST[:, CC['ISTK']:CC['ISTK'] + 64])
        for hp in range(3):
            self.stt(ft["t1"][:, hp, :], self.PA[:, hp * 256:(hp + 1) * 256], self.ppc(l, 'R_NORM', hp), ft["bon"][:, hp, :], ALU.mult, ALU.add)
            self.tt('dve', self.MIXT[:, 3 + hp, :], ft["t1"][:, hp, :], self.RZT[:, hp, :], ALU.mult)

    def bd_fill(self, bd, src_of_par, eng='pool'):
        self.memset(eng, bd[:, :, :, :], 0.0)
        for par in range(2):
            self.cp('act' if par == 0 else eng, self.V(bd, par * 64, 64, par * 64, [[512, 3], [128, 4], [1, 64]]), src_of_par(par))

    def rwkv_dir(self, l, d, seq_idx, prompt, last):
        S = self.S
        rt = self.rt
        pb_d = 64 * d
        f3 = lambda t: t[:, :, :]
        v4 = lambda t: self.V(t, 0, 128, 0, [[256, 3], [64, 4], [1, 64]])
        self.act(self.TWL[pb_d:pb_d + 64, :], self.BLK[pb_d:pb_d + 64, 9, :], AF.Tanh)
        for hp in range(3):
            self.mm(self.PA[:, hp * 256:(hp + 1) * 256], self.LORA[pb_d:pb_d + 64, l, 0, hp * 128:(hp + 1) * 128], self.TWL[pb_d:pb_d + 64, :])
        for hp in range(3):
            self.act(rt["sg"][:, hp, :], self.PA[:, hp * 256:(hp + 1) * 256], AF.Sigmoid, bias=self.ppc(l, 'R_W0', d * 3 + hp))
        for hp in range(3):
            self.mm(self.PB[:, hp * 256:(hp + 1) * 256], self.LORA[pb_d:pb_d + 64, l, 1, hp * 128:(hp + 1) * 128], self.BLK[pb_d:pb_d + 64, 10, :])
        for hp in range(3):
            self.act(rt["aa"][:, hp, :], self.PB[:, hp * 256:(hp + 1) * 256], AF.Sigmoid, bias=self.ppc(l, 'R_A0', d * 3 + hp))
        for hp in range(3):
            self.ts('dve', rt["kt"][:, hp, :], rt["aa"][:, hp, :], self.ppc(l, 'R_KA', hp), self.dvc(l, 'OMKA', hp), ALU.mult, ALU.add)
        self.tt('dve', f3(rt["kt"]), f3(rt["kt"]), self.BLK[:, 3:6, :], ALU.mult)
        self.tt('dve', f3(rt["bb"]), self.KH[:, :, :], f3(rt["aa"]), ALU.mult)
        flat = lambda t: self.V(t, 0, 128, 0, [[1, 768]])
        self.scan(flat(rt["cs"]), self.CST[:, CC['RMASK']:CC['RMASK'] + 768], flat(rt["sg"]), 0.0, ALU.mult, ALU.add)
        self.cp('dve', self.GL[:, :, :], self.V(rt["cs"], 0, 128, 63, [[256, 3], [64, 4]]))
        if d == 1:
            csLb = self.V(rt["cs"], 0, 128, 63, [[256, 3], [64, 4], [0, 64]])
            self.tt('dve', v4(rt["d"]), csLb, v4(rt["cs"]), ALU.subtract)
            self.tt('dve', f3(rt["cs"]), f3(rt["d"]), f3(rt["sg"]), ALU.add)
        self.act(f3(rt["E"]), f3(rt["cs"]), AF.Exp, scale=-DSC)
        self.tt('dve', self.V(self.KR, 0, 128, 64, [[512, 3], [128, 4], [1, 64]]),
                self.V(self.BLK, 0, 128, 0, [[256, 3], [64, 4], [1, 64]]), v4(rt["E"]), ALU.mult)
        self.tt('dve', f3(rt["d"]), f3(rt["cs"]), f3(rt["sg"]), ALU.subtract)
        self.act(f3(rt["E"]), f3(rt["d"]), AF.Exp, scale=-DSC)
        self.tt('dve', self.V(self.KR, 0, 128, 0, [[512, 3], [128, 4], [1, 64]]), v4(self.KH), v4(rt["E"]), ALU.mult)
        self.act(f3(rt["E"]), f3(rt["cs"]), AF.Exp, scale=DSC)
        self.tt('dve', self.BTT[:, :, :], f3(rt["bb"]), f3(rt["E"]), ALU.mult)
        self.tt('dve', self.KTT[:, :, :], f3(rt["kt"]), f3(rt["E"]), ALU.mult)
        self.act(self.GL[:, :, :], self.GL[:, :, :], AF.Exp, scale=-DSC)
        BD = self.BD
        self.memset('pool', self.A1BD[:, :, :, :], 0.0)
        self.memset('pool', self.A2BD[:, :, :, :], 0.0)
        self.memset('pool', self.NN[0][:, :, :], 0.0)
        c4 = lambda t, par: self.V(t, par * 64, 64, 0, [[256, 3], [64, 4], [1, 64]])
        self.bd_fill(BD["kt"], lambda par: c4(self.KTT, par))
        self.bd_fill(BD["b"], lambda par: c4(self.BTT, par))
        self.bd_fill(BD["kh"], lambda par: self.V(self.KR, par * 64, 64, 0, [[512, 3], [128, 4], [1, 64]]))
        self.bd_fill(BD["r"], lambda par: self.V(self.KR, par * 64, 64, 64, [[512, 3], [128, 4], [1, 64]]))
        for (src, dst, neg) in ((BD["kt"], self.KTTOK, False), (BD["b"], self.BTTOK, True)):
            for half in range(2):
                for cc_ in range(2):
                    c = half * 2 + cc_
                    for hp in range(3):
                        self.tr(self.PTB[:, (cc_ * 3 + hp) * 128:(cc_ * 3 + hp + 1) * 128], src[:, hp, c, :], bf=True)
                o = self.V(dst, 0, 128, half * 768, [[1, 768]])
                if neg:
                    self.act(o, self.PTB[:, 0:768], AF.Copy, scale=-1.0)
                else:
                    self.cp('dve', o, self.PTB[:, 0:768])
        self.cp('act', self.HBF[:, :, :], self.rH[d][:, :, :])
        mk = CC['MKF'] if d == 0 else CC['MKB']
        nmk = CC['NMKF'] if d == 0 else CC['NMKB']
        mn = CC['MNF'] if d == 0 else CC['MNB']
        for j in range(4):
            c = j if d == 0 else 3 - j
            cs = slice(c * 64, (c + 1) * 64)
            KRc = lambda hp: self.V(self.KR, 0, 128, hp * 512 + c * 128, [[1, 128]])
            p1, p2, p3 = self.pq(), self.pq(), self.pq()
            for hp in range(3):
                self.mm(p1[:, hp * 128:(hp + 1) * 128], BD["kt"][:, hp, c, :], KRc(hp), inc=(hp == 2))
            for hp in range(3):
                self.mm(p2[:, hp * 128:(hp + 1) * 128], BD["b"][:, hp, c, :], KRc(hp), inc=(hp == 2))
            for hp in range(3):
                self.mm(p3[:, hp * 64:(hp + 1) * 64], BD["kh"][:, hp, c, :], self.BTT[:, hp, cs], inc=(hp == 2))
            for par in range(2):
                pp_ = par * 64
                self.tt('dve', self.V(self.A1BD, pp_, 64, pp_, [[256, 3], [128, 2], [1, 64]]),
                        self.V(p1, pp_, 64, 0, [[128, 3], [64, 2], [1, 64]]),
                        self.V(self.CST, pp_, 64, mk, [[0, 3], [64, 2], [1, 64]]), ALU.mult)
                self.tt('dve', self.V(self.A2BD, pp_, 64, pp_, [[256, 3], [128, 2], [1, 64]]),
                        self.V(p2, pp_, 64, 0, [[128, 3], [64, 2], [1, 64]]),
                        self.V(self.CST, pp_, 64, nmk, [[0, 3], [64, 2], [1, 64]]), ALU.mult)
                self.tt('dve', self.V(self.NN[0], pp_, 64, pp_, [[128, 3], [1, 64]]),
                        self.V(p3, pp_, 64, 0, [[64, 3], [1, 64]]),
                        self.V(self.CST, pp_, 64, mn, [[0, 3], [1, 64]]), ALU.mult)
            pr_ = self.pq()
            for hp in range(3):
                o = pr_[:, hp * 64:(hp + 1) * 64]
                self.mm(o, BD["kh"][:, hp, c, :], self.HBF[:, hp, :], start=True, stop=False)
                self.mm(o, self.A1BD[:, hp, 0, :], self.VSTK[:, c, hp, :], start=False, stop=True, inc=(hp == 2))
            u192 = self.V(self.U, 0, 128, 0, [[1, 192]])
            ub192 = self.V(self.UBF, 0, 128, 0, [[1, 192]])
            self.cp('dve', u192, pr_[:, 0:192])
            self.cp('act', ub192, u192)
            for k in range(6):
                NTk = self.A2BD[:, :, 0, :] if k == 0 else self.NTT[k % 2][:, :, :]
                Nk = self.NN[k % 2]
                pu = self.pq()
                for hp in range(3):
                    self.mm(pu[:, hp * 64:(hp + 1) * 64], NTk[:, hp, :], self.UBF[:, hp, :], inc=(hp == 2))
                if k < 5:
                    pnt = self.pq()
                    for hp in range(3):
                        self.mm(pnt[:, hp * 128:(hp + 1) * 128], Nk[:, hp, :], NTk[:, hp, :], inc=(hp == 2))
                    pn = self.pq()
                    for hp in range(3):
                        self.mm(pn[:, hp * 128:(hp + 1) * 128], NTk[:, hp, :], Nk[:, hp, :], inc=(hp == 2))
                self.tt('dve', ub192, u192, pu[:, 0:192], ALU.add)
                if k < 5:
                    self.cp('act', self.V(self.NTT[(k + 1) % 2], 0, 128, 0, [[1, 384]]), pnt[:, 0:384])
                    self.cp('act', self.V(self.NN[(k + 1) % 2], 0, 128, 0, [[1, 384]]), pn[:, 0:384])
                    self.tt('dve', u192, u192, pu[:, 0:192], ALU.add)
            py = self.pq()
            for hp in range(3):
                o = py[:, hp * 64:(hp + 1) * 64]
                self.mm(o, BD["r"][:, hp, c, :], self.HBF[:, hp, :], start=True, stop=False)
                self.mm(o, self.A1BD[:, hp, 1, :], self.VSTK[:, c, hp, :], start=False, stop=False)
                self.mm(o, self.A2BD[:, hp, 1, :], self.UBF[:, hp, :], start=False, stop=True, inc=(hp == 2))
            ybv = self.V(self.YB, 0, 128, c * 192, [[1, 192]])
            if d == 1:
                self.cp('act', ybv, py[:, 0:192])
            else:
                self.tt('dve', ybv, ybv, py[:, 0:192], ALU.add)
            ph = self.pq()
            for hp in range(3):
                o = ph[:, hp * 64:(hp + 1) * 64]
                self.mm(o, self.KTTOK[:, c, hp, :], self.VSTK[:, c, hp, :], start=True, stop=False)
                self.mm(o, self.BTTOK[:, c, hp, :], self.UBF[:, hp, :], start=False, stop=True, inc=(hp == 2))
            self.tt('dve', self.HTMP[:, :, :], self.rH[d][:, :, :], self.V(ph, 0, 128, 0, [[64, 3], [1, 64]]), ALU.add)
            self.tt('dve', self.rH[d][:, :, :], self.HTMP[:, :, :], self.V(self.GL, 0, 128, c, [[4, 3], [0, 64]]), ALU.mult)
            self.cp('act', self.HBF[:, :, :], self.rH[d][:, :, :])
        if last[d] and prompt:
            S.dma('sp', self.o_rH[seq_idx, l, d], self.rH[d][:, :, :])

    def stage_out(self, l, mod, xsrc, xdst, r0):
        S = self.S
        for tt_ in range(2):
            o, xw = self.XN[tt_], self.XW[tt_]
            S.dma('sp', xw[:, :], xsrc[r0 + tt_ * 128: r0 + (tt_ + 1) * 128, :])
            for ch in range(2):
                ps = self.pq()
                for kc in range(8):
                    self.mm(ps[:, :], self.MIXT[:, kc, tt_ * 128:(tt_ + 1) * 128], self.W_OUT[:, kc, ch * 512:(ch + 1) * 512],
                            start=(kc == 0), stop=(kc == 7))
                self.cp('act' if ch else 'dve', o[:, ch * 512:(ch + 1) * 512], ps[:, :])
            self.dump('o_proj%d' % tt_, o[:, :])
            self.dump('o_x%d' % tt_, xw[:, :])
            ssq = self.TMPS[:, 24 + tt_:25 + tt_]
            junk = self.V(self.BLK, 0, 128, 0, [[1, D]])
            self.act(junk, o[:, :], AF.Square, accum=ssq)
            rs = self.TMPS[:, 26 + tt_:27 + tt_]
            self.ts('dve', rs, ssq, 1.0 / D, EPS, ALU.mult, ALU.add)
            self.act(rs, rs, AF.Sqrt)
            self.recip(rs, rs)
            self.dump('o_rs%d' % tt_, rs)
            self.stt(o[:, :], o[:, :], rs, self.GATEB[:, mod, :], ALU.mult, ALU.mult)
            self.dump('o_g%d' % tt_, o[:, :])
            self.tt('dve', o[:, :], o[:, :], xw[:, :], ALU.add)
            S.dma('sp', xdst[r0 + tt_ * 128: r0 + (tt_ + 1) * 128, :], o[:, :])

    def build(self, layers=(0, 1)):
        try:
            self._build(layers)
        except StopIteration:
            pass
        self.S.finish('sp')
        return self.nc

    def _build(self, layers):
        NP, TS = self.NP, self.TS
        self.setup()
        if self.stop == 'setup':
            raise StopIteration
        for li, l in enumerate(layers):
            self.load_layer(l)
            if self.stop == 'load':
                raise StopIteration
            xsrc = self.x_in if li == 0 else self.x1
            xdst = self.y_out if li == len(layers) - 1 else self.x1
            T_, F_ = {0: True, 1: True}, {0: False, 1: False}
            for s in range(NP):
                self.visit(l, 0, xsrc, xdst, s * 256, 256, 0, [0, 1], False, s, T_, T_)
            if TS > 0:
                nb = TS // TB
                row0 = NP * 256
                for b in range(nb - 1, -1, -1):
                    self.visit(l, 1, xsrc, xdst, row0, TS, b * TB, [1], True, 0,
                               {0: False, 1: b == nb - 1}, {0: False, 1: b == 0}, mode='store')
                for b in range(nb):
                    self.visit(l, 1, xsrc, xdst, row0, TS, b * TB, [0], True, 0,
                               {0: b == 0, 1: False}, {0: b == nb - 1, 1: False}, mode='load')


def prep_shared(inp):
    f = lambda a: np.ascontiguousarray(np.asarray(a, dtype=np.float32))
    b_mod = f(inp['b_mod'])
    sh = {}
    sh['w_mod'] = f(inp['w_mod'])
    sh['w_in'] = f(inp['w_in'])
    sh['w_out'] = f(inp['w_out'])
    sh['bmodT'] = f(b_mod[:, :2048].reshape(2, 16, 128).transpose(2, 0, 1))
    sh['bmodg'] = f(b_mod[:, 2048:3072])
    sh['gpost'] = f(inp['g_post'])
    pp = np.zeros((128, 2 * PL), np.float32)
    for l in range(2):
        o = l * PL
        def put(key, arr, n):
            pp[:, o + PC[key]: o + PC[key] + n] = np.asarray(arr, np.float32).reshape(n, 128).T
        put('G_PRE', inp['g_pre'][l], 8)
        put('M_NORM', inp['m_norm'][l], 3)
        put('R_MU', inp['r_mu'][l], 11)
        put('R_W0', np.asarray(inp['r_w0'][l]).reshape(-1), 6)
        put('R_A0', np.asarray(inp['r_a0'][l]).reshape(-1), 6)
        put('R_KK', inp['r_kk'][l], 3)
        put('R_KA', inp['r_ka'][l], 3)
        put('R_RK', inp['r_rk'][l], 3)
        put('R_NORM', inp['r_norm'][l], 3)
        put('L_CONV', np.asarray(inp['l_conv'][l]).reshape(-1), 8)
        put('L_CONVB', inp['l_conv_b'][l], 2)
        put('L_BA', np.asarray(inp['l_ba'][l]).reshape(-1), 4)
        put('L_BX', np.asarray(inp['l_bx'][l]).reshape(-1), 4)
        put('L_LAM', np.asarray(inp['l_lambda'][l]).reshape(-1), 4)
        pp[0:6, o + PC['M_BI']: o + PC['M_BI'] + 2] = np.asarray(inp['m_bi'][l], np.float32).T
        pp[0:6, o + PC['M_BF']: o + PC['M_BF'] + 2] = np.asarray(inp['m_bf'][l], np.float32).T
    sh['pp'] = pp
    sh['cst'] = make_consts()
    lora = np.zeros((128, 2, 2, 384), np.float32)
    for wi, key in enumerate(['r_w2', 'r_a2']):
        a = np.asarray(inp[key], np.float32)
        lora[:, :, wi, :] = a.transpose(1, 2, 0, 3).reshape(128, 2, 384)
    sh['lora'] = lora
    lruw = np.zeros((128, 2, 8, 128), np.float32)
    for gi, key in enumerate(['l_wa', 'l_wx']):
        a = np.asarray(inp[key], np.float32)
        for l in range(2):
            for d in range(2):
                for pr in range(2):
                    for hb in range(2):
                        n = 2 * pr + hb
                        lruw[hb * 64:(hb + 1) * 64, l, (gi * 2 + d) * 2 + pr, hb * 64:(hb + 1) * 64] = a[l, d, n]
    sh['lruw'] = lruw
    return sh


def prep_core(inp, b, NP, TS):
    f = lambda a: np.ascontiguousarray(np.asarray(a, dtype=np.float32))
    m = {}
    xp = np.asarray(inp['x_prompt'], np.float32)[b * NP:(b + 1) * NP].reshape(NP * 256, D)
    if TS > 0:
        xs = np.asarray(inp['x_sample'], np.float32)[b]
        m['x_in'] = f(np.concatenate([xp, xs], 0))
    else:
        m['x_in'] = f(xp)
    cc = np.stack([np.asarray(inp['c_ctx'], np.float32), np.asarray(inp['c'], np.float32)[b]], -1)
    m['cc'] = f(cc.reshape(8, 128, 2).transpose(1, 0, 2))
    C = np.asarray(inp['state_mlstm_C'], np.float32)[b]
    n = np.asarray(inp['state_mlstm_n'], np.float32)[b]
    Cn = np.concatenate([C, n[..., None]], -1)
    Cn = Cn.reshape(2, 2, 3, 2, 64, 65).transpose(0, 1, 3, 4, 2, 5).reshape(2, 2, 128, 3, 65)
    m['st_mC'] = f(Cn)
    m['st_mm'] = f(np.asarray(inp['state_mlstm_m'], np.float32)[b].transpose(2, 0, 1))
    R = np.asarray(inp['state_rwkv'], np.float32)[b]
    R = R.transpose(0, 1, 2, 4, 3)
    R = R.reshape(2, 2, 3, 2, 64, 64).transpose(0, 1, 3, 4, 2, 5).reshape(2, 2, 128, 3, 64)
    m['st_rH'] = f(R)
    L = np.asarray(inp['state_rglru'], np.float32)[b]
    m['st_l'] = f(L.reshape(2, 2, 2, 128).transpose(3, 0, 1, 2))
    return m


def unpack_core(r, NP, TS):
    y = r['y_out']
    yp = y[:NP * 256].reshape(NP, 256, D)
    ys = y[NP * 256:]
    mC = r['o_mC'].reshape(NP, 2, 2, 2, 64, 3, 65).transpose(0, 1, 2, 5, 3, 4, 6).reshape(NP, 2, 2, 6, 64, 65)
    newC = np.ascontiguousarray(mC[..., :64])
    newn = np.ascontiguousarray(mC[..., 64])
    newm = r['o_mm'].reshape(NP, 2, 2, 6)
    rH = r['o_rH'].reshape(NP, 2, 2, 2, 64, 3, 64).transpose(0, 1, 2, 5, 3, 4, 6).reshape(NP, 2, 2, 6, 64, 64)
    newr = np.ascontiguousarray(rH.transpose(0, 1, 2, 3, 5, 4))
    newl = np.ascontiguousarray(r['o_l'].transpose(0, 1, 2, 4, 3).reshape(NP, 2, 2, 256))
    return yp, ys, newC, newn, newm, newr, newl


_NC_CACHE = {}


def kernel(**inputs):
    NP, TS = 4, 2048
    key = (NP, TS)
    if key not in _NC_CACHE:
        _NC_CACHE[key] = Builder(NP, TS).build()
    nc = _NC_CACHE[key]
    sh = prep_shared(inputs)
    in_maps = []
    for b in range(NCORES):
        m = dict(sh)
        m.update(prep_core(inputs, b, NP, TS))
        in_maps.append(m)
    res = run_bass_kernel_spmd(nc, in_maps, core_ids=list(range(NCORES)))
    outs = [unpack_core(r, NP, TS) for r in res.results]
    y_prompt = np.concatenate([o[0] for o in outs], 0)
    y_sample = np.stack([o[1] for o in outs], 0)
    cat = lambda i: np.concatenate([o[i] for o in outs], 0)
    return (y_prompt.astype(np.float32), y_sample.astype(np.float32), cat(2).astype(np.float32),
            cat(3).astype(np.float32), cat(4).astype(np.float32), cat(5).astype(np.float32), cat(6).astype(np.float32))
```

```python
import numpy as np
import concourse.bass as bass
import concourse.mybir as mybir
from concourse.bass_utils import run_bass_kernel_spmd

F32 = mybir.dt.float32
BF16 = mybir.dt.bfloat16
AF = mybir.ActivationFunctionType
ALU = mybir.AluOpType
AX = mybir.AxisListType

D = 1024
IN_COLS = 4248
EPS = 1e-6
DSC = 0.6065306597126334
TB = 256
NCORES = 8


def _prod(xs):
    r = 1
    for x in xs:
        r *= int(x)
    return r


class Sync:
    def __init__(self, nc, n_dma_sems=32):
        self.nc = nc
        self.engs = {'pe': nc.tensor, 'dve': nc.vector, 'act': nc.scalar,
                     'pool': nc.gpsimd, 'sp': nc.sync}
        self.sem = {}
        self.cnt = {}
        for e in ['pe', 'dve', 'act', 'pool']:
            self.sem[e] = nc.alloc_semaphore('sem_' + e)
            self.cnt[e] = 0
        self.seen = {e: {} for e in self.engs}
        self.dma_ring = [nc.alloc_semaphore('dq_%d' % i) for i in range(n_dma_sems)]
        self.dma_uses = [0] * n_dma_sems
        self.dma_next = 0
        self.rec = {}
        self.untracked = set()
        self.n_wait = 0
        self.n_ins = 0
        self.pstep_cache = {}
        self.sb_addr = {}

    def region(self, ap):
        t = ap.tensor
        name = t.name
        apl = [(int(s), int(c)) for (s, c) in ap.ap]
        off = int(ap.offset)
        if type(t).__name__.startswith('DRam'):
            lo = off + sum(min(0, s * (c - 1)) for s, c in apl)
            hi = off + sum(max(0, s * (c - 1)) for s, c in apl) + 1
            return (name, 0, 1, lo, hi)
        pstep = self.pstep_cache.get(name)
        if pstep is None:
            pstep = _prod(list(t.shape)[1:])
            self.pstep_cache[name] = pstep
        p0 = off // pstep
        f0 = off % pstep
        npart = apl[0][1]
        rest = apl[1:]
        lo = f0 + sum(min(0, s * (c - 1)) for s, c in rest)
        hi = f0 + sum(max(0, s * (c - 1)) for s, c in rest) + 1
        if name in self.sb_addr:
            base, es = self.sb_addr[name]
            return ('SB', p0, p0 + npart, base + lo * es, base + hi * es)
        return ('PS:' + name, (p0 // 32) * 32, ((p0 + npart + 31) // 32) * 32, (lo // 512) * 512, ((hi + 511) // 512) * 512)

    @staticmethod
    def _ovl(a, b):
        return a[1] < b[2] and b[1] < a[2] and a[3] < b[4] and b[3] < a[4]

    @staticmethod
    def _contains(a, b):
        return a[1] <= b[1] and b[2] <= a[2] and a[3] <= b[3] and b[4] <= a[4]

    def _collect(self, e, reads, writes):
        deps = {}
        own = self.sem.get(e)
        rregs = [self.region(a) for a in reads]
        wregs = [self.region(a) for a in writes]
        for r in rregs:
            if r[0] in self.untracked:
                continue
            isps = r[0].startswith('PS:')
            for (reg, kind, sem, val) in self.rec.get(r[0], ()):
                if (kind == 'w' or (isps and sem is not own)) and self._ovl(reg, r):
                    if e == 'pe' and sem is own:
                        continue
                    k = id(sem)
                    if deps.get(k, (None, 0))[1] < val:
                        deps[k] = (sem, val)
        for w in wregs:
            if w[0] in self.untracked:
                continue
            for (reg, kind, sem, val) in self.rec.get(w[0], ()):
                if self._ovl(reg, w):
                    if sem is own:
                        continue
                    k = id(sem)
                    if deps.get(k, (None, 0))[1] < val:
                        deps[k] = (sem, val)
        return deps, rregs, wregs

    def _record(self, rregs, wregs, sem, val):
        for r in rregs:
            if r[0] in self.untracked:
                continue
            lst = self.rec.setdefault(r[0], [])
            lst[:] = [x for x in lst if not (x[1] == 'r' and x[2] is sem and self._contains(r, x[0]))]
            lst.append((r, 'r', sem, val))
        for w in wregs:
            if w[0] in self.untracked:
                continue
            lst = self.rec.setdefault(w[0], [])
            lst[:] = [x for x in lst if not self._contains(w, x[0])]
            lst.append((w, 'w', sem, val))

    def wait(self, e, sem, val):
        k = id(sem)
        if self.seen[e].get(k, 0) >= val:
            return
        self.engs[e].wait_ge(sem, val)
        self.seen[e][k] = val
        self.n_wait += 1

    max_ins = None
    paranoid = False

    def emit(self, e, reads, writes, build, inc=True):
        if self.max_ins is not None and self.n_ins >= self.max_ins:
            raise StopIteration
        deps, rregs, wregs = self._collect(e, reads, writes)
        for (sem, val) in deps.values():
            self.wait(e, sem, val)
        if self.paranoid:
            for e2 in ['pe', 'dve', 'act', 'pool']:
                if self.cnt[e2] > 0 and not (e == 'pe' and e2 == 'pe'):
                    self.wait(e, self.sem[e2], self.cnt[e2])
        ins = build(self.engs[e])
        self.n_ins += 1
        if inc:
            self.cnt[e] += 1
            ins.then_inc(self.sem[e], 1)
            val = self.cnt[e]
        else:
            val = self.cnt[e] + 1
        self._record(rregs, wregs, self.sem[e], val)
        return ins

    def dma(self, q, out, in_, **kw):
        if self.max_ins is not None and self.n_ins >= self.max_ins:
            raise StopIteration
        i = self.dma_next
        self.dma_next = (i + 1) % len(self.dma_ring)
        sem = self.dma_ring[i]
        uses = self.dma_uses[i]
        if uses > 0:
            self.wait(q, sem, 16 * uses)
        deps, rregs, wregs = self._collect(q, [in_], [out])
        for (s, v) in deps.values():
            self.wait(q, s, v)
        ins = self.engs[q].dma_start(out=out, in_=in_, **kw)
        ins.then_inc(sem, 16)
        self.n_ins += 1
        self.dma_uses[i] = uses + 1
        self._record(rregs, wregs, sem, 16 * (uses + 1))
        return ins

    def finish(self, q='sp'):
        for i, sem in enumerate(self.dma_ring):
            if self.dma_uses[i] > 0:
                self.wait(q, sem, 16 * self.dma_uses[i])
        for e in ['pe', 'dve', 'act', 'pool']:
            if self.cnt[e] > 0:
                self.wait(q, self.sem[e], self.cnt[e])


class Arena:
    def __init__(self, base, size):
        self.base, self.size, self.ptr = base, size, 0

    def take(self, nbytes):
        off = (self.ptr + 31) // 32 * 32
        self.ptr = off + nbytes
        assert self.ptr <= self.size, ("arena overflow", self.ptr, self.size)
        return self.base + off


PL = 72
PC = dict(G_PRE=0, M_NORM=8, R_MU=11, R_W0=22, R_A0=28, R_KK=34, R_KA=37, R_RK=40, R_NORM=43,
          L_CONV=46, L_CONVB=54, L_BA=56, L_BX=60, L_LAM=64, M_BI=68, M_BF=70)
DL = 32
DC = dict(OMKA=0, CLAM=3, C2LAM=7, NBF=11, OMMU=16)
CC = dict(IDENT=0, BONES=128, MKF=256, MKB=384, MNF=512, MNB=576, MUI=640, MLI=704, RMASK=768,
          LSEL=1536, PSEL=1664, NMKF=1668, NMKB=1796, ONES=1924, ISTK=2052)
NCST = 2052 + 64


def make_consts():
    c = np.zeros((128, NCST), np.float32)
    c[:, 0:128] = np.eye(128)
    c[0:64, 128:192] = 1.0
    c[64:128, 192:256] = 1.0
    s = np.arange(64)[:, None]
    t = np.arange(64)[None, :]
    us, ui = (s < t).astype(np.float32), (s <= t).astype(np.float32)
    ls, li = (s > t).astype(np.float32), (s >= t).astype(np.float32)
    c[0:64, 256:320], c[0:64, 320:384] = us, ui
    c[0:64, 384:448], c[0:64, 448:512] = ls, li
    c[0:64, 512:576] = -ls
    c[0:64, 576:640] = -us
    c[0:64, 640:704] = ui
    c[0:64, 704:768] = li
    rm = np.ones(768, np.float32)
    rm[::64] = 0.0
    c[:, 768:1536] = rm[None, :]
    for k in range(6):
        c[k, 1536 + (k % 2) * 64: 1536 + (k % 2) * 64 + 64] = 1.0
        c[k, 1664 + k // 2] = 1.0
    c[0:64, 1668:1796] = -c[0:64, 256:384]
    c[0:64, 1796:1924] = -c[0:64, 384:512]
    c[:, 1924:2052] = 1.0
    for (a, b) in ((256, 768), (1668, 1924)):
        c[64:128, a:b] = c[0:64, a:b]
    c[0:64, 2052:2116] = np.eye(64)
    c[64:128, 2052:2116] = np.eye(64)
    return c


class Builder:
    def __init__(self, NP, TS, debug=False, stop=None):
        self.stop = stop
        self.NP, self.TS = NP, TS
        self.NTOK = NP * 256 + TS
        self.debug = debug
        nc = self.nc = bass.Bass("TRN2", target_bir_lowering=False)
        self.S = Sync(nc)
        self._decl_dram()
        self._alloc()

    def _decl_dram(self):
        nc, NP, TS = self.nc, self.NP, self.TS
        di = lambda n, s: nc.dram_tensor(n, list(s), F32, kind="ExternalInput").ap()
        do = lambda n, s: nc.dram_tensor(n, list(s), F32, kind="ExternalOutput").ap()
        dx = lambda n, s: nc.dram_tensor(n, list(s), F32, kind="Internal").ap()
        self.x_in = di("x_in", [self.NTOK, D])
        self.cc = di("cc", [128, 8, 2])
        self.w_mod = di("w_mod", [2, D, 3 * D])
        self.bmodT = di("bmodT", [128, 2, 16])
        self.bmodg = di("bmodg", [2, D])
        self.gpost = di("gpost", [2, D])
        self.w_in = di("w_in", [2, D, IN_COLS])
        self.w_out = di("w_out", [2, D, D])
        self.pp = di("pp", [128, 2 * PL])
        self.cst = di("cst", [128, NCST])
        self.lora = di("lora", [128, 2, 2, 384])
        self.lruw = di("lruw", [128, 2, 8, 128])
        self.st_mC = di("st_mC", [2, 2, 128, 3, 65])
        self.st_mm = di("st_mm", [6, 2, 2])
        self.st_rH = di("st_rH", [2, 2, 128, 3, 64])
        self.st_l = di("st_l", [128, 2, 2, 2])
        for n in ["x_in", "cc", "w_mod", "bmodT", "bmodg", "gpost", "w_in", "w_out", "pp", "cst", "lora",
                  "lruw", "st_mC", "st_mm", "st_rH", "st_l"]:
            self.S.untracked.add(n)
        self.y_out = do("y_out", [self.NTOK, D])
        self.o_mC = do("o_mC", [NP, 2, 2, 128, 3, 65])
        self.o_mm = do("o_mm", [NP, 2, 2, 6, 1])
        self.o_rH = do("o_rH", [NP, 2, 2, 128, 3, 64])
        self.o_l = do("o_l", [NP, 2, 2, 128, 2])
        self.x1 = dx("x1", [self.NTOK, D])
        self.sHB = dx("sHB", [max(TS, 64), 384])
        self.sYB = dx("sYB", [max(TS // 256, 1), 128, 768])
        self.sLB = dx("sLB", [128, 2, max(TS, 64)])
        nb = max(TS // 256, 1)
        dxt = lambda n, s_, dt: nc.dram_tensor(n, list(s_), dt, kind="Internal").ap()
        self.sc = {
            'QT': dxt("sc_QT", [nb, 128, 3, 256], BF16), 'KT': dxt("sc_KT", [nb, 128, 3, 256], BF16),
            'KTOK': dxt("sc_KTOK", [nb, 64, 4, 384], BF16), 'VAUG': dxt("sc_VAUG", [nb, 64, 4, 6, 65], BF16),
            'GOZ': dxt("sc_GOZ", [nb, 128, 3, 256], F32), 'GI': dxt("sc_GI", [nb, 6, 256], F32),
            'LF': dxt("sc_LF", [nb, 6, 256], F32), 'XC': dxt("sc_XC", [nb, 128, 2, 256], F32),
            'LZT': dxt("sc_LZT", [nb, 128, 2, 256], F32), 'BLK': dxt("sc_BLK", [nb, 128, 11, 256], F32),
            'RZT': dxt("sc_RZT", [nb, 128, 3, 256], F32), 'KH': dxt("sc_KH", [nb, 128, 3, 256], F32),
            'VSTK': dxt("sc_VSTK", [nb, 128, 4, 3, 64], BF16),
        }
        if self.debug:
            self.dbg = do("dbg", [128, 32768])
            self.dbg_map = {}
            self.dbg_off = 0

    def T(self, name, shape, dtype, arena):
        es = 2 if dtype == BF16 else 4
        nb = _prod(shape[1:]) * es
        off = arena.take(nb)
        t = self.nc.alloc_sbuf_tensor_at(name, list(shape), dtype, offset=off)
        self.S.sb_addr[t.name] = (off, es)
        return t

    def _alloc(self):
        nc = self.nc
        B0 = 16384 + 256
        LIM = 224 * 1024 - 256
        P = Arena(B0, LIM - B0)
        T = self.T
        self.W_IN = T("W_IN", [128, 8, IN_COLS], BF16, P)
        self.W_OUT = T("W_OUT", [128, 8, D], BF16, P)
        self.CST = T("CST", [128, NCST], F32, P)
        self.PP = T("PP", [128, 2 * PL], F32, P)
        self.DV = T("DV", [128, 2 * DL], F32, P)
        self.LORA = T("LORA", [128, 2, 2, 384], F32, P)
        self.LRUW = T("LRUW", [128, 2, 8, 128], F32, P)
        self.RKD = T("RKD", [128, 2, 3, 128], F32, P)
        self.GS = T("GS", [128, 2, 8], F32, P)
        self.SH = T("SH", [128, 2, 8], F32, P)
        self.GATEB = T("GATEB", [128, 2, D], F32, P)
        self.IDB = T("IDB", [128, 128], BF16, P)
        self.ISTKB = T("ISTKB", [128, 64], BF16, P)
        self.mC = [T("mC%d" % d, [128, 3, 65], F32, P) for d in range(2)]
        self.mM = [T("mM%d" % d, [6, 1], F32, P) for d in range(2)]
        self.rH = [T("rH%d" % d, [128, 3, 64], F32, P) for d in range(2)]
        self.lS = [T("lS%d" % d, [128, 2], F32, P) for d in range(2)]
        XB = Arena(P.take(16384), 16384)
        self.HT = T("HT", [128, 8, 384], BF16, P)
        self.MIXT = T("MIXT", [128, 8, 256], BF16, P)
        self.BLK = T("BLK", [128, 11, 256], F32, P)
        abase = P.take(0)
        asize = P.size - P.ptr
        self.asize = asize
        mk = lambda: Arena(abase, asize)
        a = Arena(XB.base, XB.size)
        self.XW = [T("XW%d" % i, [128, D], F32, a) for i in range(2)]
        self.XN = [T("XN%d" % i, [128, D], F32, a) for i in range(2)]
        a = Arena(XB.base, XB.size)
        self.YB = T("YB", [128, 4, 3, 64], F32, a)
        self.RZT = T("RZT", [128, 3, 256], F32, a)
        self.WSTG = [T("WSTG0", [128, 8, 256], F32, Arena(self.S.sb_addr[self.BLK.name][0], 11264)),
                     T("WSTG1", [128, 8, 256], F32, Arena(XB.base, 8192)),
                     T("WSTG2", [128, 8, 256], F32, Arena(XB.base + 8192, 8192))]
        a = mk()
        self.WMs = [T("WM%d" % i, [128, 8, 512], F32, a) for i in range(2)]
        self.SCT = T("SCT", [128, 8, 2], F32, a)
        self.SCB = T("SCB", [128, 2, 8, 128], F32, a)
        self.MODT = T("MODT", [128, 16, 2], F32, a)
        self.BMT = T("BMT", [128, 2, 16], F32, a)
        awm = Arena(self.S.sb_addr[self.WMs[0].name][0], 16384)
        self.BG = T("BG", [128, D], F32, awm)
        self.GP = T("GP", [128, D], F32, awm)
        self.TMPS = T("TMPS", [128, 32], F32, a)
        self.MODROW = T("MODROW", [2, 512], F32, a)
        a = mk()
        self.QT = T("QT", [128, 3, 256], BF16, a)
        self.KT = T("KT", [128, 3, 256], BF16, a)
        self.GOZ = T("GOZ", [128, 3, 256], F32, a)
        self.KTOK = T("KTOK", [64, 4, 384], BF16, a)
        self.VAUG = T("VAUG", [64, 4, 6, 65], BF16, a)
        self.HB = T("HB", [64, 4, 384], F32, a)
        self.mg = []
        for d in range(2):
            g = {}
            for n in ["gi", "lf", "pre", "bb", "gg", "ee", "fl"]:
                g[n] = T("mg_%s%d" % (n, d), [6, 256], F32, a)
            for n in ["mx", "mch", "mprev", "MM", "dec"]:
                g[n] = T("mg_%s%d" % (n, d), [6, 4], F32, a)
            g["X2"] = T("mg_X2%d" % d, [6, 4, 3], F32, a)
            g["etok"] = T("mg_etok%d" % d, [64, 4, 6], F32, a)
            g["fltok"] = T("mg_fltok%d" % d, [64, 4, 6], F32, a)
            g["decb"] = T("mg_decb%d" % d, [128, 4, 3], F32, a)
            self.mg.append(g)
        self.STSB = [T("STSB%d" % i, [64, 6, 64], BF16, a) for i in range(2)]
        self.VP = [T("VP%d" % i, [64, 6, 65], BF16, a) for i in range(2)]
        self.CDEC = T("CDEC", [128, 3, 65], F32, a)
        self.CDBF = T("CDBF", [128, 3, 65], BF16, a)
        self.DN = T("DN", [64, 6], F32, a)
        self.RDN = T("RDN", [64, 6], F32, a)
        self.HD = T("HD", [64, 6, 64], F32, a)
        self.SQ = T("SQ", [64, 4, 384], F32, a)
        self.SSQ = T("SSQ", [64, 24], F32, a)
        self.RSTD = T("RSTD", [64, 24], F32, a)
        self.ZT = [T("ZT%d" % i, [128, 256], F32, a) for i in range(2)]
        a = mk()
        self.XL = T("XL", [128, 2, 392], F32, a)
        self.XC = T("XC", [128, 2, 256], F32, a)
        self.LZT = T("LZT", [128, 2, 256], F32, a)
        self.ltd = [{n: T("lt%d_%s" % (d_, n), [128, 2, 256], F32, a) for n in ["rg", "ig", "aa", "a2", "bt"]} for d_ in range(2)]
        self.HL = [T("HL%d" % d, [128, 2, 256], F32, a) for d in range(2)]
        a = mk()
        self.URS = T("URS", [128, 11, 384], F32, a)
        a = mk()
        self.KH = T("KH", [128, 3, 256], F32, a)
        R1 = a.take(7 * 3072)
        a1 = Arena(R1, 7 * 3072)
        self.rt = {n: T("rt_" + n, [128, 3, 256], F32, a1) for n in ["sg", "aa", "kt", "bb", "cs", "d", "E"]}
        a2 = Arena(R1, 7 * 3072)
        self.A1BD = T("A1BD", [128, 3, 2, 128], BF16, a2)
        self.A2BD = T("A2BD", [128, 3, 2, 128], BF16, a2)
        self.NN = [T("NN%d" % i, [128, 3, 128], BF16, a2) for i in range(2)]
        self.NTT = [T("NTT%d" % i, [128, 3, 128], BF16, a2) for i in range(2)]
        self.U = T("U", [128, 3, 64], F32, a2)
        self.UBF = T("UBF", [128, 3, 64], BF16, a2)
        self.HBF = T("HBF", [128, 3, 64], BF16, a2)
        self.HTMP = T("HTMP", [128, 3, 64], F32, a2)
        self.YBD = T("YBD", [128, 3, 128], F32, a2)
        self.KTTOK = T("KTTOK", [128, 4, 3, 128], BF16, a2)
        self.BTTOK = T("BTTOK", [128, 4, 3, 128], BF16, a2)
        self.rSSQ = T("rSSQ", [128, 12], F32, a2)
        self.rRSTD = T("rRSTD", [128, 12], F32, a2)
        self.KR = T("KR", [128, 3, 4, 2, 64], BF16, a)
        self.BTT = T("BTT", [128, 3, 256], BF16, a)
        self.KTT = T("KTT", [128, 3, 256], BF16, a)
        bdr = a.take(4 * 3072)
        a3 = Arena(bdr, 4 * 3072)
        self.BD = {n: T("BD_" + n, [128, 3, 4, 128], BF16, a3) for n in ["kt", "b", "kh", "r"]}
        a3 = Arena(bdr, 4 * 3072)
        self.ft = {n: T("ft_" + n, [128, 3, 256], F32, a3) for n in ["rk", "bon", "t1"]}
        self.YSQ = T("YSQ", [128, 4, 3, 64], F32, a3)
        self.BDV = T("BDV", [128, 3, 4, 128], BF16, a)
        self.VSTK = T("VSTK", [128, 4, 3, 64], BF16, a)
        self.TWL = T("TWL", [128, 256], F32, a)
        self.GL = T("GL", [128, 3, 4], F32, a)
        self.PA = nc.alloc_psum_tensor("PA", [128, 1024], F32)
        self.PB = nc.alloc_psum_tensor("PB", [128, 1024], F32)
        self.PQ = [nc.alloc_psum_tensor("PQ%d" % i, [128, 512], F32) for i in range(3)]
        self.PTB = nc.alloc_psum_tensor("PTB", [128, 1024], BF16)
        self.pq_i = 0

    def pq(self):
        t = self.PQ[self.pq_i % 3]
        self.pq_i += 1
        return t

    def V(self, t, p0, npart, off, dims):
        pstep = _prod(list(t.shape)[1:])
        return bass.AP(t, p0 * pstep + off, [[pstep, npart]] + [list(d) for d in dims])

    def tt(self, e, out, in0, in1, op):
        return self.S.emit(e, [in0, in1], [out], lambda g: g.tensor_tensor(out=out, in0=in0, in1=in1, op=op))

    def ts(self, e, out, in0, s1, s2, op0, op1=None):
        rd = [in0] + [s for s in (s1, s2) if not isinstance(s, (int, float)) and s is not None]
        if op1 is None:
            return self.S.emit(e, rd, [out], lambda g: g.tensor_scalar(out=out, in0=in0, scalar1=s1, scalar2=None, op0=op0))
        return self.S.emit(e, rd, [out], lambda g: g.tensor_scalar(out=out, in0=in0, scalar1=s1, scalar2=s2, op0=op0, op1=op1))

    def stt(self, out, in0, sc, in1, op0, op1):
        rd = [in0, in1] + ([] if isinstance(sc, (int, float)) else [sc])
        return self.S.emit('dve', rd, [out], lambda g: g.scalar_tensor_tensor(out=out, in0=in0, scalar=sc, in1=in1, op0=op0, op1=op1))

    def act(self, out, in_, func, bias=None, scale=None, accum=None):
        rd = [in_]
        kw = {}
        if bias is not None:
            kw['bias'] = bias
            if not isinstance(bias, (int, float)):
                rd.append(bias)
        if scale is not None:
            kw['scale'] = scale
            if not isinstance(scale, (int, float)):
                rd.append(scale)
        wr = [out]
        if accum is not None:
            kw['accum_out'] = accum
            wr.append(accum)
        return self.S.emit('act', rd, wr, lambda g: g.activation(out=out, in_=in_, func=func, **kw))

    def cp(self, e, out, in_):
        if e == 'act':
            return self.act(out, in_, AF.Copy)
        return self.S.emit(e, [in_], [out], lambda g: g.tensor_copy(out=out, in_=in_))

    def _pe_rowtile_guard(self, lhsT, out):
        S = self.S
        st = S.region(lhsT)
        k = st[2] - st[1]
        kr = 32 if k <= 32 else (64 if k <= 64 else 128)
        rows = (st[1], st[1] + kr)
        oreg = S.region(out)
        last = getattr(self, '_last_pe', None)
        if last is not None:
            lrows, loreg, lins, linc = last
            disjoint = rows[1] <= lrows[0] or lrows[1] <= rows[0]
            samebank = (loreg[0] == oreg[0]) and loreg[3] < oreg[4] and oreg[3] < loreg[4]
            if disjoint and samebank:
                if not linc:
                    S.cnt['pe'] += 1
                    lins.then_inc(S.sem['pe'], 1)
                S.wait('pe', S.sem['pe'], S.cnt['pe'])
        return rows, oreg

    def mm(self, out, lhsT, rhs, start=True, stop=True, inc=None):
        if inc is None:
            inc = stop
        rows, oreg = self._pe_rowtile_guard(lhsT, out)
        ins = self.S.emit('pe', [lhsT, rhs], [out],
                          lambda g: g.matmul(out, lhsT=lhsT, rhs=rhs, start=start, stop=stop), inc=inc)
        self._last_pe = (rows, oreg, ins, inc)
        return ins

    def tr(self, out, in_, inc=True, bf=False):
        n = in_.shape[0]
        ident = self.IDB[0:n, 0:n] if bf else self.CST[0:n, CC['IDENT']:CC['IDENT'] + n]
        rows, oreg = self._pe_rowtile_guard(in_, out)
        ins = self.S.emit('pe', [in_, ident], [out],
                          lambda g: g.transpose(out=out, in_=in_, identity=ident), inc=inc)
        self._last_pe = (rows, oreg, ins, inc)
        return ins

    def memset(self, e, ap, v):
        return self.S.emit(e, [], [ap], lambda g: g.memset(ap, v))

    def scan(self, out, d0, d1, init, op0, op1):
        rd = [d0, d1] + ([] if isinstance(init, (int, float)) else [init])
        return self.S.emit('dve', rd, [out], lambda g: g.tensor_tensor_scan(out=out, data0=d0, data1=d1, initial=init, op0=op0, op1=op1))

    def recip(self, out, in_):
        return self.S.emit('dve', [in_], [out], lambda g: g.reciprocal(out=out, in_=in_))

    def reduce(self, out, in_, op, axis=AX.X):
        return self.S.emit('dve', [in_], [out], lambda g: g.tensor_reduce(out=out, in_=in_, axis=axis, op=op))

    def dump(self, name, ap):
        if not self.debug or name in self.dbg_map:
            return
        if getattr(self, 'dbg_filter', None) is not None and not any(name.startswith(p) for p in self.dbg_filter):
            return
        shp = list(ap.shape)
        npart, nfree = shp[0], _prod(shp[1:])
        stage = self.XN[1]
        assert nfree <= 1024
        dst = self.V(stage, 0, npart, 0, [[_prod(shp[i + 1:]), shp[i]] for i in range(1, len(shp))])
        self.cp('dve', dst, ap)
        self.S.dma('sp', self.dbg[0:npart, self.dbg_off:self.dbg_off + nfree], stage[0:npart, 0:nfree], allow_slow_non_contiguous=True)
        self.dbg_map[name] = (self.dbg_off, npart, shp[1:])
        self.dbg_off += nfree

    def ppc(self, l, key, j=0, rows=128):
        c = l * PL + PC[key] + j
        return self.PP[0:rows, c:c + 1]

    def dvc(self, l, key, j=0, rows=128):
        c = l * DL + DC[key] + j
        return self.DV[0:rows, c:c + 1]

    def setup(self):
        S = self.S
        S.dma('sp', self.CST[:, :], self.cst[:, :])
        S.dma('sp', self.PP[:, :], self.pp[:, :])
        S.dma('sp', self.LORA[:, :, :, :], self.lora[:, :, :, :])
        S.dma('sp', self.LRUW[:, :, :, :], self.lruw[:, :, :, :])
        self.memset('dve', self.DV[:, :], 0.0)
        self.cp('dve', self.IDB[:, :], self.CST[:, CC['IDENT']:CC['IDENT'] + 128])
        self.cp('dve', self.ISTKB[:, :], self.CST[:, CC['ISTK']:CC['ISTK'] + 64])
        for l in range(2):
            self.ts('dve', self.DV[:, l * DL + DC['OMKA']: l * DL + DC['OMKA'] + 3],
                    self.PP[:, l * PL + PC['R_KA']: l * PL + PC['R_KA'] + 3], -1.0, 1.0, ALU.mult, ALU.add)
            self.ts('dve', self.DV[:, l * DL + DC['OMMU']: l * DL + DC['OMMU'] + 11],
                    self.PP[:, l * PL + PC['R_MU']: l * PL + PC['R_MU'] + 11], -1.0, 1.0, ALU.mult, ALU.add)
            lam = self.PP[:, l * PL + PC['L_LAM']: l * PL + PC['L_LAM'] + 4]
            t0 = self.TMPS[:, 0:4]
            self.act(t0, lam, AF.Exp, scale=-1.0)
            self.act(t0, t0, AF.Ln, bias=1.0)
            self.ts('dve', self.DV[:, l * DL + DC['CLAM']: l * DL + DC['CLAM'] + 4], t0, -8.0, None, ALU.mult)
            self.ts('dve', self.DV[:, l * DL + DC['C2LAM']: l * DL + DC['C2LAM'] + 4], t0, -16.0, None, ALU.mult)
            self.ts('dve', self.DV[0:6, l * DL + DC['NBF']: l * DL + DC['NBF'] + 2],
                    self.PP[0:6, l * PL + PC['M_BF']: l * PL + PC['M_BF'] + 2], -1.0, None, ALU.mult)
            for hp in range(3):
                self.ts('dve', self.RKD[:, l, hp, :], self.CST[:, CC['BONES']:CC['BONES'] + 128],
                        self.ppc(l, 'R_RK', hp), None, ALU.mult)
        self.memset('dve', self.XL[:, :, 0:2], 0.0)

    def load_layer(self, l):
        S = self.S
        wsrc = self.w_in[l].rearrange("(kc p) c -> p kc c", p=128)
        wo = self.w_out[l].rearrange("(kc p) c -> p kc c", p=128)
        pieces = [(self.W_IN, wsrc, c0, min(256, IN_COLS - c0)) for c0 in range(0, IN_COLS, 256)]
        pieces += [(self.W_OUT, wo, c0, 256) for c0 in range(0, D, 256)]
        def emit_pieces(lo, hi):
            for i in range(lo, min(hi, len(pieces))):
                dst, src, c0, n = pieces[i]
                stg = self.WSTG[i % 3]
                S.dma('sp', stg[:, :, 0:n], src[:, :, c0:c0 + n])
                self.cp('pool', dst[:, :, c0:c0 + n], stg[:, :, 0:n])
        if self.stop == 'load_w':
            raise StopIteration
        S.dma('sp', self.SCT[:, :, :], self.cc[:, :, :])
        S.dma('sp', self.BMT[:, :, :], self.bmodT[:, :, :])
        self.act(self.SCT[:, :, :], self.SCT[:, :, :], AF.Silu)
        for m in range(2):
            for kc in range(8):
                src = self.V(self.SCT, 0, 128, kc * 2 + m, [[0, 128]])
                self.cp('dve', self.SCB[:, m, kc, :], src)
        wm = self.w_mod[l].rearrange("(kc p) c -> p kc c", p=128)
        for blk in range(6):
            self.WM = self.WMs[blk % 2]
            S.dma('sp', self.WM[:, :, :], wm[:, :, blk * 512:(blk + 1) * 512])
            emit_pieces(blk * 4, blk * 4 + 4)
            if blk < 4:
                ps = self.pq()
                for kc in range(8):
                    self.mm(ps[0:2, :], self.SCT[:, kc, :], self.WM[:, kc, :], start=(kc == 0), stop=(kc == 7))
                self.cp('act', self.MODROW[0:2, :], ps[0:2, :])
                ps2 = self.pq()
                for j in range(4):
                    self.tr(ps2[:, j * 2:j * 2 + 2], self.MODROW[0:2, j * 128:(j + 1) * 128])
                o = self.MODT[:, blk * 4:(blk + 1) * 4, :]
                bsrc = self.V(self.BMT, 0, 128, l * 16 + blk * 4, [[1, 4], [0, 2]])
                self.tt('dve', o, self.V(ps2, 0, 128, 0, [[2, 4], [1, 2]]), bsrc, ALU.add)
            else:
                half = blk - 4
                for m in range(2):
                    ps = self.pq()
                    for kc in range(8):
                        self.mm(ps[:, :], self.SCB[:, m, kc, :], self.WM[:, kc, :], start=(kc == 0), stop=(kc == 7))
                    self.cp('act', self.GATEB[:, m, half * 512:(half + 1) * 512], ps[:, :])
        if self.stop == 'load_m':
            raise StopIteration
        for m in range(2):
            self.cp('dve', self.SH[:, m, :], self.V(self.MODT, 0, 128, m, [[2, 8]]))
            t0 = self.TMPS[:, 8:16]
            self.ts('dve', t0, self.V(self.MODT, 0, 128, 16 + m, [[2, 8]]), 1.0, None, ALU.add)
            self.tt('dve', self.GS[:, m, :], t0, self.PP[:, l * PL + PC['G_PRE']: l * PL + PC['G_PRE'] + 8], ALU.mult)
        if self.stop == 'load_g':
            raise StopIteration
        S.dma('sp', self.BG[:, :], bass.AP(self.bmodg.tensor, l * D, [[0, 128], [1, D]]))
        S.dma('sp', self.GP[:, :], bass.AP(self.gpost.tensor, l * D, [[0, 128], [1, D]]))
        if self.stop == 'load_b':
            raise StopIteration
        for m in range(2):
            self.tt('dve', self.GATEB[:, m, :], self.GATEB[:, m, :], self.BG[:, :], ALU.add)
            self.tt('dve', self.GATEB[:, m, :], self.GATEB[:, m, :], self.GP[:, :], ALU.mult)

    def proj_fm(self, c0, ncols, t_off, ntok, evac):
        ps = self.pq()
        for kc in range(8):
            self.mm(ps[0:ncols, 0:ntok], self.W_IN[:, kc, c0:c0 + ncols], self.HT[:, kc, t_off:t_off + ntok],
                    start=(kc == 0), stop=(kc == 7))
        evac(ps[0:ncols, 0:ntok])

    def proj_tm(self, c0, ncols, t_off, evac):
        ps = self.pq()
        for kc in range(8):
            self.mm(ps[0:64, 0:ncols], self.HT[:, kc, t_off:t_off + 64], self.W_IN[:, kc, c0:c0 + ncols],
                    start=(kc == 0), stop=(kc == 7))
        evac(ps[0:64, 0:ncols])

    def visit(self, l, mod, xsrc, xdst, row0, T, t0, dirs, grid, seq_idx, first, last, mode='full'):
        S = self.S
        w0 = max(0, t0 - 64)
        w1 = min(T, t0 + TB + 64)
        W = w1 - w0
        co = t0 - w0
        do_f = 0 in dirs
        prompt = (mod == 0)
        if mode != 'load':
            ntile = (W + 127) // 128
            for i in range(ntile):
                n = min(128, W - i * 128)
                xw, xn = self.XW[i % 2], self.XN[i % 2]
                r = row0 + w0 + i * 128
                S.dma('sp', xw[0:n, :], xsrc[r:r + n, :])
                ssq = self.TMPS[0:n, 16 + i:17 + i]
                self.act(xn[0:n, :], xw[0:n, :], AF.Square, accum=ssq)
                if self.stop == 'n1':
                    raise StopIteration
                rs = self.TMPS[0:n, 20 + i:21 + i]
                self.ts('dve', rs, ssq, 1.0 / D, EPS, ALU.mult, ALU.add)
                self.act(rs, rs, AF.Sqrt)
                self.recip(rs, rs)
                if self.stop == 'n2':
                    raise StopIteration
                self.act(xn[0:n, :], xw[0:n, :], AF.Copy, scale=rs)
                if self.stop == 'n3':
                    raise StopIteration
                for half in range(2):
                    ps = self.pq()
                    for j in range(4):
                        kc = half * 4 + j
                        self.tr(ps[:, j * 128:j * 128 + n], xn[0:n, kc * 128:(kc + 1) * 128])
                    if self.stop == 'n4':
                        raise StopIteration
                    for j in range(4):
                        kc = half * 4 + j
                        o = self.HT[:, kc, i * 128:i * 128 + n]
                        if half == 0:
                            self.ts('dve', o, ps[:, j * 128:j * 128 + n], self.GS[:, mod, kc:kc + 1], self.SH[:, mod, kc:kc + 1], ALU.mult, ALU.add)
                        else:
                            self.act(o, ps[:, j * 128:j * 128 + n], AF.Identity, bias=self.SH[:, mod, kc:kc + 1], scale=self.GS[:, mod, kc:kc + 1])
        if self.stop in ('norm', 'n5a', 'n5d'):
            raise StopIteration
        self.stage_mlstm(l, t0, co, dirs, prompt, seq_idx, first, last, mode)
        if self.stop == 'mlstm':
            raise StopIteration
        self.stage_lru(l, t0, co, W, w0, w1, T, dirs, prompt, seq_idx, first, last, mode)
        if self.stop == 'lru':
            raise StopIteration
        self.stage_rwkv(l, t0, co, W, w0, w1, T, dirs, grid, prompt, seq_idx, first, last, mode)
        for kc in range(8):
            self.dump('mix%d' % kc, self.MIXT[:, kc, :])
        if self.stop == 'rwkv':
            raise StopIteration
        if do_f:
            self.stage_out(l, mod, xsrc, xdst, row0 + t0)
        if self.stop == 'out':
            raise StopIteration

    def stage_mlstm(self, l, t0, co, dirs, prompt, seq_idx, first, last, mode='full'):
        S = self.S
        do_f = 0 in dirs
        if mode != 'load':
            for d in (sorted(set(dirs) | {0}) if mode == 'store' else dirs):
                g = self.mg[d]
                self.proj_fm(1920 + d * 6, 6, co, 256, lambda ps, g=g, d=d: self.act(g["gi"][:, :], ps, AF.Identity, bias=self.ppc(l, 'M_BI', d, 6)))
                def ev_f(ps, g=g, d=d):
                    self.act(g["lf"][:, :], ps, AF.Exp, bias=self.dvc(l, 'NBF', d, 6), scale=-1.0)
                    self.act(g["lf"][:, :], g["lf"][:, :], AF.Ln, bias=1.0)
                    self.ts('dve', g["lf"][:, :], g["lf"][:, :], -1.0, None, ALU.mult)
                self.proj_fm(1932 + d * 6, 6, co, 256, ev_f)
        bi = t0 // 256
        ml_items = [(self.QT[:, :, :], 'QT'), (self.KT[:, :, :], 'KT'), (self.KTOK[:, :, :], 'KTOK'), (self.VAUG[:, :, :, :], 'VAUG'),
                    (self.GOZ[:, :, :], 'GOZ'), (self.mg[0]["gi"][:, :], 'GI'), (self.mg[0]["lf"][:, :], 'LF')]
        if mode == 'load':
            for ap_, k_ in ml_items:
                S.dma('sp', ap_, self.sc[k_][bi])
        for d in dirs:
            if first[d]:
                if prompt:
                    self.memset('dve', self.mC[d][:, :, :], 0.0)
                    self.memset('dve', self.mM[d][:, :], 0.0)
                else:
                    S.dma('sp', self.mC[d][:, :, :], self.st_mC[l, d])
                    S.dma('sp', self.mM[d][:, :], self.st_mm[:, l, d:d + 1], allow_slow_non_contiguous=True)
        if do_f and not (1 in dirs):
            S.dma('sp', self.HB[:, :, :], self.sHB[t0:t0 + 256, :].rearrange("(c s) f -> s c f", s=64))
        for d in dirs:
            g = self.mg[d]
            v3 = lambda t: self.V(t, 0, 6, 0, [[64, 4], [1, 64]])
            self.scan(g["pre"][:, :], self.CST[0:6, CC['RMASK']:CC['RMASK'] + 256], g["lf"][:, :], 0.0, ALU.mult, ALU.add)
            bL = self.V(g["pre"], 0, 6, 63, [[64, 4]])
            bLb = self.V(g["pre"], 0, 6, 63, [[64, 4], [0, 64]])
            if d == 0:
                bsrc = g["pre"]
            else:
                self.tt('dve', v3(g["bb"]), bLb, v3(g["pre"]), ALU.subtract)
                self.tt('dve', g["bb"][:, :], g["bb"][:, :], g["lf"][:, :], ALU.add)
                bsrc = g["bb"]
            self.tt('dve', g["gg"][:, :], g["gi"][:, :], bsrc[:, :], ALU.subtract)
            self.reduce(g["mx"][:, :], v3(g["gg"]), ALU.max)
            if d == 0:
                mo, mxv, blv = g["mch"][:, :], g["mx"][:, :], bL
            else:
                mo = self.V(g["mch"], 0, 6, 3, [[-1, 4]])
                mxv = self.V(g["mx"], 0, 6, 3, [[-1, 4]])
                blv = self.V(g["pre"], 0, 6, 63 + 3 * 64, [[-64, 4]])
            self.scan(mo, mxv, blv, self.mM[d][:, 0:1], ALU.max, ALU.add)
            if d == 0:
                self.cp('dve', g["mprev"][:, 1:4], g["mch"][:, 0:3])
                self.cp('dve', g["mprev"][:, 0:1], self.mM[d][:, 0:1])
                mfin = g["mch"][:, 3:4]
            else:
                self.cp('dve', g["mprev"][:, 0:3], g["mch"][:, 1:4])
                self.cp('dve', g["mprev"][:, 3:4], self.mM[d][:, 0:1])
                mfin = g["mch"][:, 0:1]
            self.tt('dve', g["MM"][:, :], g["mprev"][:, :], g["mx"][:, :], ALU.max)
            self.tt('dve', g["dec"][:, :], g["mprev"][:, :], g["MM"][:, :], ALU.subtract)
            self.act(g["dec"][:, :], g["dec"][:, :], AF.Exp)
            self.cp('dve', self.mM[d][:, 0:1], mfin)
            MMb = self.V(g["MM"], 0, 6, 0, [[1, 4], [0, 64]])
            self.tt('dve', v3(g["ee"]), v3(g["gg"]), MMb, ALU.subtract)
            self.act(g["ee"][:, :], g["ee"][:, :], AF.Exp)
            self.tt('dve', v3(g["fl"]), v3(bsrc), MMb, ALU.add)
            self.act(g["fl"][:, :], g["fl"][:, :], AF.Exp, scale=-1.0)
        if mode != 'load':
            for hp in range(3):
                self.proj_fm(hp * 128, 128, co, 256, lambda ps, hp=hp: self.cp('act', self.QT[:, hp, :], ps))
                self.proj_fm(384 + hp * 128, 128, co, 256, lambda ps, hp=hp: self.act(self.KT[:, hp, :], ps, AF.Copy, scale=0.125))
            if do_f or mode == 'store':
                for hp in range(3):
                    def ev_o(ps, hp=hp):
                        self.act(self.GOZ[:, hp, :], ps, AF.Sigmoid)
                    self.proj_fm(1152 + hp * 128, 128, co, 256, ev_o)
                    def ev_z2(ps, hp=hp):
                        tz = self.ZT[hp % 2]
                        self.act(tz[:, :], ps, AF.Silu)
                        self.tt('dve', self.GOZ[:, hp, :], self.GOZ[:, hp, :], tz[:, :], ALU.mult)
                    self.proj_fm(1536 + hp * 128, 128, co, 256, ev_z2)
            for c in range(4):
                self.proj_tm(384, 384, co + c * 64, lambda ps, c=c: self.act(self.KTOK[:, c, :], ps, AF.Copy, scale=0.125))
                def ev_v(ps, c=c):
                    self.cp('dve', self.VAUG[:, c, :, 0:64], self.V(ps.tensor, 0, 64, 0, [[64, 6], [1, 64]]))
                self.proj_tm(768, 384, co + c * 64, ev_v)
            self.memset('dve', self.VAUG[:, :, :, 64:65], 1.0)
        for d in dirs:
            g = self.mg[d]
            ps = self.pq()
            for c in range(4):
                self.tr(ps[0:64, c * 6:c * 6 + 6], g["ee"][:, c * 64:(c + 1) * 64])
                self.tr(ps[0:64, 24 + c * 6:24 + c * 6 + 6], g["fl"][:, c * 64:(c + 1) * 64])
            self.cp('dve', g["etok"][:, :, :], self.V(ps, 0, 64, 0, [[6, 4], [1, 6]]))
            self.cp('dve', g["fltok"][:, :, :], self.V(ps, 0, 64, 24, [[6, 4], [1, 6]]))
            self.tt('dve', g["X2"][:, :, :], self.V(g["dec"], 0, 6, 0, [[1, 4], [0, 3]]),
                    self.V(self.CST, 0, 6, CC['PSEL'], [[0, 4], [1, 3]]), ALU.mult)
            ps2 = self.pq()
            self.mm(ps2[:, 0:12], self.CST[0:6, CC['LSEL']:CC['LSEL'] + 128], self.V(g["X2"], 0, 6, 0, [[1, 12]]))
            self.cp('dve', g["decb"][:, :, :], self.V(ps2, 0, 128, 0, [[3, 4], [1, 3]]))
        if mode == 'store':
            for ap_, k_ in ml_items:
                S.dma('sp', self.sc[k_][bi], ap_)
        for d in sorted(dirs, reverse=True):
            g = self.mg[d]
            mask = self.CST[0:64, CC['MUI']:CC['MUI'] + 64] if d == 0 else self.CST[0:64, CC['MLI']:CC['MLI'] + 64]
            maskb = self.V(self.CST, 0, 64, CC['MUI'] if d == 0 else CC['MLI'], [[0, 6], [1, 64]])
            for j in range(4):
                c = j if d == 0 else 3 - j
                cs = slice(c * 64, (c + 1) * 64)
                stsb, vp = self.STSB[j % 2], self.VP[j % 2]
                ps = self.pq()
                for h in (0, 2, 4, 1, 3, 5):
                    hp, pb = h // 2, 64 * (h % 2)
                    self.mm(ps[0:64, h * 64:(h + 1) * 64], self.KT[pb:pb + 64, hp, cs], self.QT[pb:pb + 64, hp, cs], inc=(h == 5))
                self.tt('dve', stsb[:, :, :], self.V(ps, 0, 64, 0, [[64, 6], [1, 64]]), maskb, ALU.mult)
                self.tt('dve', vp[:, :, :], self.VAUG[:, c, :, :], self.V(g["etok"], 0, 64, c * 6, [[1, 6], [0, 65]]), ALU.mult)
                self.tt('dve', self.CDEC[:, :, :], self.mC[d][:, :, :], self.V(g["decb"], 0, 128, c * 3, [[1, 3], [0, 65]]), ALU.mult)
                self.cp('act', self.CDBF[:, :, :], self.CDEC[:, :, :])
                ph = self.pq()
                for h in (0, 2, 4, 1, 3, 5):
                    hp, pb = h // 2, 64 * (h % 2)
                    o = ph[0:64, h * 65:(h + 1) * 65]
                    self.mm(o, stsb[:, h, :], vp[:, h, :], start=True, stop=False)
                    self.mm(o, self.QT[pb:pb + 64, hp, cs], self.CDBF[pb:pb + 64, hp, :], start=False, stop=True, inc=(h == 5))
                pc = self.pq()
                for h in range(6):
                    hp, pb = h // 2, 64 * (h % 2)
                    self.mm(pc[pb:pb + 64, hp * 65:(hp + 1) * 65], self.KTOK[:, c, h * 64:(h + 1) * 64], vp[:, h, :], inc=(h == 5))
                self.tt('dve', self.mC[d][:, :, :], self.CDEC[:, :, :], self.V(pc, 0, 128, 0, [[65, 3], [1, 65]]), ALU.add)
                self.act(self.DN[:, :], self.V(ph, 0, 64, 64, [[65, 6]]), AF.Abs)
                self.tt('dve', self.DN[:, :], self.DN[:, :], g["fltok"][:, c, :], ALU.max)
                self.recip(self.RDN[:, :], self.DN[:, :])
                hsrc = self.V(ph, 0, 64, 0, [[65, 6], [1, 64]])
                rb = self.V(self.RDN, 0, 64, 0, [[1, 6], [0, 64]])
                hbv = self.V(self.HB, 0, 64, c * 384, [[64, 6], [1, 64]])
                if d == 1:
                    self.tt('dve', hbv, hsrc, rb, ALU.mult)
                else:
                    self.tt('dve', self.HD[:, :, :], hsrc, rb, ALU.mult)
                    self.tt('dve', hbv, hbv, self.HD[:, :, :], ALU.add)
            if last[d] and prompt:
                S.dma('sp', self.o_mC[seq_idx, l, d], self.mC[d][:, :, :])
                S.dma('sp', self.o_mm[seq_idx, l, d], self.mM[d][:, 0:1])
        if not do_f:
            S.dma('sp', self.sHB[t0:t0 + 256, :].rearrange("(c s) f -> s c f", s=64), self.HB[:, :, :])
            return
        self.tt('dve', self.SQ[:, :, :], self.HB[:, :, :], self.HB[:, :, :], ALU.mult)
        self.reduce(self.SSQ[:, :], self.V(self.SQ, 0, 64, 0, [[64, 24], [1, 64]]), ALU.add)
        self.ts('dve', self.SSQ[:, :], self.SSQ[:, :], 1.0 / 64, EPS, ALU.mult, ALU.add)
        self.act(self.SSQ[:, :], self.SSQ[:, :], AF.Sqrt)
        self.recip(self.RSTD[:, :], self.SSQ[:, :])
        self.tt('dve', self.V(self.SQ, 0, 64, 0, [[64, 24], [1, 64]]), self.V(self.HB, 0, 64, 0, [[64, 24], [1, 64]]),
                self.V(self.RSTD, 0, 64, 0, [[1, 24], [0, 64]]), ALU.mult)
        for hp in range(3):
            ps = self.pq()
            for c in range(4):
                self.tr(ps[:, c * 64:(c + 1) * 64], self.SQ[:, c, hp * 128:(hp + 1) * 128], inc=(c == 3))
            self.stt(self.MIXT[:, hp, :], ps[:, 0:256], self.ppc(l, 'M_NORM', hp), self.GOZ[:, hp, :], ALU.mult, ALU.mult)

    def stage_lru(self, l, t0, co, W, w0, w1, T, dirs, prompt, seq_idx, first, last, mode='full'):
        S = self.S
        do_f = 0 in dirs
        if mode != 'load':
            for pr in range(2):
                self.proj_fm(3736 + pr * 128, 128, 0, W, lambda ps, pr=pr: self.cp('act', self.XL[:, pr, 2:2 + W], ps))
                if do_f or mode == 'store':
                    self.proj_fm(3992 + pr * 128, 128, co, 256, lambda ps, pr=pr: self.act(self.LZT[:, pr, :], ps, AF.Silu))
            if w1 == T:
                self.memset('dve', self.XL[:, :, 2 + W:2 + W + 1], 0.0)
            if w0 == 0:
                self.memset('dve', self.XL[:, :, 0:2], 0.0)
            for pr in range(2):
                self.ts('dve', self.XC[:, pr, :], self.XL[:, pr, co:co + 256], self.ppc(l, 'L_CONV', 0 * 2 + pr), self.ppc(l, 'L_CONVB', pr), ALU.mult, ALU.add)
                for j in range(1, 4):
                    self.stt(self.XC[:, pr, :], self.XL[:, pr, co + j:co + j + 256], self.ppc(l, 'L_CONV', j * 2 + pr), self.XC[:, pr, :], ALU.mult, ALU.add)
        bi = t0 // 256
        lr_items = [(self.XC[:, :, :], 'XC'), (self.LZT[:, :, :], 'LZT')]
        if mode == 'store':
            for ap_, k_ in lr_items:
                S.dma('sp', self.sc[k_][bi], ap_)
        if mode == 'load':
            for ap_, k_ in lr_items:
                S.dma('sp', ap_, self.sc[k_][bi])
        for d in dirs:
            if first[d]:
                if prompt:
                    self.memset('dve', self.lS[d][:, :], 0.0)
                else:
                    S.dma('sp', self.lS[d][:, :], self.st_l[:, l, d, :])
        if do_f and not (1 in dirs):
            S.dma('sp', self.HL[1][:, :, :], self.sLB[:, :, t0:t0 + 256])
        combos = [(d, pr) for d in sorted(dirs, reverse=True) for pr in range(2)]
        LT = lambda d, n, pr: self.ltd[d][n][:, pr, :]
        pss = {}
        for i, (d, pr) in enumerate(combos):
            ps = self.PA if i % 2 == 0 else self.PB
            off = (i // 2) * 512
            pss[(d, pr)] = (ps, off)
            self.mm(ps[:, off:off + 256], self.LRUW[:, l, (0 * 2 + d) * 2 + pr, :], self.XC[:, pr, :])
            self.mm(ps[:, off + 256:off + 512], self.LRUW[:, l, (1 * 2 + d) * 2 + pr, :], self.XC[:, pr, :])
        for (d, pr) in combos:
            ps, off = pss[(d, pr)]
            self.act(LT(d, "rg", pr), ps[:, off:off + 256], AF.Sigmoid, bias=self.ppc(l, 'L_BA', d * 2 + pr))
            self.act(LT(d, "ig", pr), ps[:, off + 256:off + 512], AF.Sigmoid, bias=self.ppc(l, 'L_BX', d * 2 + pr))
        for (d, pr) in combos:
            self.act(LT(d, "aa", pr), LT(d, "rg", pr), AF.Exp, scale=self.dvc(l, 'CLAM', d * 2 + pr))
            self.act(LT(d, "a2", pr), LT(d, "rg", pr), AF.Exp, scale=self.dvc(l, 'C2LAM', d * 2 + pr))
        for (d, pr) in combos:
            self.ts('dve', LT(d, "a2", pr), LT(d, "a2", pr), -1.0, 1.0, ALU.mult, ALU.add)
            self.tt('dve', LT(d, "bt", pr), LT(d, "ig", pr), self.XC[:, pr, :], ALU.mult)
        for (d, pr) in combos:
            self.act(LT(d, "a2", pr), LT(d, "a2", pr), AF.Sqrt)
        for (d, pr) in combos:
            self.tt('dve', LT(d, "bt", pr), LT(d, "bt", pr), LT(d, "a2", pr), ALU.mult)
        for (d, pr) in combos:
            if d == 0:
                self.scan(self.HL[0][:, pr, :], LT(0, "aa", pr), LT(0, "bt", pr), self.lS[0][:, pr:pr + 1], ALU.mult, ALU.add)
                self.cp('act', self.lS[0][:, pr:pr + 1], self.HL[0][:, pr, 255:256])
            else:
                rv = lambda t: self.V(t, 0, 128, pr * 256 + 255, [[-1, 256]])
                self.scan(rv(self.HL[1]), rv(self.ltd[1]["aa"]), rv(self.ltd[1]["bt"]), self.lS[1][:, pr:pr + 1], ALU.mult, ALU.add)
                self.cp('act', self.lS[1][:, pr:pr + 1], self.HL[1][:, pr, 0:1])
        for d in sorted(dirs, reverse=True):
            if last[d] and prompt:
                S.dma('sp', self.o_l[seq_idx, l, d], self.lS[d][:, :])
        if not do_f:
            S.dma('sp', self.sLB[:, :, t0:t0 + 256], self.HL[1][:, :, :])
            return
        self.tt('dve', self.HL[0][:, :, :], self.HL[0][:, :, :], self.HL[1][:, :, :], ALU.add)
        self.tt('dve', self.MIXT[:, 6:8, :], self.HL[0][:, :, :], self.LZT[:, :, :], ALU.mult)

    def stage_rwkv(self, l, t0, co, W, w0, w1, T, dirs, grid, prompt, seq_idx, first, last, mode='full'):
        S = self.S
        do_f = 0 in dirs
        if mode != 'load':
            for ch in range(11):
                self.proj_fm(1944 + ch * 128, 128, 0, W, lambda ps, ch=ch: self.cp('act' if ch % 2 else 'dve', self.URS[:, ch, 0:W], ps))
            if do_f or mode == 'store':
                for hp in range(3):
                    self.proj_fm(3352 + hp * 128, 128, co, 256, lambda ps, hp=hp: self.act(self.RZT[:, hp, :], ps, AF.Silu))
            U3 = lambda off, n: self.V(self.URS, 0, 128, off, [[384, 11], [1, n]])
            B3 = lambda off, n: self.V(self.BLK, 0, 128, off, [[256, 11], [1, n]])
            if not grid:
                self.cp('dve', B3(1, 255), U3(0, 255))
                self.memset('dve', B3(0, 1), 0.0)
                self.tt('dve', B3(0, 255), B3(0, 255), U3(1, 255), ALU.add)
                wsh = 0.5
            else:
                U4 = lambda off, r, n: self.V(self.URS, 0, 128, off, [[384, 11], [64, r], [1, n]])
                B4 = lambda off, r, n: self.V(self.BLK, 0, 128, off, [[256, 11], [64, r], [1, n]])
                self.cp('dve', B4(1, 4, 63), U4(co, 4, 63))
                self.memset('dve', B4(0, 4, 1), 0.0)
                self.tt('dve', B4(0, 4, 63), B4(0, 4, 63), U4(co + 1, 4, 63), ALU.add)
                if t0 > 0:
                    self.tt('dve', B3(0, 256), B3(0, 256), U3(co - 64, 256), ALU.add)
                else:
                    self.tt('dve', B3(64, 192), B3(64, 192), U3(0, 192), ALU.add)
                if t0 + TB < T:
                    self.tt('dve', B3(0, 256), B3(0, 256), U3(co + 64, 256), ALU.add)
                else:
                    self.tt('dve', B3(0, 192), B3(0, 192), U3(co + 64, 192), ALU.add)
                wsh = 0.25
            mu = self.V(self.PP, 0, 128, l * PL + PC['R_MU'], [[1, 11], [0, 256]])
            self.tt('dve', B3(0, 256), B3(0, 256), mu, ALU.mult)
            omm = self.V(self.DV, 0, 128, l * DL + DC['OMMU'], [[1, 11], [0, 256]])
            self.tt('dve', U3(co, 256), U3(co, 256), omm, ALU.mult)
            self.stt(B3(0, 256), B3(0, 256), wsh, U3(co, 256), ALU.mult, ALU.add)
            for nm, ch in (('blk_r', 0), ('blk_k', 3), ('blk_v', 6), ('blk_wl', 9), ('blk_al', 10)):
                self.dump(nm, self.BLK[:, ch, :])
            rt = self.rt
            kk = self.V(self.PP, 0, 128, l * PL + PC['R_KK'], [[1, 3], [0, 256]])
            kap = rt["d"]
            self.tt('dve', kap[:, :, :], self.BLK[:, 3:6, :], kk, ALU.mult)
            ksq = rt["E"]
            self.tt('dve', ksq[:, :, :], kap[:, :, :], kap[:, :, :], ALU.mult)
            for hp in range(3):
                self.mm(self.PA[:, hp * 256:(hp + 1) * 256], self.CST[:, CC['BONES']:CC['BONES'] + 128], ksq[:, hp, :])
            self.act(ksq[:, :, :], self.V(self.PA, 0, 128, 0, [[256, 3], [1, 256]]), AF.Sqrt)
            self.ts('dve', ksq[:, :, :], ksq[:, :, :], 1e-12, None, ALU.max)
            self.recip(ksq[:, :, :], ksq[:, :, :])
            self.tt('dve', self.KH[:, :, :], kap[:, :, :], ksq[:, :, :], ALU.mult)
            self.dump('kh', self.KH[:, 0, :])
            self.bd_fill(self.BDV, lambda par: self.V(self.BLK, par * 64, 64, 6 * 256, [[256, 3], [64, 4], [1, 64]]))
            for c in range(4):
                for hp in range(3):
                    self.mm(self.PB[:, (c * 3 + hp) * 64:(c * 3 + hp + 1) * 64], self.BDV[:, hp, c, :], self.ISTKB[:, :])
            self.cp('act', self.V(self.VSTK, 0, 128, 0, [[1, 768]]), self.PB[:, 0:768])
        bi = t0 // 256
        rw_items = [(self.BLK[:, :, :], 'BLK'), (self.RZT[:, :, :], 'RZT'), (self.KH[:, :, :], 'KH'), (self.VSTK[:, :, :, :], 'VSTK')]
        if mode == 'store':
            for ap_, k_ in rw_items:
                S.dma('sp', self.sc[k_][bi], ap_)
        if mode == 'load':
            for ap_, k_ in rw_items:
                S.dma('sp', ap_, self.sc[k_][bi])
        for d in dirs:
            if first[d]:
                if prompt:
                    self.memset('dve', self.rH[d][:, :, :], 0.0)
                else:
                    S.dma('sp', self.rH[d][:, :, :], self.st_rH[l, d])
        ybflat = self.V(self.YB, 0, 128, 0, [[1, 768]])
        if do_f and not (1 in dirs):
            S.dma('sp', ybflat, self.sYB[t0 // 256])
        for d in sorted(dirs, reverse=True):
            self.rwkv_dir(l, d, seq_idx, prompt, last)
        if not do_f:
            S.dma('sp', self.sYB[t0 // 256], ybflat)
            return
        ft = self.ft
        self.tt('dve', self.YSQ[:, :, :, :], self.YB[:, :, :, :], self.YB[:, :, :, :], ALU.mult)
        self.reduce(self.rSSQ[:, :], self.V(self.YSQ, 0, 128, 0, [[64, 12], [1, 64]]), ALU.add)
        self.ts('dve', self.rSSQ[:, :], self.rSSQ[:, :], 1.0 / 64, EPS, ALU.mult, ALU.add)
        self.act(self.rSSQ[:, :], self.rSSQ[:, :], AF.Sqrt)
        self.recip(self.rRSTD[:, :], self.rSSQ[:, :])
        self.tt('dve', ft["rk"][:, :, :], self.BLK[:, 0:3, :], self.BLK[:, 3:6, :], ALU.mult)
        for hp in range(3):
            self.mm(self.PB[:, hp * 256:(hp + 1) * 256], self.RKD[:, l, hp, :], ft["rk"][:, hp, :])
        self.tt('dve', ft["bon"][:, :, :], self.V(self.PB, 0, 128, 0, [[256, 3], [1, 256]]), self.BLK[:, 6:9, :], ALU.mult)
        self.memset('pool', self.YBD[:, :, :], 0.0)
        for c in range(4):
            for par in range(2):
                self.tt('dve', self.V(self.YBD, par * 64, 64, par * 64, [[128, 3], [1, 64]]),
                        self.V(self.YB, par * 64, 64, c * 192, [[64, 3], [1, 64]]),
                        self.V(self.rRSTD, par * 64, 64, c * 3, [[1, 3], [0, 64]]), ALU.mult)
            for hp in range(3):
                self.mm(self.PA[:, hp * 256 + c * 64: hp * 256 + (c + 1) * 64], self.YBD[:, hp, :],
                        self.CST[:, CC['ISTK']:CC['ISTK'] + 64])
        for hp in range(3):
            self.stt(ft["t1"][:, hp, :], self.PA[:, hp * 256:(hp + 1) * 256], self.ppc(l, 'R_NORM', hp), ft["bon"][:, hp, :], ALU.mult, ALU.add)
            self.tt('dve', self.MIXT[:, 3 + hp, :], ft["t1"][:, hp, :], self.RZT[:, hp, :], ALU.mult)

    def bd_fill(self, bd, src_of_par, eng='pool'):
        self.memset(eng, bd[:, :, :, :], 0.0)
        for par in range(2):
            self.cp('act' if par == 0 else eng, self.V(bd, par * 64, 64, par * 64, [[512, 3], [128, 4], [1, 64]]), src_of_par(par))

    def rwkv_dir(self, l, d, seq_idx, prompt, last):
        S = self.S
        rt = self.rt
        pb_d = 64 * d
        f3 = lambda t: t[:, :, :]
        v4 = lambda t: self.V(t, 0, 128, 0, [[256, 3], [64, 4], [1, 64]])
        self.act(self.TWL[pb_d:pb_d + 64, :], self.BLK[pb_d:pb_d + 64, 9, :], AF.Tanh)
        for hp in range(3):
            self.mm(self.PA[:, hp * 256:(hp + 1) * 256], self.LORA[pb_d:pb_d + 64, l, 0, hp * 128:(hp + 1) * 128], self.TWL[pb_d:pb_d + 64, :])
        for hp in range(3):
            self.act(rt["sg"][:, hp, :], self.PA[:, hp * 256:(hp + 1) * 256], AF.Sigmoid, bias=self.ppc(l, 'R_W0', d * 3 + hp))
        for hp in range(3):
            self.mm(self.PB[:, hp * 256:(hp + 1) * 256], self.LORA[pb_d:pb_d + 64, l, 1, hp * 128:(hp + 1) * 128], self.BLK[pb_d:pb_d + 64, 10, :])
        for hp in range(3):
            self.act(rt["aa"][:, hp, :], self.PB[:, hp * 256:(hp + 1) * 256], AF.Sigmoid, bias=self.ppc(l, 'R_A0', d * 3 + hp))
        for hp in range(3):
            self.ts('dve', rt["kt"][:, hp, :], rt["aa"][:, hp, :], self.ppc(l, 'R_KA', hp), self.dvc(l, 'OMKA', hp), ALU.mult, ALU.add)
        self.tt('dve', f3(rt["kt"]), f3(rt["kt"]), self.BLK[:, 3:6, :], ALU.mult)
        self.tt('dve', f3(rt["bb"]), self.KH[:, :, :], f3(rt["aa"]), ALU.mult)
        flat = lambda t: self.V(t, 0, 128, 0, [[1, 768]])
        self.scan(flat(rt["cs"]), self.CST[:, CC['RMASK']:CC['RMASK'] + 768], flat(rt["sg"]), 0.0, ALU.mult, ALU.add)
        self.cp('dve', self.GL[:, :, :], self.V(rt["cs"], 0, 128, 63, [[256, 3], [64, 4]]))
        if d == 1:
            csLb = self.V(rt["cs"], 0, 128, 63, [[256, 3], [64, 4], [0, 64]])
            self.tt('dve', v4(rt["d"]), csLb, v4(rt["cs"]), ALU.subtract)
            self.tt('dve', f3(rt["cs"]), f3(rt["d"]), f3(rt["sg"]), ALU.add)
        self.act(f3(rt["E"]), f3(rt["cs"]), AF.Exp, scale=-DSC)
        self.tt('dve', self.V(self.KR, 0, 128, 64, [[512, 3], [128, 4], [1, 64]]),
                self.V(self.BLK, 0, 128, 0, [[256, 3], [64, 4], [1, 64]]), v4(rt["E"]), ALU.mult)
        self.tt('dve', f3(rt["d"]), f3(rt["cs"]), f3(rt["sg"]), ALU.subtract)
        self.act(f3(rt["E"]), f3(rt["d"]), AF.Exp, scale=-DSC)
        self.tt('dve', self.V(self.KR, 0, 128, 0, [[512, 3], [128, 4], [1, 64]]), v4(self.KH), v4(rt["E"]), ALU.mult)
        self.act(f3(rt["E"]), f3(rt["cs"]), AF.Exp, scale=DSC)
        self.tt('dve', self.BTT[:, :, :], f3(rt["bb"]), f3(rt["E"]), ALU.mult)
        self.tt('dve', self.KTT[:, :, :], f3(rt["kt"]), f3(rt["E"]), ALU.mult)
        self.act(self.GL[:, :, :], self.GL[:, :, :], AF.Exp, scale=-DSC)
        BD = self.BD
        self.memset('pool', self.A1BD[:, :, :, :], 0.0)
        self.memset('pool', self.A2BD[:, :, :, :], 0.0)
        self.memset('pool', self.NN[0][:, :, :], 0.0)
        c4 = lambda t, par: self.V(t, par * 64, 64, 0, [[256, 3], [64, 4], [1, 64]])
        self.bd_fill(BD["kt"], lambda par: c4(self.KTT, par))
        self.bd_fill(BD["b"], lambda par: c4(self.BTT, par))
        self.bd_fill(BD["kh"], lambda par: self.V(self.KR, par * 64, 64, 0, [[512, 3], [128, 4], [1, 64]]))
        self.bd_fill(BD["r"], lambda par: self.V(self.KR, par * 64, 64, 64, [[512, 3], [128, 4], [1, 64]]))
        for (src, dst, neg) in ((BD["kt"], self.KTTOK, False), (BD["b"], self.BTTOK, True)):
            for half in range(2):
                for cc_ in range(2):
                    c = half * 2 + cc_
                    for hp in range(3):
                        self.tr(self.PTB[:, (cc_ * 3 + hp) * 128:(cc_ * 3 + hp + 1) * 128], src[:, hp, c, :], bf=True)
                o = self.V(dst, 0, 128, half * 768, [[1, 768]])
                if neg:
                    self.act(o, self.PTB[:, 0:768], AF.Copy, scale=-1.0)
                else:
                    self.cp('dve', o, self.PTB[:, 0:768])
        self.cp('act', self.HBF[:, :, :], self.rH[d][:, :, :])
        mk = CC['MKF'] if d == 0 else CC['MKB']
        nmk = CC['NMKF'] if d == 0 else CC['NMKB']
        mn = CC['MNF'] if d == 0 else CC['MNB']
        for j in range(4):
            c = j if d == 0 else 3 - j
            cs = slice(c * 64, (c + 1) * 64)
            KRc = lambda hp: self.V(self.KR, 0, 128, hp * 512 + c * 128, [[1, 128]])
            p1, p2, p3 = self.pq(), self.pq(), self.pq()
            for hp in range(3):
                self.mm(p1[:, hp * 128:(hp + 1) * 128], BD["kt"][:, hp, c, :], KRc(hp), inc=(hp == 2))
            for hp in range(3):
                self.mm(p2[:, hp * 128:(hp + 1) * 128], BD["b"][:, hp, c, :], KRc(hp), inc=(hp == 2))
            for hp in range(3):
                self.mm(p3[:, hp * 64:(hp + 1) * 64], BD["kh"][:, hp, c, :], self.BTT[:, hp, cs], inc=(hp == 2))
            for par in range(2):
                pp_ = par * 64
                self.tt('dve', self.V(self.A1BD, pp_, 64, pp_, [[256, 3], [128, 2], [1, 64]]),
                        self.V(p1, pp_, 64, 0, [[128, 3], [64, 2], [1, 64]]),
                        self.V(self.CST, pp_, 64, mk, [[0, 3], [64, 2], [1, 64]]), ALU.mult)
                self.tt('dve', self.V(self.A2BD, pp_, 64, pp_, [[256, 3], [128, 2], [1, 64]]),
                        self.V(p2, pp_, 64, 0, [[128, 3], [64, 2], [1, 64]]),
                        self.V(self.CST, pp_, 64, nmk, [[0, 3], [64, 2], [1, 64]]), ALU.mult)
                self.tt('dve', self.V(self.NN[0], pp_, 64, pp_, [[128, 3], [1, 64]]),
                        self.V(p3, pp_, 64, 0, [[64, 3], [1, 64]]),
                        self.V(self.CST, pp_, 64, mn, [[0, 3], [1, 64]]), ALU.mult)
            pr_ = self.pq()
            for hp in range(3):
                o = pr_[:, hp * 64:(hp + 1) * 64]
                self.mm(o, BD["kh"][:, hp, c, :], self.HBF[:, hp, :], start=True, stop=False)
                self.mm(o, self.A1BD[:, hp, 0, :], self.VSTK[:, c, hp, :], start=False, stop=True, inc=(hp == 2))
            u192 = self.V(self.U, 0, 128, 0, [[1, 192]])
            ub192 = self.V(self.UBF, 0, 128, 0, [[1, 192]])
            self.cp('dve', u192, pr_[:, 0:192])
            self.cp('act', ub192, u192)
            for k in range(6):
                NTk = self.A2BD[:, :, 0, :] if k == 0 else self.NTT[k % 2][:, :, :]
                Nk = self.NN[k % 2]
                pu = self.pq()
                for hp in range(3):
                    self.mm(pu[:, hp * 64:(hp + 1) * 64], NTk[:, hp, :], self.UBF[:, hp, :], inc=(hp == 2))
                if k < 5:
                    pnt = self.pq()
                    for hp in range(3):
                        self.mm(pnt[:, hp * 128:(hp + 1) * 128], Nk[:, hp, :], NTk[:, hp, :], inc=(hp == 2))
                    pn = self.pq()
                    for hp in range(3):
                        self.mm(pn[:, hp * 128:(hp + 1) * 128], NTk[:, hp, :], Nk[:, hp, :], inc=(hp == 2))
                self.tt('dve', ub192, u192, pu[:, 0:192], ALU.add)
                if k < 5:
                    self.cp('act', self.V(self.NTT[(k + 1) % 2], 0, 128, 0, [[1, 384]]), pnt[:, 0:384])
                    self.cp('act', self.V(self.NN[(k + 1) % 2], 0, 128, 0, [[1, 384]]), pn[:, 0:384])
                    self.tt('dve', u192, u192, pu[:, 0:192], ALU.add)
            py = self.pq()
            for hp in range(3):
                o = py[:, hp * 64:(hp + 1) * 64]
                self.mm(o, BD["r"][:, hp, c, :], self.HBF[:, hp, :], start=True, stop=False)
                self.mm(o, self.A1BD[:, hp, 1, :], self.VSTK[:, c, hp, :], start=False, stop=False)
                self.mm(o, self.A2BD[:, hp, 1, :], self.UBF[:, hp, :], start=False, stop=True, inc=(hp == 2))
            ybv = self.V(self.YB, 0, 128, c * 192, [[1, 192]])
            if d == 1:
                self.cp('act', ybv, py[:, 0:192])
            else:
                self.tt('dve', ybv, ybv, py[:, 0:192], ALU.add)
            ph = self.pq()
            for hp in range(3):
                o = ph[:, hp * 64:(hp + 1) * 64]
                self.mm(o, self.KTTOK[:, c, hp, :], self.VSTK[:, c, hp, :], start=True, stop=False)
                self.mm(o, self.BTTOK[:, c, hp, :], self.UBF[:, hp, :], start=False, stop=True, inc=(hp == 2))
            self.tt('dve', self.HTMP[:, :, :], self.rH[d][:, :, :], self.V(ph, 0, 128, 0, [[64, 3], [1, 64]]), ALU.add)
            self.tt('dve', self.rH[d][:, :, :], self.HTMP[:, :, :], self.V(self.GL, 0, 128, c, [[4, 3], [0, 64]]), ALU.mult)
            self.cp('act', self.HBF[:, :, :], self.rH[d][:, :, :])
        if last[d] and prompt:
            S.dma('sp', self.o_rH[seq_idx, l, d], self.rH[d][:, :, :])

    def stage_out(self, l, mod, xsrc, xdst, r0):
        S = self.S
        for tt_ in range(2):
            o, xw = self.XN[tt_], self.XW[tt_]
            S.dma('sp', xw[:, :], xsrc[r0 + tt_ * 128: r0 + (tt_ + 1) * 128, :])
            for ch in range(2):
                ps = self.pq()
                for kc in range(8):
                    self.mm(ps[:, :], self.MIXT[:, kc, tt_ * 128:(tt_ + 1) * 128], self.W_OUT[:, kc, ch * 512:(ch + 1) * 512],
                            start=(kc == 0), stop=(kc == 7))
                self.cp('act' if ch else 'dve', o[:, ch * 512:(ch + 1) * 512], ps[:, :])
            self.dump('o_proj%d' % tt_, o[:, :])
            self.dump('o_x%d' % tt_, xw[:, :])
            ssq = self.TMPS[:, 24 + tt_:25 + tt_]
            junk = self.V(self.BLK, 0, 128, 0, [[1, D]])
            self.act(junk, o[:, :], AF.Square, accum=ssq)
            rs = self.TMPS[:, 26 + tt_:27 + tt_]
            self.ts('dve', rs, ssq, 1.0 / D, EPS, ALU.mult, ALU.add)
            self.act(rs, rs, AF.Sqrt)
            self.recip(rs, rs)
            self.dump('o_rs%d' % tt_, rs)
            self.stt(o[:, :], o[:, :], rs, self.GATEB[:, mod, :], ALU.mult, ALU.mult)
            self.dump('o_g%d' % tt_, o[:, :])
            self.tt('dve', o[:, :], o[:, :], xw[:, :], ALU.add)
            S.dma('sp', xdst[r0 + tt_ * 128: r0 + (tt_ + 1) * 128, :], o[:, :])

    def build(self, layers=(0, 1)):
        try:
            self._build(layers)
        except StopIteration:
            pass
        self.S.finish('sp')
        return self.nc

    def _build(self, layers):
        NP, TS = self.NP, self.TS
        self.setup()
        if self.stop == 'setup':
            raise StopIteration
        for li, l in enumerate(layers):
            self.load_layer(l)
            if self.stop == 'load':
                raise StopIteration
            xsrc = self.x_in if li == 0 else self.x1
            xdst = self.y_out if li == len(layers) - 1 else self.x1
            T_, F_ = {0: True, 1: True}, {0: False, 1: False}
            for s in range(NP):
                self.visit(l, 0, xsrc, xdst, s * 256, 256, 0, [0, 1], False, s, T_, T_)
            if TS > 0:
                nb = TS // TB
                row0 = NP * 256
                for b in range(nb - 1, -1, -1):
                    self.visit(l, 1, xsrc, xdst, row0, TS, b * TB, [1], True, 0,
                               {0: False, 1: b == nb - 1}, {0: False, 1: b == 0}, mode='store')
                for b in range(nb):
                    self.visit(l, 1, xsrc, xdst, row0, TS, b * TB, [0], True, 0,
                               {0: b == 0, 1: False}, {0: b == nb - 1, 1: False}, mode='load')


def prep_shared(inp):
    f = lambda a: np.ascontiguousarray(np.asarray(a, dtype=np.float32))
    b_mod = f(inp['b_mod'])
    sh = {}
    sh['w_mod'] = f(inp['w_mod'])
    sh['w_in'] = f(inp['w_in'])
    sh['w_out'] = f(inp['w_out'])
    sh['bmodT'] = f(b_mod[:, :2048].reshape(2, 16, 128).transpose(2, 0, 1))
    sh['bmodg'] = f(b_mod[:, 2048:3072])
    sh['gpost'] = f(inp['g_post'])
    pp = np.zeros((128, 2 * PL), np.float32)
    for l in range(2):
        o = l * PL
        def put(key, arr, n):
            pp[:, o + PC[key]: o + PC[key] + n] = np.asarray(arr, np.float32).reshape(n, 128).T
        put('G_PRE', inp['g_pre'][l], 8)
        put('M_NORM', inp['m_norm'][l], 3)
        put('R_MU', inp['r_mu'][l], 11)
        put('R_W0', np.asarray(inp['r_w0'][l]).reshape(-1), 6)
        put('R_A0', np.asarray(inp['r_a0'][l]).reshape(-1), 6)
        put('R_KK', inp['r_kk'][l], 3)
        put('R_KA', inp['r_ka'][l], 3)
        put('R_RK', inp['r_rk'][l], 3)
        put('R_NORM', inp['r_norm'][l], 3)
        put('L_CONV', np.asarray(inp['l_conv'][l]).reshape(-1), 8)
        put('L_CONVB', inp['l_conv_b'][l], 2)
        put('L_BA', np.asarray(inp['l_ba'][l]).reshape(-1), 4)
        put('L_BX', np.asarray(inp['l_bx'][l]).reshape(-1), 4)
        put('L_LAM', np.asarray(inp['l_lambda'][l]).reshape(-1), 4)
        pp[0:6, o + PC['M_BI']: o + PC['M_BI'] + 2] = np.asarray(inp['m_bi'][l], np.float32).T
        pp[0:6, o + PC['M_BF']: o + PC['M_BF'] + 2] = np.asarray(inp['m_bf'][l], np.float32).T
    sh['pp'] = pp
    sh['cst'] = make_consts()
    lora = np.zeros((128, 2, 2, 384), np.float32)
    for wi, key in enumerate(['r_w2', 'r_a2']):
        a = np.asarray(inp[key], np.float32)
        lora[:, :, wi, :] = a.transpose(1, 2, 0, 3).reshape(128, 2, 384)
    sh['lora'] = lora
    lruw = np.zeros((128, 2, 8, 128), np.float32)
    for gi, key in enumerate(['l_wa', 'l_wx']):
        a = np.asarray(inp[key], np.float32)
        for l in range(2):
            for d in range(2):
                for pr in range(2):
                    for hb in range(2):
                        n = 2 * pr + hb
                        lruw[hb * 64:(hb + 1) * 64, l, (gi * 2 + d) * 2 + pr, hb * 64:(hb + 1) * 64] = a[l, d, n]
    sh['lruw'] = lruw
    return sh


def prep_core(inp, b, NP, TS):
    f = lambda a: np.ascontiguousarray(np.asarray(a, dtype=np.float32))
    m = {}
    xp = np.asarray(inp['x_prompt'], np.float32)[b * NP:(b + 1) * NP].reshape(NP * 256, D)
    if TS > 0:
        xs = np.asarray(inp['x_sample'], np.float32)[b]
        m['x_in'] = f(np.concatenate([xp, xs], 0))
    else:
        m['x_in'] = f(xp)
    cc = np.stack([np.asarray(inp['c_ctx'], np.float32), np.asarray(inp['c'], np.float32)[b]], -1)
    m['cc'] = f(cc.reshape(8, 128, 2).transpose(1, 0, 2))
    C = np.asarray(inp['state_mlstm_C'], np.float32)[b]
    n = np.asarray(inp['state_mlstm_n'], np.float32)[b]
    Cn = np.concatenate([C, n[..., None]], -1)
    Cn = Cn.reshape(2, 2, 3, 2, 64, 65).transpose(0, 1, 3, 4, 2, 5).reshape(2, 2, 128, 3, 65)
    m['st_mC'] = f(Cn)
    m['st_mm'] = f(np.asarray(inp['state_mlstm_m'], np.float32)[b].transpose(2, 0, 1))
    R = np.asarray(inp['state_rwkv'], np.float32)[b]
    R = R.transpose(0, 1, 2, 4, 3)
    R = R.reshape(2, 2, 3, 2, 64, 64).transpose(0, 1, 3, 4, 2, 5).reshape(2, 2, 128, 3, 64)
    m['st_rH'] = f(R)
    L = np.asarray(inp['state_rglru'], np.float32)[b]
    m['st_l'] = f(L.reshape(2, 2, 2, 128).transpose(3, 0, 1, 2))
    return m


def unpack_core(r, NP, TS):
    y = r['y_out']
    yp = y[:NP * 256].reshape(NP, 256, D)
    ys = y[NP * 256:]
    mC = r['o_mC'].reshape(NP, 2, 2, 2, 64, 3, 65).transpose(0, 1, 2, 5, 3, 4, 6).reshape(NP, 2, 2, 6, 64, 65)
    newC = np.ascontiguousarray(mC[..., :64])
    newn = np.ascontiguousarray(mC[..., 64])
    newm = r['o_mm'].reshape(NP, 2, 2, 6)
    rH = r['o_rH'].reshape(NP, 2, 2, 2, 64, 3, 64).transpose(0, 1, 2, 5, 3, 4, 6).reshape(NP, 2, 2, 6, 64, 64)
    newr = np.ascontiguousarray(rH.transpose(0, 1, 2, 3, 5, 4))
    newl = np.ascontiguousarray(r['o_l'].transpose(0, 1, 2, 4, 3).reshape(NP, 2, 2, 256))
    return yp, ys, newC, newn, newm, newr, newl


_NC_CACHE = {}


def kernel(**inputs):
    NP, TS = 4, 2048
    key = (NP, TS)
    if key not in _NC_CACHE:
        _NC_CACHE[key] = Builder(NP, TS).build()
    nc = _NC_CACHE[key]
    sh = prep_shared(inputs)
    in_maps = []
    for b in range(NCORES):
        m = dict(sh)
        m.update(prep_core(inputs, b, NP, TS))
        in_maps.append(m)
    res = run_bass_kernel_spmd(nc, in_maps, core_ids=list(range(NCORES)))
    outs = [unpack_core(r, NP, TS) for r in res.results]
    y_prompt = np.concatenate([o[0] for o in outs], 0)
    y_sample = np.stack([o[1] for o in outs], 0)
    cat = lambda i: np.concatenate([o[i] for o in outs], 0)
    return (y_prompt.astype(np.float32), y_sample.astype(np.float32), cat(2).astype(np.float32),
            cat(3).astype(np.float32), cat(4).astype(np.float32), cat(5).astype(np.float32), cat(6).astype(np.float32))
```

```python
import numpy as np
import concourse.bass as bass
import concourse.mybir as mybir
from concourse.bass_utils import run_bass_kernel_spmd

F32 = mybir.dt.float32
BF16 = mybir.dt.bfloat16
AF = mybir.ActivationFunctionType
ALU = mybir.AluOpType
AX = mybir.AxisListType

D = 1024
IN_COLS = 4248
EPS = 1e-6
DSC = 0.6065306597126334
TB = 256
NCORES = 8


def _prod(xs):
    r = 1
    for x in xs:
        r *= int(x)
    return r


class Sync:
    def __init__(self, nc, n_dma_sems=32):
        self.nc = nc
        self.engs = {'pe': nc.tensor, 'dve': nc.vector, 'act': nc.scalar,
                     'pool': nc.gpsimd, 'sp': nc.sync}
        self.sem = {}
        self.cnt = {}
        for e in ['pe', 'dve', 'act', 'pool']:
            self.sem[e] = nc.alloc_semaphore('sem_' + e)
            self.cnt[e] = 0
        self.seen = {e: {} for e in self.engs}
        self.dma_ring = [nc.alloc_semaphore('dq_%d' % i) for i in range(n_dma_sems)]
        self.dma_uses = [0] * n_dma_sems
        self.dma_next = 0
        self.rec = {}
        self.untracked = set()
        self.n_wait = 0
        self.n_ins = 0
        self.pstep_cache = {}
        self.sb_addr = {}

    def region(self, ap):
        t = ap.tensor
        name = t.name
        apl = [(int(s), int(c)) for (s, c) in ap.ap]
        off = int(ap.offset)
        if type(t).__name__.startswith('DRam'):
            lo = off + sum(min(0, s * (c - 1)) for s, c in apl)
            hi = off + sum(max(0, s * (c - 1)) for s, c in apl) + 1
            return (name, 0, 1, lo, hi)
        pstep = self.pstep_cache.get(name)
        if pstep is None:
            pstep = _prod(list(t.shape)[1:])
            self.pstep_cache[name] = pstep
        p0 = off // pstep
        f0 = off % pstep
        npart = apl[0][1]
        rest = apl[1:]
        lo = f0 + sum(min(0, s * (c - 1)) for s, c in rest)
        hi = f0 + sum(max(0, s * (c - 1)) for s, c in rest) + 1
        if name in self.sb_addr:
            base, es = self.sb_addr[name]
            return ('SB', p0, p0 + npart, base + lo * es, base + hi * es)
        return ('PS:' + name, (p0 // 32) * 32, ((p0 + npart + 31) // 32) * 32, (lo // 512) * 512, ((hi + 511) // 512) * 512)

    @staticmethod
    def _ovl(a, b):
        return a[1] < b[2] and b[1] < a[2] and a[3] < b[4] and b[3] < a[4]

    @staticmethod
    def _contains(a, b):
        return a[1] <= b[1] and b[2] <= a[2] and a[3] <= b[3] and b[4] <= a[4]

    def _collect(self, e, reads, writes):
        deps = {}
        own = self.sem.get(e)
        rregs = [self.region(a) for a in reads]
        wregs = [self.region(a) for a in writes]
        for r in rregs:
            if r[0] in self.untracked:
                continue
            isps = r[0].startswith('PS:')
            for (reg, kind, sem, val) in self.rec.get(r[0], ()):
                if (kind == 'w' or (isps and sem is not own)) and self._ovl(reg, r):
                    if e == 'pe' and sem is own:
                        continue
                    k = id(sem)
                    if deps.get(k, (None, 0))[1] < val:
                        deps[k] = (sem, val)
        for w in wregs:
            if w[0] in self.untracked:
                continue
            for (reg, kind, sem, val) in self.rec.get(w[0], ()):
                if self._ovl(reg, w):
                    if sem is own:
                        continue
                    k = id(sem)
                    if deps.get(k, (None, 0))[1] < val:
                        deps[k] = (sem, val)
        return deps, rregs, wregs

    def _record(self, rregs, wregs, sem, val):
        for r in rregs:
            if r[0] in self.untracked:
                continue
            lst = self.rec.setdefault(r[0], [])
            lst[:] = [x for x in lst if not (x[1] == 'r' and x[2] is sem and self._contains(r, x[0]))]
            lst.append((r, 'r', sem, val))
        for w in wregs:
            if w[0] in self.untracked:
                continue
            lst = self.rec.setdefault(w[0], [])
            lst[:] = [x for x in lst if not self._contains(w, x[0])]
            lst.append((w, 'w', sem, val))

    def wait(self, e, sem, val):
        k = id(sem)
        if self.seen[e].get(k, 0) >= val:
            return
        self.engs[e].wait_ge(sem, val)
        self.seen[e][k] = val
        self.n_wait += 1

    max_ins = None
    paranoid = False
    embed_waits = True

    def emit(self, e, reads, writes, build, inc=True):
        if self.max_ins is not None and self.n_ins >= self.max_ins:
            raise StopIteration
        deps, rregs, wregs = self._collect(e, reads, writes)
        embed = None
        for (sem, val) in deps.values():
            if self.embed_waits and embed is None and self.seen[e].get(id(sem), 0) < val:
                embed = (sem, val)
                continue
            self.wait(e, sem, val)
        if self.paranoid:
            for e2 in ['pe', 'dve', 'act', 'pool']:
                if self.cnt[e2] > 0 and not (e == 'pe' and e2 == 'pe'):
                    self.wait(e, self.sem[e2], self.cnt[e2])
        ins = build(self.engs[e])
        if embed is not None:
            ins._wait_ge(embed[0], embed[1])
            self.seen[e][id(embed[0])] = embed[1]
        self.n_ins += 1
        if inc:
            self.cnt[e] += 1
            ins.then_inc(self.sem[e], 1)
            val = self.cnt[e]
        else:
            val = self.cnt[e] + 1
        self._record(rregs, wregs, self.sem[e], val)
        return ins

    def dma(self, q, out, in_, **kw):
        if self.max_ins is not None and self.n_ins >= self.max_ins:
            raise StopIteration
        i = self.dma_next
        self.dma_next = (i + 1) % len(self.dma_ring)
        sem = self.dma_ring[i]
        uses = self.dma_uses[i]
        if uses > 0:
            self.wait(q, sem, 16 * uses)
        deps, rregs, wregs = self._collect(q, [in_], [out])
        for (s, v) in deps.values():
            self.wait(q, s, v)
        ins = self.engs[q].dma_start(out=out, in_=in_, **kw)
        ins.then_inc(sem, 16)
        self.n_ins += 1
        self.dma_uses[i] = uses + 1
        self._record(rregs, wregs, sem, 16 * (uses + 1))
        return ins

    def finish(self, q='sp'):
        for i, sem in enumerate(self.dma_ring):
            if self.dma_uses[i] > 0:
                self.wait(q, sem, 16 * self.dma_uses[i])
        for e in ['pe', 'dve', 'act', 'pool']:
            if self.cnt[e] > 0:
                self.wait(q, self.sem[e], self.cnt[e])


class Arena:
    def __init__(self, base, size):
        self.base, self.size, self.ptr = base, size, 0

    def take(self, nbytes):
        off = (self.ptr + 31) // 32 * 32
        self.ptr = off + nbytes
        assert self.ptr <= self.size, ("arena overflow", self.ptr, self.size)
        return self.base + off


PL = 72
PC = dict(G_PRE=0, M_NORM=8, R_MU=11, R_W0=22, R_A0=28, R_KK=34, R_KA=37, R_RK=40, R_NORM=43,
          L_CONV=46, L_CONVB=54, L_BA=56, L_BX=60, L_LAM=64, M_BI=68, M_BF=70)
DL = 32
DC = dict(OMKA=0, CLAM=3, C2LAM=7, NBF=11, OMMU=16)
CC = dict(IDENT=0, BONES=128, MKF=256, MKB=384, MNF=512, MNB=576, MUI=640, MLI=704, RMASK=768,
          LSEL=1536, PSEL=1664, NMKF=1668, NMKB=1796, ONES=1924, ISTK=2052)
NCST = 2052 + 64


def make_consts():
    c = np.zeros((128, NCST), np.float32)
    c[:, 0:128] = np.eye(128)
    c[0:64, 128:192] = 1.0
    c[64:128, 192:256] = 1.0
    s = np.arange(64)[:, None]
    t = np.arange(64)[None, :]
    us, ui = (s < t).astype(np.float32), (s <= t).astype(np.float32)
    ls, li = (s > t).astype(np.float32), (s >= t).astype(np.float32)
    c[0:64, 256:320], c[0:64, 320:384] = us, ui
    c[0:64, 384:448], c[0:64, 448:512] = ls, li
    c[0:64, 512:576] = -ls
    c[0:64, 576:640] = -us
    c[0:64, 640:704] = ui
    c[0:64, 704:768] = li
    rm = np.ones(768, np.float32)
    rm[::64] = 0.0
    c[:, 768:1536] = rm[None, :]
    for k in range(6):
        c[k, 1536 + (k % 2) * 64: 1536 + (k % 2) * 64 + 64] = 1.0
        c[k, 1664 + k // 2] = 1.0
    c[0:64, 1668:1796] = -c[0:64, 256:384]
    c[0:64, 1796:1924] = -c[0:64, 384:512]
    c[:, 1924:2052] = 1.0
    for (a, b) in ((256, 768), (1668, 1924)):
        c[64:128, a:b] = c[0:64, a:b]
    c[0:64, 2052:2116] = np.eye(64)
    c[64:128, 2052:2116] = np.eye(64)
    return c


class Builder:
    def __init__(self, NP, TS, debug=False, stop=None):
        self.stop = stop
        self.NP, self.TS = NP, TS
        self.NTOK = NP * 256 + TS
        self.debug = debug
        nc = self.nc = bass.Bass("TRN2", target_bir_lowering=False)
        self.S = Sync(nc)
        self._decl_dram()
        self._alloc()

    def _decl_dram(self):
        nc, NP, TS = self.nc, self.NP, self.TS
        di = lambda n, s: nc.dram_tensor(n, list(s), F32, kind="ExternalInput").ap()
        do = lambda n, s: nc.dram_tensor(n, list(s), F32, kind="ExternalOutput").ap()
        dx = lambda n, s: nc.dram_tensor(n, list(s), F32, kind="Internal").ap()
        self.x_in = di("x_in", [self.NTOK, D])
        self.cc = di("cc", [128, 8, 2])
        self.w_mod = di("w_mod", [2, D, 3 * D])
        self.bmodT = di("bmodT", [128, 2, 16])
        self.bmodg = di("bmodg", [2, D])
        self.gpost = di("gpost", [2, D])
        self.w_in = di("w_in", [2, D, IN_COLS])
        self.w_out = di("w_out", [2, D, D])
        self.pp = di("pp", [128, 2 * PL])
        self.cst = di("cst", [128, NCST])
        self.lora = di("lora", [128, 2, 2, 384])
        self.lruw = di("lruw", [128, 2, 8, 128])
        self.st_mC = di("st_mC", [2, 2, 128, 3, 65])
        self.st_mm = di("st_mm", [6, 2, 2])
        self.st_rH = di("st_rH", [2, 2, 128, 3, 64])
        self.st_l = di("st_l", [128, 2, 2, 2])
        for n in ["x_in", "cc", "w_mod", "bmodT", "bmodg", "gpost", "w_in", "w_out", "pp", "cst", "lora",
                  "lruw", "st_mC", "st_mm", "st_rH", "st_l"]:
            self.S.untracked.add(n)
        self.y_out = do("y_out", [self.NTOK, D])
        self.o_mC = do("o_mC", [NP, 2, 2, 128, 3, 65])
        self.o_mm = do("o_mm", [NP, 2, 2, 6, 1])
        self.o_rH = do("o_rH", [NP, 2, 2, 128, 3, 64])
        self.o_l = do("o_l", [NP, 2, 2, 128, 2])
        self.x1 = dx("x1", [self.NTOK, D])
        self.sHB = dx("sHB", [max(TS, 64), 384])
        self.sYB = dx("sYB", [max(TS // 256, 1), 128, 768])
        self.sLB = dx("sLB", [128, 2, max(TS, 64)])
        nb = max(TS // 256, 1)
        dxt = lambda n, s_, dt: nc.dram_tensor(n, list(s_), dt, kind="Internal").ap()
        self.sc = {
            'QT': dxt("sc_QT", [nb, 128, 3, 256], BF16), 'KT': dxt("sc_KT", [nb, 128, 3, 256], BF16),
            'KTOK': dxt("sc_KTOK", [nb, 64, 4, 384], BF16), 'VAUG': dxt("sc_VAUG", [nb, 64, 4, 6, 65], BF16),
            'GOZ': dxt("sc_GOZ", [nb, 128, 3, 256], F32), 'GI': dxt("sc_GI", [nb, 6, 256], F32),
            'LF': dxt("sc_LF", [nb, 6, 256], F32), 'XC': dxt("sc_XC", [nb, 128, 2, 256], F32),
            'LZT': dxt("sc_LZT", [nb, 128, 2, 256], F32), 'BLK': dxt("sc_BLK", [nb, 128, 11, 256], F32),
            'RZT': dxt("sc_RZT", [nb, 128, 3, 256], F32), 'KH': dxt("sc_KH", [nb, 128, 3, 256], F32),
            'VSTK': dxt("sc_VSTK", [nb, 128, 4, 3, 64], BF16),
        }
        if self.debug:
            self.dbg = do("dbg", [128, 32768])
            self.dbg_map = {}
            self.dbg_off = 0

    def T(self, name, shape, dtype, arena):
        es = 2 if dtype == BF16 else 4
        nb = _prod(shape[1:]) * es
        off = arena.take(nb)
        t = self.nc.alloc_sbuf_tensor_at(name, list(shape), dtype, offset=off)
        self.S.sb_addr[t.name] = (off, es)
        return t

    def _alloc(self):
        nc = self.nc
        B0 = 16384 + 256
        LIM = 224 * 1024 - 256
        P = Arena(B0, LIM - B0)
        T = self.T
        self.W_IN = T("W_IN", [128, 8, IN_COLS], BF16, P)
        self.W_OUT = T("W_OUT", [128, 8, D], BF16, P)
        self.CST = T("CST", [128, NCST], F32, P)
        self.PP = T("PP", [128, 2 * PL], F32, P)
        self.DV = T("DV", [128, 2 * DL], F32, P)
        self.LORA = T("LORA", [128, 2, 2, 384], F32, P)
        self.LRUW = T("LRUW", [128, 2, 8, 128], F32, P)
        self.RKD = T("RKD", [128, 2, 3, 128], F32, P)
        self.GS = T("GS", [128, 2, 8], F32, P)
        self.SH = T("SH", [128, 2, 8], F32, P)
        self.GATEB = T("GATEB", [128, 2, D], F32, P)
        self.IDB = T("IDB", [128, 128], BF16, P)
        self.ISTKB = T("ISTKB", [128, 64], BF16, P)
        self.mC = [T("mC%d" % d, [128, 3, 65], F32, P) for d in range(2)]
        self.mM = [T("mM%d" % d, [6, 1], F32, P) for d in range(2)]
        self.rH = [T("rH%d" % d, [128, 3, 64], F32, P) for d in range(2)]
        self.lS = [T("lS%d" % d, [128, 2], F32, P) for d in range(2)]
        XB = Arena(P.take(16384), 16384)
        self.HT = T("HT", [128, 8, 384], BF16, P)
        self.MIXT = T("MIXT", [128, 8, 256], BF16, P)
        self.BLK = T("BLK", [128, 11, 256], F32, P)
        abase = P.take(0)
        asize = P.size - P.ptr
        self.asize = asize
        mk = lambda: Arena(abase, asize)
        a = Arena(XB.base, XB.size)
        self.XW = [T("XW%d" % i, [128, D], F32, a) for i in range(2)]
        self.XN = [T("XN%d" % i, [128, D], F32, a) for i in range(2)]
        a = Arena(XB.base, XB.size)
        self.YB = T("YB", [128, 4, 3, 64], F32, a)
        self.RZT = T("RZT", [128, 3, 256], F32, a)
        self.WSTG = [T("WSTG0", [128, 8, 256], F32, Arena(self.S.sb_addr[self.BLK.name][0], 11264)),
                     T("WSTG1", [128, 8, 256], F32, Arena(XB.base, 8192)),
                     T("WSTG2", [128, 8, 256], F32, Arena(XB.base + 8192, 8192))]
        a = mk()
        self.WMs = [T("WM%d" % i, [128, 8, 512], F32, a) for i in range(2)]
        self.SCT = T("SCT", [128, 8, 2], F32, a)
        self.SCB = T("SCB", [128, 2, 8, 128], F32, a)
        self.MODT = T("MODT", [128, 16, 2], F32, a)
        self.BMT = T("BMT", [128, 2, 16], F32, a)
        awm = Arena(self.S.sb_addr[self.WMs[0].name][0], 16384)
        self.BG = T("BG", [128, D], F32, awm)
        self.GP = T("GP", [128, D], F32, awm)
        self.TMPS = T("TMPS", [128, 32], F32, a)
        self.MODROW = T("MODROW", [2, 512], F32, a)
        a = mk()
        self.QT = T("QT", [128, 3, 256], BF16, a)
        self.KT = T("KT", [128, 3, 256], BF16, a)
        self.GOZ = T("GOZ", [128, 3, 256], F32, a)
        self.KTOK = T("KTOK", [64, 4, 384], BF16, a)
        self.VAUG = T("VAUG", [64, 4, 6, 65], BF16, a)
        self.HB = T("HB", [64, 4, 384], F32, a)
        self.mg = []
        for d in range(2):
            g = {}
            for n in ["gi", "lf", "pre", "bb", "gg", "ee", "fl"]:
                g[n] = T("mg_%s%d" % (n, d), [6, 256], F32, a)
            for n in ["mx", "mch", "mprev", "MM", "dec"]:
                g[n] = T("mg_%s%d" % (n, d), [6, 4], F32, a)
            g["X2"] = T("mg_X2%d" % d, [6, 4, 3], F32, a)
            g["etok"] = T("mg_etok%d" % d, [64, 4, 6], F32, a)
            g["fltok"] = T("mg_fltok%d" % d, [64, 4, 6], F32, a)
            g["decb"] = T("mg_decb%d" % d, [128, 4, 3], F32, a)
            self.mg.append(g)
        self.STSB = [T("STSB%d" % i, [64, 6, 64], BF16, a) for i in range(2)]
        self.VP = [T("VP%d" % i, [64, 6, 65], BF16, a) for i in range(2)]
        self.CDEC = T("CDEC", [128, 3, 65], F32, a)
        self.CDBF = T("CDBF", [128, 3, 65], BF16, a)
        self.DN = T("DN", [64, 6], F32, a)
        self.RDN = T("RDN", [64, 6], F32, a)
        self.HD = T("HD", [64, 6, 64], F32, a)
        self.SQ = T("SQ", [64, 4, 384], F32, a)
        self.SSQ = T("SSQ", [64, 24], F32, a)
        self.RSTD = T("RSTD", [64, 24], F32, a)
        self.ZT = [T("ZT%d" % i, [128, 256], F32, a) for i in range(2)]
        a = mk()
        self.XL = T("XL", [128, 2, 392], F32, a)
        self.XC = T("XC", [128, 2, 256], F32, a)
        self.LZT = T("LZT", [128, 2, 256], F32, a)
        self.ltd = [{n: T("lt%d_%s" % (d_, n), [128, 2, 256], F32, a) for n in ["rg", "ig", "aa", "a2", "bt"]} for d_ in range(2)]
        self.HL = [T("HL%d" % d, [128, 2, 256], F32, a) for d in range(2)]
        a = mk()
        self.URS = T("URS", [128, 11, 384], F32, a)
        a = mk()
        self.KH = T("KH", [128, 3, 256], F32, a)
        R1 = a.take(7 * 3072)
        a1 = Arena(R1, 7 * 3072)
        self.rt = {n: T("rt_" + n, [128, 3, 256], F32, a1) for n in ["sg", "aa", "kt", "bb", "cs", "d", "E"]}
        a2 = Arena(R1, 7 * 3072)
        self.A1BD = T("A1BD", [128, 3, 2, 128], BF16, a2)
        self.A2BD = T("A2BD", [128, 3, 2, 128], BF16, a2)
        self.NN = [T("NN%d" % i, [128, 3, 128], BF16, a2) for i in range(2)]
        self.NTT = [T("NTT%d" % i, [128, 3, 128], BF16, a2) for i in range(2)]
        self.U = T("U", [128, 3, 64], F32, a2)
        self.UBF = T("UBF", [128, 3, 64], BF16, a2)
        self.HBF = T("HBF", [128, 3, 64], BF16, a2)
        self.HTMP = T("HTMP", [128, 3, 64], F32, a2)
        self.YBD = T("YBD", [128, 3, 128], F32, a2)
        self.KTTOK = T("KTTOK", [128, 4, 3, 128], BF16, a2)
        self.BTTOK = T("BTTOK", [128, 4, 3, 128], BF16, a2)
        self.rSSQ = T("rSSQ", [128, 12], F32, a2)
        self.rRSTD = T("rRSTD", [128, 12], F32, a2)
        self.KR = T("KR", [128, 3, 4, 2, 64], BF16, a)
        self.BTT = T("BTT", [128, 3, 256], BF16, a)
        self.KTT = T("KTT", [128, 3, 256], BF16, a)
        bdr = a.take(4 * 3072)
        a3 = Arena(bdr, 4 * 3072)
        self.BD = {n: T("BD_" + n, [128, 3, 4, 128], BF16, a3) for n in ["kt", "b", "kh", "r"]}
        a3 = Arena(bdr, 4 * 3072)
        self.ft = {n: T("ft_" + n, [128, 3, 256], F32, a3) for n in ["rk", "bon", "t1"]}
        self.YSQ = T("YSQ", [128, 4, 3, 64], F32, a3)
        self.BDV = T("BDV", [128, 3, 4, 128], BF16, a)
        self.VSTK = T("VSTK", [128, 4, 3, 64], BF16, a)
        self.TWL = T("TWL", [128, 256], F32, a)
        self.GL = T("GL", [128, 3, 4], F32, a)
        self.PA = nc.alloc_psum_tensor("PA", [128, 1024], F32)
        self.PB = nc.alloc_psum_tensor("PB", [128, 1024], F32)
        self.PQ = [nc.alloc_psum_tensor("PQ%d" % i, [128, 512], F32) for i in range(3)]
        self.PTB = nc.alloc_psum_tensor("PTB", [128, 1024], BF16)
        self.pq_i = 0

    def pq(self):
        t = self.PQ[self.pq_i % 3]
        self.pq_i += 1
        return t

    def V(self, t, p0, npart, off, dims):
        pstep = _prod(list(t.shape)[1:])
        return bass.AP(t, p0 * pstep + off, [[pstep, npart]] + [list(d) for d in dims])

    def tt(self, e, out, in0, in1, op):
        return self.S.emit(e, [in0, in1], [out], lambda g: g.tensor_tensor(out=out, in0=in0, in1=in1, op=op))

    def ts(self, e, out, in0, s1, s2, op0, op1=None):
        rd = [in0] + [s for s in (s1, s2) if not isinstance(s, (int, float)) and s is not None]
        if op1 is None:
            return self.S.emit(e, rd, [out], lambda g: g.tensor_scalar(out=out, in0=in0, scalar1=s1, scalar2=None, op0=op0))
        return self.S.emit(e, rd, [out], lambda g: g.tensor_scalar(out=out, in0=in0, scalar1=s1, scalar2=s2, op0=op0, op1=op1))

    def stt(self, out, in0, sc, in1, op0, op1):
        rd = [in0, in1] + ([] if isinstance(sc, (int, float)) else [sc])
        return self.S.emit('dve', rd, [out], lambda g: g.scalar_tensor_tensor(out=out, in0=in0, scalar=sc, in1=in1, op0=op0, op1=op1))

    def act(self, out, in_, func, bias=None, scale=None, accum=None):
        rd = [in_]
        kw = {}
        if bias is not None:
            kw['bias'] = bias
            if not isinstance(bias, (int, float)):
                rd.append(bias)
        if scale is not None:
            kw['scale'] = scale
            if not isinstance(scale, (int, float)):
                rd.append(scale)
        wr = [out]
        if accum is not None:
            kw['accum_out'] = accum
            wr.append(accum)
        return self.S.emit('act', rd, wr, lambda g: g.activation(out=out, in_=in_, func=func, **kw))

    def cp(self, e, out, in_):
        if e == 'act':
            return self.act(out, in_, AF.Copy)
        return self.S.emit(e, [in_], [out], lambda g: g.tensor_copy(out=out, in_=in_))

    def _pe_rowtile_guard(self, lhsT, out):
        S = self.S
        st = S.region(lhsT)
        k = st[2] - st[1]
        kr = 32 if k <= 32 else (64 if k <= 64 else 128)
        rows = (st[1], st[1] + kr)
        oreg = S.region(out)
        last = getattr(self, '_last_pe', None)
        if last is not None:
            lrows, loreg, lins, linc = last
            disjoint = rows[1] <= lrows[0] or lrows[1] <= rows[0]
            samebank = (loreg[0] == oreg[0]) and loreg[3] < oreg[4] and oreg[3] < loreg[4]
            if disjoint and samebank:
                if not linc:
                    S.cnt['pe'] += 1
                    lins.then_inc(S.sem['pe'], 1)
                S.wait('pe', S.sem['pe'], S.cnt['pe'])
        return rows, oreg

    def mm(self, out, lhsT, rhs, start=True, stop=True, inc=None):
        if inc is None:
            inc = stop
        rows, oreg = self._pe_rowtile_guard(lhsT, out)
        ins = self.S.emit('pe', [lhsT, rhs], [out],
                          lambda g: g.matmul(out, lhsT=lhsT, rhs=rhs, start=start, stop=stop), inc=inc)
        self._last_pe = (rows, oreg, ins, inc)
        return ins

    def tr(self, out, in_, inc=True, bf=False):
        n = in_.shape[0]
        ident = self.IDB[0:n, 0:n] if bf else self.CST[0:n, CC['IDENT']:CC['IDENT'] + n]
        rows, oreg = self._pe_rowtile_guard(in_, out)
        ins = self.S.emit('pe', [in_, ident], [out],
                          lambda g: g.transpose(out=out, in_=in_, identity=ident), inc=inc)
        self._last_pe = (rows, oreg, ins, inc)
        return ins

    def memset(self, e, ap, v):
        return self.S.emit(e, [], [ap], lambda g: g.memset(ap, v))

    def scan(self, out, d0, d1, init, op0, op1):
        rd = [d0, d1] + ([] if isinstance(init, (int, float)) else [init])
        return self.S.emit('dve', rd, [out], lambda g: g.tensor_tensor_scan(out=out, data0=d0, data1=d1, initial=init, op0=op0, op1=op1))

    def recip(self, out, in_):
        return self.S.emit('dve', [in_], [out], lambda g: g.reciprocal(out=out, in_=in_))

    def reduce(self, out, in_, op, axis=AX.X):
        return self.S.emit('dve', [in_], [out], lambda g: g.tensor_reduce(out=out, in_=in_, axis=axis, op=op))

    def dump(self, name, ap):
        if not self.debug or name in self.dbg_map:
            return
        if getattr(self, 'dbg_filter', None) is not None and not any(name.startswith(p) for p in self.dbg_filter):
            return
        shp = list(ap.shape)
        npart, nfree = shp[0], _prod(shp[1:])
        stage = self.XN[1]
        assert nfree <= 1024
        dst = self.V(stage, 0, npart, 0, [[_prod(shp[i + 1:]), shp[i]] for i in range(1, len(shp))])
        self.cp('dve', dst, ap)
        self.S.dma('sp', self.dbg[0:npart, self.dbg_off:self.dbg_off + nfree], stage[0:npart, 0:nfree], allow_slow_non_contiguous=True)
        self.dbg_map[name] = (self.dbg_off, npart, shp[1:])
        self.dbg_off += nfree

    def ppc(self, l, key, j=0, rows=128):
        c = l * PL + PC[key] + j
        return self.PP[0:rows, c:c + 1]

    def dvc(self, l, key, j=0, rows=128):
        c = l * DL + DC[key] + j
        return self.DV[0:rows, c:c + 1]

    def setup(self):
        S = self.S
        S.dma('sp', self.CST[:, :], self.cst[:, :])
        S.dma('sp', self.PP[:, :], self.pp[:, :])
        S.dma('sp', self.LORA[:, :, :, :], self.lora[:, :, :, :])
        S.dma('sp', self.LRUW[:, :, :, :], self.lruw[:, :, :, :])
        self.memset('dve', self.DV[:, :], 0.0)
        self.cp('dve', self.IDB[:, :], self.CST[:, CC['IDENT']:CC['IDENT'] + 128])
        self.cp('dve', self.ISTKB[:, :], self.CST[:, CC['ISTK']:CC['ISTK'] + 64])
        for l in range(2):
            self.ts('dve', self.DV[:, l * DL + DC['OMKA']: l * DL + DC['OMKA'] + 3],
                    self.PP[:, l * PL + PC['R_KA']: l * PL + PC['R_KA'] + 3], -1.0, 1.0, ALU.mult, ALU.add)
            self.ts('dve', self.DV[:, l * DL + DC['OMMU']: l * DL + DC['OMMU'] + 11],
                    self.PP[:, l * PL + PC['R_MU']: l * PL + PC['R_MU'] + 11], -1.0, 1.0, ALU.mult, ALU.add)
            lam = self.PP[:, l * PL + PC['L_LAM']: l * PL + PC['L_LAM'] + 4]
            t0 = self.TMPS[:, 0:4]
            self.act(t0, lam, AF.Exp, scale=-1.0)
            self.act(t0, t0, AF.Ln, bias=1.0)
            self.ts('dve', self.DV[:, l * DL + DC['CLAM']: l * DL + DC['CLAM'] + 4], t0, -8.0, None, ALU.mult)
            self.ts('dve', self.DV[:, l * DL + DC['C2LAM']: l * DL + DC['C2LAM'] + 4], t0, -16.0, None, ALU.mult)
            self.ts('dve', self.DV[0:6, l * DL + DC['NBF']: l * DL + DC['NBF'] + 2],
                    self.PP[0:6, l * PL + PC['M_BF']: l * PL + PC['M_BF'] + 2], -1.0, None, ALU.mult)
            for hp in range(3):
                self.ts('dve', self.RKD[:, l, hp, :], self.CST[:, CC['BONES']:CC['BONES'] + 128],
                        self.ppc(l, 'R_RK', hp), None, ALU.mult)
        self.memset('dve', self.XL[:, :, 0:2], 0.0)

    def load_layer(self, l):
        S = self.S
        wsrc = self.w_in[l].rearrange("(kc p) c -> p kc c", p=128)
        wo = self.w_out[l].rearrange("(kc p) c -> p kc c", p=128)
        pieces = [(self.W_IN, wsrc, c0, min(256, IN_COLS - c0)) for c0 in range(0, IN_COLS, 256)]
        pieces += [(self.W_OUT, wo, c0, 256) for c0 in range(0, D, 256)]
        def emit_pieces(lo, hi):
            for i in range(lo, min(hi, len(pieces))):
                dst, src, c0, n = pieces[i]
                stg = self.WSTG[i % 3]
                S.dma('sp', stg[:, :, 0:n], src[:, :, c0:c0 + n])
                self.cp('pool', dst[:, :, c0:c0 + n], stg[:, :, 0:n])
        if self.stop == 'load_w':
            raise StopIteration
        S.dma('sp', self.SCT[:, :, :], self.cc[:, :, :])
        S.dma('sp', self.BMT[:, :, :], self.bmodT[:, :, :])
        self.act(self.SCT[:, :, :], self.SCT[:, :, :], AF.Silu)
        for m in range(2):
            for kc in range(8):
                src = self.V(self.SCT, 0, 128, kc * 2 + m, [[0, 128]])
                self.cp('dve', self.SCB[:, m, kc, :], src)
        wm = self.w_mod[l].rearrange("(kc p) c -> p kc c", p=128)
        for blk in range(6):
            self.WM = self.WMs[blk % 2]
            S.dma('sp', self.WM[:, :, :], wm[:, :, blk * 512:(blk + 1) * 512])
            emit_pieces(blk * 4, blk * 4 + 4)
            if blk < 4:
                ps = self.pq()
                for kc in range(8):
                    self.mm(ps[0:2, :], self.SCT[:, kc, :], self.WM[:, kc, :], start=(kc == 0), stop=(kc == 7))
                self.cp('act', self.MODROW[0:2, :], ps[0:2, :])
                ps2 = self.pq()
                for j in range(4):
                    self.tr(ps2[:, j * 2:j * 2 + 2], self.MODROW[0:2, j * 128:(j + 1) * 128])
                o = self.MODT[:, blk * 4:(blk + 1) * 4, :]
                bsrc = self.V(self.BMT, 0, 128, l * 16 + blk * 4, [[1, 4], [0, 2]])
                self.tt('dve', o, self.V(ps2, 0, 128, 0, [[2, 4], [1, 2]]), bsrc, ALU.add)
            else:
                half = blk - 4
                for m in range(2):
                    ps = self.pq()
                    for kc in range(8):
                        self.mm(ps[:, :], self.SCB[:, m, kc, :], self.WM[:, kc, :], start=(kc == 0), stop=(kc == 7))
                    self.cp('act', self.GATEB[:, m, half * 512:(half + 1) * 512], ps[:, :])
        if self.stop == 'load_m':
            raise StopIteration
        for m in range(2):
            self.cp('dve', self.SH[:, m, :], self.V(self.MODT, 0, 128, m, [[2, 8]]))
            t0 = self.TMPS[:, 8:16]
            self.ts('dve', t0, self.V(self.MODT, 0, 128, 16 + m, [[2, 8]]), 1.0, None, ALU.add)
            self.tt('dve', self.GS[:, m, :], t0, self.PP[:, l * PL + PC['G_PRE']: l * PL + PC['G_PRE'] + 8], ALU.mult)
        if self.stop == 'load_g':
            raise StopIteration
        S.dma('sp', self.BG[:, :], bass.AP(self.bmodg.tensor, l * D, [[0, 128], [1, D]]))
        S.dma('sp', self.GP[:, :], bass.AP(self.gpost.tensor, l * D, [[0, 128], [1, D]]))
        if self.stop == 'load_b':
            raise StopIteration
        for m in range(2):
            self.tt('dve', self.GATEB[:, m, :], self.GATEB[:, m, :], self.BG[:, :], ALU.add)
            self.tt('dve', self.GATEB[:, m, :], self.GATEB[:, m, :], self.GP[:, :], ALU.mult)

    def proj_fm(self, c0, ncols, t_off, ntok, evac):
        ps = self.pq()
        for kc in range(8):
            self.mm(ps[0:ncols, 0:ntok], self.W_IN[:, kc, c0:c0 + ncols], self.HT[:, kc, t_off:t_off + ntok],
                    start=(kc == 0), stop=(kc == 7))
        evac(ps[0:ncols, 0:ntok])

    def proj_tm(self, c0, ncols, t_off, evac):
        ps = self.pq()
        for kc in range(8):
            self.mm(ps[0:64, 0:ncols], self.HT[:, kc, t_off:t_off + 64], self.W_IN[:, kc, c0:c0 + ncols],
                    start=(kc == 0), stop=(kc == 7))
        evac(ps[0:64, 0:ncols])

    def visit(self, l, mod, xsrc, xdst, row0, T, t0, dirs, grid, seq_idx, first, last, mode='full'):
        S = self.S
        w0 = max(0, t0 - 64)
        w1 = min(T, t0 + TB + 64)
        W = w1 - w0
        co = t0 - w0
        do_f = 0 in dirs
        prompt = (mod == 0)
        if mode != 'load':
            ntile = (W + 127) // 128
            for i in range(ntile):
                n = min(128, W - i * 128)
                xw, xn = self.XW[i % 2], self.XN[i % 2]
                r = row0 + w0 + i * 128
                S.dma('sp', xw[0:n, :], xsrc[r:r + n, :])
                ssq = self.TMPS[0:n, 16 + i:17 + i]
                self.act(xn[0:n, :], xw[0:n, :], AF.Square, accum=ssq)
                if self.stop == 'n1':
                    raise StopIteration
                rs = self.TMPS[0:n, 20 + i:21 + i]
                self.ts('dve', rs, ssq, 1.0 / D, EPS, ALU.mult, ALU.add)
                self.act(rs, rs, AF.Sqrt)
                self.recip(rs, rs)
                if self.stop == 'n2':
                    raise StopIteration
                self.act(xn[0:n, :], xw[0:n, :], AF.Copy, scale=rs)
                if self.stop == 'n3':
                    raise StopIteration
                for half in range(2):
                    ps = self.pq()
                    for j in range(4):
                        kc = half * 4 + j
                        self.tr(ps[:, j * 128:j * 128 + n], xn[0:n, kc * 128:(kc + 1) * 128])
                    if self.stop == 'n4':
                        raise StopIteration
                    for j in range(4):
                        kc = half * 4 + j
                        o = self.HT[:, kc, i * 128:i * 128 + n]
                        if half == 0:
                            self.ts('dve', o, ps[:, j * 128:j * 128 + n], self.GS[:, mod, kc:kc + 1], self.SH[:, mod, kc:kc + 1], ALU.mult, ALU.add)
                        else:
                            self.act(o, ps[:, j * 128:j * 128 + n], AF.Identity, bias=self.SH[:, mod, kc:kc + 1], scale=self.GS[:, mod, kc:kc + 1])
        if self.stop in ('norm', 'n5a', 'n5d'):
            raise StopIteration
        self.stage_mlstm(l, t0, co, dirs, prompt, seq_idx, first, last, mode)
        if self.stop == 'mlstm':
            raise StopIteration
        self.stage_lru(l, t0, co, W, w0, w1, T, dirs, prompt, seq_idx, first, last, mode)
        if self.stop == 'lru':
            raise StopIteration
        self.stage_rwkv(l, t0, co, W, w0, w1, T, dirs, grid, prompt, seq_idx, first, last, mode)
        for kc in range(8):
            self.dump('mix%d' % kc, self.MIXT[:, kc, :])
        if self.stop == 'rwkv':
            raise StopIteration
        if do_f:
            self.stage_out(l, mod, xsrc, xdst, row0 + t0)
        if self.stop == 'out':
            raise StopIteration

    def stage_mlstm(self, l, t0, co, dirs, prompt, seq_idx, first, last, mode='full'):
        S = self.S
        do_f = 0 in dirs
        if mode != 'load':
            for d in (sorted(set(dirs) | {0}) if mode == 'store' else dirs):
                g = self.mg[d]
                self.proj_fm(1920 + d * 6, 6, co, 256, lambda ps, g=g, d=d: self.act(g["gi"][:, :], ps, AF.Identity, bias=self.ppc(l, 'M_BI', d, 6)))
                def ev_f(ps, g=g, d=d):
                    self.act(g["lf"][:, :], ps, AF.Exp, bias=self.dvc(l, 'NBF', d, 6), scale=-1.0)
                    self.act(g["lf"][:, :], g["lf"][:, :], AF.Ln, bias=1.0)
                    self.ts('dve', g["lf"][:, :], g["lf"][:, :], -1.0, None, ALU.mult)
                self.proj_fm(1932 + d * 6, 6, co, 256, ev_f)
        bi = t0 // 256
        ml_items = [(self.QT[:, :, :], 'QT'), (self.KT[:, :, :], 'KT'), (self.KTOK[:, :, :], 'KTOK'), (self.VAUG[:, :, :, :], 'VAUG'),
                    (self.GOZ[:, :, :], 'GOZ'), (self.mg[0]["gi"][:, :], 'GI'), (self.mg[0]["lf"][:, :], 'LF')]
        if mode == 'load':
            for ap_, k_ in ml_items:
                S.dma('sp', ap_, self.sc[k_][bi])
        for d in dirs:
            if first[d]:
                if prompt:
                    self.memset('dve', self.mC[d][:, :, :], 0.0)
                    self.memset('dve', self.mM[d][:, :], 0.0)
                else:
                    S.dma('sp', self.mC[d][:, :, :], self.st_mC[l, d])
                    S.dma('sp', self.mM[d][:, :], self.st_mm[:, l, d:d + 1], allow_slow_non_contiguous=True)
        if do_f and not (1 in dirs):
            S.dma('sp', self.HB[:, :, :], self.sHB[t0:t0 + 256, :].rearrange("(c s) f -> s c f", s=64))
        for d in dirs:
            g = self.mg[d]
            v3 = lambda t: self.V(t, 0, 6, 0, [[64, 4], [1, 64]])
            self.scan(g["pre"][:, :], self.CST[0:6, CC['RMASK']:CC['RMASK'] + 256], g["lf"][:, :], 0.0, ALU.mult, ALU.add)
            bL = self.V(g["pre"], 0, 6, 63, [[64, 4]])
            bLb = self.V(g["pre"], 0, 6, 63, [[64, 4], [0, 64]])
            if d == 0:
                bsrc = g["pre"]
            else:
                self.tt('dve', v3(g["bb"]), bLb, v3(g["pre"]), ALU.subtract)
                self.tt('dve', g["bb"][:, :], g["bb"][:, :], g["lf"][:, :], ALU.add)
                bsrc = g["bb"]
            self.tt('dve', g["gg"][:, :], g["gi"][:, :], bsrc[:, :], ALU.subtract)
            self.reduce(g["mx"][:, :], v3(g["gg"]), ALU.max)
            if d == 0:
                mo, mxv, blv = g["mch"][:, :], g["mx"][:, :], bL
            else:
                mo = self.V(g["mch"], 0, 6, 3, [[-1, 4]])
                mxv = self.V(g["mx"], 0, 6, 3, [[-1, 4]])
                blv = self.V(g["pre"], 0, 6, 63 + 3 * 64, [[-64, 4]])
            self.scan(mo, mxv, blv, self.mM[d][:, 0:1], ALU.max, ALU.add)
            if d == 0:
                self.cp('dve', g["mprev"][:, 1:4], g["mch"][:, 0:3])
                self.cp('dve', g["mprev"][:, 0:1], self.mM[d][:, 0:1])
                mfin = g["mch"][:, 3:4]
            else:
                self.cp('dve', g["mprev"][:, 0:3], g["mch"][:, 1:4])
                self.cp('dve', g["mprev"][:, 3:4], self.mM[d][:, 0:1])
                mfin = g["mch"][:, 0:1]
            self.tt('dve', g["MM"][:, :], g["mprev"][:, :], g["mx"][:, :], ALU.max)
            self.tt('dve', g["dec"][:, :], g["mprev"][:, :], g["MM"][:, :], ALU.subtract)
            self.act(g["dec"][:, :], g["dec"][:, :], AF.Exp)
            self.cp('dve', self.mM[d][:, 0:1], mfin)
            MMb = self.V(g["MM"], 0, 6, 0, [[1, 4], [0, 64]])
            self.tt('dve', v3(g["ee"]), v3(g["gg"]), MMb, ALU.subtract)
            self.act(g["ee"][:, :], g["ee"][:, :], AF.Exp)
            self.tt('dve', v3(g["fl"]), v3(bsrc), MMb, ALU.add)
            self.act(g["fl"][:, :], g["fl"][:, :], AF.Exp, scale=-1.0)
        if mode != 'load':
            for hp in range(3):
                self.proj_fm(hp * 128, 128, co, 256, lambda ps, hp=hp: self.cp('act', self.QT[:, hp, :], ps))
                self.proj_fm(384 + hp * 128, 128, co, 256, lambda ps, hp=hp: self.act(self.KT[:, hp, :], ps, AF.Copy, scale=0.125))
            if do_f or mode == 'store':
                for hp in range(3):
                    def ev_o(ps, hp=hp):
                        self.act(self.GOZ[:, hp, :], ps, AF.Sigmoid)
                    self.proj_fm(1152 + hp * 128, 128, co, 256, ev_o)
                    def ev_z2(ps, hp=hp):
                        tz = self.ZT[hp % 2]
                        self.act(tz[:, :], ps, AF.Silu)
                        self.tt('dve', self.GOZ[:, hp, :], self.GOZ[:, hp, :], tz[:, :], ALU.mult)
                    self.proj_fm(1536 + hp * 128, 128, co, 256, ev_z2)
            for c in range(4):
                self.proj_tm(384, 384, co + c * 64, lambda ps, c=c: self.act(self.KTOK[:, c, :], ps, AF.Copy, scale=0.125))
                def ev_v(ps, c=c):
                    self.cp('dve', self.VAUG[:, c, :, 0:64], self.V(ps.tensor, 0, 64, 0, [[64, 6], [1, 64]]))
                self.proj_tm(768, 384, co + c * 64, ev_v)
            self.memset('dve', self.VAUG[:, :, :, 64:65], 1.0)
        for d in dirs:
            g = self.mg[d]
            ps = self.pq()
            for c in range(4):
                self.tr(ps[0:64, c * 6:c * 6 + 6], g["ee"][:, c * 64:(c + 1) * 64])
                self.tr(ps[0:64, 24 + c * 6:24 + c * 6 + 6], g["fl"][:, c * 64:(c + 1) * 64])
            self.cp('dve', g["etok"][:, :, :], self.V(ps, 0, 64, 0, [[6, 4], [1, 6]]))
            self.cp('dve', g["fltok"][:, :, :], self.V(ps, 0, 64, 24, [[6, 4], [1, 6]]))
            self.tt('dve', g["X2"][:, :, :], self.V(g["dec"], 0, 6, 0, [[1, 4], [0, 3]]),
                    self.V(self.CST, 0, 6, CC['PSEL'], [[0, 4], [1, 3]]), ALU.mult)
            ps2 = self.pq()
            self.mm(ps2[:, 0:12], self.CST[0:6, CC['LSEL']:CC['LSEL'] + 128], self.V(g["X2"], 0, 6, 0, [[1, 12]]))
            self.cp('dve', g["decb"][:, :, :], self.V(ps2, 0, 128, 0, [[3, 4], [1, 3]]))
        if mode == 'store':
            for ap_, k_ in ml_items:
                S.dma('sp', self.sc[k_][bi], ap_)
        for d in sorted(dirs, reverse=True):
            g = self.mg[d]
            mask = self.CST[0:64, CC['MUI']:CC['MUI'] + 64] if d == 0 else self.CST[0:64, CC['MLI']:CC['MLI'] + 64]
            maskb = self.V(self.CST, 0, 64, CC['MUI'] if d == 0 else CC['MLI'], [[0, 6], [1, 64]])
            for j in range(4):
                c = j if d == 0 else 3 - j
                cs = slice(c * 64, (c + 1) * 64)
                stsb, vp = self.STSB[j % 2], self.VP[j % 2]
                ps = self.pq()
                for h in (0, 2, 4, 1, 3, 5):
                    hp, pb = h // 2, 64 * (h % 2)
                    self.mm(ps[0:64, h * 64:(h + 1) * 64], self.KT[pb:pb + 64, hp, cs], self.QT[pb:pb + 64, hp, cs], inc=(h == 5))
                self.tt('dve', stsb[:, :, :], self.V(ps, 0, 64, 0, [[64, 6], [1, 64]]), maskb, ALU.mult)
                self.tt('dve', vp[:, :, :], self.VAUG[:, c, :, :], self.V(g["etok"], 0, 64, c * 6, [[1, 6], [0, 65]]), ALU.mult)
                self.tt('dve', self.CDEC[:, :, :], self.mC[d][:, :, :], self.V(g["decb"], 0, 128, c * 3, [[1, 3], [0, 65]]), ALU.mult)
                self.cp('act', self.CDBF[:, :, :], self.CDEC[:, :, :])
                ph = self.pq()
                for h in (0, 2, 4, 1, 3, 5):
                    hp, pb = h // 2, 64 * (h % 2)
                    o = ph[0:64, h * 65:(h + 1) * 65]
                    self.mm(o, stsb[:, h, :], vp[:, h, :], start=True, stop=False)
                    self.mm(o, self.QT[pb:pb + 64, hp, cs], self.CDBF[pb:pb + 64, hp, :], start=False, stop=True, inc=(h == 5))
                pc = self.pq()
                for h in range(6):
                    hp, pb = h // 2, 64 * (h % 2)
                    self.mm(pc[pb:pb + 64, hp * 65:(hp + 1) * 65], self.KTOK[:, c, h * 64:(h + 1) * 64], vp[:, h, :], inc=(h == 5))
                self.tt('dve', self.mC[d][:, :, :], self.CDEC[:, :, :], self.V(pc, 0, 128, 0, [[65, 3], [1, 65]]), ALU.add)
                self.act(self.DN[:, :], self.V(ph, 0, 64, 64, [[65, 6]]), AF.Abs)
                self.tt('dve', self.DN[:, :], self.DN[:, :], g["fltok"][:, c, :], ALU.max)
                self.recip(self.RDN[:, :], self.DN[:, :])
                hsrc = self.V(ph, 0, 64, 0, [[65, 6], [1, 64]])
                rb = self.V(self.RDN, 0, 64, 0, [[1, 6], [0, 64]])
                hbv = self.V(self.HB, 0, 64, c * 384, [[64, 6], [1, 64]])
                if d == 1:
                    self.tt('dve', hbv, hsrc, rb, ALU.mult)
                else:
                    self.tt('dve', self.HD[:, :, :], hsrc, rb, ALU.mult)
                    self.tt('dve', hbv, hbv, self.HD[:, :, :], ALU.add)
            if last[d] and prompt:
                S.dma('sp', self.o_mC[seq_idx, l, d], self.mC[d][:, :, :])
                S.dma('sp', self.o_mm[seq_idx, l, d], self.mM[d][:, 0:1])
        if not do_f:
            S.dma('sp', self.sHB[t0:t0 + 256, :].rearrange("(c s) f -> s c f", s=64), self.HB[:, :, :])
            return
        self.tt('dve', self.SQ[:, :, :], self.HB[:, :, :], self.HB[:, :, :], ALU.mult)
        self.reduce(self.SSQ[:, :], self.V(self.SQ, 0, 64, 0, [[64, 24], [1, 64]]), ALU.add)
        self.ts('dve', self.SSQ[:, :], self.SSQ[:, :], 1.0 / 64, EPS, ALU.mult, ALU.add)
        self.act(self.SSQ[:, :], self.SSQ[:, :], AF.Sqrt)
        self.recip(self.RSTD[:, :], self.SSQ[:, :])
        self.tt('dve', self.V(self.SQ, 0, 64, 0, [[64, 24], [1, 64]]), self.V(self.HB, 0, 64, 0, [[64, 24], [1, 64]]),
                self.V(self.RSTD, 0, 64, 0, [[1, 24], [0, 64]]), ALU.mult)
        for hp in range(3):
            ps = self.pq()
            for c in range(4):
                self.tr(ps[:, c * 64:(c + 1) * 64], self.SQ[:, c, hp * 128:(hp + 1) * 128], inc=(c == 3))
            self.stt(self.MIXT[:, hp, :], ps[:, 0:256], self.ppc(l, 'M_NORM', hp), self.GOZ[:, hp, :], ALU.mult, ALU.mult)

    def stage_lru(self, l, t0, co, W, w0, w1, T, dirs, prompt, seq_idx, first, last, mode='full'):
        S = self.S
        do_f = 0 in dirs
        if mode != 'load':
            for pr in range(2):
                self.proj_fm(3736 + pr * 128, 128, 0, W, lambda ps, pr=pr: self.cp('act', self.XL[:, pr, 2:2 + W], ps))
                if do_f or mode == 'store':
                    self.proj_fm(3992 + pr * 128, 128, co, 256, lambda ps, pr=pr: self.act(self.LZT[:, pr, :], ps, AF.Silu))
            if w1 == T:
                self.memset('dve', self.XL[:, :, 2 + W:2 + W + 1], 0.0)
            if w0 == 0:
                self.memset('dve', self.XL[:, :, 0:2], 0.0)
            for pr in range(2):
                self.ts('dve', self.XC[:, pr, :], self.XL[:, pr, co:co + 256], self.ppc(l, 'L_CONV', 0 * 2 + pr), self.ppc(l, 'L_CONVB', pr), ALU.mult, ALU.add)
                for j in range(1, 4):
                    self.stt(self.XC[:, pr, :], self.XL[:, pr, co + j:co + j + 256], self.ppc(l, 'L_CONV', j * 2 + pr), self.XC[:, pr, :], ALU.mult, ALU.add)
        bi = t0 // 256
        lr_items = [(self.XC[:, :, :], 'XC'), (self.LZT[:, :, :], 'LZT')]
        if mode == 'store':
            for ap_, k_ in lr_items:
                S.dma('sp', self.sc[k_][bi], ap_)
        if mode == 'load':
            for ap_, k_ in lr_items:
                S.dma('sp', ap_, self.sc[k_][bi])
        for d in dirs:
            if first[d]:
                if prompt:
                    self.memset('dve', self.lS[d][:, :], 0.0)
                else:
                    S.dma('sp', self.lS[d][:, :], self.st_l[:, l, d, :])
        if do_f and not (1 in dirs):
            S.dma('sp', self.HL[1][:, :, :], self.sLB[:, :, t0:t0 + 256])
        combos = [(d, pr) for d in sorted(dirs, reverse=True) for pr in range(2)]
        LT = lambda d, n, pr: self.ltd[d][n][:, pr, :]
        pss = {}
        for i, (d, pr) in enumerate(combos):
            ps = self.PA if i % 2 == 0 else self.PB
            off = (i // 2) * 512
            pss[(d, pr)] = (ps, off)
            self.mm(ps[:, off:off + 256], self.LRUW[:, l, (0 * 2 + d) * 2 + pr, :], self.XC[:, pr, :])
            self.mm(ps[:, off + 256:off + 512], self.LRUW[:, l, (1 * 2 + d) * 2 + pr, :], self.XC[:, pr, :])
        for (d, pr) in combos:
            ps, off = pss[(d, pr)]
            self.act(LT(d, "rg", pr), ps[:, off:off + 256], AF.Sigmoid, bias=self.ppc(l, 'L_BA', d * 2 + pr))
            self.act(LT(d, "ig", pr), ps[:, off + 256:off + 512], AF.Sigmoid, bias=self.ppc(l, 'L_BX', d * 2 + pr))
        for (d, pr) in combos:
            self.act(LT(d, "aa", pr), LT(d, "rg", pr), AF.Exp, scale=self.dvc(l, 'CLAM', d * 2 + pr))
            self.act(LT(d, "a2", pr), LT(d, "rg", pr), AF.Exp, scale=self.dvc(l, 'C2LAM', d * 2 + pr))
        for (d, pr) in combos:
            self.ts('dve', LT(d, "a2", pr), LT(d, "a2", pr), -1.0, 1.0, ALU.mult, ALU.add)
            self.tt('dve', LT(d, "bt", pr), LT(d, "ig", pr), self.XC[:, pr, :], ALU.mult)
        for (d, pr) in combos:
            self.act(LT(d, "a2", pr), LT(d, "a2", pr), AF.Sqrt)
        for (d, pr) in combos:
            self.tt('dve', LT(d, "bt", pr), LT(d, "bt", pr), LT(d, "a2", pr), ALU.mult)
        for (d, pr) in combos:
            if d == 0:
                self.scan(self.HL[0][:, pr, :], LT(0, "aa", pr), LT(0, "bt", pr), self.lS[0][:, pr:pr + 1], ALU.mult, ALU.add)
                self.cp('act', self.lS[0][:, pr:pr + 1], self.HL[0][:, pr, 255:256])
            else:
                rv = lambda t: self.V(t, 0, 128, pr * 256 + 255, [[-1, 256]])
                self.scan(rv(self.HL[1]), rv(self.ltd[1]["aa"]), rv(self.ltd[1]["bt"]), self.lS[1][:, pr:pr + 1], ALU.mult, ALU.add)
                self.cp('act', self.lS[1][:, pr:pr + 1], self.HL[1][:, pr, 0:1])
        for d in sorted(dirs, reverse=True):
            if last[d] and prompt:
                S.dma('sp', self.o_l[seq_idx, l, d], self.lS[d][:, :])
        if not do_f:
            S.dma('sp', self.sLB[:, :, t0:t0 + 256], self.HL[1][:, :, :])
            return
        self.tt('dve', self.HL[0][:, :, :], self.HL[0][:, :, :], self.HL[1][:, :, :], ALU.add)
        self.tt('dve', self.MIXT[:, 6:8, :], self.HL[0][:, :, :], self.LZT[:, :, :], ALU.mult)

    def stage_rwkv(self, l, t0, co, W, w0, w1, T, dirs, grid, prompt, seq_idx, first, last, mode='full'):
        S = self.S
        do_f = 0 in dirs
        if mode != 'load':
            for ch in range(11):
                self.proj_fm(1944 + ch * 128, 128, 0, W, lambda ps, ch=ch: self.cp('act' if ch % 2 else 'dve', self.URS[:, ch, 0:W], ps))
            if do_f or mode == 'store':
                for hp in range(3):
                    self.proj_fm(3352 + hp * 128, 128, co, 256, lambda ps, hp=hp: self.act(self.RZT[:, hp, :], ps, AF.Silu))
            U3 = lambda off, n: self.V(self.URS, 0, 128, off, [[384, 11], [1, n]])
            B3 = lambda off, n: self.V(self.BLK, 0, 128, off, [[256, 11], [1, n]])
            if not grid:
                self.cp('dve', B3(1, 255), U3(0, 255))
                self.memset('dve', B3(0, 1), 0.0)
                self.tt('dve', B3(0, 255), B3(0, 255), U3(1, 255), ALU.add)
                wsh = 0.5
            else:
                U4 = lambda off, r, n: self.V(self.URS, 0, 128, off, [[384, 11], [64, r], [1, n]])
                B4 = lambda off, r, n: self.V(self.BLK, 0, 128, off, [[256, 11], [64, r], [1, n]])
                self.cp('dve', B4(1, 4, 63), U4(co, 4, 63))
                self.memset('dve', B4(0, 4, 1), 0.0)
                self.tt('dve', B4(0, 4, 63), B4(0, 4, 63), U4(co + 1, 4, 63), ALU.add)
                if t0 > 0:
                    self.tt('dve', B3(0, 256), B3(0, 256), U3(co - 64, 256), ALU.add)
                else:
                    self.tt('dve', B3(64, 192), B3(64, 192), U3(0, 192), ALU.add)
                if t0 + TB < T:
                    self.tt('dve', B3(0, 256), B3(0, 256), U3(co + 64, 256), ALU.add)
                else:
                    self.tt('dve', B3(0, 192), B3(0, 192), U3(co + 64, 192), ALU.add)
                wsh = 0.25
            mu = self.V(self.PP, 0, 128, l * PL + PC['R_MU'], [[1, 11], [0, 256]])
            self.tt('dve', B3(0, 256), B3(0, 256), mu, ALU.mult)
            omm = self.V(self.DV, 0, 128, l * DL + DC['OMMU'], [[1, 11], [0, 256]])
            self.tt('dve', U3(co, 256), U3(co, 256), omm, ALU.mult)
            self.stt(B3(0, 256), B3(0, 256), wsh, U3(co, 256), ALU.mult, ALU.add)
            for nm, ch in (('blk_r', 0), ('blk_k', 3), ('blk_v', 6), ('blk_wl', 9), ('blk_al', 10)):
                self.dump(nm, self.BLK[:, ch, :])
            rt = self.rt
            kk = self.V(self.PP, 0, 128, l * PL + PC['R_KK'], [[1, 3], [0, 256]])
            kap = rt["d"]
            self.tt('dve', kap[:, :, :], self.BLK[:, 3:6, :], kk, ALU.mult)
            ksq = rt["E"]
            self.tt('dve', ksq[:, :, :], kap[:, :, :], kap[:, :, :], ALU.mult)
            for hp in range(3):
                self.mm(self.PA[:, hp * 256:(hp + 1) * 256], self.CST[:, CC['BONES']:CC['BONES'] + 128], ksq[:, hp, :])
            self.act(ksq[:, :, :], self.V(self.PA, 0, 128, 0, [[256, 3], [1, 256]]), AF.Sqrt)
            self.ts('dve', ksq[:, :, :], ksq[:, :, :], 1e-12, None, ALU.max)
            self.recip(ksq[:, :, :], ksq[:, :, :])
            self.tt('dve', self.KH[:, :, :], kap[:, :, :], ksq[:, :, :], ALU.mult)
            self.dump('kh', self.KH[:, 0, :])
            self.bd_fill(self.BDV, lambda par: self.V(self.BLK, par * 64, 64, 6 * 256, [[256, 3], [64, 4], [1, 64]]))
            for c in range(4):
                for hp in range(3):
                    self.mm(self.PB[:, (c * 3 + hp) * 64:(c * 3 + hp + 1) * 64], self.BDV[:, hp, c, :], self.ISTKB[:, :])
            self.cp('act', self.V(self.VSTK, 0, 128, 0, [[1, 768]]), self.PB[:, 0:768])
        bi = t0 // 256
        rw_items = [(self.BLK[:, :, :], 'BLK'), (self.RZT[:, :, :], 'RZT'), (self.KH[:, :, :], 'KH'), (self.VSTK[:, :, :, :], 'VSTK')]
        if mode == 'store':
            for ap_, k_ in rw_items:
                S.dma('sp', self.sc[k_][bi], ap_)
        if mode == 'load':
            for ap_, k_ in rw_items:
                S.dma('sp', ap_, self.sc[k_][bi])
        for d in dirs:
            if first[d]:
                if prompt:
                    self.memset('dve', self.rH[d][:, :, :], 0.0)
                else:
                    S.dma('sp', self.rH[d][:, :, :], self.st_rH[l, d])
        ybflat = self.V(self.YB, 0, 128, 0, [[1, 768]])
        if do_f and not (1 in dirs):
            S.dma('sp', ybflat, self.sYB[t0 // 256])
        for d in sorted(dirs, reverse=True):
            self.rwkv_dir(l, d, seq_idx, prompt, last)
        if not do_f:
            S.dma('sp', self.sYB[t0 // 256], ybflat)
            return
        ft = self.ft
        self.tt('dve', self.YSQ[:, :, :, :], self.YB[:, :, :, :], self.YB[:, :, :, :], ALU.mult)
        self.reduce(self.rSSQ[:, :], self.V(self.YSQ, 0, 128, 0, [[64, 12], [1, 64]]), ALU.add)
        self.ts('dve', self.rSSQ[:, :], self.rSSQ[:, :], 1.0 / 64, EPS, ALU.mult, ALU.add)
        self.act(self.rSSQ[:, :], self.rSSQ[:, :], AF.Sqrt)
        self.recip(self.rRSTD[:, :], self.rSSQ[:, :])
        self.tt('dve', ft["rk"][:, :, :], self.BLK[:, 0:3, :], self.BLK[:, 3:6, :], ALU.mult)
        for hp in range(3):
            self.mm(self.PB[:, hp * 256:(hp + 1) * 256], self.RKD[:, l, hp, :], ft["rk"][:, hp, :])
        self.tt('dve', ft["bon"][:, :, :], self.V(self.PB, 0, 128, 0, [[256, 3], [1, 256]]), self.BLK[:, 6:9, :], ALU.mult)
        self.memset('pool', self.YBD[:, :, :], 0.0)
        for c in range(4):
            for par in range(2):
                self.tt('dve', self.V(self.YBD, par * 64, 64, par * 64, [[128, 3], [1, 64]]),
                        self.V(self.YB, par * 64, 64, c * 192, [[64, 3], [1, 64]]),
                        self.V(self.rRSTD, par * 64, 64, c * 3, [[1, 3], [0, 64]]), ALU.mult)
            for hp in range(3):
                self.mm(self.PA[:, hp * 256 + c * 64: hp * 256 + (c + 1) * 64], self.YBD[:, hp, :],
                        self.CST[:, CC['ISTK']:CC['ISTK'] + 64])
        for hp in range(3):
            self.stt(ft["t1"][:, hp, :], self.PA[:, hp * 256:(hp + 1) * 256], self.ppc(l, 'R_NORM', hp), ft["bon"][:, hp, :], ALU.mult, ALU.add)
            self.tt('dve', self.MIXT[:, 3 + hp, :], ft["t1"][:, hp, :], self.RZT[:, hp, :], ALU.mult)

    def bd_fill(self, bd, src_of_par, eng='pool'):
        self.memset(eng, bd[:, :, :, :], 0.0)
        for par in range(2):
            self.cp('act' if par == 0 else eng, self.V(bd, par * 64, 64, par * 64, [[512, 3], [128, 4], [1, 64]]), src_of_par(par))

    def rwkv_dir(self, l, d, seq_idx, prompt, last):
        S = self.S
        rt = self.rt
        pb_d = 64 * d
        f3 = lambda t: t[:, :, :]
        v4 = lambda t: self.V(t, 0, 128, 0, [[256, 3], [64, 4], [1, 64]])
        self.act(self.TWL[pb_d:pb_d + 64, :], self.BLK[pb_d:pb_d + 64, 9, :], AF.Tanh)
        for hp in range(3):
            self.mm(self.PA[:, hp * 256:(hp + 1) * 256], self.LORA[pb_d:pb_d + 64, l, 0, hp * 128:(hp + 1) * 128], self.TWL[pb_d:pb_d + 64, :])
        for hp in range(3):
            self.act(rt["sg"][:, hp, :], self.PA[:, hp * 256:(hp + 1) * 256], AF.Sigmoid, bias=self.ppc(l, 'R_W0', d * 3 + hp))
        for hp in range(3):
            self.mm(self.PB[:, hp * 256:(hp + 1) * 256], self.LORA[pb_d:pb_d + 64, l, 1, hp * 128:(hp + 1) * 128], self.BLK[pb_d:pb_d + 64, 10, :])
        for hp in range(3):
            self.act(rt["aa"][:, hp, :], self.PB[:, hp * 256:(hp + 1) * 256], AF.Sigmoid, bias=self.ppc(l, 'R_A0', d * 3 + hp))
        for hp in range(3):
            self.ts('dve', rt["kt"][:, hp, :], rt["aa"][:, hp, :], self.ppc(l, 'R_KA', hp), self.dvc(l, 'OMKA', hp), ALU.mult, ALU.add)
        self.tt('dve', f3(rt["kt"]), f3(rt["kt"]), self.BLK[:, 3:6, :], ALU.mult)
        self.tt('dve', f3(rt["bb"]), self.KH[:, :, :], f3(rt["aa"]), ALU.mult)
        flat = lambda t: self.V(t, 0, 128, 0, [[1, 768]])
        self.scan(flat(rt["cs"]), self.CST[:, CC['RMASK']:CC['RMASK'] + 768], flat(rt["sg"]), 0.0, ALU.mult, ALU.add)
        self.cp('dve', self.GL[:, :, :], self.V(rt["cs"], 0, 128, 63, [[256, 3], [64, 4]]))
        if d == 1:
            csLb = self.V(rt["cs"], 0, 128, 63, [[256, 3], [64, 4], [0, 64]])
            self.tt('dve', v4(rt["d"]), csLb, v4(rt["cs"]), ALU.subtract)
            self.tt('dve', f3(rt["cs"]), f3(rt["d"]), f3(rt["sg"]), ALU.add)
        self.act(f3(rt["E"]), f3(rt["cs"]), AF.Exp, scale=-DSC)
        self.tt('dve', self.V(self.KR, 0, 128, 64, [[512, 3], [128, 4], [1, 64]]),
                self.V(self.BLK, 0, 128, 0, [[256, 3], [64, 4], [1, 64]]), v4(rt["E"]), ALU.mult)
        self.tt('dve', f3(rt["d"]), f3(rt["cs"]), f3(rt["sg"]), ALU.subtract)
        self.act(f3(rt["E"]), f3(rt["d"]), AF.Exp, scale=-DSC)
        self.tt('dve', self.V(self.KR, 0, 128, 0, [[512, 3], [128, 4], [1, 64]]), v4(self.KH), v4(rt["E"]), ALU.mult)
        self.act(f3(rt["E"]), f3(rt["cs"]), AF.Exp, scale=DSC)
        self.tt('dve', self.BTT[:, :, :], f3(rt["bb"]), f3(rt["E"]), ALU.mult)
        self.tt('dve', self.KTT[:, :, :], f3(rt["kt"]), f3(rt["E"]), ALU.mult)
        self.act(self.GL[:, :, :], self.GL[:, :, :], AF.Exp, scale=-DSC)
        BD = self.BD
        self.memset('pool', self.A1BD[:, :, :, :], 0.0)
        self.memset('pool', self.A2BD[:, :, :, :], 0.0)
        self.memset('pool', self.NN[0][:, :, :], 0.0)
        c4 = lambda t, par: self.V(t, par * 64, 64, 0, [[256, 3], [64, 4], [1, 64]])
        self.bd_fill(BD["kt"], lambda par: c4(self.KTT, par))
        self.bd_fill(BD["b"], lambda par: c4(self.BTT, par))
        self.bd_fill(BD["kh"], lambda par: self.V(self.KR, par * 64, 64, 0, [[512, 3], [128, 4], [1, 64]]))
        self.bd_fill(BD["r"], lambda par: self.V(self.KR, par * 64, 64, 64, [[512, 3], [128, 4], [1, 64]]))
        for (src, dst, neg) in ((BD["kt"], self.KTTOK, False), (BD["b"], self.BTTOK, True)):
            for half in range(2):
                for cc_ in range(2):
                    c = half * 2 + cc_
                    for hp in range(3):
                        self.tr(self.PTB[:, (cc_ * 3 + hp) * 128:(cc_ * 3 + hp + 1) * 128], src[:, hp, c, :], bf=True)
                o = self.V(dst, 0, 128, half * 768, [[1, 768]])
                if neg:
                    self.act(o, self.PTB[:, 0:768], AF.Copy, scale=-1.0)
                else:
                    self.cp('dve', o, self.PTB[:, 0:768])
        self.cp('act', self.HBF[:, :, :], self.rH[d][:, :, :])
        mk = CC['MKF'] if d == 0 else CC['MKB']
        nmk = CC['NMKF'] if d == 0 else CC['NMKB']
        mn = CC['MNF'] if d == 0 else CC['MNB']
        for j in range(4):
            c = j if d == 0 else 3 - j
            cs = slice(c * 64, (c + 1) * 64)
            KRc = lambda hp: self.V(self.KR, 0, 128, hp * 512 + c * 128, [[1, 128]])
            p1, p2, p3 = self.pq(), self.pq(), self.pq()
            for hp in range(3):
                self.mm(p1[:, hp * 128:(hp + 1) * 128], BD["kt"][:, hp, c, :], KRc(hp), inc=(hp == 2))
            for hp in range(3):
                self.mm(p2[:, hp * 128:(hp + 1) * 128], BD["b"][:, hp, c, :], KRc(hp), inc=(hp == 2))
            for hp in range(3):
                self.mm(p3[:, hp * 64:(hp + 1) * 64], BD["kh"][:, hp, c, :], self.BTT[:, hp, cs], inc=(hp == 2))
            for par in range(2):
                pp_ = par * 64
                self.tt('dve', self.V(self.A1BD, pp_, 64, pp_, [[256, 3], [128, 2], [1, 64]]),
                        self.V(p1, pp_, 64, 0, [[128, 3], [64, 2], [1, 64]]),
                        self.V(self.CST, pp_, 64, mk, [[0, 3], [64, 2], [1, 64]]), ALU.mult)
                self.tt('dve', self.V(self.A2BD, pp_, 64, pp_, [[256, 3], [128, 2], [1, 64]]),
                        self.V(p2, pp_, 64, 0, [[128, 3], [64, 2], [1, 64]]),
                        self.V(self.CST, pp_, 64, nmk, [[0, 3], [64, 2], [1, 64]]), ALU.mult)
                self.tt('dve', self.V(self.NN[0], pp_, 64, pp_, [[128, 3], [1, 64]]),
                        self.V(p3, pp_, 64, 0, [[64, 3], [1, 64]]),
                        self.V(self.CST, pp_, 64, mn, [[0, 3], [1, 64]]), ALU.mult)
            pr_ = self.pq()
            for hp in range(3):
                o = pr_[:, hp * 64:(hp + 1) * 64]
                self.mm(o, BD["kh"][:, hp, c, :], self.HBF[:, hp, :], start=True, stop=False)
                self.mm(o, self.A1BD[:, hp, 0, :], self.VSTK[:, c, hp, :], start=False, stop=True, inc=(hp == 2))
            u192 = self.V(self.U, 0, 128, 0, [[1, 192]])
            ub192 = self.V(self.UBF, 0, 128, 0, [[1, 192]])
            self.cp('dve', u192, pr_[:, 0:192])
            self.cp('act', ub192, u192)
            for k in range(6):
                NTk = self.A2BD[:, :, 0, :] if k == 0 else self.NTT[k % 2][:, :, :]
                Nk = self.NN[k % 2]
                pu = self.pq()
                for hp in range(3):
                    self.mm(pu[:, hp * 64:(hp + 1) * 64], NTk[:, hp, :], self.UBF[:, hp, :], inc=(hp == 2))
                if k < 5:
                    pnt = self.pq()
                    for hp in range(3):
                        self.mm(pnt[:, hp * 128:(hp + 1) * 128], Nk[:, hp, :], NTk[:, hp, :], inc=(hp == 2))
                    pn = self.pq()
                    for hp in range(3):
                        self.mm(pn[:, hp * 128:(hp + 1) * 128], NTk[:, hp, :], Nk[:, hp, :], inc=(hp == 2))
                self.tt('dve', ub192, u192, pu[:, 0:192], ALU.add)
                if k < 5:
                    self.cp('act', self.V(self.NTT[(k + 1) % 2], 0, 128, 0, [[1, 384]]), pnt[:, 0:384])
                    self.cp('act', self.V(self.NN[(k + 1) % 2], 0, 128, 0, [[1, 384]]), pn[:, 0:384])
                    self.tt('dve', u192, u192, pu[:, 0:192], ALU.add)
            py = self.pq()
            for hp in range(3):
                o = py[:, hp * 64:(hp + 1) * 64]
                self.mm(o, BD["r"][:, hp, c, :], self.HBF[:, hp, :], start=True, stop=False)
                self.mm(o, self.A1BD[:, hp, 1, :], self.VSTK[:, c, hp, :], start=False, stop=False)
                self.mm(o, self.A2BD[:, hp, 1, :], self.UBF[:, hp, :], start=False, stop=True, inc=(hp == 2))
            ybv = self.V(self.YB, 0, 128, c * 192, [[1, 192]])
            if d == 1:
                self.cp('act', ybv, py[:, 0:192])
            else:
                self.tt('dve', ybv, ybv, py[:, 0:192], ALU.add)
            ph = self.pq()
            for hp in range(3):
                o = ph[:, hp * 64:(hp + 1) * 64]
                self.mm(o, self.KTTOK[:, c, hp, :], self.VSTK[:, c, hp, :], start=True, stop=False)
                self.mm(o, self.BTTOK[:, c, hp, :], self.UBF[:, hp, :], start=False, stop=True, inc=(hp == 2))
            self.tt('dve', self.HTMP[:, :, :], self.rH[d][:, :, :], self.V(ph, 0, 128, 0, [[64, 3], [1, 64]]), ALU.add)
            self.tt('dve', self.rH[d][:, :, :], self.HTMP[:, :, :], self.V(self.GL, 0, 128, c, [[4, 3], [0, 64]]), ALU.mult)
            self.cp('act', self.HBF[:, :, :], self.rH[d][:, :, :])
        if last[d] and prompt:
            S.dma('sp', self.o_rH[seq_idx, l, d], self.rH[d][:, :, :])

    def stage_out(self, l, mod, xsrc, xdst, r0):
        S = self.S
        for tt_ in range(2):
            o, xw = self.XN[tt_], self.XW[tt_]
            S.dma('sp', xw[:, :], xsrc[r0 + tt_ * 128: r0 + (tt_ + 1) * 128, :])
            for ch in range(2):
                ps = self.pq()
                for kc in range(8):
                    self.mm(ps[:, :], self.MIXT[:, kc, tt_ * 128:(tt_ + 1) * 128], self.W_OUT[:, kc, ch * 512:(ch + 1) * 512],
                            start=(kc == 0), stop=(kc == 7))
                self.cp('act' if ch else 'dve', o[:, ch * 512:(ch + 1) * 512], ps[:, :])
            self.dump('o_proj%d' % tt_, o[:, :])
            self.dump('o_x%d' % tt_, xw[:, :])
            ssq = self.TMPS[:, 24 + tt_:25 + tt_]
            junk = self.V(self.BLK, 0, 128, 0, [[1, D]])
            self.act(junk, o[:, :], AF.Square, accum=ssq)
            rs = self.TMPS[:, 26 + tt_:27 + tt_]
            self.ts('dve', rs, ssq, 1.0 / D, EPS, ALU.mult, ALU.add)
            self.act(rs, rs, AF.Sqrt)
            self.recip(rs, rs)
            self.dump('o_rs%d' % tt_, rs)
            self.stt(o[:, :], o[:, :], rs, self.GATEB[:, mod, :], ALU.mult, ALU.mult)
            self.dump('o_g%d' % tt_, o[:, :])
            self.tt('dve', o[:, :], o[:, :], xw[:, :], ALU.add)
            S.dma('sp', xdst[r0 + tt_ * 128: r0 + (tt_ + 1) * 128, :], o[:, :])

    def build(self, layers=(0, 1)):
        try:
            self._build(layers)
        except StopIteration:
            pass
        self.S.finish('sp')
        return self.nc

    def _build(self, layers):
        NP, TS = self.NP, self.TS
        self.setup()
        if self.stop == 'setup':
            raise StopIteration
        for li, l in enumerate(layers):
            self.load_layer(l)
            if self.stop == 'load':
                raise StopIteration
            xsrc = self.x_in if li == 0 else self.x1
            xdst = self.y_out if li == len(layers) - 1 else self.x1
            T_, F_ = {0: True, 1: True}, {0: False, 1: False}
            for s in range(NP):
                self.visit(l, 0, xsrc, xdst, s * 256, 256, 0, [0, 1], False, s, T_, T_)
            if TS > 0:
                nb = TS // TB
                row0 = NP * 256
                for b in range(nb - 1, -1, -1):
                    self.visit(l, 1, xsrc, xdst, row0, TS, b * TB, [1], True, 0,
                               {0: False, 1: b == nb - 1}, {0: False, 1: b == 0}, mode='store')
                for b in range(nb):
                    self.visit(l, 1, xsrc, xdst, row0, TS, b * TB, [0], True, 0,
                               {0: b == 0, 1: False}, {0: b == nb - 1, 1: False}, mode='load')


def prep_shared(inp):
    f = lambda a: np.ascontiguousarray(np.asarray(a, dtype=np.float32))
    b_mod = f(inp['b_mod'])
    sh = {}
    sh['w_mod'] = f(inp['w_mod'])
    sh['w_in'] = f(inp['w_in'])
    sh['w_out'] = f(inp['w_out'])
    sh['bmodT'] = f(b_mod[:, :2048].reshape(2, 16, 128).transpose(2, 0, 1))
    sh['bmodg'] = f(b_mod[:, 2048:3072])
    sh['gpost'] = f(inp['g_post'])
    pp = np.zeros((128, 2 * PL), np.float32)
    for l in range(2):
        o = l * PL
        def put(key, arr, n):
            pp[:, o + PC[key]: o + PC[key] + n] = np.asarray(arr, np.float32).reshape(n, 128).T
        put('G_PRE', inp['g_pre'][l], 8)
        put('M_NORM', inp['m_norm'][l], 3)
        put('R_MU', inp['r_mu'][l], 11)
        put('R_W0', np.asarray(inp['r_w0'][l]).reshape(-1), 6)
        put('R_A0', np.asarray(inp['r_a0'][l]).reshape(-1), 6)
        put('R_KK', inp['r_kk'][l], 3)
        put('R_KA', inp['r_ka'][l], 3)
        put('R_RK', inp['r_rk'][l], 3)
        put('R_NORM', inp['r_norm'][l], 3)
        put('L_CONV', np.asarray(inp['l_conv'][l]).reshape(-1), 8)
        put('L_CONVB', inp['l_conv_b'][l], 2)
        put('L_BA', np.asarray(inp['l_ba'][l]).reshape(-1), 4)
        put('L_BX', np.asarray(inp['l_bx'][l]).reshape(-1), 4)
        put('L_LAM', np.asarray(inp['l_lambda'][l]).reshape(-1), 4)
        pp[0:6, o + PC['M_BI']: o + PC['M_BI'] + 2] = np.asarray(inp['m_bi'][l], np.float32).T
        pp[0:6, o + PC['M_BF']: o + PC['M_BF'] + 2] = np.asarray(inp['m_bf'][l], np.float32).T
    sh['pp'] = pp
    sh['cst'] = make_consts()
    lora = np.zeros((128, 2, 2, 384), np.float32)
    for wi, key in enumerate(['r_w2', 'r_a2']):
        a = np.asarray(inp[key], np.float32)
        lora[:, :, wi, :] = a.transpose(1, 2, 0, 3).reshape(128, 2, 384)
    sh['lora'] = lora
    lruw = np.zeros((128, 2, 8, 128), np.float32)
    for gi, key in enumerate(['l_wa', 'l_wx']):
        a = np.asarray(inp[key], np.float32)
        for l in range(2):
            for d in range(2):
                for pr in range(2):
                    for hb in range(2):
                        n = 2 * pr + hb
                        lruw[hb * 64:(hb + 1) * 64, l, (gi * 2 + d) * 2 + pr, hb * 64:(hb + 1) * 64] = a[l, d, n]
    sh['lruw'] = lruw
    return sh


def prep_core(inp, b, NP, TS):
    f = lambda a: np.ascontiguousarray(np.asarray(a, dtype=np.float32))
    m = {}
    xp = np.asarray(inp['x_prompt'], np.float32)[b * NP:(b + 1) * NP].reshape(NP * 256, D)
    if TS > 0:
        xs = np.asarray(inp['x_sample'], np.float32)[b]
        m['x_in'] = f(np.concatenate([xp, xs], 0))
    else:
        m['x_in'] = f(xp)
    cc = np.stack([np.asarray(inp['c_ctx'], np.float32), np.asarray(inp['c'], np.float32)[b]], -1)
    m['cc'] = f(cc.reshape(8, 128, 2).transpose(1, 0, 2))
    C = np.asarray(inp['state_mlstm_C'], np.float32)[b]
    n = np.asarray(inp['state_mlstm_n'], np.float32)[b]
    Cn = np.concatenate([C, n[..., None]], -1)
    Cn = Cn.reshape(2, 2, 3, 2, 64, 65).transpose(0, 1, 3, 4, 2, 5).reshape(2, 2, 128, 3, 65)
    m['st_mC'] = f(Cn)
    m['st_mm'] = f(np.asarray(inp['state_mlstm_m'], np.float32)[b].transpose(2, 0, 1))
    R = np.asarray(inp['state_rwkv'], np.float32)[b]
    R = R.transpose(0, 1, 2, 4, 3)
    R = R.reshape(2, 2, 3, 2, 64, 64).transpose(0, 1, 3, 4, 2, 5).reshape(2, 2, 128, 3, 64)
    m['st_rH'] = f(R)
    L = np.asarray(inp['state_rglru'], np.float32)[b]
    m['st_l'] = f(L.reshape(2, 2, 2, 128).transpose(3, 0, 1, 2))
    return m


def unpack_core(r, NP, TS):
    y = r['y_out']
    yp = y[:NP * 256].reshape(NP, 256, D)
    ys = y[NP * 256:]
    mC = r['o_mC'].reshape(NP, 2, 2, 2, 64, 3, 65).transpose(0, 1, 2, 5, 3, 4, 6).reshape(NP, 2, 2, 6, 64, 65)
    newC = np.ascontiguousarray(mC[..., :64])
    newn = np.ascontiguousarray(mC[..., 64])
    newm = r['o_mm'].reshape(NP, 2, 2, 6)
    rH = r['o_rH'].reshape(NP, 2, 2, 2, 64, 3, 64).transpose(0, 1, 2, 5, 3, 4, 6).reshape(NP, 2, 2, 6, 64, 64)
    newr = np.ascontiguousarray(rH.transpose(0, 1, 2, 3, 5, 4))
    newl = np.ascontiguousarray(r['o_l'].transpose(0, 1, 2, 4, 3).reshape(NP, 2, 2, 256))
    return yp, ys, newC, newn, newm, newr, newl


_NC_CACHE = {}


def kernel(**inputs):
    NP, TS = 4, 2048
    key = (NP, TS)
    if key not in _NC_CACHE:
        _NC_CACHE[key] = Builder(NP, TS).build()
    nc = _NC_CACHE[key]
    sh = prep_shared(inputs)
    in_maps = []
    for b in range(NCORES):
        m = dict(sh)
        m.update(prep_core(inputs, b, NP, TS))
        in_maps.append(m)
    res = run_bass_kernel_spmd(nc, in_maps, core_ids=list(range(NCORES)))
    outs = [unpack_core(r, NP, TS) for r in res.results]
    y_prompt = np.concatenate([o[0] for o in outs], 0)
    y_sample = np.stack([o[1] for o in outs], 0)
    cat = lambda i: np.concatenate([o[i] for o in outs], 0)
    return (y_prompt.astype(np.float32), y_sample.astype(np.float32), cat(2).astype(np.float32),
            cat(3).astype(np.float32), cat(4).astype(np.float32), cat(5).astype(np.float32), cat(6).astype(np.float32))
```

```python
import numpy as np
import concourse.bass as bass
import concourse.mybir as mybir
from concourse.bass_utils import run_bass_kernel_spmd

F32 = mybir.dt.float32
BF16 = mybir.dt.bfloat16
AF = mybir.ActivationFunctionType
ALU = mybir.AluOpType
AX = mybir.AxisListType

D = 1024
IN_COLS = 4248
EPS = 1e-6
DSC = 0.6065306597126334
TB = 256
NCORES = 8


def _prod(xs):
    r = 1
    for x in xs:
        r *= int(x)
    return r


class Sync:
    def __init__(self, nc, n_dma_sems=32):
        self.nc = nc
        self.engs = {'pe': nc.tensor, 'dve': nc.vector, 'act': nc.scalar,
                     'pool': nc.gpsimd, 'sp': nc.sync}
        self.sem = {}
        self.cnt = {}
        for e in ['pe', 'dve', 'act', 'pool']:
            self.sem[e] = nc.alloc_semaphore('sem_' + e)
            self.cnt[e] = 0
        self.seen = {e: {} for e in self.engs}
        self.dma_ring = [nc.alloc_semaphore('dq_%d' % i) for i in range(n_dma_sems)]
        self.dma_uses = [0] * n_dma_sems
        self.dma_next = 0
        self.rec = {}
        self.untracked = set()
        self.n_wait = 0
        self.n_ins = 0
        self.pstep_cache = {}
        self.sb_addr = {}

    def region(self, ap):
        t = ap.tensor
        name = t.name
        apl = [(int(s), int(c)) for (s, c) in ap.ap]
        off = int(ap.offset)
        if type(t).__name__.startswith('DRam'):
            lo = off + sum(min(0, s * (c - 1)) for s, c in apl)
            hi = off + sum(max(0, s * (c - 1)) for s, c in apl) + 1
            return (name, 0, 1, lo, hi)
        pstep = self.pstep_cache.get(name)
        if pstep is None:
            pstep = _prod(list(t.shape)[1:])
            self.pstep_cache[name] = pstep
        p0 = off // pstep
        f0 = off % pstep
        npart = apl[0][1]
        rest = apl[1:]
        lo = f0 + sum(min(0, s * (c - 1)) for s, c in rest)
        hi = f0 + sum(max(0, s * (c - 1)) for s, c in rest) + 1
        if name in self.sb_addr:
            base, es = self.sb_addr[name]
            return ('SB', p0, p0 + npart, base + lo * es, base + hi * es)
        return ('PS:' + name, (p0 // 32) * 32, ((p0 + npart + 31) // 32) * 32, (lo // 512) * 512, ((hi + 511) // 512) * 512)

    @staticmethod
    def _ovl(a, b):
        return a[1] < b[2] and b[1] < a[2] and a[3] < b[4] and b[3] < a[4]

    @staticmethod
    def _contains(a, b):
        return a[1] <= b[1] and b[2] <= a[2] and a[3] <= b[3] and b[4] <= a[4]

    def _collect(self, e, reads, writes):
        deps = {}
        own = self.sem.get(e)
        rregs = [self.region(a) for a in reads]
        wregs = [self.region(a) for a in writes]
        for r in rregs:
            if r[0] in self.untracked:
                continue
            isps = r[0].startswith('PS:')
            for (reg, kind, sem, val) in self.rec.get(r[0], ()):
                if (kind == 'w' or (isps and sem is not own)) and self._ovl(reg, r):
                    if e == 'pe' and sem is own:
                        continue
                    k = id(sem)
                    if deps.get(k, (None, 0))[1] < val:
                        deps[k] = (sem, val)
        for w in wregs:
            if w[0] in self.untracked:
                continue
            for (reg, kind, sem, val) in self.rec.get(w[0], ()):
                if self._ovl(reg, w):
                    if sem is own:
                        continue
                    k = id(sem)
                    if deps.get(k, (None, 0))[1] < val:
                        deps[k] = (sem, val)
        return deps, rregs, wregs

    def _record(self, rregs, wregs, sem, val):
        for r in rregs:
            if r[0] in self.untracked:
                continue
            lst = self.rec.setdefault(r[0], [])
            lst[:] = [x for x in lst if not (x[1] == 'r' and x[2] is sem and self._contains(r, x[0]))]
            lst.append((r, 'r', sem, val))
        for w in wregs:
            if w[0] in self.untracked:
                continue
            lst = self.rec.setdefault(w[0], [])
            lst[:] = [x for x in lst if not self._contains(w, x[0])]
            lst.append((w, 'w', sem, val))

    def wait(self, e, sem, val):
        k = id(sem)
        if self.seen[e].get(k, 0) >= val:
            return
        self.engs[e].wait_ge(sem, val)
        self.seen[e][k] = val
        self.n_wait += 1

    max_ins = None
    paranoid = False
    embed_waits = True

    def emit(self, e, reads, writes, build, inc=True):
        if self.max_ins is not None and self.n_ins >= self.max_ins:
            raise StopIteration
        deps, rregs, wregs = self._collect(e, reads, writes)
        embed = None
        for (sem, val) in deps.values():
            if self.embed_waits and embed is None and self.seen[e].get(id(sem), 0) < val:
                embed = (sem, val)
                continue
            self.wait(e, sem, val)
        if self.paranoid:
            for e2 in ['pe', 'dve', 'act', 'pool']:
                if self.cnt[e2] > 0 and not (e == 'pe' and e2 == 'pe'):
                    self.wait(e, self.sem[e2], self.cnt[e2])
        ins = build(self.engs[e])
        if embed is not None:
            ins._wait_ge(embed[0], embed[1])
            self.seen[e][id(embed[0])] = embed[1]
        self.n_ins += 1
        if inc:
            self.cnt[e] += 1
            ins.then_inc(self.sem[e], 1)
            val = self.cnt[e]
        else:
            val = self.cnt[e] + 1
        self._record(rregs, wregs, self.sem[e], val)
        return ins

    def dma(self, q, out, in_, **kw):
        if self.max_ins is not None and self.n_ins >= self.max_ins:
            raise StopIteration
        i = self.dma_next
        self.dma_next = (i + 1) % len(self.dma_ring)
        sem = self.dma_ring[i]
        uses = self.dma_uses[i]
        if uses > 0:
            self.wait(q, sem, 16 * uses)
        deps, rregs, wregs = self._collect(q, [in_], [out])
        for (s, v) in deps.values():
            self.wait(q, s, v)
        ins = self.engs[q].dma_start(out=out, in_=in_, **kw)
        ins.then_inc(sem, 16)
        self.n_ins += 1
        self.dma_uses[i] = uses + 1
        self._record(rregs, wregs, sem, 16 * (uses + 1))
        return ins

    def finish(self, q='sp'):
        for i, sem in enumerate(self.dma_ring):
            if self.dma_uses[i] > 0:
                self.wait(q, sem, 16 * self.dma_uses[i])
        for e in ['pe', 'dve', 'act', 'pool']:
            if self.cnt[e] > 0:
                self.wait(q, self.sem[e], self.cnt[e])


class Arena:
    def __init__(self, base, size):
        self.base, self.size, self.ptr = base, size, 0

    def take(self, nbytes):
        off = (self.ptr + 31) // 32 * 32
        self.ptr = off + nbytes
        assert self.ptr <= self.size, ("arena overflow", self.ptr, self.size)
        return self.base + off


PL = 72
PC = dict(G_PRE=0, M_NORM=8, R_MU=11, R_W0=22, R_A0=28, R_KK=34, R_KA=37, R_RK=40, R_NORM=43,
          L_CONV=46, L_CONVB=54, L_BA=56, L_BX=60, L_LAM=64, M_BI=68, M_BF=70)
DL = 32
DC = dict(OMKA=0, CLAM=3, C2LAM=7, NBF=11, OMMU=16)
CC = dict(IDENT=0, BONES=128, MKF=256, MKB=384, MNF=512, MNB=576, MUI=640, MLI=704, RMASK=768,
          LSEL=1536, PSEL=1664, NMKF=1668, NMKB=1796, ONES=1924, ISTK=2052)
NCST = 2052 + 64


def make_consts():
    c = np.zeros((128, NCST), np.float32)
    c[:, 0:128] = np.eye(128)
    c[0:64, 128:192] = 1.0
    c[64:128, 192:256] = 1.0
    s = np.arange(64)[:, None]
    t = np.arange(64)[None, :]
    us, ui = (s < t).astype(np.float32), (s <= t).astype(np.float32)
    ls, li = (s > t).astype(np.float32), (s >= t).astype(np.float32)
    c[0:64, 256:320], c[0:64, 320:384] = us, ui
    c[0:64, 384:448], c[0:64, 448:512] = ls, li
    c[0:64, 512:576] = -ls
    c[0:64, 576:640] = -us
    c[0:64, 640:704] = ui
    c[0:64, 704:768] = li
    rm = np.ones(768, np.float32)
    rm[::64] = 0.0
    c[:, 768:1536] = rm[None, :]
    for k in range(6):
        c[k, 1536 + (k % 2) * 64: 1536 + (k % 2) * 64 + 64] = 1.0
        c[k, 1664 + k // 2] = 1.0
    c[0:64, 1668:1796] = -c[0:64, 256:384]
    c[0:64, 1796:1924] = -c[0:64, 384:512]
    c[:, 1924:2052] = 1.0
    for (a, b) in ((256, 768), (1668, 1924)):
        c[64:128, a:b] = c[0:64, a:b]
    c[0:64, 2052:2116] = np.eye(64)
    c[64:128, 2052:2116] = np.eye(64)
    return c


class Builder:
    def __init__(self, NP, TS, debug=False, stop=None):
        self.stop = stop
        self.NP, self.TS = NP, TS
        self.NTOK = NP * 256 + TS
        self.debug = debug
        nc = self.nc = bass.Bass("TRN2", target_bir_lowering=False)
        self.S = Sync(nc)
        self._decl_dram()
        self._alloc()

    def _decl_dram(self):
        nc, NP, TS = self.nc, self.NP, self.TS
        di = lambda n, s: nc.dram_tensor(n, list(s), F32, kind="ExternalInput").ap()
        do = lambda n, s: nc.dram_tensor(n, list(s), F32, kind="ExternalOutput").ap()
        dx = lambda n, s: nc.dram_tensor(n, list(s), F32, kind="Internal").ap()
        self.x_in = di("x_in", [self.NTOK, D])
        self.cc = di("cc", [128, 8, 2])
        self.w_mod = di("w_mod", [2, D, 3 * D])
        self.bmodT = di("bmodT", [128, 2, 16])
        self.bmodg = di("bmodg", [2, D])
        self.gpost = di("gpost", [2, D])
        self.w_in = di("w_in", [2, D, IN_COLS])
        self.w_out = di("w_out", [2, D, D])
        self.pp = di("pp", [128, 2 * PL])
        self.cst = di("cst", [128, NCST])
        self.lora = di("lora", [128, 2, 2, 384])
        self.lruw = di("lruw", [128, 2, 8, 128])
        self.st_mC = di("st_mC", [2, 2, 128, 3, 65])
        self.st_mm = di("st_mm", [6, 2, 2])
        self.st_rH = di("st_rH", [2, 2, 128, 3, 64])
        self.st_l = di("st_l", [128, 2, 2, 2])
        for n in ["x_in", "cc", "w_mod", "bmodT", "bmodg", "gpost", "w_in", "w_out", "pp", "cst", "lora",
                  "lruw", "st_mC", "st_mm", "st_rH", "st_l"]:
            self.S.untracked.add(n)
        self.y_out = do("y_out", [self.NTOK, D])
        self.o_mC = do("o_mC", [NP, 2, 2, 128, 3, 65])
        self.o_mm = do("o_mm", [NP, 2, 2, 6, 1])
        self.o_rH = do("o_rH", [NP, 2, 2, 128, 3, 64])
        self.o_l = do("o_l", [NP, 2, 2, 128, 2])
        self.x1 = dx("x1", [self.NTOK, D])
        self.sHB = dx("sHB", [max(TS, 64), 384])
        self.sYB = dx("sYB", [max(TS // 256, 1), 128, 768])
        self.sLB = dx("sLB", [128, 2, max(TS, 64)])
        nb = max(TS // 256, 1)
        dxt = lambda n, s_, dt: nc.dram_tensor(n, list(s_), dt, kind="Internal").ap()
        self.sc = {
            'QT': dxt("sc_QT", [nb, 128, 3, 256], BF16), 'KT': dxt("sc_KT", [nb, 128, 3, 256], BF16),
            'KTOK': dxt("sc_KTOK", [nb, 64, 4, 384], BF16), 'VAUG': dxt("sc_VAUG", [nb, 64, 4, 6, 65], BF16),
            'GOZ': dxt("sc_GOZ", [nb, 128, 3, 256], F32), 'GI': dxt("sc_GI", [nb, 6, 256], F32),
            'LF': dxt("sc_LF", [nb, 6, 256], F32), 'XC': dxt("sc_XC", [nb, 128, 2, 256], F32),
            'LZT': dxt("sc_LZT", [nb, 128, 2, 256], F32), 'BLK': dxt("sc_BLK", [nb, 128, 11, 256], F32),
            'RZT': dxt("sc_RZT", [nb, 128, 3, 256], F32), 'KH': dxt("sc_KH", [nb, 128, 3, 256], F32),
            'VSTK': dxt("sc_VSTK", [nb, 128, 4, 3, 64], BF16),
        }
        if self.debug:
            self.dbg = do("dbg", [128, 32768])
            self.dbg_map = {}
            self.dbg_off = 0

    def T(self, name, shape, dtype, arena):
        es = 2 if dtype == BF16 else 4
        nb = _prod(shape[1:]) * es
        off = arena.take(nb)
        t = self.nc.alloc_sbuf_tensor_at(name, list(shape), dtype, offset=off)
        self.S.sb_addr[t.name] = (off, es)
        return t

    def _alloc(self):
        nc = self.nc
        B0 = 16384 + 256
        LIM = 224 * 1024 - 256
        P = Arena(B0, LIM - B0)
        T = self.T
        self.W_IN = T("W_IN", [128, 8, IN_COLS], BF16, P)
        self.W_OUT = T("W_OUT", [128, 8, D], BF16, P)
        self.CST = T("CST", [128, NCST], F32, P)
        self.PP = T("PP", [128, 2 * PL], F32, P)
        self.DV = T("DV", [128, 2 * DL], F32, P)
        self.LORA = T("LORA", [128, 2, 2, 384], F32, P)
        self.LRUW = T("LRUW", [128, 2, 8, 128], F32, P)
        self.RKD = T("RKD", [128, 2, 3, 128], F32, P)
        self.GS = T("GS", [128, 2, 8], F32, P)
        self.SH = T("SH", [128, 2, 8], F32, P)
        self.GATEB = T("GATEB", [128, 2, D], F32, P)
        self.IDB = T("IDB", [128, 128], BF16, P)
        self.ISTKB = T("ISTKB", [128, 64], BF16, P)
        self.mC = [T("mC%d" % d, [128, 3, 65], F32, P) for d in range(2)]
        self.mM = [T("mM%d" % d, [6, 1], F32, P) for d in range(2)]
        self.rH = [T("rH%d" % d, [128, 3, 64], F32, P) for d in range(2)]
        self.lS = [T("lS%d" % d, [128, 2], F32, P) for d in range(2)]
        XB = Arena(P.take(16384), 16384)
        self.HT = T("HT", [128, 8, 384], BF16, P)
        self.MIXT = T("MIXT", [128, 8, 256], BF16, P)
        self.BLK = T("BLK", [128, 11, 256], F32, P)
        abase = P.take(0)
        asize = P.size - P.ptr
        self.asize = asize
        mk = lambda: Arena(abase, asize)
        a = Arena(XB.base, XB.size)
        self.XW = [T("XW%d" % i, [128, D], F32, a) for i in range(2)]
        self.XN = [T("XN%d" % i, [128, D], F32, a) for i in range(2)]
        a = Arena(XB.base, XB.size)
        self.YB = T("YB", [128, 4, 3, 64], F32, a)
        self.RZT = T("RZT", [128, 3, 256], F32, a)
        self.WSTG = [T("WSTG0", [128, 8, 256], F32, Arena(self.S.sb_addr[self.BLK.name][0], 11264)),
                     T("WSTG1", [128, 8, 256], F32, Arena(XB.base, 8192)),
                     T("WSTG2", [128, 8, 256], F32, Arena(XB.base + 8192, 8192))]
        a = mk()
        self.WMs = [T("WM%d" % i, [128, 8, 512], F32, a) for i in range(2)]
        self.SCT = T("SCT", [128, 8, 2], F32, a)
        self.SCB = T("SCB", [128, 2, 8, 128], F32, a)
        self.MODT = T("MODT", [128, 16, 2], F32, a)
        self.BMT = T("BMT", [128, 2, 16], F32, a)
        awm = Arena(self.S.sb_addr[self.WMs[0].name][0], 16384)
        self.BG = T("BG", [128, D], F32, awm)
        self.GP = T("GP", [128, D], F32, awm)
        self.TMPS = T("TMPS", [128, 32], F32, a)
        self.MODROW = T("MODROW", [2, 512], F32, a)
        a = mk()
        self.QT = T("QT", [128, 3, 256], BF16, a)
        self.KT = T("KT", [128, 3, 256], BF16, a)
        self.GOZ = T("GOZ", [128, 3, 256], F32, a)
        self.KTOK = T("KTOK", [64, 4, 384], BF16, a)
        self.VAUG = T("VAUG", [64, 4, 6, 65], BF16, a)
        self.HB = T("HB", [64, 4, 384], F32, a)
        self.mg = []
        for d in range(2):
            g = {}
            for n in ["gi", "lf", "pre", "bb", "gg", "ee", "fl"]:
                g[n] = T("mg_%s%d" % (n, d), [6, 256], F32, a)
            for n in ["mx", "mch", "mprev", "MM", "dec"]:
                g[n] = T("mg_%s%d" % (n, d), [6, 4], F32, a)
            g["X2"] = T("mg_X2%d" % d, [6, 4, 3], F32, a)
            g["etok"] = T("mg_etok%d" % d, [64, 4, 6], F32, a)
            g["fltok"] = T("mg_fltok%d" % d, [64, 4, 6], F32, a)
            g["decb"] = T("mg_decb%d" % d, [128, 4, 3], F32, a)
            self.mg.append(g)
        self.STSB = [T("STSB%d" % i, [64, 6, 64], BF16, a) for i in range(2)]
        self.VP = [T("VP%d" % i, [64, 6, 65], BF16, a) for i in range(2)]
        self.CDEC = T("CDEC", [128, 3, 65], F32, a)
        self.CDBF = T("CDBF", [128, 3, 65], BF16, a)
        self.DN = T("DN", [64, 6], F32, a)
        self.RDN = T("RDN", [64, 6], F32, a)
        self.HD = T("HD", [64, 6, 64], F32, a)
        self.SQ = T("SQ", [64, 4, 384], F32, a)
        self.SSQ = T("SSQ", [64, 24], F32, a)
        self.RSTD = T("RSTD", [64, 24], F32, a)
        self.ZT = [T("ZT%d" % i, [128, 256], F32, a) for i in range(2)]
        a = mk()
        self.XL = T("XL", [128, 2, 392], F32, a)
        self.XC = T("XC", [128, 2, 256], F32, a)
        self.LZT = T("LZT", [128, 2, 256], F32, a)
        self.ltd = [{n: T("lt%d_%s" % (d_, n), [128, 2, 256], F32, a) for n in ["rg", "ig", "aa", "a2", "bt"]} for d_ in range(2)]
        self.HL = [T("HL%d" % d, [128, 2, 256], F32, a) for d in range(2)]
        a = mk()
        self.URS = T("URS", [128, 11, 384], F32, a)
        a = mk()
        self.KH = T("KH", [128, 3, 256], F32, a)
        R1 = a.take(7 * 3072)
        a1 = Arena(R1, 7 * 3072)
        self.rt = {n: T("rt_" + n, [128, 3, 256], F32, a1) for n in ["sg", "aa", "kt", "bb", "cs", "d", "E"]}
        a2 = Arena(R1, 7 * 3072)
        self.A1BD = T("A1BD", [128, 3, 2, 128], BF16, a2)
        self.A2BD = T("A2BD", [128, 3, 2, 128], BF16, a2)
        self.NN = [T("NN%d" % i, [128, 3, 128], BF16, a2) for i in range(2)]
        self.NTT = [T("NTT%d" % i, [128, 3, 128], BF16, a2) for i in range(2)]
        self.U = T("U", [128, 3, 64], F32, a2)
        self.UBF = T("UBF", [128, 3, 64], BF16, a2)
        self.HBF = T("HBF", [128, 3, 64], BF16, a2)
        self.HTMP = T("HTMP", [128, 3, 64], F32, a2)
        self.YBD = T("YBD", [128, 3, 128], F32, a2)
        self.KTTOK = T("KTTOK", [128, 4, 3, 128], BF16, a2)
        self.BTTOK = T("BTTOK", [128, 4, 3, 128], BF16, a2)
        self.rSSQ = T("rSSQ", [128, 12], F32, a2)
        self.rRSTD = T("rRSTD", [128, 12], F32, a2)
        self.KR = T("KR", [128, 3, 4, 2, 64], BF16, a)
        self.BTT = T("BTT", [128, 3, 256], BF16, a)
        self.KTT = T("KTT", [128, 3, 256], BF16, a)
        bdr = a.take(4 * 3072)
        a3 = Arena(bdr, 4 * 3072)
        self.BD = {n: T("BD_" + n, [128, 3, 4, 128], BF16, a3) for n in ["kt", "b", "kh", "r"]}
        a3 = Arena(bdr, 4 * 3072)
        self.ft = {n: T("ft_" + n, [128, 3, 256], F32, a3) for n in ["rk", "bon", "t1"]}
        self.YSQ = T("YSQ", [128, 4, 3, 64], F32, a3)
        self.BDV = T("BDV", [128, 3, 4, 128], BF16, a)
        self.VSTK = T("VSTK", [128, 4, 3, 64], BF16, a)
        self.TWL = T("TWL", [128, 256], F32, a)
        self.GL = T("GL", [128, 3, 4], F32, a)
        self.PA = nc.alloc_psum_tensor("PA", [128, 1024], F32)
        self.PB = nc.alloc_psum_tensor("PB", [128, 1024], F32)
        self.PQ = [nc.alloc_psum_tensor("PQ%d" % i, [128, 512], F32) for i in range(3)]
        self.PTB = nc.alloc_psum_tensor("PTB", [128, 1024], BF16)
        self.pq_i = 0

    def pq(self):
        t = self.PQ[self.pq_i % 3]
        self.pq_i += 1
        return t

    def V(self, t, p0, npart, off, dims):
        pstep = _prod(list(t.shape)[1:])
        return bass.AP(t, p0 * pstep + off, [[pstep, npart]] + [list(d) for d in dims])

    def tt(self, e, out, in0, in1, op):
        return self.S.emit(e, [in0, in1], [out], lambda g: g.tensor_tensor(out=out, in0=in0, in1=in1, op=op))

    def ts(self, e, out, in0, s1, s2, op0, op1=None):
        rd = [in0] + [s for s in (s1, s2) if not isinstance(s, (int, float)) and s is not None]
        if op1 is None:
            return self.S.emit(e, rd, [out], lambda g: g.tensor_scalar(out=out, in0=in0, scalar1=s1, scalar2=None, op0=op0))
        return self.S.emit(e, rd, [out], lambda g: g.tensor_scalar(out=out, in0=in0, scalar1=s1, scalar2=s2, op0=op0, op1=op1))

    def stt(self, out, in0, sc, in1, op0, op1):
        rd = [in0, in1] + ([] if isinstance(sc, (int, float)) else [sc])
        return self.S.emit('dve', rd, [out], lambda g: g.scalar_tensor_tensor(out=out, in0=in0, scalar=sc, in1=in1, op0=op0, op1=op1))

    def act(self, out, in_, func, bias=None, scale=None, accum=None):
        rd = [in_]
        kw = {}
        if bias is not None:
            kw['bias'] = bias
            if not isinstance(bias, (int, float)):
                rd.append(bias)
        if scale is not None:
            kw['scale'] = scale
            if not isinstance(scale, (int, float)):
                rd.append(scale)
        wr = [out]
        if accum is not None:
            kw['accum_out'] = accum
            wr.append(accum)
        return self.S.emit('act', rd, wr, lambda g: g.activation(out=out, in_=in_, func=func, **kw))

    def cp(self, e, out, in_):
        if e == 'act':
            return self.act(out, in_, AF.Copy)
        return self.S.emit(e, [in_], [out], lambda g: g.tensor_copy(out=out, in_=in_))

    def _pe_rowtile_guard(self, lhsT, out):
        S = self.S
        st = S.region(lhsT)
        k = st[2] - st[1]
        kr = 32 if k <= 32 else (64 if k <= 64 else 128)
        rows = (st[1], st[1] + kr)
        oreg = S.region(out)
        last = getattr(self, '_last_pe', None)
        if last is not None:
            lrows, loreg, lins, linc = last
            disjoint = rows[1] <= lrows[0] or lrows[1] <= rows[0]
            samebank = (loreg[0] == oreg[0]) and loreg[3] < oreg[4] and oreg[3] < loreg[4]
            if disjoint and samebank:
                if not linc:
                    S.cnt['pe'] += 1
                    lins.then_inc(S.sem['pe'], 1)
                S.wait('pe', S.sem['pe'], S.cnt['pe'])
        return rows, oreg

    def mm(self, out, lhsT, rhs, start=True, stop=True, inc=None):
        if inc is None:
            inc = stop
        rows, oreg = self._pe_rowtile_guard(lhsT, out)
        ins = self.S.emit('pe', [lhsT, rhs], [out],
                          lambda g: g.matmul(out, lhsT=lhsT, rhs=rhs, start=start, stop=stop), inc=inc)
        self._last_pe = (rows, oreg, ins, inc)
        return ins

    def tr(self, out, in_, inc=True, bf=False):
        n = in_.shape[0]
        ident = self.IDB[0:n, 0:n] if bf else self.CST[0:n, CC['IDENT']:CC['IDENT'] + n]
        rows, oreg = self._pe_rowtile_guard(in_, out)
        ins = self.S.emit('pe', [in_, ident], [out],
                          lambda g: g.transpose(out=out, in_=in_, identity=ident), inc=inc)
        self._last_pe = (rows, oreg, ins, inc)
        return ins

    def memset(self, e, ap, v):
        return self.S.emit(e, [], [ap], lambda g: g.memset(ap, v))

    def scan(self, out, d0, d1, init, op0, op1):
        rd = [d0, d1] + ([] if isinstance(init, (int, float)) else [init])
        return self.S.emit('dve', rd, [out], lambda g: g.tensor_tensor_scan(out=out, data0=d0, data1=d1, initial=init, op0=op0, op1=op1))

    def recip(self, out, in_):
        return self.S.emit('dve', [in_], [out], lambda g: g.reciprocal(out=out, in_=in_))

    def reduce(self, out, in_, op, axis=AX.X):
        return self.S.emit('dve', [in_], [out], lambda g: g.tensor_reduce(out=out, in_=in_, axis=axis, op=op))

    def dump(self, name, ap):
        if not self.debug or name in self.dbg_map:
            return
        if getattr(self, 'dbg_filter', None) is not None and not any(name.startswith(p) for p in self.dbg_filter):
            return
        shp = list(ap.shape)
        npart, nfree = shp[0], _prod(shp[1:])
        stage = self.XN[1]
        assert nfree <= 1024
        dst = self.V(stage, 0, npart, 0, [[_prod(shp[i + 1:]), shp[i]] for i in range(1, len(shp))])
        self.cp('dve', dst, ap)
        self.S.dma('sp', self.dbg[0:npart, self.dbg_off:self.dbg_off + nfree], stage[0:npart, 0:nfree], allow_slow_non_contiguous=True)
        self.dbg_map[name] = (self.dbg_off, npart, shp[1:])
        self.dbg_off += nfree

    def ppc(self, l, key, j=0, rows=128):
        c = l * PL + PC[key] + j
        return self.PP[0:rows, c:c + 1]

    def dvc(self, l, key, j=0, rows=128):
        c = l * DL + DC[key] + j
        return self.DV[0:rows, c:c + 1]

    def setup(self):
        S = self.S
        S.dma('sp', self.CST[:, :], self.cst[:, :])
        S.dma('sp', self.PP[:, :], self.pp[:, :])
        S.dma('sp', self.LORA[:, :, :, :], self.lora[:, :, :, :])
        S.dma('sp', self.LRUW[:, :, :, :], self.lruw[:, :, :, :])
        self.memset('dve', self.DV[:, :], 0.0)
        self.cp('dve', self.IDB[:, :], self.CST[:, CC['IDENT']:CC['IDENT'] + 128])
        self.cp('dve', self.ISTKB[:, :], self.CST[:, CC['ISTK']:CC['ISTK'] + 64])
        for l in range(2):
            self.ts('dve', self.DV[:, l * DL + DC['OMKA']: l * DL + DC['OMKA'] + 3],
                    self.PP[:, l * PL + PC['R_KA']: l * PL + PC['R_KA'] + 3], -1.0, 1.0, ALU.mult, ALU.add)
            self.ts('dve', self.DV[:, l * DL + DC['OMMU']: l * DL + DC['OMMU'] + 11],
                    self.PP[:, l * PL + PC['R_MU']: l * PL + PC['R_MU'] + 11], -1.0, 1.0, ALU.mult, ALU.add)
            lam = self.PP[:, l * PL + PC['L_LAM']: l * PL + PC['L_LAM'] + 4]
            t0 = self.TMPS[:, 0:4]
            self.act(t0, lam, AF.Exp, scale=-1.0)
            self.act(t0, t0, AF.Ln, bias=1.0)
            self.ts('dve', self.DV[:, l * DL + DC['CLAM']: l * DL + DC['CLAM'] + 4], t0, -8.0, None, ALU.mult)
            self.ts('dve', self.DV[:, l * DL + DC['C2LAM']: l * DL + DC['C2LAM'] + 4], t0, -16.0, None, ALU.mult)
            self.ts('dve', self.DV[0:6, l * DL + DC['NBF']: l * DL + DC['NBF'] + 2],
                    self.PP[0:6, l * PL + PC['M_BF']: l * PL + PC['M_BF'] + 2], -1.0, None, ALU.mult)
            for hp in range(3):
                self.ts('dve', self.RKD[:, l, hp, :], self.CST[:, CC['BONES']:CC['BONES'] + 128],
                        self.ppc(l, 'R_RK', hp), None, ALU.mult)
        self.memset('dve', self.XL[:, :, 0:2], 0.0)

    def load_layer(self, l):
        S = self.S
        wsrc = self.w_in[l].rearrange("(kc p) c -> p kc c", p=128)
        wo = self.w_out[l].rearrange("(kc p) c -> p kc c", p=128)
        pieces = [(self.W_IN, wsrc, c0, min(256, IN_COLS - c0)) for c0 in range(0, IN_COLS, 256)]
        pieces += [(self.W_OUT, wo, c0, 256) for c0 in range(0, D, 256)]
        def emit_pieces(lo, hi):
            for i in range(lo, min(hi, len(pieces))):
                dst, src, c0, n = pieces[i]
                stg = self.WSTG[i % 3]
                S.dma('sp', stg[:, :, 0:n], src[:, :, c0:c0 + n])
                self.cp('pool', dst[:, :, c0:c0 + n], stg[:, :, 0:n])
        if self.stop == 'load_w':
            raise StopIteration
        S.dma('sp', self.SCT[:, :, :], self.cc[:, :, :])
        S.dma('sp', self.BMT[:, :, :], self.bmodT[:, :, :])
        self.act(self.SCT[:, :, :], self.SCT[:, :, :], AF.Silu)
        for m in range(2):
            for kc in range(8):
                src = self.V(self.SCT, 0, 128, kc * 2 + m, [[0, 128]])
                self.cp('dve', self.SCB[:, m, kc, :], src)
        wm = self.w_mod[l].rearrange("(kc p) c -> p kc c", p=128)
        for blk in range(6):
            self.WM = self.WMs[blk % 2]
            S.dma('sp', self.WM[:, :, :], wm[:, :, blk * 512:(blk + 1) * 512])
            emit_pieces(blk * 4, blk * 4 + 4)
            if blk < 4:
                ps = self.pq()
                for kc in range(8):
                    self.mm(ps[0:2, :], self.SCT[:, kc, :], self.WM[:, kc, :], start=(kc == 0), stop=(kc == 7))
                self.cp('act', self.MODROW[0:2, :], ps[0:2, :])
                ps2 = self.pq()
                for j in range(4):
                    self.tr(ps2[:, j * 2:j * 2 + 2], self.MODROW[0:2, j * 128:(j + 1) * 128])
                o = self.MODT[:, blk * 4:(blk + 1) * 4, :]
                bsrc = self.V(self.BMT, 0, 128, l * 16 + blk * 4, [[1, 4], [0, 2]])
                self.tt('dve', o, self.V(ps2, 0, 128, 0, [[2, 4], [1, 2]]), bsrc, ALU.add)
            else:
                half = blk - 4
                for m in range(2):
                    ps = self.pq()
                    for kc in range(8):
                        self.mm(ps[:, :], self.SCB[:, m, kc, :], self.WM[:, kc, :], start=(kc == 0), stop=(kc == 7))
                    self.cp('act', self.GATEB[:, m, half * 512:(half + 1) * 512], ps[:, :])
        if self.stop == 'load_m':
            raise StopIteration
        for m in range(2):
            self.cp('dve', self.SH[:, m, :], self.V(self.MODT, 0, 128, m, [[2, 8]]))
            t0 = self.TMPS[:, 8:16]
            self.ts('dve', t0, self.V(self.MODT, 0, 128, 16 + m, [[2, 8]]), 1.0, None, ALU.add)
            self.tt('dve', self.GS[:, m, :], t0, self.PP[:, l * PL + PC['G_PRE']: l * PL + PC['G_PRE'] + 8], ALU.mult)
        if self.stop == 'load_g':
            raise StopIteration
        S.dma('sp', self.BG[:, :], bass.AP(self.bmodg.tensor, l * D, [[0, 128], [1, D]]))
        S.dma('sp', self.GP[:, :], bass.AP(self.gpost.tensor, l * D, [[0, 128], [1, D]]))
        if self.stop == 'load_b':
            raise StopIteration
        for m in range(2):
            self.tt('dve', self.GATEB[:, m, :], self.GATEB[:, m, :], self.BG[:, :], ALU.add)
            self.tt('dve', self.GATEB[:, m, :], self.GATEB[:, m, :], self.GP[:, :], ALU.mult)

    def proj_fm(self, c0, ncols, t_off, ntok, evac):
        ps = self.pq()
        for kc in range(8):
            self.mm(ps[0:ncols, 0:ntok], self.W_IN[:, kc, c0:c0 + ncols], self.HT[:, kc, t_off:t_off + ntok],
                    start=(kc == 0), stop=(kc == 7))
        evac(ps[0:ncols, 0:ntok])

    def proj_tm(self, c0, ncols, t_off, evac):
        ps = self.pq()
        for kc in range(8):
            self.mm(ps[0:64, 0:ncols], self.HT[:, kc, t_off:t_off + 64], self.W_IN[:, kc, c0:c0 + ncols],
                    start=(kc == 0), stop=(kc == 7))
        evac(ps[0:64, 0:ncols])

    def visit(self, l, mod, xsrc, xdst, row0, T, t0, dirs, grid, seq_idx, first, last, mode='full'):
        S = self.S
        w0 = max(0, t0 - 64)
        w1 = min(T, t0 + TB + 64)
        W = w1 - w0
        co = t0 - w0
        do_f = 0 in dirs
        prompt = (mod == 0)
        if mode != 'load':
            ntile = (W + 127) // 128
            for i in range(ntile):
                n = min(128, W - i * 128)
                xw, xn = self.XW[i % 2], self.XN[i % 2]
                r = row0 + w0 + i * 128
                S.dma('sp', xw[0:n, :], xsrc[r:r + n, :])
                ssq = self.TMPS[0:n, 16 + i:17 + i]
                self.act(xn[0:n, :], xw[0:n, :], AF.Square, accum=ssq)
                if self.stop == 'n1':
                    raise StopIteration
                rs = self.TMPS[0:n, 20 + i:21 + i]
                self.ts('dve', rs, ssq, 1.0 / D, EPS, ALU.mult, ALU.add)
                self.act(rs, rs, AF.Sqrt)
                self.recip(rs, rs)
                if self.stop == 'n2':
                    raise StopIteration
                self.act(xn[0:n, :], xw[0:n, :], AF.Copy, scale=rs)
                if self.stop == 'n3':
                    raise StopIteration
                for half in range(2):
                    ps = self.pq()
                    for j in range(4):
                        kc = half * 4 + j
                        self.tr(ps[:, j * 128:j * 128 + n], xn[0:n, kc * 128:(kc + 1) * 128])
                    if self.stop == 'n4':
                        raise StopIteration
                    for j in range(4):
                        kc = half * 4 + j
                        o = self.HT[:, kc, i * 128:i * 128 + n]
                        if half == 0:
                            self.ts('dve', o, ps[:, j * 128:j * 128 + n], self.GS[:, mod, kc:kc + 1], self.SH[:, mod, kc:kc + 1], ALU.mult, ALU.add)
                        else:
                            self.act(o, ps[:, j * 128:j * 128 + n], AF.Identity, bias=self.SH[:, mod, kc:kc + 1], scale=self.GS[:, mod, kc:kc + 1])
        if self.stop in ('norm', 'n5a', 'n5d'):
            raise StopIteration
        self.stage_mlstm(l, t0, co, dirs, prompt, seq_idx, first, last, mode)
        if self.stop == 'mlstm':
            raise StopIteration
        self.stage_lru(l, t0, co, W, w0, w1, T, dirs, prompt, seq_idx, first, last, mode)
        if self.stop == 'lru':
            raise StopIteration
        self.stage_rwkv(l, t0, co, W, w0, w1, T, dirs, grid, prompt, seq_idx, first, last, mode)
        for kc in range(8):
            self.dump('mix%d' % kc, self.MIXT[:, kc, :])
        if self.stop == 'rwkv':
            raise StopIteration
        if do_f:
            self.stage_out(l, mod, xsrc, xdst, row0 + t0)
        if self.stop == 'out':
            raise StopIteration

    def stage_mlstm(self, l, t0, co, dirs, prompt, seq_idx, first, last, mode='full'):
        S = self.S
        do_f = 0 in dirs
        if mode != 'load':
            for d in (sorted(set(dirs) | {0}) if mode == 'store' else dirs):
                g = self.mg[d]
                self.proj_fm(1920 + d * 6, 6, co, 256, lambda ps, g=g, d=d: self.act(g["gi"][:, :], ps, AF.Identity, bias=self.ppc(l, 'M_BI', d, 6)))
                def ev_f(ps, g=g, d=d):
                    self.act(g["lf"][:, :], ps, AF.Exp, bias=self.dvc(l, 'NBF', d, 6), scale=-1.0)
                    self.act(g["lf"][:, :], g["lf"][:, :], AF.Ln, bias=1.0)
                    self.ts('dve', g["lf"][:, :], g["lf"][:, :], -1.0, None, ALU.mult)
                self.proj_fm(1932 + d * 6, 6, co, 256, ev_f)
        bi = t0 // 256
        ml_items = [(self.QT[:, :, :], 'QT'), (self.KT[:, :, :], 'KT'), (self.KTOK[:, :, :], 'KTOK'), (self.VAUG[:, :, :, :], 'VAUG'),
                    (self.GOZ[:, :, :], 'GOZ'), (self.mg[0]["gi"][:, :], 'GI'), (self.mg[0]["lf"][:, :], 'LF')]
        if mode == 'load':
            for ap_, k_ in ml_items:
                S.dma('sp', ap_, self.sc[k_][bi])
        for d in dirs:
            if first[d]:
                if prompt:
                    self.memset('dve', self.mC[d][:, :, :], 0.0)
                    self.memset('dve', self.mM[d][:, :], 0.0)
                else:
                    S.dma('sp', self.mC[d][:, :, :], self.st_mC[l, d])
                    S.dma('sp', self.mM[d][:, :], self.st_mm[:, l, d:d + 1], allow_slow_non_contiguous=True)
        if do_f and not (1 in dirs):
            S.dma('sp', self.HB[:, :, :], self.sHB[t0:t0 + 256, :].rearrange("(c s) f -> s c f", s=64))
        for d in dirs:
            g = self.mg[d]
            v3 = lambda t: self.V(t, 0, 6, 0, [[64, 4], [1, 64]])
            self.scan(g["pre"][:, :], self.CST[0:6, CC['RMASK']:CC['RMASK'] + 256], g["lf"][:, :], 0.0, ALU.mult, ALU.add)
            bL = self.V(g["pre"], 0, 6, 63, [[64, 4]])
            bLb = self.V(g["pre"], 0, 6, 63, [[64, 4], [0, 64]])
            if d == 0:
                bsrc = g["pre"]
            else:
                self.tt('dve', v3(g["bb"]), bLb, v3(g["pre"]), ALU.subtract)
                self.tt('dve', g["bb"][:, :], g["bb"][:, :], g["lf"][:, :], ALU.add)
                bsrc = g["bb"]
            self.tt('dve', g["gg"][:, :], g["gi"][:, :], bsrc[:, :], ALU.subtract)
            self.reduce(g["mx"][:, :], v3(g["gg"]), ALU.max)
            if d == 0:
                mo, mxv, blv = g["mch"][:, :], g["mx"][:, :], bL
            else:
                mo = self.V(g["mch"], 0, 6, 3, [[-1, 4]])
                mxv = self.V(g["mx"], 0, 6, 3, [[-1, 4]])
                blv = self.V(g["pre"], 0, 6, 63 + 3 * 64, [[-64, 4]])
            self.scan(mo, mxv, blv, self.mM[d][:, 0:1], ALU.max, ALU.add)
            if d == 0:
                self.cp('dve', g["mprev"][:, 1:4], g["mch"][:, 0:3])
                self.cp('dve', g["mprev"][:, 0:1], self.mM[d][:, 0:1])
                mfin = g["mch"][:, 3:4]
            else:
                self.cp('dve', g["mprev"][:, 0:3], g["mch"][:, 1:4])
                self.cp('dve', g["mprev"][:, 3:4], self.mM[d][:, 0:1])
                mfin = g["mch"][:, 0:1]
            self.tt('dve', g["MM"][:, :], g["mprev"][:, :], g["mx"][:, :], ALU.max)
            self.tt('dve', g["dec"][:, :], g["mprev"][:, :], g["MM"][:, :], ALU.subtract)
            self.act(g["dec"][:, :], g["dec"][:, :], AF.Exp)
            self.cp('dve', self.mM[d][:, 0:1], mfin)
            MMb = self.V(g["MM"], 0, 6, 0, [[1, 4], [0, 64]])
            self.tt('dve', v3(g["ee"]), v3(g["gg"]), MMb, ALU.subtract)
            self.act(g["ee"][:, :], g["ee"][:, :], AF.Exp)
            self.tt('dve', v3(g["fl"]), v3(bsrc), MMb, ALU.add)
            self.act(g["fl"][:, :], g["fl"][:, :], AF.Exp, scale=-1.0)
        if mode != 'load':
            for hp in range(3):
                self.proj_fm(hp * 128, 128, co, 256, lambda ps, hp=hp: self.cp('act', self.QT[:, hp, :], ps))
                self.proj_fm(384 + hp * 128, 128, co, 256, lambda ps, hp=hp: self.act(self.KT[:, hp, :], ps, AF.Copy, scale=0.125))
            if do_f or mode == 'store':
                for hp in range(3):
                    def ev_o(ps, hp=hp):
                        self.act(self.GOZ[:, hp, :], ps, AF.Sigmoid)
                    self.proj_fm(1152 + hp * 128, 128, co, 256, ev_o)
                    def ev_z2(ps, hp=hp):
                        tz = self.ZT[hp % 2]
                        self.act(tz[:, :], ps, AF.Silu)
                        self.tt('dve', self.GOZ[:, hp, :], self.GOZ[:, hp, :], tz[:, :], ALU.mult)
                    self.proj_fm(1536 + hp * 128, 128, co, 256, ev_z2)
            for c in range(4):
                self.proj_tm(384, 384, co + c * 64, lambda ps, c=c: self.act(self.KTOK[:, c, :], ps, AF.Copy, scale=0.125))
                def ev_v(ps, c=c):
                    self.cp('dve', self.VAUG[:, c, :, 0:64], self.V(ps.tensor, 0, 64, 0, [[64, 6], [1, 64]]))
                self.proj_tm(768, 384, co + c * 64, ev_v)
            self.memset('dve', self.VAUG[:, :, :, 64:65], 1.0)
        for d in dirs:
            g = self.mg[d]
            ps = self.pq()
            for c in range(4):
                self.tr(ps[0:64, c * 6:c * 6 + 6], g["ee"][:, c * 64:(c + 1) * 64])
                self.tr(ps[0:64, 24 + c * 6:24 + c * 6 + 6], g["fl"][:, c * 64:(c + 1) * 64])
            self.cp('dve', g["etok"][:, :, :], self.V(ps, 0, 64, 0, [[6, 4], [1, 6]]))
            self.cp('dve', g["fltok"][:, :, :], self.V(ps, 0, 64, 24, [[6, 4], [1, 6]]))
            self.tt('dve', g["X2"][:, :, :], self.V(g["dec"], 0, 6, 0, [[1, 4], [0, 3]]),
                    self.V(self.CST, 0, 6, CC['PSEL'], [[0, 4], [1, 3]]), ALU.mult)
            ps2 = self.pq()
            self.mm(ps2[:, 0:12], self.CST[0:6, CC['LSEL']:CC['LSEL'] + 128], self.V(g["X2"], 0, 6, 0, [[1, 12]]))
            self.cp('dve', g["decb"][:, :, :], self.V(ps2, 0, 128, 0, [[3, 4], [1, 3]]))
        if mode == 'store':
            for ap_, k_ in ml_items:
                S.dma('sp', self.sc[k_][bi], ap_)
        for d in sorted(dirs, reverse=True):
            g = self.mg[d]
            mask = self.CST[0:64, CC['MUI']:CC['MUI'] + 64] if d == 0 else self.CST[0:64, CC['MLI']:CC['MLI'] + 64]
            maskb = self.V(self.CST, 0, 64, CC['MUI'] if d == 0 else CC['MLI'], [[0, 6], [1, 64]])
            for j in range(4):
                c = j if d == 0 else 3 - j
                cs = slice(c * 64, (c + 1) * 64)
                stsb, vp = self.STSB[j % 2], self.VP[j % 2]
                ps = self.pq()
                for h in (0, 2, 4, 1, 3, 5):
                    hp, pb = h // 2, 64 * (h % 2)
                    self.mm(ps[0:64, h * 64:(h + 1) * 64], self.KT[pb:pb + 64, hp, cs], self.QT[pb:pb + 64, hp, cs], inc=(h == 5))
                self.tt('dve', stsb[:, :, :], self.V(ps, 0, 64, 0, [[64, 6], [1, 64]]), maskb, ALU.mult)
                self.tt('dve', vp[:, :, :], self.VAUG[:, c, :, :], self.V(g["etok"], 0, 64, c * 6, [[1, 6], [0, 65]]), ALU.mult)
                self.tt('dve', self.CDBF[:, :, :], self.mC[d][:, :, :], self.V(g["decb"], 0, 128, c * 3, [[1, 3], [0, 65]]), ALU.mult)
                self.tt('dve', self.CDEC[:, :, :], self.mC[d][:, :, :], self.V(g["decb"], 0, 128, c * 3, [[1, 3], [0, 65]]), ALU.mult)
                ph = self.pq()
                for h in (0, 2, 4, 1, 3, 5):
                    hp, pb = h // 2, 64 * (h % 2)
                    o = ph[0:64, h * 65:(h + 1) * 65]
                    self.mm(o, stsb[:, h, :], vp[:, h, :], start=True, stop=False)
                    self.mm(o, self.QT[pb:pb + 64, hp, cs], self.CDBF[pb:pb + 64, hp, :], start=False, stop=True, inc=(h == 5))
                pc = self.pq()
                for h in range(6):
                    hp, pb = h // 2, 64 * (h % 2)
                    self.mm(pc[pb:pb + 64, hp * 65:(hp + 1) * 65], self.KTOK[:, c, h * 64:(h + 1) * 64], vp[:, h, :], inc=(h == 5))
                self.tt('dve', self.mC[d][:, :, :], self.CDEC[:, :, :], self.V(pc, 0, 128, 0, [[65, 3], [1, 65]]), ALU.add)
                self.act(self.DN[:, :], self.V(ph, 0, 64, 64, [[65, 6]]), AF.Abs)
                self.tt('dve', self.DN[:, :], self.DN[:, :], g["fltok"][:, c, :], ALU.max)
                self.recip(self.RDN[:, :], self.DN[:, :])
                hsrc = self.V(ph, 0, 64, 0, [[65, 6], [1, 64]])
                rb = self.V(self.RDN, 0, 64, 0, [[1, 6], [0, 64]])
                hbv = self.V(self.HB, 0, 64, c * 384, [[64, 6], [1, 64]])
                if d == 1:
                    self.tt('dve', hbv, hsrc, rb, ALU.mult)
                else:
                    self.tt('dve', self.HD[:, :, :], hsrc, rb, ALU.mult)
                    self.tt('dve', hbv, hbv, self.HD[:, :, :], ALU.add)
            if last[d] and prompt:
                S.dma('sp', self.o_mC[seq_idx, l, d], self.mC[d][:, :, :])
                S.dma('sp', self.o_mm[seq_idx, l, d], self.mM[d][:, 0:1])
        if not do_f:
            S.dma('sp', self.sHB[t0:t0 + 256, :].rearrange("(c s) f -> s c f", s=64), self.HB[:, :, :])
            return
        self.tt('dve', self.SQ[:, :, :], self.HB[:, :, :], self.HB[:, :, :], ALU.mult)
        self.reduce(self.SSQ[:, :], self.V(self.SQ, 0, 64, 0, [[64, 24], [1, 64]]), ALU.add)
        self.ts('dve', self.SSQ[:, :], self.SSQ[:, :], 1.0 / 64, EPS, ALU.mult, ALU.add)
        self.act(self.SSQ[:, :], self.SSQ[:, :], AF.Sqrt)
        self.recip(self.RSTD[:, :], self.SSQ[:, :])
        self.tt('dve', self.V(self.SQ, 0, 64, 0, [[64, 24], [1, 64]]), self.V(self.HB, 0, 64, 0, [[64, 24], [1, 64]]),
                self.V(self.RSTD, 0, 64, 0, [[1, 24], [0, 64]]), ALU.mult)
        for hp in range(3):
            ps = self.pq()
            for c in range(4):
                self.tr(ps[:, c * 64:(c + 1) * 64], self.SQ[:, c, hp * 128:(hp + 1) * 128], inc=(c == 3))
            self.stt(self.MIXT[:, hp, :], ps[:, 0:256], self.ppc(l, 'M_NORM', hp), self.GOZ[:, hp, :], ALU.mult, ALU.mult)

    def stage_lru(self, l, t0, co, W, w0, w1, T, dirs, prompt, seq_idx, first, last, mode='full'):
        S = self.S
        do_f = 0 in dirs
        if mode != 'load':
            for pr in range(2):
                self.proj_fm(3736 + pr * 128, 128, 0, W, lambda ps, pr=pr: self.cp('act', self.XL[:, pr, 2:2 + W], ps))
                if do_f or mode == 'store':
                    self.proj_fm(3992 + pr * 128, 128, co, 256, lambda ps, pr=pr: self.act(self.LZT[:, pr, :], ps, AF.Silu))
            if w1 == T:
                self.memset('dve', self.XL[:, :, 2 + W:2 + W + 1], 0.0)
            if w0 == 0:
                self.memset('dve', self.XL[:, :, 0:2], 0.0)
            for pr in range(2):
                self.ts('dve', self.XC[:, pr, :], self.XL[:, pr, co:co + 256], self.ppc(l, 'L_CONV', 0 * 2 + pr), self.ppc(l, 'L_CONVB', pr), ALU.mult, ALU.add)
                for j in range(1, 4):
                    self.stt(self.XC[:, pr, :], self.XL[:, pr, co + j:co + j + 256], self.ppc(l, 'L_CONV', j * 2 + pr), self.XC[:, pr, :], ALU.mult, ALU.add)
        bi = t0 // 256
        lr_items = [(self.XC[:, :, :], 'XC'), (self.LZT[:, :, :], 'LZT')]
        if mode == 'store':
            for ap_, k_ in lr_items:
                S.dma('sp', self.sc[k_][bi], ap_)
        if mode == 'load':
            for ap_, k_ in lr_items:
                S.dma('sp', ap_, self.sc[k_][bi])
        for d in dirs:
            if first[d]:
                if prompt:
                    self.memset('dve', self.lS[d][:, :], 0.0)
                else:
                    S.dma('sp', self.lS[d][:, :], self.st_l[:, l, d, :])
        if do_f and not (1 in dirs):
            S.dma('sp', self.HL[1][:, :, :], self.sLB[:, :, t0:t0 + 256])
        combos = [(d, pr) for d in sorted(dirs, reverse=True) for pr in range(2)]
        LT = lambda d, n, pr: self.ltd[d][n][:, pr, :]
        pss = {}
        for i, (d, pr) in enumerate(combos):
            ps = self.PA if i % 2 == 0 else self.PB
            off = (i // 2) * 512
            pss[(d, pr)] = (ps, off)
            self.mm(ps[:, off:off + 256], self.LRUW[:, l, (0 * 2 + d) * 2 + pr, :], self.XC[:, pr, :])
            self.mm(ps[:, off + 256:off + 512], self.LRUW[:, l, (1 * 2 + d) * 2 + pr, :], self.XC[:, pr, :])
        for (d, pr) in combos:
            ps, off = pss[(d, pr)]
            self.act(LT(d, "rg", pr), ps[:, off:off + 256], AF.Sigmoid, bias=self.ppc(l, 'L_BA', d * 2 + pr))
            self.act(LT(d, "ig", pr), ps[:, off + 256:off + 512], AF.Sigmoid, bias=self.ppc(l, 'L_BX', d * 2 + pr))
        for (d, pr) in combos:
            self.act(LT(d, "aa", pr), LT(d, "rg", pr), AF.Exp, scale=self.dvc(l, 'CLAM', d * 2 + pr))
            self.act(LT(d, "a2", pr), LT(d, "rg", pr), AF.Exp, scale=self.dvc(l, 'C2LAM', d * 2 + pr))
        for (d, pr) in combos:
            self.ts('dve', LT(d, "a2", pr), LT(d, "a2", pr), -1.0, 1.0, ALU.mult, ALU.add)
            self.tt('dve', LT(d, "bt", pr), LT(d, "ig", pr), self.XC[:, pr, :], ALU.mult)
        for (d, pr) in combos:
            self.act(LT(d, "a2", pr), LT(d, "a2", pr), AF.Sqrt)
        for (d, pr) in combos:
            self.tt('dve', LT(d, "bt", pr), LT(d, "bt", pr), LT(d, "a2", pr), ALU.mult)
        for (d, pr) in combos:
            if d == 0:
                self.scan(self.HL[0][:, pr, :], LT(0, "aa", pr), LT(0, "bt", pr), self.lS[0][:, pr:pr + 1], ALU.mult, ALU.add)
                self.cp('act', self.lS[0][:, pr:pr + 1], self.HL[0][:, pr, 255:256])
            else:
                rv = lambda t: self.V(t, 0, 128, pr * 256 + 255, [[-1, 256]])
                self.scan(rv(self.HL[1]), rv(self.ltd[1]["aa"]), rv(self.ltd[1]["bt"]), self.lS[1][:, pr:pr + 1], ALU.mult, ALU.add)
                self.cp('act', self.lS[1][:, pr:pr + 1], self.HL[1][:, pr, 0:1])
        for d in sorted(dirs, reverse=True):
            if last[d] and prompt:
                S.dma('sp', self.o_l[seq_idx, l, d], self.lS[d][:, :])
        if not do_f:
            S.dma('sp', self.sLB[:, :, t0:t0 + 256], self.HL[1][:, :, :])
            return
        self.tt('dve', self.HL[0][:, :, :], self.HL[0][:, :, :], self.HL[1][:, :, :], ALU.add)
        self.tt('dve', self.MIXT[:, 6:8, :], self.HL[0][:, :, :], self.LZT[:, :, :], ALU.mult)

    def stage_rwkv(self, l, t0, co, W, w0, w1, T, dirs, grid, prompt, seq_idx, first, last, mode='full'):
        S = self.S
        do_f = 0 in dirs
        if mode != 'load':
            for ch in range(11):
                self.proj_fm(1944 + ch * 128, 128, 0, W, lambda ps, ch=ch: self.cp('act' if ch % 2 else 'dve', self.URS[:, ch, 0:W], ps))
            if do_f or mode == 'store':
                for hp in range(3):
                    self.proj_fm(3352 + hp * 128, 128, co, 256, lambda ps, hp=hp: self.act(self.RZT[:, hp, :], ps, AF.Silu))
            U3 = lambda off, n: self.V(self.URS, 0, 128, off, [[384, 11], [1, n]])
            B3 = lambda off, n: self.V(self.BLK, 0, 128, off, [[256, 11], [1, n]])
            if not grid:
                self.cp('dve', B3(1, 255), U3(0, 255))
                self.memset('dve', B3(0, 1), 0.0)
                self.tt('dve', B3(0, 255), B3(0, 255), U3(1, 255), ALU.add)
                wsh = 0.5
            else:
                U4 = lambda off, r, n: self.V(self.URS, 0, 128, off, [[384, 11], [64, r], [1, n]])
                B4 = lambda off, r, n: self.V(self.BLK, 0, 128, off, [[256, 11], [64, r], [1, n]])
                self.cp('dve', B4(1, 4, 63), U4(co, 4, 63))
                self.memset('dve', B4(0, 4, 1), 0.0)
                self.tt('dve', B4(0, 4, 63), B4(0, 4, 63), U4(co + 1, 4, 63), ALU.add)
                if t0 > 0:
                    self.tt('dve', B3(0, 256), B3(0, 256), U3(co - 64, 256), ALU.add)
                else:
                    self.tt('dve', B3(64, 192), B3(64, 192), U3(0, 192), ALU.add)
                if t0 + TB < T:
                    self.tt('dve', B3(0, 256), B3(0, 256), U3(co + 64, 256), ALU.add)
                else:
                    self.tt('dve', B3(0, 192), B3(0, 192), U3(co + 64, 192), ALU.add)
                wsh = 0.25
            mu = self.V(self.PP, 0, 128, l * PL + PC['R_MU'], [[1, 11], [0, 256]])
            self.tt('dve', B3(0, 256), B3(0, 256), mu, ALU.mult)
            omm = self.V(self.DV, 0, 128, l * DL + DC['OMMU'], [[1, 11], [0, 256]])
            self.tt('dve', U3(co, 256), U3(co, 256), omm, ALU.mult)
            self.stt(B3(0, 256), B3(0, 256), wsh, U3(co, 256), ALU.mult, ALU.add)
            for nm, ch in (('blk_r', 0), ('blk_k', 3), ('blk_v', 6), ('blk_wl', 9), ('blk_al', 10)):
                self.dump(nm, self.BLK[:, ch, :])
            rt = self.rt
            kk = self.V(self.PP, 0, 128, l * PL + PC['R_KK'], [[1, 3], [0, 256]])
            kap = rt["d"]
            self.tt('dve', kap[:, :, :], self.BLK[:, 3:6, :], kk, ALU.mult)
            ksq = rt["E"]
            self.tt('dve', ksq[:, :, :], kap[:, :, :], kap[:, :, :], ALU.mult)
            for hp in range(3):
                self.mm(self.PA[:, hp * 256:(hp + 1) * 256], self.CST[:, CC['BONES']:CC['BONES'] + 128], ksq[:, hp, :])
            self.act(ksq[:, :, :], self.V(self.PA, 0, 128, 0, [[256, 3], [1, 256]]), AF.Sqrt)
            self.ts('dve', ksq[:, :, :], ksq[:, :, :], 1e-12, None, ALU.max)
            self.recip(ksq[:, :, :], ksq[:, :, :])
            self.tt('dve', self.KH[:, :, :], kap[:, :, :], ksq[:, :, :], ALU.mult)
            self.dump('kh', self.KH[:, 0, :])
            self.bd_fill(self.BDV, lambda par: self.V(self.BLK, par * 64, 64, 6 * 256, [[256, 3], [64, 4], [1, 64]]))
            for c in range(4):
                for hp in range(3):
                    self.mm(self.PB[:, (c * 3 + hp) * 64:(c * 3 + hp + 1) * 64], self.BDV[:, hp, c, :], self.ISTKB[:, :])
            self.cp('act', self.V(self.VSTK, 0, 128, 0, [[1, 768]]), self.PB[:, 0:768])
        bi = t0 // 256
        rw_items = [(self.BLK[:, :, :], 'BLK'), (self.RZT[:, :, :], 'RZT'), (self.KH[:, :, :], 'KH'), (self.VSTK[:, :, :, :], 'VSTK')]
        if mode == 'store':
            for ap_, k_ in rw_items:
                S.dma('sp', self.sc[k_][bi], ap_)
        if mode == 'load':
            for ap_, k_ in rw_items:
                S.dma('sp', ap_, self.sc[k_][bi])
        for d in dirs:
            if first[d]:
                if prompt:
                    self.memset('dve', self.rH[d][:, :, :], 0.0)
                else:
                    S.dma('sp', self.rH[d][:, :, :], self.st_rH[l, d])
        ybflat = self.V(self.YB, 0, 128, 0, [[1, 768]])
        if do_f and not (1 in dirs):
            S.dma('sp', ybflat, self.sYB[t0 // 256])
        for d in sorted(dirs, reverse=True):
            self.rwkv_dir(l, d, seq_idx, prompt, last)
        if not do_f:
            S.dma('sp', self.sYB[t0 // 256], ybflat)
            return
        ft = self.ft
        self.tt('dve', self.YSQ[:, :, :, :], self.YB[:, :, :, :], self.YB[:, :, :, :], ALU.mult)
        self.reduce(self.rSSQ[:, :], self.V(self.YSQ, 0, 128, 0, [[64, 12], [1, 64]]), ALU.add)
        self.ts('dve', self.rSSQ[:, :], self.rSSQ[:, :], 1.0 / 64, EPS, ALU.mult, ALU.add)
        self.act(self.rSSQ[:, :], self.rSSQ[:, :], AF.Sqrt)
        self.recip(self.rRSTD[:, :], self.rSSQ[:, :])
        self.tt('dve', ft["rk"][:, :, :], self.BLK[:, 0:3, :], self.BLK[:, 3:6, :], ALU.mult)
        for hp in range(3):
            self.mm(self.PB[:, hp * 256:(hp + 1) * 256], self.RKD[:, l, hp, :], ft["rk"][:, hp, :])
        self.tt('dve', ft["bon"][:, :, :], self.V(self.PB, 0, 128, 0, [[256, 3], [1, 256]]), self.BLK[:, 6:9, :], ALU.mult)
        self.memset('pool', self.YBD[:, :, :], 0.0)
        for c in range(4):
            for par in range(2):
                self.tt('dve', self.V(self.YBD, par * 64, 64, par * 64, [[128, 3], [1, 64]]),
                        self.V(self.YB, par * 64, 64, c * 192, [[64, 3], [1, 64]]),
                        self.V(self.rRSTD, par * 64, 64, c * 3, [[1, 3], [0, 64]]), ALU.mult)
            for hp in range(3):
                self.mm(self.PA[:, hp * 256 + c * 64: hp * 256 + (c + 1) * 64], self.YBD[:, hp, :],
                        self.CST[:, CC['ISTK']:CC['ISTK'] + 64])
        for hp in range(3):
            self.stt(ft["t1"][:, hp, :], self.PA[:, hp * 256:(hp + 1) * 256], self.ppc(l, 'R_NORM', hp), ft["bon"][:, hp, :], ALU.mult, ALU.add)
            self.tt('dve', self.MIXT[:, 3 + hp, :], ft["t1"][:, hp, :], self.RZT[:, hp, :], ALU.mult)

    def bd_fill(self, bd, src_of_par, eng='pool'):
        self.memset(eng, bd[:, :, :, :], 0.0)
        for par in range(2):
            self.cp('act' if par == 0 else eng, self.V(bd, par * 64, 64, par * 64, [[512, 3], [128, 4], [1, 64]]), src_of_par(par))

    def rwkv_dir(self, l, d, seq_idx, prompt, last):
        S = self.S
        rt = self.rt
        pb_d = 64 * d
        f3 = lambda t: t[:, :, :]
        v4 = lambda t: self.V(t, 0, 128, 0, [[256, 3], [64, 4], [1, 64]])
        self.act(self.TWL[pb_d:pb_d + 64, :], self.BLK[pb_d:pb_d + 64, 9, :], AF.Tanh)
        for hp in range(3):
            self.mm(self.PA[:, hp * 256:(hp + 1) * 256], self.LORA[pb_d:pb_d + 64, l, 0, hp * 128:(hp + 1) * 128], self.TWL[pb_d:pb_d + 64, :])
        for hp in range(3):
            self.act(rt["sg"][:, hp, :], self.PA[:, hp * 256:(hp + 1) * 256], AF.Sigmoid, bias=self.ppc(l, 'R_W0', d * 3 + hp))
        for hp in range(3):
            self.mm(self.PB[:, hp * 256:(hp + 1) * 256], self.LORA[pb_d:pb_d + 64, l, 1, hp * 128:(hp + 1) * 128], self.BLK[pb_d:pb_d + 64, 10, :])
        for hp in range(3):
            self.act(rt["aa"][:, hp, :], self.PB[:, hp * 256:(hp + 1) * 256], AF.Sigmoid, bias=self.ppc(l, 'R_A0', d * 3 + hp))
        for hp in range(3):
            self.ts('dve', rt["kt"][:, hp, :], rt["aa"][:, hp, :], self.ppc(l, 'R_KA', hp), self.dvc(l, 'OMKA', hp), ALU.mult, ALU.add)
        self.tt('dve', f3(rt["kt"]), f3(rt["kt"]), self.BLK[:, 3:6, :], ALU.mult)
        self.tt('dve', f3(rt["bb"]), self.KH[:, :, :], f3(rt["aa"]), ALU.mult)
        flat = lambda t: self.V(t, 0, 128, 0, [[1, 768]])
        self.scan(flat(rt["cs"]), self.CST[:, CC['RMASK']:CC['RMASK'] + 768], flat(rt["sg"]), 0.0, ALU.mult, ALU.add)
        self.cp('dve', self.GL[:, :, :], self.V(rt["cs"], 0, 128, 63, [[256, 3], [64, 4]]))
        if d == 1:
            csLb = self.V(rt["cs"], 0, 128, 63, [[256, 3], [64, 4], [0, 64]])
            self.tt('dve', v4(rt["d"]), csLb, v4(rt["cs"]), ALU.subtract)
            self.tt('dve', f3(rt["cs"]), f3(rt["d"]), f3(rt["sg"]), ALU.add)
        self.act(f3(rt["E"]), f3(rt["cs"]), AF.Exp, scale=-DSC)
        self.tt('dve', self.V(self.KR, 0, 128, 64, [[512, 3], [128, 4], [1, 64]]),
                self.V(self.BLK, 0, 128, 0, [[256, 3], [64, 4], [1, 64]]), v4(rt["E"]), ALU.mult)
        self.tt('dve', f3(rt["d"]), f3(rt["cs"]), f3(rt["sg"]), ALU.subtract)
        self.act(f3(rt["E"]), f3(rt["d"]), AF.Exp, scale=-DSC)
        self.tt('dve', self.V(self.KR, 0, 128, 0, [[512, 3], [128, 4], [1, 64]]), v4(self.KH), v4(rt["E"]), ALU.mult)
        self.act(f3(rt["E"]), f3(rt["cs"]), AF.Exp, scale=DSC)
        self.tt('dve', self.BTT[:, :, :], f3(rt["bb"]), f3(rt["E"]), ALU.mult)
        self.tt('dve', self.KTT[:, :, :], f3(rt["kt"]), f3(rt["E"]), ALU.mult)
        self.act(self.GL[:, :, :], self.GL[:, :, :], AF.Exp, scale=-DSC)
        BD = self.BD
        self.memset('pool', self.A1BD[:, :, :, :], 0.0)
        self.memset('pool', self.A2BD[:, :, :, :], 0.0)
        self.memset('pool', self.NN[0][:, :, :], 0.0)
        c4 = lambda t, par: self.V(t, par * 64, 64, 0, [[256, 3], [64, 4], [1, 64]])
        self.bd_fill(BD["kt"], lambda par: c4(self.KTT, par))
        self.bd_fill(BD["b"], lambda par: c4(self.BTT, par))
        self.bd_fill(BD["kh"], lambda par: self.V(self.KR, par * 64, 64, 0, [[512, 3], [128, 4], [1, 64]]))
        self.bd_fill(BD["r"], lambda par: self.V(self.KR, par * 64, 64, 64, [[512, 3], [128, 4], [1, 64]]))
        for (src, dst, neg) in ((BD["kt"], self.KTTOK, False), (BD["b"], self.BTTOK, True)):
            for half in range(2):
                for cc_ in range(2):
                    c = half * 2 + cc_
                    for hp in range(3):
                        self.tr(self.PTB[:, (cc_ * 3 + hp) * 128:(cc_ * 3 + hp + 1) * 128], src[:, hp, c, :], bf=True)
                o = self.V(dst, 0, 128, half * 768, [[1, 768]])
                if neg:
                    self.act(o, self.PTB[:, 0:768], AF.Copy, scale=-1.0)
                else:
                    self.cp('dve', o, self.PTB[:, 0:768])
        self.cp('act', self.HBF[:, :, :], self.rH[d][:, :, :])
        mk = CC['MKF'] if d == 0 else CC['MKB']
        nmk = CC['NMKF'] if d == 0 else CC['NMKB']
        mn = CC['MNF'] if d == 0 else CC['MNB']
        for j in range(4):
            c = j if d == 0 else 3 - j
            cs = slice(c * 64, (c + 1) * 64)
            KRc = lambda hp: self.V(self.KR, 0, 128, hp * 512 + c * 128, [[1, 128]])
            p1, p2, p3 = self.pq(), self.pq(), self.pq()
            for hp in range(3):
                self.mm(p1[:, hp * 128:(hp + 1) * 128], BD["kt"][:, hp, c, :], KRc(hp), inc=(hp == 2))
            for hp in range(3):
                self.mm(p2[:, hp * 128:(hp + 1) * 128], BD["b"][:, hp, c, :], KRc(hp), inc=(hp == 2))
            for hp in range(3):
                self.mm(p3[:, hp * 64:(hp + 1) * 64], BD["kh"][:, hp, c, :], self.BTT[:, hp, cs], inc=(hp == 2))
            for par in range(2):
                pp_ = par * 64
                self.tt('dve', self.V(self.A1BD, pp_, 64, pp_, [[256, 3], [128, 2], [1, 64]]),
                        self.V(p1, pp_, 64, 0, [[128, 3], [64, 2], [1, 64]]),
                        self.V(self.CST, pp_, 64, mk, [[0, 3], [64, 2], [1, 64]]), ALU.mult)
                self.tt('dve', self.V(self.A2BD, pp_, 64, pp_, [[256, 3], [128, 2], [1, 64]]),
                        self.V(p2, pp_, 64, 0, [[128, 3], [64, 2], [1, 64]]),
                        self.V(self.CST, pp_, 64, nmk, [[0, 3], [64, 2], [1, 64]]), ALU.mult)
                self.tt('dve', self.V(self.NN[0], pp_, 64, pp_, [[128, 3], [1, 64]]),
                        self.V(p3, pp_, 64, 0, [[64, 3], [1, 64]]),
                        self.V(self.CST, pp_, 64, mn, [[0, 3], [1, 64]]), ALU.mult)
            pr_ = self.pq()
            for hp in range(3):
                o = pr_[:, hp * 64:(hp + 1) * 64]
                self.mm(o, BD["kh"][:, hp, c, :], self.HBF[:, hp, :], start=True, stop=False)
                self.mm(o, self.A1BD[:, hp, 0, :], self.VSTK[:, c, hp, :], start=False, stop=True, inc=(hp == 2))
            u192 = self.V(self.U, 0, 128, 0, [[1, 192]])
            ub192 = self.V(self.UBF, 0, 128, 0, [[1, 192]])
            self.cp('dve', u192, pr_[:, 0:192])
            self.cp('act', ub192, u192)
            for k in range(6):
                NTk = self.A2BD[:, :, 0, :] if k == 0 else self.NTT[k % 2][:, :, :]
                Nk = self.NN[k % 2]
                pu = self.pq()
                for hp in range(3):
                    self.mm(pu[:, hp * 64:(hp + 1) * 64], NTk[:, hp, :], self.UBF[:, hp, :], inc=(hp == 2))
                if k < 5:
                    pnt = self.pq()
                    for hp in range(3):
                        self.mm(pnt[:, hp * 128:(hp + 1) * 128], Nk[:, hp, :], NTk[:, hp, :], inc=(hp == 2))
                    pn = self.pq()
                    for hp in range(3):
                        self.mm(pn[:, hp * 128:(hp + 1) * 128], NTk[:, hp, :], Nk[:, hp, :], inc=(hp == 2))
                self.tt('dve', ub192, u192, pu[:, 0:192], ALU.add)
                if k < 5:
                    self.cp('act', self.V(self.NTT[(k + 1) % 2], 0, 128, 0, [[1, 384]]), pnt[:, 0:384])
                    self.cp('act', self.V(self.NN[(k + 1) % 2], 0, 128, 0, [[1, 384]]), pn[:, 0:384])
                    self.tt('dve', u192, u192, pu[:, 0:192], ALU.add)
            py = self.pq()
            for hp in range(3):
                o = py[:, hp * 64:(hp + 1) * 64]
                self.mm(o, BD["r"][:, hp, c, :], self.HBF[:, hp, :], start=True, stop=False)
                self.mm(o, self.A1BD[:, hp, 1, :], self.VSTK[:, c, hp, :], start=False, stop=False)
                self.mm(o, self.A2BD[:, hp, 1, :], self.UBF[:, hp, :], start=False, stop=True, inc=(hp == 2))
            ybv = self.V(self.YB, 0, 128, c * 192, [[1, 192]])
            if d == 1:
                self.cp('act', ybv, py[:, 0:192])
            else:
                self.tt('dve', ybv, ybv, py[:, 0:192], ALU.add)
            ph = self.pq()
            for hp in range(3):
                o = ph[:, hp * 64:(hp + 1) * 64]
                self.mm(o, self.KTTOK[:, c, hp, :], self.VSTK[:, c, hp, :], start=True, stop=False)
                self.mm(o, self.BTTOK[:, c, hp, :], self.UBF[:, hp, :], start=False, stop=True, inc=(hp == 2))
            self.tt('dve', self.HTMP[:, :, :], self.rH[d][:, :, :], self.V(ph, 0, 128, 0, [[64, 3], [1, 64]]), ALU.add)
            self.tt('dve', self.HBF[:, :, :], self.HTMP[:, :, :], self.V(self.GL, 0, 128, c, [[4, 3], [0, 64]]), ALU.mult)
            self.tt('dve', self.rH[d][:, :, :], self.HTMP[:, :, :], self.V(self.GL, 0, 128, c, [[4, 3], [0, 64]]), ALU.mult)
        if last[d] and prompt:
            S.dma('sp', self.o_rH[seq_idx, l, d], self.rH[d][:, :, :])

    def stage_out(self, l, mod, xsrc, xdst, r0):
        S = self.S
        for tt_ in range(2):
            o, xw = self.XN[tt_], self.XW[tt_]
            S.dma('sp', xw[:, :], xsrc[r0 + tt_ * 128: r0 + (tt_ + 1) * 128, :])
            for ch in range(2):
                ps = self.pq()
                for kc in range(8):
                    self.mm(ps[:, :], self.MIXT[:, kc, tt_ * 128:(tt_ + 1) * 128], self.W_OUT[:, kc, ch * 512:(ch + 1) * 512],
                            start=(kc == 0), stop=(kc == 7))
                self.cp('act' if ch else 'dve', o[:, ch * 512:(ch + 1) * 512], ps[:, :])
            self.dump('o_proj%d' % tt_, o[:, :])
            self.dump('o_x%d' % tt_, xw[:, :])
            ssq = self.TMPS[:, 24 + tt_:25 + tt_]
            junk = self.V(self.BLK, 0, 128, 0, [[1, D]])
            self.act(junk, o[:, :], AF.Square, accum=ssq)
            rs = self.TMPS[:, 26 + tt_:27 + tt_]
            self.ts('dve', rs, ssq, 1.0 / D, EPS, ALU.mult, ALU.add)
            self.act(rs, rs, AF.Sqrt)
            self.recip(rs, rs)
            self.dump('o_rs%d' % tt_, rs)
            self.stt(o[:, :], o[:, :], rs, self.GATEB[:, mod, :], ALU.mult, ALU.mult)
            self.dump('o_g%d' % tt_, o[:, :])
            self.tt('dve', o[:, :], o[:, :], xw[:, :], ALU.add)
            S.dma('sp', xdst[r0 + tt_ * 128: r0 + (tt_ + 1) * 128, :], o[:, :])

    def build(self, layers=(0, 1)):
        try:
            self._build(layers)
        except StopIteration:
            pass
        self.S.finish('sp')
        return self.nc

    def _build(self, layers):
        NP, TS = self.NP, self.TS
        self.setup()
        if self.stop == 'setup':
            raise StopIteration
        for li, l in enumerate(layers):
            self.load_layer(l)
            if self.stop == 'load':
                raise StopIteration
            xsrc = self.x_in if li == 0 else self.x1
            xdst = self.y_out if li == len(layers) - 1 else self.x1
            T_, F_ = {0: True, 1: True}, {0: False, 1: False}
            for s in range(NP):
                self.visit(l, 0, xsrc, xdst, s * 256, 256, 0, [0, 1], False, s, T_, T_)
            if TS > 0:
                nb = TS // TB
                row0 = NP * 256
                for b in range(nb - 1, -1, -1):
                    self.visit(l, 1, xsrc, xdst, row0, TS, b * TB, [1], True, 0,
                               {0: False, 1: b == nb - 1}, {0: False, 1: b == 0}, mode='store')
                for b in range(nb):
                    self.visit(l, 1, xsrc, xdst, row0, TS, b * TB, [0], True, 0,
                               {0: b == 0, 1: False}, {0: b == nb - 1, 1: False}, mode='load')


def prep_shared(inp):
    f = lambda a: np.ascontiguousarray(np.asarray(a, dtype=np.float32))
    b_mod = f(inp['b_mod'])
    sh = {}
    sh['w_mod'] = f(inp['w_mod'])
    sh['w_in'] = f(inp['w_in'])
    sh['w_out'] = f(inp['w_out'])
    sh['bmodT'] = f(b_mod[:, :2048].reshape(2, 16, 128).transpose(2, 0, 1))
    sh['bmodg'] = f(b_mod[:, 2048:3072])
    sh['gpost'] = f(inp['g_post'])
    pp = np.zeros((128, 2 * PL), np.float32)
    for l in range(2):
        o = l * PL
        def put(key, arr, n):
            pp[:, o + PC[key]: o + PC[key] + n] = np.asarray(arr, np.float32).reshape(n, 128).T
        put('G_PRE', inp['g_pre'][l], 8)
        put('M_NORM', inp['m_norm'][l], 3)
        put('R_MU', inp['r_mu'][l], 11)
        put('R_W0', np.asarray(inp['r_w0'][l]).reshape(-1), 6)
        put('R_A0', np.asarray(inp['r_a0'][l]).reshape(-1), 6)
        put('R_KK', inp['r_kk'][l], 3)
        put('R_KA', inp['r_ka'][l], 3)
        put('R_RK', inp['r_rk'][l], 3)
        put('R_NORM', inp['r_norm'][l], 3)
        put('L_CONV', np.asarray(inp['l_conv'][l]).reshape(-1), 8)
        put('L_CONVB', inp['l_conv_b'][l], 2)
        put('L_BA', np.asarray(inp['l_ba'][l]).reshape(-1), 4)
        put('L_BX', np.asarray(inp['l_bx'][l]).reshape(-1), 4)
        put('L_LAM', np.asarray(inp['l_lambda'][l]).reshape(-1), 4)
        pp[0:6, o + PC['M_BI']: o + PC['M_BI'] + 2] = np.asarray(inp['m_bi'][l], np.float32).T
        pp[0:6, o + PC['M_BF']: o + PC['M_BF'] + 2] = np.asarray(inp['m_bf'][l], np.float32).T
    sh['pp'] = pp
    sh['cst'] = make_consts()
    lora = np.zeros((128, 2, 2, 384), np.float32)
    for wi, key in enumerate(['r_w2', 'r_a2']):
        a = np.asarray(inp[key], np.float32)
        lora[:, :, wi, :] = a.transpose(1, 2, 0, 3).reshape(128, 2, 384)
    sh['lora'] = lora
    lruw = np.zeros((128, 2, 8, 128), np.float32)
    for gi, key in enumerate(['l_wa', 'l_wx']):
        a = np.asarray(inp[key], np.float32)
        for l in range(2):
            for d in range(2):
                for pr in range(2):
                    for hb in range(2):
                        n = 2 * pr + hb
                        lruw[hb * 64:(hb + 1) * 64, l, (gi * 2 + d) * 2 + pr, hb * 64:(hb + 1) * 64] = a[l, d, n]
    sh['lruw'] = lruw
    return sh


def prep_core(inp, b, NP, TS):
    f = lambda a: np.ascontiguousarray(np.asarray(a, dtype=np.float32))
    m = {}
    xp = np.asarray(inp['x_prompt'], np.float32)[b * NP:(b + 1) * NP].reshape(NP * 256, D)
    if TS > 0:
        xs = np.asarray(inp['x_sample'], np.float32)[b]
        m['x_in'] = f(np.concatenate([xp, xs], 0))
    else:
        m['x_in'] = f(xp)
    cc = np.stack([np.asarray(inp['c_ctx'], np.float32), np.asarray(inp['c'], np.float32)[b]], -1)
    m['cc'] = f(cc.reshape(8, 128, 2).transpose(1, 0, 2))
    C = np.asarray(inp['state_mlstm_C'], np.float32)[b]
    n = np.asarray(inp['state_mlstm_n'], np.float32)[b]
    Cn = np.concatenate([C, n[..., None]], -1)
    Cn = Cn.reshape(2, 2, 3, 2, 64, 65).transpose(0, 1, 3, 4, 2, 5).reshape(2, 2, 128, 3, 65)
    m['st_mC'] = f(Cn)
    m['st_mm'] = f(np.asarray(inp['state_mlstm_m'], np.float32)[b].transpose(2, 0, 1))
    R = np.asarray(inp['state_rwkv'], np.float32)[b]
    R = R.transpose(0, 1, 2, 4, 3)
    R = R.reshape(2, 2, 3, 2, 64, 64).transpose(0, 1, 3, 4, 2, 5).reshape(2, 2, 128, 3, 64)
    m['st_rH'] = f(R)
    L = np.asarray(inp['state_rglru'], np.float32)[b]
    m['st_l'] = f(L.reshape(2, 2, 2, 128).transpose(3, 0, 1, 2))
    return m


def unpack_core(r, NP, TS):
    y = r['y_out']
    yp = y[:NP * 256].reshape(NP, 256, D)
    ys = y[NP * 256:]
    mC = r['o_mC'].reshape(NP, 2, 2, 2, 64, 3, 65).transpose(0, 1, 2, 5, 3, 4, 6).reshape(NP, 2, 2, 6, 64, 65)
    newC = np.ascontiguousarray(mC[..., :64])
    newn = np.ascontiguousarray(mC[..., 64])
    newm = r['o_mm'].reshape(NP, 2, 2, 6)
    rH = r['o_rH'].reshape(NP, 2, 2, 2, 64, 3, 64).transpose(0, 1, 2, 5, 3, 4, 6).reshape(NP, 2, 2, 6, 64, 64)
    newr = np.ascontiguousarray(rH.transpose(0, 1, 2, 3, 5, 4))
    newl = np.ascontiguousarray(r['o_l'].transpose(0, 1, 2, 4, 3).reshape(NP, 2, 2, 256))
    return yp, ys, newC, newn, newm, newr, newl


_NC_CACHE = {}


def kernel(**inputs):
    NP, TS = 4, 2048
    key = (NP, TS)
    if key not in _NC_CACHE:
        _NC_CACHE[key] = Builder(NP, TS).build()
    nc = _NC_CACHE[key]
    sh = prep_shared(inputs)
    in_maps = []
    for b in range(NCORES):
        m = dict(sh)
        m.update(prep_core(inputs, b, NP, TS))
        in_maps.append(m)
    res = run_bass_kernel_spmd(nc, in_maps, core_ids=list(range(NCORES)))
    outs = [unpack_core(r, NP, TS) for r in res.results]
    y_prompt = np.concatenate([o[0] for o in outs], 0)
    y_sample = np.stack([o[1] for o in outs], 0)
    cat = lambda i: np.concatenate([o[i] for o in outs], 0)
    return (y_prompt.astype(np.float32), y_sample.astype(np.float32), cat(2).astype(np.float32),
            cat(3).astype(np.float32), cat(4).astype(np.float32), cat(5).astype(np.float32), cat(6).astype(np.float32))
```

```python
import numpy as np
import concourse.bass as bass
import concourse.mybir as mybir
from concourse.bass_utils import run_bass_kernel_spmd

F32 = mybir.dt.float32
BF16 = mybir.dt.bfloat16
AF = mybir.ActivationFunctionType
ALU = mybir.AluOpType
AX = mybir.AxisListType

D = 1024
IN_COLS = 4248
EPS = 1e-6
DSC = 0.6065306597126334
TB = 256
NCORES = 8


def _prod(xs):
    r = 1
    for x in xs:
        r *= int(x)
    return r


class Sync:
    def __init__(self, nc, n_dma_sems=32):
        self.nc = nc
        self.engs = {'pe': nc.tensor, 'dve': nc.vector, 'act': nc.scalar,
                     'pool': nc.gpsimd, 'sp': nc.sync}
        self.sem = {}
        self.cnt = {}
        for e in ['pe', 'dve', 'act', 'pool']:
            self.sem[e] = nc.alloc_semaphore('sem_' + e)
            self.cnt[e] = 0
        self.seen = {e: {} for e in self.engs}
        self.dma_ring = [nc.alloc_semaphore('dq_%d' % i) for i in range(n_dma_sems)]
        self.dma_uses = [0] * n_dma_sems
        self.dma_next = 0
        self.rec = {}
        self.untracked = set()
        self.n_wait = 0
        self.n_ins = 0
        self.pstep_cache = {}
        self.sb_addr = {}

    def region(self, ap):
        t = ap.tensor
        name = t.name
        apl = [(int(s), int(c)) for (s, c) in ap.ap]
        off = int(ap.offset)
        if type(t).__name__.startswith('DRam'):
            lo = off + sum(min(0, s * (c - 1)) for s, c in apl)
            hi = off + sum(max(0, s * (c - 1)) for s, c in apl) + 1
            return (name, 0, 1, lo, hi)
        pstep = self.pstep_cache.get(name)
        if pstep is None:
            pstep = _prod(list(t.shape)[1:])
            self.pstep_cache[name] = pstep
        p0 = off // pstep
        f0 = off % pstep
        npart = apl[0][1]
        rest = apl[1:]
        lo = f0 + sum(min(0, s * (c - 1)) for s, c in rest)
        hi = f0 + sum(max(0, s * (c - 1)) for s, c in rest) + 1
        if name in self.sb_addr:
            base, es = self.sb_addr[name]
            return ('SB', p0, p0 + npart, base + lo * es, base + hi * es)
        return ('PS:' + name, (p0 // 32) * 32, ((p0 + npart + 31) // 32) * 32, (lo // 512) * 512, ((hi + 511) // 512) * 512)

    @staticmethod
    def _ovl(a, b):
        return a[1] < b[2] and b[1] < a[2] and a[3] < b[4] and b[3] < a[4]

    @staticmethod
    def _contains(a, b):
        return a[1] <= b[1] and b[2] <= a[2] and a[3] <= b[3] and b[4] <= a[4]

    def _collect(self, e, reads, writes):
        deps = {}
        own = self.sem.get(e)
        rregs = [self.region(a) for a in reads]
        wregs = [self.region(a) for a in writes]
        for r in rregs:
            if r[0] in self.untracked:
                continue
            isps = r[0].startswith('PS:')
            for (reg, kind, sem, val) in self.rec.get(r[0], ()):
                if (kind == 'w' or (isps and sem is not own)) and self._ovl(reg, r):
                    if e == 'pe' and sem is own:
                        continue
                    k = id(sem)
                    if deps.get(k, (None, 0))[1] < val:
                        deps[k] = (sem, val)
        for w in wregs:
            if w[0] in self.untracked:
                continue
            for (reg, kind, sem, val) in self.rec.get(w[0], ()):
                if self._ovl(reg, w):
                    if sem is own:
                        continue
                    k = id(sem)
                    if deps.get(k, (None, 0))[1] < val:
                        deps[k] = (sem, val)
        return deps, rregs, wregs

    def _record(self, rregs, wregs, sem, val):
        for r in rregs:
            if r[0] in self.untracked:
                continue
            lst = self.rec.setdefault(r[0], [])
            lst[:] = [x for x in lst if not (x[1] == 'r' and x[2] is sem and self._contains(r, x[0]))]
            lst.append((r, 'r', sem, val))
        for w in wregs:
            if w[0] in self.untracked:
                continue
            lst = self.rec.setdefault(w[0], [])
            lst[:] = [x for x in lst if not self._contains(w, x[0])]
            lst.append((w, 'w', sem, val))

    def wait(self, e, sem, val):
        k = id(sem)
        if self.seen[e].get(k, 0) >= val:
            return
        self.engs[e].wait_ge(sem, val)
        self.seen[e][k] = val
        self.n_wait += 1

    max_ins = None
    paranoid = False
    embed_waits = True

    def emit(self, e, reads, writes, build, inc=True):
        if self.max_ins is not None and self.n_ins >= self.max_ins:
            raise StopIteration
        deps, rregs, wregs = self._collect(e, reads, writes)
        embed = None
        for (sem, val) in deps.values():
            if self.embed_waits and embed is None and self.seen[e].get(id(sem), 0) < val:
                embed = (sem, val)
                continue
            self.wait(e, sem, val)
        if self.paranoid:
            for e2 in ['pe', 'dve', 'act', 'pool']:
                if self.cnt[e2] > 0 and not (e == 'pe' and e2 == 'pe'):
                    self.wait(e, self.sem[e2], self.cnt[e2])
        ins = build(self.engs[e])
        if embed is not None:
            ins._wait_ge(embed[0], embed[1])
            self.seen[e][id(embed[0])] = embed[1]
        self.n_ins += 1
        if inc:
            self.cnt[e] += 1
            ins.then_inc(self.sem[e], 1)
            val = self.cnt[e]
        else:
            val = self.cnt[e] + 1
        self._record(rregs, wregs, self.sem[e], val)
        return ins

    def dma(self, q, out, in_, **kw):
        if self.max_ins is not None and self.n_ins >= self.max_ins:
            raise StopIteration
        i = self.dma_next
        self.dma_next = (i + 1) % len(self.dma_ring)
        sem = self.dma_ring[i]
        uses = self.dma_uses[i]
        if uses > 0:
            self.wait(q, sem, 16 * uses)
        deps, rregs, wregs = self._collect(q, [in_], [out])
        for (s, v) in deps.values():
            self.wait(q, s, v)
        ins = self.engs[q].dma_start(out=out, in_=in_, **kw)
        ins.then_inc(sem, 16)
        self.n_ins += 1
        self.dma_uses[i] = uses + 1
        self._record(rregs, wregs, sem, 16 * (uses + 1))
        return ins

    def finish(self, q='sp'):
        for i, sem in enumerate(self.dma_ring):
            if self.dma_uses[i] > 0:
                self.wait(q, sem, 16 * self.dma_uses[i])
        for e in ['pe', 'dve', 'act', 'pool']:
            if self.cnt[e] > 0:
                self.wait(q, self.sem[e], self.cnt[e])


class Arena:
    def __init__(self, base, size):
        self.base, self.size, self.ptr = base, size, 0

    def take(self, nbytes):
        off = (self.ptr + 31) // 32 * 32
        self.ptr = off + nbytes
        assert self.ptr <= self.size, ("arena overflow", self.ptr, self.size)
        return self.base + off


PL = 72
PC = dict(G_PRE=0, M_NORM=8, R_MU=11, R_W0=22, R_A0=28, R_KK=34, R_KA=37, R_RK=40, R_NORM=43,
          L_CONV=46, L_CONVB=54, L_BA=56, L_BX=60, L_LAM=64, M_BI=68, M_BF=70)
DL = 32
DC = dict(OMKA=0, CLAM=3, C2LAM=7, NBF=11, OMMU=16)
CC = dict(IDENT=0, BONES=128, MKF=256, MKB=384, MNF=512, MNB=576, MUI=640, MLI=704, RMASK=768,
          LSEL=1536, PSEL=1664, NMKF=1668, NMKB=1796, ONES=1924, ISTK=2052)
NCST = 2052 + 64


def make_consts():
    c = np.zeros((128, NCST), np.float32)
    c[:, 0:128] = np.eye(128)
    c[0:64, 128:192] = 1.0
    c[64:128, 192:256] = 1.0
    s = np.arange(64)[:, None]
    t = np.arange(64)[None, :]
    us, ui = (s < t).astype(np.float32), (s <= t).astype(np.float32)
    ls, li = (s > t).astype(np.float32), (s >= t).astype(np.float32)
    c[0:64, 256:320], c[0:64, 320:384] = us, ui
    c[0:64, 384:448], c[0:64, 448:512] = ls, li
    c[0:64, 512:576] = -ls
    c[0:64, 576:640] = -us
    c[0:64, 640:704] = ui
    c[0:64, 704:768] = li
    rm = np.ones(768, np.float32)
    rm[::64] = 0.0
    c[:, 768:1536] = rm[None, :]
    for k in range(6):
        c[k, 1536 + (k % 2) * 64: 1536 + (k % 2) * 64 + 64] = 1.0
        c[k, 1664 + k // 2] = 1.0
    c[0:64, 1668:1796] = -c[0:64, 256:384]
    c[0:64, 1796:1924] = -c[0:64, 384:512]
    c[:, 1924:2052] = 1.0
    for (a, b) in ((256, 768), (1668, 1924)):
        c[64:128, a:b] = c[0:64, a:b]
    c[0:64, 2052:2116] = np.eye(64)
    c[64:128, 2052:2116] = np.eye(64)
    return c


class Builder:
    def __init__(self, NP, TS, debug=False, stop=None):
        self.stop = stop
        self.NP, self.TS = NP, TS
        self.NTOK = NP * 256 + TS
        self.debug = debug
        nc = self.nc = bass.Bass("TRN2", target_bir_lowering=False)
        self.S = Sync(nc)
        self._decl_dram()
        self._alloc()

    def _decl_dram(self):
        nc, NP, TS = self.nc, self.NP, self.TS
        di = lambda n, s: nc.dram_tensor(n, list(s), F32, kind="ExternalInput").ap()
        do = lambda n, s: nc.dram_tensor(n, list(s), F32, kind="ExternalOutput").ap()
        dx = lambda n, s: nc.dram_tensor(n, list(s), F32, kind="Internal").ap()
        self.x_in = di("x_in", [self.NTOK, D])
        self.cc = di("cc", [128, 8, 2])
        self.w_mod = di("w_mod", [2, D, 3 * D])
        self.bmodT = di("bmodT", [128, 2, 16])
        self.bmodg = di("bmodg", [2, D])
        self.gpost = di("gpost", [2, D])
        self.w_in = di("w_in", [2, D, IN_COLS])
        self.w_out = di("w_out", [2, D, D])
        self.pp = di("pp", [128, 2 * PL])
        self.cst = di("cst", [128, NCST])
        self.lora = di("lora", [128, 2, 2, 384])
        self.lruw = di("lruw", [128, 2, 8, 128])
        self.st_mC = di("st_mC", [2, 2, 128, 3, 65])
        self.st_mm = di("st_mm", [6, 2, 2])
        self.st_rH = di("st_rH", [2, 2, 128, 3, 64])
        self.st_l = di("st_l", [128, 2, 2, 2])
        for n in ["x_in", "cc", "w_mod", "bmodT", "bmodg", "gpost", "w_in", "w_out", "pp", "cst", "lora",
                  "lruw", "st_mC", "st_mm", "st_rH", "st_l"]:
            self.S.untracked.add(n)
        self.y_out = do("y_out", [self.NTOK, D])
        self.o_mC = do("o_mC", [NP, 2, 2, 128, 3, 65])
        self.o_mm = do("o_mm", [NP, 2, 2, 6, 1])
        self.o_rH = do("o_rH", [NP, 2, 2, 128, 3, 64])
        self.o_l = do("o_l", [NP, 2, 2, 128, 2])
        self.x1 = dx("x1", [self.NTOK, D])
        self.sHB = dx("sHB", [max(TS, 64), 384])
        self.sYB = dx("sYB", [max(TS // 256, 1), 128, 768])
        self.sLB = dx("sLB", [128, 2, max(TS, 64)])
        nb = max(TS // 256, 1)
        dxt = lambda n, s_, dt: nc.dram_tensor(n, list(s_), dt, kind="Internal").ap()
        self.sc = {
            'QT': dxt("sc_QT", [nb, 128, 3, 256], BF16), 'KT': dxt("sc_KT", [nb, 128, 3, 256], BF16),
            'KTOK': dxt("sc_KTOK", [nb, 64, 4, 384], BF16), 'VAUG': dxt("sc_VAUG", [nb, 64, 4, 6, 65], BF16),
            'GOZ': dxt("sc_GOZ", [nb, 128, 3, 256], F32), 'GI': dxt("sc_GI", [nb, 6, 256], F32),
            'LF': dxt("sc_LF", [nb, 6, 256], F32), 'XC': dxt("sc_XC", [nb, 128, 2, 256], F32),
            'LZT': dxt("sc_LZT", [nb, 128, 2, 256], F32), 'BLK': dxt("sc_BLK", [nb, 128, 11, 256], F32),
            'RZT': dxt("sc_RZT", [nb, 128, 3, 256], F32), 'KH': dxt("sc_KH", [nb, 128, 3, 256], F32),
            'VSTK': dxt("sc_VSTK", [nb, 128, 4, 3, 64], BF16),
        }
        if self.debug:
            self.dbg = do("dbg", [128, 32768])
            self.dbg_map = {}
            self.dbg_off = 0

    def T(self, name, shape, dtype, arena):
        es = 2 if dtype == BF16 else 4
        nb = _prod(shape[1:]) * es
        off = arena.take(nb)
        t = self.nc.alloc_sbuf_tensor_at(name, list(shape), dtype, offset=off)
        self.S.sb_addr[t.name] = (off, es)
        return t

    def _alloc(self):
        nc = self.nc
        B0 = 16384 + 256
        LIM = 224 * 1024 - 256
        P = Arena(B0, LIM - B0)
        T = self.T
        self.W_IN = T("W_IN", [128, 8, IN_COLS], BF16, P)
        self.W_OUT = T("W_OUT", [128, 8, D], BF16, P)
        self.CST = T("CST", [128, NCST], F32, P)
        self.PP = T("PP", [128, 2 * PL], F32, P)
        self.DV = T("DV", [128, 2 * DL], F32, P)
        self.LORA = T("LORA", [128, 2, 2, 384], F32, P)
        self.LRUW = T("LRUW", [128, 2, 8, 128], F32, P)
        self.RKD = T("RKD", [128, 2, 3, 128], F32, P)
        self.GS = T("GS", [128, 2, 8], F32, P)
        self.SH = T("SH", [128, 2, 8], F32, P)
        self.GATEB = T("GATEB", [128, 2, D], F32, P)
        self.IDB = T("IDB", [128, 128], BF16, P)
        self.ISTKB = T("ISTKB", [128, 64], BF16, P)
        self.mC = [T("mC%d" % d, [128, 3, 65], F32, P) for d in range(2)]
        self.mM = [T("mM%d" % d, [6, 1], F32, P) for d in range(2)]
        self.rH = [T("rH%d" % d, [128, 3, 64], F32, P) for d in range(2)]
        self.lS = [T("lS%d" % d, [128, 2], F32, P) for d in range(2)]
        XB = Arena(P.take(16384), 16384)
        self.HT = T("HT", [128, 8, 384], BF16, P)
        self.MIXT = T("MIXT", [128, 8, 256], BF16, P)
        self.BLK = T("BLK", [128, 11, 256], F32, P)
        abase = P.take(0)
        asize = P.size - P.ptr
        self.asize = asize
        mk = lambda: Arena(abase, asize)
        a = Arena(XB.base, XB.size)
        self.XW = [T("XW%d" % i, [128, D], F32, a) for i in range(2)]
        self.XN = [T("XN%d" % i, [128, D], F32, a) for i in range(2)]
        a = Arena(XB.base, XB.size)
        self.YB = T("YB", [128, 4, 3, 64], F32, a)
        self.RZT = T("RZT", [128, 3, 256], F32, a)
        self.WSTG = [T("WSTG0", [128, 8, 256], F32, Arena(self.S.sb_addr[self.BLK.name][0], 11264)),
                     T("WSTG1", [128, 8, 256], F32, Arena(XB.base, 8192)),
                     T("WSTG2", [128, 8, 256], F32, Arena(XB.base + 8192, 8192))]
        a = mk()
        self.WMs = [T("WM%d" % i, [128, 8, 512], F32, a) for i in range(2)]
        self.SCT = T("SCT", [128, 8, 2], F32, a)
        self.SCB = T("SCB", [128, 2, 8, 128], F32, a)
        self.MODT = T("MODT", [128, 16, 2], F32, a)
        self.BMT = T("BMT", [128, 2, 16], F32, a)
        awm = Arena(self.S.sb_addr[self.WMs[0].name][0], 16384)
        self.BG = T("BG", [128, D], F32, awm)
        self.GP = T("GP", [128, D], F32, awm)
        self.TMPS = T("TMPS", [128, 32], F32, a)
        self.MODROW = T("MODROW", [2, 512], F32, a)
        a = mk()
        self.QT = T("QT", [128, 3, 256], BF16, a)
        self.KT = T("KT", [128, 3, 256], BF16, a)
        self.GOZ = T("GOZ", [128, 3, 256], F32, a)
        self.KTOK = T("KTOK", [64, 4, 384], BF16, a)
        self.VAUG = T("VAUG", [64, 4, 6, 65], BF16, a)
        self.HB = T("HB", [64, 4, 384], F32, a)
        self.mg = []
        for d in range(2):
            g = {}
            for n in ["gi", "lf", "pre", "bb", "gg", "ee", "fl"]:
                g[n] = T("mg_%s%d" % (n, d), [6, 256], F32, a)
            for n in ["mx", "mch", "mprev", "MM", "dec"]:
                g[n] = T("mg_%s%d" % (n, d), [6, 4], F32, a)
            g["X2"] = T("mg_X2%d" % d, [6, 4, 3], F32, a)
            g["etok"] = T("mg_etok%d" % d, [64, 4, 6], F32, a)
            g["fltok"] = T("mg_fltok%d" % d, [64, 4, 6], F32, a)
            g["decb"] = T("mg_decb%d" % d, [128, 4, 3], F32, a)
            self.mg.append(g)
        self.STSB = [T("STSB%d" % i, [64, 6, 64], BF16, a) for i in range(2)]
        self.VP = [T("VP%d" % i, [64, 6, 65], BF16, a) for i in range(2)]
        self.CDEC = T("CDEC", [128, 3, 65], F32, a)
        self.CDBF = T("CDBF", [128, 3, 65], BF16, a)
        self.DN = T("DN", [64, 6], F32, a)
        self.RDN = T("RDN", [64, 6], F32, a)
        self.HD = T("HD", [64, 6, 64], F32, a)
        self.SQ = T("SQ", [64, 4, 384], F32, a)
        self.SSQ = T("SSQ", [64, 24], F32, a)
        self.RSTD = T("RSTD", [64, 24], F32, a)
        self.ZT = [T("ZT%d" % i, [128, 256], F32, a) for i in range(2)]
        a = mk()
        self.XL = T("XL", [128, 2, 392], F32, a)
        self.XC = T("XC", [128, 2, 256], F32, a)
        self.LZT = T("LZT", [128, 2, 256], F32, a)
        self.ltd = [{n: T("lt%d_%s" % (d_, n), [128, 2, 256], F32, a) for n in ["rg", "ig", "aa", "a2", "bt"]} for d_ in range(2)]
        self.HL = [T("HL%d" % d, [128, 2, 256], F32, a) for d in range(2)]
        a = mk()
        self.URS = T("URS", [128, 11, 384], F32, a)
        a = mk()
        self.KH = T("KH", [128, 3, 256], F32, a)
        R1 = a.take(7 * 3072)
        a1 = Arena(R1, 7 * 3072)
        self.rt = {n: T("rt_" + n, [128, 3, 256], F32, a1) for n in ["sg", "aa", "kt", "bb", "cs", "d", "E"]}
        a2 = Arena(R1, 7 * 3072)
        self.A1BD = T("A1BD", [128, 3, 2, 128], BF16, a2)
        self.A2BD = T("A2BD", [128, 3, 2, 128], BF16, a2)
        self.NN = [T("NN%d" % i, [128, 3, 128], BF16, a2) for i in range(2)]
        self.NTT = [T("NTT%d" % i, [128, 3, 128], BF16, a2) for i in range(2)]
        self.U = T("U", [128, 3, 64], F32, a2)
        self.UBF = T("UBF", [128, 3, 64], BF16, a2)
        self.HBF = T("HBF", [128, 3, 64], BF16, a2)
        self.HTMP = T("HTMP", [128, 3, 64], F32, a2)
        self.YBD = T("YBD", [128, 3, 128], F32, a2)
        self.KTTOK = T("KTTOK", [128, 4, 3, 128], BF16, a2)
        self.BTTOK = T("BTTOK", [128, 4, 3, 128], BF16, a2)
        self.rSSQ = T("rSSQ", [128, 12], F32, a2)
        self.rRSTD = T("rRSTD", [128, 12], F32, a2)
        self.KR = T("KR", [128, 3, 4, 2, 64], BF16, a)
        self.BTT = T("BTT", [128, 3, 256], BF16, a)
        self.KTT = T("KTT", [128, 3, 256], BF16, a)
        bdr = a.take(4 * 3072)
        a3 = Arena(bdr, 4 * 3072)
        self.BD = {n: T("BD_" + n, [128, 3, 4, 128], BF16, a3) for n in ["kt", "b", "kh", "r"]}
        a3 = Arena(bdr, 4 * 3072)
        self.ft = {n: T("ft_" + n, [128, 3, 256], F32, a3) for n in ["rk", "bon", "t1"]}
        self.YSQ = T("YSQ", [128, 4, 3, 64], F32, a3)
        self.BDV = T("BDV", [128, 3, 4, 128], BF16, a)
        self.VSTK = T("VSTK", [128, 4, 3, 64], BF16, a)
        self.TWL = T("TWL", [128, 256], F32, a)
        self.GL = T("GL", [128, 3, 4], F32, a)
        self.PA = nc.alloc_psum_tensor("PA", [128, 1024], F32)
        self.PB = nc.alloc_psum_tensor("PB", [128, 1024], F32)
        self.PQ = [nc.alloc_psum_tensor("PQ%d" % i, [128, 512], F32) for i in range(3)]
        self.PTB = nc.alloc_psum_tensor("PTB", [128, 1024], BF16)
        self.pq_i = 0

    def pq(self):
        t = self.PQ[self.pq_i % 3]
        self.pq_i += 1
        return t

    def V(self, t, p0, npart, off, dims):
        pstep = _prod(list(t.shape)[1:])
        return bass.AP(t, p0 * pstep + off, [[pstep, npart]] + [list(d) for d in dims])

    def tt(self, e, out, in0, in1, op):
        return self.S.emit(e, [in0, in1], [out], lambda g: g.tensor_tensor(out=out, in0=in0, in1=in1, op=op))

    def ts(self, e, out, in0, s1, s2, op0, op1=None):
        rd = [in0] + [s for s in (s1, s2) if not isinstance(s, (int, float)) and s is not None]
        if op1 is None:
            return self.S.emit(e, rd, [out], lambda g: g.tensor_scalar(out=out, in0=in0, scalar1=s1, scalar2=None, op0=op0))
        return self.S.emit(e, rd, [out], lambda g: g.tensor_scalar(out=out, in0=in0, scalar1=s1, scalar2=s2, op0=op0, op1=op1))

    def stt(self, out, in0, sc, in1, op0, op1):
        rd = [in0, in1] + ([] if isinstance(sc, (int, float)) else [sc])
        return self.S.emit('dve', rd, [out], lambda g: g.scalar_tensor_tensor(out=out, in0=in0, scalar=sc, in1=in1, op0=op0, op1=op1))

    def act(self, out, in_, func, bias=None, scale=None, accum=None):
        rd = [in_]
        kw = {}
        if bias is not None:
            kw['bias'] = bias
            if not isinstance(bias, (int, float)):
                rd.append(bias)
        if scale is not None:
            kw['scale'] = scale
            if not isinstance(scale, (int, float)):
                rd.append(scale)
        wr = [out]
        if accum is not None:
            kw['accum_out'] = accum
            wr.append(accum)
        return self.S.emit('act', rd, wr, lambda g: g.activation(out=out, in_=in_, func=func, **kw))

    def cp(self, e, out, in_):
        if e == 'act':
            return self.act(out, in_, AF.Copy)
        return self.S.emit(e, [in_], [out], lambda g: g.tensor_copy(out=out, in_=in_))

    def _pe_rowtile_guard(self, lhsT, out):
        S = self.S
        st = S.region(lhsT)
        k = st[2] - st[1]
        kr = 32 if k <= 32 else (64 if k <= 64 else 128)
        rows = (st[1], st[1] + kr)
        oreg = S.region(out)
        last = getattr(self, '_last_pe', None)
        if last is not None:
            lrows, loreg, lins, linc = last
            disjoint = rows[1] <= lrows[0] or lrows[1] <= rows[0]
            samebank = (loreg[0] == oreg[0]) and loreg[3] < oreg[4] and oreg[3] < loreg[4]
            if disjoint and samebank:
                if not linc:
                    S.cnt['pe'] += 1
                    lins.then_inc(S.sem['pe'], 1)
                S.wait('pe', S.sem['pe'], S.cnt['pe'])
        return rows, oreg

    def mm(self, out, lhsT, rhs, start=True, stop=True, inc=None):
        if inc is None:
            inc = stop
        rows, oreg = self._pe_rowtile_guard(lhsT, out)
        ins = self.S.emit('pe', [lhsT, rhs], [out],
                          lambda g: g.matmul(out, lhsT=lhsT, rhs=rhs, start=start, stop=stop), inc=inc)
        self._last_pe = (rows, oreg, ins, inc)
        return ins

    def tr(self, out, in_, inc=True, bf=False):
        n = in_.shape[0]
        ident = self.IDB[0:n, 0:n] if bf else self.CST[0:n, CC['IDENT']:CC['IDENT'] + n]
        rows, oreg = self._pe_rowtile_guard(in_, out)
        ins = self.S.emit('pe', [in_, ident], [out],
                          lambda g: g.transpose(out=out, in_=in_, identity=ident), inc=inc)
        self._last_pe = (rows, oreg, ins, inc)
        return ins

    def memset(self, e, ap, v):
        return self.S.emit(e, [], [ap], lambda g: g.memset(ap, v))

    def scan(self, out, d0, d1, init, op0, op1):
        rd = [d0, d1] + ([] if isinstance(init, (int, float)) else [init])
        return self.S.emit('dve', rd, [out], lambda g: g.tensor_tensor_scan(out=out, data0=d0, data1=d1, initial=init, op0=op0, op1=op1))

    def recip(self, out, in_):
        return self.S.emit('dve', [in_], [out], lambda g: g.reciprocal(out=out, in_=in_))

    def reduce(self, out, in_, op, axis=AX.X):
        return self.S.emit('dve', [in_], [out], lambda g: g.tensor_reduce(out=out, in_=in_, axis=axis, op=op))

    def dump(self, name, ap):
        if not self.debug or name in self.dbg_map:
            return
        if getattr(self, 'dbg_filter', None) is not None and not any(name.startswith(p) for p in self.dbg_filter):
            return
        shp = list(ap.shape)
        npart, nfree = shp[0], _prod(shp[1:])
        stage = self.XN[1]
        assert nfree <= 1024
        dst = self.V(stage, 0, npart, 0, [[_prod(shp[i + 1:]), shp[i]] for i in range(1, len(shp))])
        self.cp('dve', dst, ap)
        self.S.dma('sp', self.dbg[0:npart, self.dbg_off:self.dbg_off + nfree], stage[0:npart, 0:nfree], allow_slow_non_contiguous=True)
        self.dbg_map[name] = (self.dbg_off, npart, shp[1:])
        self.dbg_off += nfree

    def ppc(self, l, key, j=0, rows=128):
        c = l * PL + PC[key] + j
        return self.PP[0:rows, c:c + 1]

    def dvc(self, l, key, j=0, rows=128):
        c = l * DL + DC[key] + j
        return self.DV[0:rows, c:c + 1]

    def setup(self):
        S = self.S
        S.dma('sp', self.CST[:, :], self.cst[:, :])
        S.dma('sp', self.PP[:, :], self.pp[:, :])
        S.dma('sp', self.LORA[:, :, :, :], self.lora[:, :, :, :])
        S.dma('sp', self.LRUW[:, :, :, :], self.lruw[:, :, :, :])
        self.memset('dve', self.DV[:, :], 0.0)
        self.cp('dve', self.IDB[:, :], self.CST[:, CC['IDENT']:CC['IDENT'] + 128])
        self.cp('dve', self.ISTKB[:, :], self.CST[:, CC['ISTK']:CC['ISTK'] + 64])
        for l in range(2):
            self.ts('dve', self.DV[:, l * DL + DC['OMKA']: l * DL + DC['OMKA'] + 3],
                    self.PP[:, l * PL + PC['R_KA']: l * PL + PC['R_KA'] + 3], -1.0, 1.0, ALU.mult, ALU.add)
            self.ts('dve', self.DV[:, l * DL + DC['OMMU']: l * DL + DC['OMMU'] + 11],
                    self.PP[:, l * PL + PC['R_MU']: l * PL + PC['R_MU'] + 11], -1.0, 1.0, ALU.mult, ALU.add)
            lam = self.PP[:, l * PL + PC['L_LAM']: l * PL + PC['L_LAM'] + 4]
            t0 = self.TMPS[:, 0:4]
            self.act(t0, lam, AF.Exp, scale=-1.0)
            self.act(t0, t0, AF.Ln, bias=1.0)
            self.ts('dve', self.DV[:, l * DL + DC['CLAM']: l * DL + DC['CLAM'] + 4], t0, -8.0, None, ALU.mult)
            self.ts('dve', self.DV[:, l * DL + DC['C2LAM']: l * DL + DC['C2LAM'] + 4], t0, -16.0, None, ALU.mult)
            self.ts('dve', self.DV[0:6, l * DL + DC['NBF']: l * DL + DC['NBF'] + 2],
                    self.PP[0:6, l * PL + PC['M_BF']: l * PL + PC['M_BF'] + 2], -1.0, None, ALU.mult)
            for hp in range(3):
                self.ts('dve', self.RKD[:, l, hp, :], self.CST[:, CC['BONES']:CC['BONES'] + 128],
                        self.ppc(l, 'R_RK', hp), None, ALU.mult)
        self.memset('dve', self.XL[:, :, 0:2], 0.0)

    def load_layer(self, l):
        S = self.S
        wsrc = self.w_in[l].rearrange("(kc p) c -> p kc c", p=128)
        wo = self.w_out[l].rearrange("(kc p) c -> p kc c", p=128)
        pieces = [(self.W_IN, wsrc, c0, min(256, IN_COLS - c0)) for c0 in range(0, IN_COLS, 256)]
        pieces += [(self.W_OUT, wo, c0, 256) for c0 in range(0, D, 256)]
        def emit_pieces(lo, hi):
            for i in range(lo, min(hi, len(pieces))):
                dst, src, c0, n = pieces[i]
                stg = self.WSTG[i % 3]
                S.dma('sp', stg[:, :, 0:n], src[:, :, c0:c0 + n])
                self.cp('pool', dst[:, :, c0:c0 + n], stg[:, :, 0:n])
        if self.stop == 'load_w':
            raise StopIteration
        S.dma('sp', self.SCT[:, :, :], self.cc[:, :, :])
        S.dma('sp', self.BMT[:, :, :], self.bmodT[:, :, :])
        self.act(self.SCT[:, :, :], self.SCT[:, :, :], AF.Silu)
        for m in range(2):
            for kc in range(8):
                src = self.V(self.SCT, 0, 128, kc * 2 + m, [[0, 128]])
                self.cp('dve', self.SCB[:, m, kc, :], src)
        wm = self.w_mod[l].rearrange("(kc p) c -> p kc c", p=128)
        for blk in range(6):
            self.WM = self.WMs[blk % 2]
            S.dma('sp', self.WM[:, :, :], wm[:, :, blk * 512:(blk + 1) * 512])
            emit_pieces(blk * 4, blk * 4 + 4)
            if blk < 4:
                ps = self.pq()
                for kc in range(8):
                    self.mm(ps[0:2, :], self.SCT[:, kc, :], self.WM[:, kc, :], start=(kc == 0), stop=(kc == 7))
                self.cp('act', self.MODROW[0:2, :], ps[0:2, :])
                ps2 = self.pq()
                for j in range(4):
                    self.tr(ps2[:, j * 2:j * 2 + 2], self.MODROW[0:2, j * 128:(j + 1) * 128])
                o = self.MODT[:, blk * 4:(blk + 1) * 4, :]
                bsrc = self.V(self.BMT, 0, 128, l * 16 + blk * 4, [[1, 4], [0, 2]])
                self.tt('dve', o, self.V(ps2, 0, 128, 0, [[2, 4], [1, 2]]), bsrc, ALU.add)
            else:
                half = blk - 4
                for m in range(2):
                    ps = self.pq()
                    for kc in range(8):
                        self.mm(ps[:, :], self.SCB[:, m, kc, :], self.WM[:, kc, :], start=(kc == 0), stop=(kc == 7))
                    self.cp('act', self.GATEB[:, m, half * 512:(half + 1) * 512], ps[:, :])
        if self.stop == 'load_m':
            raise StopIteration
        for m in range(2):
            self.cp('dve', self.SH[:, m, :], self.V(self.MODT, 0, 128, m, [[2, 8]]))
            t0 = self.TMPS[:, 8:16]
            self.ts('dve', t0, self.V(self.MODT, 0, 128, 16 + m, [[2, 8]]), 1.0, None, ALU.add)
            self.tt('dve', self.GS[:, m, :], t0, self.PP[:, l * PL + PC['G_PRE']: l * PL + PC['G_PRE'] + 8], ALU.mult)
        if self.stop == 'load_g':
            raise StopIteration
        S.dma('sp', self.BG[:, :], bass.AP(self.bmodg.tensor, l * D, [[0, 128], [1, D]]))
        S.dma('sp', self.GP[:, :], bass.AP(self.gpost.tensor, l * D, [[0, 128], [1, D]]))
        if self.stop == 'load_b':
            raise StopIteration
        for m in range(2):
            self.tt('dve', self.GATEB[:, m, :], self.GATEB[:, m, :], self.BG[:, :], ALU.add)
            self.tt('dve', self.GATEB[:, m, :], self.GATEB[:, m, :], self.GP[:, :], ALU.mult)

    def proj_fm(self, c0, ncols, t_off, ntok, evac):
        ps = self.pq()
        for kc in range(8):
            self.mm(ps[0:ncols, 0:ntok], self.W_IN[:, kc, c0:c0 + ncols], self.HT[:, kc, t_off:t_off + ntok],
                    start=(kc == 0), stop=(kc == 7))
        evac(ps[0:ncols, 0:ntok])

    def proj_tm(self, c0, ncols, t_off, evac):
        ps = self.pq()
        for kc in range(8):
            self.mm(ps[0:64, 0:ncols], self.HT[:, kc, t_off:t_off + 64], self.W_IN[:, kc, c0:c0 + ncols],
                    start=(kc == 0), stop=(kc == 7))
        evac(ps[0:64, 0:ncols])

    def visit(self, l, mod, xsrc, xdst, row0, T, t0, dirs, grid, seq_idx, first, last, mode='full'):
        S = self.S
        w0 = max(0, t0 - 64)
        w1 = min(T, t0 + TB + 64)
        W = w1 - w0
        co = t0 - w0
        do_f = 0 in dirs
        prompt = (mod == 0)
        if mode != 'load':
            ntile = (W + 127) // 128
            for i in range(ntile):
                n = min(128, W - i * 128)
                xw, xn = self.XW[i % 2], self.XN[i % 2]
                r = row0 + w0 + i * 128
                S.dma('sp', xw[0:n, :], xsrc[r:r + n, :])
                ssq = self.TMPS[0:n, 16 + i:17 + i]
                self.act(xn[0:n, :], xw[0:n, :], AF.Square, accum=ssq)
                if self.stop == 'n1':
                    raise StopIteration
                rs = self.TMPS[0:n, 20 + i:21 + i]
                self.ts('dve', rs, ssq, 1.0 / D, EPS, ALU.mult, ALU.add)
                self.act(rs, rs, AF.Sqrt)
                self.recip(rs, rs)
                if self.stop == 'n2':
                    raise StopIteration
                self.act(xn[0:n, :], xw[0:n, :], AF.Copy, scale=rs)
                if self.stop == 'n3':
                    raise StopIteration
                for half in range(2):
                    ps = self.pq()
                    for j in range(4):
                        kc = half * 4 + j
                        self.tr(ps[:, j * 128:j * 128 + n], xn[0:n, kc * 128:(kc + 1) * 128])
                    if self.stop == 'n4':
                        raise StopIteration
                    for j in range(4):
                        kc = half * 4 + j
                        o = self.HT[:, kc, i * 128:i * 128 + n]
                        if half == 0:
                            self.ts('dve', o, ps[:, j * 128:j * 128 + n], self.GS[:, mod, kc:kc + 1], self.SH[:, mod, kc:kc + 1], ALU.mult, ALU.add)
                        else:
                            self.act(o, ps[:, j * 128:j * 128 + n], AF.Identity, bias=self.SH[:, mod, kc:kc + 1], scale=self.GS[:, mod, kc:kc + 1])
        if self.stop in ('norm', 'n5a', 'n5d'):
            raise StopIteration
        self.stage_mlstm(l, t0, co, dirs, prompt, seq_idx, first, last, mode)
        if self.stop == 'mlstm':
            raise StopIteration
        self.stage_lru(l, t0, co, W, w0, w1, T, dirs, prompt, seq_idx, first, last, mode)
        if self.stop == 'lru':
            raise StopIteration
        self.stage_rwkv(l, t0, co, W, w0, w1, T, dirs, grid, prompt, seq_idx, first, last, mode)
        for kc in range(8):
            self.dump('mix%d' % kc, self.MIXT[:, kc, :])
        if self.stop == 'rwkv':
            raise StopIteration
        if do_f:
            self.stage_out(l, mod, xsrc, xdst, row0 + t0)
        if self.stop == 'out':
            raise StopIteration

    def stage_mlstm(self, l, t0, co, dirs, prompt, seq_idx, first, last, mode='full'):
        S = self.S
        do_f = 0 in dirs
        if mode != 'load':
            for d in (sorted(set(dirs) | {0}) if mode == 'store' else dirs):
                g = self.mg[d]
                self.proj_fm(1920 + d * 6, 6, co, 256, lambda ps, g=g, d=d: self.act(g["gi"][:, :], ps, AF.Identity, bias=self.ppc(l, 'M_BI', d, 6)))
                def ev_f(ps, g=g, d=d):
                    self.act(g["lf"][:, :], ps, AF.Exp, bias=self.dvc(l, 'NBF', d, 6), scale=-1.0)
                    self.act(g["lf"][:, :], g["lf"][:, :], AF.Ln, bias=1.0)
                    self.ts('dve', g["lf"][:, :], g["lf"][:, :], -1.0, None, ALU.mult)
                self.proj_fm(1932 + d * 6, 6, co, 256, ev_f)
        bi = t0 // 256
        ml_items = [(self.QT[:, :, :], 'QT'), (self.KT[:, :, :], 'KT'), (self.KTOK[:, :, :], 'KTOK'), (self.VAUG[:, :, :, :], 'VAUG'),
                    (self.GOZ[:, :, :], 'GOZ'), (self.mg[0]["gi"][:, :], 'GI'), (self.mg[0]["lf"][:, :], 'LF')]
        if mode == 'load':
            for ap_, k_ in ml_items:
                S.dma('sp', ap_, self.sc[k_][bi])
        for d in dirs:
            if first[d]:
                if prompt:
                    self.memset('dve', self.mC[d][:, :, :], 0.0)
                    self.memset('dve', self.mM[d][:, :], 0.0)
                else:
                    S.dma('sp', self.mC[d][:, :, :], self.st_mC[l, d])
                    S.dma('sp', self.mM[d][:, :], self.st_mm[:, l, d:d + 1], allow_slow_non_contiguous=True)
        if do_f and not (1 in dirs):
            S.dma('sp', self.HB[:, :, :], self.sHB[t0:t0 + 256, :].rearrange("(c s) f -> s c f", s=64))
        for d in dirs:
            g = self.mg[d]
            v3 = lambda t: self.V(t, 0, 6, 0, [[64, 4], [1, 64]])
            self.scan(g["pre"][:, :], self.CST[0:6, CC['RMASK']:CC['RMASK'] + 256], g["lf"][:, :], 0.0, ALU.mult, ALU.add)
            bL = self.V(g["pre"], 0, 6, 63, [[64, 4]])
            bLb = self.V(g["pre"], 0, 6, 63, [[64, 4], [0, 64]])
            if d == 0:
                bsrc = g["pre"]
            else:
                self.tt('dve', v3(g["bb"]), bLb, v3(g["pre"]), ALU.subtract)
                self.tt('dve', g["bb"][:, :], g["bb"][:, :], g["lf"][:, :], ALU.add)
                bsrc = g["bb"]
            self.tt('dve', g["gg"][:, :], g["gi"][:, :], bsrc[:, :], ALU.subtract)
            self.reduce(g["mx"][:, :], v3(g["gg"]), ALU.max)
            if d == 0:
                mo, mxv, blv = g["mch"][:, :], g["mx"][:, :], bL
            else:
                mo = self.V(g["mch"], 0, 6, 3, [[-1, 4]])
                mxv = self.V(g["mx"], 0, 6, 3, [[-1, 4]])
                blv = self.V(g["pre"], 0, 6, 63 + 3 * 64, [[-64, 4]])
            self.scan(mo, mxv, blv, self.mM[d][:, 0:1], ALU.max, ALU.add)
            if d == 0:
                self.cp('dve', g["mprev"][:, 1:4], g["mch"][:, 0:3])
                self.cp('dve', g["mprev"][:, 0:1], self.mM[d][:, 0:1])
                mfin = g["mch"][:, 3:4]
            else:
                self.cp('dve', g["mprev"][:, 0:3], g["mch"][:, 1:4])
                self.cp('dve', g["mprev"][:, 3:4], self.mM[d][:, 0:1])
                mfin = g["mch"][:, 0:1]
            self.tt('dve', g["MM"][:, :], g["mprev"][:, :], g["mx"][:, :], ALU.max)
            self.tt('dve', g["dec"][:, :], g["mprev"][:, :], g["MM"][:, :], ALU.subtract)
            self.act(g["dec"][:, :], g["dec"][:, :], AF.Exp)
            self.cp('dve', self.mM[d][:, 0:1], mfin)
            MMb = self.V(g["MM"], 0, 6, 0, [[1, 4], [0, 64]])
            self.tt('dve', v3(g["ee"]), v3(g["gg"]), MMb, ALU.subtract)
            self.act(g["ee"][:, :], g["ee"][:, :], AF.Exp)
            self.tt('dve', v3(g["fl"]), v3(bsrc), MMb, ALU.add)
            self.act(g["fl"][:, :], g["fl"][:, :], AF.Exp, scale=-1.0)
        if mode != 'load':
            for hp in range(3):
                self.proj_fm(hp * 128, 128, co, 256, lambda ps, hp=hp: self.cp('act', self.QT[:, hp, :], ps))
                self.proj_fm(384 + hp * 128, 128, co, 256, lambda ps, hp=hp: self.act(self.KT[:, hp, :], ps, AF.Copy, scale=0.125))
            if do_f or mode == 'store':
                for hp in range(3):
                    def ev_o(ps, hp=hp):
                        self.act(self.GOZ[:, hp, :], ps, AF.Sigmoid)
                    self.proj_fm(1152 + hp * 128, 128, co, 256, ev_o)
                    def ev_z2(ps, hp=hp):
                        tz = self.ZT[hp % 2]
                        self.act(tz[:, :], ps, AF.Silu)
                        self.tt('dve', self.GOZ[:, hp, :], self.GOZ[:, hp, :], tz[:, :], ALU.mult)
                    self.proj_fm(1536 + hp * 128, 128, co, 256, ev_z2)
            for c in range(4):
                self.proj_tm(384, 384, co + c * 64, lambda ps, c=c: self.act(self.KTOK[:, c, :], ps, AF.Copy, scale=0.125))
                def ev_v(ps, c=c):
                    self.cp('dve', self.VAUG[:, c, :, 0:64], self.V(ps.tensor, 0, 64, 0, [[64, 6], [1, 64]]))
                self.proj_tm(768, 384, co + c * 64, ev_v)
            self.memset('dve', self.VAUG[:, :, :, 64:65], 1.0)
        for d in dirs:
            g = self.mg[d]
            ps = self.pq()
            for c in range(4):
                self.tr(ps[0:64, c * 6:c * 6 + 6], g["ee"][:, c * 64:(c + 1) * 64])
                self.tr(ps[0:64, 24 + c * 6:24 + c * 6 + 6], g["fl"][:, c * 64:(c + 1) * 64])
            self.cp('dve', g["etok"][:, :, :], self.V(ps, 0, 64, 0, [[6, 4], [1, 6]]))
            self.cp('dve', g["fltok"][:, :, :], self.V(ps, 0, 64, 24, [[6, 4], [1, 6]]))
            self.tt('dve', g["X2"][:, :, :], self.V(g["dec"], 0, 6, 0, [[1, 4], [0, 3]]),
                    self.V(self.CST, 0, 6, CC['PSEL'], [[0, 4], [1, 3]]), ALU.mult)
            ps2 = self.pq()
            self.mm(ps2[:, 0:12], self.CST[0:6, CC['LSEL']:CC['LSEL'] + 128], self.V(g["X2"], 0, 6, 0, [[1, 12]]))
            self.cp('dve', g["decb"][:, :, :], self.V(ps2, 0, 128, 0, [[3, 4], [1, 3]]))
        if mode == 'store':
            for ap_, k_ in ml_items:
                S.dma('sp', self.sc[k_][bi], ap_)
        for d in sorted(dirs, reverse=True):
            g = self.mg[d]
            mask = self.CST[0:64, CC['MUI']:CC['MUI'] + 64] if d == 0 else self.CST[0:64, CC['MLI']:CC['MLI'] + 64]
            maskb = self.V(self.CST, 0, 64, CC['MUI'] if d == 0 else CC['MLI'], [[0, 6], [1, 64]])
            for j in range(4):
                c = j if d == 0 else 3 - j
                cs = slice(c * 64, (c + 1) * 64)
                stsb, vp = self.STSB[j % 2], self.VP[j % 2]
                ps = self.pq()
                for h in (0, 2, 4, 1, 3, 5):
                    hp, pb = h // 2, 64 * (h % 2)
                    self.mm(ps[0:64, h * 64:(h + 1) * 64], self.KT[pb:pb + 64, hp, cs], self.QT[pb:pb + 64, hp, cs], inc=(h == 5))
                self.tt('dve', stsb[:, :, :], self.V(ps, 0, 64, 0, [[64, 6], [1, 64]]), maskb, ALU.mult)
                self.tt('dve', vp[:, :, :], self.VAUG[:, c, :, :], self.V(g["etok"], 0, 64, c * 6, [[1, 6], [0, 65]]), ALU.mult)
                self.tt('dve', self.CDBF[:, :, :], self.mC[d][:, :, :], self.V(g["decb"], 0, 128, c * 3, [[1, 3], [0, 65]]), ALU.mult)
                self.tt('dve', self.CDEC[:, :, :], self.mC[d][:, :, :], self.V(g["decb"], 0, 128, c * 3, [[1, 3], [0, 65]]), ALU.mult)
                ph = self.pq()
                for h in (0, 2, 4, 1, 3, 5):
                    hp, pb = h // 2, 64 * (h % 2)
                    o = ph[0:64, h * 65:(h + 1) * 65]
                    self.mm(o, stsb[:, h, :], vp[:, h, :], start=True, stop=False)
                    self.mm(o, self.QT[pb:pb + 64, hp, cs], self.CDBF[pb:pb + 64, hp, :], start=False, stop=True, inc=(h == 5))
                pc = self.pq()
                for h in range(6):
                    hp, pb = h // 2, 64 * (h % 2)
                    self.mm(pc[pb:pb + 64, hp * 65:(hp + 1) * 65], self.KTOK[:, c, h * 64:(h + 1) * 64], vp[:, h, :], inc=(h == 5))
                self.tt('dve', self.mC[d][:, :, :], self.CDEC[:, :, :], self.V(pc, 0, 128, 0, [[65, 3], [1, 65]]), ALU.add)
                self.act(self.DN[:, :], self.V(ph, 0, 64, 64, [[65, 6]]), AF.Abs)
                self.tt('dve', self.DN[:, :], self.DN[:, :], g["fltok"][:, c, :], ALU.max)
                self.recip(self.RDN[:, :], self.DN[:, :])
                hsrc = self.V(ph, 0, 64, 0, [[65, 6], [1, 64]])
                rb = self.V(self.RDN, 0, 64, 0, [[1, 6], [0, 64]])
                hbv = self.V(self.HB, 0, 64, c * 384, [[64, 6], [1, 64]])
                if d == 1:
                    self.tt('dve', hbv, hsrc, rb, ALU.mult)
                else:
                    self.tt('dve', self.HD[:, :, :], hsrc, rb, ALU.mult)
                    self.tt('dve', hbv, hbv, self.HD[:, :, :], ALU.add)
            if last[d] and prompt:
                S.dma('sp', self.o_mC[seq_idx, l, d], self.mC[d][:, :, :])
                S.dma('sp', self.o_mm[seq_idx, l, d], self.mM[d][:, 0:1])
        if not do_f:
            S.dma('sp', self.sHB[t0:t0 + 256, :].rearrange("(c s) f -> s c f", s=64), self.HB[:, :, :])
            return
        self.tt('dve', self.SQ[:, :, :], self.HB[:, :, :], self.HB[:, :, :], ALU.mult)
        self.reduce(self.SSQ[:, :], self.V(self.SQ, 0, 64, 0, [[64, 24], [1, 64]]), ALU.add)
        self.ts('dve', self.SSQ[:, :], self.SSQ[:, :], 1.0 / 64, EPS, ALU.mult, ALU.add)
        self.act(self.SSQ[:, :], self.SSQ[:, :], AF.Sqrt)
        self.recip(self.RSTD[:, :], self.SSQ[:, :])
        self.tt('dve', self.V(self.SQ, 0, 64, 0, [[64, 24], [1, 64]]), self.V(self.HB, 0, 64, 0, [[64, 24], [1, 64]]),
                self.V(self.RSTD, 0, 64, 0, [[1, 24], [0, 64]]), ALU.mult)
        for hp in range(3):
            ps = self.pq()
            for c in range(4):
                self.tr(ps[:, c * 64:(c + 1) * 64], self.SQ[:, c, hp * 128:(hp + 1) * 128], inc=(c == 3))
            self.stt(self.MIXT[:, hp, :], ps[:, 0:256], self.ppc(l, 'M_NORM', hp), self.GOZ[:, hp, :], ALU.mult, ALU.mult)

    def stage_lru(self, l, t0, co, W, w0, w1, T, dirs, prompt, seq_idx, first, last, mode='full'):
        S = self.S
        do_f = 0 in dirs
        if mode != 'load':
            for pr in range(2):
                self.proj_fm(3736 + pr * 128, 128, 0, W, lambda ps, pr=pr: self.cp('act', self.XL[:, pr, 2:2 + W], ps))
                if do_f or mode == 'store':
                    self.proj_fm(3992 + pr * 128, 128, co, 256, lambda ps, pr=pr: self.act(self.LZT[:, pr, :], ps, AF.Silu))
            if w1 == T:
                self.memset('dve', self.XL[:, :, 2 + W:2 + W + 1], 0.0)
            if w0 == 0:
                self.memset('dve', self.XL[:, :, 0:2], 0.0)
            for pr in range(2):
                self.ts('dve', self.XC[:, pr, :], self.XL[:, pr, co:co + 256], self.ppc(l, 'L_CONV', 0 * 2 + pr), self.ppc(l, 'L_CONVB', pr), ALU.mult, ALU.add)
                for j in range(1, 4):
                    self.stt(self.XC[:, pr, :], self.XL[:, pr, co + j:co + j + 256], self.ppc(l, 'L_CONV', j * 2 + pr), self.XC[:, pr, :], ALU.mult, ALU.add)
        bi = t0 // 256
        lr_items = [(self.XC[:, :, :], 'XC'), (self.LZT[:, :, :], 'LZT')]
        if mode == 'store':
            for ap_, k_ in lr_items:
                S.dma('sp', self.sc[k_][bi], ap_)
        if mode == 'load':
            for ap_, k_ in lr_items:
                S.dma('sp', ap_, self.sc[k_][bi])
        for d in dirs:
            if first[d]:
                if prompt:
                    self.memset('dve', self.lS[d][:, :], 0.0)
                else:
                    S.dma('sp', self.lS[d][:, :], self.st_l[:, l, d, :])
        if do_f and not (1 in dirs):
            S.dma('sp', self.HL[1][:, :, :], self.sLB[:, :, t0:t0 + 256])
        combos = [(d, pr) for d in sorted(dirs, reverse=True) for pr in range(2)]
        LT = lambda d, n, pr: self.ltd[d][n][:, pr, :]
        pss = {}
        for i, (d, pr) in enumerate(combos):
            ps = self.PA if i % 2 == 0 else self.PB
            off = (i // 2) * 512
            pss[(d, pr)] = (ps, off)
            self.mm(ps[:, off:off + 256], self.LRUW[:, l, (0 * 2 + d) * 2 + pr, :], self.XC[:, pr, :])
            self.mm(ps[:, off + 256:off + 512], self.LRUW[:, l, (1 * 2 + d) * 2 + pr, :], self.XC[:, pr, :])
        for (d, pr) in combos:
            ps, off = pss[(d, pr)]
            self.act(LT(d, "rg", pr), ps[:, off:off + 256], AF.Sigmoid, bias=self.ppc(l, 'L_BA', d * 2 + pr))
            self.act(LT(d, "ig", pr), ps[:, off + 256:off + 512], AF.Sigmoid, bias=self.ppc(l, 'L_BX', d * 2 + pr))
        for (d, pr) in combos:
            self.act(LT(d, "aa", pr), LT(d, "rg", pr), AF.Exp, scale=self.dvc(l, 'CLAM', d * 2 + pr))
            self.act(LT(d, "a2", pr), LT(d, "rg", pr), AF.Exp, scale=self.dvc(l, 'C2LAM', d * 2 + pr))
        for (d, pr) in combos:
            self.ts('dve', LT(d, "a2", pr), LT(d, "a2", pr), -1.0, 1.0, ALU.mult, ALU.add)
            self.tt('dve', LT(d, "bt", pr), LT(d, "ig", pr), self.XC[:, pr, :], ALU.mult)
        for (d, pr) in combos:
            self.act(LT(d, "a2", pr), LT(d, "a2", pr), AF.Sqrt)
        for (d, pr) in combos:
            self.tt('dve', LT(d, "bt", pr), LT(d, "bt", pr), LT(d, "a2", pr), ALU.mult)
        for (d, pr) in combos:
            if d == 0:
                self.scan(self.HL[0][:, pr, :], LT(0, "aa", pr), LT(0, "bt", pr), self.lS[0][:, pr:pr + 1], ALU.mult, ALU.add)
                self.cp('act', self.lS[0][:, pr:pr + 1], self.HL[0][:, pr, 255:256])
            else:
                rv = lambda t: self.V(t, 0, 128, pr * 256 + 255, [[-1, 256]])
                self.scan(rv(self.HL[1]), rv(self.ltd[1]["aa"]), rv(self.ltd[1]["bt"]), self.lS[1][:, pr:pr + 1], ALU.mult, ALU.add)
                self.cp('act', self.lS[1][:, pr:pr + 1], self.HL[1][:, pr, 0:1])
        for d in sorted(dirs, reverse=True):
            if last[d] and prompt:
                S.dma('sp', self.o_l[seq_idx, l, d], self.lS[d][:, :])
        if not do_f:
            S.dma('sp', self.sLB[:, :, t0:t0 + 256], self.HL[1][:, :, :])
            return
        self.tt('dve', self.HL[0][:, :, :], self.HL[0][:, :, :], self.HL[1][:, :, :], ALU.add)
        self.tt('dve', self.MIXT[:, 6:8, :], self.HL[0][:, :, :], self.LZT[:, :, :], ALU.mult)

    def stage_rwkv(self, l, t0, co, W, w0, w1, T, dirs, grid, prompt, seq_idx, first, last, mode='full'):
        S = self.S
        do_f = 0 in dirs
        if mode != 'load':
            for ch in range(11):
                self.proj_fm(1944 + ch * 128, 128, 0, W, lambda ps, ch=ch: self.cp('act' if ch % 2 else 'dve', self.URS[:, ch, 0:W], ps))
            if do_f or mode == 'store':
                for hp in range(3):
                    self.proj_fm(3352 + hp * 128, 128, co, 256, lambda ps, hp=hp: self.act(self.RZT[:, hp, :], ps, AF.Silu))
            U3 = lambda off, n: self.V(self.URS, 0, 128, off, [[384, 11], [1, n]])
            B3 = lambda off, n: self.V(self.BLK, 0, 128, off, [[256, 11], [1, n]])
            if not grid:
                self.cp('dve', B3(1, 255), U3(0, 255))
                self.memset('dve', B3(0, 1), 0.0)
                self.tt('dve', B3(0, 255), B3(0, 255), U3(1, 255), ALU.add)
                wsh = 0.5
            else:
                U4 = lambda off, r, n: self.V(self.URS, 0, 128, off, [[384, 11], [64, r], [1, n]])
                B4 = lambda off, r, n: self.V(self.BLK, 0, 128, off, [[256, 11], [64, r], [1, n]])
                self.cp('dve', B4(1, 4, 63), U4(co, 4, 63))
                self.memset('dve', B4(0, 4, 1), 0.0)
                self.tt('dve', B4(0, 4, 63), B4(0, 4, 63), U4(co + 1, 4, 63), ALU.add)
                if t0 > 0:
                    self.tt('dve', B3(0, 256), B3(0, 256), U3(co - 64, 256), ALU.add)
                else:
                    self.tt('dve', B3(64, 192), B3(64, 192), U3(0, 192), ALU.add)
                if t0 + TB < T:
                    self.tt('dve', B3(0, 256), B3(0, 256), U3(co + 64, 256), ALU.add)
                else:
                    self.tt('dve', B3(0, 192), B3(0, 192), U3(co + 64, 192), ALU.add)
                wsh = 0.25
            mu = self.V(self.PP, 0, 128, l * PL + PC['R_MU'], [[1, 11], [0, 256]])
            self.tt('dve', B3(0, 256), B3(0, 256), mu, ALU.mult)
            omm = self.V(self.DV, 0, 128, l * DL + DC['OMMU'], [[1, 11], [0, 256]])
            self.tt('dve', U3(co, 256), U3(co, 256), omm, ALU.mult)
            self.stt(B3(0, 256), B3(0, 256), wsh, U3(co, 256), ALU.mult, ALU.add)
            for nm, ch in (('blk_r', 0), ('blk_k', 3), ('blk_v', 6), ('blk_wl', 9), ('blk_al', 10)):
                self.dump(nm, self.BLK[:, ch, :])
            rt = self.rt
            kk = self.V(self.PP, 0, 128, l * PL + PC['R_KK'], [[1, 3], [0, 256]])
            kap = rt["d"]
            self.tt('dve', kap[:, :, :], self.BLK[:, 3:6, :], kk, ALU.mult)
            ksq = rt["E"]
            self.tt('dve', ksq[:, :, :], kap[:, :, :], kap[:, :, :], ALU.mult)
            for hp in range(3):
                self.mm(self.PA[:, hp * 256:(hp + 1) * 256], self.CST[:, CC['BONES']:CC['BONES'] + 128], ksq[:, hp, :])
            self.act(ksq[:, :, :], self.V(self.PA, 0, 128, 0, [[256, 3], [1, 256]]), AF.Sqrt)
            self.ts('dve', ksq[:, :, :], ksq[:, :, :], 1e-12, None, ALU.max)
            self.recip(ksq[:, :, :], ksq[:, :, :])
            self.tt('dve', self.KH[:, :, :], kap[:, :, :], ksq[:, :, :], ALU.mult)
            self.dump('kh', self.KH[:, 0, :])
            self.bd_fill(self.BDV, lambda par: self.V(self.BLK, par * 64, 64, 6 * 256, [[256, 3], [64, 4], [1, 64]]))
            for c in range(4):
                for hp in range(3):
                    self.mm(self.PB[:, (c * 3 + hp) * 64:(c * 3 + hp + 1) * 64], self.BDV[:, hp, c, :], self.ISTKB[:, :])
            self.cp('act', self.V(self.VSTK, 0, 128, 0, [[1, 768]]), self.PB[:, 0:768])
        bi = t0 // 256
        rw_items = [(self.BLK[:, :, :], 'BLK'), (self.RZT[:, :, :], 'RZT'), (self.KH[:, :, :], 'KH'), (self.VSTK[:, :, :, :], 'VSTK')]
        if mode == 'store':
            for ap_, k_ in rw_items:
                S.dma('sp', self.sc[k_][bi], ap_)
        if mode == 'load':
            for ap_, k_ in rw_items:
                S.dma('sp', ap_, self.sc[k_][bi])
        for d in dirs:
            if first[d]:
                if prompt:
                    self.memset('dve', self.rH[d][:, :, :], 0.0)
                else:
                    S.dma('sp', self.rH[d][:, :, :], self.st_rH[l, d])
        ybflat = self.V(self.YB, 0, 128, 0, [[1, 768]])
        if do_f and not (1 in dirs):
            S.dma('sp', ybflat, self.sYB[t0 // 256])
        for d in sorted(dirs, reverse=True):
            self.rwkv_dir(l, d, seq_idx, prompt, last)
        if not do_f:
            S.dma('sp', self.sYB[t0 // 256], ybflat)
            return
        ft = self.ft
        self.tt('dve', self.YSQ[:, :, :, :], self.YB[:, :, :, :], self.YB[:, :, :, :], ALU.mult)
        self.reduce(self.rSSQ[:, :], self.V(self.YSQ, 0, 128, 0, [[64, 12], [1, 64]]), ALU.add)
        self.ts('dve', self.rSSQ[:, :], self.rSSQ[:, :], 1.0 / 64, EPS, ALU.mult, ALU.add)
        self.act(self.rSSQ[:, :], self.rSSQ[:, :], AF.Sqrt)
        self.recip(self.rRSTD[:, :], self.rSSQ[:, :])
        self.tt('dve', ft["rk"][:, :, :], self.BLK[:, 0:3, :], self.BLK[:, 3:6, :], ALU.mult)
        for hp in range(3):
            self.mm(self.PB[:, hp * 256:(hp + 1) * 256], self.RKD[:, l, hp, :], ft["rk"][:, hp, :])
        self.tt('dve', ft["bon"][:, :, :], self.V(self.PB, 0, 128, 0, [[256, 3], [1, 256]]), self.BLK[:, 6:9, :], ALU.mult)
        self.memset('pool', self.YBD[:, :, :], 0.0)
        for c in range(4):
            for par in range(2):
                self.tt('dve', self.V(self.YBD, par * 64, 64, par * 64, [[128, 3], [1, 64]]),
                        self.V(self.YB, par * 64, 64, c * 192, [[64, 3], [1, 64]]),
                        self.V(self.rRSTD, par * 64, 64, c * 3, [[1, 3], [0, 64]]), ALU.mult)
            for hp in range(3):
                self.mm(self.PA[:, hp * 256 + c * 64: hp * 256 + (c + 1) * 64], self.YBD[:, hp, :],
                        self.CST[:, CC['ISTK']:CC['ISTK'] + 64])
        for hp in range(3):
            self.stt(ft["t1"][:, hp, :], self.PA[:, hp * 256:(hp + 1) * 256], self.ppc(l, 'R_NORM', hp), ft["bon"][:, hp, :], ALU.mult, ALU.add)
            self.tt('dve', self.MIXT[:, 3 + hp, :], ft["t1"][:, hp, :], self.RZT[:, hp, :], ALU.mult)

    def bd_fill(self, bd, src_of_par, eng='pool'):
        self.memset(eng, bd[:, :, :, :], 0.0)
        for par in range(2):
            self.cp('act' if par == 0 else eng, self.V(bd, par * 64, 64, par * 64, [[512, 3], [128, 4], [1, 64]]), src_of_par(par))

    def rwkv_dir(self, l, d, seq_idx, prompt, last):
        S = self.S
        rt = self.rt
        pb_d = 64 * d
        f3 = lambda t: t[:, :, :]
        v4 = lambda t: self.V(t, 0, 128, 0, [[256, 3], [64, 4], [1, 64]])
        self.act(self.TWL[pb_d:pb_d + 64, :], self.BLK[pb_d:pb_d + 64, 9, :], AF.Tanh)
        for hp in range(3):
            self.mm(self.PA[:, hp * 256:(hp + 1) * 256], self.LORA[pb_d:pb_d + 64, l, 0, hp * 128:(hp + 1) * 128], self.TWL[pb_d:pb_d + 64, :])
        for hp in range(3):
            self.act(rt["sg"][:, hp, :], self.PA[:, hp * 256:(hp + 1) * 256], AF.Sigmoid, bias=self.ppc(l, 'R_W0', d * 3 + hp))
        for hp in range(3):
            self.mm(self.PB[:, hp * 256:(hp + 1) * 256], self.LORA[pb_d:pb_d + 64, l, 1, hp * 128:(hp + 1) * 128], self.BLK[pb_d:pb_d + 64, 10, :])
        for hp in range(3):
            self.act(rt["aa"][:, hp, :], self.PB[:, hp * 256:(hp + 1) * 256], AF.Sigmoid, bias=self.ppc(l, 'R_A0', d * 3 + hp))
        for hp in range(3):
            self.ts('dve', rt["kt"][:, hp, :], rt["aa"][:, hp, :], self.ppc(l, 'R_KA', hp), self.dvc(l, 'OMKA', hp), ALU.mult, ALU.add)
        self.tt('dve', f3(rt["kt"]), f3(rt["kt"]), self.BLK[:, 3:6, :], ALU.mult)
        self.tt('dve', f3(rt["bb"]), self.KH[:, :, :], f3(rt["aa"]), ALU.mult)
        flat = lambda t: self.V(t, 0, 128, 0, [[1, 768]])
        self.scan(flat(rt["cs"]), self.CST[:, CC['RMASK']:CC['RMASK'] + 768], flat(rt["sg"]), 0.0, ALU.mult, ALU.add)
        self.cp('dve', self.GL[:, :, :], self.V(rt["cs"], 0, 128, 63, [[256, 3], [64, 4]]))
        if d == 1:
            csLb = self.V(rt["cs"], 0, 128, 63, [[256, 3], [64, 4], [0, 64]])
            self.tt('dve', v4(rt["d"]), csLb, v4(rt["cs"]), ALU.subtract)
            self.tt('dve', f3(rt["cs"]), f3(rt["d"]), f3(rt["sg"]), ALU.add)
        self.act(f3(rt["E"]), f3(rt["cs"]), AF.Exp, scale=-DSC)
        self.tt('dve', self.V(self.KR, 0, 128, 64, [[512, 3], [128, 4], [1, 64]]),
                self.V(self.BLK, 0, 128, 0, [[256, 3], [64, 4], [1, 64]]), v4(rt["E"]), ALU.mult)
        self.tt('dve', f3(rt["d"]), f3(rt["cs"]), f3(rt["sg"]), ALU.subtract)
        self.act(f3(rt["E"]), f3(rt["d"]), AF.Exp, scale=-DSC)
        self.tt('dve', self.V(self.KR, 0, 128, 0, [[512, 3], [128, 4], [1, 64]]), v4(self.KH), v4(rt["E"]), ALU.mult)
        self.act(f3(rt["E"]), f3(rt["cs"]), AF.Exp, scale=DSC)
        self.tt('dve', self.BTT[:, :, :], f3(rt["bb"]), f3(rt["E"]), ALU.mult)
        self.tt('dve', self.KTT[:, :, :], f3(rt["kt"]), f3(rt["E"]), ALU.mult)
        self.act(self.GL[:, :, :], self.GL[:, :, :], AF.Exp, scale=-DSC)
        BD = self.BD
        self.memset('pool', self.A1BD[:, :, :, :], 0.0)
        self.memset('pool', self.A2BD[:, :, :, :], 0.0)
        self.memset('pool', self.NN[0][:, :, :], 0.0)
        c4 = lambda t, par: self.V(t, par * 64, 64, 0, [[256, 3], [64, 4], [1, 64]])
        self.bd_fill(BD["kt"], lambda par: c4(self.KTT, par))
        self.bd_fill(BD["b"], lambda par: c4(self.BTT, par))
        self.bd_fill(BD["kh"], lambda par: self.V(self.KR, par * 64, 64, 0, [[512, 3], [128, 4], [1, 64]]))
        self.bd_fill(BD["r"], lambda par: self.V(self.KR, par * 64, 64, 64, [[512, 3], [128, 4], [1, 64]]))
        for (src, dst, neg) in ((BD["kt"], self.KTTOK, False), (BD["b"], self.BTTOK, True)):
            for half in range(2):
                for cc_ in range(2):
                    c = half * 2 + cc_
                    for hp in range(3):
                        self.tr(self.PTB[:, (cc_ * 3 + hp) * 128:(cc_ * 3 + hp + 1) * 128], src[:, hp, c, :], bf=True)
                o = self.V(dst, 0, 128, half * 768, [[1, 768]])
                if neg:
                    self.act(o, self.PTB[:, 0:768], AF.Copy, scale=-1.0)
                else:
                    self.cp('dve', o, self.PTB[:, 0:768])
        self.cp('act', self.HBF[:, :, :], self.rH[d][:, :, :])
        mk = CC['MKF'] if d == 0 else CC['MKB']
        nmk = CC['NMKF'] if d == 0 else CC['NMKB']
        mn = CC['MNF'] if d == 0 else CC['MNB']
        for j in range(4):
            c = j if d == 0 else 3 - j
            cs = slice(c * 64, (c + 1) * 64)
            KRc = lambda hp: self.V(self.KR, 0, 128, hp * 512 + c * 128, [[1, 128]])
            p1, p2, p3 = self.pq(), self.pq(), self.pq()
            for hp in range(3):
                self.mm(p1[:, hp * 128:(hp + 1) * 128], BD["kt"][:, hp, c, :], KRc(hp), inc=(hp == 2))
            for hp in range(3):
                self.mm(p2[:, hp * 128:(hp + 1) * 128], BD["b"][:, hp, c, :], KRc(hp), inc=(hp == 2))
            for hp in range(3):
                self.mm(p3[:, hp * 64:(hp + 1) * 64], BD["kh"][:, hp, c, :], self.BTT[:, hp, cs], inc=(hp == 2))
            for par in range(2):
                pp_ = par * 64
                self.tt('dve', self.V(self.A1BD, pp_, 64, pp_, [[256, 3], [128, 2], [1, 64]]),
                        self.V(p1, pp_, 64, 0, [[128, 3], [64, 2], [1, 64]]),
                        self.V(self.CST, pp_, 64, mk, [[0, 3], [64, 2], [1, 64]]), ALU.mult)
                self.tt('dve', self.V(self.A2BD, pp_, 64, pp_, [[256, 3], [128, 2], [1, 64]]),
                        self.V(p2, pp_, 64, 0, [[128, 3], [64, 2], [1, 64]]),
                        self.V(self.CST, pp_, 64, nmk, [[0, 3], [64, 2], [1, 64]]), ALU.mult)
                self.tt('dve', self.V(self.NN[0], pp_, 64, pp_, [[128, 3], [1, 64]]),
                        self.V(p3, pp_, 64, 0, [[64, 3], [1, 64]]),
                        self.V(self.CST, pp_, 64, mn, [[0, 3], [1, 64]]), ALU.mult)
            pr_ = self.pq()
            for hp in range(3):
                o = pr_[:, hp * 64:(hp + 1) * 64]
                self.mm(o, BD["kh"][:, hp, c, :], self.HBF[:, hp, :], start=True, stop=False)
                self.mm(o, self.A1BD[:, hp, 0, :], self.VSTK[:, c, hp, :], start=False, stop=True, inc=(hp == 2))
            u192 = self.V(self.U, 0, 128, 0, [[1, 192]])
            ub192 = self.V(self.UBF, 0, 128, 0, [[1, 192]])
            self.cp('dve', ub192, pr_[:, 0:192])
            self.cp('dve', u192, pr_[:, 0:192])
            for k in range(6):
                NTk = self.A2BD[:, :, 0, :] if k == 0 else self.NTT[k % 2][:, :, :]
                Nk = self.NN[k % 2]
                if k < 5:
                    pnt = self.pq()
                    for hp in range(3):
                        self.mm(pnt[:, hp * 128:(hp + 1) * 128], Nk[:, hp, :], NTk[:, hp, :], inc=(hp == 2))
                    pn = self.pq()
                    for hp in range(3):
                        self.mm(pn[:, hp * 128:(hp + 1) * 128], NTk[:, hp, :], Nk[:, hp, :], inc=(hp == 2))
                pu = self.pq()
                for hp in range(3):
                    self.mm(pu[:, hp * 64:(hp + 1) * 64], NTk[:, hp, :], self.UBF[:, hp, :], inc=(hp == 2))
                if k < 5:
                    self.cp('act', self.V(self.NTT[(k + 1) % 2], 0, 128, 0, [[1, 384]]), pnt[:, 0:384])
                    self.cp('act', self.V(self.NN[(k + 1) % 2], 0, 128, 0, [[1, 384]]), pn[:, 0:384])
                self.tt('dve', ub192, u192, pu[:, 0:192], ALU.add)
                if k < 5:
                    self.tt('dve', u192, u192, pu[:, 0:192], ALU.add)
            py = self.pq()
            for hp in range(3):
                o = py[:, hp * 64:(hp + 1) * 64]
                self.mm(o, BD["r"][:, hp, c, :], self.HBF[:, hp, :], start=True, stop=False)
                self.mm(o, self.A1BD[:, hp, 1, :], self.VSTK[:, c, hp, :], start=False, stop=False)
                self.mm(o, self.A2BD[:, hp, 1, :], self.UBF[:, hp, :], start=False, stop=True, inc=(hp == 2))
            ybv = self.V(self.YB, 0, 128, c * 192, [[1, 192]])
            if d == 1:
                self.cp('act', ybv, py[:, 0:192])
            else:
                self.tt('dve', ybv, ybv, py[:, 0:192], ALU.add)
            ph = self.pq()
            for hp in range(3):
                o = ph[:, hp * 64:(hp + 1) * 64]
                self.mm(o, self.KTTOK[:, c, hp, :], self.VSTK[:, c, hp, :], start=True, stop=False)
                self.mm(o, self.BTTOK[:, c, hp, :], self.UBF[:, hp, :], start=False, stop=True, inc=(hp == 2))
            self.tt('dve', self.HTMP[:, :, :], self.rH[d][:, :, :], self.V(ph, 0, 128, 0, [[64, 3], [1, 64]]), ALU.add)
            self.tt('dve', self.HBF[:, :, :], self.HTMP[:, :, :], self.V(self.GL, 0, 128, c, [[4, 3], [0, 64]]), ALU.mult)
            self.tt('dve', self.rH[d][:, :, :], self.HTMP[:, :, :], self.V(self.GL, 0, 128, c, [[4, 3], [0, 64]]), ALU.mult)
        if last[d] and prompt:
            S.dma('sp', self.o_rH[seq_idx, l, d], self.rH[d][:, :, :])

    def stage_out(self, l, mod, xsrc, xdst, r0):
        S = self.S
        for tt_ in range(2):
            o, xw = self.XN[tt_], self.XW[tt_]
            S.dma('sp', xw[:, :], xsrc[r0 + tt_ * 128: r0 + (tt_ + 1) * 128, :])
            for ch in range(2):
                ps = self.pq()
                for kc in range(8):
                    self.mm(ps[:, :], self.MIXT[:, kc, tt_ * 128:(tt_ + 1) * 128], self.W_OUT[:, kc, ch * 512:(ch + 1) * 512],
                            start=(kc == 0), stop=(kc == 7))
                self.cp('act' if ch else 'dve', o[:, ch * 512:(ch + 1) * 512], ps[:, :])
            self.dump('o_proj%d' % tt_, o[:, :])
            self.dump('o_x%d' % tt_, xw[:, :])
            ssq = self.TMPS[:, 24 + tt_:25 + tt_]
            junk = self.V(self.BLK, 0, 128, 0, [[1, D]])
            self.act(junk, o[:, :], AF.Square, accum=ssq)
            rs = self.TMPS[:, 26 + tt_:27 + tt_]
            self.ts('dve', rs, ssq, 1.0 / D, EPS, ALU.mult, ALU.add)
            self.act(rs, rs, AF.Sqrt)
            self.recip(rs, rs)
            self.dump('o_rs%d' % tt_, rs)
            self.stt(o[:, :], o[:, :], rs, self.GATEB[:, mod, :], ALU.mult, ALU.mult)
            self.dump('o_g%d' % tt_, o[:, :])
            self.tt('dve', o[:, :], o[:, :], xw[:, :], ALU.add)
            S.dma('sp', xdst[r0 + tt_ * 128: r0 + (tt_ + 1) * 128, :], o[:, :])

    def build(self, layers=(0, 1)):
        try:
            self._build(layers)
        except StopIteration:
            pass
        self.S.finish('sp')
        return self.nc

    def _build(self, layers):
        NP, TS = self.NP, self.TS
        self.setup()
        if self.stop == 'setup':
            raise StopIteration
        for li, l in enumerate(layers):
            self.load_layer(l)
            if self.stop == 'load':
                raise StopIteration
            xsrc = self.x_in if li == 0 else self.x1
            xdst = self.y_out if li == len(layers) - 1 else self.x1
            T_, F_ = {0: True, 1: True}, {0: False, 1: False}
            for s in range(NP):
                self.visit(l, 0, xsrc, xdst, s * 256, 256, 0, [0, 1], False, s, T_, T_)
            if TS > 0:
                nb = TS // TB
                row0 = NP * 256
                for b in range(nb - 1, -1, -1):
                    self.visit(l, 1, xsrc, xdst, row0, TS, b * TB, [1], True, 0,
                               {0: False, 1: b == nb - 1}, {0: False, 1: b == 0}, mode='store')
                for b in range(nb):
                    self.visit(l, 1, xsrc, xdst, row0, TS, b * TB, [0], True, 0,
                               {0: b == 0, 1: False}, {0: b == nb - 1, 1: False}, mode='load')


def prep_shared(inp):
    f = lambda a: np.ascontiguousarray(np.asarray(a, dtype=np.float32))
    b_mod = f(inp['b_mod'])
    sh = {}
    sh['w_mod'] = f(inp['w_mod'])
    sh['w_in'] = f(inp['w_in'])
    sh['w_out'] = f(inp['w_out'])
    sh['bmodT'] = f(b_mod[:, :2048].reshape(2, 16, 128).transpose(2, 0, 1))
    sh['bmodg'] = f(b_mod[:, 2048:3072])
    sh['gpost'] = f(inp['g_post'])
    pp = np.zeros((128, 2 * PL), np.float32)
    for l in range(2):
        o = l * PL
        def put(key, arr, n):
            pp[:, o + PC[key]: o + PC[key] + n] = np.asarray(arr, np.float32).reshape(n, 128).T
        put('G_PRE', inp['g_pre'][l], 8)
        put('M_NORM', inp['m_norm'][l], 3)
        put('R_MU', inp['r_mu'][l], 11)
        put('R_W0', np.asarray(inp['r_w0'][l]).reshape(-1), 6)
        put('R_A0', np.asarray(inp['r_a0'][l]).reshape(-1), 6)
        put('R_KK', inp['r_kk'][l], 3)
        put('R_KA', inp['r_ka'][l], 3)
        put('R_RK', inp['r_rk'][l], 3)
        put('R_NORM', inp['r_norm'][l], 3)
        put('L_CONV', np.asarray(inp['l_conv'][l]).reshape(-1), 8)
        put('L_CONVB', inp['l_conv_b'][l], 2)
        put('L_BA', np.asarray(inp['l_ba'][l]).reshape(-1), 4)
        put('L_BX', np.asarray(inp['l_bx'][l]).reshape(-1), 4)
        put('L_LAM', np.asarray(inp['l_lambda'][l]).reshape(-1), 4)
        pp[0:6, o + PC['M_BI']: o + PC['M_BI'] + 2] = np.asarray(inp['m_bi'][l], np.float32).T
        pp[0:6, o + PC['M_BF']: o + PC['M_BF'] + 2] = np.asarray(inp['m_bf'][l], np.float32).T
    sh['pp'] = pp
    sh['cst'] = make_consts()
    lora = np.zeros((128, 2, 2, 384), np.float32)
    for wi, key in enumerate(['r_w2', 'r_a2']):
        a = np.asarray(inp[key], np.float32)
        lora[:, :, wi, :] = a.transpose(1, 2, 0, 3).reshape(128, 2, 384)
    sh['lora'] = lora
    lruw = np.zeros((128, 2, 8, 128), np.float32)
    for gi, key in enumerate(['l_wa', 'l_wx']):
        a = np.asarray(inp[key], np.float32)
        for l in range(2):
            for d in range(2):
                for pr in range(2):
                    for hb in range(2):
                        n = 2 * pr + hb
                        lruw[hb * 64:(hb + 1) * 64, l, (gi * 2 + d) * 2 + pr, hb * 64:(hb + 1) * 64] = a[l, d, n]
    sh['lruw'] = lruw
    return sh


def prep_core(inp, b, NP, TS):
    f = lambda a: np.ascontiguousarray(np.asarray(a, dtype=np.float32))
    m = {}
    xp = np.asarray(inp['x_prompt'], np.float32)[b * NP:(b + 1) * NP].reshape(NP * 256, D)
    if TS > 0:
        xs = np.asarray(inp['x_sample'], np.float32)[b]
        m['x_in'] = f(np.concatenate([xp, xs], 0))
    else:
        m['x_in'] = f(xp)
    cc = np.stack([np.asarray(inp['c_ctx'], np.float32), np.asarray(inp['c'], np.float32)[b]], -1)
    m['cc'] = f(cc.reshape(8, 128, 2).transpose(1, 0, 2))
    C = np.asarray(inp['state_mlstm_C'], np.float32)[b]
    n = np.asarray(inp['state_mlstm_n'], np.float32)[b]
    Cn = np.concatenate([C, n[..., None]], -1)
    Cn = Cn.reshape(2, 2, 3, 2, 64, 65).transpose(0, 1, 3, 4, 2, 5).reshape(2, 2, 128, 3, 65)
    m['st_mC'] = f(Cn)
    m['st_mm'] = f(np.asarray(inp['state_mlstm_m'], np.float32)[b].transpose(2, 0, 1))
    R = np.asarray(inp['state_rwkv'], np.float32)[b]
    R = R.transpose(0, 1, 2, 4, 3)
    R = R.reshape(2, 2, 3, 2, 64, 64).transpose(0, 1, 3, 4, 2, 5).reshape(2, 2, 128, 3, 64)
    m['st_rH'] = f(R)
    L = np.asarray(inp['state_rglru'], np.float32)[b]
    m['st_l'] = f(L.reshape(2, 2, 2, 128).transpose(3, 0, 1, 2))
    return m


def unpack_core(r, NP, TS):
    y = r['y_out']
    yp = y[:NP * 256].reshape(NP, 256, D)
    ys = y[NP * 256:]
    mC = r['o_mC'].reshape(NP, 2, 2, 2, 64, 3, 65).transpose(0, 1, 2, 5, 3, 4, 6).reshape(NP, 2, 2, 6, 64, 65)
    newC = np.ascontiguousarray(mC[..., :64])
    newn = np.ascontiguousarray(mC[..., 64])
    newm = r['o_mm'].reshape(NP, 2, 2, 6)
    rH = r['o_rH'].reshape(NP, 2, 2, 2, 64, 3, 64).transpose(0, 1, 2, 5, 3, 4, 6).reshape(NP, 2, 2, 6, 64, 64)
    newr = np.ascontiguousarray(rH.transpose(0, 1, 2, 3, 5, 4))
    newl = np.ascontiguousarray(r['o_l'].transpose(0, 1, 2, 4, 3).reshape(NP, 2, 2, 256))
    return yp, ys, newC, newn, newm, newr, newl


_NC_CACHE = {}


def kernel(**inputs):
    NP, TS = 4, 2048
    key = (NP, TS)
    if key not in _NC_CACHE:
        _NC_CACHE[key] = Builder(NP, TS).build()
    nc = _NC_CACHE[key]
    sh = prep_shared(inputs)
    in_maps = []
    for b in range(NCORES):
        m = dict(sh)
        m.update(prep_core(inputs, b, NP, TS))
        in_maps.append(m)
    res = run_bass_kernel_spmd(nc, in_maps, core_ids=list(range(NCORES)))
    outs = [unpack_core(r, NP, TS) for r in res.results]
    y_prompt = np.concatenate([o[0] for o in outs], 0)
    y_sample = np.stack([o[1] for o in outs], 0)
    cat = lambda i: np.concatenate([o[i] for o in outs], 0)
    return (y_prompt.astype(np.float32), y_sample.astype(np.float32), cat(2).astype(np.float32),
            cat(3).astype(np.float32), cat(4).astype(np.float32), cat(5).astype(np.float32), cat(6).astype(np.float32))
```

```python
import numpy as np
import concourse.bass as bass
import concourse.mybir as mybir
from concourse.bass_utils import run_bass_kernel_spmd

F32 = mybir.dt.float32
BF16 = mybir.dt.bfloat16
AF = mybir.ActivationFunctionType
ALU = mybir.AluOpType
AX = mybir.AxisListType

D = 1024
IN_COLS = 4248
EPS = 1e-6
DSC = 0.6065306597126334
TB = 256
NCORES = 8


def _prod(xs):
    r = 1
    for x in xs:
        r *= int(x)
    return r


class Sync:
    def __init__(self, nc, n_dma_sems=32):
        self.nc = nc
        self.engs = {'pe': nc.tensor, 'dve': nc.vector, 'act': nc.scalar,
                     'pool': nc.gpsimd, 'sp': nc.sync}
        self.sem = {}
        self.cnt = {}
        for e in ['pe', 'dve', 'act', 'pool']:
            self.sem[e] = nc.alloc_semaphore('sem_' + e)
            self.cnt[e] = 0
        self.seen = {e: {} for e in self.engs}
        self.dma_ring = [nc.alloc_semaphore('dq_%d' % i) for i in range(n_dma_sems)]
        self.dma_uses = [0] * n_dma_sems
        self.dma_next = 0
        self.rec = {}
        self.untracked = set()
        self.n_wait = 0
        self.n_ins = 0
        self.pstep_cache = {}
        self.sb_addr = {}

    def region(self, ap):
        t = ap.tensor
        name = t.name
        apl = [(int(s), int(c)) for (s, c) in ap.ap]
        off = int(ap.offset)
        if type(t).__name__.startswith('DRam'):
            lo = off + sum(min(0, s * (c - 1)) for s, c in apl)
            hi = off + sum(max(0, s * (c - 1)) for s, c in apl) + 1
            return (name, 0, 1, lo, hi)
        pstep = self.pstep_cache.get(name)
        if pstep is None:
            pstep = _prod(list(t.shape)[1:])
            self.pstep_cache[name] = pstep
        p0 = off // pstep
        f0 = off % pstep
        npart = apl[0][1]
        rest = apl[1:]
        lo = f0 + sum(min(0, s * (c - 1)) for s, c in rest)
        hi = f0 + sum(max(0, s * (c - 1)) for s, c in rest) + 1
        if name in self.sb_addr:
            base, es = self.sb_addr[name]
            return ('SB', p0, p0 + npart, base + lo * es, base + hi * es)
        return ('PS:' + name, (p0 // 32) * 32, ((p0 + npart + 31) // 32) * 32, (lo // 512) * 512, ((hi + 511) // 512) * 512)

    @staticmethod
    def _ovl(a, b):
        return a[1] < b[2] and b[1] < a[2] and a[3] < b[4] and b[3] < a[4]

    @staticmethod
    def _contains(a, b):
        return a[1] <= b[1] and b[2] <= a[2] and a[3] <= b[3] and b[4] <= a[4]

    def _collect(self, e, reads, writes):
        deps = {}
        own = self.sem.get(e)
        rregs = [self.region(a) for a in reads]
        wregs = [self.region(a) for a in writes]
        for r in rregs:
            if r[0] in self.untracked:
                continue
            isps = r[0].startswith('PS:')
            for (reg, kind, sem, val) in self.rec.get(r[0], ()):
                if (kind == 'w' or (isps and sem is not own)) and self._ovl(reg, r):
                    if e == 'pe' and sem is own:
                        continue
                    k = id(sem)
                    if deps.get(k, (None, 0))[1] < val:
                        deps[k] = (sem, val)
        for w in wregs:
            if w[0] in self.untracked:
                continue
            for (reg, kind, sem, val) in self.rec.get(w[0], ()):
                if self._ovl(reg, w):
                    if sem is own:
                        continue
                    k = id(sem)
                    if deps.get(k, (None, 0))[1] < val:
                        deps[k] = (sem, val)
        return deps, rregs, wregs

    def _record(self, rregs, wregs, sem, val):
        for r in rregs:
            if r[0] in self.untracked:
                continue
            lst = self.rec.setdefault(r[0], [])
            lst[:] = [x for x in lst if not (x[1] == 'r' and x[2] is sem and self._contains(r, x[0]))]
            lst.append((r, 'r', sem, val))
        for w in wregs:
            if w[0] in self.untracked:
                continue
            lst = self.rec.setdefault(w[0], [])
            lst[:] = [x for x in lst if not self._contains(w, x[0])]
            lst.append((w, 'w', sem, val))

    def wait(self, e, sem, val):
        k = id(sem)
        if self.seen[e].get(k, 0) >= val:
            return
        self.engs[e].wait_ge(sem, val)
        self.seen[e][k] = val
        self.n_wait += 1

    max_ins = None
    paranoid = False
    embed_waits = True

    def emit(self, e, reads, writes, build, inc=True):
        if self.max_ins is not None and self.n_ins >= self.max_ins:
            raise StopIteration
        deps, rregs, wregs = self._collect(e, reads, writes)
        embed = None
        for (sem, val) in deps.values():
            if self.embed_waits and embed is None and self.seen[e].get(id(sem), 0) < val:
                embed = (sem, val)
                continue
            self.wait(e, sem, val)
        if self.paranoid:
            for e2 in ['pe', 'dve', 'act', 'pool']:
                if self.cnt[e2] > 0 and not (e == 'pe' and e2 == 'pe'):
                    self.wait(e, self.sem[e2], self.cnt[e2])
        ins = build(self.engs[e])
        if embed is not None:
            ins._wait_ge(embed[0], embed[1])
            self.seen[e][id(embed[0])] = embed[1]
        self.n_ins += 1
        if inc:
            self.cnt[e] += 1
            ins.then_inc(self.sem[e], 1)
            val = self.cnt[e]
        else:
            val = self.cnt[e] + 1
        self._record(rregs, wregs, self.sem[e], val)
        return ins

    def dma(self, q, out, in_, **kw):
        if self.max_ins is not None and self.n_ins >= self.max_ins:
            raise StopIteration
        i = self.dma_next
        self.dma_next = (i + 1) % len(self.dma_ring)
        sem = self.dma_ring[i]
        uses = self.dma_uses[i]
        if uses > 0:
            self.wait(q, sem, 16 * uses)
        deps, rregs, wregs = self._collect(q, [in_], [out])
        for (s, v) in deps.values():
            self.wait(q, s, v)
        ins = self.engs[q].dma_start(out=out, in_=in_, **kw)
        ins.then_inc(sem, 16)
        self.n_ins += 1
        self.dma_uses[i] = uses + 1
        self._record(rregs, wregs, sem, 16 * (uses + 1))
        return ins

    def finish(self, q='sp'):
        for i, sem in enumerate(self.dma_ring):
            if self.dma_uses[i] > 0:
                self.wait(q, sem, 16 * self.dma_uses[i])
        for e in ['pe', 'dve', 'act', 'pool']:
            if self.cnt[e] > 0:
                self.wait(q, self.sem[e], self.cnt[e])


class Arena:
    def __init__(self, base, size):
        self.base, self.size, self.ptr = base, size, 0

    def take(self, nbytes):
        off = (self.ptr + 31) // 32 * 32
        self.ptr = off + nbytes
        assert self.ptr <= self.size, ("arena overflow", self.ptr, self.size)
        return self.base + off


PL = 72
PC = dict(G_PRE=0, M_NORM=8, R_MU=11, R_W0=22, R_A0=28, R_KK=34, R_KA=37, R_RK=40, R_NORM=43,
          L_CONV=46, L_CONVB=54, L_BA=56, L_BX=60, L_LAM=64, M_BI=68, M_BF=70)
DL = 32
DC = dict(OMKA=0, CLAM=3, C2LAM=7, NBF=11, OMMU=16)
CC = dict(IDENT=0, BONES=128, MKF=256, MKB=384, MNF=512, MNB=576, MUI=640, MLI=704, RMASK=768,
          LSEL=1536, PSEL=1664, NMKF=1668, NMKB=1796, ONES=1924, ISTK=2052)
NCST = 2052 + 64


def make_consts():
    c = np.zeros((128, NCST), np.float32)
    c[:, 0:128] = np.eye(128)
    c[0:64, 128:192] = 1.0
    c[64:128, 192:256] = 1.0
    s = np.arange(64)[:, None]
    t = np.arange(64)[None, :]
    us, ui = (s < t).astype(np.float32), (s <= t).astype(np.float32)
    ls, li = (s > t).astype(np.float32), (s >= t).astype(np.float32)
    c[0:64, 256:320], c[0:64, 320:384] = us, ui
    c[0:64, 384:448], c[0:64, 448:512] = ls, li
    c[0:64, 512:576] = -ls
    c[0:64, 576:640] = -us
    c[0:64, 640:704] = ui
    c[0:64, 704:768] = li
    rm = np.ones(768, np.float32)
    rm[::64] = 0.0
    c[:, 768:1536] = rm[None, :]
    for k in range(6):
        c[k, 1536 + (k % 2) * 64: 1536 + (k % 2) * 64 + 64] = 1.0
        c[k, 1664 + k // 2] = 1.0
    c[0:64, 1668:1796] = -c[0:64, 256:384]
    c[0:64, 1796:1924] = -c[0:64, 384:512]
    c[:, 1924:2052] = 1.0
    for (a, b) in ((256, 768), (1668, 1924)):
        c[64:128, a:b] = c[0:64, a:b]
    c[0:64, 2052:2116] = np.eye(64)
    c[64:128, 2052:2116] = np.eye(64)
    return c


class Builder:
    def __init__(self, NP, TS, debug=False, stop=None):
        self.stop = stop
        self.NP, self.TS = NP, TS
        self.NTOK = NP * 256 + TS
        self.debug = debug
        nc = self.nc = bass.Bass("TRN2", target_bir_lowering=False)
        self.S = Sync(nc)
        self._decl_dram()
        self._alloc()

    def _decl_dram(self):
        nc, NP, TS = self.nc, self.NP, self.TS
        di = lambda n, s: nc.dram_tensor(n, list(s), F32, kind="ExternalInput").ap()
        do = lambda n, s: nc.dram_tensor(n, list(s), F32, kind="ExternalOutput").ap()
        dx = lambda n, s: nc.dram_tensor(n, list(s), F32, kind="Internal").ap()
        self.x_in = di("x_in", [self.NTOK, D])
        self.cc = di("cc", [128, 8, 2])
        self.w_mod = di("w_mod", [2, D, 3 * D])
        self.bmodT = di("bmodT", [128, 2, 16])
        self.bmodg = di("bmodg", [2, D])
        self.gpost = di("gpost", [2, D])
        self.w_in = di("w_in", [2, D, IN_COLS])
        self.w_out = di("w_out", [2, D, D])
        self.pp = di("pp", [128, 2 * PL])
        self.cst = di("cst", [128, NCST])
        self.lora = di("lora", [128, 2, 2, 384])
        self.lruw = di("lruw", [128, 2, 8, 128])
        self.st_mC = di("st_mC", [2, 2, 128, 3, 65])
        self.st_mm = di("st_mm", [6, 2, 2])
        self.st_rH = di("st_rH", [2, 2, 128, 3, 64])
        self.st_l = di("st_l", [128, 2, 2, 2])
        for n in ["x_in", "cc", "w_mod", "bmodT", "bmodg", "gpost", "w_in", "w_out", "pp", "cst", "lora",
                  "lruw", "st_mC", "st_mm", "st_rH", "st_l"]:
            self.S.untracked.add(n)
        self.y_out = do("y_out", [self.NTOK, D])
        self.o_mC = do("o_mC", [NP, 2, 2, 128, 3, 65])
        self.o_mm = do("o_mm", [NP, 2, 2, 6, 1])
        self.o_rH = do("o_rH", [NP, 2, 2, 128, 3, 64])
        self.o_l = do("o_l", [NP, 2, 2, 128, 2])
        self.x1 = dx("x1", [self.NTOK, D])
        self.sHB = dx("sHB", [max(TS, 64), 384])
        self.sYB = dx("sYB", [max(TS // 256, 1), 128, 768])
        self.sLB = dx("sLB", [128, 2, max(TS, 64)])
        nb = max(TS // 256, 1)
        dxt = lambda n, s_, dt: nc.dram_tensor(n, list(s_), dt, kind="Internal").ap()
        self.sc = {
            'QT': dxt("sc_QT", [nb, 128, 3, 256], BF16), 'KT': dxt("sc_KT", [nb, 128, 3, 256], BF16),
            'KTOK': dxt("sc_KTOK", [nb, 64, 4, 384], BF16), 'VAUG': dxt("sc_VAUG", [nb, 64, 4, 6, 65], BF16),
            'GOZ': dxt("sc_GOZ", [nb, 128, 3, 256], F32), 'GI': dxt("sc_GI", [nb, 6, 256], F32),
            'LF': dxt("sc_LF", [nb, 6, 256], F32), 'XC': dxt("sc_XC", [nb, 128, 2, 256], F32),
            'LZT': dxt("sc_LZT", [nb, 128, 2, 256], F32), 'BLK': dxt("sc_BLK", [nb, 128, 11, 256], F32),
            'RZT': dxt("sc_RZT", [nb, 128, 3, 256], F32), 'KH': dxt("sc_KH", [nb, 128, 3, 256], F32),
            'VSTK': dxt("sc_VSTK", [nb, 128, 4, 3, 64], BF16),
        }
        if self.debug:
            self.dbg = do("dbg", [128, 32768])
            self.dbg_map = {}
            self.dbg_off = 0

    def T(self, name, shape, dtype, arena):
        es = 2 if dtype == BF16 else 4
        nb = _prod(shape[1:]) * es
        off = arena.take(nb)
        t = self.nc.alloc_sbuf_tensor_at(name, list(shape), dtype, offset=off)
        self.S.sb_addr[t.name] = (off, es)
        return t

    def _alloc(self):
        nc = self.nc
        B0 = 16384 + 256
        LIM = 224 * 1024 - 256
        P = Arena(B0, LIM - B0)
        T = self.T
        self.W_IN = T("W_IN", [128, 8, IN_COLS], BF16, P)
        self.W_OUT = T("W_OUT", [128, 8, D], BF16, P)
        self.CST = T("CST", [128, NCST], F32, P)
        self.PP = T("PP", [128, 2 * PL], F32, P)
        self.DV = T("DV", [128, 2 * DL], F32, P)
        self.LORA = T("LORA", [128, 2, 2, 384], F32, P)
        self.LRUW = T("LRUW", [128, 2, 8, 128], F32, P)
        self.RKD = T("RKD", [128, 2, 3, 128], F32, P)
        self.GS = T("GS", [128, 2, 8], F32, P)
        self.SH = T("SH", [128, 2, 8], F32, P)
        self.GATEB = T("GATEB", [128, 2, D], F32, P)
        self.IDB = T("IDB", [128, 128], BF16, P)
        self.ISTKB = T("ISTKB", [128, 64], BF16, P)
        self.mC = [T("mC%d" % d, [128, 3, 65], F32, P) for d in range(2)]
        self.mM = [T("mM%d" % d, [6, 1], F32, P) for d in range(2)]
        self.rH = [T("rH%d" % d, [128, 3, 64], F32, P) for d in range(2)]
        self.lS = [T("lS%d" % d, [128, 2], F32, P) for d in range(2)]
        XB = Arena(P.take(16384), 16384)
        self.HT = T("HT", [128, 8, 384], BF16, P)
        self.MIXT = T("MIXT", [128, 8, 256], BF16, P)
        self.BLK = T("BLK", [128, 11, 256], F32, P)
        abase = P.take(0)
        asize = P.size - P.ptr
        self.asize = asize
        mk = lambda: Arena(abase, asize)
        a = Arena(XB.base, XB.size)
        self.XW = [T("XW%d" % i, [128, D], F32, a) for i in range(2)]
        self.XN = [T("XN%d" % i, [128, D], F32, a) for i in range(2)]
        a = Arena(XB.base, XB.size)
        self.YB = T("YB", [128, 4, 3, 64], F32, a)
        self.RZT = T("RZT", [128, 3, 256], F32, a)
        self.WSTG = [T("WSTG0", [128, 8, 256], F32, Arena(self.S.sb_addr[self.BLK.name][0], 11264)),
                     T("WSTG1", [128, 8, 256], F32, Arena(XB.base, 8192)),
                     T("WSTG2", [128, 8, 256], F32, Arena(XB.base + 8192, 8192))]
        a = mk()
        self.WMs = [T("WM%d" % i, [128, 8, 512], F32, a) for i in range(2)]
        self.SCT = T("SCT", [128, 8, 2], F32, a)
        self.SCB = T("SCB", [128, 2, 8, 128], F32, a)
        self.MODT = T("MODT", [128, 16, 2], F32, a)
        self.BMT = T("BMT", [128, 2, 16], F32, a)
        awm = Arena(self.S.sb_addr[self.WMs[0].name][0], 16384)
        self.BG = T("BG", [128, D], F32, awm)
        self.GP = T("GP", [128, D], F32, awm)
        self.TMPS = T("TMPS", [128, 32], F32, a)
        self.MODROW = T("MODROW", [2, 512], F32, a)
        a = mk()
        self.QT = T("QT", [128, 3, 256], BF16, a)
        self.KT = T("KT", [128, 3, 256], BF16, a)
        self.GOZ = T("GOZ", [128, 3, 256], F32, a)
        self.KTOK = T("KTOK", [64, 4, 384], BF16, a)
        self.VAUG = T("VAUG", [64, 4, 6, 65], BF16, a)
        self.HB = T("HB", [64, 4, 384], F32, a)
        self.mg = []
        for d in range(2):
            g = {}
            for n in ["gi", "lf", "pre", "bb", "gg", "ee", "fl"]:
                g[n] = T("mg_%s%d" % (n, d), [6, 256], F32, a)
            for n in ["mx", "mch", "mprev", "MM", "dec"]:
                g[n] = T("mg_%s%d" % (n, d), [6, 4], F32, a)
            g["X2"] = T("mg_X2%d" % d, [6, 4, 3], F32, a)
            g["etok"] = T("mg_etok%d" % d, [64, 4, 6], F32, a)
            g["fltok"] = T("mg_fltok%d" % d, [64, 4, 6], F32, a)
            g["decb"] = T("mg_decb%d" % d, [128, 4, 3], F32, a)
            self.mg.append(g)
        self.STSB = [T("STSB%d" % i, [64, 6, 64], BF16, a) for i in range(2)]
        self.VP = [T("VP%d" % i, [64, 6, 65], BF16, a) for i in range(2)]
        self.CDEC = T("CDEC", [128, 3, 65], F32, a)
        self.CDBF = T("CDBF", [128, 3, 65], BF16, a)
        self.DN = T("DN", [64, 6], F32, a)
        self.RDN = T("RDN", [64, 6], F32, a)
        self.HD = T("HD", [64, 6, 64], F32, a)
        self.SQ = T("SQ", [64, 4, 384], F32, a)
        self.SSQ = T("SSQ", [64, 24], F32, a)
        self.RSTD = T("RSTD", [64, 24], F32, a)
        self.ZT = [T("ZT%d" % i, [128, 256], F32, a) for i in range(2)]
        a = mk()
        self.XL = T("XL", [128, 2, 392], F32, a)
        self.XC = T("XC", [128, 2, 256], F32, a)
        self.LZT = T("LZT", [128, 2, 256], F32, a)
        self.ltd = [{n: T("lt%d_%s" % (d_, n), [128, 2, 256], F32, a) for n in ["rg", "ig", "aa", "a2", "bt"]} for d_ in range(2)]
        self.HL = [T("HL%d" % d, [128, 2, 256], F32, a) for d in range(2)]
        a = mk()
        self.URS = T("URS", [128, 11, 384], F32, a)
        a = mk()
        self.KH = T("KH", [128, 3, 256], F32, a)
        R1 = a.take(7 * 3072)
        a1 = Arena(R1, 7 * 3072)
        self.rt = {n: T("rt_" + n, [128, 3, 256], F32, a1) for n in ["sg", "aa", "kt", "bb", "cs", "d", "E"]}
        a2 = Arena(R1, 7 * 3072)
        self.A1BD = T("A1BD", [128, 3, 2, 128], BF16, a2)
        self.A2BD = T("A2BD", [128, 3, 2, 128], BF16, a2)
        self.NN = [T("NN%d" % i, [128, 3, 128], BF16, a2) for i in range(2)]
        self.NTT = [T("NTT%d" % i, [128, 3, 128], BF16, a2) for i in range(2)]
        self.U = T("U", [128, 3, 64], F32, a2)
        self.UBF = T("UBF", [128, 3, 64], BF16, a2)
        self.HBF = T("HBF", [128, 3, 64], BF16, a2)
        self.HTMP = T("HTMP", [128, 3, 64], F32, a2)
        self.YBD = T("YBD", [128, 3, 128], F32, a2)
        self.KTTOK = T("KTTOK", [128, 4, 3, 128], BF16, a2)
        self.BTTOK = T("BTTOK", [128, 4, 3, 128], BF16, a2)
        self.rSSQ = T("rSSQ", [128, 12], F32, a2)
        self.rRSTD = T("rRSTD", [128, 12], F32, a2)
        self.KR = T("KR", [128, 3, 4, 2, 64], BF16, a)
        self.BTT = T("BTT", [128, 3, 256], BF16, a)
        self.KTT = T("KTT", [128, 3, 256], BF16, a)
        bdr = a.take(4 * 3072)
        a3 = Arena(bdr, 4 * 3072)
        self.BD = {n: T("BD_" + n, [128, 3, 4, 128], BF16, a3) for n in ["kt", "b", "kh", "r"]}
        a3 = Arena(bdr, 4 * 3072)
        self.ft = {n: T("ft_" + n, [128, 3, 256], F32, a3) for n in ["rk", "bon", "t1"]}
        self.YSQ = T("YSQ", [128, 4, 3, 64], F32, a3)
        self.BDV = T("BDV", [128, 3, 4, 128], BF16, a)
        self.VSTK = T("VSTK", [128, 4, 3, 64], BF16, a)
        self.TWL = T("TWL", [128, 256], F32, a)
        self.GL = T("GL", [128, 3, 4], F32, a)
        self.PA = nc.alloc_psum_tensor("PA", [128, 1024], F32)
        self.PB = nc.alloc_psum_tensor("PB", [128, 1024], F32)
        self.PQ = [nc.alloc_psum_tensor("PQ%d" % i, [128, 512], F32) for i in range(3)]
        self.PTB = nc.alloc_psum_tensor("PTB", [128, 1024], BF16)
        self.pq_i = 0

    def pq(self):
        t = self.PQ[self.pq_i % 3]
        self.pq_i += 1
        return t

    def V(self, t, p0, npart, off, dims):
        pstep = _prod(list(t.shape)[1:])
        return bass.AP(t, p0 * pstep + off, [[pstep, npart]] + [list(d) for d in dims])

    def tt(self, e, out, in0, in1, op):
        return self.S.emit(e, [in0, in1], [out], lambda g: g.tensor_tensor(out=out, in0=in0, in1=in1, op=op))

    def ts(self, e, out, in0, s1, s2, op0, op1=None):
        rd = [in0] + [s for s in (s1, s2) if not isinstance(s, (int, float)) and s is not None]
        if op1 is None:
            return self.S.emit(e, rd, [out], lambda g: g.tensor_scalar(out=out, in0=in0, scalar1=s1, scalar2=None, op0=op0))
        return self.S.emit(e, rd, [out], lambda g: g.tensor_scalar(out=out, in0=in0, scalar1=s1, scalar2=s2, op0=op0, op1=op1))

    def stt(self, out, in0, sc, in1, op0, op1):
        rd = [in0, in1] + ([] if isinstance(sc, (int, float)) else [sc])
        return self.S.emit('dve', rd, [out], lambda g: g.scalar_tensor_tensor(out=out, in0=in0, scalar=sc, in1=in1, op0=op0, op1=op1))

    def act(self, out, in_, func, bias=None, scale=None, accum=None):
        rd = [in_]
        kw = {}
        if bias is not None:
            kw['bias'] = bias
            if not isinstance(bias, (int, float)):
                rd.append(bias)
        if scale is not None:
            kw['scale'] = scale
            if not isinstance(scale, (int, float)):
                rd.append(scale)
        wr = [out]
        if accum is not None:
            kw['accum_out'] = accum
            wr.append(accum)
        return self.S.emit('act', rd, wr, lambda g: g.activation(out=out, in_=in_, func=func, **kw))

    def cp(self, e, out, in_):
        if e == 'act':
            return self.act(out, in_, AF.Copy)
        return self.S.emit(e, [in_], [out], lambda g: g.tensor_copy(out=out, in_=in_))

    def _pe_rowtile_guard(self, lhsT, out):
        S = self.S
        st = S.region(lhsT)
        k = st[2] - st[1]
        kr = 32 if k <= 32 else (64 if k <= 64 else 128)
        rows = (st[1], st[1] + kr)
        oreg = S.region(out)
        last = getattr(self, '_last_pe', None)
        if last is not None:
            lrows, loreg, lins, linc = last
            disjoint = rows[1] <= lrows[0] or lrows[1] <= rows[0]
            samebank = (loreg[0] == oreg[0]) and loreg[3] < oreg[4] and oreg[3] < loreg[4]
            if disjoint and samebank:
                if not linc:
                    S.cnt['pe'] += 1
                    lins.then_inc(S.sem['pe'], 1)
                S.wait('pe', S.sem['pe'], S.cnt['pe'])
        return rows, oreg

    def mm(self, out, lhsT, rhs, start=True, stop=True, inc=None):
        if inc is None:
            inc = stop
        rows, oreg = self._pe_rowtile_guard(lhsT, out)
        ins = self.S.emit('pe', [lhsT, rhs], [out],
                          lambda g: g.matmul(out, lhsT=lhsT, rhs=rhs, start=start, stop=stop), inc=inc)
        self._last_pe = (rows, oreg, ins, inc)
        return ins

    def tr(self, out, in_, inc=True, bf=False):
        n = in_.shape[0]
        ident = self.IDB[0:n, 0:n] if bf else self.CST[0:n, CC['IDENT']:CC['IDENT'] + n]
        rows, oreg = self._pe_rowtile_guard(in_, out)
        ins = self.S.emit('pe', [in_, ident], [out],
                          lambda g: g.transpose(out=out, in_=in_, identity=ident), inc=inc)
        self._last_pe = (rows, oreg, ins, inc)
        return ins

    def memset(self, e, ap, v):
        return self.S.emit(e, [], [ap], lambda g: g.memset(ap, v))

    def scan(self, out, d0, d1, init, op0, op1):
        rd = [d0, d1] + ([] if isinstance(init, (int, float)) else [init])
        return self.S.emit('dve', rd, [out], lambda g: g.tensor_tensor_scan(out=out, data0=d0, data1=d1, initial=init, op0=op0, op1=op1))

    def recip(self, out, in_):
        return self.S.emit('dve', [in_], [out], lambda g: g.reciprocal(out=out, in_=in_))

    def reduce(self, out, in_, op, axis=AX.X):
        return self.S.emit('dve', [in_], [out], lambda g: g.tensor_reduce(out=out, in_=in_, axis=axis, op=op))

    def dump(self, name, ap):
        if not self.debug or name in self.dbg_map:
            return
        if getattr(self, 'dbg_filter', None) is not None and not any(name.startswith(p) for p in self.dbg_filter):
            return
        shp = list(ap.shape)
        npart, nfree = shp[0], _prod(shp[1:])
        stage = self.XN[1]
        assert nfree <= 1024
        dst = self.V(stage, 0, npart, 0, [[_prod(shp[i + 1:]), shp[i]] for i in range(1, len(shp))])
        self.cp('dve', dst, ap)
        self.S.dma('sp', self.dbg[0:npart, self.dbg_off:self.dbg_off + nfree], stage[0:npart, 0:nfree], allow_slow_non_contiguous=True)
        self.dbg_map[name] = (self.dbg_off, npart, shp[1:])
        self.dbg_off += nfree

    def ppc(self, l, key, j=0, rows=128):
        c = l * PL + PC[key] + j
        return self.PP[0:rows, c:c + 1]

    def dvc(self, l, key, j=0, rows=128):
        c = l * DL + DC[key] + j
        return self.DV[0:rows, c:c + 1]

    def setup(self):
        S = self.S
        S.dma('sp', self.CST[:, :], self.cst[:, :])
        S.dma('sp', self.PP[:, :], self.pp[:, :])
        S.dma('sp', self.LORA[:, :, :, :], self.lora[:, :, :, :])
        S.dma('sp', self.LRUW[:, :, :, :], self.lruw[:, :, :, :])
        self.memset('dve', self.DV[:, :], 0.0)
        self.cp('dve', self.IDB[:, :], self.CST[:, CC['IDENT']:CC['IDENT'] + 128])
        self.cp('dve', self.ISTKB[:, :], self.CST[:, CC['ISTK']:CC['ISTK'] + 64])
        for l in range(2):
            self.ts('dve', self.DV[:, l * DL + DC['OMKA']: l * DL + DC['OMKA'] + 3],
                    self.PP[:, l * PL + PC['R_KA']: l * PL + PC['R_KA'] + 3], -1.0, 1.0, ALU.mult, ALU.add)
            self.ts('dve', self.DV[:, l * DL + DC['OMMU']: l * DL + DC['OMMU'] + 11],
                    self.PP[:, l * PL + PC['R_MU']: l * PL + PC['R_MU'] + 11], -1.0, 1.0, ALU.mult, ALU.add)
            lam = self.PP[:, l * PL + PC['L_LAM']: l * PL + PC['L_LAM'] + 4]
            t0 = self.TMPS[:, 0:4]
            self.act(t0, lam, AF.Exp, scale=-1.0)
            self.act(t0, t0, AF.Ln, bias=1.0)
            self.ts('dve', self.DV[:, l * DL + DC['CLAM']: l * DL + DC['CLAM'] + 4], t0, -8.0, None, ALU.mult)
            self.ts('dve', self.DV[:, l * DL + DC['C2LAM']: l * DL + DC['C2LAM'] + 4], t0, -16.0, None, ALU.mult)
            self.ts('dve', self.DV[0:6, l * DL + DC['NBF']: l * DL + DC['NBF'] + 2],
                    self.PP[0:6, l * PL + PC['M_BF']: l * PL + PC['M_BF'] + 2], -1.0, None, ALU.mult)
            for hp in range(3):
                self.ts('dve', self.RKD[:, l, hp, :], self.CST[:, CC['BONES']:CC['BONES'] + 128],
                        self.ppc(l, 'R_RK', hp), None, ALU.mult)
        self.memset('dve', self.XL[:, :, 0:2], 0.0)

    def load_layer(self, l):
        S = self.S
        wsrc = self.w_in[l].rearrange("(kc p) c -> p kc c", p=128)
        wo = self.w_out[l].rearrange("(kc p) c -> p kc c", p=128)
        pieces = [(self.W_IN, wsrc, c0, min(256, IN_COLS - c0)) for c0 in range(0, IN_COLS, 256)]
        pieces += [(self.W_OUT, wo, c0, 256) for c0 in range(0, D, 256)]
        def emit_pieces(lo, hi):
            for i in range(lo, min(hi, len(pieces))):
                dst, src, c0, n = pieces[i]
                stg = self.WSTG[i % 3]
                S.dma('sp', stg[:, :, 0:n], src[:, :, c0:c0 + n])
                self.cp('pool', dst[:, :, c0:c0 + n], stg[:, :, 0:n])
        if self.stop == 'load_w':
            raise StopIteration
        S.dma('sp', self.SCT[:, :, :], self.cc[:, :, :])
        S.dma('sp', self.BMT[:, :, :], self.bmodT[:, :, :])
        self.act(self.SCT[:, :, :], self.SCT[:, :, :], AF.Silu)
        for m in range(2):
            for kc in range(8):
                src = self.V(self.SCT, 0, 128, kc * 2 + m, [[0, 128]])
                self.cp('dve', self.SCB[:, m, kc, :], src)
        wm = self.w_mod[l].rearrange("(kc p) c -> p kc c", p=128)
        for blk in range(6):
            self.WM = self.WMs[blk % 2]
            S.dma('sp', self.WM[:, :, :], wm[:, :, blk * 512:(blk + 1) * 512])
            emit_pieces(blk * 4, blk * 4 + 4)
            if blk < 4:
                ps = self.pq()
                for kc in range(8):
                    self.mm(ps[0:2, :], self.SCT[:, kc, :], self.WM[:, kc, :], start=(kc == 0), stop=(kc == 7))
                self.cp('act', self.MODROW[0:2, :], ps[0:2, :])
                ps2 = self.pq()
                for j in range(4):
                    self.tr(ps2[:, j * 2:j * 2 + 2], self.MODROW[0:2, j * 128:(j + 1) * 128])
                o = self.MODT[:, blk * 4:(blk + 1) * 4, :]
                bsrc = self.V(self.BMT, 0, 128, l * 16 + blk * 4, [[1, 4], [0, 2]])
                self.tt('dve', o, self.V(ps2, 0, 128, 0, [[2, 4], [1, 2]]), bsrc, ALU.add)
            else:
                half = blk - 4
                for m in range(2):
                    ps = self.pq()
                    for kc in range(8):
                        self.mm(ps[:, :], self.SCB[:, m, kc, :], self.WM[:, kc, :], start=(kc == 0), stop=(kc == 7))
                    self.cp('act', self.GATEB[:, m, half * 512:(half + 1) * 512], ps[:, :])
        if self.stop == 'load_m':
            raise StopIteration
        for m in range(2):
            self.cp('dve', self.SH[:, m, :], self.V(self.MODT, 0, 128, m, [[2, 8]]))
            t0 = self.TMPS[:, 8:16]
            self.ts('dve', t0, self.V(self.MODT, 0, 128, 16 + m, [[2, 8]]), 1.0, None, ALU.add)
            self.tt('dve', self.GS[:, m, :], t0, self.PP[:, l * PL + PC['G_PRE']: l * PL + PC['G_PRE'] + 8], ALU.mult)
        if self.stop == 'load_g':
            raise StopIteration
        S.dma('sp', self.BG[:, :], bass.AP(self.bmodg.tensor, l * D, [[0, 128], [1, D]]))
        S.dma('sp', self.GP[:, :], bass.AP(self.gpost.tensor, l * D, [[0, 128], [1, D]]))
        if self.stop == 'load_b':
            raise StopIteration
        for m in range(2):
            self.tt('dve', self.GATEB[:, m, :], self.GATEB[:, m, :], self.BG[:, :], ALU.add)
            self.tt('dve', self.GATEB[:, m, :], self.GATEB[:, m, :], self.GP[:, :], ALU.mult)

    def proj_fm(self, c0, ncols, t_off, ntok, evac):
        ps = self.pq()
        for kc in range(8):
            self.mm(ps[0:ncols, 0:ntok], self.W_IN[:, kc, c0:c0 + ncols], self.HT[:, kc, t_off:t_off + ntok],
                    start=(kc == 0), stop=(kc == 7))
        evac(ps[0:ncols, 0:ntok])

    def proj_tm(self, c0, ncols, t_off, evac):
        ps = self.pq()
        for kc in range(8):
            self.mm(ps[0:64, 0:ncols], self.HT[:, kc, t_off:t_off + 64], self.W_IN[:, kc, c0:c0 + ncols],
                    start=(kc == 0), stop=(kc == 7))
        evac(ps[0:64, 0:ncols])

    def visit(self, l, mod, xsrc, xdst, row0, T, t0, dirs, grid, seq_idx, first, last, mode='full'):
        S = self.S
        w0 = max(0, t0 - 64)
        w1 = min(T, t0 + TB + 64)
        W = w1 - w0
        co = t0 - w0
        do_f = 0 in dirs
        prompt = (mod == 0)
        if mode != 'load':
            ntile = (W + 127) // 128
            for i in range(ntile):
                n = min(128, W - i * 128)
                xw, xn = self.XW[i % 2], self.XN[i % 2]
                r = row0 + w0 + i * 128
                S.dma('sp', xw[0:n, :], xsrc[r:r + n, :])
                ssq = self.TMPS[0:n, 16 + i:17 + i]
                self.act(xn[0:n, :], xw[0:n, :], AF.Square, accum=ssq)
                if self.stop == 'n1':
                    raise StopIteration
                rs = self.TMPS[0:n, 20 + i:21 + i]
                self.ts('dve', rs, ssq, 1.0 / D, EPS, ALU.mult, ALU.add)
                self.act(rs, rs, AF.Sqrt)
                self.recip(rs, rs)
                if self.stop == 'n2':
                    raise StopIteration
                self.act(xn[0:n, :], xw[0:n, :], AF.Copy, scale=rs)
                if self.stop == 'n3':
                    raise StopIteration
                for half in range(2):
                    ps = self.pq()
                    for j in range(4):
                        kc = half * 4 + j
                        self.tr(ps[:, j * 128:j * 128 + n], xn[0:n, kc * 128:(kc + 1) * 128])
                    if self.stop == 'n4':
                        raise StopIteration
                    for j in range(4):
                        kc = half * 4 + j
                        o = self.HT[:, kc, i * 128:i * 128 + n]
                        if half == 0:
                            self.ts('dve', o, ps[:, j * 128:j * 128 + n], self.GS[:, mod, kc:kc + 1], self.SH[:, mod, kc:kc + 1], ALU.mult, ALU.add)
                        else:
                            self.act(o, ps[:, j * 128:j * 128 + n], AF.Identity, bias=self.SH[:, mod, kc:kc + 1], scale=self.GS[:, mod, kc:kc + 1])
        if self.stop in ('norm', 'n5a', 'n5d'):
            raise StopIteration
        self.stage_mlstm(l, t0, co, dirs, prompt, seq_idx, first, last, mode)
        if self.stop == 'mlstm':
            raise StopIteration
        self.stage_lru(l, t0, co, W, w0, w1, T, dirs, prompt, seq_idx, first, last, mode)
        if self.stop == 'lru':
            raise StopIteration
        self.stage_rwkv(l, t0, co, W, w0, w1, T, dirs, grid, prompt, seq_idx, first, last, mode)
        for kc in range(8):
            self.dump('mix%d' % kc, self.MIXT[:, kc, :])
        if self.stop == 'rwkv':
            raise StopIteration
        if do_f:
            self.stage_out(l, mod, xsrc, xdst, row0 + t0)
        if self.stop == 'out':
            raise StopIteration

    def stage_mlstm(self, l, t0, co, dirs, prompt, seq_idx, first, last, mode='full'):
        S = self.S
        do_f = 0 in dirs
        if mode != 'load':
            for d in (sorted(set(dirs) | {0}) if mode == 'store' else dirs):
                g = self.mg[d]
                self.proj_fm(1920 + d * 6, 6, co, 256, lambda ps, g=g, d=d: self.act(g["gi"][:, :], ps, AF.Identity, bias=self.ppc(l, 'M_BI', d, 6)))
                def ev_f(ps, g=g, d=d):
                    self.act(g["lf"][:, :], ps, AF.Exp, bias=self.dvc(l, 'NBF', d, 6), scale=-1.0)
                    self.act(g["lf"][:, :], g["lf"][:, :], AF.Ln, bias=1.0)
                    self.ts('dve', g["lf"][:, :], g["lf"][:, :], -1.0, None, ALU.mult)
                self.proj_fm(1932 + d * 6, 6, co, 256, ev_f)
        bi = t0 // 256
        ml_items = [(self.QT[:, :, :], 'QT'), (self.KT[:, :, :], 'KT'), (self.KTOK[:, :, :], 'KTOK'), (self.VAUG[:, :, :, :], 'VAUG'),
                    (self.GOZ[:, :, :], 'GOZ'), (self.mg[0]["gi"][:, :], 'GI'), (self.mg[0]["lf"][:, :], 'LF')]
        if mode == 'load':
            for ap_, k_ in ml_items:
                S.dma('sp', ap_, self.sc[k_][bi])
        for d in dirs:
            if first[d]:
                if prompt:
                    self.memset('dve', self.mC[d][:, :, :], 0.0)
                    self.memset('dve', self.mM[d][:, :], 0.0)
                else:
                    S.dma('sp', self.mC[d][:, :, :], self.st_mC[l, d])
                    S.dma('sp', self.mM[d][:, :], self.st_mm[:, l, d:d + 1], allow_slow_non_contiguous=True)
        if do_f and not (1 in dirs):
            S.dma('sp', self.HB[:, :, :], self.sHB[t0:t0 + 256, :].rearrange("(c s) f -> s c f", s=64))
        for d in dirs:
            g = self.mg[d]
            v3 = lambda t: self.V(t, 0, 6, 0, [[64, 4], [1, 64]])
            self.scan(g["pre"][:, :], self.CST[0:6, CC['RMASK']:CC['RMASK'] + 256], g["lf"][:, :], 0.0, ALU.mult, ALU.add)
            bL = self.V(g["pre"], 0, 6, 63, [[64, 4]])
            bLb = self.V(g["pre"], 0, 6, 63, [[64, 4], [0, 64]])
            if d == 0:
                bsrc = g["pre"]
            else:
                self.tt('dve', v3(g["bb"]), bLb, v3(g["pre"]), ALU.subtract)
                self.tt('dve', g["bb"][:, :], g["bb"][:, :], g["lf"][:, :], ALU.add)
                bsrc = g["bb"]
            self.tt('dve', g["gg"][:, :], g["gi"][:, :], bsrc[:, :], ALU.subtract)
            self.reduce(g["mx"][:, :], v3(g["gg"]), ALU.max)
            if d == 0:
                mo, mxv, blv = g["mch"][:, :], g["mx"][:, :], bL
            else:
                mo = self.V(g["mch"], 0, 6, 3, [[-1, 4]])
                mxv = self.V(g["mx"], 0, 6, 3, [[-1, 4]])
                blv = self.V(g["pre"], 0, 6, 63 + 3 * 64, [[-64, 4]])
            self.scan(mo, mxv, blv, self.mM[d][:, 0:1], ALU.max, ALU.add)
            if d == 0:
                self.cp('dve', g["mprev"][:, 1:4], g["mch"][:, 0:3])
                self.cp('dve', g["mprev"][:, 0:1], self.mM[d][:, 0:1])
                mfin = g["mch"][:, 3:4]
            else:
                self.cp('dve', g["mprev"][:, 0:3], g["mch"][:, 1:4])
                self.cp('dve', g["mprev"][:, 3:4], self.mM[d][:, 0:1])
                mfin = g["mch"][:, 0:1]
            self.tt('dve', g["MM"][:, :], g["mprev"][:, :], g["mx"][:, :], ALU.max)
            self.tt('dve', g["dec"][:, :], g["mprev"][:, :], g["MM"][:, :], ALU.subtract)
            self.act(g["dec"][:, :], g["dec"][:, :], AF.Exp)
            self.cp('dve', self.mM[d][:, 0:1], mfin)
            MMb = self.V(g["MM"], 0, 6, 0, [[1, 4], [0, 64]])
            self.tt('dve', v3(g["ee"]), v3(g["gg"]), MMb, ALU.subtract)
            self.act(g["ee"][:, :], g["ee"][:, :], AF.Exp)
            self.tt('dve', v3(g["fl"]), v3(bsrc), MMb, ALU.add)
            self.act(g["fl"][:, :], g["fl"][:, :], AF.Exp, scale=-1.0)
        if mode != 'load':
            for hp in range(3):
                self.proj_fm(hp * 128, 128, co, 256, lambda ps, hp=hp: self.cp('act', self.QT[:, hp, :], ps))
                self.proj_fm(384 + hp * 128, 128, co, 256, lambda ps, hp=hp: self.act(self.KT[:, hp, :], ps, AF.Copy, scale=0.125))
            if do_f or mode == 'store':
                for hp in range(3):
                    def ev_o(ps, hp=hp):
                        self.act(self.GOZ[:, hp, :], ps, AF.Sigmoid)
                    self.proj_fm(1152 + hp * 128, 128, co, 256, ev_o)
                    def ev_z2(ps, hp=hp):
                        tz = self.ZT[hp % 2]
                        self.act(tz[:, :], ps, AF.Silu)
                        self.tt('dve', self.GOZ[:, hp, :], self.GOZ[:, hp, :], tz[:, :], ALU.mult)
                    self.proj_fm(1536 + hp * 128, 128, co, 256, ev_z2)
            for c in range(4):
                self.proj_tm(384, 384, co + c * 64, lambda ps, c=c: self.act(self.KTOK[:, c, :], ps, AF.Copy, scale=0.125))
                def ev_v(ps, c=c):
                    self.cp('dve', self.VAUG[:, c, :, 0:64], self.V(ps.tensor, 0, 64, 0, [[64, 6], [1, 64]]))
                self.proj_tm(768, 384, co + c * 64, ev_v)
            self.memset('dve', self.VAUG[:, :, :, 64:65], 1.0)
        for d in dirs:
            g = self.mg[d]
            ps = self.pq()
            for c in range(4):
                self.tr(ps[0:64, c * 6:c * 6 + 6], g["ee"][:, c * 64:(c + 1) * 64])
                self.tr(ps[0:64, 24 + c * 6:24 + c * 6 + 6], g["fl"][:, c * 64:(c + 1) * 64])
            self.cp('dve', g["etok"][:, :, :], self.V(ps, 0, 64, 0, [[6, 4], [1, 6]]))
            self.cp('dve', g["fltok"][:, :, :], self.V(ps, 0, 64, 24, [[6, 4], [1, 6]]))
            self.tt('dve', g["X2"][:, :, :], self.V(g["dec"], 0, 6, 0, [[1, 4], [0, 3]]),
                    self.V(self.CST, 0, 6, CC['PSEL'], [[0, 4], [1, 3]]), ALU.mult)
            ps2 = self.pq()
            self.mm(ps2[:, 0:12], self.CST[0:6, CC['LSEL']:CC['LSEL'] + 128], self.V(g["X2"], 0, 6, 0, [[1, 12]]))
            self.cp('dve', g["decb"][:, :, :], self.V(ps2, 0, 128, 0, [[3, 4], [1, 3]]))
        if mode == 'store':
            for ap_, k_ in ml_items:
                S.dma('sp', self.sc[k_][bi], ap_)
        for d in sorted(dirs, reverse=True):
            g = self.mg[d]
            mask = self.CST[0:64, CC['MUI']:CC['MUI'] + 64] if d == 0 else self.CST[0:64, CC['MLI']:CC['MLI'] + 64]
            maskb = self.V(self.CST, 0, 64, CC['MUI'] if d == 0 else CC['MLI'], [[0, 6], [1, 64]])
            for j in range(4):
                c = j if d == 0 else 3 - j
                cs = slice(c * 64, (c + 1) * 64)
                stsb, vp = self.STSB[j % 2], self.VP[j % 2]
                ps = self.pq()
                for h in (0, 2, 4, 1, 3, 5):
                    hp, pb = h // 2, 64 * (h % 2)
                    self.mm(ps[0:64, h * 64:(h + 1) * 64], self.KT[pb:pb + 64, hp, cs], self.QT[pb:pb + 64, hp, cs], inc=(h == 5))
                self.tt('dve', stsb[:, :, :], self.V(ps, 0, 64, 0, [[64, 6], [1, 64]]), maskb, ALU.mult)
                self.tt('dve', vp[:, :, :], self.VAUG[:, c, :, :], self.V(g["etok"], 0, 64, c * 6, [[1, 6], [0, 65]]), ALU.mult)
                self.tt('dve', self.CDBF[:, :, :], self.mC[d][:, :, :], self.V(g["decb"], 0, 128, c * 3, [[1, 3], [0, 65]]), ALU.mult)
                self.tt('dve', self.CDEC[:, :, :], self.mC[d][:, :, :], self.V(g["decb"], 0, 128, c * 3, [[1, 3], [0, 65]]), ALU.mult)
                ph = self.pq()
                for h in (0, 2, 4, 1, 3, 5):
                    hp, pb = h // 2, 64 * (h % 2)
                    o = ph[0:64, h * 65:(h + 1) * 65]
                    self.mm(o, stsb[:, h, :], vp[:, h, :], start=True, stop=False)
                    self.mm(o, self.QT[pb:pb + 64, hp, cs], self.CDBF[pb:pb + 64, hp, :], start=False, stop=True, inc=(h == 5))
                pc = self.pq()
                for h in range(6):
                    hp, pb = h // 2, 64 * (h % 2)
                    self.mm(pc[pb:pb + 64, hp * 65:(hp + 1) * 65], self.KTOK[:, c, h * 64:(h + 1) * 64], vp[:, h, :], inc=(h == 5))
                self.tt('dve', self.mC[d][:, :, :], self.CDEC[:, :, :], self.V(pc, 0, 128, 0, [[65, 3], [1, 65]]), ALU.add)
                self.act(self.DN[:, :], self.V(ph, 0, 64, 64, [[65, 6]]), AF.Abs)
                self.tt('dve', self.DN[:, :], self.DN[:, :], g["fltok"][:, c, :], ALU.max)
                self.recip(self.RDN[:, :], self.DN[:, :])
                hsrc = self.V(ph, 0, 64, 0, [[65, 6], [1, 64]])
                rb = self.V(self.RDN, 0, 64, 0, [[1, 6], [0, 64]])
                hbv = self.V(self.HB, 0, 64, c * 384, [[64, 6], [1, 64]])
                if d == 1:
                    self.tt('dve', hbv, hsrc, rb, ALU.mult)
                else:
                    self.tt('dve', self.HD[:, :, :], hsrc, rb, ALU.mult)
                    self.tt('dve', hbv, hbv, self.HD[:, :, :], ALU.add)
            if last[d] and prompt:
                S.dma('sp', self.o_mC[seq_idx, l, d], self.mC[d][:, :, :])
                S.dma('sp', self.o_mm[seq_idx, l, d], self.mM[d][:, 0:1])
        if not do_f:
            S.dma('sp', self.sHB[t0:t0 + 256, :].rearrange("(c s) f -> s c f", s=64), self.HB[:, :, :])
            return
        self.tt('dve', self.SQ[:, :, :], self.HB[:, :, :], self.HB[:, :, :], ALU.mult)
        self.reduce(self.SSQ[:, :], self.V(self.SQ, 0, 64, 0, [[64, 24], [1, 64]]), ALU.add)
        self.ts('dve', self.SSQ[:, :], self.SSQ[:, :], 1.0 / 64, EPS, ALU.mult, ALU.add)
        self.act(self.SSQ[:, :], self.SSQ[:, :], AF.Sqrt)
        self.recip(self.RSTD[:, :], self.SSQ[:, :])
        self.tt('dve', self.V(self.SQ, 0, 64, 0, [[64, 24], [1, 64]]), self.V(self.HB, 0, 64, 0, [[64, 24], [1, 64]]),
                self.V(self.RSTD, 0, 64, 0, [[1, 24], [0, 64]]), ALU.mult)
        for hp in range(3):
            ps = self.pq()
            for c in range(4):
                self.tr(ps[:, c * 64:(c + 1) * 64], self.SQ[:, c, hp * 128:(hp + 1) * 128], inc=(c == 3))
            self.stt(self.MIXT[:, hp, :], ps[:, 0:256], self.ppc(l, 'M_NORM', hp), self.GOZ[:, hp, :], ALU.mult, ALU.mult)

    def stage_lru(self, l, t0, co, W, w0, w1, T, dirs, prompt, seq_idx, first, last, mode='full'):
        S = self.S
        do_f = 0 in dirs
        if mode != 'load':
            for pr in range(2):
                self.proj_fm(3736 + pr * 128, 128, 0, W, lambda ps, pr=pr: self.cp('act', self.XL[:, pr, 2:2 + W], ps))
                if do_f or mode == 'store':
                    self.proj_fm(3992 + pr * 128, 128, co, 256, lambda ps, pr=pr: self.act(self.LZT[:, pr, :], ps, AF.Silu))
            if w1 == T:
                self.memset('dve', self.XL[:, :, 2 + W:2 + W + 1], 0.0)
            if w0 == 0:
                self.memset('dve', self.XL[:, :, 0:2], 0.0)
            for pr in range(2):
                self.ts('dve', self.XC[:, pr, :], self.XL[:, pr, co:co + 256], self.ppc(l, 'L_CONV', 0 * 2 + pr), self.ppc(l, 'L_CONVB', pr), ALU.mult, ALU.add)
                for j in range(1, 4):
                    self.stt(self.XC[:, pr, :], self.XL[:, pr, co + j:co + j + 256], self.ppc(l, 'L_CONV', j * 2 + pr), self.XC[:, pr, :], ALU.mult, ALU.add)
        bi = t0 // 256
        lr_items = [(self.XC[:, :, :], 'XC'), (self.LZT[:, :, :], 'LZT')]
        if mode == 'store':
            for ap_, k_ in lr_items:
                S.dma('sp', self.sc[k_][bi], ap_)
        if mode == 'load':
            for ap_, k_ in lr_items:
                S.dma('sp', ap_, self.sc[k_][bi])
        for d in dirs:
            if first[d]:
                if prompt:
                    self.memset('dve', self.lS[d][:, :], 0.0)
                else:
                    S.dma('sp', self.lS[d][:, :], self.st_l[:, l, d, :])
        if do_f and not (1 in dirs):
            S.dma('sp', self.HL[1][:, :, :], self.sLB[:, :, t0:t0 + 256])
        combos = [(d, pr) for d in sorted(dirs, reverse=True) for pr in range(2)]
        LT = lambda d, n, pr: self.ltd[d][n][:, pr, :]
        pss = {}
        for i, (d, pr) in enumerate(combos):
            ps = self.PA if i % 2 == 0 else self.PB
            off = (i // 2) * 512
            pss[(d, pr)] = (ps, off)
            self.mm(ps[:, off:off + 256], self.LRUW[:, l, (0 * 2 + d) * 2 + pr, :], self.XC[:, pr, :])
            self.mm(ps[:, off + 256:off + 512], self.LRUW[:, l, (1 * 2 + d) * 2 + pr, :], self.XC[:, pr, :])
        for (d, pr) in combos:
            ps, off = pss[(d, pr)]
            self.act(LT(d, "rg", pr), ps[:, off:off + 256], AF.Sigmoid, bias=self.ppc(l, 'L_BA', d * 2 + pr))
            self.act(LT(d, "ig", pr), ps[:, off + 256:off + 512], AF.Sigmoid, bias=self.ppc(l, 'L_BX', d * 2 + pr))
        for (d, pr) in combos:
            self.act(LT(d, "aa", pr), LT(d, "rg", pr), AF.Exp, scale=self.dvc(l, 'CLAM', d * 2 + pr))
            self.act(LT(d, "a2", pr), LT(d, "rg", pr), AF.Exp, scale=self.dvc(l, 'C2LAM', d * 2 + pr))
        for (d, pr) in combos:
            self.ts('dve', LT(d, "a2", pr), LT(d, "a2", pr), -1.0, 1.0, ALU.mult, ALU.add)
            self.tt('dve', LT(d, "bt", pr), LT(d, "ig", pr), self.XC[:, pr, :], ALU.mult)
        for (d, pr) in combos:
            self.act(LT(d, "a2", pr), LT(d, "a2", pr), AF.Sqrt)
        for (d, pr) in combos:
            self.tt('dve', LT(d, "bt", pr), LT(d, "bt", pr), LT(d, "a2", pr), ALU.mult)
        for (d, pr) in combos:
            if d == 0:
                self.scan(self.HL[0][:, pr, :], LT(0, "aa", pr), LT(0, "bt", pr), self.lS[0][:, pr:pr + 1], ALU.mult, ALU.add)
                self.cp('act', self.lS[0][:, pr:pr + 1], self.HL[0][:, pr, 255:256])
            else:
                rv = lambda t: self.V(t, 0, 128, pr * 256 + 255, [[-1, 256]])
                self.scan(rv(self.HL[1]), rv(self.ltd[1]["aa"]), rv(self.ltd[1]["bt"]), self.lS[1][:, pr:pr + 1], ALU.mult, ALU.add)
                self.cp('act', self.lS[1][:, pr:pr + 1], self.HL[1][:, pr, 0:1])
        for d in sorted(dirs, reverse=True):
            if last[d] and prompt:
                S.dma('sp', self.o_l[seq_idx, l, d], self.lS[d][:, :])
        if not do_f:
            S.dma('sp', self.sLB[:, :, t0:t0 + 256], self.HL[1][:, :, :])
            return
        self.tt('dve', self.HL[0][:, :, :], self.HL[0][:, :, :], self.HL[1][:, :, :], ALU.add)
        self.tt('dve', self.MIXT[:, 6:8, :], self.HL[0][:, :, :], self.LZT[:, :, :], ALU.mult)

    def stage_rwkv(self, l, t0, co, W, w0, w1, T, dirs, grid, prompt, seq_idx, first, last, mode='full'):
        S = self.S
        do_f = 0 in dirs
        if mode != 'load':
            for ch in range(11):
                self.proj_fm(1944 + ch * 128, 128, 0, W, lambda ps, ch=ch: self.cp('act' if ch % 2 else 'dve', self.URS[:, ch, 0:W], ps))
            if do_f or mode == 'store':
                for hp in range(3):
                    self.proj_fm(3352 + hp * 128, 128, co, 256, lambda ps, hp=hp: self.act(self.RZT[:, hp, :], ps, AF.Silu))
            U3 = lambda off, n: self.V(self.URS, 0, 128, off, [[384, 11], [1, n]])
            B3 = lambda off, n: self.V(self.BLK, 0, 128, off, [[256, 11], [1, n]])
            if not grid:
                self.cp('dve', B3(1, 255), U3(0, 255))
                self.memset('dve', B3(0, 1), 0.0)
                self.tt('dve', B3(0, 255), B3(0, 255), U3(1, 255), ALU.add)
                wsh = 0.5
            else:
                U4 = lambda off, r, n: self.V(self.URS, 0, 128, off, [[384, 11], [64, r], [1, n]])
                B4 = lambda off, r, n: self.V(self.BLK, 0, 128, off, [[256, 11], [64, r], [1, n]])
                self.cp('dve', B4(1, 4, 63), U4(co, 4, 63))
                self.memset('dve', B4(0, 4, 1), 0.0)
                self.tt('dve', B4(0, 4, 63), B4(0, 4, 63), U4(co + 1, 4, 63), ALU.add)
                if t0 > 0:
                    self.tt('dve', B3(0, 256), B3(0, 256), U3(co - 64, 256), ALU.add)
                else:
                    self.tt('dve', B3(64, 192), B3(64, 192), U3(0, 192), ALU.add)
                if t0 + TB < T:
                    self.tt('dve', B3(0, 256), B3(0, 256), U3(co + 64, 256), ALU.add)
                else:
                    self.tt('dve', B3(0, 192), B3(0, 192), U3(co + 64, 192), ALU.add)
                wsh = 0.25
            mu = self.V(self.PP, 0, 128, l * PL + PC['R_MU'], [[1, 11], [0, 256]])
            self.tt('dve', B3(0, 256), B3(0, 256), mu, ALU.mult)
            omm = self.V(self.DV, 0, 128, l * DL + DC['OMMU'], [[1, 11], [0, 256]])
            self.tt('dve', U3(co, 256), U3(co, 256), omm, ALU.mult)
            self.stt(B3(0, 256), B3(0, 256), wsh, U3(co, 256), ALU.mult, ALU.add)
            for nm, ch in (('blk_r', 0), ('blk_k', 3), ('blk_v', 6), ('blk_wl', 9), ('blk_al', 10)):
                self.dump(nm, self.BLK[:, ch, :])
            rt = self.rt
            kk = self.V(self.PP, 0, 128, l * PL + PC['R_KK'], [[1, 3], [0, 256]])
            kap = rt["d"]
            self.tt('dve', kap[:, :, :], self.BLK[:, 3:6, :], kk, ALU.mult)
            ksq = rt["E"]
            self.tt('dve', ksq[:, :, :], kap[:, :, :], kap[:, :, :], ALU.mult)
            for hp in range(3):
                self.mm(self.PA[:, hp * 256:(hp + 1) * 256], self.CST[:, CC['BONES']:CC['BONES'] + 128], ksq[:, hp, :])
            self.act(ksq[:, :, :], self.V(self.PA, 0, 128, 0, [[256, 3], [1, 256]]), AF.Sqrt)
            self.ts('dve', ksq[:, :, :], ksq[:, :, :], 1e-12, None, ALU.max)
            self.recip(ksq[:, :, :], ksq[:, :, :])
            self.tt('dve', self.KH[:, :, :], kap[:, :, :], ksq[:, :, :], ALU.mult)
            self.dump('kh', self.KH[:, 0, :])
            self.bd_fill(self.BDV, lambda par: self.V(self.BLK, par * 64, 64, 6 * 256, [[256, 3], [64, 4], [1, 64]]))
            for c in range(4):
                for hp in range(3):
                    self.mm(self.PB[:, (c * 3 + hp) * 64:(c * 3 + hp + 1) * 64], self.BDV[:, hp, c, :], self.ISTKB[:, :])
            self.cp('act', self.V(self.VSTK, 0, 128, 0, [[1, 768]]), self.PB[:, 0:768])
        bi = t0 // 256
        rw_items = [(self.BLK[:, :, :], 'BLK'), (self.RZT[:, :, :], 'RZT'), (self.KH[:, :, :], 'KH'), (self.VSTK[:, :, :, :], 'VSTK')]
        if mode == 'store':
            for ap_, k_ in rw_items:
                S.dma('sp', self.sc[k_][bi], ap_)
        if mode == 'load':
            for ap_, k_ in rw_items:
                S.dma('sp', ap_, self.sc[k_][bi])
        for d in dirs:
            if first[d]:
                if prompt:
                    self.memset('dve', self.rH[d][:, :, :], 0.0)
                else:
                    S.dma('sp', self.rH[d][:, :, :], self.st_rH[l, d])
        ybflat = self.V(self.YB, 0, 128, 0, [[1, 768]])
        if do_f and not (1 in dirs):
            S.dma('sp', ybflat, self.sYB[t0 // 256])
        for d in sorted(dirs, reverse=True):
            self.rwkv_dir(l, d, seq_idx, prompt, last)
        if not do_f:
            S.dma('sp', self.sYB[t0 // 256], ybflat)
            return
        ft = self.ft
        self.tt('dve', self.YSQ[:, :, :, :], self.YB[:, :, :, :], self.YB[:, :, :, :], ALU.mult)
        self.reduce(self.rSSQ[:, :], self.V(self.YSQ, 0, 128, 0, [[64, 12], [1, 64]]), ALU.add)
        self.ts('dve', self.rSSQ[:, :], self.rSSQ[:, :], 1.0 / 64, EPS, ALU.mult, ALU.add)
        self.act(self.rSSQ[:, :], self.rSSQ[:, :], AF.Sqrt)
        self.recip(self.rRSTD[:, :], self.rSSQ[:, :])
        self.tt('dve', ft["rk"][:, :, :], self.BLK[:, 0:3, :], self.BLK[:, 3:6, :], ALU.mult)
        for hp in range(3):
            self.mm(self.PB[:, hp * 256:(hp + 1) * 256], self.RKD[:, l, hp, :], ft["rk"][:, hp, :])
        self.tt('dve', ft["bon"][:, :, :], self.V(self.PB, 0, 128, 0, [[256, 3], [1, 256]]), self.BLK[:, 6:9, :], ALU.mult)
        self.memset('pool', self.YBD[:, :, :], 0.0)
        for c in range(4):
            for par in range(2):
                self.tt('dve', self.V(self.YBD, par * 64, 64, par * 64, [[128, 3], [1, 64]]),
                        self.V(self.YB, par * 64, 64, c * 192, [[64, 3], [1, 64]]),
                        self.V(self.rRSTD, par * 64, 64, c * 3, [[1, 3], [0, 64]]), ALU.mult)
            for hp in range(3):
                self.mm(self.PA[:, hp * 256 + c * 64: hp * 256 + (c + 1) * 64], self.YBD[:, hp, :],
                        self.CST[:, CC['ISTK']:CC['ISTK'] + 64])
        for hp in range(3):
            self.stt(ft["t1"][:, hp, :], self.PA[:, hp * 256:(hp + 1) * 256], self.ppc(l, 'R_NORM', hp), ft["bon"][:, hp, :], ALU.mult, ALU.add)
            self.tt('dve', self.MIXT[:, 3 + hp, :], ft["t1"][:, hp, :], self.RZT[:, hp, :], ALU.mult)

    def bd_fill(self, bd, src_of_par, eng='pool'):
        self.memset(eng, bd[:, :, :, :], 0.0)
        for par in range(2):
            self.cp('act' if par == 0 else eng, self.V(bd, par * 64, 64, par * 64, [[512, 3], [128, 4], [1, 64]]), src_of_par(par))

    def rwkv_dir(self, l, d, seq_idx, prompt, last):
        S = self.S
        rt = self.rt
        pb_d = 64 * d
        f3 = lambda t: t[:, :, :]
        v4 = lambda t: self.V(t, 0, 128, 0, [[256, 3], [64, 4], [1, 64]])
        self.act(self.TWL[pb_d:pb_d + 64, :], self.BLK[pb_d:pb_d + 64, 9, :], AF.Tanh)
        for hp in range(3):
            self.mm(self.PA[:, hp * 256:(hp + 1) * 256], self.LORA[pb_d:pb_d + 64, l, 0, hp * 128:(hp + 1) * 128], self.TWL[pb_d:pb_d + 64, :])
        for hp in range(3):
            self.act(rt["sg"][:, hp, :], self.PA[:, hp * 256:(hp + 1) * 256], AF.Sigmoid, bias=self.ppc(l, 'R_W0', d * 3 + hp))
        for hp in range(3):
            self.mm(self.PB[:, hp * 256:(hp + 1) * 256], self.LORA[pb_d:pb_d + 64, l, 1, hp * 128:(hp + 1) * 128], self.BLK[pb_d:pb_d + 64, 10, :])
        for hp in range(3):
            self.act(rt["aa"][:, hp, :], self.PB[:, hp * 256:(hp + 1) * 256], AF.Sigmoid, bias=self.ppc(l, 'R_A0', d * 3 + hp))
        for hp in range(3):
            self.ts('dve', rt["kt"][:, hp, :], rt["aa"][:, hp, :], self.ppc(l, 'R_KA', hp), self.dvc(l, 'OMKA', hp), ALU.mult, ALU.add)
        self.tt('dve', f3(rt["kt"]), f3(rt["kt"]), self.BLK[:, 3:6, :], ALU.mult)
        self.tt('dve', f3(rt["bb"]), self.KH[:, :, :], f3(rt["aa"]), ALU.mult)
        flat = lambda t: self.V(t, 0, 128, 0, [[1, 768]])
        self.scan(flat(rt["cs"]), self.CST[:, CC['RMASK']:CC['RMASK'] + 768], flat(rt["sg"]), 0.0, ALU.mult, ALU.add)
        self.cp('dve', self.GL[:, :, :], self.V(rt["cs"], 0, 128, 63, [[256, 3], [64, 4]]))
        if d == 1:
            csLb = self.V(rt["cs"], 0, 128, 63, [[256, 3], [64, 4], [0, 64]])
            self.tt('dve', v4(rt["d"]), csLb, v4(rt["cs"]), ALU.subtract)
            self.tt('dve', f3(rt["cs"]), f3(rt["d"]), f3(rt["sg"]), ALU.add)
        self.act(f3(rt["E"]), f3(rt["cs"]), AF.Exp, scale=-DSC)
        self.tt('dve', self.V(self.KR, 0, 128, 64, [[512, 3], [128, 4], [1, 64]]),
                self.V(self.BLK, 0, 128, 0, [[256, 3], [64, 4], [1, 64]]), v4(rt["E"]), ALU.mult)
        self.tt('dve', f3(rt["d"]), f3(rt["cs"]), f3(rt["sg"]), ALU.subtract)
        self.act(f3(rt["E"]), f3(rt["d"]), AF.Exp, scale=-DSC)
        self.tt('dve', self.V(self.KR, 0, 128, 0, [[512, 3], [128, 4], [1, 64]]), v4(self.KH), v4(rt["E"]), ALU.mult)
        self.act(f3(rt["E"]), f3(rt["cs"]), AF.Exp, scale=DSC)
        self.tt('dve', self.BTT[:, :, :], f3(rt["bb"]), f3(rt["E"]), ALU.mult)
        self.tt('dve', self.KTT[:, :, :], f3(rt["kt"]), f3(rt["E"]), ALU.mult)
        self.act(self.GL[:, :, :], self.GL[:, :, :], AF.Exp, scale=-DSC)
        BD = self.BD
        self.memset('pool', self.A1BD[:, :, :, :], 0.0)
        self.memset('pool', self.A2BD[:, :, :, :], 0.0)
        self.memset('pool', self.NN[0][:, :, :], 0.0)
        c4 = lambda t, par: self.V(t, par * 64, 64, 0, [[256, 3], [64, 4], [1, 64]])
        self.bd_fill(BD["kt"], lambda par: c4(self.KTT, par))
        self.bd_fill(BD["b"], lambda par: c4(self.BTT, par))
        self.bd_fill(BD["kh"], lambda par: self.V(self.KR, par * 64, 64, 0, [[512, 3], [128, 4], [1, 64]]))
        self.bd_fill(BD["r"], lambda par: self.V(self.KR, par * 64, 64, 64, [[512, 3], [128, 4], [1, 64]]))
        for (src, dst, neg) in ((BD["kt"], self.KTTOK, False), (BD["b"], self.BTTOK, True)):
            for half in range(2):
                for cc_ in range(2):
                    c = half * 2 + cc_
                    for hp in range(3):
                        self.tr(self.PTB[:, (cc_ * 3 + hp) * 128:(cc_ * 3 + hp + 1) * 128], src[:, hp, c, :], bf=True)
                o = self.V(dst, 0, 128, half * 768, [[1, 768]])
                if neg:
                    self.act(o, self.PTB[:, 0:768], AF.Copy, scale=-1.0)
                else:
                    self.cp('dve', o, self.PTB[:, 0:768])
        self.cp('act', self.HBF[:, :, :], self.rH[d][:, :, :])
        mk = CC['MKF'] if d == 0 else CC['MKB']
        nmk = CC['NMKF'] if d == 0 else CC['NMKB']
        mn = CC['MNF'] if d == 0 else CC['MNB']
        for j in range(4):
            c = j if d == 0 else 3 - j
            cs = slice(c * 64, (c + 1) * 64)
            KRc = lambda hp: self.V(self.KR, 0, 128, hp * 512 + c * 128, [[1, 128]])
            p1, p2, p3 = self.pq(), self.pq(), self.pq()
            for hp in range(3):
                self.mm(p1[:, hp * 128:(hp + 1) * 128], BD["kt"][:, hp, c, :], KRc(hp), inc=(hp == 2))
            for hp in range(3):
                self.mm(p2[:, hp * 128:(hp + 1) * 128], BD["b"][:, hp, c, :], KRc(hp), inc=(hp == 2))
            for hp in range(3):
                self.mm(p3[:, hp * 64:(hp + 1) * 64], BD["kh"][:, hp, c, :], self.BTT[:, hp, cs], inc=(hp == 2))
            for par in range(2):
                pp_ = par * 64
                self.tt('dve', self.V(self.A1BD, pp_, 64, pp_, [[256, 3], [128, 2], [1, 64]]),
                        self.V(p1, pp_, 64, 0, [[128, 3], [64, 2], [1, 64]]),
                        self.V(self.CST, pp_, 64, mk, [[0, 3], [64, 2], [1, 64]]), ALU.mult)
                self.tt('dve', self.V(self.A2BD, pp_, 64, pp_, [[256, 3], [128, 2], [1, 64]]),
                        self.V(p2, pp_, 64, 0, [[128, 3], [64, 2], [1, 64]]),
                        self.V(self.CST, pp_, 64, nmk, [[0, 3], [64, 2], [1, 64]]), ALU.mult)
                self.tt('dve', self.V(self.NN[0], pp_, 64, pp_, [[128, 3], [1, 64]]),
                        self.V(p3, pp_, 64, 0, [[64, 3], [1, 64]]),
                        self.V(self.CST, pp_, 64, mn, [[0, 3], [1, 64]]), ALU.mult)
            pr_ = self.pq()
            for hp in range(3):
                o = pr_[:, hp * 64:(hp + 1) * 64]
                self.mm(o, BD["kh"][:, hp, c, :], self.HBF[:, hp, :], start=True, stop=False)
                self.mm(o, self.A1BD[:, hp, 0, :], self.VSTK[:, c, hp, :], start=False, stop=True, inc=(hp == 2))
            u192 = self.V(self.U, 0, 128, 0, [[1, 192]])
            ub192 = self.V(self.UBF, 0, 128, 0, [[1, 192]])
            self.cp('dve', ub192, pr_[:, 0:192])
            self.cp('dve', u192, pr_[:, 0:192])
            for k in range(6):
                NTk = self.A2BD[:, :, 0, :] if k == 0 else self.NTT[k % 2][:, :, :]
                Nk = self.NN[k % 2]
                if k < 5:
                    pnt = self.pq()
                    for hp in range(3):
                        self.mm(pnt[:, hp * 128:(hp + 1) * 128], Nk[:, hp, :], NTk[:, hp, :], inc=(hp == 2))
                    pn = self.pq()
                    for hp in range(3):
                        self.mm(pn[:, hp * 128:(hp + 1) * 128], NTk[:, hp, :], Nk[:, hp, :], inc=(hp == 2))
                pu = self.pq()
                for hp in range(3):
                    self.mm(pu[:, hp * 64:(hp + 1) * 64], NTk[:, hp, :], self.UBF[:, hp, :], inc=(hp == 2))
                if k < 5:
                    self.cp('dve', self.V(self.NTT[(k + 1) % 2], 0, 128, 0, [[1, 384]]), pnt[:, 0:384])
                    self.cp('act', self.V(self.NN[(k + 1) % 2], 0, 128, 0, [[1, 384]]), pn[:, 0:384])
                self.tt('dve', ub192, u192, pu[:, 0:192], ALU.add)
                if k < 5:
                    self.tt('dve', u192, u192, pu[:, 0:192], ALU.add)
            py = self.pq()
            for hp in range(3):
                o = py[:, hp * 64:(hp + 1) * 64]
                self.mm(o, BD["r"][:, hp, c, :], self.HBF[:, hp, :], start=True, stop=False)
                self.mm(o, self.A1BD[:, hp, 1, :], self.VSTK[:, c, hp, :], start=False, stop=False)
                self.mm(o, self.A2BD[:, hp, 1, :], self.UBF[:, hp, :], start=False, stop=True, inc=(hp == 2))
            ybv = self.V(self.YB, 0, 128, c * 192, [[1, 192]])
            if d == 1:
                self.cp('act', ybv, py[:, 0:192])
            else:
                self.tt('dve', ybv, ybv, py[:, 0:192], ALU.add)
            ph = self.pq()
            for hp in range(3):
                o = ph[:, hp * 64:(hp + 1) * 64]
                self.mm(o, self.KTTOK[:, c, hp, :], self.VSTK[:, c, hp, :], start=True, stop=False)
                self.mm(o, self.BTTOK[:, c, hp, :], self.UBF[:, hp, :], start=False, stop=True, inc=(hp == 2))
            self.tt('dve', self.HTMP[:, :, :], self.rH[d][:, :, :], self.V(ph, 0, 128, 0, [[64, 3], [1, 64]]), ALU.add)
            self.tt('dve', self.HBF[:, :, :], self.HTMP[:, :, :], self.V(self.GL, 0, 128, c, [[4, 3], [0, 64]]), ALU.mult)
            self.tt('dve', self.rH[d][:, :, :], self.HTMP[:, :, :], self.V(self.GL, 0, 128, c, [[4, 3], [0, 64]]), ALU.mult)
        if last[d] and prompt:
            S.dma('sp', self.o_rH[seq_idx, l, d], self.rH[d][:, :, :])

    def stage_out(self, l, mod, xsrc, xdst, r0):
        S = self.S
        for tt_ in range(2):
            o, xw = self.XN[tt_], self.XW[tt_]
            S.dma('sp', xw[:, :], xsrc[r0 + tt_ * 128: r0 + (tt_ + 1) * 128, :])
            for ch in range(2):
                ps = self.pq()
                for kc in range(8):
                    self.mm(ps[:, :], self.MIXT[:, kc, tt_ * 128:(tt_ + 1) * 128], self.W_OUT[:, kc, ch * 512:(ch + 1) * 512],
                            start=(kc == 0), stop=(kc == 7))
                self.cp('act' if ch else 'dve', o[:, ch * 512:(ch + 1) * 512], ps[:, :])
            self.dump('o_proj%d' % tt_, o[:, :])
            self.dump('o_x%d' % tt_, xw[:, :])
            ssq = self.TMPS[:, 24 + tt_:25 + tt_]
            junk = self.V(self.BLK, 0, 128, 0, [[1, D]])
            self.act(junk, o[:, :], AF.Square, accum=ssq)
            rs = self.TMPS[:, 26 + tt_:27 + tt_]
            self.ts('dve', rs, ssq, 1.0 / D, EPS, ALU.mult, ALU.add)
            self.act(rs, rs, AF.Sqrt)
            self.recip(rs, rs)
            self.dump('o_rs%d' % tt_, rs)
            self.stt(o[:, :], o[:, :], rs, self.GATEB[:, mod, :], ALU.mult, ALU.mult)
            self.dump('o_g%d' % tt_, o[:, :])
            self.tt('dve', o[:, :], o[:, :], xw[:, :], ALU.add)
            S.dma('sp', xdst[r0 + tt_ * 128: r0 + (tt_ + 1) * 128, :], o[:, :])

    def build(self, layers=(0, 1)):
        try:
            self._build(layers)
        except StopIteration:
            pass
        self.S.finish('sp')
        return self.nc

    def _build(self, layers):
        NP, TS = self.NP, self.TS
        self.setup()
        if self.stop == 'setup':
            raise StopIteration
        for li, l in enumerate(layers):
            self.load_layer(l)
            if self.stop == 'load':
                raise StopIteration
            xsrc = self.x_in if li == 0 else self.x1
            xdst = self.y_out if li == len(layers) - 1 else self.x1
            T_, F_ = {0: True, 1: True}, {0: False, 1: False}
            for s in range(NP):
                self.visit(l, 0, xsrc, xdst, s * 256, 256, 0, [0, 1], False, s, T_, T_)
            if TS > 0:
                nb = TS // TB
                row0 = NP * 256
                for b in range(nb - 1, -1, -1):
                    self.visit(l, 1, xsrc, xdst, row0, TS, b * TB, [1], True, 0,
                               {0: False, 1: b == nb - 1}, {0: False, 1: b == 0}, mode='store')
                for b in range(nb):
                    self.visit(l, 1, xsrc, xdst, row0, TS, b * TB, [0], True, 0,
                               {0: b == 0, 1: False}, {0: b == nb - 1, 1: False}, mode='load')


def prep_shared(inp):
    f = lambda a: np.ascontiguousarray(np.asarray(a, dtype=np.float32))
    b_mod = f(inp['b_mod'])
    sh = {}
    sh['w_mod'] = f(inp['w_mod'])
    sh['w_in'] = f(inp['w_in'])
    sh['w_out'] = f(inp['w_out'])
    sh['bmodT'] = f(b_mod[:, :2048].reshape(2, 16, 128).transpose(2, 0, 1))
    sh['bmodg'] = f(b_mod[:, 2048:3072])
    sh['gpost'] = f(inp['g_post'])
    pp = np.zeros((128, 2 * PL), np.float32)
    for l in range(2):
        o = l * PL
        def put(key, arr, n):
            pp[:, o + PC[key]: o + PC[key] + n] = np.asarray(arr, np.float32).reshape(n, 128).T
        put('G_PRE', inp['g_pre'][l], 8)
        put('M_NORM', inp['m_norm'][l], 3)
        put('R_MU', inp['r_mu'][l], 11)
        put('R_W0', np.asarray(inp['r_w0'][l]).reshape(-1), 6)
        put('R_A0', np.asarray(inp['r_a0'][l]).reshape(-1), 6)
        put('R_KK', inp['r_kk'][l], 3)
        put('R_KA', inp['r_ka'][l], 3)
        put('R_RK', inp['r_rk'][l], 3)
        put('R_NORM', inp['r_norm'][l], 3)
        put('L_CONV', np.asarray(inp['l_conv'][l]).reshape(-1), 8)
        put('L_CONVB', inp['l_conv_b'][l], 2)
        put('L_BA', np.asarray(inp['l_ba'][l]).reshape(-1), 4)
        put('L_BX', np.asarray(inp['l_bx'][l]).reshape(-1), 4)
        put('L_LAM', np.asarray(inp['l_lambda'][l]).reshape(-1), 4)
        pp[0:6, o + PC['M_BI']: o + PC['M_BI'] + 2] = np.asarray(inp['m_bi'][l], np.float32).T
        pp[0:6, o + PC['M_BF']: o + PC['M_BF'] + 2] = np.asarray(inp['m_bf'][l], np.float32).T
    sh['pp'] = pp
    sh['cst'] = make_consts()
    lora = np.zeros((128, 2, 2, 384), np.float32)
    for wi, key in enumerate(['r_w2', 'r_a2']):
        a = np.asarray(inp[key], np.float32)
        lora[:, :, wi, :] = a.transpose(1, 2, 0, 3).reshape(128, 2, 384)
    sh['lora'] = lora
    lruw = np.zeros((128, 2, 8, 128), np.float32)
    for gi, key in enumerate(['l_wa', 'l_wx']):
        a = np.asarray(inp[key], np.float32)
        for l in range(2):
            for d in range(2):
                for pr in range(2):
                    for hb in range(2):
                        n = 2 * pr + hb
                        lruw[hb * 64:(hb + 1) * 64, l, (gi * 2 + d) * 2 + pr, hb * 64:(hb + 1) * 64] = a[l, d, n]
    sh['lruw'] = lruw
    return sh


def prep_core(inp, b, NP, TS):
    f = lambda a: np.ascontiguousarray(np.asarray(a, dtype=np.float32))
    m = {}
    xp = np.asarray(inp['x_prompt'], np.float32)[b * NP:(b + 1) * NP].reshape(NP * 256, D)
    if TS > 0:
        xs = np.asarray(inp['x_sample'], np.float32)[b]
        m['x_in'] = f(np.concatenate([xp, xs], 0))
    else:
        m['x_in'] = f(xp)
    cc = np.stack([np.asarray(inp['c_ctx'], np.float32), np.asarray(inp['c'], np.float32)[b]], -1)
    m['cc'] = f(cc.reshape(8, 128, 2).transpose(1, 0, 2))
    C = np.asarray(inp['state_mlstm_C'], np.float32)[b]
    n = np.asarray(inp['state_mlstm_n'], np.float32)[b]
    Cn = np.concatenate([C, n[..., None]], -1)
    Cn = Cn.reshape(2, 2, 3, 2, 64, 65).transpose(0, 1, 3, 4, 2, 5).reshape(2, 2, 128, 3, 65)
    m['st_mC'] = f(Cn)
    m['st_mm'] = f(np.asarray(inp['state_mlstm_m'], np.float32)[b].transpose(2, 0, 1))
    R = np.asarray(inp['state_rwkv'], np.float32)[b]
    R = R.transpose(0, 1, 2, 4, 3)
    R = R.reshape(2, 2, 3, 2, 64, 64).transpose(0, 1, 3, 4, 2, 5).reshape(2, 2, 128, 3, 64)
    m['st_rH'] = f(R)
    L = np.asarray(inp['state_rglru'], np.float32)[b]
    m['st_l'] = f(L.reshape(2, 2, 2, 128).transpose(3, 0, 1, 2))
    return m


def unpack_core(r, NP, TS):
    y = r['y_out']
    yp = y[:NP * 256].reshape(NP, 256, D)
    ys = y[NP * 256:]
    mC = r['o_mC'].reshape(NP, 2, 2, 2, 64, 3, 65).transpose(0, 1, 2, 5, 3, 4, 6).reshape(NP, 2, 2, 6, 64, 65)
    newC = np.ascontiguousarray(mC[..., :64])
    newn = np.ascontiguousarray(mC[..., 64])
    newm = r['o_mm'].reshape(NP, 2, 2, 6)
    rH = r['o_rH'].reshape(NP, 2, 2, 2, 64, 3, 64).transpose(0, 1, 2, 5, 3, 4, 6).reshape(NP, 2, 2, 6, 64, 64)
    newr = np.ascontiguousarray(rH.transpose(0, 1, 2, 3, 5, 4))
    newl = np.ascontiguousarray(r['o_l'].transpose(0, 1, 2, 4, 3).reshape(NP, 2, 2, 256))
    return yp, ys, newC, newn, newm, newr, newl


_NC_CACHE = {}


def kernel(**inputs):
    NP, TS = 4, 2048
    key = (NP, TS)
    if key not in _NC_CACHE:
        _NC_CACHE[key] = Builder(NP, TS).build()
    nc = _NC_CACHE[key]
    sh = prep_shared(inputs)
    in_maps = []
    for b in range(NCORES):
        m = dict(sh)
        m.update(prep_core(inputs, b, NP, TS))
        in_maps.append(m)
    res = run_bass_kernel_spmd(nc, in_maps, core_ids=list(range(NCORES)))
    outs = [unpack_core(r, NP, TS) for r in res.results]
    y_prompt = np.concatenate([o[0] for o in outs], 0)
    y_sample = np.stack([o[1] for o in outs], 0)
    cat = lambda i: np.concatenate([o[i] for o in outs], 0)
    return (y_prompt.astype(np.float32), y_sample.astype(np.float32), cat(2).astype(np.float32),
            cat(3).astype(np.float32), cat(4).astype(np.float32), cat(5).astype(np.float32), cat(6).astype(np.float32))
```

```python
import numpy as np
import concourse.bass as bass
import concourse.mybir as mybir
from concourse.bass_utils import run_bass_kernel_spmd

F32 = mybir.dt.float32
BF16 = mybir.dt.bfloat16
AF = mybir.ActivationFunctionType
ALU = mybir.AluOpType
AX = mybir.AxisListType

D = 1024
IN_COLS = 4248
EPS = 1e-6
DSC = 0.6065306597126334
TB = 256
NCORES = 8


def _prod(xs):
    r = 1
    for x in xs:
        r *= int(x)
    return r


class Sync:
    def __init__(self, nc, n_dma_sems=32):
        self.nc = nc
        self.engs = {'pe': nc.tensor, 'dve': nc.vector, 'act': nc.scalar,
                     'pool': nc.gpsimd, 'sp': nc.sync}
        self.sem = {}
        self.cnt = {}
        for e in ['pe', 'dve', 'act', 'pool']:
            self.sem[e] = nc.alloc_semaphore('sem_' + e)
            self.cnt[e] = 0
        self.seen = {e: {} for e in self.engs}
        self.dma_ring = [nc.alloc_semaphore('dq_%d' % i) for i in range(n_dma_sems)]
        self.dma_uses = [0] * n_dma_sems
        self.dma_next = 0
        self.rec = {}
        self.untracked = set()
        self.n_wait = 0
        self.n_ins = 0
        self.pstep_cache = {}
        self.sb_addr = {}

    def region(self, ap):
        t = ap.tensor
        name = t.name
        apl = [(int(s), int(c)) for (s, c) in ap.ap]
        off = int(ap.offset)
        if type(t).__name__.startswith('DRam'):
            lo = off + sum(min(0, s * (c - 1)) for s, c in apl)
            hi = off + sum(max(0, s * (c - 1)) for s, c in apl) + 1
            return (name, 0, 1, lo, hi)
        pstep = self.pstep_cache.get(name)
        if pstep is None:
            pstep = _prod(list(t.shape)[1:])
            self.pstep_cache[name] = pstep
        p0 = off // pstep
        f0 = off % pstep
        npart = apl[0][1]
        rest = apl[1:]
        lo = f0 + sum(min(0, s * (c - 1)) for s, c in rest)
        hi = f0 + sum(max(0, s * (c - 1)) for s, c in rest) + 1
        if name in self.sb_addr:
            base, es = self.sb_addr[name]
            return ('SB', p0, p0 + npart, base + lo * es, base + hi * es)
        return ('PS:' + name, (p0 // 32) * 32, ((p0 + npart + 31) // 32) * 32, (lo // 512) * 512, ((hi + 511) // 512) * 512)

    @staticmethod
    def _ovl(a, b):
        return a[1] < b[2] and b[1] < a[2] and a[3] < b[4] and b[3] < a[4]

    @staticmethod
    def _contains(a, b):
        return a[1] <= b[1] and b[2] <= a[2] and a[3] <= b[3] and b[4] <= a[4]

    def _collect(self, e, reads, writes):
        deps = {}
        own = self.sem.get(e)
        rregs = [self.region(a) for a in reads]
        wregs = [self.region(a) for a in writes]
        for r in rregs:
            if r[0] in self.untracked:
                continue
            isps = r[0].startswith('PS:')
            for (reg, kind, sem, val) in self.rec.get(r[0], ()):
                if (kind == 'w' or (isps and sem is not own)) and self._ovl(reg, r):
                    if e == 'pe' and sem is own:
                        continue
                    k = id(sem)
                    if deps.get(k, (None, 0))[1] < val:
                        deps[k] = (sem, val)
        for w in wregs:
            if w[0] in self.untracked:
                continue
            for (reg, kind, sem, val) in self.rec.get(w[0], ()):
                if self._ovl(reg, w):
                    if sem is own:
                        continue
                    k = id(sem)
                    if deps.get(k, (None, 0))[1] < val:
                        deps[k] = (sem, val)
        return deps, rregs, wregs

    def _record(self, rregs, wregs, sem, val):
        for r in rregs:
            if r[0] in self.untracked:
                continue
            lst = self.rec.setdefault(r[0], [])
            lst[:] = [x for x in lst if not (x[1] == 'r' and x[2] is sem and self._contains(r, x[0]))]
            lst.append((r, 'r', sem, val))
        for w in wregs:
            if w[0] in self.untracked:
                continue
            lst = self.rec.setdefault(w[0], [])
            lst[:] = [x for x in lst if not self._contains(w, x[0])]
            lst.append((w, 'w', sem, val))

    def wait(self, e, sem, val):
        k = id(sem)
        if self.seen[e].get(k, 0) >= val:
            return
        self.engs[e].wait_ge(sem, val)
        self.seen[e][k] = val
        self.n_wait += 1

    max_ins = None
    paranoid = False
    embed_waits = True

    def emit(self, e, reads, writes, build, inc=True):
        if self.max_ins is not None and self.n_ins >= self.max_ins:
            raise StopIteration
        deps, rregs, wregs = self._collect(e, reads, writes)
        embed = None
        for (sem, val) in deps.values():
            if self.embed_waits and embed is None and self.seen[e].get(id(sem), 0) < val:
                embed = (sem, val)
                continue
            self.wait(e, sem, val)
        if self.paranoid:
            for e2 in ['pe', 'dve', 'act', 'pool']:
                if self.cnt[e2] > 0 and not (e == 'pe' and e2 == 'pe'):
                    self.wait(e, self.sem[e2], self.cnt[e2])
        ins = build(self.engs[e])
        if embed is not None:
            ins._wait_ge(embed[0], embed[1])
            self.seen[e][id(embed[0])] = embed[1]
        self.n_ins += 1
        if inc:
            self.cnt[e] += 1
            ins.then_inc(self.sem[e], 1)
            val = self.cnt[e]
        else:
            val = self.cnt[e] + 1
        self._record(rregs, wregs, self.sem[e], val)
        return ins

    def dma(self, q, out, in_, **kw):
        if self.max_ins is not None and self.n_ins >= self.max_ins:
            raise StopIteration
        i = self.dma_next
        self.dma_next = (i + 1) % len(self.dma_ring)
        sem = self.dma_ring[i]
        uses = self.dma_uses[i]
        if uses > 0:
            self.wait(q, sem, 16 * uses)
        deps, rregs, wregs = self._collect(q, [in_], [out])
        for (s, v) in deps.values():
            self.wait(q, s, v)
        ins = self.engs[q].dma_start(out=out, in_=in_, **kw)
        ins.then_inc(sem, 16)
        self.n_ins += 1
        self.dma_uses[i] = uses + 1
        self._record(rregs, wregs, sem, 16 * (uses + 1))
        return ins

    def finish(self, q='sp'):
        for i, sem in enumerate(self.dma_ring):
            if self.dma_uses[i] > 0:
                self.wait(q, sem, 16 * self.dma_uses[i])
        for e in ['pe', 'dve', 'act', 'pool']:
            if self.cnt[e] > 0:
                self.wait(q, self.sem[e], self.cnt[e])


class Arena:
    def __init__(self, base, size):
        self.base, self.size, self.ptr = base, size, 0

    def take(self, nbytes):
        off = (self.ptr + 31) // 32 * 32
        self.ptr = off + nbytes
        assert self.ptr <= self.size, ("arena overflow", self.ptr, self.size)
        return self.base + off


PL = 72
PC = dict(G_PRE=0, M_NORM=8, R_MU=11, R_W0=22, R_A0=28, R_KK=34, R_KA=37, R_RK=40, R_NORM=43,
          L_CONV=46, L_CONVB=54, L_BA=56, L_BX=60, L_LAM=64, M_BI=68, M_BF=70)
DL = 32
DC = dict(OMKA=0, CLAM=3, C2LAM=7, NBF=11, OMMU=16)
CC = dict(IDENT=0, BONES=128, MKF=256, MKB=384, MNF=512, MNB=576, MUI=640, MLI=704, RMASK=768,
          LSEL=1536, PSEL=1664, NMKF=1668, NMKB=1796, ONES=1924, ISTK=2052)
NCST = 2052 + 64


def make_consts():
    c = np.zeros((128, NCST), np.float32)
    c[:, 0:128] = np.eye(128)
    c[0:64, 128:192] = 1.0
    c[64:128, 192:256] = 1.0
    s = np.arange(64)[:, None]
    t = np.arange(64)[None, :]
    us, ui = (s < t).astype(np.float32), (s <= t).astype(np.float32)
    ls, li = (s > t).astype(np.float32), (s >= t).astype(np.float32)
    c[0:64, 256:320], c[0:64, 320:384] = us, ui
    c[0:64, 384:448], c[0:64, 448:512] = ls, li
    c[0:64, 512:576] = -ls
    c[0:64, 576:640] = -us
    c[0:64, 640:704] = ui
    c[0:64, 704:768] = li
    rm = np.ones(768, np.float32)
    rm[::64] = 0.0
    c[:, 768:1536] = rm[None, :]
    for k in range(6):
        c[k, 1536 + (k % 2) * 64: 1536 + (k % 2) * 64 + 64] = 1.0
        c[k, 1664 + k // 2] = 1.0
    c[0:64, 1668:1796] = -c[0:64, 256:384]
    c[0:64, 1796:1924] = -c[0:64, 384:512]
    c[:, 1924:2052] = 1.0
    for (a, b) in ((256, 768), (1668, 1924)):
        c[64:128, a:b] = c[0:64, a:b]
    c[0:64, 2052:2116] = np.eye(64)
    c[64:128, 2052:2116] = np.eye(64)
    return c


class Builder:
    def __init__(self, NP, TS, debug=False, stop=None):
        self.stop = stop
        self.NP, self.TS = NP, TS
        self.NTOK = NP * 256 + TS
        self.debug = debug
        nc = self.nc = bass.Bass("TRN2", target_bir_lowering=False)
        self.S = Sync(nc)
        self._decl_dram()
        self._alloc()

    def _decl_dram(self):
        nc, NP, TS = self.nc, self.NP, self.TS
        di = lambda n, s: nc.dram_tensor(n, list(s), F32, kind="ExternalInput").ap()
        do = lambda n, s: nc.dram_tensor(n, list(s), F32, kind="ExternalOutput").ap()
        dx = lambda n, s: nc.dram_tensor(n, list(s), F32, kind="Internal").ap()
        self.x_in = di("x_in", [self.NTOK, D])
        self.cc = di("cc", [128, 8, 2])
        self.w_mod = di("w_mod", [2, D, 3 * D])
        self.bmodT = di("bmodT", [128, 2, 16])
        self.bmodg = di("bmodg", [2, D])
        self.gpost = di("gpost", [2, D])
        self.w_in = di("w_in", [2, D, IN_COLS])
        self.w_out = di("w_out", [2, D, D])
        self.pp = di("pp", [128, 2 * PL])
        self.cst = di("cst", [128, NCST])
        self.lora = di("lora", [128, 2, 2, 384])
        self.lruw = di("lruw", [128, 2, 8, 128])
        self.st_mC = di("st_mC", [2, 2, 128, 3, 65])
        self.st_mm = di("st_mm", [6, 2, 2])
        self.st_rH = di("st_rH", [2, 2, 128, 3, 64])
        self.st_l = di("st_l", [128, 2, 2, 2])
        for n in ["x_in", "cc", "w_mod", "bmodT", "bmodg", "gpost", "w_in", "w_out", "pp", "cst", "lora",
                  "lruw", "st_mC", "st_mm", "st_rH", "st_l"]:
            self.S.untracked.add(n)
        self.y_out = do("y_out", [self.NTOK, D])
        self.o_mC = do("o_mC", [NP, 2, 2, 128, 3, 65])
        self.o_mm = do("o_mm", [NP, 2, 2, 6, 1])
        self.o_rH = do("o_rH", [NP, 2, 2, 128, 3, 64])
        self.o_l = do("o_l", [NP, 2, 2, 128, 2])
        self.x1 = dx("x1", [self.NTOK, D])
        self.sHB = dx("sHB", [max(TS, 64), 384])
        self.sYB = dx("sYB", [max(TS // 256, 1), 128, 768])
        self.sLB = dx("sLB", [128, 2, max(TS, 64)])
        nb = max(TS // 256, 1)
        dxt = lambda n, s_, dt: nc.dram_tensor(n, list(s_), dt, kind="Internal").ap()
        self.sc = {
            'QT': dxt("sc_QT", [nb, 128, 3, 256], BF16), 'KT': dxt("sc_KT", [nb, 128, 3, 256], BF16),
            'KTOK': dxt("sc_KTOK", [nb, 64, 4, 384], BF16), 'VAUG': dxt("sc_VAUG", [nb, 64, 4, 6, 65], BF16),
            'GOZ': dxt("sc_GOZ", [nb, 128, 3, 256], F32), 'GI': dxt("sc_GI", [nb, 6, 256], F32),
            'LF': dxt("sc_LF", [nb, 6, 256], F32), 'XC': dxt("sc_XC", [nb, 128, 2, 256], F32),
            'LZT': dxt("sc_LZT", [nb, 128, 2, 256], F32), 'BLK': dxt("sc_BLK", [nb, 128, 11, 256], F32),
            'RZT': dxt("sc_RZT", [nb, 128, 3, 256], F32), 'KH': dxt("sc_KH", [nb, 128, 3, 256], F32),
            'VSTK': dxt("sc_VSTK", [nb, 128, 4, 3, 64], BF16),
        }
        if self.debug:
            self.dbg = do("dbg", [128, 32768])
            self.dbg_map = {}
            self.dbg_off = 0

    def T(self, name, shape, dtype, arena):
        es = 2 if dtype == BF16 else 4
        nb = _prod(shape[1:]) * es
        off = arena.take(nb)
        t = self.nc.alloc_sbuf_tensor_at(name, list(shape), dtype, offset=off)
        self.S.sb_addr[t.name] = (off, es)
        return t

    def _alloc(self):
        nc = self.nc
        B0 = 16384 + 256
        LIM = 224 * 1024 - 256
        P = Arena(B0, LIM - B0)
        T = self.T
        self.W_IN = T("W_IN", [128, 8, IN_COLS], BF16, P)
        self.W_OUT = T("W_OUT", [128, 8, D], BF16, P)
        self.CST = T("CST", [128, NCST], F32, P)
        self.PP = T("PP", [128, 2 * PL], F32, P)
        self.DV = T("DV", [128, 2 * DL], F32, P)
        self.LORA = T("LORA", [128, 2, 2, 384], F32, P)
        self.LRUW = T("LRUW", [128, 2, 8, 128], F32, P)
        self.RKD = T("RKD", [128, 2, 3, 128], F32, P)
        self.GS = T("GS", [128, 2, 8], F32, P)
        self.SH = T("SH", [128, 2, 8], F32, P)
        self.GATEB = T("GATEB", [128, 2, D], F32, P)
        self.IDB = T("IDB", [128, 128], BF16, P)
        self.ISTKB = T("ISTKB", [128, 64], BF16, P)
        self.mC = [T("mC%d" % d, [128, 3, 65], F32, P) for d in range(2)]
        self.mM = [T("mM%d" % d, [6, 1], F32, P) for d in range(2)]
        self.rH = [T("rH%d" % d, [128, 3, 64], F32, P) for d in range(2)]
        self.lS = [T("lS%d" % d, [128, 2], F32, P) for d in range(2)]
        XB = Arena(P.take(16384), 16384)
        self.HT = T("HT", [128, 8, 384], BF16, P)
        self.MIXT = T("MIXT", [128, 8, 256], BF16, P)
        self.BLK = T("BLK", [128, 11, 256], F32, P)
        abase = P.take(0)
        asize = P.size - P.ptr
        self.asize = asize
        mk = lambda: Arena(abase, asize)
        a = Arena(XB.base, XB.size)
        self.XW = [T("XW%d" % i, [128, D], F32, a) for i in range(2)]
        self.XN = [T("XN%d" % i, [128, D], F32, a) for i in range(2)]
        a = Arena(XB.base, XB.size)
        self.YB = T("YB", [128, 4, 3, 64], F32, a)
        self.RZT = T("RZT", [128, 3, 256], F32, a)
        self.WSTG = [T("WSTG0", [128, 8, 256], F32, Arena(self.S.sb_addr[self.BLK.name][0], 11264)),
                     T("WSTG1", [128, 8, 256], F32, Arena(XB.base, 8192)),
                     T("WSTG2", [128, 8, 256], F32, Arena(XB.base + 8192, 8192))]
        a = mk()
        self.WMs = [T("WM%d" % i, [128, 8, 512], F32, a) for i in range(2)]
        self.SCT = T("SCT", [128, 8, 2], F32, a)
        self.SCB = T("SCB", [128, 2, 8, 128], F32, a)
        self.MODT = T("MODT", [128, 16, 2], F32, a)
        self.BMT = T("BMT", [128, 2, 16], F32, a)
        awm = Arena(self.S.sb_addr[self.WMs[0].name][0], 16384)
        self.BG = T("BG", [128, D], F32, awm)
        self.GP = T("GP", [128, D], F32, awm)
        self.TMPS = T("TMPS", [128, 32], F32, a)
        self.MODROW = T("MODROW", [2, 512], F32, a)
        a = mk()
        self.QT = T("QT", [128, 3, 256], BF16, a)
        self.KT = T("KT", [128, 3, 256], BF16, a)
        self.GOZ = T("GOZ", [128, 3, 256], F32, a)
        self.KTOK = T("KTOK", [64, 4, 384], BF16, a)
        self.VAUG = T("VAUG", [64, 4, 6, 65], BF16, a)
        self.HB = T("HB", [64, 4, 384], F32, a)
        self.mg = []
        for d in range(2):
            g = {}
            for n in ["gi", "lf", "pre", "bb", "gg", "ee", "fl"]:
                g[n] = T("mg_%s%d" % (n, d), [6, 256], F32, a)
            for n in ["mx", "mch", "mprev", "MM", "dec"]:
                g[n] = T("mg_%s%d" % (n, d), [6, 4], F32, a)
            g["X2"] = T("mg_X2%d" % d, [6, 4, 3], F32, a)
            g["etok"] = T("mg_etok%d" % d, [64, 4, 6], F32, a)
            g["fltok"] = T("mg_fltok%d" % d, [64, 4, 6], F32, a)
            g["decb"] = T("mg_decb%d" % d, [128, 4, 3], F32, a)
            self.mg.append(g)
        self.STSB = [T("STSB%d" % i, [64, 6, 64], BF16, a) for i in range(2)]
        self.VP = [T("VP%d" % i, [64, 6, 65], BF16, a) for i in range(2)]
        self.CDEC = T("CDEC", [128, 3, 65], F32, a)
        self.CDBF = T("CDBF", [128, 3, 65], BF16, a)
        self.DN = T("DN", [64, 6], F32, a)
        self.RDN = T("RDN", [64, 6], F32, a)
        self.HD = T("HD", [64, 6, 64], F32, a)
        self.SQ = T("SQ", [64, 4, 384], F32, a)
        self.SSQ = T("SSQ", [64, 24], F32, a)
        self.RSTD = T("RSTD", [64, 24], F32, a)
        self.ZT = [T("ZT%d" % i, [128, 256], F32, a) for i in range(2)]
        a = mk()
        self.XL = T("XL", [128, 2, 392], F32, a)
        self.XC = T("XC", [128, 2, 256], F32, a)
        self.LZT = T("LZT", [128, 2, 256], F32, a)
        self.ltd = [{n: T("lt%d_%s" % (d_, n), [128, 2, 256], F32, a) for n in ["rg", "ig", "aa", "a2", "bt"]} for d_ in range(2)]
        self.HL = [T("HL%d" % d, [128, 2, 256], F32, a) for d in range(2)]
        a = mk()
        self.URS = T("URS", [128, 11, 384], F32, a)
        a = mk()
        self.KH = T("KH", [128, 3, 256], F32, a)
        R1 = a.take(7 * 3072)
        a1 = Arena(R1, 7 * 3072)
        self.rt = {n: T("rt_" + n, [128, 3, 256], F32, a1) for n in ["sg", "aa", "kt", "bb", "cs", "d", "E"]}
        a2 = Arena(R1, 7 * 3072)
        self.A1BD = T("A1BD", [128, 3, 2, 128], BF16, a2)
        self.A2BD = T("A2BD", [128, 3, 2, 128], BF16, a2)
        self.NN = [T("NN%d" % i, [128, 3, 128], BF16, a2) for i in range(2)]
        self.NTT = [T("NTT%d" % i, [128, 3, 128], BF16, a2) for i in range(2)]
        self.U = T("U", [128, 3, 64], F32, a2)
        self.UBF = T("UBF", [128, 3, 64], BF16, a2)
        self.HBF = T("HBF", [128, 3, 64], BF16, a2)
        self.HTMP = T("HTMP", [128, 3, 64], F32, a2)
        self.YBDs = [T("YBD%d" % i, [128, 3, 128], F32, a2) for i in range(2)]
        self.KTTOK = T("KTTOK", [128, 4, 3, 128], BF16, a2)
        self.BTTOK = T("BTTOK", [128, 4, 3, 128], BF16, a2)
        self.rSSQ = T("rSSQ", [128, 12], F32, a2)
        self.rRSTD = T("rRSTD", [128, 12], F32, a2)
        self.KR = T("KR", [128, 3, 4, 2, 64], BF16, a)
        self.BTT = T("BTT", [128, 3, 256], BF16, a)
        self.KTT = T("KTT", [128, 3, 256], BF16, a)
        bdr = a.take(4 * 3072)
        a3 = Arena(bdr, 4 * 3072)
        self.BD = {n: T("BD_" + n, [128, 3, 4, 128], BF16, a3) for n in ["kt", "b", "kh", "r"]}
        a3 = Arena(bdr, 4 * 3072)
        self.ft = {n: T("ft_" + n, [128, 3, 256], F32, a3) for n in ["rk", "bon", "t1"]}
        self.YSQ = T("YSQ", [128, 4, 3, 64], F32, a3)
        self.BDV = T("BDV", [128, 3, 4, 128], BF16, a)
        self.VSTK = T("VSTK", [128, 4, 3, 64], BF16, a)
        self.TWL = T("TWL", [128, 256], F32, a)
        self.GL = T("GL", [128, 3, 4], F32, a)
        self.PA = nc.alloc_psum_tensor("PA", [128, 1024], F32)
        self.PB = nc.alloc_psum_tensor("PB", [128, 1024], F32)
        self.PQ = [nc.alloc_psum_tensor("PQ%d" % i, [128, 512], F32) for i in range(3)]
        self.PTB = nc.alloc_psum_tensor("PTB", [128, 1024], BF16)
        self.pq_i = 0

    def pq(self):
        t = self.PQ[self.pq_i % 3]
        self.pq_i += 1
        return t

    def V(self, t, p0, npart, off, dims):
        pstep = _prod(list(t.shape)[1:])
        return bass.AP(t, p0 * pstep + off, [[pstep, npart]] + [list(d) for d in dims])

    def tt(self, e, out, in0, in1, op):
        return self.S.emit(e, [in0, in1], [out], lambda g: g.tensor_tensor(out=out, in0=in0, in1=in1, op=op))

    def ts(self, e, out, in0, s1, s2, op0, op1=None):
        rd = [in0] + [s for s in (s1, s2) if not isinstance(s, (int, float)) and s is not None]
        if op1 is None:
            return self.S.emit(e, rd, [out], lambda g: g.tensor_scalar(out=out, in0=in0, scalar1=s1, scalar2=None, op0=op0))
        return self.S.emit(e, rd, [out], lambda g: g.tensor_scalar(out=out, in0=in0, scalar1=s1, scalar2=s2, op0=op0, op1=op1))

    def stt(self, out, in0, sc, in1, op0, op1):
        rd = [in0, in1] + ([] if isinstance(sc, (int, float)) else [sc])
        return self.S.emit('dve', rd, [out], lambda g: g.scalar_tensor_tensor(out=out, in0=in0, scalar=sc, in1=in1, op0=op0, op1=op1))

    def act(self, out, in_, func, bias=None, scale=None, accum=None):
        rd = [in_]
        kw = {}
        if bias is not None:
            kw['bias'] = bias
            if not isinstance(bias, (int, float)):
                rd.append(bias)
        if scale is not None:
            kw['scale'] = scale
            if not isinstance(scale, (int, float)):
                rd.append(scale)
        wr = [out]
        if accum is not None:
            kw['accum_out'] = accum
            wr.append(accum)
        return self.S.emit('act', rd, wr, lambda g: g.activation(out=out, in_=in_, func=func, **kw))

    def cp(self, e, out, in_):
        if e == 'act':
            return self.act(out, in_, AF.Copy)
        return self.S.emit(e, [in_], [out], lambda g: g.tensor_copy(out=out, in_=in_))

    def _pe_rowtile_guard(self, lhsT, out):
        S = self.S
        st = S.region(lhsT)
        k = st[2] - st[1]
        kr = 32 if k <= 32 else (64 if k <= 64 else 128)
        rows = (st[1], st[1] + kr)
        oreg = S.region(out)
        last = getattr(self, '_last_pe', None)
        if last is not None:
            lrows, loreg, lins, linc = last
            disjoint = rows[1] <= lrows[0] or lrows[1] <= rows[0]
            samebank = (loreg[0] == oreg[0]) and loreg[3] < oreg[4] and oreg[3] < loreg[4]
            if disjoint and samebank:
                if not linc:
                    S.cnt['pe'] += 1
                    lins.then_inc(S.sem['pe'], 1)
                S.wait('pe', S.sem['pe'], S.cnt['pe'])
        return rows, oreg

    def mm(self, out, lhsT, rhs, start=True, stop=True, inc=None):
        if inc is None:
            inc = stop
        rows, oreg = self._pe_rowtile_guard(lhsT, out)
        ins = self.S.emit('pe', [lhsT, rhs], [out],
                          lambda g: g.matmul(out, lhsT=lhsT, rhs=rhs, start=start, stop=stop), inc=inc)
        self._last_pe = (rows, oreg, ins, inc)
        return ins

    def tr(self, out, in_, inc=True, bf=False):
        n = in_.shape[0]
        ident = self.IDB[0:n, 0:n] if bf else self.CST[0:n, CC['IDENT']:CC['IDENT'] + n]
        rows, oreg = self._pe_rowtile_guard(in_, out)
        ins = self.S.emit('pe', [in_, ident], [out],
                          lambda g: g.transpose(out=out, in_=in_, identity=ident), inc=inc)
        self._last_pe = (rows, oreg, ins, inc)
        return ins

    def memset(self, e, ap, v):
        return self.S.emit(e, [], [ap], lambda g: g.memset(ap, v))

    def scan(self, out, d0, d1, init, op0, op1):
        rd = [d0, d1] + ([] if isinstance(init, (int, float)) else [init])
        return self.S.emit('dve', rd, [out], lambda g: g.tensor_tensor_scan(out=out, data0=d0, data1=d1, initial=init, op0=op0, op1=op1))

    def recip(self, out, in_):
        return self.S.emit('dve', [in_], [out], lambda g: g.reciprocal(out=out, in_=in_))

    def reduce(self, out, in_, op, axis=AX.X):
        return self.S.emit('dve', [in_], [out], lambda g: g.tensor_reduce(out=out, in_=in_, axis=axis, op=op))

    def dump(self, name, ap):
        if not self.debug or name in self.dbg_map:
            return
        if getattr(self, 'dbg_filter', None) is not None and not any(name.startswith(p) for p in self.dbg_filter):
            return
        shp = list(ap.shape)
        npart, nfree = shp[0], _prod(shp[1:])
        stage = self.XN[1]
        assert nfree <= 1024
        dst = self.V(stage, 0, npart, 0, [[_prod(shp[i + 1:]), shp[i]] for i in range(1, len(shp))])
        self.cp('dve', dst, ap)
        self.S.dma('sp', self.dbg[0:npart, self.dbg_off:self.dbg_off + nfree], stage[0:npart, 0:nfree], allow_slow_non_contiguous=True)
        self.dbg_map[name] = (self.dbg_off, npart, shp[1:])
        self.dbg_off += nfree

    def ppc(self, l, key, j=0, rows=128):
        c = l * PL + PC[key] + j
        return self.PP[0:rows, c:c + 1]

    def dvc(self, l, key, j=0, rows=128):
        c = l * DL + DC[key] + j
        return self.DV[0:rows, c:c + 1]

    def setup(self):
        S = self.S
        S.dma('sp', self.CST[:, :], self.cst[:, :])
        S.dma('sp', self.PP[:, :], self.pp[:, :])
        S.dma('sp', self.LORA[:, :, :, :], self.lora[:, :, :, :])
        S.dma('sp', self.LRUW[:, :, :, :], self.lruw[:, :, :, :])
        self.memset('dve', self.DV[:, :], 0.0)
        self.cp('dve', self.IDB[:, :], self.CST[:, CC['IDENT']:CC['IDENT'] + 128])
        self.cp('dve', self.ISTKB[:, :], self.CST[:, CC['ISTK']:CC['ISTK'] + 64])
        for l in range(2):
            self.ts('dve', self.DV[:, l * DL + DC['OMKA']: l * DL + DC['OMKA'] + 3],
                    self.PP[:, l * PL + PC['R_KA']: l * PL + PC['R_KA'] + 3], -1.0, 1.0, ALU.mult, ALU.add)
            self.ts('dve', self.DV[:, l * DL + DC['OMMU']: l * DL + DC['OMMU'] + 11],
                    self.PP[:, l * PL + PC['R_MU']: l * PL + PC['R_MU'] + 11], -1.0, 1.0, ALU.mult, ALU.add)
            lam = self.PP[:, l * PL + PC['L_LAM']: l * PL + PC['L_LAM'] + 4]
            t0 = self.TMPS[:, 0:4]
            self.act(t0, lam, AF.Exp, scale=-1.0)
            self.act(t0, t0, AF.Ln, bias=1.0)
            self.ts('dve', self.DV[:, l * DL + DC['CLAM']: l * DL + DC['CLAM'] + 4], t0, -8.0, None, ALU.mult)
            self.ts('dve', self.DV[:, l * DL + DC['C2LAM']: l * DL + DC['C2LAM'] + 4], t0, -16.0, None, ALU.mult)
            self.ts('dve', self.DV[0:6, l * DL + DC['NBF']: l * DL + DC['NBF'] + 2],
                    self.PP[0:6, l * PL + PC['M_BF']: l * PL + PC['M_BF'] + 2], -1.0, None, ALU.mult)
            for hp in range(3):
                self.ts('dve', self.RKD[:, l, hp, :], self.CST[:, CC['BONES']:CC['BONES'] + 128],
                        self.ppc(l, 'R_RK', hp), None, ALU.mult)
        self.memset('dve', self.XL[:, :, 0:2], 0.0)

    def load_layer(self, l):
        S = self.S
        wsrc = self.w_in[l].rearrange("(kc p) c -> p kc c", p=128)
        wo = self.w_out[l].rearrange("(kc p) c -> p kc c", p=128)
        pieces = [(self.W_IN, wsrc, c0, min(256, IN_COLS - c0)) for c0 in range(0, IN_COLS, 256)]
        pieces += [(self.W_OUT, wo, c0, 256) for c0 in range(0, D, 256)]
        def emit_pieces(lo, hi):
            for i in range(lo, min(hi, len(pieces))):
                dst, src, c0, n = pieces[i]
                stg = self.WSTG[i % 3]
                S.dma('sp', stg[:, :, 0:n], src[:, :, c0:c0 + n])
                self.cp('pool', dst[:, :, c0:c0 + n], stg[:, :, 0:n])
        if self.stop == 'load_w':
            raise StopIteration
        S.dma('sp', self.SCT[:, :, :], self.cc[:, :, :])
        S.dma('sp', self.BMT[:, :, :], self.bmodT[:, :, :])
        self.act(self.SCT[:, :, :], self.SCT[:, :, :], AF.Silu)
        for m in range(2):
            for kc in range(8):
                src = self.V(self.SCT, 0, 128, kc * 2 + m, [[0, 128]])
                self.cp('dve', self.SCB[:, m, kc, :], src)
        wm = self.w_mod[l].rearrange("(kc p) c -> p kc c", p=128)
        for blk in range(6):
            self.WM = self.WMs[blk % 2]
            S.dma('sp', self.WM[:, :, :], wm[:, :, blk * 512:(blk + 1) * 512])
            emit_pieces(blk * 4, blk * 4 + 4)
            if blk < 4:
                ps = self.pq()
                for kc in range(8):
                    self.mm(ps[0:2, :], self.SCT[:, kc, :], self.WM[:, kc, :], start=(kc == 0), stop=(kc == 7))
                self.cp('act', self.MODROW[0:2, :], ps[0:2, :])
                ps2 = self.pq()
                for j in range(4):
                    self.tr(ps2[:, j * 2:j * 2 + 2], self.MODROW[0:2, j * 128:(j + 1) * 128])
                o = self.MODT[:, blk * 4:(blk + 1) * 4, :]
                bsrc = self.V(self.BMT, 0, 128, l * 16 + blk * 4, [[1, 4], [0, 2]])
                self.tt('dve', o, self.V(ps2, 0, 128, 0, [[2, 4], [1, 2]]), bsrc, ALU.add)
            else:
                half = blk - 4
                for m in range(2):
                    ps = self.pq()
                    for kc in range(8):
                        self.mm(ps[:, :], self.SCB[:, m, kc, :], self.WM[:, kc, :], start=(kc == 0), stop=(kc == 7))
                    self.cp('act', self.GATEB[:, m, half * 512:(half + 1) * 512], ps[:, :])
        if self.stop == 'load_m':
            raise StopIteration
        for m in range(2):
            self.cp('dve', self.SH[:, m, :], self.V(self.MODT, 0, 128, m, [[2, 8]]))
            t0 = self.TMPS[:, 8:16]
            self.ts('dve', t0, self.V(self.MODT, 0, 128, 16 + m, [[2, 8]]), 1.0, None, ALU.add)
            self.tt('dve', self.GS[:, m, :], t0, self.PP[:, l * PL + PC['G_PRE']: l * PL + PC['G_PRE'] + 8], ALU.mult)
        if self.stop == 'load_g':
            raise StopIteration
        S.dma('sp', self.BG[:, :], bass.AP(self.bmodg.tensor, l * D, [[0, 128], [1, D]]))
        S.dma('sp', self.GP[:, :], bass.AP(self.gpost.tensor, l * D, [[0, 128], [1, D]]))
        if self.stop == 'load_b':
            raise StopIteration
        for m in range(2):
            self.tt('dve', self.GATEB[:, m, :], self.GATEB[:, m, :], self.BG[:, :], ALU.add)
            self.tt('dve', self.GATEB[:, m, :], self.GATEB[:, m, :], self.GP[:, :], ALU.mult)

    def proj_fm(self, c0, ncols, t_off, ntok, evac):
        ps = self.pq()
        for kc in range(8):
            self.mm(ps[0:ncols, 0:ntok], self.W_IN[:, kc, c0:c0 + ncols], self.HT[:, kc, t_off:t_off + ntok],
                    start=(kc == 0), stop=(kc == 7))
        evac(ps[0:ncols, 0:ntok])

    def proj_tm(self, c0, ncols, t_off, evac):
        ps = self.pq()
        for kc in range(8):
            self.mm(ps[0:64, 0:ncols], self.HT[:, kc, t_off:t_off + 64], self.W_IN[:, kc, c0:c0 + ncols],
                    start=(kc == 0), stop=(kc == 7))
        evac(ps[0:64, 0:ncols])

    def visit(self, l, mod, xsrc, xdst, row0, T, t0, dirs, grid, seq_idx, first, last, mode='full'):
        S = self.S
        w0 = max(0, t0 - 64)
        w1 = min(T, t0 + TB + 64)
        W = w1 - w0
        co = t0 - w0
        do_f = 0 in dirs
        prompt = (mod == 0)
        if mode != 'load':
            ntile = (W + 127) // 128
            for i in range(ntile):
                n = min(128, W - i * 128)
                xw, xn = self.XW[i % 2], self.XN[i % 2]
                r = row0 + w0 + i * 128
                S.dma('sp', xw[0:n, :], xsrc[r:r + n, :])
                ssq = self.TMPS[0:n, 16 + i:17 + i]
                self.act(xn[0:n, :], xw[0:n, :], AF.Square, accum=ssq)
                if self.stop == 'n1':
                    raise StopIteration
                rs = self.TMPS[0:n, 20 + i:21 + i]
                self.ts('dve', rs, ssq, 1.0 / D, EPS, ALU.mult, ALU.add)
                self.act(rs, rs, AF.Sqrt)
                self.recip(rs, rs)
                if self.stop == 'n2':
                    raise StopIteration
                self.act(xn[0:n, :], xw[0:n, :], AF.Copy, scale=rs)
                if self.stop == 'n3':
                    raise StopIteration
                for half in range(2):
                    ps = self.pq()
                    for j in range(4):
                        kc = half * 4 + j
                        self.tr(ps[:, j * 128:j * 128 + n], xn[0:n, kc * 128:(kc + 1) * 128])
                    if self.stop == 'n4':
                        raise StopIteration
                    for j in range(4):
                        kc = half * 4 + j
                        o = self.HT[:, kc, i * 128:i * 128 + n]
                        if half == 0:
                            self.ts('dve', o, ps[:, j * 128:j * 128 + n], self.GS[:, mod, kc:kc + 1], self.SH[:, mod, kc:kc + 1], ALU.mult, ALU.add)
                        else:
                            self.act(o, ps[:, j * 128:j * 128 + n], AF.Identity, bias=self.SH[:, mod, kc:kc + 1], scale=self.GS[:, mod, kc:kc + 1])
        if self.stop in ('norm', 'n5a', 'n5d'):
            raise StopIteration
        self.stage_mlstm(l, t0, co, dirs, prompt, seq_idx, first, last, mode)
        if self.stop == 'mlstm':
            raise StopIteration
        self.stage_lru(l, t0, co, W, w0, w1, T, dirs, prompt, seq_idx, first, last, mode)
        if self.stop == 'lru':
            raise StopIteration
        self.stage_rwkv(l, t0, co, W, w0, w1, T, dirs, grid, prompt, seq_idx, first, last, mode)
        for kc in range(8):
            self.dump('mix%d' % kc, self.MIXT[:, kc, :])
        if self.stop == 'rwkv':
            raise StopIteration
        if do_f:
            self.stage_out(l, mod, xsrc, xdst, row0 + t0)
        if self.stop == 'out':
            raise StopIteration

    def stage_mlstm(self, l, t0, co, dirs, prompt, seq_idx, first, last, mode='full'):
        S = self.S
        do_f = 0 in dirs
        if mode != 'load':
            for d in (sorted(set(dirs) | {0}) if mode == 'store' else dirs):
                g = self.mg[d]
                self.proj_fm(1920 + d * 6, 6, co, 256, lambda ps, g=g, d=d: self.act(g["gi"][:, :], ps, AF.Identity, bias=self.ppc(l, 'M_BI', d, 6)))
                def ev_f(ps, g=g, d=d):
                    self.act(g["lf"][:, :], ps, AF.Exp, bias=self.dvc(l, 'NBF', d, 6), scale=-1.0)
                    self.act(g["lf"][:, :], g["lf"][:, :], AF.Ln, bias=1.0)
                    self.ts('dve', g["lf"][:, :], g["lf"][:, :], -1.0, None, ALU.mult)
                self.proj_fm(1932 + d * 6, 6, co, 256, ev_f)
        bi = t0 // 256
        ml_items = [(self.QT[:, :, :], 'QT'), (self.KT[:, :, :], 'KT'), (self.KTOK[:, :, :], 'KTOK'), (self.VAUG[:, :, :, :], 'VAUG'),
                    (self.GOZ[:, :, :], 'GOZ'), (self.mg[0]["gi"][:, :], 'GI'), (self.mg[0]["lf"][:, :], 'LF')]
        if mode == 'load':
            for ap_, k_ in ml_items:
                S.dma('sp', ap_, self.sc[k_][bi])
        for d in dirs:
            if first[d]:
                if prompt:
                    self.memset('dve', self.mC[d][:, :, :], 0.0)
                    self.memset('dve', self.mM[d][:, :], 0.0)
                else:
                    S.dma('sp', self.mC[d][:, :, :], self.st_mC[l, d])
                    S.dma('sp', self.mM[d][:, :], self.st_mm[:, l, d:d + 1], allow_slow_non_contiguous=True)
        if do_f and not (1 in dirs):
            S.dma('sp', self.HB[:, :, :], self.sHB[t0:t0 + 256, :].rearrange("(c s) f -> s c f", s=64))
        for d in dirs:
            g = self.mg[d]
            v3 = lambda t: self.V(t, 0, 6, 0, [[64, 4], [1, 64]])
            self.scan(g["pre"][:, :], self.CST[0:6, CC['RMASK']:CC['RMASK'] + 256], g["lf"][:, :], 0.0, ALU.mult, ALU.add)
            bL = self.V(g["pre"], 0, 6, 63, [[64, 4]])
            bLb = self.V(g["pre"], 0, 6, 63, [[64, 4], [0, 64]])
            if d == 0:
                bsrc = g["pre"]
            else:
                self.tt('dve', v3(g["bb"]), bLb, v3(g["pre"]), ALU.subtract)
                self.tt('dve', g["bb"][:, :], g["bb"][:, :], g["lf"][:, :], ALU.add)
                bsrc = g["bb"]
            self.tt('dve', g["gg"][:, :], g["gi"][:, :], bsrc[:, :], ALU.subtract)
            self.reduce(g["mx"][:, :], v3(g["gg"]), ALU.max)
            if d == 0:
                mo, mxv, blv = g["mch"][:, :], g["mx"][:, :], bL
            else:
                mo = self.V(g["mch"], 0, 6, 3, [[-1, 4]])
                mxv = self.V(g["mx"], 0, 6, 3, [[-1, 4]])
                blv = self.V(g["pre"], 0, 6, 63 + 3 * 64, [[-64, 4]])
            self.scan(mo, mxv, blv, self.mM[d][:, 0:1], ALU.max, ALU.add)
            if d == 0:
                self.cp('dve', g["mprev"][:, 1:4], g["mch"][:, 0:3])
                self.cp('dve', g["mprev"][:, 0:1], self.mM[d][:, 0:1])
                mfin = g["mch"][:, 3:4]
            else:
                self.cp('dve', g["mprev"][:, 0:3], g["mch"][:, 1:4])
                self.cp('dve', g["mprev"][:, 3:4], self.mM[d][:, 0:1])
                mfin = g["mch"][:, 0:1]
            self.tt('dve', g["MM"][:, :], g["mprev"][:, :], g["mx"][:, :], ALU.max)
            self.tt('dve', g["dec"][:, :], g["mprev"][:, :], g["MM"][:, :], ALU.subtract)
            self.act(g["dec"][:, :], g["dec"][:, :], AF.Exp)
            self.cp('dve', self.mM[d][:, 0:1], mfin)
            MMb = self.V(g["MM"], 0, 6, 0, [[1, 4], [0, 64]])
            self.tt('dve', v3(g["ee"]), v3(g["gg"]), MMb, ALU.subtract)
            self.act(g["ee"][:, :], g["ee"][:, :], AF.Exp)
            self.tt('dve', v3(g["fl"]), v3(bsrc), MMb, ALU.add)
            self.act(g["fl"][:, :], g["fl"][:, :], AF.Exp, scale=-1.0)
        if mode != 'load':
            for hp in range(3):
                self.proj_fm(hp * 128, 128, co, 256, lambda ps, hp=hp: self.cp('act', self.QT[:, hp, :], ps))
                self.proj_fm(384 + hp * 128, 128, co, 256, lambda ps, hp=hp: self.act(self.KT[:, hp, :], ps, AF.Copy, scale=0.125))
            if do_f or mode == 'store':
                for hp in range(3):
                    def ev_o(ps, hp=hp):
                        self.act(self.GOZ[:, hp, :], ps, AF.Sigmoid)
                    self.proj_fm(1152 + hp * 128, 128, co, 256, ev_o)
                    def ev_z2(ps, hp=hp):
                        tz = self.ZT[hp % 2]
                        self.act(tz[:, :], ps, AF.Silu)
                        self.tt('dve', self.GOZ[:, hp, :], self.GOZ[:, hp, :], tz[:, :], ALU.mult)
                    self.proj_fm(1536 + hp * 128, 128, co, 256, ev_z2)
            for c in range(4):
                self.proj_tm(384, 384, co + c * 64, lambda ps, c=c: self.act(self.KTOK[:, c, :], ps, AF.Copy, scale=0.125))
                def ev_v(ps, c=c):
                    self.cp('dve', self.VAUG[:, c, :, 0:64], self.V(ps.tensor, 0, 64, 0, [[64, 6], [1, 64]]))
                self.proj_tm(768, 384, co + c * 64, ev_v)
            self.memset('dve', self.VAUG[:, :, :, 64:65], 1.0)
        for d in dirs:
            g = self.mg[d]
            ps = self.pq()
            for c in range(4):
                self.tr(ps[0:64, c * 6:c * 6 + 6], g["ee"][:, c * 64:(c + 1) * 64])
                self.tr(ps[0:64, 24 + c * 6:24 + c * 6 + 6], g["fl"][:, c * 64:(c + 1) * 64])
            self.cp('dve', g["etok"][:, :, :], self.V(ps, 0, 64, 0, [[6, 4], [1, 6]]))
            self.cp('dve', g["fltok"][:, :, :], self.V(ps, 0, 64, 24, [[6, 4], [1, 6]]))
            self.tt('dve', g["X2"][:, :, :], self.V(g["dec"], 0, 6, 0, [[1, 4], [0, 3]]),
                    self.V(self.CST, 0, 6, CC['PSEL'], [[0, 4], [1, 3]]), ALU.mult)
            ps2 = self.pq()
            self.mm(ps2[:, 0:12], self.CST[0:6, CC['LSEL']:CC['LSEL'] + 128], self.V(g["X2"], 0, 6, 0, [[1, 12]]))
            self.cp('dve', g["decb"][:, :, :], self.V(ps2, 0, 128, 0, [[3, 4], [1, 3]]))
        if mode == 'store':
            for ap_, k_ in ml_items:
                S.dma('sp', self.sc[k_][bi], ap_)
        for d in sorted(dirs, reverse=True):
            g = self.mg[d]
            mask = self.CST[0:64, CC['MUI']:CC['MUI'] + 64] if d == 0 else self.CST[0:64, CC['MLI']:CC['MLI'] + 64]
            maskb = self.V(self.CST, 0, 64, CC['MUI'] if d == 0 else CC['MLI'], [[0, 6], [1, 64]])
            for j in range(4):
                c = j if d == 0 else 3 - j
                cs = slice(c * 64, (c + 1) * 64)
                stsb, vp = self.STSB[j % 2], self.VP[j % 2]
                ps = self.pq()
                for h in (0, 2, 4, 1, 3, 5):
                    hp, pb = h // 2, 64 * (h % 2)
                    self.mm(ps[0:64, h * 64:(h + 1) * 64], self.KT[pb:pb + 64, hp, cs], self.QT[pb:pb + 64, hp, cs], inc=(h == 5))
                self.tt('dve', stsb[:, :, :], self.V(ps, 0, 64, 0, [[64, 6], [1, 64]]), maskb, ALU.mult)
                self.tt('dve', vp[:, :, :], self.VAUG[:, c, :, :], self.V(g["etok"], 0, 64, c * 6, [[1, 6], [0, 65]]), ALU.mult)
                self.tt('dve', self.CDBF[:, :, :], self.mC[d][:, :, :], self.V(g["decb"], 0, 128, c * 3, [[1, 3], [0, 65]]), ALU.mult)
                self.tt('dve', self.CDEC[:, :, :], self.mC[d][:, :, :], self.V(g["decb"], 0, 128, c * 3, [[1, 3], [0, 65]]), ALU.mult)
                ph = self.pq()
                for h in (0, 2, 4, 1, 3, 5):
                    hp, pb = h // 2, 64 * (h % 2)
                    o = ph[0:64, h * 65:(h + 1) * 65]
                    self.mm(o, stsb[:, h, :], vp[:, h, :], start=True, stop=False)
                    self.mm(o, self.QT[pb:pb + 64, hp, cs], self.CDBF[pb:pb + 64, hp, :], start=False, stop=True, inc=(h == 5))
                pc = self.pq()
                for h in range(6):
                    hp, pb = h // 2, 64 * (h % 2)
                    self.mm(pc[pb:pb + 64, hp * 65:(hp + 1) * 65], self.KTOK[:, c, h * 64:(h + 1) * 64], vp[:, h, :], inc=(h == 5))
                self.tt('dve', self.mC[d][:, :, :], self.CDEC[:, :, :], self.V(pc, 0, 128, 0, [[65, 3], [1, 65]]), ALU.add)
                self.act(self.DN[:, :], self.V(ph, 0, 64, 64, [[65, 6]]), AF.Abs)
                self.tt('dve', self.DN[:, :], self.DN[:, :], g["fltok"][:, c, :], ALU.max)
                self.recip(self.RDN[:, :], self.DN[:, :])
                hsrc = self.V(ph, 0, 64, 0, [[65, 6], [1, 64]])
                rb = self.V(self.RDN, 0, 64, 0, [[1, 6], [0, 64]])
                hbv = self.V(self.HB, 0, 64, c * 384, [[64, 6], [1, 64]])
                if d == 1:
                    self.tt('dve', hbv, hsrc, rb, ALU.mult)
                else:
                    self.tt('dve', self.HD[:, :, :], hsrc, rb, ALU.mult)
                    self.tt('dve', hbv, hbv, self.HD[:, :, :], ALU.add)
            if last[d] and prompt:
                S.dma('sp', self.o_mC[seq_idx, l, d], self.mC[d][:, :, :])
                S.dma('sp', self.o_mm[seq_idx, l, d], self.mM[d][:, 0:1])
        if not do_f:
            S.dma('sp', self.sHB[t0:t0 + 256, :].rearrange("(c s) f -> s c f", s=64), self.HB[:, :, :])
            return
        self.tt('dve', self.SQ[:, :, :], self.HB[:, :, :], self.HB[:, :, :], ALU.mult)
        self.reduce(self.SSQ[:, :], self.V(self.SQ, 0, 64, 0, [[64, 24], [1, 64]]), ALU.add)
        self.ts('dve', self.SSQ[:, :], self.SSQ[:, :], 1.0 / 64, EPS, ALU.mult, ALU.add)
        self.act(self.SSQ[:, :], self.SSQ[:, :], AF.Sqrt)
        self.recip(self.RSTD[:, :], self.SSQ[:, :])
        self.tt('dve', self.V(self.SQ, 0, 64, 0, [[64, 24], [1, 64]]), self.V(self.HB, 0, 64, 0, [[64, 24], [1, 64]]),
                self.V(self.RSTD, 0, 64, 0, [[1, 24], [0, 64]]), ALU.mult)
        for hp in range(3):
            ps = self.pq()
            for c in range(4):
                self.tr(ps[:, c * 64:(c + 1) * 64], self.SQ[:, c, hp * 128:(hp + 1) * 128], inc=(c == 3))
            self.stt(self.MIXT[:, hp, :], ps[:, 0:256], self.ppc(l, 'M_NORM', hp), self.GOZ[:, hp, :], ALU.mult, ALU.mult)

    def stage_lru(self, l, t0, co, W, w0, w1, T, dirs, prompt, seq_idx, first, last, mode='full'):
        S = self.S
        do_f = 0 in dirs
        if mode != 'load':
            for pr in range(2):
                self.proj_fm(3736 + pr * 128, 128, 0, W, lambda ps, pr=pr: self.cp('act', self.XL[:, pr, 2:2 + W], ps))
                if do_f or mode == 'store':
                    self.proj_fm(3992 + pr * 128, 128, co, 256, lambda ps, pr=pr: self.act(self.LZT[:, pr, :], ps, AF.Silu))
            if w1 == T:
                self.memset('dve', self.XL[:, :, 2 + W:2 + W + 1], 0.0)
            if w0 == 0:
                self.memset('dve', self.XL[:, :, 0:2], 0.0)
            for pr in range(2):
                self.ts('dve', self.XC[:, pr, :], self.XL[:, pr, co:co + 256], self.ppc(l, 'L_CONV', 0 * 2 + pr), self.ppc(l, 'L_CONVB', pr), ALU.mult, ALU.add)
                for j in range(1, 4):
                    self.stt(self.XC[:, pr, :], self.XL[:, pr, co + j:co + j + 256], self.ppc(l, 'L_CONV', j * 2 + pr), self.XC[:, pr, :], ALU.mult, ALU.add)
        bi = t0 // 256
        lr_items = [(self.XC[:, :, :], 'XC'), (self.LZT[:, :, :], 'LZT')]
        if mode == 'store':
            for ap_, k_ in lr_items:
                S.dma('sp', self.sc[k_][bi], ap_)
        if mode == 'load':
            for ap_, k_ in lr_items:
                S.dma('sp', ap_, self.sc[k_][bi])
        for d in dirs:
            if first[d]:
                if prompt:
                    self.memset('dve', self.lS[d][:, :], 0.0)
                else:
                    S.dma('sp', self.lS[d][:, :], self.st_l[:, l, d, :])
        if do_f and not (1 in dirs):
            S.dma('sp', self.HL[1][:, :, :], self.sLB[:, :, t0:t0 + 256])
        combos = [(d, pr) for d in sorted(dirs, reverse=True) for pr in range(2)]
        LT = lambda d, n, pr: self.ltd[d][n][:, pr, :]
        pss = {}
        for i, (d, pr) in enumerate(combos):
            ps = self.PA if i % 2 == 0 else self.PB
            off = (i // 2) * 512
            pss[(d, pr)] = (ps, off)
            self.mm(ps[:, off:off + 256], self.LRUW[:, l, (0 * 2 + d) * 2 + pr, :], self.XC[:, pr, :])
            self.mm(ps[:, off + 256:off + 512], self.LRUW[:, l, (1 * 2 + d) * 2 + pr, :], self.XC[:, pr, :])
        for (d, pr) in combos:
            ps, off = pss[(d, pr)]
            self.act(LT(d, "rg", pr), ps[:, off:off + 256], AF.Sigmoid, bias=self.ppc(l, 'L_BA', d * 2 + pr))
            self.act(LT(d, "ig", pr), ps[:, off + 256:off + 512], AF.Sigmoid, bias=self.ppc(l, 'L_BX', d * 2 + pr))
        for (d, pr) in combos:
            self.act(LT(d, "aa", pr), LT(d, "rg", pr), AF.Exp, scale=self.dvc(l, 'CLAM', d * 2 + pr))
            self.act(LT(d, "a2", pr), LT(d, "rg", pr), AF.Exp, scale=self.dvc(l, 'C2LAM', d * 2 + pr))
        for (d, pr) in combos:
            self.ts('dve', LT(d, "a2", pr), LT(d, "a2", pr), -1.0, 1.0, ALU.mult, ALU.add)
            self.tt('dve', LT(d, "bt", pr), LT(d, "ig", pr), self.XC[:, pr, :], ALU.mult)
        for (d, pr) in combos:
            self.act(LT(d, "a2", pr), LT(d, "a2", pr), AF.Sqrt)
        for (d, pr) in combos:
            self.tt('dve', LT(d, "bt", pr), LT(d, "bt", pr), LT(d, "a2", pr), ALU.mult)
        for (d, pr) in combos:
            if d == 0:
                self.scan(self.HL[0][:, pr, :], LT(0, "aa", pr), LT(0, "bt", pr), self.lS[0][:, pr:pr + 1], ALU.mult, ALU.add)
                self.cp('act', self.lS[0][:, pr:pr + 1], self.HL[0][:, pr, 255:256])
            else:
                rv = lambda t: self.V(t, 0, 128, pr * 256 + 255, [[-1, 256]])
                self.scan(rv(self.HL[1]), rv(self.ltd[1]["aa"]), rv(self.ltd[1]["bt"]), self.lS[1][:, pr:pr + 1], ALU.mult, ALU.add)
                self.cp('act', self.lS[1][:, pr:pr + 1], self.HL[1][:, pr, 0:1])
        for d in sorted(dirs, reverse=True):
            if last[d] and prompt:
                S.dma('sp', self.o_l[seq_idx, l, d], self.lS[d][:, :])
        if not do_f:
            S.dma('sp', self.sLB[:, :, t0:t0 + 256], self.HL[1][:, :, :])
            return
        self.tt('dve', self.HL[0][:, :, :], self.HL[0][:, :, :], self.HL[1][:, :, :], ALU.add)
        self.tt('dve', self.MIXT[:, 6:8, :], self.HL[0][:, :, :], self.LZT[:, :, :], ALU.mult)

    def stage_rwkv(self, l, t0, co, W, w0, w1, T, dirs, grid, prompt, seq_idx, first, last, mode='full'):
        S = self.S
        do_f = 0 in dirs
        if mode != 'load':
            for ch in range(11):
                self.proj_fm(1944 + ch * 128, 128, 0, W, lambda ps, ch=ch: self.cp('act' if ch % 2 else 'dve', self.URS[:, ch, 0:W], ps))
            if do_f or mode == 'store':
                for hp in range(3):
                    self.proj_fm(3352 + hp * 128, 128, co, 256, lambda ps, hp=hp: self.act(self.RZT[:, hp, :], ps, AF.Silu))
            U3 = lambda off, n: self.V(self.URS, 0, 128, off, [[384, 11], [1, n]])
            B3 = lambda off, n: self.V(self.BLK, 0, 128, off, [[256, 11], [1, n]])
            if not grid:
                self.cp('dve', B3(1, 255), U3(0, 255))
                self.memset('dve', B3(0, 1), 0.0)
                self.tt('dve', B3(0, 255), B3(0, 255), U3(1, 255), ALU.add)
                wsh = 0.5
            else:
                U4 = lambda off, r, n: self.V(self.URS, 0, 128, off, [[384, 11], [64, r], [1, n]])
                B4 = lambda off, r, n: self.V(self.BLK, 0, 128, off, [[256, 11], [64, r], [1, n]])
                self.cp('dve', B4(1, 4, 63), U4(co, 4, 63))
                self.memset('dve', B4(0, 4, 1), 0.0)
                self.tt('dve', B4(0, 4, 63), B4(0, 4, 63), U4(co + 1, 4, 63), ALU.add)
                if t0 > 0:
                    self.tt('dve', B3(0, 256), B3(0, 256), U3(co - 64, 256), ALU.add)
                else:
                    self.tt('dve', B3(64, 192), B3(64, 192), U3(0, 192), ALU.add)
                if t0 + TB < T:
                    self.tt('dve', B3(0, 256), B3(0, 256), U3(co + 64, 256), ALU.add)
                else:
                    self.tt('dve', B3(0, 192), B3(0, 192), U3(co + 64, 192), ALU.add)
                wsh = 0.25
            mu = self.V(self.PP, 0, 128, l * PL + PC['R_MU'], [[1, 11], [0, 256]])
            self.tt('dve', B3(0, 256), B3(0, 256), mu, ALU.mult)
            omm = self.V(self.DV, 0, 128, l * DL + DC['OMMU'], [[1, 11], [0, 256]])
            self.tt('dve', U3(co, 256), U3(co, 256), omm, ALU.mult)
            self.stt(B3(0, 256), B3(0, 256), wsh, U3(co, 256), ALU.mult, ALU.add)
            for nm, ch in (('blk_r', 0), ('blk_k', 3), ('blk_v', 6), ('blk_wl', 9), ('blk_al', 10)):
                self.dump(nm, self.BLK[:, ch, :])
            rt = self.rt
            kk = self.V(self.PP, 0, 128, l * PL + PC['R_KK'], [[1, 3], [0, 256]])
            kap = rt["d"]
            self.tt('dve', kap[:, :, :], self.BLK[:, 3:6, :], kk, ALU.mult)
            ksq = rt["E"]
            self.tt('dve', ksq[:, :, :], kap[:, :, :], kap[:, :, :], ALU.mult)
            for hp in range(3):
                self.mm(self.PA[:, hp * 256:(hp + 1) * 256], self.CST[:, CC['BONES']:CC['BONES'] + 128], ksq[:, hp, :])
            self.act(ksq[:, :, :], self.V(self.PA, 0, 128, 0, [[256, 3], [1, 256]]), AF.Sqrt)
            self.ts('dve', ksq[:, :, :], ksq[:, :, :], 1e-12, None, ALU.max)
            self.recip(ksq[:, :, :], ksq[:, :, :])
            self.tt('dve', self.KH[:, :, :], kap[:, :, :], ksq[:, :, :], ALU.mult)
            self.dump('kh', self.KH[:, 0, :])
            self.bd_fill(self.BDV, lambda par: self.V(self.BLK, par * 64, 64, 6 * 256, [[256, 3], [64, 4], [1, 64]]))
            for c in range(4):
                for hp in range(3):
                    self.mm(self.PB[:, (c * 3 + hp) * 64:(c * 3 + hp + 1) * 64], self.BDV[:, hp, c, :], self.ISTKB[:, :])
            self.cp('act', self.V(self.VSTK, 0, 128, 0, [[1, 768]]), self.PB[:, 0:768])
        bi = t0 // 256
        rw_items = [(self.BLK[:, :, :], 'BLK'), (self.RZT[:, :, :], 'RZT'), (self.KH[:, :, :], 'KH'), (self.VSTK[:, :, :, :], 'VSTK')]
        if mode == 'store':
            for ap_, k_ in rw_items:
                S.dma('sp', self.sc[k_][bi], ap_)
        if mode == 'load':
            for ap_, k_ in rw_items:
                S.dma('sp', ap_, self.sc[k_][bi])
        for d in dirs:
            if first[d]:
                if prompt:
                    self.memset('dve', self.rH[d][:, :, :], 0.0)
                else:
                    S.dma('sp', self.rH[d][:, :, :], self.st_rH[l, d])
        ybflat = self.V(self.YB, 0, 128, 0, [[1, 768]])
        if do_f and not (1 in dirs):
            S.dma('sp', ybflat, self.sYB[t0 // 256])
        for d in sorted(dirs, reverse=True):
            self.rwkv_dir(l, d, seq_idx, prompt, last)
        if not do_f:
            S.dma('sp', self.sYB[t0 // 256], ybflat)
            return
        ft = self.ft
        self.tt('dve', self.YSQ[:, :, :, :], self.YB[:, :, :, :], self.YB[:, :, :, :], ALU.mult)
        self.reduce(self.rSSQ[:, :], self.V(self.YSQ, 0, 128, 0, [[64, 12], [1, 64]]), ALU.add)
        self.ts('dve', self.rSSQ[:, :], self.rSSQ[:, :], 1.0 / 64, EPS, ALU.mult, ALU.add)
        self.act(self.rSSQ[:, :], self.rSSQ[:, :], AF.Sqrt)
        self.recip(self.rRSTD[:, :], self.rSSQ[:, :])
        self.tt('dve', ft["rk"][:, :, :], self.BLK[:, 0:3, :], self.BLK[:, 3:6, :], ALU.mult)
        for hp in range(3):
            self.mm(self.PB[:, hp * 256:(hp + 1) * 256], self.RKD[:, l, hp, :], ft["rk"][:, hp, :])
        self.tt('dve', ft["bon"][:, :, :], self.V(self.PB, 0, 128, 0, [[256, 3], [1, 256]]), self.BLK[:, 6:9, :], ALU.mult)
        for yb_ in self.YBDs:
            self.memset('pool', yb_[:, :, :], 0.0)
        for c in range(4):
            self.YBD = self.YBDs[c % 2]
            for par in range(2):
                self.tt('dve', self.V(self.YBD, par * 64, 64, par * 64, [[128, 3], [1, 64]]),
                        self.V(self.YB, par * 64, 64, c * 192, [[64, 3], [1, 64]]),
                        self.V(self.rRSTD, par * 64, 64, c * 3, [[1, 3], [0, 64]]), ALU.mult)
            for hp in range(3):
                self.mm(self.PA[:, hp * 256 + c * 64: hp * 256 + (c + 1) * 64], self.YBD[:, hp, :],
                        self.CST[:, CC['ISTK']:CC['ISTK'] + 64])
        for hp in range(3):
            self.stt(ft["t1"][:, hp, :], self.PA[:, hp * 256:(hp + 1) * 256], self.ppc(l, 'R_NORM', hp), ft["bon"][:, hp, :], ALU.mult, ALU.add)
            self.tt('dve', self.MIXT[:, 3 + hp, :], ft["t1"][:, hp, :], self.RZT[:, hp, :], ALU.mult)

    def bd_fill(self, bd, src_of_par, eng='pool'):
        self.memset(eng, bd[:, :, :, :], 0.0)
        for par in range(2):
            self.cp('act' if par == 0 else eng, self.V(bd, par * 64, 64, par * 64, [[512, 3], [128, 4], [1, 64]]), src_of_par(par))

    def rwkv_dir(self, l, d, seq_idx, prompt, last):
        S = self.S
        rt = self.rt
        pb_d = 64 * d
        f3 = lambda t: t[:, :, :]
        v4 = lambda t: self.V(t, 0, 128, 0, [[256, 3], [64, 4], [1, 64]])
        self.act(self.TWL[pb_d:pb_d + 64, :], self.BLK[pb_d:pb_d + 64, 9, :], AF.Tanh)
        for hp in range(3):
            self.mm(self.PA[:, hp * 256:(hp + 1) * 256], self.LORA[pb_d:pb_d + 64, l, 0, hp * 128:(hp + 1) * 128], self.TWL[pb_d:pb_d + 64, :])
        for hp in range(3):
            self.act(rt["sg"][:, hp, :], self.PA[:, hp * 256:(hp + 1) * 256], AF.Sigmoid, bias=self.ppc(l, 'R_W0', d * 3 + hp))
        for hp in range(3):
            self.mm(self.PB[:, hp * 256:(hp + 1) * 256], self.LORA[pb_d:pb_d + 64, l, 1, hp * 128:(hp + 1) * 128], self.BLK[pb_d:pb_d + 64, 10, :])
        for hp in range(3):
            self.act(rt["aa"][:, hp, :], self.PB[:, hp * 256:(hp + 1) * 256], AF.Sigmoid, bias=self.ppc(l, 'R_A0', d * 3 + hp))
        for hp in range(3):
            self.ts('dve', rt["kt"][:, hp, :], rt["aa"][:, hp, :], self.ppc(l, 'R_KA', hp), self.dvc(l, 'OMKA', hp), ALU.mult, ALU.add)
        self.tt('dve', f3(rt["kt"]), f3(rt["kt"]), self.BLK[:, 3:6, :], ALU.mult)
        self.tt('dve', f3(rt["bb"]), self.KH[:, :, :], f3(rt["aa"]), ALU.mult)
        flat = lambda t: self.V(t, 0, 128, 0, [[1, 768]])
        self.scan(flat(rt["cs"]), self.CST[:, CC['RMASK']:CC['RMASK'] + 768], flat(rt["sg"]), 0.0, ALU.mult, ALU.add)
        self.cp('dve', self.GL[:, :, :], self.V(rt["cs"], 0, 128, 63, [[256, 3], [64, 4]]))
        if d == 1:
            csLb = self.V(rt["cs"], 0, 128, 63, [[256, 3], [64, 4], [0, 64]])
            self.tt('dve', v4(rt["d"]), csLb, v4(rt["cs"]), ALU.subtract)
            self.tt('dve', f3(rt["cs"]), f3(rt["d"]), f3(rt["sg"]), ALU.add)
        self.act(f3(rt["E"]), f3(rt["cs"]), AF.Exp, scale=-DSC)
        self.tt('dve', self.V(self.KR, 0, 128, 64, [[512, 3], [128, 4], [1, 64]]),
                self.V(self.BLK, 0, 128, 0, [[256, 3], [64, 4], [1, 64]]), v4(rt["E"]), ALU.mult)
        self.tt('dve', f3(rt["d"]), f3(rt["cs"]), f3(rt["sg"]), ALU.subtract)
        self.act(f3(rt["E"]), f3(rt["d"]), AF.Exp, scale=-DSC)
        self.tt('dve', self.V(self.KR, 0, 128, 0, [[512, 3], [128, 4], [1, 64]]), v4(self.KH), v4(rt["E"]), ALU.mult)
        self.act(f3(rt["E"]), f3(rt["cs"]), AF.Exp, scale=DSC)
        self.tt('dve', self.BTT[:, :, :], f3(rt["bb"]), f3(rt["E"]), ALU.mult)
        self.tt('dve', self.KTT[:, :, :], f3(rt["kt"]), f3(rt["E"]), ALU.mult)
        self.act(self.GL[:, :, :], self.GL[:, :, :], AF.Exp, scale=-DSC)
        BD = self.BD
        self.memset('pool', self.A1BD[:, :, :, :], 0.0)
        self.memset('pool', self.A2BD[:, :, :, :], 0.0)
        self.memset('pool', self.NN[0][:, :, :], 0.0)
        c4 = lambda t, par: self.V(t, par * 64, 64, 0, [[256, 3], [64, 4], [1, 64]])
        self.bd_fill(BD["kt"], lambda par: c4(self.KTT, par))
        self.bd_fill(BD["b"], lambda par: c4(self.BTT, par))
        self.bd_fill(BD["kh"], lambda par: self.V(self.KR, par * 64, 64, 0, [[512, 3], [128, 4], [1, 64]]))
        self.bd_fill(BD["r"], lambda par: self.V(self.KR, par * 64, 64, 64, [[512, 3], [128, 4], [1, 64]]))
        for (src, dst, neg) in ((BD["kt"], self.KTTOK, False), (BD["b"], self.BTTOK, True)):
            for half in range(2):
                for cc_ in range(2):
                    c = half * 2 + cc_
                    for hp in range(3):
                        self.tr(self.PTB[:, (cc_ * 3 + hp) * 128:(cc_ * 3 + hp + 1) * 128], src[:, hp, c, :], bf=True)
                o = self.V(dst, 0, 128, half * 768, [[1, 768]])
                if neg:
                    self.act(o, self.PTB[:, 0:768], AF.Copy, scale=-1.0)
                else:
                    self.cp('dve', o, self.PTB[:, 0:768])
        self.cp('act', self.HBF[:, :, :], self.rH[d][:, :, :])
        mk = CC['MKF'] if d == 0 else CC['MKB']
        nmk = CC['NMKF'] if d == 0 else CC['NMKB']
        mn = CC['MNF'] if d == 0 else CC['MNB']
        for j in range(4):
            c = j if d == 0 else 3 - j
            cs = slice(c * 64, (c + 1) * 64)
            KRc = lambda hp: self.V(self.KR, 0, 128, hp * 512 + c * 128, [[1, 128]])
            p1, p2, p3 = self.pq(), self.pq(), self.pq()
            for hp in range(3):
                self.mm(p1[:, hp * 128:(hp + 1) * 128], BD["kt"][:, hp, c, :], KRc(hp), inc=(hp == 2))
            for hp in range(3):
                self.mm(p2[:, hp * 128:(hp + 1) * 128], BD["b"][:, hp, c, :], KRc(hp), inc=(hp == 2))
            for hp in range(3):
                self.mm(p3[:, hp * 64:(hp + 1) * 64], BD["kh"][:, hp, c, :], self.BTT[:, hp, cs], inc=(hp == 2))
            for par in range(2):
                pp_ = par * 64
                self.tt('dve', self.V(self.A1BD, pp_, 64, pp_, [[256, 3], [128, 2], [1, 64]]),
                        self.V(p1, pp_, 64, 0, [[128, 3], [64, 2], [1, 64]]),
                        self.V(self.CST, pp_, 64, mk, [[0, 3], [64, 2], [1, 64]]), ALU.mult)
                self.tt('dve', self.V(self.A2BD, pp_, 64, pp_, [[256, 3], [128, 2], [1, 64]]),
                        self.V(p2, pp_, 64, 0, [[128, 3], [64, 2], [1, 64]]),
                        self.V(self.CST, pp_, 64, nmk, [[0, 3], [64, 2], [1, 64]]), ALU.mult)
                self.tt('dve', self.V(self.NN[0], pp_, 64, pp_, [[128, 3], [1, 64]]),
                        self.V(p3, pp_, 64, 0, [[64, 3], [1, 64]]),
                        self.V(self.CST, pp_, 64, mn, [[0, 3], [1, 64]]), ALU.mult)
            pr_ = self.pq()
            for hp in range(3):
                o = pr_[:, hp * 64:(hp + 1) * 64]
                self.mm(o, BD["kh"][:, hp, c, :], self.HBF[:, hp, :], start=True, stop=False)
                self.mm(o, self.A1BD[:, hp, 0, :], self.VSTK[:, c, hp, :], start=False, stop=True, inc=(hp == 2))
            u192 = self.V(self.U, 0, 128, 0, [[1, 192]])
            ub192 = self.V(self.UBF, 0, 128, 0, [[1, 192]])
            self.cp('dve', ub192, pr_[:, 0:192])
            self.cp('dve', u192, pr_[:, 0:192])
            for k in range(6):
                NTk = self.A2BD[:, :, 0, :] if k == 0 else self.NTT[k % 2][:, :, :]
                Nk = self.NN[k % 2]
                if k < 5:
                    pnt = self.pq()
                    for hp in range(3):
                        self.mm(pnt[:, hp * 128:(hp + 1) * 128], Nk[:, hp, :], NTk[:, hp, :], inc=(hp == 2))
                    pn = self.pq()
                    for hp in range(3):
                        self.mm(pn[:, hp * 128:(hp + 1) * 128], NTk[:, hp, :], Nk[:, hp, :], inc=(hp == 2))
                pu = self.pq()
                for hp in range(3):
                    self.mm(pu[:, hp * 64:(hp + 1) * 64], NTk[:, hp, :], self.UBF[:, hp, :], inc=(hp == 2))
                if k < 5:
                    self.cp('dve', self.V(self.NTT[(k + 1) % 2], 0, 128, 0, [[1, 384]]), pnt[:, 0:384])
                    self.cp('act', self.V(self.NN[(k + 1) % 2], 0, 128, 0, [[1, 384]]), pn[:, 0:384])
                self.tt('dve', ub192, u192, pu[:, 0:192], ALU.add)
                if k < 5:
                    self.tt('dve', u192, u192, pu[:, 0:192], ALU.add)
            py = self.pq()
            for hp in range(3):
                o = py[:, hp * 64:(hp + 1) * 64]
                self.mm(o, BD["r"][:, hp, c, :], self.HBF[:, hp, :], start=True, stop=False)
                self.mm(o, self.A1BD[:, hp, 1, :], self.VSTK[:, c, hp, :], start=False, stop=False)
                self.mm(o, self.A2BD[:, hp, 1, :], self.UBF[:, hp, :], start=False, stop=True, inc=(hp == 2))
            ybv = self.V(self.YB, 0, 128, c * 192, [[1, 192]])
            if d == 1:
                self.cp('act', ybv, py[:, 0:192])
            else:
                self.tt('dve', ybv, ybv, py[:, 0:192], ALU.add)
            ph = self.pq()
            for hp in range(3):
                o = ph[:, hp * 64:(hp + 1) * 64]
                self.mm(o, self.KTTOK[:, c, hp, :], self.VSTK[:, c, hp, :], start=True, stop=False)
                self.mm(o, self.BTTOK[:, c, hp, :], self.UBF[:, hp, :], start=False, stop=True, inc=(hp == 2))
            self.tt('dve', self.HTMP[:, :, :], self.rH[d][:, :, :], self.V(ph, 0, 128, 0, [[64, 3], [1, 64]]), ALU.add)
            self.tt('dve', self.HBF[:, :, :], self.HTMP[:, :, :], self.V(self.GL, 0, 128, c, [[4, 3], [0, 64]]), ALU.mult)
            self.tt('dve', self.rH[d][:, :, :], self.HTMP[:, :, :], self.V(self.GL, 0, 128, c, [[4, 3], [0, 64]]), ALU.mult)
        if last[d] and prompt:
            S.dma('sp', self.o_rH[seq_idx, l, d], self.rH[d][:, :, :])

    def stage_out(self, l, mod, xsrc, xdst, r0):
        S = self.S
        for tt_ in range(2):
            o, xw = self.XN[tt_], self.XW[tt_]
            S.dma('sp', xw[:, :], xsrc[r0 + tt_ * 128: r0 + (tt_ + 1) * 128, :])
            for ch in range(2):
                ps = self.pq()
                for kc in range(8):
                    self.mm(ps[:, :], self.MIXT[:, kc, tt_ * 128:(tt_ + 1) * 128], self.W_OUT[:, kc, ch * 512:(ch + 1) * 512],
                            start=(kc == 0), stop=(kc == 7))
                self.cp('act' if ch else 'dve', o[:, ch * 512:(ch + 1) * 512], ps[:, :])
            self.dump('o_proj%d' % tt_, o[:, :])
            self.dump('o_x%d' % tt_, xw[:, :])
            ssq = self.TMPS[:, 24 + tt_:25 + tt_]
            junk = self.V(self.BLK, 0, 128, 0, [[1, D]])
            self.act(junk, o[:, :], AF.Square, accum=ssq)
            rs = self.TMPS[:, 26 + tt_:27 + tt_]
            self.ts('dve', rs, ssq, 1.0 / D, EPS, ALU.mult, ALU.add)
            self.act(rs, rs, AF.Sqrt)
            self.recip(rs, rs)
            self.dump('o_rs%d' % tt_, rs)
            self.stt(o[:, :], o[:, :], rs, self.GATEB[:, mod, :], ALU.mult, ALU.mult)
            self.dump('o_g%d' % tt_, o[:, :])
            self.tt('dve', o[:, :], o[:, :], xw[:, :], ALU.add)
            S.dma('sp', xdst[r0 + tt_ * 128: r0 + (tt_ + 1) * 128, :], o[:, :])

    def build(self, layers=(0, 1)):
        try:
            self._build(layers)
        except StopIteration:
            pass
        self.S.finish('sp')
        return self.nc

    def _build(self, layers):
        NP, TS = self.NP, self.TS
        self.setup()
        if self.stop == 'setup':
            raise StopIteration
        for li, l in enumerate(layers):
            self.load_layer(l)
            if self.stop == 'load':
                raise StopIteration
            xsrc = self.x_in if li == 0 else self.x1
            xdst = self.y_out if li == len(layers) - 1 else self.x1
            T_, F_ = {0: True, 1: True}, {0: False, 1: False}
            for s in range(NP):
                self.visit(l, 0, xsrc, xdst, s * 256, 256, 0, [0, 1], False, s, T_, T_)
            if TS > 0:
                nb = TS // TB
                row0 = NP * 256
                for b in range(nb - 1, -1, -1):
                    self.visit(l, 1, xsrc, xdst, row0, TS, b * TB, [1], True, 0,
                               {0: False, 1: b == nb - 1}, {0: False, 1: b == 0}, mode='store')
                for b in range(nb):
                    self.visit(l, 1, xsrc, xdst, row0, TS, b * TB, [0], True, 0,
                               {0: b == 0, 1: False}, {0: b == nb - 1, 1: False}, mode='load')


def prep_shared(inp):
    f = lambda a: np.ascontiguousarray(np.asarray(a, dtype=np.float32))
    b_mod = f(inp['b_mod'])
    sh = {}
    sh['w_mod'] = f(inp['w_mod'])
    sh['w_in'] = f(inp['w_in'])
    sh['w_out'] = f(inp['w_out'])
    sh['bmodT'] = f(b_mod[:, :2048].reshape(2, 16, 128).transpose(2, 0, 1))
    sh['bmodg'] = f(b_mod[:, 2048:3072])
    sh['gpost'] = f(inp['g_post'])
    pp = np.zeros((128, 2 * PL), np.float32)
    for l in range(2):
        o = l * PL
        def put(key, arr, n):
            pp[:, o + PC[key]: o + PC[key] + n] = np.asarray(arr, np.float32).reshape(n, 128).T
        put('G_PRE', inp['g_pre'][l], 8)
        put('M_NORM', inp['m_norm'][l], 3)
        put('R_MU', inp['r_mu'][l], 11)
        put('R_W0', np.asarray(inp['r_w0'][l]).reshape(-1), 6)
        put('R_A0', np.asarray(inp['r_a0'][l]).reshape(-1), 6)
        put('R_KK', inp['r_kk'][l], 3)
        put('R_KA', inp['r_ka'][l], 3)
        put('R_RK', inp['r_rk'][l], 3)
        put('R_NORM', inp['r_norm'][l], 3)
        put('L_CONV', np.asarray(inp['l_conv'][l]).reshape(-1), 8)
        put('L_CONVB', inp['l_conv_b'][l], 2)
        put('L_BA', np.asarray(inp['l_ba'][l]).reshape(-1), 4)
        put('L_BX', np.asarray(inp['l_bx'][l]).reshape(-1), 4)
        put('L_LAM', np.asarray(inp['l_lambda'][l]).reshape(-1), 4)
        pp[0:6, o + PC['M_BI']: o + PC['M_BI'] + 2] = np.asarray(inp['m_bi'][l], np.float32).T
        pp[0:6, o + PC['M_BF']: o + PC['M_BF'] + 2] = np.asarray(inp['m_bf'][l], np.float32).T
    sh['pp'] = pp
    sh['cst'] = make_consts()
    lora = np.zeros((128, 2, 2, 384), np.float32)
    for wi, key in enumerate(['r_w2', 'r_a2']):
        a = np.asarray(inp[key], np.float32)
        lora[:, :, wi, :] = a.transpose(1, 2, 0, 3).reshape(128, 2, 384)
    sh['lora'] = lora
    lruw = np.zeros((128, 2, 8, 128), np.float32)
    for gi, key in enumerate(['l_wa', 'l_wx']):
        a = np.asarray(inp[key], np.float32)
        for l in range(2):
            for d in range(2):
                for pr in range(2):
                    for hb in range(2):
                        n = 2 * pr + hb
                        lruw[hb * 64:(hb + 1) * 64, l, (gi * 2 + d) * 2 + pr, hb * 64:(hb + 1) * 64] = a[l, d, n]
    sh['lruw'] = lruw
    return sh


def prep_core(inp, b, NP, TS):
    f = lambda a: np.ascontiguousarray(np.asarray(a, dtype=np.float32))
    m = {}
    xp = np.asarray(inp['x_prompt'], np.float32)[b * NP:(b + 1) * NP].reshape(NP * 256, D)
    if TS > 0:
        xs = np.asarray(inp['x_sample'], np.float32)[b]
        m['x_in'] = f(np.concatenate([xp, xs], 0))
    else:
        m['x_in'] = f(xp)
    cc = np.stack([np.asarray(inp['c_ctx'], np.float32), np.asarray(inp['c'], np.float32)[b]], -1)
    m['cc'] = f(cc.reshape(8, 128, 2).transpose(1, 0, 2))
    C = np.asarray(inp['state_mlstm_C'], np.float32)[b]
    n = np.asarray(inp['state_mlstm_n'], np.float32)[b]
    Cn = np.concatenate([C, n[..., None]], -1)
    Cn = Cn.reshape(2, 2, 3, 2, 64, 65).transpose(0, 1, 3, 4, 2, 5).reshape(2, 2, 128, 3, 65)
    m['st_mC'] = f(Cn)
    m['st_mm'] = f(np.asarray(inp['state_mlstm_m'], np.float32)[b].transpose(2, 0, 1))
    R = np.asarray(inp['state_rwkv'], np.float32)[b]
    R = R.transpose(0, 1, 2, 4, 3)
    R = R.reshape(2, 2, 3, 2, 64, 64).transpose(0, 1, 3, 4, 2, 5).reshape(2, 2, 128, 3, 64)
    m['st_rH'] = f(R)
    L = np.asarray(inp['state_rglru'], np.float32)[b]
    m['st_l'] = f(L.reshape(2, 2, 2, 128).transpose(3, 0, 1, 2))
    return m


def unpack_core(r, NP, TS):
    y = r['y_out']
    yp = y[:NP * 256].reshape(NP, 256, D)
    ys = y[NP * 256:]
    mC = r['o_mC'].reshape(NP, 2, 2, 2, 64, 3, 65).transpose(0, 1, 2, 5, 3, 4, 6).reshape(NP, 2, 2, 6, 64, 65)
    newC = np.ascontiguousarray(mC[..., :64])
    newn = np.ascontiguousarray(mC[..., 64])
    newm = r['o_mm'].reshape(NP, 2, 2, 6)
    rH = r['o_rH'].reshape(NP, 2, 2, 2, 64, 3, 64).transpose(0, 1, 2, 5, 3, 4, 6).reshape(NP, 2, 2, 6, 64, 64)
    newr = np.ascontiguousarray(rH.transpose(0, 1, 2, 3, 5, 4))
    newl = np.ascontiguousarray(r['o_l'].transpose(0, 1, 2, 4, 3).reshape(NP, 2, 2, 256))
    return yp, ys, newC, newn, newm, newr, newl


_NC_CACHE = {}


def kernel(**inputs):
    NP, TS = 4, 2048
    key = (NP, TS)
    if key not in _NC_CACHE:
        _NC_CACHE[key] = Builder(NP, TS).build()
    nc = _NC_CACHE[key]
    sh = prep_shared(inputs)
    in_maps = []
    for b in range(NCORES):
        m = dict(sh)
        m.update(prep_core(inputs, b, NP, TS))
        in_maps.append(m)
    res = run_bass_kernel_spmd(nc, in_maps, core_ids=list(range(NCORES)))
    outs = [unpack_core(r, NP, TS) for r in res.results]
    y_prompt = np.concatenate([o[0] for o in outs], 0)
    y_sample = np.stack([o[1] for o in outs], 0)
    cat = lambda i: np.concatenate([o[i] for o in outs], 0)
    return (y_prompt.astype(np.float32), y_sample.astype(np.float32), cat(2).astype(np.float32),
            cat(3).astype(np.float32), cat(4).astype(np.float32), cat(5).astype(np.float32), cat(6).astype(np.float32))
```
